# Optimizing a Trainium2 kernel written in Bass

```python
import jax, jax.numpy as jnp
from jax import lax
import numpy as np

D_MODEL = 1024
BATCH = 8
SEQ = 4096
DEPTH = 2

GRID_W = 64
CTX_LEN = 256
EPS = 1e-6
ATTN_QBLOCK = 128
ROPE_BASE = 10000.0

MLA_HEADS = 8
MLA_Q_RANK = 384
MLA_KV_RANK = 256
MLA_NOPE = 64
MLA_ROPE = 32
MLA_V = 64
ROPE_PAIRS = MLA_ROPE // 4
MLSTM_HEADS = 4
MLSTM_DH = 128
MLSTM_CONV = 3
MLSTM_CHUNK = 64
GLA_HEADS = 4
GLA_DK = 64
GLA_DV = 128
GLA_GATE_RANK = 16
GLA_TAU = 16.0
GLA_CHUNK = 64
NA_HEADS = 8
NA_DH = 64
NA_KH = 8
NA_KW = 16

BRANCH_A = MLA_HEADS * MLA_V
BRANCH_B = MLSTM_HEADS * MLSTM_DH
BRANCH_C = GLA_HEADS * GLA_DV
BRANCH_D = NA_HEADS * NA_DH
EVEN_SPLITS = (MLA_Q_RANK, MLA_KV_RANK, MLA_ROPE, BRANCH_B, BRANCH_B, BRANCH_B, BRANCH_B, 4 * MLSTM_HEADS, BRANCH_A + BRANCH_B)
ODD_SPLITS = (GLA_HEADS * GLA_DK, GLA_HEADS * GLA_DK, BRANCH_C, 2 * GLA_GATE_RANK, BRANCH_D, BRANCH_D, BRANCH_D, BRANCH_C + BRANCH_D)
EVEN_IN = sum(EVEN_SPLITS)
ODD_IN = sum(ODD_SPLITS)

kernel_name = 'hybrid_mla_mlstm_gla_natten_dit'


def split_cols(a, sizes):
    return jnp.split(a, np.cumsum(sizes)[:-1].tolist(), axis=-1)


def rmsnorm(x, g):
    xf = x.astype(jnp.float32)
    xf = xf * lax.rsqrt(jnp.mean(xf * xf, axis=-1, keepdims=True) + EPS)
    return xf.astype(x.dtype) * g


def head_rmsnorm(x, g, n_heads):
    B, T, W = x.shape
    return rmsnorm(x.reshape(B, T, n_heads, W // n_heads), g.reshape(n_heads, -1)).reshape(B, T, W)


def to_heads(a, n_heads):
    B, T, W = a.shape
    return a.reshape(B, T, n_heads, W // n_heads).transpose(0, 2, 1, 3)


def merge_heads(a):
    B, H, T, d = a.shape
    return a.transpose(0, 2, 1, 3).reshape(B, T, H * d)


def centred_dwconv(x, w, b):
    K = w.shape[0]
    T = x.shape[1]
    pad = K // 2
    xp = jnp.pad(x, ((0, 0), (pad, K - 1 - pad), (0, 0)))
    y = xp[:, 0:T] * w[0]
    for j in range(1, K):
        y = y + xp[:, j:j + T] * w[j]
    return y + b


def rope2d_tables(S):
    t = jnp.arange(S)
    inv = 1.0 / (ROPE_BASE ** (jnp.arange(ROPE_PAIRS, dtype=jnp.float32) / ROPE_PAIRS))
    ang = jnp.concatenate([(t // GRID_W)[:, None] * inv, (t % GRID_W)[:, None] * inv], axis=-1)
    return jnp.cos(ang), jnp.sin(ang)


def apply_rope2d(x, cos, sin):
    cos = cos.astype(x.dtype)
    sin = sin.astype(x.dtype)
    parts = []
    for a in range(2):
        y = x[..., a * 2 * ROPE_PAIRS:(a + 1) * 2 * ROPE_PAIRS]
        y1, y2 = y[..., :ROPE_PAIRS], y[..., ROPE_PAIRS:]
        ca = cos[:, a * ROPE_PAIRS:(a + 1) * ROPE_PAIRS]
        sa = sin[:, a * ROPE_PAIRS:(a + 1) * ROPE_PAIRS]
        parts += [y1 * ca - y2 * sa, y2 * ca + y1 * sa]
    return jnp.concatenate(parts, axis=-1)


def to_chunks(a, L):
    B, H, T = a.shape[:3]
    return jnp.moveaxis(a.reshape(B, H, T // L, L, *a.shape[3:]), 2, 0)


def from_chunks(a):
    nc, B, H, L = a.shape[:4]
    return jnp.moveaxis(a, 0, 2).reshape(B, H, nc * L, *a.shape[4:])


def block_attention(q, k, v, scale):
    B, H, S, dq = q.shape
    nb = S // ATTN_QBLOCK
    qb = jnp.moveaxis(q.reshape(B, H, nb, ATTN_QBLOCK, dq), 2, 0)

    def one_block(qi):
        s = jnp.einsum('bhqd,bhkd->bhqk', qi, k).astype(jnp.float32) * scale
        p = jax.nn.softmax(s, axis=-1).astype(v.dtype)
        return jnp.einsum('bhqk,bhkd->bhqd', p, v)

    o = lax.map(one_block, qb)
    return jnp.moveaxis(o, 0, 2).reshape(B, H, S, v.shape[-1])


def mlstm_chunked(q, k, v, ig, lf, state):
    L = MLSTM_CHUNK
    causal = jnp.tril(jnp.ones((L, L), dtype=bool))

    def step(carry, xs):
        C, n, m = carry
        qc, kc, vc, ic, fc = xs
        b = jnp.cumsum(fc, axis=-1)
        logw = jnp.where(causal, b[..., :, None] - b[..., None, :] + ic[..., None, :], -jnp.inf)
        inter = b + m[..., None]
        m_t = jnp.maximum(inter, jnp.max(logw, axis=-1))
        w_intra = jnp.exp(logw - m_t[..., None])
        w_inter = jnp.exp(inter - m_t)
        qk = jnp.einsum('bhtd,bhsd->bhts', qc, kc) * w_intra
        num = jnp.einsum('bhts,bhsv->bhtv', qk, vc) + w_inter[..., None] * jnp.einsum('bhtd,bhdv->bhtv', qc, C)
        den = jnp.sum(qk, axis=-1) + w_inter * jnp.einsum('bhtd,bhd->bht', qc, n)
        h = num / jnp.maximum(jnp.abs(den), jnp.exp(-m_t))[..., None]
        b_last = b[..., -1]
        logk = b_last[..., None] - b + ic
        m_new = jnp.maximum(b_last + m, jnp.max(logk, axis=-1))
        carry_decay = jnp.exp(b_last + m - m_new)
        kw = jnp.exp(logk - m_new[..., None])[..., None] * kc
        C = carry_decay[..., None, None] * C + jnp.einsum('bhsd,bhsv->bhdv', kw, vc)
        n = carry_decay[..., None] * n + jnp.sum(kw, axis=2)
        return (C, n, m_new), h

    xs = tuple(to_chunks(a.astype(jnp.float32), L) for a in (q, k, v, ig, lf))
    state, h = lax.scan(step, state, xs)
    return from_chunks(h), state


def gla_chunked(q, k, v, lg, S0):
    L = GLA_CHUNK
    causal = jnp.tril(jnp.ones((L, L), dtype=bool))

    def step(S, xs):
        qc, kc, vc, gc = xs
        bcum = jnp.cumsum(gc, axis=2)
        rel = bcum[:, :, :, None, :] - bcum[:, :, None, :, :]
        decay = jnp.exp(jnp.where(causal[:, :, None], rel, -jnp.inf))
        att = jnp.einsum('bhtd,bhsd,bhtsd->bhts', qc, kc, decay)
        o = jnp.einsum('bhts,bhsv->bhtv', att, vc) + jnp.einsum('bhtd,bhdv->bhtv', qc * jnp.exp(bcum), S)
        b_last = bcum[:, :, -1:]
        S_new = jnp.exp(b_last[:, :, 0])[..., None] * S + jnp.einsum('bhsd,bhsv->bhdv', kc * jnp.exp(b_last - bcum), vc)
        return S_new, o

    xs = tuple(to_chunks(a.astype(jnp.float32), L) for a in (q, k, v, lg))
    S, o = lax.scan(step, S0, xs)
    return from_chunks(o), S


def bidirectional_scan(chunk_fn, init_state, shared_c, gates_c, shared_l, gates_l):
    outs_c, outs_l = [], []
    for d in range(2):
        if d == 0:
            f = lambda a: a
        else:
            f = lambda a: jnp.flip(a, axis=2)
        hc, st = chunk_fn(*[f(a) for a in shared_c], *[f(g[d]) for g in gates_c], init_state)
        hl, _ = chunk_fn(*[f(a) for a in shared_l], *[f(g[d]) for g in gates_l], st)
        outs_c.append(f(hc))
        outs_l.append(f(hl))
    return outs_c[0] + outs_c[1], outs_l[0] + outs_l[1]


def neighbourhood_attention(q, k, v, k_ctx, v_ctx, rpb, rows):
    B, H, S, d = q.shape
    kh = min(NA_KH, rows)
    qg = q.reshape(B, H, rows, GRID_W, d)
    kg = k.reshape(B, H, rows, GRID_W, d)
    vg = v.reshape(B, H, rows, GRID_W, d)
    r_idx = jnp.arange(rows)
    c_idx = jnp.arange(GRID_W)
    win_rows = jnp.clip(r_idx - kh // 2, 0, rows - kh)[:, None] + jnp.arange(kh)
    win_cols = jnp.clip(c_idx - NA_KW // 2, 0, GRID_W - NA_KW)[:, None] + jnp.arange(NA_KW)
    dcol = win_cols - c_idx[:, None] + (NA_KW - 1)
    scale = d ** -0.5
    n_loc = kh * NA_KW

    def one_row(r):
        rws = win_rows[r]
        drow = rws - r + (NA_KH - 1)
        bias = rpb[:, drow[None, :, None], dcol[:, None, :]].astype(jnp.float32)
        qr = qg[:, :, r]
        kw = kg[:, :, rws][:, :, :, win_cols]
        vw = vg[:, :, rws][:, :, :, win_cols]
        s_loc = jnp.einsum('bhcd,bhicjd->bhcij', qr, kw).astype(jnp.float32) * scale + bias
        s_ctx = jnp.einsum('bhcd,bhnd->bhcn', qr, k_ctx).astype(jnp.float32) * scale
        logits = jnp.concatenate([s_loc.reshape(B, H, GRID_W, n_loc), s_ctx], axis=-1)
        p = jax.nn.softmax(logits, axis=-1).astype(v.dtype)
        p_loc = p[..., :n_loc].reshape(B, H, GRID_W, kh, NA_KW)
        return jnp.einsum('bhcij,bhicjd->bhcd', p_loc, vw) + jnp.einsum('bhcn,bhnd->bhcd', p[..., n_loc:], v_ctx)

    o = lax.map(one_row, r_idx)
    return jnp.moveaxis(o, 0, 2).reshape(B, H, S, d)


def mla_mlstm_mixer(u_lat, u_ctx, cos, sin, w_in, q_norm, w_uq, kv_norm, w_ukv, conv_w, conv_b, b_i, b_f, h_norm, w_out, need_ctx):
    def project(u, rotate):
        B, T, _ = u.shape
        cq, ckv, k_rope, mq, mk, mv, mo, gates, z = split_cols(u @ w_in, EVEN_SPLITS)
        q = to_heads(rmsnorm(cq, q_norm) @ w_uq, MLA_HEADS)
        kv = to_heads(rmsnorm(ckv, kv_norm) @ w_ukv, MLA_HEADS)
        q_nope, q_rope = q[..., :MLA_NOPE], q[..., MLA_NOPE:]
        k_nope, v = kv[..., :MLA_NOPE], kv[..., MLA_NOPE:]
        k_rope = k_rope[:, None]
        if rotate:
            q_rope = apply_rope2d(q_rope, cos, sin)
            k_rope = apply_rope2d(k_rope, cos, sin)
        q = jnp.concatenate([q_nope, q_rope], axis=-1)
        k = jnp.concatenate([k_nope, jnp.broadcast_to(k_rope, (B, MLA_HEADS, T, MLA_ROPE))], axis=-1)
        qk = jax.nn.silu(centred_dwconv(jnp.concatenate([mq, mk], axis=-1), conv_w, conv_b))
        mq = to_heads(qk[..., :BRANCH_B], MLSTM_HEADS) * MLSTM_DH ** -0.5
        mk = to_heads(qk[..., BRANCH_B:], MLSTM_HEADS)
        mv = to_heads(mv, MLSTM_HEADS)
        g = gates.reshape(B, T, 2, 2, MLSTM_HEADS).astype(jnp.float32)
        ig = (g[:, :, :, 0] + b_i).transpose(2, 0, 3, 1)
        lf = jax.nn.log_sigmoid(g[:, :, :, 1] + b_f).transpose(2, 0, 3, 1)
        return (q, k, v), (mq, mk, mv), (ig, lf), mo, z

    (qa_l, ka_l, va_l), seq_l, gates_l, mo_l, z_l = project(u_lat, True)
    (qa_c, ka_c, va_c), seq_c, gates_c, mo_c, z_c = project(u_ctx, False)
    scale = (MLA_NOPE + MLA_ROPE) ** -0.5
    a_lat = block_attention(qa_l, jnp.concatenate([ka_c, ka_l], axis=2), jnp.concatenate([va_c, va_l], axis=2), scale)
    B = u_lat.shape[0]
    init = (jnp.zeros((B, MLSTM_HEADS, MLSTM_DH, MLSTM_DH), jnp.float32),
            jnp.zeros((B, MLSTM_HEADS, MLSTM_DH), jnp.float32),
            jnp.zeros((B, MLSTM_HEADS), jnp.float32))
    h_c, h_l = bidirectional_scan(mlstm_chunked, init, seq_c, gates_c, seq_l, gates_l)

    def combine(a, h, mo, z, dtype):
        hm = head_rmsnorm(jax.nn.sigmoid(mo.astype(jnp.float32)) * merge_heads(h), h_norm, MLSTM_HEADS).astype(dtype)
        return (jnp.concatenate([merge_heads(a), hm], axis=-1) * jax.nn.silu(z)) @ w_out

    y_lat = combine(a_lat, h_l, mo_l, z_l, u_lat.dtype)
    y_ctx = None
    if need_ctx:
        y_ctx = combine(block_attention(qa_c, ka_c, va_c, scale), h_c, mo_c, z_c, u_ctx.dtype)
    return y_lat, y_ctx


def gla_na_mixer(u_lat, u_ctx, rows, w_in, w_gate, b_gate, gla_norm, rpb, w_out, need_ctx):
    def project(u):
        B, T, _ = u.shape
        gq, gk, gv, ga, nq, nk, nv, z = split_cols(u @ w_in, ODD_SPLITS)
        ga = ga.reshape(B, T, 2, GLA_GATE_RANK)
        lg = jax.nn.log_sigmoid(jnp.einsum('btdr,drk->dbtk', ga, w_gate).astype(jnp.float32) + b_gate[:, None, None]) / GLA_TAU
        lg = lg.reshape(2, B, T, GLA_HEADS, GLA_DK).transpose(0, 1, 3, 2, 4)
        seq = (to_heads(gq, GLA_HEADS) * GLA_DK ** -0.5, to_heads(gk, GLA_HEADS), to_heads(gv, GLA_HEADS))
        na = (to_heads(nq, NA_HEADS), to_heads(nk, NA_HEADS), to_heads(nv, NA_HEADS))
        return seq, (lg,), na, z

    seq_l, gates_l, (nq_l, nk_l, nv_l), z_l = project(u_lat)
    seq_c, gates_c, (nq_c, nk_c, nv_c), z_c = project(u_ctx)
    B = u_lat.shape[0]
    init = jnp.zeros((B, GLA_HEADS, GLA_DK, GLA_DV), jnp.float32)
    o_c, o_l = bidirectional_scan(gla_chunked, init, seq_c, gates_c, seq_l, gates_l)
    na_l = neighbourhood_attention(nq_l, nk_l, nv_l, nk_c, nv_c, rpb, rows)

    def combine(o, na, z, dtype):
        g = head_rmsnorm(merge_heads(o), gla_norm, GLA_HEADS).astype(dtype)
        return (jnp.concatenate([g, merge_heads(na)], axis=-1) * jax.nn.silu(z)) @ w_out

    y_lat = combine(o_l, na_l, z_l, u_lat.dtype)
    y_ctx = None
    if need_ctx:
        y_ctx = combine(o_c, block_attention(nq_c, nk_c, nv_c, NA_DH ** -0.5), z_c, u_ctx.dtype)
    return y_lat, y_ctx


def setup_inputs(seed: int = 0) -> dict:
    key = jax.random.key(seed)
    keys = iter(jax.random.split(key, 32))

    def rnd(shape, s):
        return jax.random.normal(next(keys), shape, jnp.float32) * s

    def gain(n):
        return 1.0 + rnd((n,), 0.05)

    D = D_MODEL
    return {
        'x': rnd((BATCH, SEQ, D), 1.0),
        'c': rnd((BATCH, D), 1.0),
        'ctx': rnd((BATCH, CTX_LEN, D), 1.0),
        'c_ctx': rnd((D,), 1.0),
        'l0_norm': gain(D),
        'l0_w_mod': rnd((D, 3 * D), 0.5 * D ** -0.5),
        'l0_b_mod': rnd((3 * D,), 0.02),
        'l0_w_in': rnd((D, EVEN_IN), D ** -0.5),
        'l0_mla_q_norm': gain(MLA_Q_RANK),
        'l0_mla_w_uq': rnd((MLA_Q_RANK, MLA_HEADS * (MLA_NOPE + MLA_ROPE)), MLA_Q_RANK ** -0.5),
        'l0_mla_kv_norm': gain(MLA_KV_RANK),
        'l0_mla_w_ukv': rnd((MLA_KV_RANK, MLA_HEADS * (MLA_NOPE + MLA_V)), MLA_KV_RANK ** -0.5),
        'l0_mlstm_conv_w': rnd((MLSTM_CONV, 2 * BRANCH_B), MLSTM_CONV ** -0.5),
        'l0_mlstm_conv_b': rnd((2 * BRANCH_B,), 0.02),
        'l0_mlstm_b_i': rnd((2, MLSTM_HEADS), 0.1),
        'l0_mlstm_b_f': jnp.linspace(3.0, 6.0, MLSTM_HEADS)[None] + rnd((2, MLSTM_HEADS), 0.1),
        'l0_mlstm_norm': gain(BRANCH_B),
        'l0_w_out': rnd((BRANCH_A + BRANCH_B, D), (BRANCH_A + BRANCH_B) ** -0.5),
        'l1_norm': gain(D),
        'l1_w_mod': rnd((D, 3 * D), 0.5 * D ** -0.5),
        'l1_b_mod': rnd((3 * D,), 0.02),
        'l1_w_in': rnd((D, ODD_IN), D ** -0.5),
        'l1_gla_w_gate': rnd((2, GLA_GATE_RANK, GLA_HEADS * GLA_DK), GLA_GATE_RANK ** -0.5),
        'l1_gla_b_gate': rnd((2, GLA_HEADS * GLA_DK), 0.5),
        'l1_gla_norm': gain(BRANCH_C),
        'l1_na_rpb': rnd((NA_HEADS, 2 * NA_KH - 1, 2 * NA_KW - 1), 0.1),
        'l1_w_out': rnd((BRANCH_C + BRANCH_D, D), (BRANCH_C + BRANCH_D) ** -0.5),
        'final_norm': gain(D),
    }


def reference(x, c, ctx, c_ctx,
              l0_norm, l0_w_mod, l0_b_mod, l0_w_in, l0_mla_q_norm, l0_mla_w_uq, l0_mla_kv_norm, l0_mla_w_ukv,
              l0_mlstm_conv_w, l0_mlstm_conv_b, l0_mlstm_b_i, l0_mlstm_b_f, l0_mlstm_norm, l0_w_out,
              l1_norm, l1_w_mod, l1_b_mod, l1_w_in, l1_gla_w_gate, l1_gla_b_gate, l1_gla_norm, l1_na_rpb, l1_w_out,
              final_norm):
    S = x.shape[1]
    rows = S // GRID_W
    cos, sin = rope2d_tables(S)
    norms = (l0_norm, l1_norm)
    mods = ((l0_w_mod, l0_b_mod), (l1_w_mod, l1_b_mod))
    mixer_params = (
        (l0_w_in, l0_mla_q_norm, l0_mla_w_uq, l0_mla_kv_norm, l0_mla_w_ukv, l0_mlstm_conv_w, l0_mlstm_conv_b,
         l0_mlstm_b_i, l0_mlstm_b_f, l0_mlstm_norm, l0_w_out),
        (l1_w_in, l1_gla_w_gate, l1_gla_b_gate, l1_gla_norm, l1_na_rpb, l1_w_out),
    )
    h_lat, h_ctx = x, ctx
    for layer in range(DEPTH):
        need_ctx = layer < DEPTH - 1
        w_mod, b_mod = mods[layer]
        shift_l, scale_l, gate_l = jnp.split(jax.nn.silu(c) @ w_mod + b_mod, 3, axis=-1)
        shift_c, scale_c, gate_c = jnp.split(jax.nn.silu(c_ctx) @ w_mod + b_mod, 3, axis=-1)
        u_lat = rmsnorm(h_lat, norms[layer]) * (1 + scale_l[:, None]) + shift_l[:, None]
        u_ctx = rmsnorm(h_ctx, norms[layer]) * (1 + scale_c) + shift_c
        if layer % 2 == 0:
            y_lat, y_ctx = mla_mlstm_mixer(u_lat, u_ctx, cos, sin, *mixer_params[layer], need_ctx=need_ctx)
        else:
            y_lat, y_ctx = gla_na_mixer(u_lat, u_ctx, rows, *mixer_params[layer], need_ctx=need_ctx)
        h_lat = h_lat + gate_l[:, None] * y_lat
        if need_ctx:
            h_ctx = h_ctx + gate_c * y_ctx
    return rmsnorm(h_lat, final_norm)
```

```python
import numpy as np
from contextlib import ExitStack
import ml_dtypes
import concourse.bass as bass
import concourse.mybir as mybir
from concourse.bass_utils import run_bass_kernel_spmd

F32 = mybir.dt.float32
BF16 = mybir.dt.bfloat16
AF = mybir.ActivationFunctionType
ALU = mybir.AluOpType
AX = mybir.AxisListType

D = 1024
TC = 256
TL = 4096
T = TC + TL
NT = T // 128
EPS = 1e-6
MASKV = -30000.0

GROUPS = [(0, 256)] + [(256 + 512 * i, 512) for i in range(8)]


class Dep:
    __slots__ = ("w", "r")

    def __init__(self):
        self.w = None
        self.r = {}


class Tile:
    def __init__(self, t):
        self.t = t
        self.d = Dep()

    def __getitem__(self, k):
        return self.t[k]


class Prog:
    def __init__(self, nc, es):
        self.nc = nc
        self.eng = {"pe": nc.tensor, "act": nc.scalar, "dve": nc.vector, "pool": nc.gpsimd, "sp": nc.sync}
        self.R = 12
        self.keys = [("pe", "c"), ("act", "c"), ("dve", "c"), ("pool", "c")]
        for q in ("sp", "pool"):
            self.keys += [(q, "d%d" % i) for i in range(self.R)]
        self.ndma = {"sp": 0, "pool": 0}
        self.sem = {k: es.enter_context(nc.semaphore("s_%s_%s" % k)) for k in self.keys}
        self.cnt = {k: 0 for k in self.keys}
        self.waited = {e: {} for e in self.eng}
        self.n = 0

    def _emit(self, eng, kind, fn, reads, writes):
        if kind == "d":
            kind = "d%d" % (self.ndma[eng] % self.R)
            self.ndma[eng] += 1
        key = (eng, kind)
        deps = {}
        if kind != "c" and self.cnt[key] > 0:
            deps[key] = self.cnt[key]

        def add(tok):
            if tok is None:
                return
            k, v = tok
            if deps.get(k, 0) < v:
                deps[k] = v

        for b in reads:
            add(b.d.w)
        for b in writes:
            add(b.d.w)
            for k, v in b.d.r.items():
                add((k, v))
        e = self.eng[eng]
        wd = self.waited[eng]
        for k, v in deps.items():
            if k == ("pe", "c") and eng == "pe":
                continue
            if wd.get(k, 0) >= v:
                continue
            e.wait_ge(self.sem[k], v)
            wd[k] = v
        inc = 16 if kind != "c" else 1
        self.cnt[key] += inc
        fn(e).then_inc(self.sem[key], inc)
        v = self.cnt[key]
        for b in reads:
            if b.d.r.get(key, 0) < v:
                b.d.r[key] = v
        for b in writes:
            b.d.w = (key, v)
            b.d.r = {}
        self.n += 1

    def op(self, eng, name, reads, writes, *a, **kw):
        self._emit(eng, "c", lambda e: getattr(e, name)(*a, **kw), reads, writes)

    def dma(self, q, out, in_, reads=(), writes=(), **kw):
        self._emit(q, "d", lambda e: e.dma_start(out=out, in_=in_, **kw), reads, writes)

    def barrier(self):
        for en, e in self.eng.items():
            wd = self.waited[en]
            for k in self.keys:
                v = self.cnt[k]
                if v > 0 and wd.get(k, 0) < v:
                    e.wait_ge(self.sem[k], v)
                    wd[k] = v


class Alloc:
    def __init__(self, nc, es):
        self.nc = nc
        self.es = es
        _CTR.setdefault(id(nc), 0)

    def _nm(self, name):
        _CTR[id(self.nc)] = _CTR.get(id(self.nc), 0) + 1
        return "%s_%d" % (name, _CTR[id(self.nc)])

    def sb(self, shape, dt, name=None):
        return Tile(self.es.enter_context(self.nc.sbuf_tensor(self._nm(name or "sb"), list(shape), dt)))

    def ps(self, shape, dt, name=None):
        return Tile(self.es.enter_context(self.nc.psum_tensor(self._nm(name or "ps"), list(shape), dt)))


_CTR = {}


def _fm(v, nchunk):
    return np.ascontiguousarray(v.reshape(nchunk, 128).T)


def _rope_perm():
    perm = np.zeros(32, np.int64)
    for i in range(32):
        r = i % 16
        perm[i] = i + 8 if r < 8 else i - 8
    return perm


def _rope_tables():
    t = np.arange(TL)
    inv = (1.0 / (10000.0 ** (np.arange(8, dtype=np.float32) / 8))).astype(np.float32)
    pos = [(t // 64).astype(np.float32), (t % 64).astype(np.float32)]
    C = np.zeros((32, TL), np.float32)
    S = np.zeros((32, TL), np.float32)
    for i in range(32):
        a = i // 16
        r = i % 16
        p = r % 8
        ang = (pos[a] * inv[p]).astype(np.float32)
        C[i] = np.cos(ang)
        S[i] = -np.sin(ang) if r < 8 else np.sin(ang)
    Cf = np.zeros((128, TL), np.float32)
    Sf = np.zeros((128, TL), np.float32)
    Cf[0:32] = C
    Cf[64:96] = C
    Sf[0:32] = S
    Sf[64:96] = S
    return Cf, Sf


def prep_inputs(inp):
    sh = {}
    sh["ident_bf"] = np.eye(128, dtype=np.float32).astype(ml_dtypes.bfloat16)
    sh["ident_f"] = np.eye(128, dtype=np.float32)
    perm = _rope_perm()
    w_in = inp["l0_w_in"]
    gi_cols = [2720 + d * 8 + h for d in range(2) for h in range(4)]
    gf_cols = [2720 + d * 8 + 4 + h for d in range(2) for h in range(4)]
    sh["l0_w_in"] = np.ascontiguousarray(
        np.concatenate([w_in, w_in[:, 640:672][:, perm], w_in[:, gi_cols], w_in[:, gf_cols]], axis=1))
    w_uq = inp["l0_mla_w_uq"].reshape(384, 8, 96)
    ext = np.concatenate([w_uq, w_uq[:, :, 0:64], w_uq[:, :, 64:96][:, :, perm]], axis=2)
    sh["l0_w_uq"] = np.ascontiguousarray(ext.reshape(384, 8 * 192))
    w_ukv = inp["l0_mla_w_ukv"].reshape(256, 8, 128)
    sh["l0_w_ukv"] = np.ascontiguousarray(
        np.concatenate([w_ukv[:, :, 0:64].reshape(256, 512), w_ukv[:, :, 64:128].reshape(256, 512)], axis=1))
    sh["l0_qnT"] = _fm(inp["l0_mla_q_norm"], 3)
    sh["l0_kvnT"] = _fm(inp["l0_mla_kv_norm"], 2)
    Cf, Sf = _rope_tables()
    sh["ropeC"] = Cf
    sh["ropeS"] = Sf
    cw = inp["l0_mlstm_conv_w"]
    sh["l0_convT"] = np.ascontiguousarray(
        np.concatenate([cw.reshape(3, 8, 128).transpose(2, 1, 0), inp["l0_mlstm_conv_b"].reshape(8, 128).T[:, :, None]],
                       axis=2))
    gb = np.zeros((16, 1), np.float32)
    for d in range(2):
        for h in range(4):
            gb[d * 8 + h, 0] = inp["l0_mlstm_b_i"][d, h]
            gb[d * 8 + 4 + h, 0] = inp["l0_mlstm_b_f"][d, h]
    sh["l0_gbias"] = gb
    gb2 = np.zeros((64, 2), np.float32)
    for d in range(2):
        for h in range(4):
            gb2[d * 32 + h, 0] = inp["l0_mlstm_b_i"][d, h]
            gb2[d * 32 + h, 1] = inp["l0_mlstm_b_f"][d, h]
    sh["l0_gb2"] = gb2
    sh["l0_hnorm"] = np.ascontiguousarray(inp["l0_mlstm_norm"].reshape(1, 512))
    sh["l0_w_out"] = inp["l0_w_out"]
    sh["l1_w_in"] = inp["l1_w_in"]
    sh["l1_w_gate"] = np.ascontiguousarray(inp["l1_gla_w_gate"])
    sh["l1_bgT"] = np.ascontiguousarray(inp["l1_gla_b_gate"].reshape(2, 2, 128).transpose(2, 0, 1))
    sh["l1_gnorm"] = np.ascontiguousarray(inp["l1_gla_norm"].reshape(1, 512))
    sh["l1_w_out"] = inp["l1_w_out"]
    sh["final_norm"] = np.ascontiguousarray(inp["final_norm"].reshape(1, 1024))
    rpb = inp["l1_na_rpb"]
    kc = np.arange(64)[:, None]
    qc = np.arange(64)[None, :]
    wc0 = np.clip(qc - 8, 0, 48)
    okc = (kc >= wc0) & (kc < wc0 + 16)
    dcol = np.clip(kc - qc + 15, 0, 30)
    Tb = np.full((8, 15, 64, 64), MASKV, np.float32)
    for m in range(15):
        dr = 7 - m
        blk = rpb[:, dr + 7][:, dcol]
        Tb[:, m] = np.where(okc[None], blk, np.float32(MASKV))
    sh["na_bias"] = Tb
    mods = [(inp["l0_norm"], inp["l0_w_mod"], inp["l0_b_mod"]), (inp["l1_norm"], inp["l1_w_mod"], inp["l1_b_mod"])]
    for l, (g_, wm_, bm_) in enumerate(mods):
        sh["l%d_w_mod" % l] = wm_
        sh["l%d_bmodT" % l] = _fm(bm_, 24)
        sh["l%d_bmod_gate" % l] = np.ascontiguousarray(bm_[2048:3072].reshape(1, 1024))
        sh["l%d_gT" % l] = _fm(g_, 8)
    per = []
    for b in range(8):
        d = {}
        d["x"] = inp["x"][b]
        d["ctx"] = inp["ctx"][b]
        cv = np.stack([inp["c"][b], inp["c_ctx"]], axis=1)
        d["cvec"] = np.ascontiguousarray(cv.reshape(8, 128, 2).transpose(1, 0, 2))
        per.append(d)
    return sh, per


def build(sh_shapes, per_shapes, stage=99, debug=(), skip=()):
    nc = bass.Bass("TRN2", target_bir_lowering=False)
    IN = {}
    for k, (shape, dt) in list(sh_shapes.items()) + list(per_shapes.items()):
        IN[k] = nc.dram_tensor(k, list(shape), BF16 if dt == "bf16" else F32, kind="ExternalInput").ap()
    out = nc.dram_tensor("out", [TL, D], F32, kind="ExternalOutput").ap()

    def scratch(name, shape, dt):
        kind = "ExternalOutput" if name in debug else "Internal"
        return nc.dram_tensor(name, list(shape), dt, kind=kind).ap()

    SC = {}
    SC["H1"] = scratch("H1", [T, D], F32)
    SC["SZT"] = scratch("SZT", [1024, T], BF16)
    SC["CATT"] = scratch("CATT", [1024, T], BF16)
    SC["QT"] = scratch("QT", [8, 96, T], BF16)
    SC["KT"] = scratch("KT", [8, 96, T], BF16)
    SC["V"] = scratch("V", [T, 512], BF16)
    SC["MQK"] = scratch("MQK", [1024, T], F32)
    SC["GI"] = scratch("GI", [8, T], F32)
    SC["GF"] = scratch("GF", [8, T], F32)
    SC["MV"] = scratch("MV", [T, 512], BF16)
    SC["MO"] = scratch("MO", [T, 512], BF16)
    SC["HM"] = scratch("HM", [2, T, 512], F32)
    SC["LG"] = scratch("LG", [2, 256, T], F32)
    SC["NQ"] = scratch("NQ", [512, T], BF16)
    SC["NK"] = scratch("NK", [512, T], BF16)

    with ExitStack() as es0:
        P = Prog(nc, es0)
        A0 = Alloc(nc, es0)
        ident_bf = A0.sb([128, 128], BF16, "identbf")
        ident_f = A0.sb([128, 128], F32, "identf")
        ones_f = A0.sb([128, 128], F32, "onesf")
        P.dma("sp", ident_bf[:], IN["ident_bf"][:, :], writes=[ident_bf])
        P.dma("sp", ident_f[:], IN["ident_f"][:, :], writes=[ident_f])
        P.op("pool", "memset", [], [ones_f], ones_f[:], 1.0)
        affA = [A0.sb([128, 8, 2], F32, "affA%d" % l) for l in range(2)]
        affB = [A0.sb([128, 8, 2], F32, "affB%d" % l) for l in range(2)]
        gateR = [[A0.sb([128, 1024], F32, "gateR%d_%d" % (l, s)) for s in range(2 if l == 0 else 1)] for l in range(2)]

        with ExitStack() as es:
            A = Alloc(nc, es)
            cv = A.sb([128, 8, 2], F32, "cv")
            sc = A.sb([128, 8, 2], F32, "sc")
            screp = [A.sb([128, 8, 128], F32, "screp%d" % s) for s in range(2)]
            P.dma("sp", cv[:], IN["cvec"][:, :, :], writes=[cv])
            P.op("act", "activation", [cv], [sc], out=sc[:], in_=cv[:], func=AF.Silu)
            for s in range(2):
                for k in range(8):
                    P.op("dve", "tensor_copy", [sc], [screp[s]], out=screp[s][:, k, :],
                         in_=sc[:, k, s:s + 1].to_broadcast([128, 128]))
            wpan = [A.sb([128, 8, 384], F32, "wpan%d" % i) for i in range(2)]
            wgate = [A.sb([128, 512], F32, "wgate%d" % i) for i in range(3)]
            pm = A.ps([128, 24, 2], F32, "pm")
            pg = [A.ps([128, 512], F32, "pg%d" % i) for i in range(2)]
            bmT = A.sb([128, 24], F32, "bmT")
            gT = A.sb([128, 8], F32, "gT")
            modT = A.sb([128, 24, 2], F32, "modT")
            bgrow = A.sb([128, 1024], F32, "bgrow")
            for l in range(2):
                wm = IN["l%d_w_mod" % l]
                P.dma("sp", bmT[:], IN["l%d_bmodT" % l][:, :], writes=[bmT])
                P.dma("sp", gT[:], IN["l%d_gT" % l][:, :], writes=[gT])
                P.dma("sp", bgrow[:], IN["l%d_bmod_gate" % l][0:1, :].to_broadcast([128, 1024]), writes=[bgrow])
                for pn in range(8):
                    wp = wpan[pn % 2]
                    P.dma("sp" if pn % 2 == 0 else "pool", wp[:],
                          wm[:, pn * 384:(pn + 1) * 384].rearrange("(k p) n -> p k n", p=128), writes=[wp])
                    for j in range(3):
                        n = pn * 3 + j
                        for k in range(8):
                            P.op("pe", "matmul", [wp, sc], [pm], pm[:, n, :], wp[:, k, j * 128:(j + 1) * 128],
                                 sc[:, k, :], start=(k == 0), stop=(k == 7))
                P.op("dve", "tensor_tensor", [pm, bmT], [modT], out=modT[:], in0=pm[:],
                     in1=bmT[:].unsqueeze(2).to_broadcast([128, 24, 2]), op=ALU.add)
                P.op("dve", "tensor_scalar", [modT], [affA[l]], out=affA[l][:], in0=modT[:, 8:16, :], scalar1=1.0,
                     scalar2=None, op0=ALU.add)
                P.op("dve", "tensor_tensor", [affA[l], gT], [affA[l]], out=affA[l][:], in0=affA[l][:],
                     in1=gT[:].unsqueeze(2).to_broadcast([128, 8, 2]), op=ALU.mult)
                P.op("dve", "tensor_copy", [modT], [affB[l]], out=affB[l][:], in_=modT[:, 0:8, :])
                for s in range(len(gateR[l])):
                    for hf in range(2):
                        ps = pg[hf]
                        for k in range(8):
                            wg = wgate[(hf * 8 + k) % 3]
                            P.dma("sp" if k % 2 == 0 else "pool", wg[:],
                                  wm[k * 128:(k + 1) * 128, 2048 + hf * 512:2048 + (hf + 1) * 512], writes=[wg])
                            P.op("pe", "matmul", [wg, screp[s]], [ps], ps[:], screp[s][:, k, :], wg[:],
                                 start=(k == 0), stop=(k == 7))
                        P.op("dve", "tensor_tensor", [ps, bgrow], [gateR[l][s]],
                             out=gateR[l][s][:, hf * 512:(hf + 1) * 512], in0=ps[:],
                             in1=bgrow[:, hf * 512:(hf + 1) * 512], op=ALU.add)
            P.barrier()
        if stage <= 0:
            dbg = nc.dram_tensor("dbg_mod", [128, 2, 2, 8, 2], F32, kind="ExternalOutput").ap()
            dbg2 = nc.dram_tensor("dbg_gate", [128, 1024], F32, kind="ExternalOutput").ap()
            for l in range(2):
                P.dma("sp", dbg[:, l, 0], affA[l][:], reads=[affA[l]])
                P.dma("sp", dbg[:, l, 1], affB[l][:], reads=[affB[l]])
            P.dma("sp", dbg2[:, :], gateR[0][1][:], reads=[gateR[0][1]])
            P.barrier()
            return nc

        phase_A(nc, P, IN, SC, 0, affA[0], affB[0], ident_bf, ones_f)
        if stage <= 1:
            return nc
        if 2 not in skip:
            mla_groups = [(0, 256, [0, 1], 0)] + [(256 + 512 * g, 512, list(range(NT)), 0) for g in range(8)]
            attention(nc, P, SC, ones_f, 8, 96, 96.0 ** -0.5, lambda h: SC["QT"][h, :, :], lambda h: SC["KT"][h, :, :],
                      SC["V"], 0, mla_groups)
        if stage <= 2:
            return nc
        if 3 not in skip:
            mlstm_phase(nc, P, IN, SC, ident_bf, ident_f, ones_f)
        if stage <= 3:
            return nc
        combine_phase(nc, P, IN, SC, ident_bf, SC["HM"][0], SC["HM"][1], SC["MO"], "l0_hnorm", 512, GROUPS)
        if stage <= 4:
            return nc
        phase_C(nc, P, IN, SC, 0, gateR[0], out)
        if stage <= 5:
            return nc
        phase_A(nc, P, IN, SC, 1, affA[1], affB[1], ident_bf, ones_f)
        if stage <= 6:
            return nc
        if 7 not in skip:
            gla_phase(nc, P, IN, SC, ident_bf)
            combine_phase(nc, P, IN, SC, ident_bf, SC["HM"][0], SC["HM"][1], None, "l1_gnorm", 0, GROUPS[1:])
        if stage <= 7:
            return nc
        if 8 not in skip:
            na_groups = []
            for g in range(8):
                t_lo, nt_ = (0, 6) if g == 0 else ((26, 6) if g == 7 else (4 * g - 2, 8))
                na_groups.append((256 + 512 * g, 512, [0, 1] + [2 + t_lo + r for r in range(nt_)], g))
            attention(nc, P, SC, ones_f, 8, 64, 64.0 ** -0.5, lambda h: SC["NQ"][h * 64:(h + 1) * 64, :],
                      lambda h: SC["NK"][h * 64:(h + 1) * 64, :], SC["V"], 512, na_groups, bias_fn=na_bias_fn(nc, P, IN, {}))
        if stage <= 8:
            return nc
        phase_C(nc, P, IN, SC, 1, gateR[1], out)
    return nc


def phase_A(nc, P, IN, SC, layer, affA, affB, ident_bf, ones_f):
    NW = 3808 if layer == 0 else 3616
    w_in_d = IN["l%d_w_in" % layer]
    with ExitStack() as es:
        A = Alloc(nc, es)
        w_in = A.sb([128, 8, NW], BF16, "w_in")
        if layer == 0:
            w_uq = A.sb([128, 3, 1536], BF16, "w_uq")
            w_ukv = A.sb([128, 2, 1024], BF16, "w_ukv")
        with ExitStack() as es2:
            A2 = Alloc(nc, es2)
            stg = [A2.sb([128, 8, 512], F32, "stg%d" % i) for i in range(2)]
            i = 0
            for c0 in range(0, NW, 512):
                cw = min(512, NW - c0)
                s = stg[i % 2]
                P.dma("sp" if i % 2 == 0 else "pool", s[:, :, 0:cw],
                      w_in_d[:, c0:c0 + cw].rearrange("(k p) n -> p k n", p=128), writes=[s])
                P.op("dve" if i % 2 == 0 else "act", "tensor_copy" if i % 2 == 0 else "copy", [s], [w_in],
                     out=w_in[:, :, c0:c0 + cw], in_=s[:, :, 0:cw])
                i += 1
            if layer == 0:
                s = stg[i % 2]
                for kk in range(3):
                    s = stg[i % 2]
                    P.dma("sp", s[:, 0:3, :], IN["l0_w_uq"][kk * 128:(kk + 1) * 128, :].rearrange("p (a n) -> p a n", a=3),
                          writes=[s])
                    P.op("dve", "tensor_copy", [s], [w_uq], out=w_uq[:, kk, :].rearrange("p (a n) -> p a n", a=3),
                         in_=s[:, 0:3, :])
                    i += 1
                s = stg[i % 2]
                for kk in range(2):
                    P.dma("sp", s[:, 2 * kk:2 * kk + 2, :],
                          IN["l0_w_ukv"][kk * 128:(kk + 1) * 128, :].rearrange("p (a n) -> p a n", a=2), writes=[s])
                P.op("dve", "tensor_copy", [s], [w_ukv], out=w_ukv[:].rearrange("p k (a n) -> p (k a) n", a=2),
                     in_=s[:, 0:4, :])
                i += 1
            P.barrier()
        if layer == 0:
            qnT = A.sb([128, 3], F32, "qnT")
            kvnT = A.sb([128, 2], F32, "kvnT")
            P.dma("sp", qnT[:], IN["l0_qnT"][:, :], writes=[qnT])
            P.dma("sp", kvnT[:], IN["l0_kvnT"][:, :], writes=[kvnT])
            cqT = A.sb([128, 3, 512], F32, "cqT")
            ckvT = A.sb([128, 2, 512], F32, "ckvT")
            sq = A.sb([128, 3, 512], F32, "sq")
            rstd = A.sb([128, 512], F32, "rstd")
            cqn = A.sb([128, 3, 512], BF16, "cqn")
            ckvn = A.sb([128, 2, 512], BF16, "ckvn")
            rC = A.sb([128, 512], F32, "rC")
            rS = A.sb([128, 512], F32, "rS")
            rt1 = A.sb([128, 512], F32, "rt1")
            rt2 = A.sb([128, 512], F32, "rt2")
            qo = [A.sb([128, 512], BF16, "qo%d" % i) for i in range(2)]
            kro = A.sb([32, 512], BF16, "kro")
        else:
            gaT = [A.sb([16, 512], F32, "gaT%d" % d) for d in range(2)]
            wg = A.sb([16, 2, 256], F32, "wg")
            P.dma("sp", wg[:], IN["l1_w_gate"].rearrange("d r k -> r d k"), writes=[wg])
            bgT = A.sb([128, 2, 2], F32, "bgT")
            nbg = A.sb([128, 2, 2], F32, "nbg")
            P.dma("sp", bgT[:], IN["l1_bgT"][:, :, :], writes=[bgT])
            P.op("dve", "tensor_scalar", [bgT], [nbg], out=nbg[:], in0=bgT[:], scalar1=-1.0, scalar2=None, op0=ALU.mult)
            one1 = A.sb([128, 1], F32, "one1a")
            P.op("pool", "memset", [], [one1], one1[:], 1.0)
            lge = A.sb([128, 512], F32, "lge")
            lgo = [A.sb([128, 512], F32, "lgo%d" % i) for i in range(2)]
        hb = [A.sb([128, 1024], F32, "hb%d" % i) for i in range(3)]
        junk = A.sb([128, 1024], F32, "junk")
        st = [A.sb([128, 4], F32, "st%d" % i) for i in range(2)]
        xn = [A.sb([128, 1024], BF16, "xn%d" % i) for i in range(4)]
        epsT = A.sb([128, 1], F32, "epsT")
        P.op("pool", "memset", [], [epsT], epsT[:], EPS)
        uT = [A.sb([128, 8, 512], BF16, "uT%d" % i) for i in range(2)]
        fo_bf = [A.sb([128, 512], BF16, "fobf%d" % i) for i in range(4)]
        fo_f = [A.sb([128, 512], F32, "fof%d" % i) for i in range(3)]
        tp = [A.ps([128, 512], BF16, "tp%d" % i) for i in range(2)]
        acc = [A.ps([128, 512], F32, "acc%d" % i) for i in range(5)]
        cnt = {"acc": 0, "fobf": 0, "fof": 0, "ev": 0, "q": 0, "hb": 0, "xn": 0, "tp": 0}

        def nxt(name, lst):
            r = lst[cnt[name] % len(lst)]
            cnt[name] += 1
            return r

        def evac_engine():
            cnt["ev"] += 1
            return "dve" if cnt["ev"] % 2 == 0 else "act"

        def copy_op(eng, src_t, src_ap, dst_t, dst_ap):
            if eng == "act":
                P.op("act", "copy", [src_t], [dst_t], out=dst_ap, in_=src_ap)
            else:
                P.op(eng, "tensor_copy", [src_t], [dst_t], out=dst_ap, in_=src_ap)

        def stq():
            cnt["q"] += 1
            return "pool" if cnt["q"] % 2 == 0 else "sp"

        for gi, (t0, n) in enumerate(GROUPS):
            ntl = n // 128
            s = 1 if gi == 0 else 0
            u = uT[gi % 2]
            sta = st[gi % 2]
            hs = []
            for ti in range(ntl):
                h = nxt("hb", hb)
                tok = t0 + ti * 128
                if layer == 0:
                    src = IN["ctx"][tok:tok + 128, :] if gi == 0 else IN["x"][tok - TC:tok - TC + 128, :]
                else:
                    src = SC["H1"][tok:tok + 128, :]
                P.dma("sp", h[:], src, writes=[h])
                P.op("act", "activation", [h], [junk, sta], out=junk[:], in_=h[:], func=AF.Square,
                     accum_out=sta[:, ti:ti + 1])
                P.op("act", "activation", [sta], [sta], out=sta[:, ti:ti + 1], in_=sta[:, ti:ti + 1], func=AF.Sqrt,
                     scale=1.0 / D, bias=epsT[:, 0:1])
                P.op("dve", "reciprocal", [sta], [sta], out=sta[:, ti:ti + 1], in_=sta[:, ti:ti + 1])
                x_ = xn[ti]
                P.op("dve", "tensor_scalar", [h, sta], [x_], out=x_[:], in0=h[:], scalar1=sta[:, ti:ti + 1],
                     scalar2=None, op0=ALU.mult)
            for j in range(8):
                tpp = nxt("tp", tp)
                for ti in range(ntl):
                    P.op("pe", "transpose", [xn[ti], ident_bf], [tpp], tpp[:, ti * 128:(ti + 1) * 128],
                         xn[ti][:, j * 128:(j + 1) * 128], ident_bf[:])
                P.op("dve", "tensor_scalar", [tpp, affA, affB], [u], out=u[:, j, 0:n],
                     in0=tpp[:, 0:n], scalar1=affA[:, j, s:s + 1], scalar2=affB[:, j, s:s + 1], op0=ALU.mult,
                     op1=ALU.add)

            def fm_proj(c0, ncol):
                ps = nxt("acc", acc)
                for k in range(8):
                    P.op("pe", "matmul", [w_in, u], [ps], ps[0:ncol, 0:n], w_in[:, k, c0:c0 + ncol], u[:, k, 0:n],
                         start=(k == 0), stop=(k == 7))
                return ps

            def store_fm(ps, ncol, dst, dt, func=None, eng=None):
                o = nxt("fobf", fo_bf) if dt == BF16 else nxt("fof", fo_f)
                if func is not None:
                    P.op("act", "activation", [ps], [o], out=o[0:ncol, 0:n], in_=ps[0:ncol, 0:n], func=func)
                else:
                    copy_op(eng or evac_engine(), ps, ps[0:ncol, 0:n], o, o[0:ncol, 0:n])
                P.dma(stq(), dst, o[0:ncol, 0:n], reads=[o])

            tsl = slice(t0, t0 + n)
            if layer == 0:
                for j in range(3):
                    ps = fm_proj(j * 128, 128)
                    copy_op(evac_engine(), ps, ps[:, 0:n], cqT, cqT[:, j, 0:n])
                for j in range(2):
                    ps = fm_proj(384 + j * 128, 128)
                    copy_op(evac_engine(), ps, ps[:, 0:n], ckvT, ckvT[:, j, 0:n])
                for (src_t, nk, nrm, dst_t, dim) in ((cqT, 3, qnT, cqn, 384.0), (ckvT, 2, kvnT, ckvn, 256.0)):
                    P.op("act", "activation", [src_t], [sq], out=sq[:, 0:nk, 0:n], in_=src_t[:, 0:nk, 0:n], func=AF.Square)
                    ps = nxt("acc", acc)
                    for k in range(nk):
                        P.op("pe", "matmul", [ones_f, sq], [ps], ps[:, 0:n], ones_f[:], sq[:, k, 0:n], start=(k == 0),
                             stop=(k == nk - 1))
                    P.op("act", "activation", [ps], [rstd], out=rstd[:, 0:n], in_=ps[:, 0:n], func=AF.Sqrt,
                         scale=1.0 / dim, bias=epsT[:, 0:1])
                    P.op("dve", "reciprocal", [rstd], [rstd], out=rstd[:, 0:n], in_=rstd[:, 0:n])
                    for k in range(nk):
                        P.op("dve", "scalar_tensor_tensor", [src_t, nrm, rstd], [dst_t], out=dst_t[:, k, 0:n],
                             in0=src_t[:, k, 0:n], scalar=nrm[:, k:k + 1], in1=rstd[:, 0:n], op0=ALU.mult, op1=ALU.mult)
                rot = gi > 0
                if rot:
                    P.dma("sp", rC[:, 0:n], IN["ropeC"][:, t0 - TC:t0 - TC + n], writes=[rC])
                    P.dma("sp", rS[:, 0:n], IN["ropeS"][:, t0 - TC:t0 - TC + n], writes=[rS])
                for hh in range(8):
                    ps = nxt("acc", acc)
                    for k in range(3):
                        P.op("pe", "matmul", [w_uq, cqn], [ps], ps[0:96, 0:n], w_uq[:, k, hh * 192:hh * 192 + 96],
                             cqn[:, k, 0:n], start=(k == 0), stop=(k == 2))
                    o = nxt("fobf", fo_bf)
                    if rot:
                        ps2 = nxt("acc", acc)
                        for k in range(3):
                            P.op("pe", "matmul", [w_uq, cqn], [ps2], ps2[0:96, 0:n],
                                 w_uq[:, k, hh * 192 + 96:hh * 192 + 192], cqn[:, k, 0:n], start=(k == 0), stop=(k == 2))
                        copy_op("act", ps, ps[0:64, 0:n], o, o[0:64, 0:n])
                        P.op("dve", "tensor_tensor", [ps, rC], [rt1], out=rt1[64:96, 0:n], in0=ps[64:96, 0:n],
                             in1=rC[64:96, 0:n], op=ALU.mult)
                        P.op("dve", "tensor_tensor", [ps2, rS], [rt2], out=rt2[64:96, 0:n], in0=ps2[64:96, 0:n],
                             in1=rS[64:96, 0:n], op=ALU.mult)
                        P.op("pool", "tensor_tensor", [rt1, rt2], [o], out=o[64:96, 0:n], in0=rt1[64:96, 0:n],
                             in1=rt2[64:96, 0:n], op=ALU.add)
                    else:
                        copy_op(evac_engine(), ps, ps[0:96, 0:n], o, o[0:96, 0:n])
                    P.dma(stq(), SC["QT"][hh, :, tsl], o[0:96, 0:n], reads=[o])
                for c in range(4):
                    ps = nxt("acc", acc)
                    for k in range(2):
                        P.op("pe", "matmul", [w_ukv, ckvn], [ps], ps[:, 0:n], w_ukv[:, k, c * 128:(c + 1) * 128],
                             ckvn[:, k, 0:n], start=(k == 0), stop=(k == 1))
                    o = nxt("fobf", fo_bf)
                    copy_op(evac_engine(), ps, ps[:, 0:n], o, o[:, 0:n])
                    for hh in range(2):
                        P.dma(stq(), SC["KT"][c * 2 + hh, 0:64, tsl], o[hh * 64:(hh + 1) * 64, 0:n], reads=[o])
                for ti in range(ntl):
                    ps = nxt("acc", acc)
                    for k in range(2):
                        P.op("pe", "matmul", [w_ukv, ckvn], [ps], ps[:, :], ckvn[:, k, ti * 128:(ti + 1) * 128],
                             w_ukv[:, k, 512:1024], start=(k == 0), stop=(k == 1))
                    o = nxt("fobf", fo_bf)
                    copy_op(evac_engine(), ps, ps[:, :], o, o[:, :])
                    P.dma(stq(), SC["V"][t0 + ti * 128:t0 + (ti + 1) * 128, :], o[:, :], reads=[o])
                ps = fm_proj(640, 32)
                if rot:
                    ps2 = fm_proj(3760, 32)
                    P.op("dve", "tensor_tensor", [ps, rC], [rt1], out=rt1[0:32, 0:n], in0=ps[0:32, 0:n], in1=rC[0:32, 0:n],
                         op=ALU.mult)
                    P.op("dve", "tensor_tensor", [ps2, rS], [rt2], out=rt2[0:32, 0:n], in0=ps2[0:32, 0:n],
                         in1=rS[0:32, 0:n], op=ALU.mult)
                    P.op("pool", "tensor_tensor", [rt1, rt2], [kro], out=kro[0:32, 0:n], in0=rt1[0:32, 0:n],
                         in1=rt2[0:32, 0:n], op=ALU.add)
                else:
                    copy_op("dve", ps, ps[0:32, 0:n], kro, kro[0:32, 0:n])
                for hh in range(8):
                    P.dma(stq(), SC["KT"][hh, 64:96, tsl], kro[0:32, 0:n], reads=[kro])
                for c in range(8):
                    ps = fm_proj(672 + c * 128, 128)
                    store_fm(ps, 128, SC["MQK"][c * 128:(c + 1) * 128, tsl], F32)
                ps = fm_proj(3792, 8)
                store_fm(ps, 8, SC["GI"][:, tsl], F32)
                ps = fm_proj(3800, 8)
                store_fm(ps, 8, SC["GF"][:, tsl], F32)
                for c in range(8):
                    ps = fm_proj(2736 + c * 128, 128)
                    store_fm(ps, 128, SC["SZT"][c * 128:(c + 1) * 128, tsl], BF16, func=AF.Silu)
                tm_specs = [(1696, SC["MV"], None), (2208, SC["MO"], AF.Sigmoid)]
            else:
                for c in range(4):
                    ps = fm_proj(c * 128, 128)
                    store_fm(ps, 128, SC["MQK"][c * 128:(c + 1) * 128, tsl], F32)
                for d in range(2):
                    ps = fm_proj(1024 + 16 * d, 16)
                    copy_op("dve", ps, ps[0:16, 0:n], gaT[d], gaT[d][0:16, 0:n])
                for d in range(2):
                    for c2 in range(2):
                        ps = nxt("acc", acc)
                        P.op("pe", "matmul", [wg, gaT[d]], [ps], ps[:, 0:n], wg[0:16, d, c2 * 128:(c2 + 1) * 128],
                             gaT[d][0:16, 0:n], start=True, stop=True)
                        P.op("act", "activation", [ps, nbg], [lge], out=lge[:, 0:n], in_=ps[:, 0:n], func=AF.Exp, scale=-1.0,
                             bias=nbg[:, d, c2:c2 + 1])
                        P.op("act", "activation", [lge, one1], [lge], out=lge[:, 0:n], in_=lge[:, 0:n], func=AF.Ln,
                             bias=one1[:, 0:1])
                        o = lgo[(d * 2 + c2) % 2]
                        P.op("dve", "tensor_scalar", [lge], [o], out=o[:, 0:n], in0=lge[:, 0:n], scalar1=-1.0 / 16.0,
                             scalar2=None, op0=ALU.mult)
                        P.dma(stq(), SC["LG"][d, c2 * 128:(c2 + 1) * 128, tsl], o[:, 0:n], reads=[o])
                for c in range(4):
                    ps = fm_proj(1056 + c * 128, 128)
                    store_fm(ps, 128, SC["NQ"][c * 128:(c + 1) * 128, tsl], BF16)
                for c in range(4):
                    ps = fm_proj(1568 + c * 128, 128)
                    store_fm(ps, 128, SC["NK"][c * 128:(c + 1) * 128, tsl], BF16)
                for c in range(8):
                    ps = fm_proj(2592 + c * 128, 128)
                    store_fm(ps, 128, SC["SZT"][c * 128:(c + 1) * 128, tsl], BF16, func=AF.Silu)
                tm_specs = [(512, SC["MV"], None), (2080, SC["V"], None)]
            for (c0, dst, func) in tm_specs:
                for ti in range(ntl):
                    ps = nxt("acc", acc)
                    for k in range(8):
                        P.op("pe", "matmul", [w_in, u], [ps], ps[:, :], u[:, k, ti * 128:(ti + 1) * 128],
                             w_in[:, k, c0:c0 + 512], start=(k == 0), stop=(k == 7))
                    o = nxt("fobf", fo_bf)
                    if func is not None:
                        P.op("act", "activation", [ps], [o], out=o[:, :], in_=ps[:, :], func=func)
                    else:
                        copy_op(evac_engine(), ps, ps[:, :], o, o[:, :])
                    P.dma(stq(), dst[t0 + ti * 128:t0 + (ti + 1) * 128, :], o[:, :], reads=[o])
        P.barrier()


def attention(nc, P, SC, ones_f, heads, dq, scale, load_q, load_k, Vd, cat_row0, groups, bias_fn=None):
    with ExitStack() as es:
        A = Alloc(nc, es)
        V = A.sb([128, NT, heads, 65], BF16, "Vall")
        P.op("pool", "memset", [], [V], V[:, :, :, 64:65], 1.0)
        for half in range(2):
            tl = slice(half * 17, (half + 1) * 17)
            for hh in range(heads):
                P.dma("sp" if hh % 2 == 0 else "pool", V[:, tl, hh, 0:64],
                      Vd[half * 17 * 128:(half + 1) * 17 * 128, hh * 64:(hh + 1) * 64].rearrange("(t p) d -> p t d", p=128),
                      writes=[V])
        kT = [A.sb([128, T], BF16, "kT%d" % i) for i in range(2)]
        qT = [A.sb([128, T], BF16, "qT%d" % i) for i in range(2)]
        pt = [A.sb([128, 512], BF16, "pt%d" % i) for i in range(3)]
        sb_t = [A.sb([128, 512], F32, "sbt%d" % i) for i in range(2)] if bias_fn is not None else None
        rden = A.sb([128, 512], F32, "rden")
        bcs = A.sb([128, 512], F32, "bcs")
        szt = [A.sb([64, 512], BF16, "szt%d" % i) for i in range(2)]
        tmp = A.sb([64, 512], F32, "atmp")
        ao = [A.sb([64, 512], BF16, "ao%d" % i) for i in range(2)]
        Sps = [A.ps([128, 512], F32, "Sps%d" % i) for i in range(3)]
        Ops = [A.ps([128, 512], F32, "Ops%d" % i) for i in range(2)]
        Bps = A.ps([128, 512], F32, "Bps")
        it = 0
        gi_ = 0
        for h in range(heads):
            k_ = kT[h % 2]
            q_ = qT[h % 2]
            P.dma("sp", k_[0:dq, :], load_k(h), writes=[k_])
            P.dma("pool", q_[0:dq, :], load_q(h), writes=[q_])
            bias_tiles = bias_fn(h, A if h == 0 else None) if bias_fn is not None else None
            for (q0, n, tiles, gkey) in groups:
                O = Ops[gi_ % 2]
                sz = szt[gi_ % 2]
                a_ = ao[gi_ % 2]
                gi_ += 1
                r0 = cat_row0 + h * 64
                P.dma("sp", sz[:, 0:n], SC["SZT"][r0:r0 + 64, q0:q0 + n], writes=[sz])
                for j, kt in enumerate(tiles):
                    S = Sps[it % 3]
                    p_ = pt[it % 3]
                    P.op("pe", "matmul", [k_, q_], [S], S[:, 0:n], k_[0:dq, kt * 128:(kt + 1) * 128], q_[0:dq, q0:q0 + n],
                         start=True, stop=True)
                    bt = bias_tiles.get((gkey, kt)) if bias_tiles is not None else None
                    if bt is not None:
                        sb = sb_t[it % 2]
                        P.op("dve", "scalar_tensor_tensor", [S, bt], [sb], out=sb[:, 0:n], in0=S[:, 0:n], scalar=scale,
                             in1=bt[:, 0:n], op0=ALU.mult, op1=ALU.add)
                        P.op("act", "activation", [sb], [p_], out=p_[:, 0:n], in_=sb[:, 0:n], func=AF.Exp)
                    else:
                        P.op("act", "activation", [S], [p_], out=p_[:, 0:n], in_=S[:, 0:n], func=AF.Exp, scale=scale)
                    P.op("pe", "matmul", [V, p_], [O], O[0:65, 0:n], V[:, kt, h, :], p_[:, 0:n], start=(j == 0),
                         stop=(j == len(tiles) - 1))
                    it += 1
                P.op("dve", "reciprocal", [O], [rden], out=rden[64:65, 0:n], in_=O[64:65, 0:n])
                P.op("pe", "matmul", [ones_f, rden], [Bps], Bps[0:64, 0:n], ones_f[64:65, 0:64], rden[64:65, 0:n],
                     start=True, stop=True)
                P.op("act", "copy", [Bps], [bcs], out=bcs[0:64, 0:n], in_=Bps[0:64, 0:n])
                P.op("dve", "tensor_tensor", [O, bcs], [tmp], out=tmp[:, 0:n], in0=O[0:64, 0:n], in1=bcs[0:64, 0:n],
                     op=ALU.mult)
                P.op("pool", "tensor_tensor", [tmp, sz], [a_], out=a_[:, 0:n], in0=tmp[:, 0:n], in1=sz[:, 0:n], op=ALU.mult)
                P.dma("pool", SC["CATT"][r0:r0 + 64, q0:q0 + n], a_[:, 0:n], reads=[a_])
        P.barrier()


def mlstm_phase(nc, P, IN, SC, ident_bf, ident_f, ones_f):
    NB = NT
    NCH = T // 64
    with ExitStack() as es:
        A = Alloc(nc, es)
        esT = A.sb([128, NB, 64], F32, "esT")
        fT = A.sb([128, NB, 64], F32, "fT")
        decbc = A.sb([128, 8, NCH], F32, "decbc")
        mask = [A.sb([128, 64], F32, "mask%d" % d) for d in range(2)]
        for d in range(2):
            P.op("pool", "memset", [], [mask[d]], mask[d][:], 1.0)
            for half in range(2):
                pr = slice(half * 64, half * 64 + 64)
                P.op("pool", "affine_select", [mask[d]], [mask[d]], out=mask[d][pr, :], in_=mask[d][pr, :],
                     pattern=[[1 if d == 0 else -1, 64]], compare_op=ALU.is_ge, fill=0.0, base=0,
                     channel_multiplier=-1 if d == 0 else 1)
        with ExitStack() as es2:
            A2 = Alloc(nc, es2)
            X1 = A2.sb([64, T], F32, "X1")
            X2 = A2.sb([64, T], F32, "X2")
            X3 = A2.sb([64, T], F32, "X3")
            X4 = A2.sb([64, T], F32, "X4")
            gb = A2.sb([64, 2], F32, "gb")
            nbf = A2.sb([64, 1], F32, "nbf")
            one1 = A2.sb([64, 1], F32, "one1")
            dec = A2.sb([64, NCH], F32, "dec")
            aprev = A2.sb([64, NCH], F32, "aprev")
            sel = A2.sb([64, 128], F32, "sel")
            pst = [A2.ps([128, 8, 64], F32, "pst%d" % i) for i in range(2)]
            psd = A2.ps([128, NCH], F32, "psd")
            P.op("pool", "memset", [], [X1], X1[:], 0.0)
            P.op("pool", "memset", [], [X3], X3[:], 0.0)
            P.op("pool", "memset", [], [one1], one1[:], 1.0)
            P.dma("sp", gb[:], IN["l0_gb2"][:, :], writes=[gb])
            for d in range(2):
                P.dma("sp", X1[d * 32:d * 32 + 4, :], SC["GF"][d * 4:d * 4 + 4, :], writes=[X1])
                P.dma("pool", X3[d * 32:d * 32 + 4, :], SC["GI"][d * 4:d * 4 + 4, :], writes=[X3])
            P.op("dve", "tensor_scalar", [gb], [nbf], out=nbf[:], in0=gb[:, 1:2], scalar1=-1.0, scalar2=None, op0=ALU.mult)
            P.op("act", "activation", [X1, nbf], [X1], out=X1[:], in_=X1[:], func=AF.Exp, scale=-1.0, bias=nbf[:, 0:1])
            P.op("act", "activation", [X1, one1], [X1], out=X1[:], in_=X1[:], func=AF.Ln, bias=one1[:, 0:1])

            def seg_views(tile_, prng, d):
                if d == 0:
                    return [tile_[prng, 0:T]]
                return [tile_[prng, 0:TC][:, ::-1], tile_[prng, TC:T][:, ::-1]]

            def scan(dst, src, op0, d):
                prng = slice(d * 32, d * 32 + 32)
                dv = seg_views(dst, prng, d)
                sv = seg_views(src, prng, d)
                for i in range(len(dv)):
                    init = 0.0 if i == 0 else dst[prng, 0:1]
                    P.op("dve", "tensor_tensor_scan", [src, dst], [dst], out=dv[i], data0=sv[i], data1=sv[i],
                         initial=init, op0=op0, op1=ALU.bypass)

            for d in range(2):
                scan(X2, X1, ALU.add, d)
            P.op("dve", "scalar_tensor_tensor", [X3, gb, X2], [X3], out=X3[:], in0=X3[:], scalar=gb[:, 0:1], in1=X2[:],
                 op0=ALU.add, op1=ALU.add)
            for d in range(2):
                scan(X1, X3, ALU.max, d)
            for d in range(2):
                prng = slice(d * 32, d * 32 + 32)
                jj = 63 if d == 0 else 0
                P.op("dve", "tensor_copy", [X1], [X4], out=X4[prng, :].rearrange("p (c j) -> p c j", j=64),
                     in_=X1[prng, :].rearrange("p (c j) -> p c j", j=64)[:, :, jj:jj + 1].to_broadcast([32, NCH, 64]))
            aend = X4[:, :].rearrange("p (c j) -> p c j", j=64)[:, :, 0]
            P.op("pool", "memset", [], [aprev], aprev[:], 0.0)
            P.op("dve", "tensor_copy", [X4], [aprev], out=aprev[0:32, 1:NCH], in_=aend[0:32, 0:NCH - 1])
            P.op("dve", "tensor_copy", [X4], [aprev], out=aprev[32:64, 0:3], in_=aend[32:64, 1:4])
            P.op("dve", "tensor_copy", [X4], [aprev], out=aprev[32:64, 4:NCH - 1], in_=aend[32:64, 5:NCH])
            P.op("dve", "tensor_copy", [X4], [aprev], out=aprev[32:64, NCH - 1:NCH], in_=aend[32:64, 0:1])
            P.op("dve", "tensor_tensor", [aprev, X4], [dec], out=dec[:], in0=aprev[:], in1=aend, op=ALU.subtract)
            P.op("act", "activation", [dec], [dec], out=dec[:], in_=dec[:], func=AF.Exp)
            P.op("dve", "tensor_tensor", [X3, X4], [X3], out=X3[:], in0=X3[:], in1=X4[:], op=ALU.subtract)
            P.op("act", "activation", [X3], [X3], out=X3[:], in_=X3[:], func=AF.Exp)
            P.op("dve", "tensor_tensor", [X2, X4], [X2], out=X2[:], in0=X2[:], in1=X4[:], op=ALU.subtract)
            P.op("act", "activation", [X2], [X2], out=X2[:], in_=X2[:], func=AF.Exp)
            for (srcX, dstT) in ((X3, esT), (X2, fT)):
                for b0 in range(0, NB, 8):
                    nb = min(8, NB - b0)
                    ps = pst[(b0 // 8) % 2]
                    for bb in range(nb):
                        P.op("pe", "transpose", [srcX, ident_f], [ps], ps[:, bb, :], srcX[:, (b0 + bb) * 128:(b0 + bb + 1) * 128],
                             ident_f[0:64, 0:64])
                    P.op("act", "copy", [ps], [dstT], out=dstT[:, b0:b0 + nb, :], in_=ps[:, 0:nb, :])
            for idx in range(8):
                r = (idx // 4) * 32 + idx % 4
                P.op("dve", "tensor_copy", [ident_f], [sel], out=sel[:], in_=ident_f[0:64, r:r + 1].to_broadcast([64, 128]))
                P.op("pe", "matmul", [sel, dec], [psd], psd[:, :], sel[:, :], dec[:, :], start=True, stop=True)
                P.op("act", "copy", [psd], [decbc], out=decbc[:, idx, :], in_=psd[:, :])
            P.barrier()
        xraw = A.sb([128, T], F32, "xraw")
        ycv = A.sb([128, T], F32, "ycv")
        cvw = A.sb([128, 8, 4], F32, "cvw")
        P.dma("sp", cvw[:], IN["l0_convT"][:, :, :], writes=[cvw])
        qT = A.sb([128, T], BF16, "mqT")
        kT = A.sb([128, T], BF16, "mkT")
        kTok = A.sb([128, NB, 128], BF16, "kTok")
        vtok = A.sb([128, NB, 128], BF16, "vtok")
        vpp = [A.sb([128, NB, 129], BF16, "vpp%d" % d) for d in range(2)]
        SmT = [A.sb([128, NB, 64], BF16, "SmT%d" % d) for d in range(2)]
        hbuf = [A.sb([128, NB, 128], F32, "hbuf%d" % d) for d in range(2)]
        Cst = [[A.sb([128, 129], F32, "C%d_%d" % (d, i)) for i in range(2)] for d in range(2)]
        Cdb = [[A.sb([128, 129], BF16, "Cdb%d_%d" % (d, i)) for i in range(2)] for d in range(2)]
        d1 = [[A.sb([128, 1], F32, "d1_%d_%d" % (d, i)) for i in range(2)] for d in range(2)]
        rd = [[A.sb([128, 1], F32, "rd_%d_%d" % (d, i)) for i in range(2)] for d in range(2)]
        ptr = [A.ps([128, 512], BF16, "ptr%d" % i) for i in range(2)]
        pS = [A.ps([128, 8, 64], F32, "pS%d" % i) for i in range(2)]
        pU = [A.ps([128, 129], F32, "pU%d" % i) for i in range(2)]
        pN = [A.ps([128, 129], F32, "pN%d" % i) for i in range(2)]
        order = [list(range(NCH)), [3, 2, 1, 0] + list(range(NCH - 1, 3, -1))]
        zero1 = A.sb([128, 1], F32, "zero1")
        P.op("pool", "memset", [], [zero1], zero1[:], 0.0)
        for h in range(4):
            for which in range(2):
                ch = which * 4 + h
                P.dma("sp" if which == 0 else "pool", xraw[:], SC["MQK"][ch * 128:(ch + 1) * 128, :], writes=[xraw])
                for (s0, s1) in ((0, TC), (TC, T)):
                    P.op("dve", "tensor_scalar", [xraw, cvw], [ycv], out=ycv[:, s0:s1], in0=xraw[:, s0:s1],
                         scalar1=cvw[:, ch, 1:2], scalar2=cvw[:, ch, 3:4], op0=ALU.mult, op1=ALU.add)
                    P.op("dve", "scalar_tensor_tensor", [xraw, cvw, ycv], [ycv], out=ycv[:, s0 + 1:s1], in0=xraw[:, s0:s1 - 1],
                         scalar=cvw[:, ch, 0:1], in1=ycv[:, s0 + 1:s1], op0=ALU.mult, op1=ALU.add)
                    P.op("dve", "scalar_tensor_tensor", [xraw, cvw, ycv], [ycv], out=ycv[:, s0:s1 - 1], in0=xraw[:, s0 + 1:s1],
                         scalar=cvw[:, ch, 2:3], in1=ycv[:, s0:s1 - 1], op0=ALU.mult, op1=ALU.add)
                if which == 0:
                    P.op("act", "activation", [ycv], [ycv], out=ycv[:], in_=ycv[:], func=AF.Silu)
                    P.op("dve", "tensor_scalar", [ycv], [qT], out=qT[:], in0=ycv[:], scalar1=128.0 ** -0.5, scalar2=None,
                         op0=ALU.mult)
                else:
                    P.op("act", "activation", [ycv], [kT], out=kT[:], in_=ycv[:], func=AF.Silu)
            for b0 in range(0, NB, 4):
                nb = min(4, NB - b0)
                ps = ptr[(b0 // 4) % 2]
                for bb in range(nb):
                    P.op("pe", "transpose", [kT, ident_bf], [ps], ps[:, bb * 128:(bb + 1) * 128],
                         kT[:, (b0 + bb) * 128:(b0 + bb + 1) * 128], ident_bf[:])
                P.op("act", "copy", [ps], [kTok], out=kTok[:, b0:b0 + nb, :],
                     in_=ps[:, 0:nb * 128].rearrange("p (b j) -> p b j", j=128))
            P.dma("sp", vtok[:], SC["MV"][:, h * 128:(h + 1) * 128].rearrange("(b p) j -> p b j", p=128), writes=[vtok])
            for d in range(2):
                col = d * 32 + h
                P.op("pool" if d == 0 else "dve", "tensor_tensor", [vtok, esT], [vpp[d]], out=vpp[d][:, :, 0:128], in0=vtok[:],
                     in1=esT[:, :, col:col + 1].to_broadcast([128, NB, 128]), op=ALU.mult)
                P.op("dve", "tensor_copy", [esT], [vpp[d]], out=vpp[d][:, :, 128:129], in_=esT[:, :, col:col + 1])
            for b0 in range(0, NB, 8):
                nb = min(8, NB - b0)
                ps = pS[(b0 // 8) % 2]
                for bb in range(nb):
                    for half in range(2):
                        c = (b0 + bb) * 2 + half
                        pr = slice(half * 64, half * 64 + 64)
                        P.op("pe", "matmul", [kT, qT], [ps], ps[pr, bb, :], kT[:, c * 64:(c + 1) * 64], qT[:, c * 64:(c + 1) * 64],
                             start=True, stop=True)
                for d in range(2):
                    P.op("dve", "tensor_tensor", [ps, mask[d]], [SmT[d]], out=SmT[d][:, b0:b0 + nb, :], in0=ps[:, 0:nb, :],
                         in1=mask[d][:].unsqueeze(1).to_broadcast([128, nb, 64]), op=ALU.mult)
            for d in range(2):
                P.op("pool", "memset", [], [Cst[d][0]], Cst[d][0][:], 0.0)
            for i in range(NCH):
                for d in range(2):
                    c = order[d][i]
                    b, half = c // 2, c % 2
                    pr = slice(half * 64, half * 64 + 64)
                    idx = d * 4 + h
                    col = d * 32 + h
                    Cold = Cst[d][i % 2]
                    Cnew = Cst[d][(i + 1) % 2]
                    cdb = Cdb[d][i % 2]
                    U = pU[d]
                    N = pN[d]
                    P.op("pool", "tensor_scalar", [Cold, decbc], [cdb], out=cdb[:], in0=Cold[:], scalar1=decbc[:, idx, c:c + 1],
                         scalar2=None, op0=ALU.mult)
                    P.op("pe", "matmul", [kTok, vpp[d]], [U], U[:, :], kTok[pr, b, :], vpp[d][pr, b, :], start=True, stop=True)
                    P.op("pe", "matmul", [SmT[d], vpp[d]], [N], N[pr, :], SmT[d][pr, b, :], vpp[d][pr, b, :], start=True,
                         stop=False)
                    P.op("pe", "matmul", [qT, cdb], [N], N[pr, :], qT[:, c * 64:(c + 1) * 64], cdb[:], start=False, stop=True)
                    P.op("dve", "scalar_tensor_tensor", [Cold, decbc, U], [Cnew], out=Cnew[:], in0=Cold[:],
                         scalar=decbc[:, idx, c:c + 1], in1=U[:, :], op0=ALU.mult, op1=ALU.add)
                    d1_ = d1[d][i % 2]
                    rd_ = rd[d][i % 2]
                    P.op("act", "activation", [N], [d1_], out=d1_[pr, :], in_=N[pr, 128:129], func=AF.Abs)
                    P.op("dve", "tensor_tensor", [d1_, fT], [d1_], out=d1_[pr, :], in0=d1_[pr, :], in1=fT[pr, b, col:col + 1],
                         op=ALU.max)
                    P.op("dve", "reciprocal", [d1_], [rd_], out=rd_[pr, :], in_=d1_[pr, :])
                    P.op("act", "activation", [N, rd_], [hbuf[d]], out=hbuf[d][pr, b, :], in_=N[pr, 0:128], func=AF.Copy,
                         scale=rd_[pr, 0:1])
            for d in range(2):
                P.dma("sp" if d == 0 else "pool", SC["HM"][d, :, h * 128:(h + 1) * 128].rearrange("(b p) j -> p b j", p=128),
                      hbuf[d][:], reads=[hbuf[d]])
        P.barrier()


def combine_phase(nc, P, IN, SC, ident_bf, src0, src1, mul, norm_name, cat_row0, groups):
    with ExitStack() as es:
        A = Alloc(nc, es)
        nrow = A.sb([128, 512], F32, "nrow")
        P.dma("sp", nrow[:], IN[norm_name][0:1, :].to_broadcast([128, 512]), writes=[nrow])
        epsT = A.sb([128, 1], F32, "epsTc")
        P.op("pool", "memset", [], [epsT], epsT[:], EPS)
        a_ = [A.sb([128, 512], F32, "cA%d" % i) for i in range(2)]
        b_ = [A.sb([128, 512], F32, "cB%d" % i) for i in range(2)]
        m_ = [A.sb([128, 512], BF16, "cM%d" % i) for i in range(2)]
        junk = A.sb([128, 128], F32, "cjunk")
        ss = [A.sb([128, 4], F32, "css%d" % i) for i in range(2)]
        hn = [A.sb([128, 512], F32, "chn%d" % i) for i in range(2)]
        hb = [A.sb([128, 512], BF16, "chb%d" % i) for i in range(2)]
        sz = [A.sb([128, 512], BF16, "csz%d" % i) for i in range(2)]
        oo = [A.sb([128, 512], BF16, "coo%d" % i) for i in range(2)]
        tp = [A.ps([128, 512], BF16, "ctp%d" % i) for i in range(4)]
        it = 0
        for (t0, n) in groups:
            ntl = n // 128
            for ti in range(ntl):
                tok = t0 + ti * 128
                a, b, m, s_, h_, hb_ = a_[it % 2], b_[it % 2], m_[it % 2], ss[it % 2], hn[it % 2], hb[it % 2]
                it += 1
                P.dma("sp", a[:], src0[tok:tok + 128, :], writes=[a])
                P.dma("pool", b[:], src1[tok:tok + 128, :], writes=[b])
                P.op("dve", "tensor_tensor", [a, b], [a], out=a[:], in0=a[:], in1=b[:], op=ALU.add)
                if mul is not None:
                    P.dma("sp", m[:], mul[tok:tok + 128, :], writes=[m])
                    P.op("pool", "tensor_tensor", [a, m], [a], out=a[:], in0=a[:], in1=m[:], op=ALU.mult)
                for hh in range(4):
                    P.op("act", "activation", [a], [junk, s_], out=junk[:], in_=a[:, hh * 128:(hh + 1) * 128], func=AF.Square,
                         accum_out=s_[:, hh:hh + 1])
                P.op("act", "activation", [s_, epsT], [s_], out=s_[:], in_=s_[:], func=AF.Sqrt, scale=1.0 / 128, bias=epsT[:, 0:1])
                P.op("dve", "reciprocal", [s_], [s_], out=s_[:], in_=s_[:])
                for hh in range(4):
                    P.op("act", "activation", [a, s_], [h_], out=h_[:, hh * 128:(hh + 1) * 128], in_=a[:, hh * 128:(hh + 1) * 128],
                         func=AF.Copy, scale=s_[:, hh:hh + 1])
                P.op("dve", "tensor_tensor", [h_, nrow], [hb_], out=hb_[:], in0=h_[:], in1=nrow[:], op=ALU.mult)
                for j in range(4):
                    P.op("pe", "transpose", [hb_, ident_bf], [tp[j]], tp[j][:, ti * 128:(ti + 1) * 128],
                         hb_[:, j * 128:(j + 1) * 128], ident_bf[:])
            for j in range(4):
                r0 = cat_row0 + j * 128
                z_, o_ = sz[j % 2], oo[j % 2]
                P.dma("sp", z_[:, 0:n], SC["SZT"][r0:r0 + 128, t0:t0 + n], writes=[z_])
                P.op("dve", "tensor_tensor", [tp[j], z_], [o_], out=o_[:, 0:n], in0=tp[j][:, 0:n], in1=z_[:, 0:n], op=ALU.mult)
                P.dma("pool", SC["CATT"][r0:r0 + 128, t0:t0 + n], o_[:, 0:n], reads=[o_])
        P.barrier()


def phase_C(nc, P, IN, SC, layer, gateR, out_ap):
    with ExitStack() as es:
        A = Alloc(nc, es)
        w = A.sb([128, 8, 1024], BF16, "w_out")
        with ExitStack() as es2:
            A2 = Alloc(nc, es2)
            stg = [A2.sb([128, 8, 512], F32, "wstg%d" % i) for i in range(2)]
            for i in range(2):
                P.dma("sp" if i == 0 else "pool", stg[i][:],
                      IN["l%d_w_out" % layer][:, i * 512:(i + 1) * 512].rearrange("(k p) n -> p k n", p=128), writes=[stg[i]])
                P.op("dve" if i == 0 else "act", "tensor_copy" if i == 0 else "copy", [stg[i]], [w], out=w[:, :, i * 512:(i + 1) * 512],
                     in_=stg[i][:])
            P.barrier()
        cat = [A.sb([128, 8, 512], BF16, "catT%d" % i) for i in range(2)]
        hold = [A.sb([128, 1024], F32, "hold%d" % i) for i in range(2)]
        tmp = [A.sb([128, 1024], F32, "ctmp%d" % i) for i in range(2)]
        hnew = [A.sb([128, 1024], F32, "hnew%d" % i) for i in range(2)]
        ps = [A.ps([128, 512], F32, "yps%d" % i) for i in range(4)]
        if layer == 1:
            frow = A.sb([128, 1024], F32, "frow")
            P.dma("sp", frow[:], IN["final_norm"][0:1, :].to_broadcast([128, 1024]), writes=[frow])
            epsT = A.sb([128, 1], F32, "epsTf")
            P.op("pool", "memset", [], [epsT], epsT[:], EPS)
            junk = A.sb([128, 1024], F32, "fjunk")
            st = [A.sb([128, 1], F32, "fst%d" % i) for i in range(2)]
            ob = [A.sb([128, 1024], F32, "fob%d" % i) for i in range(2)]
        it = 0
        groups = GROUPS if layer == 0 else GROUPS[1:]
        for gi, (t0, n) in enumerate(groups):
            c_ = cat[gi % 2]
            for k2 in range(2):
                P.dma("sp" if k2 == 0 else "pool", c_[:, k2 * 4:(k2 + 1) * 4, 0:n],
                      SC["CATT"][k2 * 512:(k2 + 1) * 512, t0:t0 + n].rearrange("(k p) t -> p k t", p=128), writes=[c_])
            g_ = gateR[1] if (layer == 0 and t0 == 0) else gateR[0]
            for ti in range(n // 128):
                tok = t0 + ti * 128
                ho, tm, hn_ = hold[it % 2], tmp[it % 2], hnew[it % 2]
                if layer == 0:
                    srcp = IN["ctx"][tok:tok + 128, :] if t0 == 0 else IN["x"][tok - TC:tok - TC + 128, :]
                else:
                    srcp = SC["H1"][tok:tok + 128, :]
                P.dma("sp", ho[:], srcp, writes=[ho])
                for half in range(2):
                    p_ = ps[(it * 2 + half) % 4]
                    for k in range(8):
                        P.op("pe", "matmul", [c_, w], [p_], p_[:, :], c_[:, k, ti * 128:(ti + 1) * 128],
                             w[:, k, half * 512:(half + 1) * 512], start=(k == 0), stop=(k == 7))
                    P.op("dve", "tensor_tensor", [p_, g_], [tm], out=tm[:, half * 512:(half + 1) * 512], in0=p_[:, :],
                         in1=g_[:, half * 512:(half + 1) * 512], op=ALU.mult)
                P.op("pool", "tensor_tensor", [tm, ho], [hn_], out=hn_[:], in0=tm[:], in1=ho[:], op=ALU.add)
                if layer == 0:
                    P.dma("pool", SC["H1"][tok:tok + 128, :], hn_[:], reads=[hn_])
                else:
                    s_, o_ = st[it % 2], ob[it % 2]
                    P.op("act", "activation", [hn_], [junk, s_], out=junk[:], in_=hn_[:], func=AF.Square, accum_out=s_[:, 0:1])
                    P.op("act", "activation", [s_, epsT], [s_], out=s_[:], in_=s_[:], func=AF.Sqrt, scale=1.0 / D, bias=epsT[:, 0:1])
                    P.op("dve", "reciprocal", [s_], [s_], out=s_[:], in_=s_[:])
                    P.op("act", "activation", [hn_, s_], [o_], out=o_[:], in_=hn_[:], func=AF.Copy, scale=s_[:, 0:1])
                    P.op("dve", "tensor_tensor", [o_, frow], [o_], out=o_[:], in0=o_[:], in1=frow[:], op=ALU.mult)
                    P.dma("pool", out_ap[tok - TC:tok - TC + 128, :], o_[:], reads=[o_])
                it += 1
        P.barrier()


def gla_phase(nc, P, IN, SC, ident_bf):
    NB = NT
    NCH = T // 64
    with ExitStack() as es:
        A = Alloc(nc, es)
        mask = [A.sb([128, 64], F32, "gmask%d" % d) for d in range(2)]
        for d in range(2):
            P.op("pool", "memset", [], [mask[d]], mask[d][:], 1.0)
            for half in range(2):
                pr = slice(half * 64, half * 64 + 64)
                P.op("pool", "affine_select", [mask[d]], [mask[d]], out=mask[d][pr, :], in_=mask[d][pr, :],
                     pattern=[[1 if d == 0 else -1, 64]], compare_op=ALU.is_ge, fill=0.0, base=0,
                     channel_multiplier=-1 if d == 0 else 1)
        rm = [A.sb([64, T], F32, "rm%d" % d) for d in range(2)]
        for d in range(2):
            P.op("pool", "memset", [], [rm[d]], rm[d][:], 1.0)
            j0 = 0 if d == 0 else 63
            P.op("pool", "memset", [rm[d]], [rm[d]], rm[d][:, :].rearrange("p (c j) -> p c j", j=64)[:, :, j0:j0 + 1], 0.0)
        qf = A.sb([64, T], F32, "gqf")
        kf = A.sb([64, T], F32, "gkf")
        lg = A.sb([64, T], F32, "glg")
        bc = A.sb([64, T], F32, "gbc")
        tmp = A.sb([64, T], F32, "gtmp")
        qg = [A.sb([64, T], BF16, "qg%d" % d) for d in range(2)]
        kg = [A.sb([64, T], BF16, "kg%d" % d) for d in range(2)]
        kbf = A.sb([64, T], BF16, "kbf")
        eb = [A.sb([64, NCH], F32, "eb%d" % d) for d in range(2)]
        kbTok = [A.sb([128, NB, 64], BF16, "kbTok%d" % d) for d in range(2)]
        SmT = [A.sb([128, NB, 64], BF16, "gSmT%d" % d) for d in range(2)]
        vtok = A.sb([128, NB, 128], BF16, "gvtok")
        obuf = [[A.sb([128, 128], F32, "gob%d_%d" % (d, i)) for i in range(3)] for d in range(2)]
        Sst = [[A.sb([64, 128], F32, "S%d_%d" % (d, i)) for i in range(2)] for d in range(2)]
        Sb = [[A.sb([64, 128], BF16, "Sb%d_%d" % (d, i)) for i in range(2)] for d in range(2)]
        ptr = [A.ps([128, 8, 64], BF16, "gptr%d" % i) for i in range(2)]
        pS = [A.ps([128, 8, 64], F32, "gpS%d" % i) for i in range(2)]
        pU = [A.ps([128, 128], F32, "gpU%d" % i) for i in range(2)]
        pN = [A.ps([128, 128], F32, "gpN%d" % i) for i in range(2)]
        order = [list(range(NCH)), [3, 2, 1, 0] + list(range(NCH - 1, 3, -1))]
        for h in range(4):
            P.dma("sp", qf[:], SC["MQK"][h * 64:(h + 1) * 64, :], writes=[qf])
            P.dma("pool", kf[:], SC["MQK"][256 + h * 64:256 + (h + 1) * 64, :], writes=[kf])
            P.dma("sp", vtok[:], SC["MV"][:, h * 128:(h + 1) * 128].rearrange("(b p) j -> p b j", p=128), writes=[vtok])
            for d in range(2):
                P.dma("pool", lg[:], SC["LG"][d, h * 64:(h + 1) * 64, :], writes=[lg])
                if d == 0:
                    P.op("dve", "tensor_tensor_scan", [rm[d], lg], [bc], out=bc[:, :], data0=rm[d][:, :], data1=lg[:, :],
                         initial=0.0, op0=ALU.mult, op1=ALU.add)
                else:
                    P.op("dve", "tensor_tensor_scan", [rm[d], lg], [bc], out=bc[:, ::-1], data0=rm[d][:, ::-1],
                         data1=lg[:, ::-1], initial=0.0, op0=ALU.mult, op1=ALU.add)
                jl = 63 if d == 0 else 0
                bl = bc[:, :].rearrange("p (c j) -> p c j", j=64)[:, :, jl:jl + 1]
                P.op("act", "activation", [bc], [eb[d]], out=eb[d][:, :].unsqueeze(2), in_=bl, func=AF.Exp)
                P.op("act", "activation", [bc], [tmp], out=tmp[:], in_=bc[:], func=AF.Exp)
                P.op("dve", "scalar_tensor_tensor", [qf, tmp], [qg[d]], out=qg[d][:], in0=qf[:], scalar=64.0 ** -0.5, in1=tmp[:],
                     op0=ALU.mult, op1=ALU.mult)
                P.op("act", "activation", [bc], [tmp], out=tmp[:], in_=bc[:], func=AF.Exp, scale=-1.0)
                P.op("dve", "tensor_tensor", [kf, tmp], [kg[d]], out=kg[d][:], in0=kf[:], in1=tmp[:], op=ALU.mult)
                P.op("dve", "tensor_tensor", [bc], [tmp], out=tmp[:, :].rearrange("p (c j) -> p c j", j=64),
                     in0=bl.to_broadcast([64, NCH, 64]), in1=bc[:, :].rearrange("p (c j) -> p c j", j=64), op=ALU.subtract)
                P.op("act", "activation", [tmp], [tmp], out=tmp[:], in_=tmp[:], func=AF.Exp)
                P.op("dve", "tensor_tensor", [kf, tmp], [kbf], out=kbf[:], in0=kf[:], in1=tmp[:], op=ALU.mult)
                for b0 in range(0, NB, 8):
                    nb = min(8, NB - b0)
                    ps = ptr[(b0 // 8) % 2]
                    for bb in range(nb):
                        P.op("pe", "transpose", [kbf, ident_bf], [ps], ps[:, bb, :], kbf[:, (b0 + bb) * 128:(b0 + bb + 1) * 128],
                             ident_bf[0:64, 0:64])
                    P.op("act", "copy", [ps], [kbTok[d]], out=kbTok[d][:, b0:b0 + nb, :], in_=ps[:, 0:nb, :])
                for b0 in range(0, NB, 8):
                    nb = min(8, NB - b0)
                    ps = pS[(b0 // 8) % 2]
                    for bb in range(nb):
                        for half in range(2):
                            c = (b0 + bb) * 2 + half
                            pr = slice(half * 64, half * 64 + 64)
                            P.op("pe", "matmul", [kg[d], qg[d]], [ps], ps[pr, bb, :], kg[d][:, c * 64:(c + 1) * 64],
                                 qg[d][:, c * 64:(c + 1) * 64], start=True, stop=True)
                    P.op("dve", "tensor_tensor", [ps, mask[d]], [SmT[d]], out=SmT[d][:, b0:b0 + nb, :], in0=ps[:, 0:nb, :],
                         in1=mask[d][:].unsqueeze(1).to_broadcast([128, nb, 64]), op=ALU.mult)
            for d in range(2):
                P.op("pool", "memset", [], [Sst[d][0]], Sst[d][0][:], 0.0)
                P.op("pool", "memset", [], [Sb[d][0]], Sb[d][0][:], 0.0)
            for i in range(NCH):
                for d in range(2):
                    c = order[d][i]
                    b, half = c // 2, c % 2
                    pr = slice(half * 64, half * 64 + 64)
                    Sold, Snew = Sst[d][i % 2], Sst[d][(i + 1) % 2]
                    sbo, sbn = Sb[d][i % 2], Sb[d][(i + 1) % 2]
                    U, N = pU[d], pN[d]
                    ob = obuf[d][(i // 2) % 3]
                    P.op("pe", "matmul", [SmT[d], vtok], [N], N[pr, :], SmT[d][pr, b, :], vtok[pr, b, :], start=True, stop=False)
                    P.op("pe", "matmul", [qg[d], sbo], [N], N[pr, :], qg[d][:, c * 64:(c + 1) * 64], sbo[:, :], start=False,
                         stop=True)
                    P.op("pe", "matmul", [kbTok[d], vtok], [U], U[0:64, :], kbTok[d][pr, b, :], vtok[pr, b, :], start=True,
                         stop=True)
                    P.op("dve", "scalar_tensor_tensor", [Sold, eb[d], U], [Snew], out=Snew[:], in0=Sold[:],
                         scalar=eb[d][:, c:c + 1], in1=U[0:64, :], op0=ALU.mult, op1=ALU.add)
                    P.op("pool", "tensor_copy", [Snew], [sbn], out=sbn[:], in_=Snew[:])
                    P.op("act", "copy", [N], [ob], out=ob[pr, :], in_=N[pr, :])
                    if i % 2 == 1:
                        P.dma("sp" if d == 0 else "pool", SC["HM"][d, b * 128:(b + 1) * 128, h * 128:(h + 1) * 128], ob[:],
                              reads=[ob])
        P.barrier()


def na_bias_fn(nc, P, IN, state):
    def rows_ok(qr, kr):
        lo = min(max(qr - 4, 0), 56)
        return lo <= kr < lo + 8

    def fn(h, A):
        if A is not None:
            state["sets"] = {}
            for key, ntile in (("first", 6), ("mid", 8), ("last", 6)):
                state["sets"][key] = [A.sb([128, 512], F32, "nab_%s%d" % (key, i)) for i in range(ntile)]
        out = {}
        for key, g, t_lo in (("first", 0, 0), ("mid", 1, 2), ("last", 7, 26)):
            tiles = state["sets"][key]
            for r, bt in enumerate(tiles):
                ktl = t_lo + r
                P.op("pool", "memset", [], [bt], bt[:], MASKV)
                for i in range(2):
                    kr = 2 * ktl + i
                    js = [j for j in range(8) if rows_ok(8 * g + j, kr)]
                    if not js:
                        continue
                    j0, j1 = js[0], js[-1]
                    assert js == list(range(j0, j1 + 1))
                    m0 = 7 - (kr - 8 * g - j0)
                    nj = j1 - j0 + 1
                    P.dma("sp" if i == 0 else "pool", bt[i * 64:(i + 1) * 64, j0 * 64:(j1 + 1) * 64].rearrange("p (m q) -> p m q", q=64),
                          IN["na_bias"][h, m0:m0 + nj, :, :].rearrange("m k q -> k m q"), writes=[bt])
        for g in range(8):
            if g == 0:
                key, t_lo, nt_ = "first", 0, 6
            elif g == 7:
                key, t_lo, nt_ = "last", 26, 6
            else:
                key, t_lo, nt_ = "mid", 4 * g - 2, 8
            for r in range(nt_):
                out[(g, 2 + t_lo + r)] = state["sets"][key][r]
        return out

    return fn


def _shapes(d):
    return {k: (v.shape, "bf16" if v.dtype == ml_dtypes.bfloat16 else "f32") for k, v in d.items()}


def run(inputs, stage=99, debug=(), cores=8, skip=()):
    inputs = {k: np.asarray(v) for k, v in inputs.items()}
    sh, per = prep_inputs(inputs)
    nc = build(_shapes(sh), _shapes(per[0]), stage=stage, debug=debug, skip=skip)
    in_maps = [dict(sh, **per[b]) for b in range(cores)]
    res = run_bass_kernel_spmd(nc, in_maps, core_ids=list(range(cores)))
    return res


def kernel(**inputs):
    res = run(inputs)
    return np.stack([np.asarray(r["out"], dtype=np.float32) for r in res.results], axis=0)
```

```python
import numpy as np
from contextlib import ExitStack
import ml_dtypes
import concourse.bass as bass
import concourse.mybir as mybir
from concourse.bass_utils import run_bass_kernel_spmd

F32 = mybir.dt.float32
BF16 = mybir.dt.bfloat16
AF = mybir.ActivationFunctionType
ALU = mybir.AluOpType
AX = mybir.AxisListType

D = 1024
TC = 256
TL = 4096
T = TC + TL
NT = T // 128
EPS = 1e-6
MASKV = -30000.0

GROUPS = [(0, 256)] + [(256 + 512 * i, 512) for i in range(8)]


class Dep:
    __slots__ = ("w", "r")

    def __init__(self):
        self.w = None
        self.r = {}


class Tile:
    def __init__(self, t):
        self.t = t
        self.d = Dep()

    def __getitem__(self, k):
        return self.t[k]


class Prog:
    def __init__(self, nc, es):
        self.nc = nc
        self.eng = {"pe": nc.tensor, "act": nc.scalar, "dve": nc.vector, "pool": nc.gpsimd, "sp": nc.sync}
        self.R = 12
        self.keys = [("pe", "c"), ("act", "c"), ("dve", "c"), ("pool", "c")]
        for q in ("sp", "pool"):
            self.keys += [(q, "d%d" % i) for i in range(self.R)]
        self.ndma = {"sp": 0, "pool": 0}
        self.sem = {k: es.enter_context(nc.semaphore("s_%s_%s" % k)) for k in self.keys}
        self.cnt = {k: 0 for k in self.keys}
        self.waited = {e: {} for e in self.eng}
        self.n = 0

    def _emit(self, eng, kind, fn, reads, writes):
        if kind == "d":
            kind = "d%d" % (self.ndma[eng] % self.R)
            self.ndma[eng] += 1
        key = (eng, kind)
        deps = {}
        if kind != "c" and self.cnt[key] > 0:
            deps[key] = self.cnt[key]

        def add(tok):
            if tok is None:
                return
            k, v = tok
            if deps.get(k, 0) < v:
                deps[k] = v

        for b in reads:
            add(b.d.w)
        for b in writes:
            add(b.d.w)
            for k, v in b.d.r.items():
                add((k, v))
        e = self.eng[eng]
        wd = self.waited[eng]
        for k, v in deps.items():
            if k == ("pe", "c") and eng == "pe":
                continue
            if wd.get(k, 0) >= v:
                continue
            e.wait_ge(self.sem[k], v)
            wd[k] = v
        inc = 16 if kind != "c" else 1
        self.cnt[key] += inc
        fn(e).then_inc(self.sem[key], inc)
        v = self.cnt[key]
        for b in reads:
            if b.d.r.get(key, 0) < v:
                b.d.r[key] = v
        for b in writes:
            b.d.w = (key, v)
            b.d.r = {}
        self.n += 1

    def op(self, eng, name, reads, writes, *a, **kw):
        self._emit(eng, "c", lambda e: getattr(e, name)(*a, **kw), reads, writes)

    def dma(self, q, out, in_, reads=(), writes=(), **kw):
        self._emit(q, "d", lambda e: e.dma_start(out=out, in_=in_, **kw), reads, writes)

    def barrier(self):
        for en, e in self.eng.items():
            wd = self.waited[en]
            for k in self.keys:
                v = self.cnt[k]
                if v > 0 and wd.get(k, 0) < v:
                    e.wait_ge(self.sem[k], v)
                    wd[k] = v


class Alloc:
    def __init__(self, nc, es):
        self.nc = nc
        self.es = es
        _CTR.setdefault(id(nc), 0)

    def _nm(self, name):
        _CTR[id(self.nc)] = _CTR.get(id(self.nc), 0) + 1
        return "%s_%d" % (name, _CTR[id(self.nc)])

    def sb(self, shape, dt, name=None):
        return Tile(self.es.enter_context(self.nc.sbuf_tensor(self._nm(name or "sb"), list(shape), dt)))

    def ps(self, shape, dt, name=None):
        return Tile(self.es.enter_context(self.nc.psum_tensor(self._nm(name or "ps"), list(shape), dt)))


_CTR = {}


def _fm(v, nchunk):
    return np.ascontiguousarray(v.reshape(nchunk, 128).T)


def _rope_perm():
    perm = np.zeros(32, np.int64)
    for i in range(32):
        r = i % 16
        perm[i] = i + 8 if r < 8 else i - 8
    return perm


def _rope_tables():
    t = np.arange(TL)
    inv = (1.0 / (10000.0 ** (np.arange(8, dtype=np.float32) / 8))).astype(np.float32)
    pos = [(t // 64).astype(np.float32), (t % 64).astype(np.float32)]
    C = np.zeros((32, TL), np.float32)
    S = np.zeros((32, TL), np.float32)
    for i in range(32):
        a = i // 16
        r = i % 16
        p = r % 8
        ang = (pos[a] * inv[p]).astype(np.float32)
        C[i] = np.cos(ang)
        S[i] = -np.sin(ang) if r < 8 else np.sin(ang)
    Cf = np.zeros((128, TL), np.float32)
    Sf = np.zeros((128, TL), np.float32)
    Cf[0:32] = C
    Cf[64:96] = C
    Sf[0:32] = S
    Sf[64:96] = S
    return Cf, Sf


def prep_inputs(inp):
    sh = {}
    sh["ident_bf"] = np.eye(128, dtype=np.float32).astype(ml_dtypes.bfloat16)
    sh["ident_f"] = np.eye(128, dtype=np.float32)
    perm = _rope_perm()
    w_in = inp["l0_w_in"]
    gi_cols = [2720 + d * 8 + h for d in range(2) for h in range(4)]
    gf_cols = [2720 + d * 8 + 4 + h for d in range(2) for h in range(4)]
    sh["l0_w_in"] = np.ascontiguousarray(
        np.concatenate([w_in, w_in[:, 640:672][:, perm], w_in[:, gi_cols], w_in[:, gf_cols]], axis=1))
    w_uq = inp["l0_mla_w_uq"].reshape(384, 8, 96)
    ext = np.concatenate([w_uq, w_uq[:, :, 0:64], w_uq[:, :, 64:96][:, :, perm]], axis=2)
    sh["l0_w_uq"] = np.ascontiguousarray(ext.reshape(384, 8 * 192))
    w_ukv = inp["l0_mla_w_ukv"].reshape(256, 8, 128)
    sh["l0_w_ukv"] = np.ascontiguousarray(
        np.concatenate([w_ukv[:, :, 0:64].reshape(256, 512), w_ukv[:, :, 64:128].reshape(256, 512)], axis=1))
    sh["l0_qnT"] = _fm(inp["l0_mla_q_norm"], 3)
    sh["l0_kvnT"] = _fm(inp["l0_mla_kv_norm"], 2)
    Cf, Sf = _rope_tables()
    sh["ropeC"] = Cf
    sh["ropeS"] = Sf
    cw = inp["l0_mlstm_conv_w"]
    sh["l0_convT"] = np.ascontiguousarray(
        np.concatenate([cw.reshape(3, 8, 128).transpose(2, 1, 0), inp["l0_mlstm_conv_b"].reshape(8, 128).T[:, :, None]],
                       axis=2))
    gb = np.zeros((16, 1), np.float32)
    for d in range(2):
        for h in range(4):
            gb[d * 8 + h, 0] = inp["l0_mlstm_b_i"][d, h]
            gb[d * 8 + 4 + h, 0] = inp["l0_mlstm_b_f"][d, h]
    sh["l0_gbias"] = gb
    gb2 = np.zeros((64, 2), np.float32)
    for d in range(2):
        for h in range(4):
            gb2[d * 32 + h, 0] = inp["l0_mlstm_b_i"][d, h]
            gb2[d * 32 + h, 1] = inp["l0_mlstm_b_f"][d, h]
    sh["l0_gb2"] = gb2
    sh["l0_hnorm"] = np.ascontiguousarray(inp["l0_mlstm_norm"].reshape(1, 512))
    sh["l0_w_out"] = inp["l0_w_out"]
    sh["l1_w_in"] = inp["l1_w_in"]
    sh["l1_w_gate"] = np.ascontiguousarray(inp["l1_gla_w_gate"])
    sh["l1_bgT"] = np.ascontiguousarray(inp["l1_gla_b_gate"].reshape(2, 2, 128).transpose(2, 0, 1))
    sh["l1_gnorm"] = np.ascontiguousarray(inp["l1_gla_norm"].reshape(1, 512))
    sh["l1_w_out"] = inp["l1_w_out"]
    sh["final_norm"] = np.ascontiguousarray(inp["final_norm"].reshape(1, 1024))
    rpb = inp["l1_na_rpb"]
    kc = np.arange(64)[:, None]
    qc = np.arange(64)[None, :]
    wc0 = np.clip(qc - 8, 0, 48)
    okc = (kc >= wc0) & (kc < wc0 + 16)
    dcol = np.clip(kc - qc + 15, 0, 30)
    Tb = np.full((8, 15, 64, 64), MASKV, np.float32)
    for m in range(15):
        dr = 7 - m
        blk = rpb[:, dr + 7][:, dcol]
        Tb[:, m] = np.where(okc[None], blk, np.float32(MASKV))
    sh["na_bias"] = Tb
    mods = [(inp["l0_norm"], inp["l0_w_mod"], inp["l0_b_mod"]), (inp["l1_norm"], inp["l1_w_mod"], inp["l1_b_mod"])]
    for l, (g_, wm_, bm_) in enumerate(mods):
        sh["l%d_w_mod" % l] = wm_
        sh["l%d_bmodT" % l] = _fm(bm_, 24)
        sh["l%d_bmod_gate" % l] = np.ascontiguousarray(bm_[2048:3072].reshape(1, 1024))
        sh["l%d_gT" % l] = _fm(g_, 8)
    per = []
    for b in range(8):
        d = {}
        d["x"] = inp["x"][b]
        d["ctx"] = inp["ctx"][b]
        cv = np.stack([inp["c"][b], inp["c_ctx"]], axis=1)
        d["cvec"] = np.ascontiguousarray(cv.reshape(8, 128, 2).transpose(1, 0, 2))
        per.append(d)
    return sh, per


def build(sh_shapes, per_shapes, stage=99, debug=(), skip=()):
    nc = bass.Bass("TRN2", target_bir_lowering=False)
    IN = {}
    for k, (shape, dt) in list(sh_shapes.items()) + list(per_shapes.items()):
        IN[k] = nc.dram_tensor(k, list(shape), BF16 if dt == "bf16" else F32, kind="ExternalInput").ap()
    out = nc.dram_tensor("out", [TL, D], F32, kind="ExternalOutput").ap()

    def scratch(name, shape, dt):
        kind = "ExternalOutput" if name in debug else "Internal"
        return nc.dram_tensor(name, list(shape), dt, kind=kind).ap()

    SC = {}
    SC["H1"] = scratch("H1", [T, D], F32)
    SC["SZT"] = scratch("SZT", [1024, T], BF16)
    SC["CATT"] = scratch("CATT", [1024, T], BF16)
    SC["QT"] = scratch("QT", [8, 96, T], BF16)
    SC["KT"] = scratch("KT", [8, 96, T], BF16)
    SC["V"] = scratch("V", [T, 512], BF16)
    SC["MQK"] = scratch("MQK", [1024, T], F32)
    SC["GI"] = scratch("GI", [8, T], F32)
    SC["GF"] = scratch("GF", [8, T], F32)
    SC["MV"] = scratch("MV", [T, 512], BF16)
    SC["MO"] = scratch("MO", [T, 512], BF16)
    SC["HM"] = scratch("HM", [2, T, 512], F32)
    SC["LG"] = scratch("LG", [2, 256, T], F32)
    SC["NQ"] = scratch("NQ", [512, T], BF16)
    SC["NK"] = scratch("NK", [512, T], BF16)

    with ExitStack() as es0:
        P = Prog(nc, es0)
        A0 = Alloc(nc, es0)
        ident_bf = A0.sb([128, 128], BF16, "identbf")
        ident_f = A0.sb([128, 128], F32, "identf")
        ones_f = A0.sb([128, 128], F32, "onesf")
        P.dma("sp", ident_bf[:], IN["ident_bf"][:, :], writes=[ident_bf])
        P.dma("sp", ident_f[:], IN["ident_f"][:, :], writes=[ident_f])
        P.op("pool", "memset", [], [ones_f], ones_f[:], 1.0)
        affA = [A0.sb([128, 8, 2], F32, "affA%d" % l) for l in range(2)]
        affB = [A0.sb([128, 8, 2], F32, "affB%d" % l) for l in range(2)]
        gateR = [[A0.sb([128, 1024], F32, "gateR%d_%d" % (l, s)) for s in range(2 if l == 0 else 1)] for l in range(2)]

        with ExitStack() as es:
            A = Alloc(nc, es)
            cv = A.sb([128, 8, 2], F32, "cv")
            sc = A.sb([128, 8, 2], F32, "sc")
            screp = [A.sb([128, 8, 128], F32, "screp%d" % s) for s in range(2)]
            P.dma("sp", cv[:], IN["cvec"][:, :, :], writes=[cv])
            P.op("act", "activation", [cv], [sc], out=sc[:], in_=cv[:], func=AF.Silu)
            for s in range(2):
                for k in range(8):
                    P.op("dve", "tensor_copy", [sc], [screp[s]], out=screp[s][:, k, :],
                         in_=sc[:, k, s:s + 1].to_broadcast([128, 128]))
            wpan = [A.sb([128, 8, 384], F32, "wpan%d" % i) for i in range(2)]
            wgate = [A.sb([128, 512], F32, "wgate%d" % i) for i in range(3)]
            pm = A.ps([128, 24, 2], F32, "pm")
            pg = [A.ps([128, 512], F32, "pg%d" % i) for i in range(2)]
            bmT = A.sb([128, 24], F32, "bmT")
            gT = A.sb([128, 8], F32, "gT")
            modT = A.sb([128, 24, 2], F32, "modT")
            bgrow = A.sb([128, 1024], F32, "bgrow")
            for l in range(2):
                wm = IN["l%d_w_mod" % l]
                P.dma("sp", bmT[:], IN["l%d_bmodT" % l][:, :], writes=[bmT])
                P.dma("sp", gT[:], IN["l%d_gT" % l][:, :], writes=[gT])
                P.dma("sp", bgrow[:], IN["l%d_bmod_gate" % l][0:1, :].to_broadcast([128, 1024]), writes=[bgrow])
                for pn in range(8):
                    wp = wpan[pn % 2]
                    P.dma("sp" if pn % 2 == 0 else "pool", wp[:],
                          wm[:, pn * 384:(pn + 1) * 384].rearrange("(k p) n -> p k n", p=128), writes=[wp])
                    for j in range(3):
                        n = pn * 3 + j
                        for k in range(8):
                            P.op("pe", "matmul", [wp, sc], [pm], pm[:, n, :], wp[:, k, j * 128:(j + 1) * 128],
                                 sc[:, k, :], start=(k == 0), stop=(k == 7))
                P.op("dve", "tensor_tensor", [pm, bmT], [modT], out=modT[:], in0=pm[:],
                     in1=bmT[:].unsqueeze(2).to_broadcast([128, 24, 2]), op=ALU.add)
                P.op("dve", "tensor_scalar", [modT], [affA[l]], out=affA[l][:], in0=modT[:, 8:16, :], scalar1=1.0,
                     scalar2=None, op0=ALU.add)
                P.op("dve", "tensor_tensor", [affA[l], gT], [affA[l]], out=affA[l][:], in0=affA[l][:],
                     in1=gT[:].unsqueeze(2).to_broadcast([128, 8, 2]), op=ALU.mult)
                P.op("dve", "tensor_copy", [modT], [affB[l]], out=affB[l][:], in_=modT[:, 0:8, :])
                for s in range(len(gateR[l])):
                    for hf in range(2):
                        ps = pg[hf]
                        for k in range(8):
                            wg = wgate[(hf * 8 + k) % 3]
                            P.dma("sp" if k % 2 == 0 else "pool", wg[:],
                                  wm[k * 128:(k + 1) * 128, 2048 + hf * 512:2048 + (hf + 1) * 512], writes=[wg])
                            P.op("pe", "matmul", [wg, screp[s]], [ps], ps[:], screp[s][:, k, :], wg[:],
                                 start=(k == 0), stop=(k == 7))
                        P.op("dve", "tensor_tensor", [ps, bgrow], [gateR[l][s]],
                             out=gateR[l][s][:, hf * 512:(hf + 1) * 512], in0=ps[:],
                             in1=bgrow[:, hf * 512:(hf + 1) * 512], op=ALU.add)
            P.barrier()
        if stage <= 0:
            dbg = nc.dram_tensor("dbg_mod", [128, 2, 2, 8, 2], F32, kind="ExternalOutput").ap()
            dbg2 = nc.dram_tensor("dbg_gate", [128, 1024], F32, kind="ExternalOutput").ap()
            for l in range(2):
                P.dma("sp", dbg[:, l, 0], affA[l][:], reads=[affA[l]])
                P.dma("sp", dbg[:, l, 1], affB[l][:], reads=[affB[l]])
            P.dma("sp", dbg2[:, :], gateR[0][1][:], reads=[gateR[0][1]])
            P.barrier()
            return nc

        phase_A(nc, P, IN, SC, 0, affA[0], affB[0], ident_bf, ones_f)
        if stage <= 1:
            return nc
        if 2 not in skip:
            mla_groups = [(0, 256, [0, 1], 0)] + [(256 + 512 * g, 512, list(range(NT)), 0) for g in range(8)]
            attention(nc, P, SC, ones_f, 8, 96, 96.0 ** -0.5, lambda h: SC["QT"][h, :, :], lambda h: SC["KT"][h, :, :],
                      SC["V"], 0, mla_groups)
        if stage <= 2:
            return nc
        if 3 not in skip:
            mlstm_phase(nc, P, IN, SC, ident_bf, ident_f, ones_f)
        if stage <= 3:
            return nc
        combine_phase(nc, P, IN, SC, ident_bf, SC["HM"][0], SC["HM"][1], SC["MO"], "l0_hnorm", 512, GROUPS)
        if stage <= 4:
            return nc
        phase_C(nc, P, IN, SC, 0, gateR[0], out)
        if stage <= 5:
            return nc
        phase_A(nc, P, IN, SC, 1, affA[1], affB[1], ident_bf, ones_f)
        if stage <= 6:
            return nc
        if 7 not in skip:
            gla_phase(nc, P, IN, SC, ident_bf)
            combine_phase(nc, P, IN, SC, ident_bf, SC["HM"][0], SC["HM"][1], None, "l1_gnorm", 0, GROUPS[1:])
        if stage <= 7:
            return nc
        if 8 not in skip:
            na_groups = []
            for g in range(8):
                t_lo, nt_ = (0, 6) if g == 0 else ((26, 6) if g == 7 else (4 * g - 2, 8))
                na_groups.append((256 + 512 * g, 512, [0, 1] + [2 + t_lo + r for r in range(nt_)], g))
            attention(nc, P, SC, ones_f, 8, 64, 64.0 ** -0.5, lambda h: SC["NQ"][h * 64:(h + 1) * 64, :],
                      lambda h: SC["NK"][h * 64:(h + 1) * 64, :], SC["V"], 512, na_groups, bias_fn=na_bias_fn(nc, P, IN, {}))
        if stage <= 8:
            return nc
        phase_C(nc, P, IN, SC, 1, gateR[1], out)
    return nc


def phase_A(nc, P, IN, SC, layer, affA, affB, ident_bf, ones_f):
    NW = 3808 if layer == 0 else 3616
    w_in_d = IN["l%d_w_in" % layer]
    with ExitStack() as es:
        A = Alloc(nc, es)
        w_in = A.sb([128, 8, NW], BF16, "w_in")
        if layer == 0:
            w_uq = A.sb([128, 3, 1536], BF16, "w_uq")
            w_ukv = A.sb([128, 2, 1024], BF16, "w_ukv")
        with ExitStack() as es2:
            A2 = Alloc(nc, es2)
            stg = [A2.sb([128, 8, 512], F32, "stg%d" % i) for i in range(2)]
            i = 0
            for c0 in range(0, NW, 512):
                cw = min(512, NW - c0)
                s = stg[i % 2]
                P.dma("sp" if i % 2 == 0 else "pool", s[:, :, 0:cw],
                      w_in_d[:, c0:c0 + cw].rearrange("(k p) n -> p k n", p=128), writes=[s])
                P.op("dve" if i % 2 == 0 else "act", "tensor_copy" if i % 2 == 0 else "copy", [s], [w_in],
                     out=w_in[:, :, c0:c0 + cw], in_=s[:, :, 0:cw])
                i += 1
            if layer == 0:
                s = stg[i % 2]
                for kk in range(3):
                    s = stg[i % 2]
                    P.dma("sp", s[:, 0:3, :], IN["l0_w_uq"][kk * 128:(kk + 1) * 128, :].rearrange("p (a n) -> p a n", a=3),
                          writes=[s])
                    P.op("dve", "tensor_copy", [s], [w_uq], out=w_uq[:, kk, :].rearrange("p (a n) -> p a n", a=3),
                         in_=s[:, 0:3, :])
                    i += 1
                s = stg[i % 2]
                for kk in range(2):
                    P.dma("sp", s[:, 2 * kk:2 * kk + 2, :],
                          IN["l0_w_ukv"][kk * 128:(kk + 1) * 128, :].rearrange("p (a n) -> p a n", a=2), writes=[s])
                P.op("dve", "tensor_copy", [s], [w_ukv], out=w_ukv[:].rearrange("p k (a n) -> p (k a) n", a=2),
                     in_=s[:, 0:4, :])
                i += 1
            P.barrier()
        if layer == 0:
            qnT = A.sb([128, 3], F32, "qnT")
            kvnT = A.sb([128, 2], F32, "kvnT")
            P.dma("sp", qnT[:], IN["l0_qnT"][:, :], writes=[qnT])
            P.dma("sp", kvnT[:], IN["l0_kvnT"][:, :], writes=[kvnT])
            cqT = A.sb([128, 3, 512], F32, "cqT")
            ckvT = A.sb([128, 2, 512], F32, "ckvT")
            sq = A.sb([128, 3, 512], F32, "sq")
            rstd = A.sb([128, 512], F32, "rstd")
            cqn = A.sb([128, 3, 512], BF16, "cqn")
            ckvn = A.sb([128, 2, 512], BF16, "ckvn")
            rC = A.sb([128, 512], F32, "rC")
            rS = A.sb([128, 512], F32, "rS")
            rt1 = A.sb([128, 512], F32, "rt1")
            rt2 = A.sb([128, 512], F32, "rt2")
            qo = [A.sb([128, 512], BF16, "qo%d" % i) for i in range(2)]
            kro = A.sb([32, 512], BF16, "kro")
        else:
            gaT = [A.sb([16, 512], F32, "gaT%d" % d) for d in range(2)]
            wg = A.sb([16, 2, 256], F32, "wg")
            P.dma("sp", wg[:], IN["l1_w_gate"].rearrange("d r k -> r d k"), writes=[wg])
            bgT = A.sb([128, 2, 2], F32, "bgT")
            nbg = A.sb([128, 2, 2], F32, "nbg")
            P.dma("sp", bgT[:], IN["l1_bgT"][:, :, :], writes=[bgT])
            P.op("dve", "tensor_scalar", [bgT], [nbg], out=nbg[:], in0=bgT[:], scalar1=-1.0, scalar2=None, op0=ALU.mult)
            one1 = A.sb([128, 1], F32, "one1a")
            P.op("pool", "memset", [], [one1], one1[:], 1.0)
            lge = A.sb([128, 512], F32, "lge")
            lgo = [A.sb([128, 512], F32, "lgo%d" % i) for i in range(2)]
        hb = [A.sb([128, 1024], F32, "hb%d" % i) for i in range(3)]
        junk = A.sb([128, 1024], F32, "junk")
        st = [A.sb([128, 4], F32, "st%d" % i) for i in range(2)]
        xn = [A.sb([128, 1024], BF16, "xn%d" % i) for i in range(4)]
        epsT = A.sb([128, 1], F32, "epsT")
        P.op("pool", "memset", [], [epsT], epsT[:], EPS)
        uT = [A.sb([128, 8, 512], BF16, "uT%d" % i) for i in range(2)]
        fo_bf = [A.sb([128, 512], BF16, "fobf%d" % i) for i in range(4)]
        fo_f = [A.sb([128, 512], F32, "fof%d" % i) for i in range(3)]
        tp = [A.ps([128, 512], BF16, "tp%d" % i) for i in range(2)]
        acc = [A.ps([128, 512], F32, "acc%d" % i) for i in range(5)]
        cnt = {"acc": 0, "fobf": 0, "fof": 0, "ev": 0, "q": 0, "hb": 0, "xn": 0, "tp": 0}

        def nxt(name, lst):
            r = lst[cnt[name] % len(lst)]
            cnt[name] += 1
            return r

        def evac_engine():
            cnt["ev"] += 1
            return "dve" if cnt["ev"] % 2 == 0 else "act"

        def copy_op(eng, src_t, src_ap, dst_t, dst_ap):
            if eng == "act":
                P.op("act", "copy", [src_t], [dst_t], out=dst_ap, in_=src_ap)
            else:
                P.op(eng, "tensor_copy", [src_t], [dst_t], out=dst_ap, in_=src_ap)

        def stq():
            cnt["q"] += 1
            return "pool" if cnt["q"] % 2 == 0 else "sp"

        for gi, (t0, n) in enumerate(GROUPS):
            ntl = n // 128
            s = 1 if gi == 0 else 0
            u = uT[gi % 2]
            sta = st[gi % 2]
            hs = []
            for ti in range(ntl):
                h = nxt("hb", hb)
                tok = t0 + ti * 128
                if layer == 0:
                    src = IN["ctx"][tok:tok + 128, :] if gi == 0 else IN["x"][tok - TC:tok - TC + 128, :]
                else:
                    src = SC["H1"][tok:tok + 128, :]
                P.dma("sp", h[:], src, writes=[h])
                P.op("act", "activation", [h], [junk, sta], out=junk[:], in_=h[:], func=AF.Square,
                     accum_out=sta[:, ti:ti + 1])
                P.op("act", "activation", [sta], [sta], out=sta[:, ti:ti + 1], in_=sta[:, ti:ti + 1], func=AF.Sqrt,
                     scale=1.0 / D, bias=epsT[:, 0:1])
                P.op("dve", "reciprocal", [sta], [sta], out=sta[:, ti:ti + 1], in_=sta[:, ti:ti + 1])
                x_ = xn[ti]
                P.op("dve", "tensor_scalar", [h, sta], [x_], out=x_[:], in0=h[:], scalar1=sta[:, ti:ti + 1],
                     scalar2=None, op0=ALU.mult)
            for j in range(8):
                tpp = nxt("tp", tp)
                for ti in range(ntl):
                    P.op("pe", "transpose", [xn[ti], ident_bf], [tpp], tpp[:, ti * 128:(ti + 1) * 128],
                         xn[ti][:, j * 128:(j + 1) * 128], ident_bf[:])
                P.op("dve", "tensor_scalar", [tpp, affA, affB], [u], out=u[:, j, 0:n],
                     in0=tpp[:, 0:n], scalar1=affA[:, j, s:s + 1], scalar2=affB[:, j, s:s + 1], op0=ALU.mult,
                     op1=ALU.add)

            def fm_proj(c0, ncol):
                ps = nxt("acc", acc)
                for k in range(8):
                    P.op("pe", "matmul", [w_in, u], [ps], ps[0:ncol, 0:n], w_in[:, k, c0:c0 + ncol], u[:, k, 0:n],
                         start=(k == 0), stop=(k == 7))
                return ps

            def store_fm(ps, ncol, dst, dt, func=None, eng=None):
                o = nxt("fobf", fo_bf) if dt == BF16 else nxt("fof", fo_f)
                if func is not None:
                    P.op("act", "activation", [ps], [o], out=o[0:ncol, 0:n], in_=ps[0:ncol, 0:n], func=func)
                else:
                    copy_op(eng or evac_engine(), ps, ps[0:ncol, 0:n], o, o[0:ncol, 0:n])
                P.dma(stq(), dst, o[0:ncol, 0:n], reads=[o])

            tsl = slice(t0, t0 + n)
            if layer == 0:
                for j in range(3):
                    ps = fm_proj(j * 128, 128)
                    copy_op(evac_engine(), ps, ps[:, 0:n], cqT, cqT[:, j, 0:n])
                for j in range(2):
                    ps = fm_proj(384 + j * 128, 128)
                    copy_op(evac_engine(), ps, ps[:, 0:n], ckvT, ckvT[:, j, 0:n])
                for (src_t, nk, nrm, dst_t, dim) in ((cqT, 3, qnT, cqn, 384.0), (ckvT, 2, kvnT, ckvn, 256.0)):
                    P.op("act", "activation", [src_t], [sq], out=sq[:, 0:nk, 0:n], in_=src_t[:, 0:nk, 0:n], func=AF.Square)
                    ps = nxt("acc", acc)
                    for k in range(nk):
                        P.op("pe", "matmul", [ones_f, sq], [ps], ps[:, 0:n], ones_f[:], sq[:, k, 0:n], start=(k == 0),
                             stop=(k == nk - 1))
                    P.op("act", "activation", [ps], [rstd], out=rstd[:, 0:n], in_=ps[:, 0:n], func=AF.Sqrt,
                         scale=1.0 / dim, bias=epsT[:, 0:1])
                    P.op("dve", "reciprocal", [rstd], [rstd], out=rstd[:, 0:n], in_=rstd[:, 0:n])
                    for k in range(nk):
                        P.op("dve", "scalar_tensor_tensor", [src_t, nrm, rstd], [dst_t], out=dst_t[:, k, 0:n],
                             in0=src_t[:, k, 0:n], scalar=nrm[:, k:k + 1], in1=rstd[:, 0:n], op0=ALU.mult, op1=ALU.mult)
                rot = gi > 0
                if rot:
                    P.dma("sp", rC[:, 0:n], IN["ropeC"][:, t0 - TC:t0 - TC + n], writes=[rC])
                    P.dma("sp", rS[:, 0:n], IN["ropeS"][:, t0 - TC:t0 - TC + n], writes=[rS])
                for hh in range(8):
                    ps = nxt("acc", acc)
                    for k in range(3):
                        P.op("pe", "matmul", [w_uq, cqn], [ps], ps[0:96, 0:n], w_uq[:, k, hh * 192:hh * 192 + 96],
                             cqn[:, k, 0:n], start=(k == 0), stop=(k == 2))
                    o = nxt("fobf", fo_bf)
                    if rot:
                        ps2 = nxt("acc", acc)
                        for k in range(3):
                            P.op("pe", "matmul", [w_uq, cqn], [ps2], ps2[0:96, 0:n],
                                 w_uq[:, k, hh * 192 + 96:hh * 192 + 192], cqn[:, k, 0:n], start=(k == 0), stop=(k == 2))
                        copy_op("act", ps, ps[0:64, 0:n], o, o[0:64, 0:n])
                        P.op("dve", "tensor_tensor", [ps, rC], [rt1], out=rt1[64:96, 0:n], in0=ps[64:96, 0:n],
                             in1=rC[64:96, 0:n], op=ALU.mult)
                        P.op("dve", "tensor_tensor", [ps2, rS], [rt2], out=rt2[64:96, 0:n], in0=ps2[64:96, 0:n],
                             in1=rS[64:96, 0:n], op=ALU.mult)
                        P.op("pool", "tensor_tensor", [rt1, rt2], [o], out=o[64:96, 0:n], in0=rt1[64:96, 0:n],
                             in1=rt2[64:96, 0:n], op=ALU.add)
                    else:
                        copy_op(evac_engine(), ps, ps[0:96, 0:n], o, o[0:96, 0:n])
                    P.dma(stq(), SC["QT"][hh, :, tsl], o[0:96, 0:n], reads=[o])
                for c in range(4):
                    ps = nxt("acc", acc)
                    for k in range(2):
                        P.op("pe", "matmul", [w_ukv, ckvn], [ps], ps[:, 0:n], w_ukv[:, k, c * 128:(c + 1) * 128],
                             ckvn[:, k, 0:n], start=(k == 0), stop=(k == 1))
                    o = nxt("fobf", fo_bf)
                    copy_op(evac_engine(), ps, ps[:, 0:n], o, o[:, 0:n])
                    for hh in range(2):
                        P.dma(stq(), SC["KT"][c * 2 + hh, 0:64, tsl], o[hh * 64:(hh + 1) * 64, 0:n], reads=[o])
                for ti in range(ntl):
                    ps = nxt("acc", acc)
                    for k in range(2):
                        P.op("pe", "matmul", [w_ukv, ckvn], [ps], ps[:, :], ckvn[:, k, ti * 128:(ti + 1) * 128],
                             w_ukv[:, k, 512:1024], start=(k == 0), stop=(k == 1))
                    o = nxt("fobf", fo_bf)
                    copy_op(evac_engine(), ps, ps[:, :], o, o[:, :])
                    P.dma(stq(), SC["V"][t0 + ti * 128:t0 + (ti + 1) * 128, :], o[:, :], reads=[o])
                ps = fm_proj(640, 32)
                if rot:
                    ps2 = fm_proj(3760, 32)
                    P.op("dve", "tensor_tensor", [ps, rC], [rt1], out=rt1[0:32, 0:n], in0=ps[0:32, 0:n], in1=rC[0:32, 0:n],
                         op=ALU.mult)
                    P.op("dve", "tensor_tensor", [ps2, rS], [rt2], out=rt2[0:32, 0:n], in0=ps2[0:32, 0:n],
                         in1=rS[0:32, 0:n], op=ALU.mult)
                    P.op("pool", "tensor_tensor", [rt1, rt2], [kro], out=kro[0:32, 0:n], in0=rt1[0:32, 0:n],
                         in1=rt2[0:32, 0:n], op=ALU.add)
                else:
                    copy_op("dve", ps, ps[0:32, 0:n], kro, kro[0:32, 0:n])
                for hh in range(8):
                    P.dma(stq(), SC["KT"][hh, 64:96, tsl], kro[0:32, 0:n], reads=[kro])
                for c in range(8):
                    ps = fm_proj(672 + c * 128, 128)
                    store_fm(ps, 128, SC["MQK"][c * 128:(c + 1) * 128, tsl], F32)
                ps = fm_proj(3792, 8)
                store_fm(ps, 8, SC["GI"][:, tsl], F32)
                ps = fm_proj(3800, 8)
                store_fm(ps, 8, SC["GF"][:, tsl], F32)
                for c in range(8):
                    ps = fm_proj(2736 + c * 128, 128)
                    store_fm(ps, 128, SC["SZT"][c * 128:(c + 1) * 128, tsl], BF16, func=AF.Silu)
                tm_specs = [(1696, SC["MV"], None), (2208, SC["MO"], AF.Sigmoid)]
            else:
                for c in range(4):
                    ps = fm_proj(c * 128, 128)
                    store_fm(ps, 128, SC["MQK"][c * 128:(c + 1) * 128, tsl], F32)
                for d in range(2):
                    ps = fm_proj(1024 + 16 * d, 16)
                    copy_op("dve", ps, ps[0:16, 0:n], gaT[d], gaT[d][0:16, 0:n])
                for d in range(2):
                    for c2 in range(2):
                        ps = nxt("acc", acc)
                        P.op("pe", "matmul", [wg, gaT[d]], [ps], ps[:, 0:n], wg[0:16, d, c2 * 128:(c2 + 1) * 128],
                             gaT[d][0:16, 0:n], start=True, stop=True)
                        P.op("act", "activation", [ps, nbg], [lge], out=lge[:, 0:n], in_=ps[:, 0:n], func=AF.Exp, scale=-1.0,
                             bias=nbg[:, d, c2:c2 + 1])
                        P.op("act", "activation", [lge, one1], [lge], out=lge[:, 0:n], in_=lge[:, 0:n], func=AF.Ln,
                             bias=one1[:, 0:1])
                        o = lgo[(d * 2 + c2) % 2]
                        P.op("dve", "tensor_scalar", [lge], [o], out=o[:, 0:n], in0=lge[:, 0:n], scalar1=-1.0 / 16.0,
                             scalar2=None, op0=ALU.mult)
                        P.dma(stq(), SC["LG"][d, c2 * 128:(c2 + 1) * 128, tsl], o[:, 0:n], reads=[o])
                for c in range(4):
                    ps = fm_proj(1056 + c * 128, 128)
                    store_fm(ps, 128, SC["NQ"][c * 128:(c + 1) * 128, tsl], BF16)
                for c in range(4):
                    ps = fm_proj(1568 + c * 128, 128)
                    store_fm(ps, 128, SC["NK"][c * 128:(c + 1) * 128, tsl], BF16)
                for c in range(8):
                    ps = fm_proj(2592 + c * 128, 128)
                    store_fm(ps, 128, SC["SZT"][c * 128:(c + 1) * 128, tsl], BF16, func=AF.Silu)
                tm_specs = [(512, SC["MV"], None), (2080, SC["V"], None)]
            for (c0, dst, func) in tm_specs:
                for ti in range(ntl):
                    ps = nxt("acc", acc)
                    for k in range(8):
                        P.op("pe", "matmul", [w_in, u], [ps], ps[:, :], u[:, k, ti * 128:(ti + 1) * 128],
                             w_in[:, k, c0:c0 + 512], start=(k == 0), stop=(k == 7))
                    o = nxt("fobf", fo_bf)
                    if func is not None:
                        P.op("act", "activation", [ps], [o], out=o[:, :], in_=ps[:, :], func=func)
                    else:
                        copy_op(evac_engine(), ps, ps[:, :], o, o[:, :])
                    P.dma(stq(), dst[t0 + ti * 128:t0 + (ti + 1) * 128, :], o[:, :], reads=[o])
        P.barrier()


def attention(nc, P, SC, ones_f, heads, dq, scale, load_q, load_k, Vd, cat_row0, groups, bias_fn=None):
    LOOK = 3
    NS = 4
    with ExitStack() as es:
        A = Alloc(nc, es)
        V = A.sb([128, NT, heads, 65], BF16, "Vall")
        P.op("pool", "memset", [], [V], V[:, :, :, 64:65], 1.0)
        for half in range(2):
            tl = slice(half * 17, (half + 1) * 17)
            for hh in range(heads):
                P.dma("sp" if hh % 2 == 0 else "pool", V[:, tl, hh, 0:64],
                      Vd[half * 17 * 128:(half + 1) * 17 * 128, hh * 64:(hh + 1) * 64].rearrange("(t p) d -> p t d", p=128),
                      writes=[V])
        kT = [A.sb([128, T], BF16, "kT%d" % i) for i in range(2)]
        qT = [A.sb([128, T], BF16, "qT%d" % i) for i in range(2)]
        pt = [A.sb([128, 512], BF16, "pt%d" % i) for i in range(NS)]
        sb_t = [A.sb([128, 512], F32, "sbt%d" % i) for i in range(3)] if bias_fn is not None else None
        rden = [A.sb([128, 512], F32, "rden%d" % i) for i in range(2)]
        bcs = [A.sb([128, 512], F32, "bcs%d" % i) for i in range(2)]
        szt = [A.sb([64, 512], BF16, "szt%d" % i) for i in range(3)]
        tmp = [A.sb([64, 512], F32, "atmp%d" % i) for i in range(2)]
        ao = [A.sb([64, 512], BF16, "ao%d" % i) for i in range(2)]
        Sps = [A.ps([128, 512], F32, "Sps%d" % i) for i in range(NS)]
        Ops = [A.ps([128, 512], F32, "Ops%d" % i) for i in range(2)]
        Bps = A.ps([128, 512], F32, "Bps")
        P.dma("sp", kT[0][0:dq, :], load_k(0), writes=[kT[0]])
        P.dma("pool", qT[0][0:dq, :], load_q(0), writes=[qT[0]])
        gcount = 0
        it = 0
        for h in range(heads):
            k_ = kT[h % 2]
            q_ = qT[h % 2]
            if h + 1 < heads:
                P.dma("sp", kT[(h + 1) % 2][0:dq, :], load_k(h + 1), writes=[kT[(h + 1) % 2]])
                P.dma("pool", qT[(h + 1) % 2][0:dq, :], load_q(h + 1), writes=[qT[(h + 1) % 2]])
            bias_tiles = bias_fn(h, A if h == 0 else None) if bias_fn is not None else None
            r0 = cat_row0 + h * 64
            items = []
            for (q0, n, tiles, gkey) in groups:
                gid = gcount
                gcount += 1
                for j, kt in enumerate(tiles):
                    items.append((gid, q0, n, gkey, j, kt, len(tiles)))
            pend = []
            stage_s = {}

            def emit_S(item, slot):
                gid, q0, n, gkey, j, kt, nt_ = item
                S = Sps[slot % NS]
                p_ = pt[slot % NS]
                if j == 0:
                    sz = szt[gid % 3]
                    P.dma("sp", sz[:, 0:n], SC["SZT"][r0:r0 + 64, q0:q0 + n], writes=[sz])
                P.op("pe", "matmul", [k_, q_], [S], S[:, 0:n], k_[0:dq, kt * 128:(kt + 1) * 128], q_[0:dq, q0:q0 + n],
                     start=True, stop=True)
                bt = bias_tiles.get((gkey, kt)) if bias_tiles is not None else None
                if bt is not None:
                    sb = sb_t[slot % 3]
                    P.op("dve", "scalar_tensor_tensor", [S, bt], [sb], out=sb[:, 0:n], in0=S[:, 0:n], scalar=scale,
                         in1=bt[:, 0:n], op0=ALU.mult, op1=ALU.add)
                    P.op("act", "activation", [sb], [p_], out=p_[:, 0:n], in_=sb[:, 0:n], func=AF.Exp)
                else:
                    P.op("act", "activation", [S], [p_], out=p_[:, 0:n], in_=S[:, 0:n], func=AF.Exp, scale=scale)

            def emit_PV(item, slot):
                gid, q0, n, gkey, j, kt, nt_ = item
                O = Ops[gid % 2]
                p_ = pt[slot % NS]
                P.op("pe", "matmul", [V, p_], [O], O[0:65, 0:n], V[:, kt, h, :], p_[:, 0:n], start=(j == 0),
                     stop=(j == nt_ - 1))
                if j == nt_ - 1:
                    rd = rden[gid % 2]
                    P.op("dve", "reciprocal", [O], [rd], out=rd[64:65, 0:n], in_=O[64:65, 0:n])
                    pend.append((gid, q0, n))

            def emit_epi(gid, q0, n):
                O = Ops[gid % 2]
                rd = rden[gid % 2]
                bc_ = bcs[gid % 2]
                tm_ = tmp[gid % 2]
                a_ = ao[gid % 2]
                sz = szt[gid % 3]
                P.op("pe", "matmul", [ones_f, rd], [Bps], Bps[0:64, 0:n], ones_f[64:65, 0:64], rd[64:65, 0:n],
                     start=True, stop=True)
                P.op("act", "copy", [Bps], [bc_], out=bc_[0:64, 0:n], in_=Bps[0:64, 0:n])
                P.op("dve", "tensor_tensor", [O, bc_], [tm_], out=tm_[:, 0:n], in0=O[0:64, 0:n], in1=bc_[0:64, 0:n],
                     op=ALU.mult)
                P.op("pool", "tensor_tensor", [tm_, sz], [a_], out=a_[:, 0:n], in0=tm_[:, 0:n], in1=sz[:, 0:n], op=ALU.mult)
                P.dma("pool", SC["CATT"][r0:r0 + 64, q0:q0 + n], a_[:, 0:n], reads=[a_])

            nI = len(items)
            for idx in range(nI + LOOK):
                if idx < nI:
                    emit_S(items[idx], it + idx)
                if pend and (idx >= nI or True):
                    for e_ in pend[:]:
                        if e_[3] if False else True:
                            pass
                    ready = [e_ for e_ in pend if e_ in stage_s]
                    for e_ in ready:
                        emit_epi(*e_)
                        pend.remove(e_)
                        del stage_s[e_]
                    for e_ in pend:
                        stage_s[e_] = True
                if idx - LOOK >= 0:
                    emit_PV(items[idx - LOOK], it + idx - LOOK)
            for e_ in pend:
                emit_epi(*e_)
            pend.clear()
            it += nI
        P.barrier()


def mlstm_phase(nc, P, IN, SC, ident_bf, ident_f, ones_f):
    NB = NT
    NCH = T // 64
    with ExitStack() as es:
        A = Alloc(nc, es)
        esT = A.sb([128, NB, 64], F32, "esT")
        fT = A.sb([128, NB, 64], F32, "fT")
        decbc = A.sb([128, 8, NCH], F32, "decbc")
        mask = [A.sb([128, 64], F32, "mask%d" % d) for d in range(2)]
        for d in range(2):
            P.op("pool", "memset", [], [mask[d]], mask[d][:], 1.0)
            for half in range(2):
                pr = slice(half * 64, half * 64 + 64)
                P.op("pool", "affine_select", [mask[d]], [mask[d]], out=mask[d][pr, :], in_=mask[d][pr, :],
                     pattern=[[1 if d == 0 else -1, 64]], compare_op=ALU.is_ge, fill=0.0, base=0,
                     channel_multiplier=-1 if d == 0 else 1)
        with ExitStack() as es2:
            A2 = Alloc(nc, es2)
            X1 = A2.sb([64, T], F32, "X1")
            X2 = A2.sb([64, T], F32, "X2")
            X3 = A2.sb([64, T], F32, "X3")
            X4 = A2.sb([64, T], F32, "X4")
            gb = A2.sb([64, 2], F32, "gb")
            nbf = A2.sb([64, 1], F32, "nbf")
            one1 = A2.sb([64, 1], F32, "one1")
            dec = A2.sb([64, NCH], F32, "dec")
            aprev = A2.sb([64, NCH], F32, "aprev")
            sel = A2.sb([64, 128], F32, "sel")
            pst = [A2.ps([128, 8, 64], F32, "pst%d" % i) for i in range(2)]
            psd = A2.ps([128, NCH], F32, "psd")
            P.op("pool", "memset", [], [X1], X1[:], 0.0)
            P.op("pool", "memset", [], [X3], X3[:], 0.0)
            P.op("pool", "memset", [], [one1], one1[:], 1.0)
            P.dma("sp", gb[:], IN["l0_gb2"][:, :], writes=[gb])
            for d in range(2):
                P.dma("sp", X1[d * 32:d * 32 + 4, :], SC["GF"][d * 4:d * 4 + 4, :], writes=[X1])
                P.dma("pool", X3[d * 32:d * 32 + 4, :], SC["GI"][d * 4:d * 4 + 4, :], writes=[X3])
            P.op("dve", "tensor_scalar", [gb], [nbf], out=nbf[:], in0=gb[:, 1:2], scalar1=-1.0, scalar2=None, op0=ALU.mult)
            P.op("act", "activation", [X1, nbf], [X1], out=X1[:], in_=X1[:], func=AF.Exp, scale=-1.0, bias=nbf[:, 0:1])
            P.op("act", "activation", [X1, one1], [X1], out=X1[:], in_=X1[:], func=AF.Ln, bias=one1[:, 0:1])

            def seg_views(tile_, prng, d):
                if d == 0:
                    return [tile_[prng, 0:T]]
                return [tile_[prng, 0:TC][:, ::-1], tile_[prng, TC:T][:, ::-1]]

            def scan(dst, src, op0, d):
                prng = slice(d * 32, d * 32 + 32)
                dv = seg_views(dst, prng, d)
                sv = seg_views(src, prng, d)
                for i in range(len(dv)):
                    init = 0.0 if i == 0 else dst[prng, 0:1]
                    P.op("dve", "tensor_tensor_scan", [src, dst], [dst], out=dv[i], data0=sv[i], data1=sv[i],
                         initial=init, op0=op0, op1=ALU.bypass)

            for d in range(2):
                scan(X2, X1, ALU.add, d)
            P.op("dve", "scalar_tensor_tensor", [X3, gb, X2], [X3], out=X3[:], in0=X3[:], scalar=gb[:, 0:1], in1=X2[:],
                 op0=ALU.add, op1=ALU.add)
            for d in range(2):
                scan(X1, X3, ALU.max, d)
            for d in range(2):
                prng = slice(d * 32, d * 32 + 32)
                jj = 63 if d == 0 else 0
                P.op("dve", "tensor_copy", [X1], [X4], out=X4[prng, :].rearrange("p (c j) -> p c j", j=64),
                     in_=X1[prng, :].rearrange("p (c j) -> p c j", j=64)[:, :, jj:jj + 1].to_broadcast([32, NCH, 64]))
            aend = X4[:, :].rearrange("p (c j) -> p c j", j=64)[:, :, 0]
            P.op("pool", "memset", [], [aprev], aprev[:], 0.0)
            P.op("dve", "tensor_copy", [X4], [aprev], out=aprev[0:32, 1:NCH], in_=aend[0:32, 0:NCH - 1])
            P.op("dve", "tensor_copy", [X4], [aprev], out=aprev[32:64, 0:3], in_=aend[32:64, 1:4])
            P.op("dve", "tensor_copy", [X4], [aprev], out=aprev[32:64, 4:NCH - 1], in_=aend[32:64, 5:NCH])
            P.op("dve", "tensor_copy", [X4], [aprev], out=aprev[32:64, NCH - 1:NCH], in_=aend[32:64, 0:1])
            P.op("dve", "tensor_tensor", [aprev, X4], [dec], out=dec[:], in0=aprev[:], in1=aend, op=ALU.subtract)
            P.op("act", "activation", [dec], [dec], out=dec[:], in_=dec[:], func=AF.Exp)
            P.op("dve", "tensor_tensor", [X3, X4], [X3], out=X3[:], in0=X3[:], in1=X4[:], op=ALU.subtract)
            P.op("act", "activation", [X3], [X3], out=X3[:], in_=X3[:], func=AF.Exp)
            P.op("dve", "tensor_tensor", [X2, X4], [X2], out=X2[:], in0=X2[:], in1=X4[:], op=ALU.subtract)
            P.op("act", "activation", [X2], [X2], out=X2[:], in_=X2[:], func=AF.Exp)
            for (srcX, dstT) in ((X3, esT), (X2, fT)):
                for b0 in range(0, NB, 8):
                    nb = min(8, NB - b0)
                    ps = pst[(b0 // 8) % 2]
                    for bb in range(nb):
                        P.op("pe", "transpose", [srcX, ident_f], [ps], ps[:, bb, :], srcX[:, (b0 + bb) * 128:(b0 + bb + 1) * 128],
                             ident_f[0:64, 0:64])
                    P.op("act", "copy", [ps], [dstT], out=dstT[:, b0:b0 + nb, :], in_=ps[:, 0:nb, :])
            for idx in range(8):
                r = (idx // 4) * 32 + idx % 4
                P.op("dve", "tensor_copy", [ident_f], [sel], out=sel[:], in_=ident_f[0:64, r:r + 1].to_broadcast([64, 128]))
                P.op("pe", "matmul", [sel, dec], [psd], psd[:, :], sel[:, :], dec[:, :], start=True, stop=True)
                P.op("act", "copy", [psd], [decbc], out=decbc[:, idx, :], in_=psd[:, :])
            P.barrier()
        P.op("dve", "tensor_scalar", [esT], [esT], out=esT[:], in0=esT[:], scalar1=128.0 ** -0.5, scalar2=None, op0=ALU.mult)
        xraw = A.sb([128, T], F32, "xraw")
        cvw = A.sb([128, 8, 4], F32, "cvw")
        P.dma("sp", cvw[:], IN["l0_convT"][:, :, :], writes=[cvw])
        dg = [A.sb([128, 3, 128], F32, "dg%d" % i) for i in range(2)]
        qT = A.sb([128, T], BF16, "mqT")
        qd = [A.sb([128, T], BF16, "mqd%d" % d) for d in range(2)]
        kT = A.sb([128, T], BF16, "mkT")
        kTok = A.sb([128, NB, 128], BF16, "kTok")
        vtok = A.sb([128, NB, 128], BF16, "vtok")
        vpp = [A.sb([128, NB, 129], BF16, "vpp%d" % d) for d in range(2)]
        SmT = [A.sb([128, NB, 64], BF16, "SmT%d" % d) for d in range(2)]
        hbuf = [A.sb([128, NB, 129], F32, "hbuf%d" % d) for d in range(2)]
        Cst = [[A.sb([128, 129], F32, "C%d_%d" % (d, i)) for i in range(2)] for d in range(2)]
        Cb = [[A.sb([128, 129], BF16, "Cb%d_%d" % (d, i)) for i in range(2)] for d in range(2)]
        dn = [A.sb([128, NB], F32, "dn%d" % d) for d in range(2)]
        pcv = [A.ps([128, 512], F32, "pcv%d" % i) for i in range(2)]
        pU = [A.ps([128, 129], F32, "pU%d" % i) for i in range(2)]
        pN = [[A.ps([128, 129], F32, "pN%d_%d" % (d, i)) for i in range(2)] for d in range(2)]
        order = [list(range(NCH)), [3, 2, 1, 0] + list(range(NCH - 1, 3, -1))]
        pieces = [(0, TC)] + [(TC + 512 * i, TC + 512 * (i + 1)) for i in range(8)]
        pc = 0
        for h in range(4):
            for which in range(2):
                ch = which * 4 + h
                dg_ = dg[which]
                P.dma("sp" if which == 0 else "pool", xraw[:], SC["MQK"][ch * 128:(ch + 1) * 128, :], writes=[xraw])
                for j in range(3):
                    P.op("dve", "tensor_scalar", [ident_f, cvw], [dg_], out=dg_[:, j, :], in0=ident_f[:], scalar1=cvw[:, ch, j:j + 1],
                         scalar2=None, op0=ALU.mult)
                dst = qT if which == 0 else kT
                for (a, b) in pieces:
                    s0, s1 = (0, TC) if a < TC else (TC, T)
                    ps = pcv[pc % 2]
                    pc += 1
                    P.op("pe", "matmul", [dg_, xraw], [ps], ps[:, 0:b - a], dg_[:, 1, :], xraw[:, a:b], start=True, stop=False)
                    lo = max(a, s0 + 1)
                    P.op("pe", "matmul", [dg_, xraw], [ps], ps[:, lo - a:b - a], dg_[:, 0, :], xraw[:, lo - 1:b - 1], start=False,
                         stop=False)
                    hi = min(b, s1 - 1)
                    P.op("pe", "matmul", [dg_, xraw], [ps], ps[:, 0:hi - a], dg_[:, 2, :], xraw[:, a + 1:hi + 1], start=False,
                         stop=True)
                    P.op("act", "activation", [ps, cvw], [dst], out=dst[:, a:b], in_=ps[:, 0:b - a], func=AF.Silu,
                         bias=cvw[:, ch, 3:4])
            for b0 in range(0, NB, 4):
                nb = min(4, NB - b0)
                ps = pcv[pc % 2]
                pc += 1
                psb = ps[:, 0:256].bitcast(BF16)
                for bb in range(nb):
                    P.op("pe", "transpose", [kT, ident_bf], [ps], psb[:, bb * 128:(bb + 1) * 128],
                         kT[:, (b0 + bb) * 128:(b0 + bb + 1) * 128], ident_bf[:])
                P.op("act", "copy", [ps], [kTok], out=kTok[:, b0:b0 + nb, :],
                     in_=psb[:, 0:nb * 128].rearrange("p (b j) -> p b j", j=128))
            P.dma("sp", vtok[:], SC["MV"][:, h * 128:(h + 1) * 128].rearrange("(b p) j -> p b j", p=128), writes=[vtok])
            for d in range(2):
                col = d * 32 + h
                idx = d * 4 + h
                P.op("pool" if d == 0 else "dve", "tensor_tensor", [vtok, esT], [vpp[d]], out=vpp[d][:, :, 0:128], in0=vtok[:],
                     in1=esT[:, :, col:col + 1].to_broadcast([128, NB, 128]), op=ALU.mult)
                P.op("dve", "tensor_copy", [esT], [vpp[d]], out=vpp[d][:, :, 128:129], in_=esT[:, :, col:col + 1])
                P.op("pool" if d == 1 else "dve", "tensor_tensor", [qT, decbc], [qd[d]],
                     out=qd[d][:, :].rearrange("p (c j) -> p c j", j=64), in0=qT[:, :].rearrange("p (c j) -> p c j", j=64),
                     in1=decbc[:, idx, :].unsqueeze(2).to_broadcast([128, NCH, 64]), op=ALU.mult)
            for b0 in range(0, NB, 4):
                nb = min(4, NB - b0)
                ps = pcv[pc % 2]
                pc += 1
                psv = ps[:, 0:256].rearrange("p (b j) -> p b j", j=64)
                for bb in range(nb):
                    for half in range(2):
                        c = (b0 + bb) * 2 + half
                        pr = slice(half * 64, half * 64 + 64)
                        P.op("pe", "matmul", [kT, qT], [ps], psv[pr, bb, :], kT[:, c * 64:(c + 1) * 64], qT[:, c * 64:(c + 1) * 64],
                             start=True, stop=True)
                for d in range(2):
                    P.op("dve", "tensor_tensor", [ps, mask[d]], [SmT[d]], out=SmT[d][:, b0:b0 + nb, :], in0=psv[:, 0:nb, :],
                         in1=mask[d][:].unsqueeze(1).to_broadcast([128, nb, 64]), op=ALU.mult)
            for d in range(2):
                P.op("pool", "memset", [], [Cst[d][0]], Cst[d][0][:], 0.0)
                P.op("pool", "memset", [], [Cb[d][0]], Cb[d][0][:], 0.0)
            for i in range(NCH):
                for d in range(2):
                    c = order[d][i]
                    b, half = c // 2, c % 2
                    pr = slice(half * 64, half * 64 + 64)
                    idx = d * 4 + h
                    Cold, Cnew = Cst[d][i % 2], Cst[d][(i + 1) % 2]
                    cbo, cbn = Cb[d][i % 2], Cb[d][(i + 1) % 2]
                    U = pU[d]
                    N = pN[d][i % 2]
                    P.op("pe", "matmul", [kTok, vpp[d]], [U], U[:, :], kTok[pr, b, :], vpp[d][pr, b, :], start=True, stop=True)
                    P.op("pe", "matmul", [SmT[d], vpp[d]], [N], N[pr, :], SmT[d][pr, b, :], vpp[d][pr, b, :], start=True,
                         stop=False)
                    P.op("pe", "matmul", [qd[d], cbo], [N], N[pr, :], qd[d][:, c * 64:(c + 1) * 64], cbo[:], start=False, stop=True)
                    P.op("dve", "scalar_tensor_tensor", [Cold, decbc, U], [cbn], out=cbn[:], in0=Cold[:],
                         scalar=decbc[:, idx, c:c + 1], in1=U[:, :], op0=ALU.mult, op1=ALU.add)
                    P.op("dve", "scalar_tensor_tensor", [Cold, decbc, U], [Cnew], out=Cnew[:], in0=Cold[:],
                         scalar=decbc[:, idx, c:c + 1], in1=U[:, :], op0=ALU.mult, op1=ALU.add)
                    P.op("act", "copy", [N], [hbuf[d]], out=hbuf[d][pr, b, :], in_=N[pr, :])
            for d in range(2):
                col = d * 32 + h
                P.op("act", "activation", [hbuf[d]], [dn[d]], out=dn[d][:, :].unsqueeze(2), in_=hbuf[d][:, :, 128:129], func=AF.Abs)
                P.op("dve", "tensor_tensor", [dn[d], fT], [dn[d]], out=dn[d][:, :].unsqueeze(2), in0=dn[d][:, :].unsqueeze(2),
                     in1=fT[:, :, col:col + 1], op=ALU.max)
                P.op("dve", "reciprocal", [dn[d]], [dn[d]], out=dn[d][:], in_=dn[d][:])
                P.op("dve" if d == 0 else "pool", "tensor_tensor", [hbuf[d], dn[d]], [hbuf[d]], out=hbuf[d][:, :, 0:128],
                     in0=hbuf[d][:, :, 0:128], in1=dn[d][:, :].unsqueeze(2).to_broadcast([128, NB, 128]), op=ALU.mult)
                P.dma("sp" if d == 0 else "pool", SC["HM"][d, :, h * 128:(h + 1) * 128].rearrange("(b p) j -> p b j", p=128),
                      hbuf[d][:, :, 0:128], reads=[hbuf[d]])
        P.barrier()


def combine_phase(nc, P, IN, SC, ident_bf, src0, src1, mul, norm_name, cat_row0, groups):
    with ExitStack() as es:
        A = Alloc(nc, es)
        nrow = A.sb([128, 512], F32, "nrow")
        P.dma("sp", nrow[:], IN[norm_name][0:1, :].to_broadcast([128, 512]), writes=[nrow])
        epsT = A.sb([128, 1], F32, "epsTc")
        P.op("pool", "memset", [], [epsT], epsT[:], EPS)
        a_ = [A.sb([128, 512], F32, "cA%d" % i) for i in range(2)]
        b_ = [A.sb([128, 512], F32, "cB%d" % i) for i in range(2)]
        m_ = [A.sb([128, 512], BF16, "cM%d" % i) for i in range(2)]
        junk = A.sb([128, 128], F32, "cjunk")
        ss = [A.sb([128, 4], F32, "css%d" % i) for i in range(2)]
        hn = [A.sb([128, 512], F32, "chn%d" % i) for i in range(2)]
        hb = [A.sb([128, 512], BF16, "chb%d" % i) for i in range(2)]
        sz = [A.sb([128, 512], BF16, "csz%d" % i) for i in range(2)]
        oo = [A.sb([128, 512], BF16, "coo%d" % i) for i in range(2)]
        tp = [A.ps([128, 512], BF16, "ctp%d" % i) for i in range(4)]
        it = 0
        for (t0, n) in groups:
            ntl = n // 128
            for ti in range(ntl):
                tok = t0 + ti * 128
                a, b, m, s_, h_, hb_ = a_[it % 2], b_[it % 2], m_[it % 2], ss[it % 2], hn[it % 2], hb[it % 2]
                it += 1
                P.dma("sp", a[:], src0[tok:tok + 128, :], writes=[a])
                P.dma("pool", b[:], src1[tok:tok + 128, :], writes=[b])
                P.op("dve", "tensor_tensor", [a, b], [a], out=a[:], in0=a[:], in1=b[:], op=ALU.add)
                if mul is not None:
                    P.dma("sp", m[:], mul[tok:tok + 128, :], writes=[m])
                    P.op("pool", "tensor_tensor", [a, m], [a], out=a[:], in0=a[:], in1=m[:], op=ALU.mult)
                for hh in range(4):
                    P.op("act", "activation", [a], [junk, s_], out=junk[:], in_=a[:, hh * 128:(hh + 1) * 128], func=AF.Square,
                         accum_out=s_[:, hh:hh + 1])
                P.op("act", "activation", [s_, epsT], [s_], out=s_[:], in_=s_[:], func=AF.Sqrt, scale=1.0 / 128, bias=epsT[:, 0:1])
                P.op("dve", "reciprocal", [s_], [s_], out=s_[:], in_=s_[:])
                for hh in range(4):
                    P.op("act", "activation", [a, s_], [h_], out=h_[:, hh * 128:(hh + 1) * 128], in_=a[:, hh * 128:(hh + 1) * 128],
                         func=AF.Copy, scale=s_[:, hh:hh + 1])
                P.op("dve", "tensor_tensor", [h_, nrow], [hb_], out=hb_[:], in0=h_[:], in1=nrow[:], op=ALU.mult)
                for j in range(4):
                    P.op("pe", "transpose", [hb_, ident_bf], [tp[j]], tp[j][:, ti * 128:(ti + 1) * 128],
                         hb_[:, j * 128:(j + 1) * 128], ident_bf[:])
            for j in range(4):
                r0 = cat_row0 + j * 128
                z_, o_ = sz[j % 2], oo[j % 2]
                P.dma("sp", z_[:, 0:n], SC["SZT"][r0:r0 + 128, t0:t0 + n], writes=[z_])
                P.op("dve", "tensor_tensor", [tp[j], z_], [o_], out=o_[:, 0:n], in0=tp[j][:, 0:n], in1=z_[:, 0:n], op=ALU.mult)
                P.dma("pool", SC["CATT"][r0:r0 + 128, t0:t0 + n], o_[:, 0:n], reads=[o_])
        P.barrier()


def phase_C(nc, P, IN, SC, layer, gateR, out_ap):
    with ExitStack() as es:
        A = Alloc(nc, es)
        w = A.sb([128, 8, 1024], BF16, "w_out")
        with ExitStack() as es2:
            A2 = Alloc(nc, es2)
            stg = [A2.sb([128, 8, 512], F32, "wstg%d" % i) for i in range(2)]
            for i in range(2):
                P.dma("sp" if i == 0 else "pool", stg[i][:],
                      IN["l%d_w_out" % layer][:, i * 512:(i + 1) * 512].rearrange("(k p) n -> p k n", p=128), writes=[stg[i]])
                P.op("dve" if i == 0 else "act", "tensor_copy" if i == 0 else "copy", [stg[i]], [w], out=w[:, :, i * 512:(i + 1) * 512],
                     in_=stg[i][:])
            P.barrier()
        cat = [A.sb([128, 8, 512], BF16, "catT%d" % i) for i in range(2)]
        hold = [A.sb([128, 1024], F32, "hold%d" % i) for i in range(2)]
        tmp = [A.sb([128, 1024], F32, "ctmp%d" % i) for i in range(2)]
        hnew = [A.sb([128, 1024], F32, "hnew%d" % i) for i in range(2)]
        ps = [A.ps([128, 512], F32, "yps%d" % i) for i in range(4)]
        if layer == 1:
            frow = A.sb([128, 1024], F32, "frow")
            P.dma("sp", frow[:], IN["final_norm"][0:1, :].to_broadcast([128, 1024]), writes=[frow])
            epsT = A.sb([128, 1], F32, "epsTf")
            P.op("pool", "memset", [], [epsT], epsT[:], EPS)
            junk = A.sb([128, 1024], F32, "fjunk")
            st = [A.sb([128, 1], F32, "fst%d" % i) for i in range(2)]
            ob = [A.sb([128, 1024], F32, "fob%d" % i) for i in range(2)]
        it = 0
        groups = GROUPS if layer == 0 else GROUPS[1:]
        for gi, (t0, n) in enumerate(groups):
            c_ = cat[gi % 2]
            for k2 in range(2):
                P.dma("sp" if k2 == 0 else "pool", c_[:, k2 * 4:(k2 + 1) * 4, 0:n],
                      SC["CATT"][k2 * 512:(k2 + 1) * 512, t0:t0 + n].rearrange("(k p) t -> p k t", p=128), writes=[c_])
            g_ = gateR[1] if (layer == 0 and t0 == 0) else gateR[0]
            for ti in range(n // 128):
                tok = t0 + ti * 128
                ho, tm, hn_ = hold[it % 2], tmp[it % 2], hnew[it % 2]
                if layer == 0:
                    srcp = IN["ctx"][tok:tok + 128, :] if t0 == 0 else IN["x"][tok - TC:tok - TC + 128, :]
                else:
                    srcp = SC["H1"][tok:tok + 128, :]
                P.dma("sp", ho[:], srcp, writes=[ho])
                for half in range(2):
                    p_ = ps[(it * 2 + half) % 4]
                    for k in range(8):
                        P.op("pe", "matmul", [c_, w], [p_], p_[:, :], c_[:, k, ti * 128:(ti + 1) * 128],
                             w[:, k, half * 512:(half + 1) * 512], start=(k == 0), stop=(k == 7))
                    P.op("dve", "tensor_tensor", [p_, g_], [tm], out=tm[:, half * 512:(half + 1) * 512], in0=p_[:, :],
                         in1=g_[:, half * 512:(half + 1) * 512], op=ALU.mult)
                P.op("pool", "tensor_tensor", [tm, ho], [hn_], out=hn_[:], in0=tm[:], in1=ho[:], op=ALU.add)
                if layer == 0:
                    P.dma("pool", SC["H1"][tok:tok + 128, :], hn_[:], reads=[hn_])
                else:
                    s_, o_ = st[it % 2], ob[it % 2]
                    P.op("act", "activation", [hn_], [junk, s_], out=junk[:], in_=hn_[:], func=AF.Square, accum_out=s_[:, 0:1])
                    P.op("act", "activation", [s_, epsT], [s_], out=s_[:], in_=s_[:], func=AF.Sqrt, scale=1.0 / D, bias=epsT[:, 0:1])
                    P.op("dve", "reciprocal", [s_], [s_], out=s_[:], in_=s_[:])
                    P.op("act", "activation", [hn_, s_], [o_], out=o_[:], in_=hn_[:], func=AF.Copy, scale=s_[:, 0:1])
                    P.op("dve", "tensor_tensor", [o_, frow], [o_], out=o_[:], in0=o_[:], in1=frow[:], op=ALU.mult)
                    P.dma("pool", out_ap[tok - TC:tok - TC + 128, :], o_[:], reads=[o_])
                it += 1
        P.barrier()


def gla_phase(nc, P, IN, SC, ident_bf):
    NB = NT
    NCH = T // 64
    with ExitStack() as es:
        A = Alloc(nc, es)
        mask = [A.sb([128, 64], F32, "gmask%d" % d) for d in range(2)]
        for d in range(2):
            P.op("pool", "memset", [], [mask[d]], mask[d][:], 1.0)
            for half in range(2):
                pr = slice(half * 64, half * 64 + 64)
                P.op("pool", "affine_select", [mask[d]], [mask[d]], out=mask[d][pr, :], in_=mask[d][pr, :],
                     pattern=[[1 if d == 0 else -1, 64]], compare_op=ALU.is_ge, fill=0.0, base=0,
                     channel_multiplier=-1 if d == 0 else 1)
        rm = [A.sb([64, T], F32, "rm%d" % d) for d in range(2)]
        for d in range(2):
            P.op("pool", "memset", [], [rm[d]], rm[d][:], 1.0)
            j0 = 0 if d == 0 else 63
            P.op("pool", "memset", [rm[d]], [rm[d]], rm[d][:, :].rearrange("p (c j) -> p c j", j=64)[:, :, j0:j0 + 1], 0.0)
        qf = A.sb([64, T], F32, "gqf")
        kf = A.sb([64, T], F32, "gkf")
        lg = A.sb([64, T], F32, "glg")
        bc = A.sb([64, T], F32, "gbc")
        tmp = A.sb([64, T], F32, "gtmp")
        qg = [A.sb([64, T], BF16, "qg%d" % d) for d in range(2)]
        kg = [A.sb([64, T], BF16, "kg%d" % d) for d in range(2)]
        kbf = A.sb([64, T], BF16, "kbf")
        eb = [A.sb([64, NCH], F32, "eb%d" % d) for d in range(2)]
        kbTok = [A.sb([128, NB, 64], BF16, "kbTok%d" % d) for d in range(2)]
        SmT = [A.sb([128, NB, 64], BF16, "gSmT%d" % d) for d in range(2)]
        vtok = A.sb([128, NB, 128], BF16, "gvtok")
        obuf = [[A.sb([128, 128], F32, "gob%d_%d" % (d, i)) for i in range(3)] for d in range(2)]
        Sst = [[A.sb([64, 128], F32, "S%d_%d" % (d, i)) for i in range(2)] for d in range(2)]
        Sb = [[A.sb([64, 128], BF16, "Sb%d_%d" % (d, i)) for i in range(2)] for d in range(2)]
        ptr = [A.ps([128, 8, 64], BF16, "gptr%d" % i) for i in range(2)]
        pS = [A.ps([128, 8, 64], F32, "gpS%d" % i) for i in range(2)]
        pU = [A.ps([128, 128], F32, "gpU%d" % i) for i in range(2)]
        pN = [A.ps([128, 128], F32, "gpN%d" % i) for i in range(2)]
        order = [list(range(NCH)), [3, 2, 1, 0] + list(range(NCH - 1, 3, -1))]
        for h in range(4):
            P.dma("sp", qf[:], SC["MQK"][h * 64:(h + 1) * 64, :], writes=[qf])
            P.dma("pool", kf[:], SC["MQK"][256 + h * 64:256 + (h + 1) * 64, :], writes=[kf])
            P.dma("sp", vtok[:], SC["MV"][:, h * 128:(h + 1) * 128].rearrange("(b p) j -> p b j", p=128), writes=[vtok])
            for d in range(2):
                P.dma("pool", lg[:], SC["LG"][d, h * 64:(h + 1) * 64, :], writes=[lg])
                if d == 0:
                    P.op("dve", "tensor_tensor_scan", [rm[d], lg], [bc], out=bc[:, :], data0=rm[d][:, :], data1=lg[:, :],
                         initial=0.0, op0=ALU.mult, op1=ALU.add)
                else:
                    P.op("dve", "tensor_tensor_scan", [rm[d], lg], [bc], out=bc[:, ::-1], data0=rm[d][:, ::-1],
                         data1=lg[:, ::-1], initial=0.0, op0=ALU.mult, op1=ALU.add)
                jl = 63 if d == 0 else 0
                bl = bc[:, :].rearrange("p (c j) -> p c j", j=64)[:, :, jl:jl + 1]
                P.op("act", "activation", [bc], [eb[d]], out=eb[d][:, :].unsqueeze(2), in_=bl, func=AF.Exp)
                P.op("act", "activation", [bc], [tmp], out=tmp[:], in_=bc[:], func=AF.Exp)
                P.op("dve", "scalar_tensor_tensor", [qf, tmp], [qg[d]], out=qg[d][:], in0=qf[:], scalar=64.0 ** -0.5, in1=tmp[:],
                     op0=ALU.mult, op1=ALU.mult)
                P.op("act", "activation", [bc], [tmp], out=tmp[:], in_=bc[:], func=AF.Exp, scale=-1.0)
                P.op("dve", "tensor_tensor", [kf, tmp], [kg[d]], out=kg[d][:], in0=kf[:], in1=tmp[:], op=ALU.mult)
                P.op("dve", "tensor_tensor", [bc], [tmp], out=tmp[:, :].rearrange("p (c j) -> p c j", j=64),
                     in0=bl.to_broadcast([64, NCH, 64]), in1=bc[:, :].rearrange("p (c j) -> p c j", j=64), op=ALU.subtract)
                P.op("act", "activation", [tmp], [tmp], out=tmp[:], in_=tmp[:], func=AF.Exp)
                P.op("dve", "tensor_tensor", [kf, tmp], [kbf], out=kbf[:], in0=kf[:], in1=tmp[:], op=ALU.mult)
                for b0 in range(0, NB, 8):
                    nb = min(8, NB - b0)
                    ps = ptr[(b0 // 8) % 2]
                    for bb in range(nb):
                        P.op("pe", "transpose", [kbf, ident_bf], [ps], ps[:, bb, :], kbf[:, (b0 + bb) * 128:(b0 + bb + 1) * 128],
                             ident_bf[0:64, 0:64])
                    P.op("act", "copy", [ps], [kbTok[d]], out=kbTok[d][:, b0:b0 + nb, :], in_=ps[:, 0:nb, :])
                for b0 in range(0, NB, 8):
                    nb = min(8, NB - b0)
                    ps = pS[(b0 // 8) % 2]
                    for bb in range(nb):
                        for half in range(2):
                            c = (b0 + bb) * 2 + half
                            pr = slice(half * 64, half * 64 + 64)
                            P.op("pe", "matmul", [kg[d], qg[d]], [ps], ps[pr, bb, :], kg[d][:, c * 64:(c + 1) * 64],
                                 qg[d][:, c * 64:(c + 1) * 64], start=True, stop=True)
                    P.op("dve", "tensor_tensor", [ps, mask[d]], [SmT[d]], out=SmT[d][:, b0:b0 + nb, :], in0=ps[:, 0:nb, :],
                         in1=mask[d][:].unsqueeze(1).to_broadcast([128, nb, 64]), op=ALU.mult)
            for d in range(2):
                P.op("pool", "memset", [], [Sst[d][0]], Sst[d][0][:], 0.0)
                P.op("pool", "memset", [], [Sb[d][0]], Sb[d][0][:], 0.0)
            for i in range(NCH):
                for d in range(2):
                    c = order[d][i]
                    b, half = c // 2, c % 2
                    pr = slice(half * 64, half * 64 + 64)
                    Sold, Snew = Sst[d][i % 2], Sst[d][(i + 1) % 2]
                    sbo, sbn = Sb[d][i % 2], Sb[d][(i + 1) % 2]
                    U, N = pU[d], pN[d]
                    ob = obuf[d][(i // 2) % 3]
                    P.op("pe", "matmul", [SmT[d], vtok], [N], N[pr, :], SmT[d][pr, b, :], vtok[pr, b, :], start=True, stop=False)
                    P.op("pe", "matmul", [qg[d], sbo], [N], N[pr, :], qg[d][:, c * 64:(c + 1) * 64], sbo[:, :], start=False,
                         stop=True)
                    P.op("pe", "matmul", [kbTok[d], vtok], [U], U[0:64, :], kbTok[d][pr, b, :], vtok[pr, b, :], start=True,
                         stop=True)
                    P.op("dve", "scalar_tensor_tensor", [Sold, eb[d], U], [Snew], out=Snew[:], in0=Sold[:],
                         scalar=eb[d][:, c:c + 1], in1=U[0:64, :], op0=ALU.mult, op1=ALU.add)
                    P.op("pool", "tensor_copy", [Snew], [sbn], out=sbn[:], in_=Snew[:])
                    P.op("act", "copy", [N], [ob], out=ob[pr, :], in_=N[pr, :])
                    if i % 2 == 1:
                        P.dma("sp" if d == 0 else "pool", SC["HM"][d, b * 128:(b + 1) * 128, h * 128:(h + 1) * 128], ob[:],
                              reads=[ob])
        P.barrier()


def na_bias_fn(nc, P, IN, state):
    def rows_ok(qr, kr):
        lo = min(max(qr - 4, 0), 56)
        return lo <= kr < lo + 8

    def fn(h, A):
        if A is not None:
            state["sets"] = {}
            for key, ntile in (("first", 6), ("mid", 8), ("last", 6)):
                state["sets"][key] = [A.sb([128, 512], F32, "nab_%s%d" % (key, i)) for i in range(ntile)]
        out = {}
        for key, g, t_lo in (("first", 0, 0), ("mid", 1, 2), ("last", 7, 26)):
            tiles = state["sets"][key]
            for r, bt in enumerate(tiles):
                ktl = t_lo + r
                P.op("pool", "memset", [], [bt], bt[:], MASKV)
                for i in range(2):
                    kr = 2 * ktl + i
                    js = [j for j in range(8) if rows_ok(8 * g + j, kr)]
                    if not js:
                        continue
                    j0, j1 = js[0], js[-1]
                    assert js == list(range(j0, j1 + 1))
                    m0 = 7 - (kr - 8 * g - j0)
                    nj = j1 - j0 + 1
                    P.dma("sp" if i == 0 else "pool", bt[i * 64:(i + 1) * 64, j0 * 64:(j1 + 1) * 64].rearrange("p (m q) -> p m q", q=64),
                          IN["na_bias"][h, m0:m0 + nj, :, :].rearrange("m k q -> k m q"), writes=[bt])
        for g in range(8):
            if g == 0:
                key, t_lo, nt_ = "first", 0, 6
            elif g == 7:
                key, t_lo, nt_ = "last", 26, 6
            else:
                key, t_lo, nt_ = "mid", 4 * g - 2, 8
            for r in range(nt_):
                out[(g, 2 + t_lo + r)] = state["sets"][key][r]
        return out

    return fn


def _shapes(d):
    return {k: (v.shape, "bf16" if v.dtype == ml_dtypes.bfloat16 else "f32") for k, v in d.items()}


def run(inputs, stage=99, debug=(), cores=8, skip=()):
    inputs = {k: np.asarray(v) for k, v in inputs.items()}
    sh, per = prep_inputs(inputs)
    nc = build(_shapes(sh), _shapes(per[0]), stage=stage, debug=debug, skip=skip)
    in_maps = [dict(sh, **per[b]) for b in range(cores)]
    res = run_bass_kernel_spmd(nc, in_maps, core_ids=list(range(cores)))
    return res


def kernel(**inputs):
    res = run(inputs)
    return np.stack([np.asarray(r["out"], dtype=np.float32) for r in res.results], axis=0)
```

```python
import numpy as np
from contextlib import ExitStack
import ml_dtypes
import concourse.bass as bass
import concourse.mybir as mybir
from concourse.bass_utils import run_bass_kernel_spmd

F32 = mybir.dt.float32
BF16 = mybir.dt.bfloat16
AF = mybir.ActivationFunctionType
ALU = mybir.AluOpType
AX = mybir.AxisListType

D = 1024
TC = 256
TL = 4096
T = TC + TL
NT = T // 128
EPS = 1e-6
MASKV = -30000.0

GROUPS = [(0, 256)] + [(256 + 512 * i, 512) for i in range(8)]


class Dep:
    __slots__ = ("w", "r")

    def __init__(self):
        self.w = None
        self.r = {}


class Tile:
    def __init__(self, t):
        self.t = t
        self.d = Dep()

    def __getitem__(self, k):
        return self.t[k]


class Prog:
    def __init__(self, nc, es):
        self.nc = nc
        self.eng = {"pe": nc.tensor, "act": nc.scalar, "dve": nc.vector, "pool": nc.gpsimd, "sp": nc.sync}
        self.R = 12
        self.keys = [("pe", "c"), ("act", "c"), ("dve", "c"), ("pool", "c")]
        for q in ("sp", "pool"):
            self.keys += [(q, "d%d" % i) for i in range(self.R)]
        self.ndma = {"sp": 0, "pool": 0}
        self.sem = {k: es.enter_context(nc.semaphore("s_%s_%s" % k)) for k in self.keys}
        self.cnt = {k: 0 for k in self.keys}
        self.waited = {e: {} for e in self.eng}
        self.n = 0

    def _emit(self, eng, kind, fn, reads, writes):
        if kind == "d":
            kind = "d%d" % (self.ndma[eng] % self.R)
            self.ndma[eng] += 1
        key = (eng, kind)
        deps = {}
        if kind != "c" and self.cnt[key] > 0:
            deps[key] = self.cnt[key]

        def add(tok):
            if tok is None:
                return
            k, v = tok
            if deps.get(k, 0) < v:
                deps[k] = v

        for b in reads:
            add(b.d.w)
        for b in writes:
            add(b.d.w)
            for k, v in b.d.r.items():
                add((k, v))
        e = self.eng[eng]
        wd = self.waited[eng]
        for k, v in deps.items():
            if k == ("pe", "c") and eng == "pe":
                continue
            if wd.get(k, 0) >= v:
                continue
            e.wait_ge(self.sem[k], v)
            wd[k] = v
        inc = 16 if kind != "c" else 1
        self.cnt[key] += inc
        fn(e).then_inc(self.sem[key], inc)
        v = self.cnt[key]
        for b in reads:
            if b.d.r.get(key, 0) < v:
                b.d.r[key] = v
        for b in writes:
            b.d.w = (key, v)
            b.d.r = {}
        self.n += 1

    def op(self, eng, name, reads, writes, *a, **kw):
        self._emit(eng, "c", lambda e: getattr(e, name)(*a, **kw), reads, writes)

    def dma(self, q, out, in_, reads=(), writes=(), **kw):
        self._emit(q, "d", lambda e: e.dma_start(out=out, in_=in_, **kw), reads, writes)

    def barrier(self):
        for en, e in self.eng.items():
            wd = self.waited[en]
            for k in self.keys:
                v = self.cnt[k]
                if v > 0 and wd.get(k, 0) < v:
                    e.wait_ge(self.sem[k], v)
                    wd[k] = v


class Alloc:
    def __init__(self, nc, es):
        self.nc = nc
        self.es = es
        _CTR.setdefault(id(nc), 0)

    def _nm(self, name):
        _CTR[id(self.nc)] = _CTR.get(id(self.nc), 0) + 1
        return "%s_%d" % (name, _CTR[id(self.nc)])

    def sb(self, shape, dt, name=None):
        return Tile(self.es.enter_context(self.nc.sbuf_tensor(self._nm(name or "sb"), list(shape), dt)))

    def ps(self, shape, dt, name=None):
        return Tile(self.es.enter_context(self.nc.psum_tensor(self._nm(name or "ps"), list(shape), dt)))


_CTR = {}


def _fm(v, nchunk):
    return np.ascontiguousarray(v.reshape(nchunk, 128).T)


def _rope_perm():
    perm = np.zeros(32, np.int64)
    for i in range(32):
        r = i % 16
        perm[i] = i + 8 if r < 8 else i - 8
    return perm


def _rope_tables():
    t = np.arange(TL)
    inv = (1.0 / (10000.0 ** (np.arange(8, dtype=np.float32) / 8))).astype(np.float32)
    pos = [(t // 64).astype(np.float32), (t % 64).astype(np.float32)]
    C = np.zeros((32, TL), np.float32)
    S = np.zeros((32, TL), np.float32)
    for i in range(32):
        a = i // 16
        r = i % 16
        p = r % 8
        ang = (pos[a] * inv[p]).astype(np.float32)
        C[i] = np.cos(ang)
        S[i] = -np.sin(ang) if r < 8 else np.sin(ang)
    Cf = np.zeros((128, TL), np.float32)
    Sf = np.zeros((128, TL), np.float32)
    Cf[0:32] = C
    Cf[64:96] = C
    Sf[0:32] = S
    Sf[64:96] = S
    return Cf, Sf


def prep_inputs(inp):
    sh = {}
    sh["ident_bf"] = np.eye(128, dtype=np.float32).astype(ml_dtypes.bfloat16)
    sh["ident_f"] = np.eye(128, dtype=np.float32)
    perm = _rope_perm()
    w_in = inp["l0_w_in"]
    gi_cols = [2720 + d * 8 + h for d in range(2) for h in range(4)]
    gf_cols = [2720 + d * 8 + 4 + h for d in range(2) for h in range(4)]
    sh["l0_w_in"] = np.ascontiguousarray(
        np.concatenate([w_in, w_in[:, 640:672][:, perm], w_in[:, gi_cols], w_in[:, gf_cols]], axis=1))
    w_uq = inp["l0_mla_w_uq"].reshape(384, 8, 96)
    ext = np.concatenate([w_uq, w_uq[:, :, 0:64], w_uq[:, :, 64:96][:, :, perm]], axis=2)
    sh["l0_w_uq"] = np.ascontiguousarray(ext.reshape(384, 8 * 192))
    w_ukv = inp["l0_mla_w_ukv"].reshape(256, 8, 128)
    sh["l0_w_ukv"] = np.ascontiguousarray(
        np.concatenate([w_ukv[:, :, 0:64].reshape(256, 512), w_ukv[:, :, 64:128].reshape(256, 512)], axis=1))
    sh["l0_qnT"] = _fm(inp["l0_mla_q_norm"], 3)
    sh["l0_kvnT"] = _fm(inp["l0_mla_kv_norm"], 2)
    Cf, Sf = _rope_tables()
    sh["ropeC"] = Cf
    sh["ropeS"] = Sf
    cw = inp["l0_mlstm_conv_w"]
    sh["l0_convT"] = np.ascontiguousarray(
        np.concatenate([cw.reshape(3, 8, 128).transpose(2, 1, 0), inp["l0_mlstm_conv_b"].reshape(8, 128).T[:, :, None]],
                       axis=2))
    gb = np.zeros((16, 1), np.float32)
    for d in range(2):
        for h in range(4):
            gb[d * 8 + h, 0] = inp["l0_mlstm_b_i"][d, h]
            gb[d * 8 + 4 + h, 0] = inp["l0_mlstm_b_f"][d, h]
    sh["l0_gbias"] = gb
    gb2 = np.zeros((64, 2), np.float32)
    for d in range(2):
        for h in range(4):
            gb2[d * 32 + h, 0] = inp["l0_mlstm_b_i"][d, h]
            gb2[d * 32 + h, 1] = inp["l0_mlstm_b_f"][d, h]
    sh["l0_gb2"] = gb2
    sh["l0_hnorm"] = np.ascontiguousarray(inp["l0_mlstm_norm"].reshape(1, 512))
    sh["l0_w_out"] = inp["l0_w_out"]
    sh["l1_w_in"] = inp["l1_w_in"]
    sh["l1_w_gate"] = np.ascontiguousarray(inp["l1_gla_w_gate"])
    sh["l1_bgT"] = np.ascontiguousarray(inp["l1_gla_b_gate"].reshape(2, 2, 128).transpose(2, 0, 1))
    sh["l1_gnorm"] = np.ascontiguousarray(inp["l1_gla_norm"].reshape(1, 512))
    sh["l1_w_out"] = inp["l1_w_out"]
    sh["final_norm"] = np.ascontiguousarray(inp["final_norm"].reshape(1, 1024))
    rpb = inp["l1_na_rpb"]
    kc = np.arange(64)[:, None]
    qc = np.arange(64)[None, :]
    wc0 = np.clip(qc - 8, 0, 48)
    okc = (kc >= wc0) & (kc < wc0 + 16)
    dcol = np.clip(kc - qc + 15, 0, 30)
    Tb = np.full((8, 15, 64, 64), MASKV, np.float32)
    for m in range(15):
        dr = 7 - m
        blk = rpb[:, dr + 7][:, dcol]
        Tb[:, m] = np.where(okc[None], blk, np.float32(MASKV))
    sh["na_bias"] = Tb
    mods = [(inp["l0_norm"], inp["l0_w_mod"], inp["l0_b_mod"]), (inp["l1_norm"], inp["l1_w_mod"], inp["l1_b_mod"])]
    for l, (g_, wm_, bm_) in enumerate(mods):
        sh["l%d_w_mod" % l] = wm_
        sh["l%d_bmodT" % l] = _fm(bm_, 24)
        sh["l%d_bmod_gate" % l] = np.ascontiguousarray(bm_[2048:3072].reshape(1, 1024))
        sh["l%d_gT" % l] = _fm(g_, 8)
    per = []
    for b in range(8):
        d = {}
        d["x"] = inp["x"][b]
        d["ctx"] = inp["ctx"][b]
        cv = np.stack([inp["c"][b], inp["c_ctx"]], axis=1)
        d["cvec"] = np.ascontiguousarray(cv.reshape(8, 128, 2).transpose(1, 0, 2))
        per.append(d)
    return sh, per


def build(sh_shapes, per_shapes, stage=99, debug=(), skip=()):
    nc = bass.Bass("TRN2", target_bir_lowering=False)
    IN = {}
    for k, (shape, dt) in list(sh_shapes.items()) + list(per_shapes.items()):
        IN[k] = nc.dram_tensor(k, list(shape), BF16 if dt == "bf16" else F32, kind="ExternalInput").ap()
    out = nc.dram_tensor("out", [TL, D], F32, kind="ExternalOutput").ap()

    def scratch(name, shape, dt):
        kind = "ExternalOutput" if name in debug else "Internal"
        return nc.dram_tensor(name, list(shape), dt, kind=kind).ap()

    SC = {}
    SC["H1"] = scratch("H1", [T, D], F32)
    SC["SZT"] = scratch("SZT", [1024, T], BF16)
    SC["CATT"] = scratch("CATT", [1024, T], BF16)
    SC["QT"] = scratch("QT", [8, 96, T], BF16)
    SC["KT"] = scratch("KT", [8, 96, T], BF16)
    SC["V"] = scratch("V", [T, 512], BF16)
    SC["MQK"] = scratch("MQK", [1024, T], F32)
    SC["GI"] = scratch("GI", [8, T], F32)
    SC["GF"] = scratch("GF", [8, T], F32)
    SC["MV"] = scratch("MV", [T, 512], BF16)
    SC["MO"] = scratch("MO", [T, 512], BF16)
    SC["HM"] = scratch("HM", [2, T, 512], F32)
    SC["LG"] = scratch("LG", [2, 256, T], F32)
    SC["NQ"] = scratch("NQ", [512, T], BF16)
    SC["NK"] = scratch("NK", [512, T], BF16)

    with ExitStack() as es0:
        P = Prog(nc, es0)
        A0 = Alloc(nc, es0)
        ident_bf = A0.sb([128, 128], BF16, "identbf")
        ident_f = A0.sb([128, 128], F32, "identf")
        ones_f = A0.sb([128, 128], F32, "onesf")
        P.dma("sp", ident_bf[:], IN["ident_bf"][:, :], writes=[ident_bf])
        P.dma("sp", ident_f[:], IN["ident_f"][:, :], writes=[ident_f])
        P.op("pool", "memset", [], [ones_f], ones_f[:], 1.0)
        affA = [A0.sb([128, 8, 2], F32, "affA%d" % l) for l in range(2)]
        affB = [A0.sb([128, 8, 2], F32, "affB%d" % l) for l in range(2)]
        gateR = [[A0.sb([128, 1024], F32, "gateR%d_%d" % (l, s)) for s in range(2 if l == 0 else 1)] for l in range(2)]

        with ExitStack() as es:
            A = Alloc(nc, es)
            cv = A.sb([128, 8, 2], F32, "cv")
            sc = A.sb([128, 8, 2], F32, "sc")
            screp = [A.sb([128, 8, 128], F32, "screp%d" % s) for s in range(2)]
            P.dma("sp", cv[:], IN["cvec"][:, :, :], writes=[cv])
            P.op("act", "activation", [cv], [sc], out=sc[:], in_=cv[:], func=AF.Silu)
            for s in range(2):
                for k in range(8):
                    P.op("dve", "tensor_copy", [sc], [screp[s]], out=screp[s][:, k, :],
                         in_=sc[:, k, s:s + 1].to_broadcast([128, 128]))
            wpan = [A.sb([128, 8, 384], F32, "wpan%d" % i) for i in range(2)]
            wgate = [A.sb([128, 512], F32, "wgate%d" % i) for i in range(3)]
            pm = A.ps([128, 24, 2], F32, "pm")
            pg = [A.ps([128, 512], F32, "pg%d" % i) for i in range(2)]
            bmT = A.sb([128, 24], F32, "bmT")
            gT = A.sb([128, 8], F32, "gT")
            modT = A.sb([128, 24, 2], F32, "modT")
            bgrow = A.sb([128, 1024], F32, "bgrow")
            for l in range(2):
                wm = IN["l%d_w_mod" % l]
                P.dma("sp", bmT[:], IN["l%d_bmodT" % l][:, :], writes=[bmT])
                P.dma("sp", gT[:], IN["l%d_gT" % l][:, :], writes=[gT])
                P.dma("sp", bgrow[:], IN["l%d_bmod_gate" % l][0:1, :].to_broadcast([128, 1024]), writes=[bgrow])
                for pn in range(8):
                    wp = wpan[pn % 2]
                    P.dma("sp" if pn % 2 == 0 else "pool", wp[:],
                          wm[:, pn * 384:(pn + 1) * 384].rearrange("(k p) n -> p k n", p=128), writes=[wp])
                    for j in range(3):
                        n = pn * 3 + j
                        for k in range(8):
                            P.op("pe", "matmul", [wp, sc], [pm], pm[:, n, :], wp[:, k, j * 128:(j + 1) * 128],
                                 sc[:, k, :], start=(k == 0), stop=(k == 7))
                P.op("dve", "tensor_tensor", [pm, bmT], [modT], out=modT[:], in0=pm[:],
                     in1=bmT[:].unsqueeze(2).to_broadcast([128, 24, 2]), op=ALU.add)
                P.op("dve", "tensor_scalar", [modT], [affA[l]], out=affA[l][:], in0=modT[:, 8:16, :], scalar1=1.0,
                     scalar2=None, op0=ALU.add)
                P.op("dve", "tensor_tensor", [affA[l], gT], [affA[l]], out=affA[l][:], in0=affA[l][:],
                     in1=gT[:].unsqueeze(2).to_broadcast([128, 8, 2]), op=ALU.mult)
                P.op("dve", "tensor_copy", [modT], [affB[l]], out=affB[l][:], in_=modT[:, 0:8, :])
                for s in range(len(gateR[l])):
                    for hf in range(2):
                        ps = pg[hf]
                        for k in range(8):
                            wg = wgate[(hf * 8 + k) % 3]
                            P.dma("sp" if k % 2 == 0 else "pool", wg[:],
                                  wm[k * 128:(k + 1) * 128, 2048 + hf * 512:2048 + (hf + 1) * 512], writes=[wg])
                            P.op("pe", "matmul", [wg, screp[s]], [ps], ps[:], screp[s][:, k, :], wg[:],
                                 start=(k == 0), stop=(k == 7))
                        P.op("dve", "tensor_tensor", [ps, bgrow], [gateR[l][s]],
                             out=gateR[l][s][:, hf * 512:(hf + 1) * 512], in0=ps[:],
                             in1=bgrow[:, hf * 512:(hf + 1) * 512], op=ALU.add)
            P.barrier()
        if stage <= 0:
            dbg = nc.dram_tensor("dbg_mod", [128, 2, 2, 8, 2], F32, kind="ExternalOutput").ap()
            dbg2 = nc.dram_tensor("dbg_gate", [128, 1024], F32, kind="ExternalOutput").ap()
            for l in range(2):
                P.dma("sp", dbg[:, l, 0], affA[l][:], reads=[affA[l]])
                P.dma("sp", dbg[:, l, 1], affB[l][:], reads=[affB[l]])
            P.dma("sp", dbg2[:, :], gateR[0][1][:], reads=[gateR[0][1]])
            P.barrier()
            return nc

        phase_A(nc, P, IN, SC, 0, affA[0], affB[0], ident_bf, ones_f)
        if stage <= 1:
            return nc
        if 2 not in skip:
            mla_groups = [(0, 256, [0, 1], 0)] + [(256 + 512 * g, 512, list(range(NT)), 0) for g in range(8)]
            attention(nc, P, SC, ones_f, 8, 96, 96.0 ** -0.5, lambda h: SC["QT"][h, :, :], lambda h: SC["KT"][h, :, :],
                      SC["V"], 0, mla_groups)
        if stage <= 2:
            return nc
        if 3 not in skip:
            mlstm_phase(nc, P, IN, SC, ident_bf, ident_f, ones_f)
        if stage <= 3:
            return nc
        combine_phase(nc, P, IN, SC, ident_bf, SC["HM"][0], SC["HM"][1], SC["MO"], "l0_hnorm", 512, GROUPS)
        if stage <= 4:
            return nc
        phase_C(nc, P, IN, SC, 0, gateR[0], out)
        if stage <= 5:
            return nc
        phase_A(nc, P, IN, SC, 1, affA[1], affB[1], ident_bf, ones_f)
        if stage <= 6:
            return nc
        if 7 not in skip:
            gla_phase(nc, P, IN, SC, ident_bf)
            combine_phase(nc, P, IN, SC, ident_bf, SC["HM"][0], SC["HM"][1], None, "l1_gnorm", 0, GROUPS[1:])
        if stage <= 7:
            return nc
        if 8 not in skip:
            na_groups = []
            for g in range(8):
                key, t_lo, nt_ = _na_cfg(g)
                loc = []
                for r in range(nt_):
                    u0, u1 = _na_range(g, t_lo + r)
                    loc.append((2 + t_lo + r, u0 * 64, (u1 + 1) * 64))
                na_groups.append((256 + 512 * g, 512, [0, 1] + loc, g))
            attention(nc, P, SC, ones_f, 8, 64, 64.0 ** -0.5, lambda h: SC["NQ"][h * 64:(h + 1) * 64, :],
                      lambda h: SC["NK"][h * 64:(h + 1) * 64, :], SC["V"], 512, na_groups, bias_fn=na_bias_fn(nc, P, IN, {}))
        if stage <= 8:
            return nc
        phase_C(nc, P, IN, SC, 1, gateR[1], out)
    return nc


def phase_A(nc, P, IN, SC, layer, affA, affB, ident_bf, ones_f):
    NW = 3808 if layer == 0 else 3616
    w_in_d = IN["l%d_w_in" % layer]
    with ExitStack() as es:
        A = Alloc(nc, es)
        w_in = A.sb([128, 8, NW], BF16, "w_in")
        if layer == 0:
            w_uq = A.sb([128, 3, 1536], BF16, "w_uq")
            w_ukv = A.sb([128, 2, 1024], BF16, "w_ukv")
        with ExitStack() as es2:
            A2 = Alloc(nc, es2)
            stg = [A2.sb([128, 8, 512], F32, "stg%d" % i) for i in range(2)]
            i = 0
            for c0 in range(0, NW, 512):
                cw = min(512, NW - c0)
                s = stg[i % 2]
                P.dma("sp" if i % 2 == 0 else "pool", s[:, :, 0:cw],
                      w_in_d[:, c0:c0 + cw].rearrange("(k p) n -> p k n", p=128), writes=[s])
                P.op("dve" if i % 2 == 0 else "act", "tensor_copy" if i % 2 == 0 else "copy", [s], [w_in],
                     out=w_in[:, :, c0:c0 + cw], in_=s[:, :, 0:cw])
                i += 1
            if layer == 0:
                s = stg[i % 2]
                for kk in range(3):
                    s = stg[i % 2]
                    P.dma("sp", s[:, 0:3, :], IN["l0_w_uq"][kk * 128:(kk + 1) * 128, :].rearrange("p (a n) -> p a n", a=3),
                          writes=[s])
                    P.op("dve", "tensor_copy", [s], [w_uq], out=w_uq[:, kk, :].rearrange("p (a n) -> p a n", a=3),
                         in_=s[:, 0:3, :])
                    i += 1
                s = stg[i % 2]
                for kk in range(2):
                    P.dma("sp", s[:, 2 * kk:2 * kk + 2, :],
                          IN["l0_w_ukv"][kk * 128:(kk + 1) * 128, :].rearrange("p (a n) -> p a n", a=2), writes=[s])
                P.op("dve", "tensor_copy", [s], [w_ukv], out=w_ukv[:].rearrange("p k (a n) -> p (k a) n", a=2),
                     in_=s[:, 0:4, :])
                i += 1
            P.barrier()
        if layer == 0:
            qnT = A.sb([128, 3], F32, "qnT")
            kvnT = A.sb([128, 2], F32, "kvnT")
            P.dma("sp", qnT[:], IN["l0_qnT"][:, :], writes=[qnT])
            P.dma("sp", kvnT[:], IN["l0_kvnT"][:, :], writes=[kvnT])
            cqT = A.sb([128, 3, 512], F32, "cqT")
            ckvT = A.sb([128, 2, 512], F32, "ckvT")
            sq = A.sb([128, 3, 512], F32, "sq")
            rstd = A.sb([128, 512], F32, "rstd")
            cqn = A.sb([128, 3, 512], BF16, "cqn")
            ckvn = A.sb([128, 2, 512], BF16, "ckvn")
            rC = A.sb([128, 512], F32, "rC")
            rS = A.sb([128, 512], F32, "rS")
            rt1 = A.sb([128, 512], F32, "rt1")
            rt2 = A.sb([128, 512], F32, "rt2")
            qo = [A.sb([128, 512], BF16, "qo%d" % i) for i in range(2)]
            kro = A.sb([32, 512], BF16, "kro")
        else:
            gaT = [A.sb([16, 512], F32, "gaT%d" % d) for d in range(2)]
            wg = A.sb([16, 2, 256], F32, "wg")
            P.dma("sp", wg[:], IN["l1_w_gate"].rearrange("d r k -> r d k"), writes=[wg])
            bgT = A.sb([128, 2, 2], F32, "bgT")
            nbg = A.sb([128, 2, 2], F32, "nbg")
            P.dma("sp", bgT[:], IN["l1_bgT"][:, :, :], writes=[bgT])
            P.op("dve", "tensor_scalar", [bgT], [nbg], out=nbg[:], in0=bgT[:], scalar1=-1.0, scalar2=None, op0=ALU.mult)
            one1 = A.sb([128, 1], F32, "one1a")
            P.op("pool", "memset", [], [one1], one1[:], 1.0)
            lge = A.sb([128, 512], F32, "lge")
            lgo = [A.sb([128, 512], F32, "lgo%d" % i) for i in range(2)]
        hb = [A.sb([128, 1024], F32, "hb%d" % i) for i in range(3)]
        junk = A.sb([128, 1024], F32, "junk")
        st = [A.sb([128, 4], F32, "st%d" % i) for i in range(2)]
        xn2 = [[A.sb([128, 1024], BF16, "xn%d_%d" % (s_, i)) for i in range(4)] for s_ in range(2)]
        epsT = A.sb([128, 1], F32, "epsT")
        P.op("pool", "memset", [], [epsT], epsT[:], EPS)
        uT = [A.sb([128, 8, 512], BF16, "uT%d" % i) for i in range(2)]
        fo_bf = [A.sb([128, 512], BF16, "fobf%d" % i) for i in range(4)]
        fo_f = [A.sb([128, 512], F32, "fof%d" % i) for i in range(3)]
        tp = [A.ps([128, 512], BF16, "tp%d" % i) for i in range(2)]
        acc = [A.ps([128, 512], F32, "acc%d" % i) for i in range(5)]
        cnt = {"acc": 0, "fobf": 0, "fof": 0, "ev": 0, "q": 0, "hb": 0, "xn": 0, "tp": 0}

        def nxt(name, lst):
            r = lst[cnt[name] % len(lst)]
            cnt[name] += 1
            return r

        def evac_engine():
            cnt["ev"] += 1
            return "dve" if cnt["ev"] % 2 == 0 else "act"

        def copy_op(eng, src_t, src_ap, dst_t, dst_ap):
            if eng == "act":
                P.op("act", "copy", [src_t], [dst_t], out=dst_ap, in_=src_ap)
            else:
                P.op(eng, "tensor_copy", [src_t], [dst_t], out=dst_ap, in_=src_ap)

        def stq():
            cnt["q"] += 1
            return "pool" if cnt["q"] % 2 == 0 else "sp"

        def norm_part(gi):
            t0, n = GROUPS[gi]
            ntl = n // 128
            sta = st[gi % 2]
            xn = xn2[gi % 2]
            for ti in range(ntl):
                h = nxt("hb", hb)
                tok = t0 + ti * 128
                if layer == 0:
                    src = IN["ctx"][tok:tok + 128, :] if gi == 0 else IN["x"][tok - TC:tok - TC + 128, :]
                else:
                    src = SC["H1"][tok:tok + 128, :]
                P.dma("sp", h[:], src, writes=[h])
                P.op("act", "activation", [h], [junk, sta], out=junk[:], in_=h[:], func=AF.Square,
                     accum_out=sta[:, ti:ti + 1])
                P.op("act", "activation", [sta, epsT], [sta], out=sta[:, ti:ti + 1], in_=sta[:, ti:ti + 1], func=AF.Sqrt,
                     scale=1.0 / D, bias=epsT[:, 0:1])
                P.op("dve", "reciprocal", [sta], [sta], out=sta[:, ti:ti + 1], in_=sta[:, ti:ti + 1])
                x_ = xn[ti]
                P.op("dve", "tensor_scalar", [h, sta], [x_], out=x_[:], in0=h[:], scalar1=sta[:, ti:ti + 1],
                     scalar2=None, op0=ALU.mult)

        def transpose_part(gi):
            t0, n = GROUPS[gi]
            ntl = n // 128
            s = 1 if gi == 0 else 0
            u = uT[gi % 2]
            xn = xn2[gi % 2]
            for j in range(8):
                tpp = nxt("tp", tp)
                for ti in range(ntl):
                    P.op("pe", "transpose", [xn[ti], ident_bf], [tpp], tpp[:, ti * 128:(ti + 1) * 128],
                         xn[ti][:, j * 128:(j + 1) * 128], ident_bf[:])
                P.op("dve", "tensor_scalar", [tpp, affA, affB], [u], out=u[:, j, 0:n],
                     in0=tpp[:, 0:n], scalar1=affA[:, j, s:s + 1], scalar2=affB[:, j, s:s + 1], op0=ALU.mult,
                     op1=ALU.add)


        def proj_part(gi):
            t0, n = GROUPS[gi]
            ntl = n // 128
            u = uT[gi % 2]

            def fm_proj(c0, ncol):
                ps = nxt("acc", acc)
                for k in range(8):
                    P.op("pe", "matmul", [w_in, u], [ps], ps[0:ncol, 0:n], w_in[:, k, c0:c0 + ncol], u[:, k, 0:n],
                         start=(k == 0), stop=(k == 7))
                return ps

            def store_fm(ps, ncol, dst, dt, func=None, eng=None):
                o = nxt("fobf", fo_bf) if dt == BF16 else nxt("fof", fo_f)
                if func is not None:
                    P.op("act", "activation", [ps], [o], out=o[0:ncol, 0:n], in_=ps[0:ncol, 0:n], func=func)
                else:
                    copy_op(eng or evac_engine(), ps, ps[0:ncol, 0:n], o, o[0:ncol, 0:n])
                P.dma(stq(), dst, o[0:ncol, 0:n], reads=[o])

            tsl = slice(t0, t0 + n)
            if layer == 0:
                for j in range(3):
                    ps = fm_proj(j * 128, 128)
                    copy_op(evac_engine(), ps, ps[:, 0:n], cqT, cqT[:, j, 0:n])
                for j in range(2):
                    ps = fm_proj(384 + j * 128, 128)
                    copy_op(evac_engine(), ps, ps[:, 0:n], ckvT, ckvT[:, j, 0:n])
                for (src_t, nk, nrm, dst_t, dim) in ((cqT, 3, qnT, cqn, 384.0), (ckvT, 2, kvnT, ckvn, 256.0)):
                    P.op("act", "activation", [src_t], [sq], out=sq[:, 0:nk, 0:n], in_=src_t[:, 0:nk, 0:n], func=AF.Square)
                    ps = nxt("acc", acc)
                    for k in range(nk):
                        P.op("pe", "matmul", [ones_f, sq], [ps], ps[:, 0:n], ones_f[:], sq[:, k, 0:n], start=(k == 0),
                             stop=(k == nk - 1))
                    P.op("act", "activation", [ps, epsT], [rstd], out=rstd[:, 0:n], in_=ps[:, 0:n], func=AF.Sqrt,
                         scale=1.0 / dim, bias=epsT[:, 0:1])
                    P.op("dve", "reciprocal", [rstd], [rstd], out=rstd[:, 0:n], in_=rstd[:, 0:n])
                    for k in range(nk):
                        P.op("dve", "scalar_tensor_tensor", [src_t, nrm, rstd], [dst_t], out=dst_t[:, k, 0:n],
                             in0=src_t[:, k, 0:n], scalar=nrm[:, k:k + 1], in1=rstd[:, 0:n], op0=ALU.mult, op1=ALU.mult)
                rot = gi > 0
                if rot:
                    P.dma("sp", rC[:, 0:n], IN["ropeC"][:, t0 - TC:t0 - TC + n], writes=[rC])
                    P.dma("sp", rS[:, 0:n], IN["ropeS"][:, t0 - TC:t0 - TC + n], writes=[rS])
                for hh in range(8):
                    ps = nxt("acc", acc)
                    for k in range(3):
                        P.op("pe", "matmul", [w_uq, cqn], [ps], ps[0:96, 0:n], w_uq[:, k, hh * 192:hh * 192 + 96],
                             cqn[:, k, 0:n], start=(k == 0), stop=(k == 2))
                    o = nxt("fobf", fo_bf)
                    if rot:
                        ps2 = nxt("acc", acc)
                        for k in range(3):
                            P.op("pe", "matmul", [w_uq, cqn], [ps2], ps2[0:96, 0:n],
                                 w_uq[:, k, hh * 192 + 96:hh * 192 + 192], cqn[:, k, 0:n], start=(k == 0), stop=(k == 2))
                        copy_op("act", ps, ps[0:64, 0:n], o, o[0:64, 0:n])
                        P.op("dve", "tensor_tensor", [ps, rC], [rt1], out=rt1[64:96, 0:n], in0=ps[64:96, 0:n],
                             in1=rC[64:96, 0:n], op=ALU.mult)
                        P.op("dve", "tensor_tensor", [ps2, rS], [rt2], out=rt2[64:96, 0:n], in0=ps2[64:96, 0:n],
                             in1=rS[64:96, 0:n], op=ALU.mult)
                        P.op("pool", "tensor_tensor", [rt1, rt2], [o], out=o[64:96, 0:n], in0=rt1[64:96, 0:n],
                             in1=rt2[64:96, 0:n], op=ALU.add)
                    else:
                        copy_op(evac_engine(), ps, ps[0:96, 0:n], o, o[0:96, 0:n])
                    P.dma(stq(), SC["QT"][hh, :, tsl], o[0:96, 0:n], reads=[o])
                for c in range(4):
                    ps = nxt("acc", acc)
                    for k in range(2):
                        P.op("pe", "matmul", [w_ukv, ckvn], [ps], ps[:, 0:n], w_ukv[:, k, c * 128:(c + 1) * 128],
                             ckvn[:, k, 0:n], start=(k == 0), stop=(k == 1))
                    o = nxt("fobf", fo_bf)
                    copy_op(evac_engine(), ps, ps[:, 0:n], o, o[:, 0:n])
                    for hh in range(2):
                        P.dma(stq(), SC["KT"][c * 2 + hh, 0:64, tsl], o[hh * 64:(hh + 1) * 64, 0:n], reads=[o])
                for ti in range(ntl):
                    ps = nxt("acc", acc)
                    for k in range(2):
                        P.op("pe", "matmul", [w_ukv, ckvn], [ps], ps[:, :], ckvn[:, k, ti * 128:(ti + 1) * 128],
                             w_ukv[:, k, 512:1024], start=(k == 0), stop=(k == 1))
                    o = nxt("fobf", fo_bf)
                    copy_op(evac_engine(), ps, ps[:, :], o, o[:, :])
                    P.dma(stq(), SC["V"][t0 + ti * 128:t0 + (ti + 1) * 128, :], o[:, :], reads=[o])
                ps = fm_proj(640, 32)
                if rot:
                    ps2 = fm_proj(3760, 32)
                    P.op("dve", "tensor_tensor", [ps, rC], [rt1], out=rt1[0:32, 0:n], in0=ps[0:32, 0:n], in1=rC[0:32, 0:n],
                         op=ALU.mult)
                    P.op("dve", "tensor_tensor", [ps2, rS], [rt2], out=rt2[0:32, 0:n], in0=ps2[0:32, 0:n],
                         in1=rS[0:32, 0:n], op=ALU.mult)
                    P.op("pool", "tensor_tensor", [rt1, rt2], [kro], out=kro[0:32, 0:n], in0=rt1[0:32, 0:n],
                         in1=rt2[0:32, 0:n], op=ALU.add)
                else:
                    copy_op("dve", ps, ps[0:32, 0:n], kro, kro[0:32, 0:n])
                for hh in range(8):
                    P.dma(stq(), SC["KT"][hh, 64:96, tsl], kro[0:32, 0:n], reads=[kro])
                yield
                for c in range(8):
                    ps = fm_proj(672 + c * 128, 128)
                    store_fm(ps, 128, SC["MQK"][c * 128:(c + 1) * 128, tsl], F32)
                ps = fm_proj(3792, 8)
                store_fm(ps, 8, SC["GI"][:, tsl], F32)
                ps = fm_proj(3800, 8)
                store_fm(ps, 8, SC["GF"][:, tsl], F32)
                for c in range(8):
                    ps = fm_proj(2736 + c * 128, 128)
                    store_fm(ps, 128, SC["SZT"][c * 128:(c + 1) * 128, tsl], BF16, func=AF.Silu)
                tm_specs = [(1696, SC["MV"], None), (2208, SC["MO"], AF.Sigmoid)]
            else:
                for c in range(4):
                    ps = fm_proj(c * 128, 128)
                    store_fm(ps, 128, SC["MQK"][c * 128:(c + 1) * 128, tsl], F32)
                yield
                for d in range(2):
                    ps = fm_proj(1024 + 16 * d, 16)
                    copy_op("dve", ps, ps[0:16, 0:n], gaT[d], gaT[d][0:16, 0:n])
                for d in range(2):
                    for c2 in range(2):
                        ps = nxt("acc", acc)
                        P.op("pe", "matmul", [wg, gaT[d]], [ps], ps[:, 0:n], wg[0:16, d, c2 * 128:(c2 + 1) * 128],
                             gaT[d][0:16, 0:n], start=True, stop=True)
                        P.op("act", "activation", [ps, nbg], [lge], out=lge[:, 0:n], in_=ps[:, 0:n], func=AF.Exp, scale=-1.0,
                             bias=nbg[:, d, c2:c2 + 1])
                        P.op("act", "activation", [lge, one1], [lge], out=lge[:, 0:n], in_=lge[:, 0:n], func=AF.Ln,
                             bias=one1[:, 0:1])
                        o = lgo[(d * 2 + c2) % 2]
                        P.op("dve", "tensor_scalar", [lge], [o], out=o[:, 0:n], in0=lge[:, 0:n], scalar1=-1.0 / 16.0,
                             scalar2=None, op0=ALU.mult)
                        P.dma(stq(), SC["LG"][d, c2 * 128:(c2 + 1) * 128, tsl], o[:, 0:n], reads=[o])
                for c in range(4):
                    ps = fm_proj(1056 + c * 128, 128)
                    store_fm(ps, 128, SC["NQ"][c * 128:(c + 1) * 128, tsl], BF16)
                for c in range(4):
                    ps = fm_proj(1568 + c * 128, 128)
                    store_fm(ps, 128, SC["NK"][c * 128:(c + 1) * 128, tsl], BF16)
                for c in range(8):
                    ps = fm_proj(2592 + c * 128, 128)
                    store_fm(ps, 128, SC["SZT"][c * 128:(c + 1) * 128, tsl], BF16, func=AF.Silu)
                tm_specs = [(512, SC["MV"], None), (2080, SC["V"], None)]
            for (c0, dst, func) in tm_specs:
                for ti in range(ntl):
                    ps = nxt("acc", acc)
                    for k in range(8):
                        P.op("pe", "matmul", [w_in, u], [ps], ps[:, :], u[:, k, ti * 128:(ti + 1) * 128],
                             w_in[:, k, c0:c0 + 512], start=(k == 0), stop=(k == 7))
                    o = nxt("fobf", fo_bf)
                    if func is not None:
                        P.op("act", "activation", [ps], [o], out=o[:, :], in_=ps[:, :], func=func)
                    else:
                        copy_op(evac_engine(), ps, ps[:, :], o, o[:, :])
                    P.dma(stq(), dst[t0 + ti * 128:t0 + (ti + 1) * 128, :], o[:, :], reads=[o])

        norm_part(0)
        transpose_part(0)
        for gi in range(len(GROUPS)):
            if gi + 1 < len(GROUPS):
                norm_part(gi + 1)
            gen = proj_part(gi)
            next(gen)
            if gi + 1 < len(GROUPS):
                transpose_part(gi + 1)
            for _ in gen:
                pass
        P.barrier()


def attention(nc, P, SC, ones_f, heads, dq, scale, load_q, load_k, Vd, cat_row0, groups, bias_fn=None):
    LOOK = 4
    NS = 5
    EPI_DELAY = 8
    with ExitStack() as es:
        A = Alloc(nc, es)
        V = A.sb([128, NT, heads, 65], BF16, "Vall")
        P.op("pool", "memset", [], [V], V[:, :, :, 64:65], 1.0)
        for half in range(2):
            tl = slice(half * 17, (half + 1) * 17)
            for hh in range(heads):
                P.dma("sp" if hh % 2 == 0 else "pool", V[:, tl, hh, 0:64],
                      Vd[half * 17 * 128:(half + 1) * 17 * 128, hh * 64:(hh + 1) * 64].rearrange("(t p) d -> p t d", p=128),
                      writes=[V])
        kT = [A.sb([128, T], BF16, "kT%d" % i) for i in range(2)]
        qT = [A.sb([128, T], BF16, "qT%d" % i) for i in range(2)]
        pt = [A.sb([128, 512], BF16, "pt%d" % i) for i in range(NS)]
        sb_t = [A.sb([128, 512], F32, "sbt%d" % i) for i in range(3)] if bias_fn is not None else None
        rden = [A.sb([128, 512], F32, "rden%d" % i) for i in range(2)]
        bcs = [A.sb([128, 512], F32, "bcs%d" % i) for i in range(2)]
        szt = [A.sb([64, 512], BF16, "szt%d" % i) for i in range(3)]
        tmp = [A.sb([64, 512], F32, "atmp%d" % i) for i in range(2)]
        ao = [A.sb([64, 512], BF16, "ao%d" % i) for i in range(2)]
        Sps = [A.ps([128, 512], F32, "Sps%d" % i) for i in range(NS)]
        Ops = [A.ps([128, 512], F32, "Ops%d" % i) for i in range(2)]
        Bps = A.ps([128, 512], F32, "Bps")
        P.dma("sp", kT[0][0:dq, :], load_k(0), writes=[kT[0]])
        P.dma("pool", qT[0][0:dq, :], load_q(0), writes=[qT[0]])
        gcount = 0
        it = 0
        pend = []
        for h in range(heads):
            k_ = kT[h % 2]
            q_ = qT[h % 2]
            if h + 1 < heads:
                P.dma("sp", kT[(h + 1) % 2][0:dq, :], load_k(h + 1), writes=[kT[(h + 1) % 2]])
                P.dma("pool", qT[(h + 1) % 2][0:dq, :], load_q(h + 1), writes=[qT[(h + 1) % 2]])
            if bias_fn is not None:
                if h == 0:
                    bias_cur = bias_fn(0, A)
                bias_tiles = bias_cur
                if h + 1 < heads:
                    bias_cur = bias_fn(h + 1, A if h == 0 else None)
            else:
                bias_tiles = None
            r0 = cat_row0 + h * 64
            items = []
            for (q0, n, tiles, gkey) in groups:
                gid = gcount
                gcount += 1
                for j, kt in enumerate(tiles):
                    if isinstance(kt, tuple):
                        kt, c0, c1 = kt
                    else:
                        c0, c1 = 0, n
                    items.append((gid, q0, n, gkey, j, kt, len(tiles), c0, c1))

            def flush(cond):
                for e_ in pend[:]:
                    if cond(e_[1][0]):
                        emit_epi(*e_[1])
                        pend.remove(e_)

            def emit_S(item, slot):
                gid, q0, n, gkey, j, kt, nt_, c0, c1 = item
                S = Sps[slot % NS]
                p_ = pt[slot % NS]
                if j == 0:
                    flush(lambda g2: g2 % 3 == gid % 3)
                    sz = szt[gid % 3]
                    P.dma("sp", sz[:, 0:n], SC["SZT"][r0:r0 + 64, q0:q0 + n], writes=[sz])
                P.op("pe", "matmul", [k_, q_], [S], S[:, c0:c1], k_[0:dq, kt * 128:(kt + 1) * 128], q_[0:dq, q0 + c0:q0 + c1],
                     start=True, stop=True)
                bt = bias_tiles.get((gkey, kt)) if bias_tiles is not None else None
                if bt is not None:
                    sb = sb_t[slot % 3]
                    P.op("dve", "scalar_tensor_tensor", [S, bt], [sb], out=sb[:, c0:c1], in0=S[:, c0:c1], scalar=scale,
                         in1=bt[:, c0:c1], op0=ALU.mult, op1=ALU.add)
                    P.op("act", "activation", [sb], [p_], out=p_[:, c0:c1], in_=sb[:, c0:c1], func=AF.Exp)
                else:
                    P.op("act", "activation", [S], [p_], out=p_[:, c0:c1], in_=S[:, c0:c1], func=AF.Exp, scale=scale)

            def emit_PV(item, slot):
                gid, q0, n, gkey, j, kt, nt_, c0, c1 = item
                O = Ops[gid % 2]
                p_ = pt[slot % NS]
                assert j > 0 or (c0 == 0 and c1 == n)
                if j == 0:
                    flush(lambda g2: g2 % 2 == gid % 2)
                P.op("pe", "matmul", [V, p_], [O], O[0:65, c0:c1], V[:, kt, h, :], p_[:, c0:c1], start=(j == 0),
                     stop=(j == nt_ - 1))
                if j == nt_ - 1:
                    rd = rden[gid % 2]
                    P.op("dve", "reciprocal", [O], [rd], out=rd[64:65, 0:n], in_=O[64:65, 0:n])
                    pend.append([EPI_DELAY, (gid, q0, n, r0, h)])

            def emit_epi(gid, q0, n, r0, h):
                O = Ops[gid % 2]
                rd = rden[gid % 2]
                bc_ = bcs[gid % 2]
                tm_ = tmp[gid % 2]
                a_ = ao[gid % 2]
                sz = szt[gid % 3]
                P.op("pe", "matmul", [ones_f, rd], [Bps], Bps[0:64, 0:n], ones_f[64:65, 0:64], rd[64:65, 0:n],
                     start=True, stop=True)
                P.op("act", "copy", [Bps], [bc_], out=bc_[0:64, 0:n], in_=Bps[0:64, 0:n])
                P.op("dve", "tensor_tensor", [O, bc_], [tm_], out=tm_[:, 0:n], in0=O[0:64, 0:n], in1=bc_[0:64, 0:n],
                     op=ALU.mult)
                P.op("pool", "tensor_tensor", [tm_, sz], [a_], out=a_[:, 0:n], in0=tm_[:, 0:n], in1=sz[:, 0:n], op=ALU.mult)
                P.dma("pool", SC["CATT"][r0:r0 + 64, q0:q0 + n], a_[:, 0:n], reads=[a_])

            nI = len(items)
            for idx in range(nI + LOOK):
                if idx < nI:
                    emit_S(items[idx], it + idx)
                for e_ in pend[:]:
                    e_[0] -= 1
                    if e_[0] <= 0:
                        emit_epi(*e_[1])
                        pend.remove(e_)
                if idx - LOOK >= 0:
                    emit_PV(items[idx - LOOK], it + idx - LOOK)
            it += nI
        for e_ in pend:
            emit_epi(*e_[1])
        P.barrier()


def mlstm_phase(nc, P, IN, SC, ident_bf, ident_f, ones_f):
    NB = NT
    NCH = T // 64
    with ExitStack() as es:
        A = Alloc(nc, es)
        esT = A.sb([128, NB, 64], F32, "esT")
        fT = A.sb([128, NB, 64], F32, "fT")
        decbc = A.sb([128, 8, NCH], F32, "decbc")
        mask = [A.sb([128, 64], F32, "mask%d" % d) for d in range(2)]
        for d in range(2):
            P.op("pool", "memset", [], [mask[d]], mask[d][:], 1.0)
            for half in range(2):
                pr = slice(half * 64, half * 64 + 64)
                P.op("pool", "affine_select", [mask[d]], [mask[d]], out=mask[d][pr, :], in_=mask[d][pr, :],
                     pattern=[[1 if d == 0 else -1, 64]], compare_op=ALU.is_ge, fill=0.0, base=0,
                     channel_multiplier=-1 if d == 0 else 1)
        with ExitStack() as es2:
            A2 = Alloc(nc, es2)
            X1 = A2.sb([64, T], F32, "X1")
            X2 = A2.sb([64, T], F32, "X2")
            X3 = A2.sb([64, T], F32, "X3")
            X4 = A2.sb([64, T], F32, "X4")
            gb = A2.sb([64, 2], F32, "gb")
            nbf = A2.sb([64, 1], F32, "nbf")
            one1 = A2.sb([64, 1], F32, "one1")
            dec = A2.sb([64, NCH], F32, "dec")
            aprev = A2.sb([64, NCH], F32, "aprev")
            sel = A2.sb([64, 128], F32, "sel")
            pst = [A2.ps([128, 8, 64], F32, "pst%d" % i) for i in range(2)]
            psd = A2.ps([128, NCH], F32, "psd")
            P.op("pool", "memset", [], [X1], X1[:], 0.0)
            P.op("pool", "memset", [], [X3], X3[:], 0.0)
            P.op("pool", "memset", [], [one1], one1[:], 1.0)
            P.dma("sp", gb[:], IN["l0_gb2"][:, :], writes=[gb])
            for d in range(2):
                P.dma("sp", X1[d * 32:d * 32 + 4, :], SC["GF"][d * 4:d * 4 + 4, :], writes=[X1])
                P.dma("pool", X3[d * 32:d * 32 + 4, :], SC["GI"][d * 4:d * 4 + 4, :], writes=[X3])
            P.op("dve", "tensor_scalar", [gb], [nbf], out=nbf[:], in0=gb[:, 1:2], scalar1=-1.0, scalar2=None, op0=ALU.mult)
            P.op("act", "activation", [X1, nbf], [X1], out=X1[:], in_=X1[:], func=AF.Exp, scale=-1.0, bias=nbf[:, 0:1])
            P.op("act", "activation", [X1, one1], [X1], out=X1[:], in_=X1[:], func=AF.Ln, bias=one1[:, 0:1])

            def seg_views(tile_, prng, d):
                if d == 0:
                    return [tile_[prng, 0:T]]
                return [tile_[prng, 0:TC][:, ::-1], tile_[prng, TC:T][:, ::-1]]

            def scan(dst, src, op0, d):
                prng = slice(d * 32, d * 32 + 32)
                dv = seg_views(dst, prng, d)
                sv = seg_views(src, prng, d)
                for i in range(len(dv)):
                    init = 0.0 if i == 0 else dst[prng, 0:1]
                    P.op("dve", "tensor_tensor_scan", [src, dst], [dst], out=dv[i], data0=sv[i], data1=sv[i],
                         initial=init, op0=op0, op1=ALU.bypass)

            for d in range(2):
                scan(X2, X1, ALU.add, d)
            P.op("dve", "scalar_tensor_tensor", [X3, gb, X2], [X3], out=X3[:], in0=X3[:], scalar=gb[:, 0:1], in1=X2[:],
                 op0=ALU.add, op1=ALU.add)
            for d in range(2):
                scan(X1, X3, ALU.max, d)
            for d in range(2):
                prng = slice(d * 32, d * 32 + 32)
                jj = 63 if d == 0 else 0
                P.op("dve", "tensor_copy", [X1], [X4], out=X4[prng, :].rearrange("p (c j) -> p c j", j=64),
                     in_=X1[prng, :].rearrange("p (c j) -> p c j", j=64)[:, :, jj:jj + 1].to_broadcast([32, NCH, 64]))
            aend = X4[:, :].rearrange("p (c j) -> p c j", j=64)[:, :, 0]
            P.op("pool", "memset", [], [aprev], aprev[:], 0.0)
            P.op("dve", "tensor_copy", [X4], [aprev], out=aprev[0:32, 1:NCH], in_=aend[0:32, 0:NCH - 1])
            P.op("dve", "tensor_copy", [X4], [aprev], out=aprev[32:64, 0:3], in_=aend[32:64, 1:4])
            P.op("dve", "tensor_copy", [X4], [aprev], out=aprev[32:64, 4:NCH - 1], in_=aend[32:64, 5:NCH])
            P.op("dve", "tensor_copy", [X4], [aprev], out=aprev[32:64, NCH - 1:NCH], in_=aend[32:64, 0:1])
            P.op("dve", "tensor_tensor", [aprev, X4], [dec], out=dec[:], in0=aprev[:], in1=aend, op=ALU.subtract)
            P.op("act", "activation", [dec], [dec], out=dec[:], in_=dec[:], func=AF.Exp)
            P.op("dve", "tensor_tensor", [X3, X4], [X3], out=X3[:], in0=X3[:], in1=X4[:], op=ALU.subtract)
            P.op("act", "activation", [X3], [X3], out=X3[:], in_=X3[:], func=AF.Exp)
            P.op("dve", "tensor_tensor", [X2, X4], [X2], out=X2[:], in0=X2[:], in1=X4[:], op=ALU.subtract)
            P.op("act", "activation", [X2], [X2], out=X2[:], in_=X2[:], func=AF.Exp)
            for (srcX, dstT) in ((X3, esT), (X2, fT)):
                for b0 in range(0, NB, 8):
                    nb = min(8, NB - b0)
                    ps = pst[(b0 // 8) % 2]
                    for bb in range(nb):
                        P.op("pe", "transpose", [srcX, ident_f], [ps], ps[:, bb, :], srcX[:, (b0 + bb) * 128:(b0 + bb + 1) * 128],
                             ident_f[0:64, 0:64])
                    P.op("act", "copy", [ps], [dstT], out=dstT[:, b0:b0 + nb, :], in_=ps[:, 0:nb, :])
            for idx in range(8):
                r = (idx // 4) * 32 + idx % 4
                P.op("dve", "tensor_copy", [ident_f], [sel], out=sel[:], in_=ident_f[0:64, r:r + 1].to_broadcast([64, 128]))
                P.op("pe", "matmul", [sel, dec], [psd], psd[:, :], sel[:, :], dec[:, :], start=True, stop=True)
                P.op("act", "copy", [psd], [decbc], out=decbc[:, idx, :], in_=psd[:, :])
            P.barrier()
        P.op("dve", "tensor_scalar", [esT], [esT], out=esT[:], in0=esT[:], scalar1=128.0 ** -0.5, scalar2=None, op0=ALU.mult)
        xraw = A.sb([128, T], F32, "xraw")
        cvw = A.sb([128, 8, 4], F32, "cvw")
        P.dma("sp", cvw[:], IN["l0_convT"][:, :, :], writes=[cvw])
        dg = [A.sb([128, 3, 128], F32, "dg%d" % i) for i in range(2)]
        qT = A.sb([128, T], BF16, "mqT")
        qd = [A.sb([128, T], BF16, "mqd%d" % d) for d in range(2)]
        kT = A.sb([128, T], BF16, "mkT")
        kTok = A.sb([128, NB, 128], BF16, "kTok")
        vtok = A.sb([128, NB, 128], BF16, "vtok")
        vpp = [A.sb([128, NB, 129], BF16, "vpp%d" % d) for d in range(2)]
        SmT = [A.sb([128, NB, 64], BF16, "SmT%d" % d) for d in range(2)]
        hbuf = [A.sb([128, NB, 129], F32, "hbuf%d" % d) for d in range(2)]
        Cst = [[A.sb([128, 129], F32, "C%d_%d" % (d, i)) for i in range(2)] for d in range(2)]
        Cb = [[A.sb([128, 129], BF16, "Cb%d_%d" % (d, i)) for i in range(2)] for d in range(2)]
        dn = [A.sb([128, NB], F32, "dn%d" % d) for d in range(2)]
        pcv = [A.ps([128, 512], F32, "pcv%d" % i) for i in range(2)]
        pU = [A.ps([128, 129], F32, "pU%d" % i) for i in range(2)]
        pN = [[A.ps([128, 129], F32, "pN%d_%d" % (d, i)) for i in range(2)] for d in range(2)]
        order = [list(range(NCH)), [3, 2, 1, 0] + list(range(NCH - 1, 3, -1))]
        pieces = [(0, TC)] + [(TC + 512 * i, TC + 512 * (i + 1)) for i in range(8)]
        pc = 0
        for h in range(4):
            for which in range(2):
                ch = which * 4 + h
                dg_ = dg[which]
                P.dma("sp" if which == 0 else "pool", xraw[:], SC["MQK"][ch * 128:(ch + 1) * 128, :], writes=[xraw])
                for j in range(3):
                    P.op("dve", "tensor_scalar", [ident_f, cvw], [dg_], out=dg_[:, j, :], in0=ident_f[:], scalar1=cvw[:, ch, j:j + 1],
                         scalar2=None, op0=ALU.mult)
                dst = qT if which == 0 else kT
                for (a, b) in pieces:
                    s0, s1 = (0, TC) if a < TC else (TC, T)
                    ps = pcv[pc % 2]
                    pc += 1
                    P.op("pe", "matmul", [dg_, xraw], [ps], ps[:, 0:b - a], dg_[:, 1, :], xraw[:, a:b], start=True, stop=False)
                    lo = max(a, s0 + 1)
                    P.op("pe", "matmul", [dg_, xraw], [ps], ps[:, lo - a:b - a], dg_[:, 0, :], xraw[:, lo - 1:b - 1], start=False,
                         stop=False)
                    hi = min(b, s1 - 1)
                    P.op("pe", "matmul", [dg_, xraw], [ps], ps[:, 0:hi - a], dg_[:, 2, :], xraw[:, a + 1:hi + 1], start=False,
                         stop=True)
                    P.op("act", "activation", [ps, cvw], [dst], out=dst[:, a:b], in_=ps[:, 0:b - a], func=AF.Silu,
                         bias=cvw[:, ch, 3:4])
            for b0 in range(0, NB, 4):
                nb = min(4, NB - b0)
                ps = pcv[pc % 2]
                pc += 1
                psb = ps[:, 0:256].bitcast(BF16)
                for bb in range(nb):
                    P.op("pe", "transpose", [kT, ident_bf], [ps], psb[:, bb * 128:(bb + 1) * 128],
                         kT[:, (b0 + bb) * 128:(b0 + bb + 1) * 128], ident_bf[:])
                P.op("act", "copy", [ps], [kTok], out=kTok[:, b0:b0 + nb, :],
                     in_=psb[:, 0:nb * 128].rearrange("p (b j) -> p b j", j=128))
            P.dma("sp", vtok[:], SC["MV"][:, h * 128:(h + 1) * 128].rearrange("(b p) j -> p b j", p=128), writes=[vtok])
            for d in range(2):
                col = d * 32 + h
                idx = d * 4 + h
                P.op("pool" if d == 0 else "dve", "tensor_tensor", [vtok, esT], [vpp[d]], out=vpp[d][:, :, 0:128], in0=vtok[:],
                     in1=esT[:, :, col:col + 1].to_broadcast([128, NB, 128]), op=ALU.mult)
                P.op("dve", "tensor_copy", [esT], [vpp[d]], out=vpp[d][:, :, 128:129], in_=esT[:, :, col:col + 1])
                P.op("pool" if d == 1 else "dve", "tensor_tensor", [qT, decbc], [qd[d]],
                     out=qd[d][:, :].rearrange("p (c j) -> p c j", j=64), in0=qT[:, :].rearrange("p (c j) -> p c j", j=64),
                     in1=decbc[:, idx, :].unsqueeze(2).to_broadcast([128, NCH, 64]), op=ALU.mult)
            for b0 in range(0, NB, 4):
                nb = min(4, NB - b0)
                ps = pcv[pc % 2]
                pc += 1
                psv = ps[:, 0:256].rearrange("p (b j) -> p b j", j=64)
                for bb in range(nb):
                    for half in range(2):
                        c = (b0 + bb) * 2 + half
                        pr = slice(half * 64, half * 64 + 64)
                        P.op("pe", "matmul", [kT, qT], [ps], psv[pr, bb, :], kT[:, c * 64:(c + 1) * 64], qT[:, c * 64:(c + 1) * 64],
                             start=True, stop=True)
                for d in range(2):
                    P.op("dve", "tensor_tensor", [ps, mask[d]], [SmT[d]], out=SmT[d][:, b0:b0 + nb, :], in0=psv[:, 0:nb, :],
                         in1=mask[d][:].unsqueeze(1).to_broadcast([128, nb, 64]), op=ALU.mult)
            for d in range(2):
                P.op("pool", "memset", [], [Cst[d][0]], Cst[d][0][:], 0.0)
                P.op("pool", "memset", [], [Cb[d][0]], Cb[d][0][:], 0.0)
            for i in range(NCH):
                for d in range(2):
                    c = order[d][i]
                    b, half = c // 2, c % 2
                    pr = slice(half * 64, half * 64 + 64)
                    idx = d * 4 + h
                    Cold, Cnew = Cst[d][i % 2], Cst[d][(i + 1) % 2]
                    cbo, cbn = Cb[d][i % 2], Cb[d][(i + 1) % 2]
                    U = pU[d]
                    N = pN[d][i % 2]
                    P.op("pe", "matmul", [kTok, vpp[d]], [U], U[:, :], kTok[pr, b, :], vpp[d][pr, b, :], start=True, stop=True)
                    P.op("pe", "matmul", [SmT[d], vpp[d]], [N], N[pr, :], SmT[d][pr, b, :], vpp[d][pr, b, :], start=True,
                         stop=False)
                    P.op("pe", "matmul", [qd[d], cbo], [N], N[pr, :], qd[d][:, c * 64:(c + 1) * 64], cbo[:], start=False, stop=True)
                    P.op("dve", "scalar_tensor_tensor", [Cold, decbc, U], [cbn], out=cbn[:], in0=Cold[:],
                         scalar=decbc[:, idx, c:c + 1], in1=U[:, :], op0=ALU.mult, op1=ALU.add)
                    P.op("dve", "scalar_tensor_tensor", [Cold, decbc, U], [Cnew], out=Cnew[:], in0=Cold[:],
                         scalar=decbc[:, idx, c:c + 1], in1=U[:, :], op0=ALU.mult, op1=ALU.add)
                    P.op("act", "copy", [N], [hbuf[d]], out=hbuf[d][pr, b, :], in_=N[pr, :])
            for d in range(2):
                col = d * 32 + h
                P.op("act", "activation", [hbuf[d]], [dn[d]], out=dn[d][:, :].unsqueeze(2), in_=hbuf[d][:, :, 128:129], func=AF.Abs)
                P.op("dve", "tensor_tensor", [dn[d], fT], [dn[d]], out=dn[d][:, :].unsqueeze(2), in0=dn[d][:, :].unsqueeze(2),
                     in1=fT[:, :, col:col + 1], op=ALU.max)
                P.op("dve", "reciprocal", [dn[d]], [dn[d]], out=dn[d][:], in_=dn[d][:])
                P.op("dve" if d == 0 else "pool", "tensor_tensor", [hbuf[d], dn[d]], [hbuf[d]], out=hbuf[d][:, :, 0:128],
                     in0=hbuf[d][:, :, 0:128], in1=dn[d][:, :].unsqueeze(2).to_broadcast([128, NB, 128]), op=ALU.mult)
                P.dma("sp" if d == 0 else "pool", SC["HM"][d, :, h * 128:(h + 1) * 128].rearrange("(b p) j -> p b j", p=128),
                      hbuf[d][:, :, 0:128], reads=[hbuf[d]])
        P.barrier()


def combine_phase(nc, P, IN, SC, ident_bf, src0, src1, mul, norm_name, cat_row0, groups):
    with ExitStack() as es:
        A = Alloc(nc, es)
        nrow = A.sb([128, 512], F32, "nrow")
        P.dma("sp", nrow[:], IN[norm_name][0:1, :].to_broadcast([128, 512]), writes=[nrow])
        epsT = A.sb([128, 1], F32, "epsTc")
        P.op("pool", "memset", [], [epsT], epsT[:], EPS)
        a_ = [A.sb([128, 512], F32, "cA%d" % i) for i in range(2)]
        b_ = [A.sb([128, 512], F32, "cB%d" % i) for i in range(2)]
        m_ = [A.sb([128, 512], BF16, "cM%d" % i) for i in range(2)]
        junk = A.sb([128, 128], F32, "cjunk")
        ss = [A.sb([128, 4], F32, "css%d" % i) for i in range(2)]
        hn = [A.sb([128, 512], F32, "chn%d" % i) for i in range(2)]
        hb = [A.sb([128, 512], BF16, "chb%d" % i) for i in range(2)]
        sz = [A.sb([128, 512], BF16, "csz%d" % i) for i in range(2)]
        oo = [A.sb([128, 512], BF16, "coo%d" % i) for i in range(2)]
        tp = [A.ps([128, 512], BF16, "ctp%d" % i) for i in range(4)]
        it = 0
        for (t0, n) in groups:
            ntl = n // 128
            for ti in range(ntl):
                tok = t0 + ti * 128
                a, b, m, s_, h_, hb_ = a_[it % 2], b_[it % 2], m_[it % 2], ss[it % 2], hn[it % 2], hb[it % 2]
                it += 1
                P.dma("sp", a[:], src0[tok:tok + 128, :], writes=[a])
                P.dma("pool", b[:], src1[tok:tok + 128, :], writes=[b])
                P.op("dve", "tensor_tensor", [a, b], [a], out=a[:], in0=a[:], in1=b[:], op=ALU.add)
                if mul is not None:
                    P.dma("sp", m[:], mul[tok:tok + 128, :], writes=[m])
                    P.op("pool", "tensor_tensor", [a, m], [a], out=a[:], in0=a[:], in1=m[:], op=ALU.mult)
                for hh in range(4):
                    P.op("act", "activation", [a], [junk, s_], out=junk[:], in_=a[:, hh * 128:(hh + 1) * 128], func=AF.Square,
                         accum_out=s_[:, hh:hh + 1])
                P.op("act", "activation", [s_, epsT], [s_], out=s_[:], in_=s_[:], func=AF.Sqrt, scale=1.0 / 128, bias=epsT[:, 0:1])
                P.op("dve", "reciprocal", [s_], [s_], out=s_[:], in_=s_[:])
                for hh in range(4):
                    P.op("act", "activation", [a, s_], [h_], out=h_[:, hh * 128:(hh + 1) * 128], in_=a[:, hh * 128:(hh + 1) * 128],
                         func=AF.Copy, scale=s_[:, hh:hh + 1])
                P.op("dve", "tensor_tensor", [h_, nrow], [hb_], out=hb_[:], in0=h_[:], in1=nrow[:], op=ALU.mult)
                for j in range(4):
                    P.op("pe", "transpose", [hb_, ident_bf], [tp[j]], tp[j][:, ti * 128:(ti + 1) * 128],
                         hb_[:, j * 128:(j + 1) * 128], ident_bf[:])
            for j in range(4):
                r0 = cat_row0 + j * 128
                z_, o_ = sz[j % 2], oo[j % 2]
                P.dma("sp", z_[:, 0:n], SC["SZT"][r0:r0 + 128, t0:t0 + n], writes=[z_])
                P.op("dve", "tensor_tensor", [tp[j], z_], [o_], out=o_[:, 0:n], in0=tp[j][:, 0:n], in1=z_[:, 0:n], op=ALU.mult)
                P.dma("pool", SC["CATT"][r0:r0 + 128, t0:t0 + n], o_[:, 0:n], reads=[o_])
        P.barrier()


def phase_C(nc, P, IN, SC, layer, gateR, out_ap):
    with ExitStack() as es:
        A = Alloc(nc, es)
        w = A.sb([128, 8, 1024], BF16, "w_out")
        with ExitStack() as es2:
            A2 = Alloc(nc, es2)
            stg = [A2.sb([128, 8, 512], F32, "wstg%d" % i) for i in range(2)]
            for i in range(2):
                P.dma("sp" if i == 0 else "pool", stg[i][:],
                      IN["l%d_w_out" % layer][:, i * 512:(i + 1) * 512].rearrange("(k p) n -> p k n", p=128), writes=[stg[i]])
                P.op("dve" if i == 0 else "act", "tensor_copy" if i == 0 else "copy", [stg[i]], [w], out=w[:, :, i * 512:(i + 1) * 512],
                     in_=stg[i][:])
            P.barrier()
        cat = [A.sb([128, 8, 512], BF16, "catT%d" % i) for i in range(2)]
        hold = [A.sb([128, 1024], F32, "hold%d" % i) for i in range(2)]
        tmp = [A.sb([128, 1024], F32, "ctmp%d" % i) for i in range(2)]
        hnew = [A.sb([128, 1024], F32, "hnew%d" % i) for i in range(2)]
        ps = [A.ps([128, 512], F32, "yps%d" % i) for i in range(4)]
        if layer == 1:
            frow = A.sb([128, 1024], F32, "frow")
            P.dma("sp", frow[:], IN["final_norm"][0:1, :].to_broadcast([128, 1024]), writes=[frow])
            epsT = A.sb([128, 1], F32, "epsTf")
            P.op("pool", "memset", [], [epsT], epsT[:], EPS)
            junk = A.sb([128, 1024], F32, "fjunk")
            st = [A.sb([128, 1], F32, "fst%d" % i) for i in range(2)]
            ob = [A.sb([128, 1024], F32, "fob%d" % i) for i in range(2)]
        it = 0
        groups = GROUPS if layer == 0 else GROUPS[1:]
        for gi, (t0, n) in enumerate(groups):
            c_ = cat[gi % 2]
            for k2 in range(2):
                P.dma("sp" if k2 == 0 else "pool", c_[:, k2 * 4:(k2 + 1) * 4, 0:n],
                      SC["CATT"][k2 * 512:(k2 + 1) * 512, t0:t0 + n].rearrange("(k p) t -> p k t", p=128), writes=[c_])
            g_ = gateR[1] if (layer == 0 and t0 == 0) else gateR[0]
            for ti in range(n // 128):
                tok = t0 + ti * 128
                ho, tm, hn_ = hold[it % 2], tmp[it % 2], hnew[it % 2]
                if layer == 0:
                    srcp = IN["ctx"][tok:tok + 128, :] if t0 == 0 else IN["x"][tok - TC:tok - TC + 128, :]
                else:
                    srcp = SC["H1"][tok:tok + 128, :]
                P.dma("sp", ho[:], srcp, writes=[ho])
                for half in range(2):
                    p_ = ps[(it * 2 + half) % 4]
                    for k in range(8):
                        P.op("pe", "matmul", [c_, w], [p_], p_[:, :], c_[:, k, ti * 128:(ti + 1) * 128],
                             w[:, k, half * 512:(half + 1) * 512], start=(k == 0), stop=(k == 7))
                    P.op("dve", "tensor_tensor", [p_, g_], [tm], out=tm[:, half * 512:(half + 1) * 512], in0=p_[:, :],
                         in1=g_[:, half * 512:(half + 1) * 512], op=ALU.mult)
                P.op("pool", "tensor_tensor", [tm, ho], [hn_], out=hn_[:], in0=tm[:], in1=ho[:], op=ALU.add)
                if layer == 0:
                    P.dma("pool", SC["H1"][tok:tok + 128, :], hn_[:], reads=[hn_])
                else:
                    s_, o_ = st[it % 2], ob[it % 2]
                    P.op("act", "activation", [hn_], [junk, s_], out=junk[:], in_=hn_[:], func=AF.Square, accum_out=s_[:, 0:1])
                    P.op("act", "activation", [s_, epsT], [s_], out=s_[:], in_=s_[:], func=AF.Sqrt, scale=1.0 / D, bias=epsT[:, 0:1])
                    P.op("dve", "reciprocal", [s_], [s_], out=s_[:], in_=s_[:])
                    P.op("act", "activation", [hn_, s_], [o_], out=o_[:], in_=hn_[:], func=AF.Copy, scale=s_[:, 0:1])
                    P.op("dve", "tensor_tensor", [o_, frow], [o_], out=o_[:], in0=o_[:], in1=frow[:], op=ALU.mult)
                    P.dma("pool", out_ap[tok - TC:tok - TC + 128, :], o_[:], reads=[o_])
                it += 1
        P.barrier()


def gla_phase(nc, P, IN, SC, ident_bf):
    NB = NT
    NCH = T // 64
    with ExitStack() as es:
        A = Alloc(nc, es)
        mask = [A.sb([128, 64], F32, "gmask%d" % d) for d in range(2)]
        for d in range(2):
            P.op("pool", "memset", [], [mask[d]], mask[d][:], 1.0)
            for half in range(2):
                pr = slice(half * 64, half * 64 + 64)
                P.op("pool", "affine_select", [mask[d]], [mask[d]], out=mask[d][pr, :], in_=mask[d][pr, :],
                     pattern=[[1 if d == 0 else -1, 64]], compare_op=ALU.is_ge, fill=0.0, base=0,
                     channel_multiplier=-1 if d == 0 else 1)
        rm = [A.sb([64, T], F32, "rm%d" % d) for d in range(2)]
        for d in range(2):
            P.op("pool", "memset", [], [rm[d]], rm[d][:], 1.0)
            j0 = 0 if d == 0 else 63
            P.op("pool", "memset", [rm[d]], [rm[d]], rm[d][:, :].rearrange("p (c j) -> p c j", j=64)[:, :, j0:j0 + 1], 0.0)
        qf = A.sb([64, T], F32, "gqf")
        kf = A.sb([64, T], F32, "gkf")
        lg = A.sb([64, T], F32, "glg")
        bc = A.sb([64, T], F32, "gbc")
        tmp = A.sb([64, T], F32, "gtmp")
        qg = [A.sb([64, T], BF16, "qg%d" % d) for d in range(2)]
        kg = [A.sb([64, T], BF16, "kg%d" % d) for d in range(2)]
        kbf = A.sb([64, T], BF16, "kbf")
        eb = [A.sb([64, NCH], F32, "eb%d" % d) for d in range(2)]
        kbTok = [A.sb([128, NB, 64], BF16, "kbTok%d" % d) for d in range(2)]
        SmT = [A.sb([128, NB, 64], BF16, "gSmT%d" % d) for d in range(2)]
        vtok = A.sb([128, NB, 128], BF16, "gvtok")
        obuf = [[A.sb([128, 128], F32, "gob%d_%d" % (d, i)) for i in range(3)] for d in range(2)]
        Sst = [[A.sb([64, 128], F32, "S%d_%d" % (d, i)) for i in range(2)] for d in range(2)]
        Sb = [[A.sb([64, 128], BF16, "Sb%d_%d" % (d, i)) for i in range(2)] for d in range(2)]
        ptr = [A.ps([128, 8, 64], BF16, "gptr%d" % i) for i in range(2)]
        pS = [A.ps([128, 8, 64], F32, "gpS%d" % i) for i in range(2)]
        pU = [A.ps([128, 128], F32, "gpU%d" % i) for i in range(2)]
        pN = [A.ps([128, 128], F32, "gpN%d" % i) for i in range(2)]
        order = [list(range(NCH)), [3, 2, 1, 0] + list(range(NCH - 1, 3, -1))]
        for h in range(4):
            P.dma("sp", qf[:], SC["MQK"][h * 64:(h + 1) * 64, :], writes=[qf])
            P.dma("pool", kf[:], SC["MQK"][256 + h * 64:256 + (h + 1) * 64, :], writes=[kf])
            P.dma("sp", vtok[:], SC["MV"][:, h * 128:(h + 1) * 128].rearrange("(b p) j -> p b j", p=128), writes=[vtok])
            for d in range(2):
                P.dma("pool", lg[:], SC["LG"][d, h * 64:(h + 1) * 64, :], writes=[lg])
                if d == 0:
                    P.op("dve", "tensor_tensor_scan", [rm[d], lg], [bc], out=bc[:, :], data0=rm[d][:, :], data1=lg[:, :],
                         initial=0.0, op0=ALU.mult, op1=ALU.add)
                else:
                    P.op("dve", "tensor_tensor_scan", [rm[d], lg], [bc], out=bc[:, ::-1], data0=rm[d][:, ::-1],
                         data1=lg[:, ::-1], initial=0.0, op0=ALU.mult, op1=ALU.add)
                jl = 63 if d == 0 else 0
                bl = bc[:, :].rearrange("p (c j) -> p c j", j=64)[:, :, jl:jl + 1]
                P.op("act", "activation", [bc], [eb[d]], out=eb[d][:, :].unsqueeze(2), in_=bl, func=AF.Exp)
                P.op("act", "activation", [bc], [tmp], out=tmp[:], in_=bc[:], func=AF.Exp)
                P.op("dve", "scalar_tensor_tensor", [qf, tmp], [qg[d]], out=qg[d][:], in0=qf[:], scalar=64.0 ** -0.5, in1=tmp[:],
                     op0=ALU.mult, op1=ALU.mult)
                P.op("act", "activation", [bc], [tmp], out=tmp[:], in_=bc[:], func=AF.Exp, scale=-1.0)
                P.op("dve", "tensor_tensor", [kf, tmp], [kg[d]], out=kg[d][:], in0=kf[:], in1=tmp[:], op=ALU.mult)
                P.op("dve", "tensor_tensor", [bc], [tmp], out=tmp[:, :].rearrange("p (c j) -> p c j", j=64),
                     in0=bl.to_broadcast([64, NCH, 64]), in1=bc[:, :].rearrange("p (c j) -> p c j", j=64), op=ALU.subtract)
                P.op("act", "activation", [tmp], [tmp], out=tmp[:], in_=tmp[:], func=AF.Exp)
                P.op("dve", "tensor_tensor", [kf, tmp], [kbf], out=kbf[:], in0=kf[:], in1=tmp[:], op=ALU.mult)
                for b0 in range(0, NB, 8):
                    nb = min(8, NB - b0)
                    ps = ptr[(b0 // 8) % 2]
                    for bb in range(nb):
                        P.op("pe", "transpose", [kbf, ident_bf], [ps], ps[:, bb, :], kbf[:, (b0 + bb) * 128:(b0 + bb + 1) * 128],
                             ident_bf[0:64, 0:64])
                    P.op("act", "copy", [ps], [kbTok[d]], out=kbTok[d][:, b0:b0 + nb, :], in_=ps[:, 0:nb, :])
                for b0 in range(0, NB, 8):
                    nb = min(8, NB - b0)
                    ps = pS[(b0 // 8) % 2]
                    for bb in range(nb):
                        for half in range(2):
                            c = (b0 + bb) * 2 + half
                            pr = slice(half * 64, half * 64 + 64)
                            P.op("pe", "matmul", [kg[d], qg[d]], [ps], ps[pr, bb, :], kg[d][:, c * 64:(c + 1) * 64],
                                 qg[d][:, c * 64:(c + 1) * 64], start=True, stop=True)
                    P.op("dve", "tensor_tensor", [ps, mask[d]], [SmT[d]], out=SmT[d][:, b0:b0 + nb, :], in0=ps[:, 0:nb, :],
                         in1=mask[d][:].unsqueeze(1).to_broadcast([128, nb, 64]), op=ALU.mult)
            for d in range(2):
                P.op("pool", "memset", [], [Sst[d][0]], Sst[d][0][:], 0.0)
                P.op("pool", "memset", [], [Sb[d][0]], Sb[d][0][:], 0.0)
            for i in range(NCH):
                for d in range(2):
                    c = order[d][i]
                    b, half = c // 2, c % 2
                    pr = slice(half * 64, half * 64 + 64)
                    Sold, Snew = Sst[d][i % 2], Sst[d][(i + 1) % 2]
                    sbo, sbn = Sb[d][i % 2], Sb[d][(i + 1) % 2]
                    U, N = pU[d], pN[d]
                    ob = obuf[d][(i // 2) % 3]
                    P.op("pe", "matmul", [SmT[d], vtok], [N], N[pr, :], SmT[d][pr, b, :], vtok[pr, b, :], start=True, stop=False)
                    P.op("pe", "matmul", [qg[d], sbo], [N], N[pr, :], qg[d][:, c * 64:(c + 1) * 64], sbo[:, :], start=False,
                         stop=True)
                    P.op("pe", "matmul", [kbTok[d], vtok], [U], U[0:64, :], kbTok[d][pr, b, :], vtok[pr, b, :], start=True,
                         stop=True)
                    P.op("dve", "scalar_tensor_tensor", [Sold, eb[d], U], [Snew], out=Snew[:], in0=Sold[:],
                         scalar=eb[d][:, c:c + 1], in1=U[0:64, :], op0=ALU.mult, op1=ALU.add)
                    P.op("pool", "tensor_copy", [Snew], [sbn], out=sbn[:], in_=Snew[:])
                    P.op("act", "copy", [N], [ob], out=ob[pr, :], in_=N[pr, :])
                    if i % 2 == 1:
                        P.dma("sp" if d == 0 else "pool", SC["HM"][d, b * 128:(b + 1) * 128, h * 128:(h + 1) * 128], ob[:],
                              reads=[ob])
        P.barrier()


def _na_rows_ok(qr, kr):
    lo = min(max(qr - 4, 0), 56)
    return lo <= kr < lo + 8


def _na_cfg(g):
    if g == 0:
        return "first", 0, 6
    if g == 7:
        return "last", 26, 6
    return "mid", 4 * g - 2, 8


def _na_range(g, ktl):
    js = [j for j in range(8) for i in range(2) if _na_rows_ok(8 * g + j, 2 * ktl + i)]
    return min(js), max(js)


def na_bias_fn(nc, P, IN, state):
    def fn(h, A):
        if A is not None:
            state.setdefault("sets", {})
            for key, ntile in (("first", 6), ("mid", 8), ("last", 6)):
                state["sets"][(key, h % 2)] = [A.sb([128, 512], F32, "nab_%s%d_%d" % (key, i, h % 2)) for i in range(ntile)]
        out = {}
        for key, g, t_lo in (("first", 0, 0), ("mid", 1, 2), ("last", 7, 26)):
            tiles = state["sets"][(key, h % 2)]
            for r, bt in enumerate(tiles):
                ktl = t_lo + r
                u0, u1 = _na_range(g, ktl)
                P.op("pool", "memset", [], [bt], bt[:, u0 * 64:(u1 + 1) * 64], MASKV)
                for i in range(2):
                    kr = 2 * ktl + i
                    js = [j for j in range(8) if _na_rows_ok(8 * g + j, kr)]
                    if not js:
                        continue
                    j0, j1 = js[0], js[-1]
                    assert js == list(range(j0, j1 + 1))
                    m0 = 7 - (kr - 8 * g - j0)
                    nj = j1 - j0 + 1
                    P.dma("sp" if i == 0 else "pool", bt[i * 64:(i + 1) * 64, j0 * 64:(j1 + 1) * 64].rearrange("p (m q) -> p m q", q=64),
                          IN["na_bias"][h, m0:m0 + nj, :, :].rearrange("m k q -> k m q"), writes=[bt])
        for g in range(8):
            key, t_lo, nt_ = _na_cfg(g)
            for r in range(nt_):
                out[(g, 2 + t_lo + r)] = state["sets"][(key, h % 2)][r]
        return out

    return fn


def _shapes(d):
    return {k: (v.shape, "bf16" if v.dtype == ml_dtypes.bfloat16 else "f32") for k, v in d.items()}


def run(inputs, stage=99, debug=(), cores=8, skip=()):
    inputs = {k: np.asarray(v) for k, v in inputs.items()}
    sh, per = prep_inputs(inputs)
    nc = build(_shapes(sh), _shapes(per[0]), stage=stage, debug=debug, skip=skip)
    in_maps = [dict(sh, **per[b]) for b in range(cores)]
    res = run_bass_kernel_spmd(nc, in_maps, core_ids=list(range(cores)))
    return res


def kernel(**inputs):
    res = run(inputs)
    return np.stack([np.asarray(r["out"], dtype=np.float32) for r in res.results], axis=0)
```

```python
import numpy as np
from contextlib import ExitStack
import ml_dtypes
import concourse.bass as bass
import concourse.mybir as mybir
from concourse.bass_utils import run_bass_kernel_spmd

F32 = mybir.dt.float32
BF16 = mybir.dt.bfloat16
AF = mybir.ActivationFunctionType
ALU = mybir.AluOpType
AX = mybir.AxisListType

D = 1024
TC = 256
TL = 4096
T = TC + TL
NT = T // 128
EPS = 1e-6
MASKV = -30000.0

GROUPS = [(0, 256)] + [(256 + 512 * i, 512) for i in range(8)]


class Dep:
    __slots__ = ("w", "r")

    def __init__(self):
        self.w = None
        self.r = {}


class Tile:
    def __init__(self, t):
        self.t = t
        self.d = Dep()

    def __getitem__(self, k):
        return self.t[k]


class Prog:
    def __init__(self, nc, es):
        self.nc = nc
        self.eng = {"pe": nc.tensor, "act": nc.scalar, "dve": nc.vector, "pool": nc.gpsimd, "sp": nc.sync}
        self.R = 12
        self.keys = [("pe", "c"), ("act", "c"), ("dve", "c"), ("pool", "c")]
        for q in ("sp", "pool"):
            self.keys += [(q, "d%d" % i) for i in range(self.R)]
        self.ndma = {"sp": 0, "pool": 0}
        self.sem = {k: es.enter_context(nc.semaphore("s_%s_%s" % k)) for k in self.keys}
        self.cnt = {k: 0 for k in self.keys}
        self.waited = {e: {} for e in self.eng}
        self.n = 0

    def _emit(self, eng, kind, fn, reads, writes):
        if kind == "d":
            kind = "d%d" % (self.ndma[eng] % self.R)
            self.ndma[eng] += 1
        key = (eng, kind)
        deps = {}
        if kind != "c" and self.cnt[key] > 0:
            deps[key] = self.cnt[key]

        def add(tok):
            if tok is None:
                return
            k, v = tok
            if deps.get(k, 0) < v:
                deps[k] = v

        for b in reads:
            add(b.d.w)
        for b in writes:
            add(b.d.w)
            for k, v in b.d.r.items():
                add((k, v))
        e = self.eng[eng]
        wd = self.waited[eng]
        for k, v in deps.items():
            if k == ("pe", "c") and eng == "pe":
                continue
            if wd.get(k, 0) >= v:
                continue
            e.wait_ge(self.sem[k], v)
            wd[k] = v
        inc = 16 if kind != "c" else 1
        self.cnt[key] += inc
        fn(e).then_inc(self.sem[key], inc)
        v = self.cnt[key]
        for b in reads:
            if b.d.r.get(key, 0) < v:
                b.d.r[key] = v
        for b in writes:
            b.d.w = (key, v)
            b.d.r = {}
        self.n += 1

    def op(self, eng, name, reads, writes, *a, **kw):
        self._emit(eng, "c", lambda e: getattr(e, name)(*a, **kw), reads, writes)

    def dma(self, q, out, in_, reads=(), writes=(), **kw):
        self._emit(q, "d", lambda e: e.dma_start(out=out, in_=in_, **kw), reads, writes)

    def barrier(self):
        for en, e in self.eng.items():
            wd = self.waited[en]
            for k in self.keys:
                v = self.cnt[k]
                if v > 0 and wd.get(k, 0) < v:
                    e.wait_ge(self.sem[k], v)
                    wd[k] = v


class Alloc:
    def __init__(self, nc, es):
        self.nc = nc
        self.es = es
        _CTR.setdefault(id(nc), 0)

    def _nm(self, name):
        _CTR[id(self.nc)] = _CTR.get(id(self.nc), 0) + 1
        return "%s_%d" % (name, _CTR[id(self.nc)])

    def sb(self, shape, dt, name=None):
        return Tile(self.es.enter_context(self.nc.sbuf_tensor(self._nm(name or "sb"), list(shape), dt)))

    def ps(self, shape, dt, name=None):
        return Tile(self.es.enter_context(self.nc.psum_tensor(self._nm(name or "ps"), list(shape), dt)))


_CTR = {}


def _fm(v, nchunk):
    return np.ascontiguousarray(v.reshape(nchunk, 128).T)


def _rope_perm():
    perm = np.zeros(32, np.int64)
    for i in range(32):
        r = i % 16
        perm[i] = i + 8 if r < 8 else i - 8
    return perm


def _rope_tables():
    t = np.arange(TL)
    inv = (1.0 / (10000.0 ** (np.arange(8, dtype=np.float32) / 8))).astype(np.float32)
    pos = [(t // 64).astype(np.float32), (t % 64).astype(np.float32)]
    C = np.zeros((32, TL), np.float32)
    S = np.zeros((32, TL), np.float32)
    for i in range(32):
        a = i // 16
        r = i % 16
        p = r % 8
        ang = (pos[a] * inv[p]).astype(np.float32)
        C[i] = np.cos(ang)
        S[i] = -np.sin(ang) if r < 8 else np.sin(ang)
    Cf = np.zeros((128, TL), np.float32)
    Sf = np.zeros((128, TL), np.float32)
    Cf[0:32] = C
    Cf[64:96] = C
    Sf[0:32] = S
    Sf[64:96] = S
    return Cf, Sf


def prep_inputs(inp):
    sh = {}
    sh["ident_bf"] = np.eye(128, dtype=np.float32).astype(ml_dtypes.bfloat16)
    sh["ident_f"] = np.eye(128, dtype=np.float32)
    perm = _rope_perm()
    w_in = inp["l0_w_in"]
    gi_cols = [2720 + d * 8 + h for d in range(2) for h in range(4)]
    gf_cols = [2720 + d * 8 + 4 + h for d in range(2) for h in range(4)]
    sh["l0_w_in"] = np.ascontiguousarray(
        np.concatenate([w_in, w_in[:, 640:672][:, perm], w_in[:, gi_cols], w_in[:, gf_cols]], axis=1))
    w_uq = inp["l0_mla_w_uq"].reshape(384, 8, 96)
    ext = np.concatenate([w_uq, w_uq[:, :, 0:64], w_uq[:, :, 64:96][:, :, perm]], axis=2)
    sh["l0_w_uq"] = np.ascontiguousarray(ext.reshape(384, 8 * 192))
    w_ukv = inp["l0_mla_w_ukv"].reshape(256, 8, 128)
    sh["l0_w_ukv"] = np.ascontiguousarray(
        np.concatenate([w_ukv[:, :, 0:64].reshape(256, 512), w_ukv[:, :, 64:128].reshape(256, 512)], axis=1))
    sh["l0_qnT"] = _fm(inp["l0_mla_q_norm"], 3)
    sh["l0_kvnT"] = _fm(inp["l0_mla_kv_norm"], 2)
    Cf, Sf = _rope_tables()
    sh["ropeC"] = Cf
    sh["ropeS"] = Sf
    cw = inp["l0_mlstm_conv_w"]
    sh["l0_convT"] = np.ascontiguousarray(
        np.concatenate([cw.reshape(3, 8, 128).transpose(2, 1, 0), inp["l0_mlstm_conv_b"].reshape(8, 128).T[:, :, None]],
                       axis=2))
    gb = np.zeros((16, 1), np.float32)
    for d in range(2):
        for h in range(4):
            gb[d * 8 + h, 0] = inp["l0_mlstm_b_i"][d, h]
            gb[d * 8 + 4 + h, 0] = inp["l0_mlstm_b_f"][d, h]
    sh["l0_gbias"] = gb
    gb2 = np.zeros((64, 2), np.float32)
    for d in range(2):
        for h in range(4):
            gb2[d * 32 + h, 0] = inp["l0_mlstm_b_i"][d, h]
            gb2[d * 32 + h, 1] = inp["l0_mlstm_b_f"][d, h]
    sh["l0_gb2"] = gb2
    sh["l0_hnorm"] = np.ascontiguousarray(inp["l0_mlstm_norm"].reshape(1, 512))
    sh["l0_w_out"] = inp["l0_w_out"]
    sh["l1_w_in"] = inp["l1_w_in"]
    sh["l1_w_gate"] = np.ascontiguousarray(inp["l1_gla_w_gate"])
    sh["l1_bgT"] = np.ascontiguousarray(inp["l1_gla_b_gate"].reshape(2, 2, 128).transpose(2, 0, 1))
    sh["l1_gnorm"] = np.ascontiguousarray(inp["l1_gla_norm"].reshape(1, 512))
    sh["l1_w_out"] = inp["l1_w_out"]
    sh["final_norm"] = np.ascontiguousarray(inp["final_norm"].reshape(1, 1024))
    rpb = inp["l1_na_rpb"]
    kc = np.arange(64)[:, None]
    qc = np.arange(64)[None, :]
    wc0 = np.clip(qc - 8, 0, 48)
    okc = (kc >= wc0) & (kc < wc0 + 16)
    dcol = np.clip(kc - qc + 15, 0, 30)
    Tb = np.full((8, 15, 64, 64), MASKV, np.float32)
    for m in range(15):
        dr = 7 - m
        blk = rpb[:, dr + 7][:, dcol]
        Tb[:, m] = np.where(okc[None], blk, np.float32(MASKV))
    sh["na_bias"] = Tb
    mods = [(inp["l0_norm"], inp["l0_w_mod"], inp["l0_b_mod"]), (inp["l1_norm"], inp["l1_w_mod"], inp["l1_b_mod"])]
    for l, (g_, wm_, bm_) in enumerate(mods):
        sh["l%d_w_mod" % l] = wm_
        sh["l%d_bmodT" % l] = _fm(bm_, 24)
        sh["l%d_bmod_gate" % l] = np.ascontiguousarray(bm_[2048:3072].reshape(1, 1024))
        sh["l%d_gT" % l] = _fm(g_, 8)
    per = []
    for b in range(8):
        d = {}
        d["x"] = inp["x"][b]
        d["ctx"] = inp["ctx"][b]
        cv = np.stack([inp["c"][b], inp["c_ctx"]], axis=1)
        d["cvec"] = np.ascontiguousarray(cv.reshape(8, 128, 2).transpose(1, 0, 2))
        per.append(d)
    return sh, per


def build(sh_shapes, per_shapes, stage=99, debug=(), skip=()):
    nc = bass.Bass("TRN2", target_bir_lowering=False)
    IN = {}
    for k, (shape, dt) in list(sh_shapes.items()) + list(per_shapes.items()):
        IN[k] = nc.dram_tensor(k, list(shape), BF16 if dt == "bf16" else F32, kind="ExternalInput").ap()
    out = nc.dram_tensor("out", [TL, D], F32, kind="ExternalOutput").ap()

    def scratch(name, shape, dt):
        kind = "ExternalOutput" if name in debug else "Internal"
        return nc.dram_tensor(name, list(shape), dt, kind=kind).ap()

    SC = {}
    SC["H1"] = scratch("H1", [T, D], F32)
    SC["SZT"] = scratch("SZT", [1024, T], BF16)
    SC["CATT"] = scratch("CATT", [1024, T], BF16)
    SC["QT"] = scratch("QT", [8, 96, T], BF16)
    SC["KT"] = scratch("KT", [8, 96, T], BF16)
    SC["V"] = scratch("V", [T, 512], BF16)
    SC["MQK"] = scratch("MQK", [1024, T], F32)
    SC["GI"] = scratch("GI", [8, T], F32)
    SC["GF"] = scratch("GF", [8, T], F32)
    SC["MV"] = scratch("MV", [T, 512], BF16)
    SC["MO"] = scratch("MO", [T, 512], BF16)
    SC["HM"] = scratch("HM", [2, T, 512], F32)
    SC["LG"] = scratch("LG", [2, 256, T], F32)
    SC["NQ"] = scratch("NQ", [512, T], BF16)
    SC["NK"] = scratch("NK", [512, T], BF16)

    with ExitStack() as es0:
        P = Prog(nc, es0)
        A0 = Alloc(nc, es0)
        ident_bf = A0.sb([128, 128], BF16, "identbf")
        ident_f = A0.sb([128, 128], F32, "identf")
        ones_f = A0.sb([128, 128], F32, "onesf")
        P.dma("sp", ident_bf[:], IN["ident_bf"][:, :], writes=[ident_bf])
        P.dma("sp", ident_f[:], IN["ident_f"][:, :], writes=[ident_f])
        P.op("pool", "memset", [], [ones_f], ones_f[:], 1.0)
        affA = [A0.sb([128, 8, 2], F32, "affA%d" % l) for l in range(2)]
        affB = [A0.sb([128, 8, 2], F32, "affB%d" % l) for l in range(2)]
        gateR = [[A0.sb([128, 1024], F32, "gateR%d_%d" % (l, s)) for s in range(2 if l == 0 else 1)] for l in range(2)]

        with ExitStack() as es:
            A = Alloc(nc, es)
            cv = A.sb([128, 8, 2], F32, "cv")
            sc = A.sb([128, 8, 2], F32, "sc")
            screp = [A.sb([128, 8, 128], F32, "screp%d" % s) for s in range(2)]
            P.dma("sp", cv[:], IN["cvec"][:, :, :], writes=[cv])
            P.op("act", "activation", [cv], [sc], out=sc[:], in_=cv[:], func=AF.Silu)
            for s in range(2):
                for k in range(8):
                    P.op("dve", "tensor_copy", [sc], [screp[s]], out=screp[s][:, k, :],
                         in_=sc[:, k, s:s + 1].to_broadcast([128, 128]))
            wpan = [A.sb([128, 8, 384], F32, "wpan%d" % i) for i in range(2)]
            wgate = [A.sb([128, 512], F32, "wgate%d" % i) for i in range(3)]
            pm = A.ps([128, 24, 2], F32, "pm")
            pg = [A.ps([128, 512], F32, "pg%d" % i) for i in range(2)]
            bmT = A.sb([128, 24], F32, "bmT")
            gT = A.sb([128, 8], F32, "gT")
            modT = A.sb([128, 24, 2], F32, "modT")
            bgrow = A.sb([128, 1024], F32, "bgrow")
            for l in range(2):
                wm = IN["l%d_w_mod" % l]
                P.dma("sp", bmT[:], IN["l%d_bmodT" % l][:, :], writes=[bmT])
                P.dma("sp", gT[:], IN["l%d_gT" % l][:, :], writes=[gT])
                P.dma("sp", bgrow[:], IN["l%d_bmod_gate" % l][0:1, :].to_broadcast([128, 1024]), writes=[bgrow])
                for pn in range(8):
                    wp = wpan[pn % 2]
                    P.dma("sp" if pn % 2 == 0 else "pool", wp[:],
                          wm[:, pn * 384:(pn + 1) * 384].rearrange("(k p) n -> p k n", p=128), writes=[wp])
                    for j in range(3):
                        n = pn * 3 + j
                        for k in range(8):
                            P.op("pe", "matmul", [wp, sc], [pm], pm[:, n, :], wp[:, k, j * 128:(j + 1) * 128],
                                 sc[:, k, :], start=(k == 0), stop=(k == 7))
                P.op("dve", "tensor_tensor", [pm, bmT], [modT], out=modT[:], in0=pm[:],
                     in1=bmT[:].unsqueeze(2).to_broadcast([128, 24, 2]), op=ALU.add)
                P.op("dve", "tensor_scalar", [modT], [affA[l]], out=affA[l][:], in0=modT[:, 8:16, :], scalar1=1.0,
                     scalar2=None, op0=ALU.add)
                P.op("dve", "tensor_tensor", [affA[l], gT], [affA[l]], out=affA[l][:], in0=affA[l][:],
                     in1=gT[:].unsqueeze(2).to_broadcast([128, 8, 2]), op=ALU.mult)
                P.op("dve", "tensor_copy", [modT], [affB[l]], out=affB[l][:], in_=modT[:, 0:8, :])
                for s in range(len(gateR[l])):
                    for hf in range(2):
                        ps = pg[hf]
                        for k in range(8):
                            wg = wgate[(hf * 8 + k) % 3]
                            P.dma("sp" if k % 2 == 0 else "pool", wg[:],
                                  wm[k * 128:(k + 1) * 128, 2048 + hf * 512:2048 + (hf + 1) * 512], writes=[wg])
                            P.op("pe", "matmul", [wg, screp[s]], [ps], ps[:], screp[s][:, k, :], wg[:],
                                 start=(k == 0), stop=(k == 7))
                        P.op("dve", "tensor_tensor", [ps, bgrow], [gateR[l][s]],
                             out=gateR[l][s][:, hf * 512:(hf + 1) * 512], in0=ps[:],
                             in1=bgrow[:, hf * 512:(hf + 1) * 512], op=ALU.add)
            P.barrier()
        if stage <= 0:
            dbg = nc.dram_tensor("dbg_mod", [128, 2, 2, 8, 2], F32, kind="ExternalOutput").ap()
            dbg2 = nc.dram_tensor("dbg_gate", [128, 1024], F32, kind="ExternalOutput").ap()
            for l in range(2):
                P.dma("sp", dbg[:, l, 0], affA[l][:], reads=[affA[l]])
                P.dma("sp", dbg[:, l, 1], affB[l][:], reads=[affB[l]])
            P.dma("sp", dbg2[:, :], gateR[0][1][:], reads=[gateR[0][1]])
            P.barrier()
            return nc

        phase_A(nc, P, IN, SC, 0, affA[0], affB[0], ident_bf, ones_f)
        if stage <= 1:
            return nc
        if 2 not in skip:
            mla_groups = [(0, 256, [0, 1], 0)] + [(256 + 512 * g, 512, list(range(NT)), 0) for g in range(8)]
            attention(nc, P, SC, ones_f, 8, 96, 96.0 ** -0.5, lambda h: SC["QT"][h, :, :], lambda h: SC["KT"][h, :, :],
                      SC["V"], 0, mla_groups)
        if stage <= 2:
            return nc
        if 3 not in skip:
            mlstm_phase(nc, P, IN, SC, ident_bf, ident_f, ones_f)
        if stage <= 3:
            return nc
        combine_phase(nc, P, IN, SC, ident_bf, SC["HM"][0], SC["HM"][1], SC["MO"], "l0_hnorm", 512, GROUPS)
        if stage <= 4:
            return nc
        phase_C(nc, P, IN, SC, 0, gateR[0], out)
        if stage <= 5:
            return nc
        phase_A(nc, P, IN, SC, 1, affA[1], affB[1], ident_bf, ones_f)
        if stage <= 6:
            return nc
        if 7 not in skip:
            gla_phase(nc, P, IN, SC, ident_bf)
            combine_phase(nc, P, IN, SC, ident_bf, SC["HM"][0], SC["HM"][1], None, "l1_gnorm", 0, GROUPS[1:])
        if stage <= 7:
            return nc
        if 8 not in skip:
            na_groups = []
            for g in range(8):
                key, t_lo, nt_ = _na_cfg(g)
                loc = []
                for r in range(nt_):
                    u0, u1 = _na_range(g, t_lo + r)
                    loc.append((2 + t_lo + r, u0 * 64, (u1 + 1) * 64))
                na_groups.append((256 + 512 * g, 512, [0, 1] + loc, g))
            attention(nc, P, SC, ones_f, 8, 64, 64.0 ** -0.5, lambda h: SC["NQ"][h * 64:(h + 1) * 64, :],
                      lambda h: SC["NK"][h * 64:(h + 1) * 64, :], SC["V"], 512, na_groups, bias_fn=na_bias_fn(nc, P, IN, {}),
                      ident_bf=ident_bf, early_release=False, act_recip=True)
        if stage <= 8:
            return nc
        phase_C(nc, P, IN, SC, 1, gateR[1], out)
    return nc


def phase_A(nc, P, IN, SC, layer, affA, affB, ident_bf, ones_f):
    NW = 3808 if layer == 0 else 3616
    w_in_d = IN["l%d_w_in" % layer]
    with ExitStack() as es:
        A = Alloc(nc, es)
        w_in = A.sb([128, 8, NW], BF16, "w_in")
        if layer == 0:
            w_uq = A.sb([128, 3, 1536], BF16, "w_uq")
            w_ukv = A.sb([128, 2, 1024], BF16, "w_ukv")
        with ExitStack() as es2:
            A2 = Alloc(nc, es2)
            stg = [A2.sb([128, 8, 512], F32, "stg%d" % i) for i in range(2)]
            i = 0
            for c0 in range(0, NW, 512):
                cw = min(512, NW - c0)
                s = stg[i % 2]
                P.dma("sp" if i % 2 == 0 else "pool", s[:, :, 0:cw],
                      w_in_d[:, c0:c0 + cw].rearrange("(k p) n -> p k n", p=128), writes=[s])
                P.op("dve" if i % 2 == 0 else "act", "tensor_copy" if i % 2 == 0 else "copy", [s], [w_in],
                     out=w_in[:, :, c0:c0 + cw], in_=s[:, :, 0:cw])
                i += 1
            if layer == 0:
                s = stg[i % 2]
                for kk in range(3):
                    s = stg[i % 2]
                    P.dma("sp", s[:, 0:3, :], IN["l0_w_uq"][kk * 128:(kk + 1) * 128, :].rearrange("p (a n) -> p a n", a=3),
                          writes=[s])
                    P.op("dve", "tensor_copy", [s], [w_uq], out=w_uq[:, kk, :].rearrange("p (a n) -> p a n", a=3),
                         in_=s[:, 0:3, :])
                    i += 1
                s = stg[i % 2]
                for kk in range(2):
                    P.dma("sp", s[:, 2 * kk:2 * kk + 2, :],
                          IN["l0_w_ukv"][kk * 128:(kk + 1) * 128, :].rearrange("p (a n) -> p a n", a=2), writes=[s])
                P.op("dve", "tensor_copy", [s], [w_ukv], out=w_ukv[:].rearrange("p k (a n) -> p (k a) n", a=2),
                     in_=s[:, 0:4, :])
                i += 1
            P.barrier()
        if layer == 0:
            qnT = A.sb([128, 3], F32, "qnT")
            kvnT = A.sb([128, 2], F32, "kvnT")
            P.dma("sp", qnT[:], IN["l0_qnT"][:, :], writes=[qnT])
            P.dma("sp", kvnT[:], IN["l0_kvnT"][:, :], writes=[kvnT])
            cqT = A.sb([128, 3, 512], F32, "cqT")
            ckvT = A.sb([128, 2, 512], F32, "ckvT")
            sq = A.sb([128, 3, 512], F32, "sq")
            rstd = A.sb([128, 512], F32, "rstd")
            cqn = A.sb([128, 3, 512], BF16, "cqn")
            ckvn = A.sb([128, 2, 512], BF16, "ckvn")
            rC = A.sb([128, 512], F32, "rC")
            rS = A.sb([128, 512], F32, "rS")
            rt1 = A.sb([128, 512], F32, "rt1")
            rt2 = A.sb([128, 512], F32, "rt2")
            qo = [A.sb([128, 512], BF16, "qo%d" % i) for i in range(2)]
            kro = A.sb([32, 512], BF16, "kro")
        else:
            gaT = [A.sb([16, 512], F32, "gaT%d" % d) for d in range(2)]
            wg = A.sb([16, 2, 256], F32, "wg")
            P.dma("sp", wg[:], IN["l1_w_gate"].rearrange("d r k -> r d k"), writes=[wg])
            bgT = A.sb([128, 2, 2], F32, "bgT")
            nbg = A.sb([128, 2, 2], F32, "nbg")
            P.dma("sp", bgT[:], IN["l1_bgT"][:, :, :], writes=[bgT])
            P.op("dve", "tensor_scalar", [bgT], [nbg], out=nbg[:], in0=bgT[:], scalar1=-1.0, scalar2=None, op0=ALU.mult)
            one1 = A.sb([128, 1], F32, "one1a")
            P.op("pool", "memset", [], [one1], one1[:], 1.0)
            lge = A.sb([128, 512], F32, "lge")
            lgo = [A.sb([128, 512], F32, "lgo%d" % i) for i in range(2)]
        hb = [A.sb([128, 1024], F32, "hb%d" % i) for i in range(3)]
        junk = A.sb([128, 1024], F32, "junk")
        st = [A.sb([128, 4], F32, "st%d" % i) for i in range(2)]
        xn2 = [[A.sb([128, 1024], BF16, "xn%d_%d" % (s_, i)) for i in range(4)] for s_ in range(2)]
        epsT = A.sb([128, 1], F32, "epsT")
        P.op("pool", "memset", [], [epsT], epsT[:], EPS)
        uT = [A.sb([128, 8, 512], BF16, "uT%d" % i) for i in range(2)]
        fo_bf = [A.sb([128, 512], BF16, "fobf%d" % i) for i in range(4)]
        fo_f = [A.sb([128, 512], F32, "fof%d" % i) for i in range(3)]
        tp = [A.ps([128, 512], BF16, "tp%d" % i) for i in range(2)]
        acc = [A.ps([128, 512], F32, "acc%d" % i) for i in range(5)]
        cnt = {"acc": 0, "fobf": 0, "fof": 0, "ev": 0, "q": 0, "hb": 0, "xn": 0, "tp": 0}

        def nxt(name, lst):
            r = lst[cnt[name] % len(lst)]
            cnt[name] += 1
            return r

        def evac_engine():
            cnt["ev"] += 1
            return "dve" if cnt["ev"] % 2 == 0 else "act"

        def copy_op(eng, src_t, src_ap, dst_t, dst_ap):
            if eng == "act":
                P.op("act", "copy", [src_t], [dst_t], out=dst_ap, in_=src_ap)
            else:
                P.op(eng, "tensor_copy", [src_t], [dst_t], out=dst_ap, in_=src_ap)

        def stq():
            cnt["q"] += 1
            return "pool" if cnt["q"] % 2 == 0 else "sp"

        def norm_part(gi):
            t0, n = GROUPS[gi]
            ntl = n // 128
            sta = st[gi % 2]
            xn = xn2[gi % 2]
            for ti in range(ntl):
                h = nxt("hb", hb)
                tok = t0 + ti * 128
                if layer == 0:
                    src = IN["ctx"][tok:tok + 128, :] if gi == 0 else IN["x"][tok - TC:tok - TC + 128, :]
                else:
                    src = SC["H1"][tok:tok + 128, :]
                P.dma("sp", h[:], src, writes=[h])
                P.op("act", "activation", [h], [junk, sta], out=junk[:], in_=h[:], func=AF.Square,
                     accum_out=sta[:, ti:ti + 1])
                P.op("act", "activation", [sta, epsT], [sta], out=sta[:, ti:ti + 1], in_=sta[:, ti:ti + 1], func=AF.Sqrt,
                     scale=1.0 / D, bias=epsT[:, 0:1])
                P.op("dve", "reciprocal", [sta], [sta], out=sta[:, ti:ti + 1], in_=sta[:, ti:ti + 1])
                x_ = xn[ti]
                P.op("dve", "tensor_scalar", [h, sta], [x_], out=x_[:], in0=h[:], scalar1=sta[:, ti:ti + 1],
                     scalar2=None, op0=ALU.mult)

        def transpose_part(gi):
            t0, n = GROUPS[gi]
            ntl = n // 128
            s = 1 if gi == 0 else 0
            u = uT[gi % 2]
            xn = xn2[gi % 2]
            for j in range(8):
                tpp = nxt("tp", tp)
                for ti in range(ntl):
                    P.op("pe", "transpose", [xn[ti], ident_bf], [tpp], tpp[:, ti * 128:(ti + 1) * 128],
                         xn[ti][:, j * 128:(j + 1) * 128], ident_bf[:])
                P.op("dve", "tensor_scalar", [tpp, affA, affB], [u], out=u[:, j, 0:n],
                     in0=tpp[:, 0:n], scalar1=affA[:, j, s:s + 1], scalar2=affB[:, j, s:s + 1], op0=ALU.mult,
                     op1=ALU.add)


        def proj_part(gi):
            t0, n = GROUPS[gi]
            ntl = n // 128
            u = uT[gi % 2]

            def fm_proj(c0, ncol):
                ps = nxt("acc", acc)
                for k in range(8):
                    P.op("pe", "matmul", [w_in, u], [ps], ps[0:ncol, 0:n], w_in[:, k, c0:c0 + ncol], u[:, k, 0:n],
                         start=(k == 0), stop=(k == 7))
                return ps

            def store_fm(ps, ncol, dst, dt, func=None, eng=None):
                o = nxt("fobf", fo_bf) if dt == BF16 else nxt("fof", fo_f)
                if func is not None:
                    P.op("act", "activation", [ps], [o], out=o[0:ncol, 0:n], in_=ps[0:ncol, 0:n], func=func)
                else:
                    copy_op(eng or evac_engine(), ps, ps[0:ncol, 0:n], o, o[0:ncol, 0:n])
                P.dma(stq(), dst, o[0:ncol, 0:n], reads=[o])

            tsl = slice(t0, t0 + n)
            if layer == 0:
                for j in range(3):
                    ps = fm_proj(j * 128, 128)
                    copy_op(evac_engine(), ps, ps[:, 0:n], cqT, cqT[:, j, 0:n])
                for j in range(2):
                    ps = fm_proj(384 + j * 128, 128)
                    copy_op(evac_engine(), ps, ps[:, 0:n], ckvT, ckvT[:, j, 0:n])
                for (src_t, nk, nrm, dst_t, dim) in ((cqT, 3, qnT, cqn, 384.0), (ckvT, 2, kvnT, ckvn, 256.0)):
                    P.op("act", "activation", [src_t], [sq], out=sq[:, 0:nk, 0:n], in_=src_t[:, 0:nk, 0:n], func=AF.Square)
                    ps = nxt("acc", acc)
                    for k in range(nk):
                        P.op("pe", "matmul", [ones_f, sq], [ps], ps[:, 0:n], ones_f[:], sq[:, k, 0:n], start=(k == 0),
                             stop=(k == nk - 1))
                    P.op("act", "activation", [ps, epsT], [rstd], out=rstd[:, 0:n], in_=ps[:, 0:n], func=AF.Sqrt,
                         scale=1.0 / dim, bias=epsT[:, 0:1])
                    P.op("dve", "reciprocal", [rstd], [rstd], out=rstd[:, 0:n], in_=rstd[:, 0:n])
                    for k in range(nk):
                        P.op("dve", "scalar_tensor_tensor", [src_t, nrm, rstd], [dst_t], out=dst_t[:, k, 0:n],
                             in0=src_t[:, k, 0:n], scalar=nrm[:, k:k + 1], in1=rstd[:, 0:n], op0=ALU.mult, op1=ALU.mult)
                rot = gi > 0
                if rot:
                    P.dma("sp", rC[:, 0:n], IN["ropeC"][:, t0 - TC:t0 - TC + n], writes=[rC])
                    P.dma("sp", rS[:, 0:n], IN["ropeS"][:, t0 - TC:t0 - TC + n], writes=[rS])
                for hh in range(8):
                    ps = nxt("acc", acc)
                    for k in range(3):
                        P.op("pe", "matmul", [w_uq, cqn], [ps], ps[0:96, 0:n], w_uq[:, k, hh * 192:hh * 192 + 96],
                             cqn[:, k, 0:n], start=(k == 0), stop=(k == 2))
                    o = nxt("fobf", fo_bf)
                    if rot:
                        ps2 = nxt("acc", acc)
                        for k in range(3):
                            P.op("pe", "matmul", [w_uq, cqn], [ps2], ps2[0:96, 0:n],
                                 w_uq[:, k, hh * 192 + 96:hh * 192 + 192], cqn[:, k, 0:n], start=(k == 0), stop=(k == 2))
                        copy_op("act", ps, ps[0:64, 0:n], o, o[0:64, 0:n])
                        P.op("dve", "tensor_tensor", [ps, rC], [rt1], out=rt1[64:96, 0:n], in0=ps[64:96, 0:n],
                             in1=rC[64:96, 0:n], op=ALU.mult)
                        P.op("dve", "tensor_tensor", [ps2, rS], [rt2], out=rt2[64:96, 0:n], in0=ps2[64:96, 0:n],
                             in1=rS[64:96, 0:n], op=ALU.mult)
                        P.op("pool", "tensor_tensor", [rt1, rt2], [o], out=o[64:96, 0:n], in0=rt1[64:96, 0:n],
                             in1=rt2[64:96, 0:n], op=ALU.add)
                    else:
                        copy_op(evac_engine(), ps, ps[0:96, 0:n], o, o[0:96, 0:n])
                    P.dma(stq(), SC["QT"][hh, :, tsl], o[0:96, 0:n], reads=[o])
                for c in range(4):
                    ps = nxt("acc", acc)
                    for k in range(2):
                        P.op("pe", "matmul", [w_ukv, ckvn], [ps], ps[:, 0:n], w_ukv[:, k, c * 128:(c + 1) * 128],
                             ckvn[:, k, 0:n], start=(k == 0), stop=(k == 1))
                    o = nxt("fobf", fo_bf)
                    copy_op(evac_engine(), ps, ps[:, 0:n], o, o[:, 0:n])
                    for hh in range(2):
                        P.dma(stq(), SC["KT"][c * 2 + hh, 0:64, tsl], o[hh * 64:(hh + 1) * 64, 0:n], reads=[o])
                for ti in range(ntl):
                    ps = nxt("acc", acc)
                    for k in range(2):
                        P.op("pe", "matmul", [w_ukv, ckvn], [ps], ps[:, :], ckvn[:, k, ti * 128:(ti + 1) * 128],
                             w_ukv[:, k, 512:1024], start=(k == 0), stop=(k == 1))
                    o = nxt("fobf", fo_bf)
                    copy_op(evac_engine(), ps, ps[:, :], o, o[:, :])
                    P.dma(stq(), SC["V"][t0 + ti * 128:t0 + (ti + 1) * 128, :], o[:, :], reads=[o])
                ps = fm_proj(640, 32)
                if rot:
                    ps2 = fm_proj(3760, 32)
                    P.op("dve", "tensor_tensor", [ps, rC], [rt1], out=rt1[0:32, 0:n], in0=ps[0:32, 0:n], in1=rC[0:32, 0:n],
                         op=ALU.mult)
                    P.op("dve", "tensor_tensor", [ps2, rS], [rt2], out=rt2[0:32, 0:n], in0=ps2[0:32, 0:n],
                         in1=rS[0:32, 0:n], op=ALU.mult)
                    P.op("pool", "tensor_tensor", [rt1, rt2], [kro], out=kro[0:32, 0:n], in0=rt1[0:32, 0:n],
                         in1=rt2[0:32, 0:n], op=ALU.add)
                else:
                    copy_op("dve", ps, ps[0:32, 0:n], kro, kro[0:32, 0:n])
                for hh in range(8):
                    P.dma(stq(), SC["KT"][hh, 64:96, tsl], kro[0:32, 0:n], reads=[kro])
                yield
                for c in range(8):
                    ps = fm_proj(672 + c * 128, 128)
                    store_fm(ps, 128, SC["MQK"][c * 128:(c + 1) * 128, tsl], F32)
                ps = fm_proj(3792, 8)
                store_fm(ps, 8, SC["GI"][:, tsl], F32)
                ps = fm_proj(3800, 8)
                store_fm(ps, 8, SC["GF"][:, tsl], F32)
                for c in range(8):
                    ps = fm_proj(2736 + c * 128, 128)
                    store_fm(ps, 128, SC["SZT"][c * 128:(c + 1) * 128, tsl], BF16, func=AF.Silu)
                tm_specs = [(1696, SC["MV"], None), (2208, SC["MO"], AF.Sigmoid)]
            else:
                for c in range(4):
                    ps = fm_proj(c * 128, 128)
                    store_fm(ps, 128, SC["MQK"][c * 128:(c + 1) * 128, tsl], F32)
                yield
                for d in range(2):
                    ps = fm_proj(1024 + 16 * d, 16)
                    copy_op("dve", ps, ps[0:16, 0:n], gaT[d], gaT[d][0:16, 0:n])
                for d in range(2):
                    for c2 in range(2):
                        ps = nxt("acc", acc)
                        P.op("pe", "matmul", [wg, gaT[d]], [ps], ps[:, 0:n], wg[0:16, d, c2 * 128:(c2 + 1) * 128],
                             gaT[d][0:16, 0:n], start=True, stop=True)
                        P.op("act", "activation", [ps, nbg], [lge], out=lge[:, 0:n], in_=ps[:, 0:n], func=AF.Exp, scale=-1.0,
                             bias=nbg[:, d, c2:c2 + 1])
                        P.op("act", "activation", [lge, one1], [lge], out=lge[:, 0:n], in_=lge[:, 0:n], func=AF.Ln,
                             bias=one1[:, 0:1])
                        o = lgo[(d * 2 + c2) % 2]
                        P.op("dve", "tensor_scalar", [lge], [o], out=o[:, 0:n], in0=lge[:, 0:n], scalar1=-1.0 / 16.0,
                             scalar2=None, op0=ALU.mult)
                        P.dma(stq(), SC["LG"][d, c2 * 128:(c2 + 1) * 128, tsl], o[:, 0:n], reads=[o])
                for c in range(4):
                    ps = fm_proj(1056 + c * 128, 128)
                    store_fm(ps, 128, SC["NQ"][c * 128:(c + 1) * 128, tsl], BF16)
                for c in range(4):
                    ps = fm_proj(1568 + c * 128, 128)
                    store_fm(ps, 128, SC["NK"][c * 128:(c + 1) * 128, tsl], BF16)
                for c in range(8):
                    ps = fm_proj(2592 + c * 128, 128)
                    store_fm(ps, 128, SC["SZT"][c * 128:(c + 1) * 128, tsl], BF16, func=AF.Silu)
                tm_specs = [(512, SC["MV"], None), (2080, SC["V"], None)]
            for (c0, dst, func) in tm_specs:
                for ti in range(ntl):
                    ps = nxt("acc", acc)
                    for k in range(8):
                        P.op("pe", "matmul", [w_in, u], [ps], ps[:, :], u[:, k, ti * 128:(ti + 1) * 128],
                             w_in[:, k, c0:c0 + 512], start=(k == 0), stop=(k == 7))
                    o = nxt("fobf", fo_bf)
                    if func is not None:
                        P.op("act", "activation", [ps], [o], out=o[:, :], in_=ps[:, :], func=func)
                    else:
                        copy_op(evac_engine(), ps, ps[:, :], o, o[:, :])
                    P.dma(stq(), dst[t0 + ti * 128:t0 + (ti + 1) * 128, :], o[:, :], reads=[o])

        norm_part(0)
        transpose_part(0)
        for gi in range(len(GROUPS)):
            if gi + 1 < len(GROUPS):
                norm_part(gi + 1)
            gen = proj_part(gi)
            next(gen)
            if gi + 1 < len(GROUPS):
                transpose_part(gi + 1)
            for _ in gen:
                pass
        P.barrier()


def attention(nc, P, SC, ones_f, heads, dq, scale, load_q, load_k, Vd, cat_row0, groups, bias_fn=None, ident_bf=None,
              early_release=False, act_recip=False):
    LOOK = 4
    NS = 5
    EPI_DELAY = 8
    with ExitStack() as es:
        A = Alloc(nc, es)
        V = A.sb([128, NT, heads, 65], BF16, "Vall")
        P.op("pool", "memset", [], [V], V[:, :, :, 64:65], 1.0)
        for half in range(2):
            tl = slice(half * 17, (half + 1) * 17)
            for hh in range(heads):
                P.dma("sp" if hh % 2 == 0 else "pool", V[:, tl, hh, 0:64],
                      Vd[half * 17 * 128:(half + 1) * 17 * 128, hh * 64:(hh + 1) * 64].rearrange("(t p) d -> p t d", p=128),
                      writes=[V])
        kT = [A.sb([128, T], BF16, "kT%d" % i) for i in range(2)]
        qT = [A.sb([128, T], BF16, "qT%d" % i) for i in range(2)]
        pt = [A.sb([128, 512], BF16, "pt%d" % i) for i in range(NS)]
        sb_t = [A.sb([128, 512], F32, "sbt%d" % i) for i in range(3)] if bias_fn is not None else None
        rden = [A.sb([128, 512], F32, "rden%d" % i) for i in range(2)]
        ocp = [A.sb([128, 512], F32, "ocp%d" % i) for i in range(3)] if early_release else None
        rsc = A.sb([128, 512], F32, "rsc")
        bcs = [A.sb([128, 512], F32, "bcs%d" % i) for i in range(2)]
        szt = [A.sb([64, 512], BF16, "szt%d" % i) for i in range(3)]
        tmp = [A.sb([64, 512], F32, "atmp%d" % i) for i in range(2)]
        ao = [A.sb([64, 512], BF16, "ao%d" % i) for i in range(2)]
        Sps = [A.ps([128, 512], F32, "Sps%d" % i) for i in range(NS)]
        Ops = [A.ps([128, 512], F32, "Ops%d" % i) for i in range(2)]
        Bps = A.ps([128, 512], F32, "Bps")
        if dq < 128:
            for t_ in kT + qT:
                P.op("pool", "memset", [], [t_], t_[64:128, :], 0.0)
        P.dma("sp", kT[0][0:dq, :], load_k(0), writes=[kT[0]])
        P.dma("pool", qT[0][0:dq, :], load_q(0), writes=[qT[0]])
        gcount = 0
        it = 0
        pend = []
        for h in range(heads):
            k_ = kT[h % 2]
            q_ = qT[h % 2]
            if h + 1 < heads:
                P.dma("sp", kT[(h + 1) % 2][0:dq, :], load_k(h + 1), writes=[kT[(h + 1) % 2]])
                P.dma("pool", qT[(h + 1) % 2][0:dq, :], load_q(h + 1), writes=[qT[(h + 1) % 2]])
            bias_loader = None
            if bias_fn is not None:
                if h == 0:
                    bias_cur, ld0 = bias_fn(0, A)
                    for _ in ld0:
                        pass
                bias_tiles = bias_cur
                if h + 1 < heads:
                    bias_cur, bias_loader = bias_fn(h + 1, A if h == 0 else None)
            else:
                bias_tiles = None
            r0 = cat_row0 + h * 64
            items = []
            for (q0, n, tiles, gkey) in groups:
                gid = gcount
                gcount += 1
                for j, kt in enumerate(tiles):
                    if isinstance(kt, tuple):
                        kt, c0, c1 = kt
                    else:
                        c0, c1 = 0, n
                    items.append((gid, q0, n, gkey, j, kt, len(tiles), c0, c1))

            def flush(cond):
                for e_ in pend[:]:
                    if cond(e_[1][0]):
                        emit_epi(*e_[1])
                        pend.remove(e_)

            def emit_S(item, slot):
                gid, q0, n, gkey, j, kt, nt_, c0, c1 = item
                S = Sps[slot % NS]
                p_ = pt[slot % NS]
                if j == 0:
                    flush(lambda g2: g2 % 3 == gid % 3)
                    sz = szt[gid % 3]
                    P.dma("sp", sz[:, 0:n], SC["SZT"][r0:r0 + 64, q0:q0 + n], writes=[sz])
                P.op("pe", "matmul", [k_, q_], [S], S[:, c0:c1], k_[:, kt * 128:(kt + 1) * 128], q_[:, q0 + c0:q0 + c1],
                     start=True, stop=True)
                bt = bias_tiles.get((gkey, kt)) if bias_tiles is not None else None
                if bt is not None:
                    sb = sb_t[slot % 3]
                    P.op("dve", "scalar_tensor_tensor", [S, bt], [sb], out=sb[:, c0:c1], in0=S[:, c0:c1], scalar=scale,
                         in1=bt[:, c0:c1], op0=ALU.mult, op1=ALU.add)
                    P.op("act", "activation", [sb], [p_], out=p_[:, c0:c1], in_=sb[:, c0:c1], func=AF.Exp)
                else:
                    P.op("act", "activation", [S], [p_], out=p_[:, c0:c1], in_=S[:, c0:c1], func=AF.Exp, scale=scale)

            def emit_PV(item, slot):
                gid, q0, n, gkey, j, kt, nt_, c0, c1 = item
                O = Ops[gid % 2]
                p_ = pt[slot % NS]
                assert j > 0 or (c0 == 0 and c1 == n)
                if j == 0:
                    flush(lambda g2: g2 % 2 == gid % 2)
                P.op("pe", "matmul", [V, p_], [O], O[0:65, c0:c1], V[:, kt, h, :], p_[:, c0:c1], start=(j == 0),
                     stop=(j == nt_ - 1))
                if j == nt_ - 1:
                    rd = rden[gid % 2]
                    if early_release:
                        oc = ocp[gid % 3]
                        P.op("act", "copy", [O], [oc], out=oc[0:65, 0:n], in_=O[0:65, 0:n])
                        P.op("dve", "reciprocal", [oc], [rd], out=rd[64:65, 0:n], in_=oc[64:65, 0:n])
                    elif act_recip:
                        P.op("act", "activation", [O], [rsc], out=rsc[64:65, 0:n], in_=O[64:65, 0:n], func=AF.Ln)
                        P.op("act", "activation", [rsc], [rd], out=rd[64:65, 0:n], in_=rsc[64:65, 0:n], func=AF.Exp, scale=-1.0)
                    else:
                        P.op("dve", "reciprocal", [O], [rd], out=rd[64:65, 0:n], in_=O[64:65, 0:n])
                    pend.append([EPI_DELAY, (gid, q0, n, r0, h)])

            def emit_epi(gid, q0, n, r0, h):
                O = Ops[gid % 2]
                rd = rden[gid % 2]
                bc_ = bcs[gid % 2]
                tm_ = tmp[gid % 2]
                a_ = ao[gid % 2]
                sz = szt[gid % 3]
                P.op("pe", "matmul", [ones_f, rd], [Bps], Bps[0:64, 0:n], ones_f[64:65, 0:64], rd[64:65, 0:n],
                     start=True, stop=True)
                if early_release:
                    oc = ocp[gid % 3]
                    P.op("dve", "tensor_tensor", [oc, Bps], [tm_], out=tm_[:, 0:n], in0=oc[0:64, 0:n], in1=Bps[0:64, 0:n],
                         op=ALU.mult)
                else:
                    P.op("act", "copy", [Bps], [bc_], out=bc_[0:64, 0:n], in_=Bps[0:64, 0:n])
                    P.op("dve", "tensor_tensor", [O, bc_], [tm_], out=tm_[:, 0:n], in0=O[0:64, 0:n], in1=bc_[0:64, 0:n],
                         op=ALU.mult)
                P.op("pool", "tensor_tensor", [tm_, sz], [a_], out=a_[:, 0:n], in0=tm_[:, 0:n], in1=sz[:, 0:n], op=ALU.mult)
                P.dma("pool", SC["CATT"][r0:r0 + 64, q0:q0 + n], a_[:, 0:n], reads=[a_])

            nI = len(items)
            for idx in range(nI + LOOK):
                if idx < nI:
                    emit_S(items[idx], it + idx)
                for e_ in pend[:]:
                    e_[0] -= 1
                    if e_[0] <= 0:
                        emit_epi(*e_[1])
                        pend.remove(e_)
                if idx - LOOK >= 0:
                    emit_PV(items[idx - LOOK], it + idx - LOOK)
                if bias_loader is not None and idx % 3 == 2:
                    next(bias_loader, None)
            if bias_loader is not None:
                for _ in bias_loader:
                    pass
            it += nI
        for e_ in pend:
            emit_epi(*e_[1])
        P.barrier()


def mlstm_phase(nc, P, IN, SC, ident_bf, ident_f, ones_f):
    NB = NT
    NCH = T // 64
    with ExitStack() as es:
        A = Alloc(nc, es)
        esT = A.sb([128, NB, 64], F32, "esT")
        fT = A.sb([128, NB, 64], F32, "fT")
        decbc = A.sb([128, 8, NCH], F32, "decbc")
        mask = [A.sb([128, 64], F32, "mask%d" % d) for d in range(2)]
        for d in range(2):
            P.op("pool", "memset", [], [mask[d]], mask[d][:], 1.0)
            for half in range(2):
                pr = slice(half * 64, half * 64 + 64)
                P.op("pool", "affine_select", [mask[d]], [mask[d]], out=mask[d][pr, :], in_=mask[d][pr, :],
                     pattern=[[1 if d == 0 else -1, 64]], compare_op=ALU.is_ge, fill=0.0, base=0,
                     channel_multiplier=-1 if d == 0 else 1)
        with ExitStack() as es2:
            A2 = Alloc(nc, es2)
            X1 = A2.sb([64, T], F32, "X1")
            X2 = A2.sb([64, T], F32, "X2")
            X3 = A2.sb([64, T], F32, "X3")
            X4 = A2.sb([64, T], F32, "X4")
            gb = A2.sb([64, 2], F32, "gb")
            nbf = A2.sb([64, 1], F32, "nbf")
            one1 = A2.sb([64, 1], F32, "one1")
            dec = A2.sb([64, NCH], F32, "dec")
            aprev = A2.sb([64, NCH], F32, "aprev")
            sel = A2.sb([64, 128], F32, "sel")
            pst = [A2.ps([128, 8, 64], F32, "pst%d" % i) for i in range(2)]
            psd = A2.ps([128, NCH], F32, "psd")
            P.op("pool", "memset", [], [X1], X1[:], 0.0)
            P.op("pool", "memset", [], [X3], X3[:], 0.0)
            P.op("pool", "memset", [], [one1], one1[:], 1.0)
            P.dma("sp", gb[:], IN["l0_gb2"][:, :], writes=[gb])
            for d in range(2):
                P.dma("sp", X1[d * 32:d * 32 + 4, :], SC["GF"][d * 4:d * 4 + 4, :], writes=[X1])
                P.dma("pool", X3[d * 32:d * 32 + 4, :], SC["GI"][d * 4:d * 4 + 4, :], writes=[X3])
            P.op("dve", "tensor_scalar", [gb], [nbf], out=nbf[:], in0=gb[:, 1:2], scalar1=-1.0, scalar2=None, op0=ALU.mult)
            P.op("act", "activation", [X1, nbf], [X1], out=X1[:], in_=X1[:], func=AF.Exp, scale=-1.0, bias=nbf[:, 0:1])
            P.op("act", "activation", [X1, one1], [X1], out=X1[:], in_=X1[:], func=AF.Ln, bias=one1[:, 0:1])

            def seg_views(tile_, prng, d):
                if d == 0:
                    return [tile_[prng, 0:T]]
                return [tile_[prng, 0:TC][:, ::-1], tile_[prng, TC:T][:, ::-1]]

            def scan(dst, src, op0, d):
                prng = slice(d * 32, d * 32 + 32)
                dv = seg_views(dst, prng, d)
                sv = seg_views(src, prng, d)
                for i in range(len(dv)):
                    init = 0.0 if i == 0 else dst[prng, 0:1]
                    P.op("dve", "tensor_tensor_scan", [src, dst], [dst], out=dv[i], data0=sv[i], data1=sv[i],
                         initial=init, op0=op0, op1=ALU.bypass)

            for d in range(2):
                scan(X2, X1, ALU.add, d)
            P.op("dve", "scalar_tensor_tensor", [X3, gb, X2], [X3], out=X3[:], in0=X3[:], scalar=gb[:, 0:1], in1=X2[:],
                 op0=ALU.add, op1=ALU.add)
            for d in range(2):
                scan(X1, X3, ALU.max, d)
            for d in range(2):
                prng = slice(d * 32, d * 32 + 32)
                jj = 63 if d == 0 else 0
                P.op("dve", "tensor_copy", [X1], [X4], out=X4[prng, :].rearrange("p (c j) -> p c j", j=64),
                     in_=X1[prng, :].rearrange("p (c j) -> p c j", j=64)[:, :, jj:jj + 1].to_broadcast([32, NCH, 64]))
            aend = X4[:, :].rearrange("p (c j) -> p c j", j=64)[:, :, 0]
            P.op("pool", "memset", [], [aprev], aprev[:], 0.0)
            P.op("dve", "tensor_copy", [X4], [aprev], out=aprev[0:32, 1:NCH], in_=aend[0:32, 0:NCH - 1])
            P.op("dve", "tensor_copy", [X4], [aprev], out=aprev[32:64, 0:3], in_=aend[32:64, 1:4])
            P.op("dve", "tensor_copy", [X4], [aprev], out=aprev[32:64, 4:NCH - 1], in_=aend[32:64, 5:NCH])
            P.op("dve", "tensor_copy", [X4], [aprev], out=aprev[32:64, NCH - 1:NCH], in_=aend[32:64, 0:1])
            P.op("dve", "tensor_tensor", [aprev, X4], [dec], out=dec[:], in0=aprev[:], in1=aend, op=ALU.subtract)
            P.op("act", "activation", [dec], [dec], out=dec[:], in_=dec[:], func=AF.Exp)
            P.op("dve", "tensor_tensor", [X3, X4], [X3], out=X3[:], in0=X3[:], in1=X4[:], op=ALU.subtract)
            P.op("act", "activation", [X3], [X3], out=X3[:], in_=X3[:], func=AF.Exp)
            P.op("dve", "tensor_tensor", [X2, X4], [X2], out=X2[:], in0=X2[:], in1=X4[:], op=ALU.subtract)
            P.op("act", "activation", [X2], [X2], out=X2[:], in_=X2[:], func=AF.Exp)
            for (srcX, dstT) in ((X3, esT), (X2, fT)):
                for b0 in range(0, NB, 8):
                    nb = min(8, NB - b0)
                    ps = pst[(b0 // 8) % 2]
                    for bb in range(nb):
                        P.op("pe", "transpose", [srcX, ident_f], [ps], ps[:, bb, :], srcX[:, (b0 + bb) * 128:(b0 + bb + 1) * 128],
                             ident_f[0:64, 0:64])
                    P.op("act", "copy", [ps], [dstT], out=dstT[:, b0:b0 + nb, :], in_=ps[:, 0:nb, :])
            for idx in range(8):
                r = (idx // 4) * 32 + idx % 4
                P.op("dve", "tensor_copy", [ident_f], [sel], out=sel[:], in_=ident_f[0:64, r:r + 1].to_broadcast([64, 128]))
                P.op("pe", "matmul", [sel, dec], [psd], psd[:, :], sel[:, :], dec[:, :], start=True, stop=True)
                P.op("act", "copy", [psd], [decbc], out=decbc[:, idx, :], in_=psd[:, :])
            P.barrier()
        P.op("dve", "tensor_scalar", [esT], [esT], out=esT[:], in0=esT[:], scalar1=128.0 ** -0.5, scalar2=None, op0=ALU.mult)
        xraw = A.sb([128, T], F32, "xraw")
        cvw = A.sb([128, 8, 4], F32, "cvw")
        P.dma("sp", cvw[:], IN["l0_convT"][:, :, :], writes=[cvw])
        dg = [A.sb([128, 3, 128], F32, "dg%d" % i) for i in range(2)]
        qT = A.sb([128, T], BF16, "mqT")
        qd = [A.sb([128, T], BF16, "mqd%d" % d) for d in range(2)]
        kT = A.sb([128, T], BF16, "mkT")
        kTok = A.sb([128, NB, 128], BF16, "kTok")
        vtok = A.sb([128, NB, 128], BF16, "vtok")
        vpp = [A.sb([128, NB, 129], BF16, "vpp%d" % d) for d in range(2)]
        SmT = [A.sb([128, NB, 64], BF16, "SmT%d" % d) for d in range(2)]
        hbuf = [A.sb([128, NB, 129], F32, "hbuf%d" % d) for d in range(2)]
        Cst = [[A.sb([128, 129], F32, "C%d_%d" % (d, i)) for i in range(2)] for d in range(2)]
        Cb = [[A.sb([128, 129], BF16, "Cb%d_%d" % (d, i)) for i in range(2)] for d in range(2)]
        dn = [A.sb([128, NB], F32, "dn%d" % d) for d in range(2)]
        pcv = [A.ps([128, 512], F32, "pcv%d" % i) for i in range(2)]
        pU = [A.ps([128, 129], F32, "pU%d" % i) for i in range(2)]
        pN = [[A.ps([128, 129], F32, "pN%d_%d" % (d, i)) for i in range(2)] for d in range(2)]
        order = [list(range(NCH)), [3, 2, 1, 0] + list(range(NCH - 1, 3, -1))]
        pieces = [(0, TC)] + [(TC + 512 * i, TC + 512 * (i + 1)) for i in range(8)]
        pc = 0
        for h in range(4):
            for which in range(2):
                ch = which * 4 + h
                dg_ = dg[which]
                P.dma("sp" if which == 0 else "pool", xraw[:], SC["MQK"][ch * 128:(ch + 1) * 128, :], writes=[xraw])
                for j in range(3):
                    P.op("dve", "tensor_scalar", [ident_f, cvw], [dg_], out=dg_[:, j, :], in0=ident_f[:], scalar1=cvw[:, ch, j:j + 1],
                         scalar2=None, op0=ALU.mult)
                dst = qT if which == 0 else kT
                for (a, b) in pieces:
                    s0, s1 = (0, TC) if a < TC else (TC, T)
                    ps = pcv[pc % 2]
                    pc += 1
                    P.op("pe", "matmul", [dg_, xraw], [ps], ps[:, 0:b - a], dg_[:, 1, :], xraw[:, a:b], start=True, stop=False)
                    lo = max(a, s0 + 1)
                    P.op("pe", "matmul", [dg_, xraw], [ps], ps[:, lo - a:b - a], dg_[:, 0, :], xraw[:, lo - 1:b - 1], start=False,
                         stop=False)
                    hi = min(b, s1 - 1)
                    P.op("pe", "matmul", [dg_, xraw], [ps], ps[:, 0:hi - a], dg_[:, 2, :], xraw[:, a + 1:hi + 1], start=False,
                         stop=True)
                    P.op("act", "activation", [ps, cvw], [dst], out=dst[:, a:b], in_=ps[:, 0:b - a], func=AF.Silu,
                         bias=cvw[:, ch, 3:4])
            for b0 in range(0, NB, 4):
                nb = min(4, NB - b0)
                ps = pcv[pc % 2]
                pc += 1
                psb = ps[:, 0:256].bitcast(BF16)
                for bb in range(nb):
                    P.op("pe", "transpose", [kT, ident_bf], [ps], psb[:, bb * 128:(bb + 1) * 128],
                         kT[:, (b0 + bb) * 128:(b0 + bb + 1) * 128], ident_bf[:])
                P.op("act", "copy", [ps], [kTok], out=kTok[:, b0:b0 + nb, :],
                     in_=psb[:, 0:nb * 128].rearrange("p (b j) -> p b j", j=128))
            P.dma("sp", vtok[:], SC["MV"][:, h * 128:(h + 1) * 128].rearrange("(b p) j -> p b j", p=128), writes=[vtok])
            for d in range(2):
                col = d * 32 + h
                idx = d * 4 + h
                P.op("pool" if d == 0 else "dve", "tensor_tensor", [vtok, esT], [vpp[d]], out=vpp[d][:, :, 0:128], in0=vtok[:],
                     in1=esT[:, :, col:col + 1].to_broadcast([128, NB, 128]), op=ALU.mult)
                P.op("dve", "tensor_copy", [esT], [vpp[d]], out=vpp[d][:, :, 128:129], in_=esT[:, :, col:col + 1])
                P.op("pool" if d == 1 else "dve", "tensor_tensor", [qT, decbc], [qd[d]],
                     out=qd[d][:, :].rearrange("p (c j) -> p c j", j=64), in0=qT[:, :].rearrange("p (c j) -> p c j", j=64),
                     in1=decbc[:, idx, :].unsqueeze(2).to_broadcast([128, NCH, 64]), op=ALU.mult)
            for b0 in range(0, NB, 4):
                nb = min(4, NB - b0)
                ps = pcv[pc % 2]
                pc += 1
                psv = ps[:, 0:256].rearrange("p (b j) -> p b j", j=64)
                for bb in range(nb):
                    for half in range(2):
                        c = (b0 + bb) * 2 + half
                        pr = slice(half * 64, half * 64 + 64)
                        P.op("pe", "matmul", [kT, qT], [ps], psv[pr, bb, :], kT[:, c * 64:(c + 1) * 64], qT[:, c * 64:(c + 1) * 64],
                             start=True, stop=True)
                for d in range(2):
                    P.op("dve", "tensor_tensor", [ps, mask[d]], [SmT[d]], out=SmT[d][:, b0:b0 + nb, :], in0=psv[:, 0:nb, :],
                         in1=mask[d][:].unsqueeze(1).to_broadcast([128, nb, 64]), op=ALU.mult)
            for d in range(2):
                P.op("pool", "memset", [], [Cst[d][0]], Cst[d][0][:], 0.0)
                P.op("pool", "memset", [], [Cb[d][0]], Cb[d][0][:], 0.0)
            for i in range(NCH):
                for d in range(2):
                    c = order[d][i]
                    b, half = c // 2, c % 2
                    pr = slice(half * 64, half * 64 + 64)
                    idx = d * 4 + h
                    Cold, Cnew = Cst[d][i % 2], Cst[d][(i + 1) % 2]
                    cbo, cbn = Cb[d][i % 2], Cb[d][(i + 1) % 2]
                    U = pU[d]
                    N = pN[d][i % 2]
                    P.op("pe", "matmul", [kTok, vpp[d]], [U], U[:, :], kTok[pr, b, :], vpp[d][pr, b, :], start=True, stop=True)
                    P.op("pe", "matmul", [SmT[d], vpp[d]], [N], N[pr, :], SmT[d][pr, b, :], vpp[d][pr, b, :], start=True,
                         stop=False)
                    P.op("pe", "matmul", [qd[d], cbo], [N], N[pr, :], qd[d][:, c * 64:(c + 1) * 64], cbo[:], start=False, stop=True)
                    P.op("dve", "scalar_tensor_tensor", [Cold, decbc, U], [cbn], out=cbn[:], in0=Cold[:],
                         scalar=decbc[:, idx, c:c + 1], in1=U[:, :], op0=ALU.mult, op1=ALU.add)
                    P.op("dve", "scalar_tensor_tensor", [Cold, decbc, U], [Cnew], out=Cnew[:], in0=Cold[:],
                         scalar=decbc[:, idx, c:c + 1], in1=U[:, :], op0=ALU.mult, op1=ALU.add)
                    P.op("act", "copy", [N], [hbuf[d]], out=hbuf[d][pr, b, :], in_=N[pr, :])
            for d in range(2):
                col = d * 32 + h
                P.op("act", "activation", [hbuf[d]], [dn[d]], out=dn[d][:, :].unsqueeze(2), in_=hbuf[d][:, :, 128:129], func=AF.Abs)
                P.op("dve", "tensor_tensor", [dn[d], fT], [dn[d]], out=dn[d][:, :].unsqueeze(2), in0=dn[d][:, :].unsqueeze(2),
                     in1=fT[:, :, col:col + 1], op=ALU.max)
                P.op("dve", "reciprocal", [dn[d]], [dn[d]], out=dn[d][:], in_=dn[d][:])
                P.op("dve" if d == 0 else "pool", "tensor_tensor", [hbuf[d], dn[d]], [hbuf[d]], out=hbuf[d][:, :, 0:128],
                     in0=hbuf[d][:, :, 0:128], in1=dn[d][:, :].unsqueeze(2).to_broadcast([128, NB, 128]), op=ALU.mult)
                P.dma("sp" if d == 0 else "pool", SC["HM"][d, :, h * 128:(h + 1) * 128].rearrange("(b p) j -> p b j", p=128),
                      hbuf[d][:, :, 0:128], reads=[hbuf[d]])
        P.barrier()


def combine_phase(nc, P, IN, SC, ident_bf, src0, src1, mul, norm_name, cat_row0, groups):
    with ExitStack() as es:
        A = Alloc(nc, es)
        nrow = A.sb([128, 512], F32, "nrow")
        P.dma("sp", nrow[:], IN[norm_name][0:1, :].to_broadcast([128, 512]), writes=[nrow])
        epsT = A.sb([128, 1], F32, "epsTc")
        P.op("pool", "memset", [], [epsT], epsT[:], EPS)
        a_ = [A.sb([128, 512], F32, "cA%d" % i) for i in range(4)]
        b_ = [A.sb([128, 512], F32, "cB%d" % i) for i in range(4)]
        m_ = [A.sb([128, 512], BF16, "cM%d" % i) for i in range(4)]
        junk = A.sb([128, 128], F32, "cjunk")
        ss = [A.sb([128, 4], F32, "css%d" % i) for i in range(4)]
        hn = [A.sb([128, 512], F32, "chn%d" % i) for i in range(4)]
        hb = [A.sb([128, 512], BF16, "chb%d" % i) for i in range(4)]
        sz = [A.sb([128, 512], BF16, "csz%d" % i) for i in range(2)]
        oo = [A.sb([128, 512], BF16, "coo%d" % i) for i in range(2)]
        tp = [A.ps([128, 512], BF16, "ctp%d" % i) for i in range(8)]
        tiles = []
        for gi, (t0, n) in enumerate(groups):
            for ti in range(n // 128):
                tiles.append((gi, t0, n, ti))

        def stage1(it):
            gi, t0, n, ti = tiles[it]
            tok = t0 + ti * 128
            a, b, m = a_[it % 4], b_[it % 4], m_[it % 4]
            P.dma("sp", a[:], src0[tok:tok + 128, :], writes=[a])
            P.dma("pool", b[:], src1[tok:tok + 128, :], writes=[b])
            if mul is not None:
                P.dma("sp", m[:], mul[tok:tok + 128, :], writes=[m])
            P.op("dve", "tensor_tensor", [a, b], [a], out=a[:], in0=a[:], in1=b[:], op=ALU.add)
            if mul is not None:
                P.op("pool", "tensor_tensor", [a, m], [a], out=a[:], in0=a[:], in1=m[:], op=ALU.mult)

        def stage2(it):
            gi, t0, n, ti = tiles[it]
            a, s_, hb_ = a_[it % 4], ss[it % 4], hb[it % 4]
            for hh in range(4):
                P.op("act", "activation", [a], [junk, s_], out=junk[:], in_=a[:, hh * 128:(hh + 1) * 128], func=AF.Square,
                     accum_out=s_[:, hh:hh + 1])
            P.op("act", "activation", [s_, epsT], [s_], out=s_[:], in_=s_[:], func=AF.Sqrt, scale=1.0 / 128, bias=epsT[:, 0:1])
            P.op("dve", "reciprocal", [s_], [s_], out=s_[:], in_=s_[:])
            for hh in range(4):
                sl = slice(hh * 128, (hh + 1) * 128)
                P.op("dve", "scalar_tensor_tensor", [a, s_, nrow], [hb_], out=hb_[:, sl], in0=a[:, sl], scalar=s_[:, hh:hh + 1],
                     in1=nrow[:, sl], op0=ALU.mult, op1=ALU.mult)
            for j in range(4):
                tpj = tp[(gi % 2) * 4 + j]
                P.op("pe", "transpose", [hb_, ident_bf], [tpj], tpj[:, ti * 128:(ti + 1) * 128],
                     hb_[:, j * 128:(j + 1) * 128], ident_bf[:])
            if ti == n // 128 - 1:
                for j in range(4):
                    tpj = tp[(gi % 2) * 4 + j]
                    r0 = cat_row0 + j * 128
                    z_, o_ = sz[j % 2], oo[j % 2]
                    P.dma("sp", z_[:, 0:n], SC["SZT"][r0:r0 + 128, t0:t0 + n], writes=[z_])
                    P.op("dve", "tensor_tensor", [tpj, z_], [o_], out=o_[:, 0:n], in0=tpj[:, 0:n], in1=z_[:, 0:n], op=ALU.mult)
                    P.dma("pool", SC["CATT"][r0:r0 + 128, t0:t0 + n], o_[:, 0:n], reads=[o_])

        stage1(0)
        for it in range(len(tiles)):
            if it + 1 < len(tiles):
                stage1(it + 1)
            stage2(it)
        P.barrier()


def phase_C(nc, P, IN, SC, layer, gateR, out_ap):
    with ExitStack() as es:
        A = Alloc(nc, es)
        w = A.sb([128, 8, 1024], BF16, "w_out")
        with ExitStack() as es2:
            A2 = Alloc(nc, es2)
            stg = [A2.sb([128, 8, 512], F32, "wstg%d" % i) for i in range(2)]
            for i in range(2):
                P.dma("sp" if i == 0 else "pool", stg[i][:],
                      IN["l%d_w_out" % layer][:, i * 512:(i + 1) * 512].rearrange("(k p) n -> p k n", p=128), writes=[stg[i]])
                P.op("dve" if i == 0 else "act", "tensor_copy" if i == 0 else "copy", [stg[i]], [w], out=w[:, :, i * 512:(i + 1) * 512],
                     in_=stg[i][:])
            P.barrier()
        cat = [A.sb([128, 8, 512], BF16, "catT%d" % i) for i in range(3)]
        hold = [A.sb([128, 1024], F32, "hold%d" % i) for i in range(4)]
        tmp = [A.sb([128, 1024], F32, "ctmp%d" % i) for i in range(4)]
        hnew = [A.sb([128, 1024], F32, "hnew%d" % i) for i in range(4)]
        ps = [A.ps([128, 512], F32, "yps%d" % i) for i in range(4)]
        if layer == 1:
            frow = A.sb([128, 1024], F32, "frow")
            P.dma("sp", frow[:], IN["final_norm"][0:1, :].to_broadcast([128, 1024]), writes=[frow])
            epsT = A.sb([128, 1], F32, "epsTf")
            P.op("pool", "memset", [], [epsT], epsT[:], EPS)
            junk = A.sb([128, 1024], F32, "fjunk")
            st = [A.sb([128, 1], F32, "fst%d" % i) for i in range(4)]
            ob = [A.sb([128, 1024], F32, "fob%d" % i) for i in range(4)]
        it = 0
        groups = GROUPS if layer == 0 else GROUPS[1:]
        for gi, (t0, n) in enumerate(groups):
            c_ = cat[gi % 3]
            for k2 in range(2):
                P.dma("sp" if k2 == 0 else "pool", c_[:, k2 * 4:(k2 + 1) * 4, 0:n],
                      SC["CATT"][k2 * 512:(k2 + 1) * 512, t0:t0 + n].rearrange("(k p) t -> p k t", p=128), writes=[c_])
            g_ = gateR[1] if (layer == 0 and t0 == 0) else gateR[0]
            for ti in range(n // 128):
                tok = t0 + ti * 128
                ho, tm, hn_ = hold[it % 4], tmp[it % 4], hnew[it % 4]
                if layer == 0:
                    srcp = IN["ctx"][tok:tok + 128, :] if t0 == 0 else IN["x"][tok - TC:tok - TC + 128, :]
                else:
                    srcp = SC["H1"][tok:tok + 128, :]
                P.dma("sp", ho[:], srcp, writes=[ho])
                for half in range(2):
                    p_ = ps[(it * 2 + half) % 4]
                    for k in range(8):
                        P.op("pe", "matmul", [c_, w], [p_], p_[:, :], c_[:, k, ti * 128:(ti + 1) * 128],
                             w[:, k, half * 512:(half + 1) * 512], start=(k == 0), stop=(k == 7))
                    P.op("dve", "tensor_tensor", [p_, g_], [tm], out=tm[:, half * 512:(half + 1) * 512], in0=p_[:, :],
                         in1=g_[:, half * 512:(half + 1) * 512], op=ALU.mult)
                P.op("pool", "tensor_tensor", [tm, ho], [hn_], out=hn_[:], in0=tm[:], in1=ho[:], op=ALU.add)
                if layer == 0:
                    P.dma("pool", SC["H1"][tok:tok + 128, :], hn_[:], reads=[hn_])
                else:
                    s_, o_ = st[it % 4], ob[it % 4]
                    P.op("act", "activation", [hn_], [junk, s_], out=junk[:], in_=hn_[:], func=AF.Square, accum_out=s_[:, 0:1])
                    P.op("act", "activation", [s_, epsT], [s_], out=s_[:], in_=s_[:], func=AF.Sqrt, scale=1.0 / D, bias=epsT[:, 0:1])
                    P.op("dve", "reciprocal", [s_], [s_], out=s_[:], in_=s_[:])
                    P.op("act", "activation", [hn_, s_], [o_], out=o_[:], in_=hn_[:], func=AF.Copy, scale=s_[:, 0:1])
                    P.op("dve", "tensor_tensor", [o_, frow], [o_], out=o_[:], in0=o_[:], in1=frow[:], op=ALU.mult)
                    P.dma("pool", out_ap[tok - TC:tok - TC + 128, :], o_[:], reads=[o_])
                it += 1
        P.barrier()


def gla_phase(nc, P, IN, SC, ident_bf):
    NB = NT
    NCH = T // 64
    with ExitStack() as es:
        A = Alloc(nc, es)
        mask = [A.sb([128, 64], F32, "gmask%d" % d) for d in range(2)]
        for d in range(2):
            P.op("pool", "memset", [], [mask[d]], mask[d][:], 1.0)
            for half in range(2):
                pr = slice(half * 64, half * 64 + 64)
                P.op("pool", "affine_select", [mask[d]], [mask[d]], out=mask[d][pr, :], in_=mask[d][pr, :],
                     pattern=[[1 if d == 0 else -1, 64]], compare_op=ALU.is_ge, fill=0.0, base=0,
                     channel_multiplier=-1 if d == 0 else 1)
        rm = [A.sb([64, T], F32, "rm%d" % d) for d in range(2)]
        for d in range(2):
            P.op("pool", "memset", [], [rm[d]], rm[d][:], 1.0)
            j0 = 0 if d == 0 else 63
            P.op("pool", "memset", [rm[d]], [rm[d]], rm[d][:, :].rearrange("p (c j) -> p c j", j=64)[:, :, j0:j0 + 1], 0.0)
        qf = A.sb([64, T], F32, "gqf")
        kf = A.sb([64, T], F32, "gkf")
        lg = A.sb([64, T], F32, "glg")
        bc = A.sb([64, T], F32, "gbc")
        tmp = A.sb([64, T], F32, "gtmp")
        qg = [A.sb([64, T], BF16, "qg%d" % d) for d in range(2)]
        kg = [A.sb([64, T], BF16, "kg%d" % d) for d in range(2)]
        kbf = A.sb([64, T], BF16, "kbf")
        eb = [A.sb([64, NCH], F32, "eb%d" % d) for d in range(2)]
        kbTok = [A.sb([128, NB, 64], BF16, "kbTok%d" % d) for d in range(2)]
        SmT = [A.sb([128, NB, 64], BF16, "gSmT%d" % d) for d in range(2)]
        vtok = A.sb([128, NB, 128], BF16, "gvtok")
        obuf = [[A.sb([128, 128], F32, "gob%d_%d" % (d, i)) for i in range(3)] for d in range(2)]
        Sst = [[A.sb([64, 128], F32, "S%d_%d" % (d, i)) for i in range(2)] for d in range(2)]
        Sb = [[A.sb([64, 128], BF16, "Sb%d_%d" % (d, i)) for i in range(2)] for d in range(2)]
        ptr = [A.ps([128, 8, 64], BF16, "gptr%d" % i) for i in range(2)]
        pS = [A.ps([128, 8, 64], F32, "gpS%d" % i) for i in range(2)]
        pU = [A.ps([128, 128], F32, "gpU%d" % i) for i in range(2)]
        pN = [A.ps([128, 128], F32, "gpN%d" % i) for i in range(2)]
        order = [list(range(NCH)), [3, 2, 1, 0] + list(range(NCH - 1, 3, -1))]
        for h in range(4):
            P.dma("sp", qf[:], SC["MQK"][h * 64:(h + 1) * 64, :], writes=[qf])
            P.dma("pool", kf[:], SC["MQK"][256 + h * 64:256 + (h + 1) * 64, :], writes=[kf])
            P.dma("sp", vtok[:], SC["MV"][:, h * 128:(h + 1) * 128].rearrange("(b p) j -> p b j", p=128), writes=[vtok])
            for d in range(2):
                P.dma("pool", lg[:], SC["LG"][d, h * 64:(h + 1) * 64, :], writes=[lg])
                if d == 0:
                    P.op("dve", "tensor_tensor_scan", [rm[d], lg], [bc], out=bc[:, :], data0=rm[d][:, :], data1=lg[:, :],
                         initial=0.0, op0=ALU.mult, op1=ALU.add)
                else:
                    P.op("dve", "tensor_tensor_scan", [rm[d], lg], [bc], out=bc[:, ::-1], data0=rm[d][:, ::-1],
                         data1=lg[:, ::-1], initial=0.0, op0=ALU.mult, op1=ALU.add)
                jl = 63 if d == 0 else 0
                bl = bc[:, :].rearrange("p (c j) -> p c j", j=64)[:, :, jl:jl + 1]
                P.op("act", "activation", [bc], [eb[d]], out=eb[d][:, :].unsqueeze(2), in_=bl, func=AF.Exp)
                P.op("act", "activation", [bc], [tmp], out=tmp[:], in_=bc[:], func=AF.Exp)
                P.op("dve", "scalar_tensor_tensor", [qf, tmp], [qg[d]], out=qg[d][:], in0=qf[:], scalar=64.0 ** -0.5, in1=tmp[:],
                     op0=ALU.mult, op1=ALU.mult)
                P.op("act", "activation", [bc], [tmp], out=tmp[:], in_=bc[:], func=AF.Exp, scale=-1.0)
                P.op("dve", "tensor_tensor", [kf, tmp], [kg[d]], out=kg[d][:], in0=kf[:], in1=tmp[:], op=ALU.mult)
                P.op("dve", "tensor_tensor", [bc], [tmp], out=tmp[:, :].rearrange("p (c j) -> p c j", j=64),
                     in0=bl.to_broadcast([64, NCH, 64]), in1=bc[:, :].rearrange("p (c j) -> p c j", j=64), op=ALU.subtract)
                P.op("act", "activation", [tmp], [tmp], out=tmp[:], in_=tmp[:], func=AF.Exp)
                P.op("dve", "tensor_tensor", [kf, tmp], [kbf], out=kbf[:], in0=kf[:], in1=tmp[:], op=ALU.mult)
                for b0 in range(0, NB, 8):
                    nb = min(8, NB - b0)
                    ps = ptr[(b0 // 8) % 2]
                    for bb in range(nb):
                        P.op("pe", "transpose", [kbf, ident_bf], [ps], ps[:, bb, :], kbf[:, (b0 + bb) * 128:(b0 + bb + 1) * 128],
                             ident_bf[0:64, 0:64])
                    P.op("act", "copy", [ps], [kbTok[d]], out=kbTok[d][:, b0:b0 + nb, :], in_=ps[:, 0:nb, :])
                for b0 in range(0, NB, 8):
                    nb = min(8, NB - b0)
                    ps = pS[(b0 // 8) % 2]
                    for bb in range(nb):
                        for half in range(2):
                            c = (b0 + bb) * 2 + half
                            pr = slice(half * 64, half * 64 + 64)
                            P.op("pe", "matmul", [kg[d], qg[d]], [ps], ps[pr, bb, :], kg[d][:, c * 64:(c + 1) * 64],
                                 qg[d][:, c * 64:(c + 1) * 64], start=True, stop=True)
                    P.op("dve", "tensor_tensor", [ps, mask[d]], [SmT[d]], out=SmT[d][:, b0:b0 + nb, :], in0=ps[:, 0:nb, :],
                         in1=mask[d][:].unsqueeze(1).to_broadcast([128, nb, 64]), op=ALU.mult)
            for d in range(2):
                P.op("pool", "memset", [], [Sst[d][0]], Sst[d][0][:], 0.0)
                P.op("pool", "memset", [], [Sb[d][0]], Sb[d][0][:], 0.0)
            for i in range(NCH):
                for d in range(2):
                    c = order[d][i]
                    b, half = c // 2, c % 2
                    pr = slice(half * 64, half * 64 + 64)
                    Sold, Snew = Sst[d][i % 2], Sst[d][(i + 1) % 2]
                    sbo, sbn = Sb[d][i % 2], Sb[d][(i + 1) % 2]
                    U, N = pU[d], pN[d]
                    ob = obuf[d][(i // 2) % 3]
                    P.op("pe", "matmul", [SmT[d], vtok], [N], N[pr, :], SmT[d][pr, b, :], vtok[pr, b, :], start=True, stop=False)
                    P.op("pe", "matmul", [qg[d], sbo], [N], N[pr, :], qg[d][:, c * 64:(c + 1) * 64], sbo[:, :], start=False,
                         stop=True)
                    P.op("pe", "matmul", [kbTok[d], vtok], [U], U[0:64, :], kbTok[d][pr, b, :], vtok[pr, b, :], start=True,
                         stop=True)
                    P.op("dve", "scalar_tensor_tensor", [Sold, eb[d], U], [Snew], out=Snew[:], in0=Sold[:],
                         scalar=eb[d][:, c:c + 1], in1=U[0:64, :], op0=ALU.mult, op1=ALU.add)
                    P.op("pool", "tensor_copy", [Snew], [sbn], out=sbn[:], in_=Snew[:])
                    P.op("act", "copy", [N], [ob], out=ob[pr, :], in_=N[pr, :])
                    if i % 2 == 1:
                        P.dma("sp" if d == 0 else "pool", SC["HM"][d, b * 128:(b + 1) * 128, h * 128:(h + 1) * 128], ob[:],
                              reads=[ob])
        P.barrier()


def _na_rows_ok(qr, kr):
    lo = min(max(qr - 4, 0), 56)
    return lo <= kr < lo + 8


def _na_cfg(g):
    if g == 0:
        return "first", 0, 6
    if g == 7:
        return "last", 26, 6
    return "mid", 4 * g - 2, 8


def _na_range(g, ktl):
    js = [j for j in range(8) for i in range(2) if _na_rows_ok(8 * g + j, 2 * ktl + i)]
    return min(js), max(js)


def na_bias_fn(nc, P, IN, state):
    def fn(h, A):
        if A is not None:
            state.setdefault("sets", {})
            for key, ntile in (("first", 6), ("mid", 8), ("last", 6)):
                state["sets"][(key, h % 2)] = [A.sb([128, 512], F32, "nab_%s%d_%d" % (key, i, h % 2)) for i in range(ntile)]
        out = {}

        def loader():
            for key, g, t_lo in (("first", 0, 0), ("mid", 1, 2), ("last", 7, 26)):
                tiles = state["sets"][(key, h % 2)]
                for r, bt in enumerate(tiles):
                    ktl = t_lo + r
                    u0, u1 = _na_range(g, ktl)
                    P.op("pool", "memset", [], [bt], bt[:, u0 * 64:(u1 + 1) * 64], MASKV)
                    for i in range(2):
                        kr = 2 * ktl + i
                        js = [j for j in range(8) if _na_rows_ok(8 * g + j, kr)]
                        if not js:
                            continue
                        j0, j1 = js[0], js[-1]
                        assert js == list(range(j0, j1 + 1))
                        m0 = 7 - (kr - 8 * g - j0)
                        nj = j1 - j0 + 1
                        P.dma("sp" if i == 0 else "pool",
                              bt[i * 64:(i + 1) * 64, j0 * 64:(j1 + 1) * 64].rearrange("p (m q) -> p m q", q=64),
                              IN["na_bias"][h, m0:m0 + nj, :, :].rearrange("m k q -> k m q"), writes=[bt])
                    yield

        for g in range(8):
            key, t_lo, nt_ = _na_cfg(g)
            for r in range(nt_):
                out[(g, 2 + t_lo + r)] = state["sets"][(key, h % 2)][r]
        return out, loader()

    return fn


def _shapes(d):
    return {k: (v.shape, "bf16" if v.dtype == ml_dtypes.bfloat16 else "f32") for k, v in d.items()}


def run(inputs, stage=99, debug=(), cores=8, skip=()):
    inputs = {k: np.asarray(v) for k, v in inputs.items()}
    sh, per = prep_inputs(inputs)
    nc = build(_shapes(sh), _shapes(per[0]), stage=stage, debug=debug, skip=skip)
    in_maps = [dict(sh, **per[b]) for b in range(cores)]
    res = run_bass_kernel_spmd(nc, in_maps, core_ids=list(range(cores)))
    return res


def kernel(**inputs):
    res = run(inputs)
    return np.stack([np.asarray(r["out"], dtype=np.float32) for r in res.results], axis=0)
```

```python
import numpy as np
from contextlib import ExitStack
import ml_dtypes
import concourse.bass as bass
import concourse.mybir as mybir
from concourse.bass_utils import run_bass_kernel_spmd

F32 = mybir.dt.float32
BF16 = mybir.dt.bfloat16
AF = mybir.ActivationFunctionType
ALU = mybir.AluOpType
AX = mybir.AxisListType

D = 1024
TC = 256
TL = 4096
T = TC + TL
NT = T // 128
EPS = 1e-6
MASKV = -30000.0

GROUPS = [(0, 256)] + [(256 + 512 * i, 512) for i in range(8)]


class Dep:
    __slots__ = ("w", "r")

    def __init__(self):
        self.w = None
        self.r = {}


class Tile:
    def __init__(self, t):
        self.t = t
        self.d = Dep()

    def __getitem__(self, k):
        return self.t[k]


class DramDep:
    def __init__(self):
        self.d = Dep()


class Prog:
    def __init__(self, nc, es):
        self.nc = nc
        self.eng = {"pe": nc.tensor, "act": nc.scalar, "dve": nc.vector, "pool": nc.gpsimd, "sp": nc.sync}
        self.R = 12
        self.keys = [("pe", "c"), ("act", "c"), ("dve", "c"), ("pool", "c")]
        for q in ("sp", "pool"):
            self.keys += [(q, "d%d" % i) for i in range(self.R)]
        self.ndma = {"sp": 0, "pool": 0}
        self.sem = {k: es.enter_context(nc.semaphore("s_%s_%s" % k)) for k in self.keys}
        self.cnt = {k: 0 for k in self.keys}
        self.waited = {e: {} for e in self.eng}
        self.n = 0

    def _emit(self, eng, kind, fn, reads, writes):
        if kind == "d":
            kind = "d%d" % (self.ndma[eng] % self.R)
            self.ndma[eng] += 1
        key = (eng, kind)
        deps = {}
        if kind != "c" and self.cnt[key] > 0:
            deps[key] = self.cnt[key]

        def add(tok):
            if tok is None:
                return
            k, v = tok
            if deps.get(k, 0) < v:
                deps[k] = v

        for b in reads:
            add(b.d.w)
        for b in writes:
            add(b.d.w)
            for k, v in b.d.r.items():
                add((k, v))
        e = self.eng[eng]
        wd = self.waited[eng]
        for k, v in deps.items():
            if k == ("pe", "c") and eng == "pe":
                continue
            if wd.get(k, 0) >= v:
                continue
            e.wait_ge(self.sem[k], v)
            wd[k] = v
        inc = 16 if kind != "c" else 1
        self.cnt[key] += inc
        fn(e).then_inc(self.sem[key], inc)
        v = self.cnt[key]
        for b in reads:
            if b.d.r.get(key, 0) < v:
                b.d.r[key] = v
        for b in writes:
            b.d.w = (key, v)
            b.d.r = {}
        self.n += 1

    def op(self, eng, name, reads, writes, *a, **kw):
        self._emit(eng, "c", lambda e: getattr(e, name)(*a, **kw), reads, writes)

    def dma(self, q, out, in_, reads=(), writes=(), **kw):
        self._emit(q, "d", lambda e: e.dma_start(out=out, in_=in_, **kw), reads, writes)

    def barrier(self):
        for en, e in self.eng.items():
            wd = self.waited[en]
            for k in self.keys:
                v = self.cnt[k]
                if v > 0 and wd.get(k, 0) < v:
                    e.wait_ge(self.sem[k], v)
                    wd[k] = v


class Alloc:
    def __init__(self, nc, es):
        self.nc = nc
        self.es = es
        _CTR.setdefault(id(nc), 0)

    def _nm(self, name):
        _CTR[id(self.nc)] = _CTR.get(id(self.nc), 0) + 1
        return "%s_%d" % (name, _CTR[id(self.nc)])

    def sb(self, shape, dt, name=None):
        return Tile(self.es.enter_context(self.nc.sbuf_tensor(self._nm(name or "sb"), list(shape), dt)))

    def ps(self, shape, dt, name=None):
        return Tile(self.es.enter_context(self.nc.psum_tensor(self._nm(name or "ps"), list(shape), dt)))


_CTR = {}


def _fm(v, nchunk):
    return np.ascontiguousarray(v.reshape(nchunk, 128).T)


def _rope_perm():
    perm = np.zeros(32, np.int64)
    for i in range(32):
        r = i % 16
        perm[i] = i + 8 if r < 8 else i - 8
    return perm


def _rope_tables():
    t = np.arange(TL)
    inv = (1.0 / (10000.0 ** (np.arange(8, dtype=np.float32) / 8))).astype(np.float32)
    pos = [(t // 64).astype(np.float32), (t % 64).astype(np.float32)]
    C = np.zeros((32, TL), np.float32)
    S = np.zeros((32, TL), np.float32)
    for i in range(32):
        a = i // 16
        r = i % 16
        p = r % 8
        ang = (pos[a] * inv[p]).astype(np.float32)
        C[i] = np.cos(ang)
        S[i] = -np.sin(ang) if r < 8 else np.sin(ang)
    Cf = np.zeros((128, TL), np.float32)
    Sf = np.zeros((128, TL), np.float32)
    Cf[0:32] = C
    Cf[64:96] = C
    Sf[0:32] = S
    Sf[64:96] = S
    return Cf, Sf


def prep_inputs(inp):
    sh = {}
    sh["ident_bf"] = np.eye(128, dtype=np.float32).astype(ml_dtypes.bfloat16)
    sh["ident_f"] = np.eye(128, dtype=np.float32)
    perm = _rope_perm()
    w_in = inp["l0_w_in"]
    gi_cols = [2720 + d * 8 + h for d in range(2) for h in range(4)]
    gf_cols = [2720 + d * 8 + 4 + h for d in range(2) for h in range(4)]
    sh["l0_w_in"] = np.ascontiguousarray(
        np.concatenate([w_in, w_in[:, 640:672][:, perm], w_in[:, gi_cols], w_in[:, gf_cols]], axis=1))
    w_uq = inp["l0_mla_w_uq"].reshape(384, 8, 96)
    ext = np.concatenate([w_uq, w_uq[:, :, 0:64], w_uq[:, :, 64:96][:, :, perm]], axis=2)
    sh["l0_w_uq"] = np.ascontiguousarray(ext.reshape(384, 8 * 192))
    w_ukv = inp["l0_mla_w_ukv"].reshape(256, 8, 128)
    sh["l0_w_ukv"] = np.ascontiguousarray(
        np.concatenate([w_ukv[:, :, 0:64].reshape(256, 512), w_ukv[:, :, 64:128].reshape(256, 512)], axis=1))
    sh["l0_qnT"] = _fm(inp["l0_mla_q_norm"], 3)
    sh["l0_kvnT"] = _fm(inp["l0_mla_kv_norm"], 2)
    Cf, Sf = _rope_tables()
    sh["ropeC"] = Cf
    sh["ropeS"] = Sf
    cw = inp["l0_mlstm_conv_w"]
    sh["l0_convT"] = np.ascontiguousarray(
        np.concatenate([cw.reshape(3, 8, 128).transpose(2, 1, 0), inp["l0_mlstm_conv_b"].reshape(8, 128).T[:, :, None]],
                       axis=2))
    gb = np.zeros((16, 1), np.float32)
    for d in range(2):
        for h in range(4):
            gb[d * 8 + h, 0] = inp["l0_mlstm_b_i"][d, h]
            gb[d * 8 + 4 + h, 0] = inp["l0_mlstm_b_f"][d, h]
    sh["l0_gbias"] = gb
    gb2 = np.zeros((64, 2), np.float32)
    for d in range(2):
        for h in range(4):
            gb2[d * 32 + h, 0] = inp["l0_mlstm_b_i"][d, h]
            gb2[d * 32 + h, 1] = inp["l0_mlstm_b_f"][d, h]
    sh["l0_gb2"] = gb2
    sh["l0_hnorm"] = np.ascontiguousarray(inp["l0_mlstm_norm"].reshape(1, 512))
    sh["l0_w_out"] = inp["l0_w_out"]
    sh["l1_w_in"] = inp["l1_w_in"]
    sh["l1_w_gate"] = np.ascontiguousarray(inp["l1_gla_w_gate"])
    sh["l1_bgT"] = np.ascontiguousarray(inp["l1_gla_b_gate"].reshape(2, 2, 128).transpose(2, 0, 1))
    sh["l1_gnorm"] = np.ascontiguousarray(inp["l1_gla_norm"].reshape(1, 512))
    sh["l1_w_out"] = inp["l1_w_out"]
    sh["final_norm"] = np.ascontiguousarray(inp["final_norm"].reshape(1, 1024))
    rpb = inp["l1_na_rpb"]
    kc = np.arange(64)[:, None]
    qc = np.arange(64)[None, :]
    wc0 = np.clip(qc - 8, 0, 48)
    okc = (kc >= wc0) & (kc < wc0 + 16)
    dcol = np.clip(kc - qc + 15, 0, 30)
    Tb = np.full((8, 15, 64, 64), MASKV, np.float32)
    for m in range(15):
        dr = 7 - m
        blk = rpb[:, dr + 7][:, dcol]
        Tb[:, m] = np.where(okc[None], blk, np.float32(MASKV))
    sh["na_bias"] = Tb
    mods = [(inp["l0_norm"], inp["l0_w_mod"], inp["l0_b_mod"]), (inp["l1_norm"], inp["l1_w_mod"], inp["l1_b_mod"])]
    for l, (g_, wm_, bm_) in enumerate(mods):
        sh["l%d_w_mod" % l] = wm_
        sh["l%d_bmodT" % l] = _fm(bm_, 24)
        sh["l%d_bmod_gate" % l] = np.ascontiguousarray(bm_[2048:3072].reshape(1, 1024))
        sh["l%d_gT" % l] = _fm(g_, 8)
    per = []
    for b in range(8):
        d = {}
        d["x"] = inp["x"][b]
        d["ctx"] = inp["ctx"][b]
        cv = np.stack([inp["c"][b], inp["c_ctx"]], axis=1)
        d["cvec"] = np.ascontiguousarray(cv.reshape(8, 128, 2).transpose(1, 0, 2))
        per.append(d)
    return sh, per


def build(sh_shapes, per_shapes, stage=99, debug=(), skip=()):
    nc = bass.Bass("TRN2", target_bir_lowering=False)
    IN = {}
    for k, (shape, dt) in list(sh_shapes.items()) + list(per_shapes.items()):
        IN[k] = nc.dram_tensor(k, list(shape), BF16 if dt == "bf16" else F32, kind="ExternalInput").ap()
    out = nc.dram_tensor("out", [TL, D], F32, kind="ExternalOutput").ap()

    def scratch(name, shape, dt):
        kind = "ExternalOutput" if name in debug else "Internal"
        return nc.dram_tensor(name, list(shape), dt, kind=kind).ap()

    SC = {}
    SC["H1"] = scratch("H1", [T, D], F32)
    SC["SZT"] = scratch("SZT", [1024, T], BF16)
    SC["CATT"] = scratch("CATT", [1024, T], BF16)
    SC["QT"] = scratch("QT", [8, 96, T], BF16)
    SC["KT"] = scratch("KT", [8, 96, T], BF16)
    SC["V"] = scratch("V", [T, 512], BF16)
    SC["MQK"] = scratch("MQK", [1024, T], F32)
    SC["GI"] = scratch("GI", [8, T], F32)
    SC["GF"] = scratch("GF", [8, T], F32)
    SC["MV"] = scratch("MV", [T, 512], BF16)
    SC["MO"] = scratch("MO", [T, 512], BF16)
    SC["HM"] = scratch("HM", [2, T, 512], F32)
    SC["RD"] = scratch("RD", [16, 512], F32)
    SC["LG"] = scratch("LG", [2, 256, T], F32)
    SC["NQ"] = scratch("NQ", [512, T], BF16)
    SC["NK"] = scratch("NK", [512, T], BF16)

    with ExitStack() as es0:
        P = Prog(nc, es0)
        A0 = Alloc(nc, es0)
        ident_bf = A0.sb([128, 128], BF16, "identbf")
        ident_f = A0.sb([128, 128], F32, "identf")
        ones_f = A0.sb([128, 128], F32, "onesf")
        P.dma("sp", ident_bf[:], IN["ident_bf"][:, :], writes=[ident_bf])
        P.dma("sp", ident_f[:], IN["ident_f"][:, :], writes=[ident_f])
        P.op("pool", "memset", [], [ones_f], ones_f[:], 1.0)
        affA = [A0.sb([128, 8, 2], F32, "affA%d" % l) for l in range(2)]
        affB = [A0.sb([128, 8, 2], F32, "affB%d" % l) for l in range(2)]
        gateR = [[A0.sb([128, 1024], F32, "gateR%d_%d" % (l, s)) for s in range(2 if l == 0 else 1)] for l in range(2)]

        with ExitStack() as es:
            A = Alloc(nc, es)
            cv = A.sb([128, 8, 2], F32, "cv")
            sc = A.sb([128, 8, 2], F32, "sc")
            screp = [A.sb([128, 8, 128], F32, "screp%d" % s) for s in range(2)]
            P.dma("sp", cv[:], IN["cvec"][:, :, :], writes=[cv])
            P.op("act", "activation", [cv], [sc], out=sc[:], in_=cv[:], func=AF.Silu)
            for s in range(2):
                for k in range(8):
                    P.op("dve", "tensor_copy", [sc], [screp[s]], out=screp[s][:, k, :],
                         in_=sc[:, k, s:s + 1].to_broadcast([128, 128]))
            wpan = [A.sb([128, 8, 384], F32, "wpan%d" % i) for i in range(2)]
            wgate = [A.sb([128, 512], F32, "wgate%d" % i) for i in range(3)]
            pm = A.ps([128, 24, 2], F32, "pm")
            pg = [A.ps([128, 512], F32, "pg%d" % i) for i in range(2)]
            bmT = A.sb([128, 24], F32, "bmT")
            gT = A.sb([128, 8], F32, "gT")
            modT = A.sb([128, 24, 2], F32, "modT")
            bgrow = A.sb([128, 1024], F32, "bgrow")
            for l in range(2):
                wm = IN["l%d_w_mod" % l]
                P.dma("sp", bmT[:], IN["l%d_bmodT" % l][:, :], writes=[bmT])
                P.dma("sp", gT[:], IN["l%d_gT" % l][:, :], writes=[gT])
                P.dma("sp", bgrow[:], IN["l%d_bmod_gate" % l][0:1, :].to_broadcast([128, 1024]), writes=[bgrow])
                for pn in range(8):
                    wp = wpan[pn % 2]
                    P.dma("sp" if pn % 2 == 0 else "pool", wp[:],
                          wm[:, pn * 384:(pn + 1) * 384].rearrange("(k p) n -> p k n", p=128), writes=[wp])
                    for j in range(3):
                        n = pn * 3 + j
                        for k in range(8):
                            P.op("pe", "matmul", [wp, sc], [pm], pm[:, n, :], wp[:, k, j * 128:(j + 1) * 128],
                                 sc[:, k, :], start=(k == 0), stop=(k == 7))
                P.op("dve", "tensor_tensor", [pm, bmT], [modT], out=modT[:], in0=pm[:],
                     in1=bmT[:].unsqueeze(2).to_broadcast([128, 24, 2]), op=ALU.add)
                P.op("dve", "tensor_scalar", [modT], [affA[l]], out=affA[l][:], in0=modT[:, 8:16, :], scalar1=1.0,
                     scalar2=None, op0=ALU.add)
                P.op("dve", "tensor_tensor", [affA[l], gT], [affA[l]], out=affA[l][:], in0=affA[l][:],
                     in1=gT[:].unsqueeze(2).to_broadcast([128, 8, 2]), op=ALU.mult)
                P.op("dve", "tensor_copy", [modT], [affB[l]], out=affB[l][:], in_=modT[:, 0:8, :])
                for s in range(len(gateR[l])):
                    for hf in range(2):
                        ps = pg[hf]
                        for k in range(8):
                            wg = wgate[(hf * 8 + k) % 3]
                            P.dma("sp" if k % 2 == 0 else "pool", wg[:],
                                  wm[k * 128:(k + 1) * 128, 2048 + hf * 512:2048 + (hf + 1) * 512], writes=[wg])
                            P.op("pe", "matmul", [wg, screp[s]], [ps], ps[:], screp[s][:, k, :], wg[:],
                                 start=(k == 0), stop=(k == 7))
                        P.op("dve", "tensor_tensor", [ps, bgrow], [gateR[l][s]],
                             out=gateR[l][s][:, hf * 512:(hf + 1) * 512], in0=ps[:],
                             in1=bgrow[:, hf * 512:(hf + 1) * 512], op=ALU.add)
            P.barrier()
        if stage <= 0:
            dbg = nc.dram_tensor("dbg_mod", [128, 2, 2, 8, 2], F32, kind="ExternalOutput").ap()
            dbg2 = nc.dram_tensor("dbg_gate", [128, 1024], F32, kind="ExternalOutput").ap()
            for l in range(2):
                P.dma("sp", dbg[:, l, 0], affA[l][:], reads=[affA[l]])
                P.dma("sp", dbg[:, l, 1], affB[l][:], reads=[affB[l]])
            P.dma("sp", dbg2[:, :], gateR[0][1][:], reads=[gateR[0][1]])
            P.barrier()
            return nc

        phase_A(nc, P, IN, SC, 0, affA[0], affB[0], ident_bf, ones_f)
        if stage <= 1:
            return nc
        if 2 not in skip:
            mla_groups = [(0, 256, [0, 1], 0)] + [(256 + 512 * g, 512, list(range(NT)), 0) for g in range(8)]
            attention(nc, P, SC, ones_f, 8, 96, 96.0 ** -0.5, lambda h: SC["QT"][h, :, :], lambda h: SC["KT"][h, :, :],
                      SC["V"], 0, mla_groups)
        if stage <= 2:
            return nc
        if 3 not in skip:
            mlstm_phase(nc, P, IN, SC, ident_bf, ident_f, ones_f)
        if stage <= 3:
            return nc
        combine_phase(nc, P, IN, SC, ident_bf, SC["HM"][0], SC["HM"][1], SC["MO"], "l0_hnorm", 512, GROUPS)
        if stage <= 4:
            return nc
        phase_C(nc, P, IN, SC, 0, gateR[0], out)
        if stage <= 5:
            return nc
        phase_A(nc, P, IN, SC, 1, affA[1], affB[1], ident_bf, ones_f)
        if stage <= 6:
            return nc
        if 7 not in skip:
            gla_phase(nc, P, IN, SC, ident_bf)
            combine_phase(nc, P, IN, SC, ident_bf, SC["HM"][0], SC["HM"][1], None, "l1_gnorm", 0, GROUPS[1:])
        if stage <= 7:
            return nc
        if 8 not in skip:
            na_groups = []
            for g in range(8):
                key, t_lo, nt_ = _na_cfg(g)
                loc = []
                for r in range(nt_):
                    u0, u1 = _na_range(g, t_lo + r)
                    loc.append((2 + t_lo + r, u0 * 64, (u1 + 1) * 64))
                na_groups.append((256 + 512 * g, 512, [0, 1] + loc, g))
            attention(nc, P, SC, ones_f, 8, 64, 64.0 ** -0.5, lambda h: SC["NQ"][h * 64:(h + 1) * 64, :],
                      lambda h: SC["NK"][h * 64:(h + 1) * 64, :], SC["V"], 512, na_groups, bias_fn=na_bias_fn(nc, P, IN, {}),
                      ident_bf=ident_bf, early_release=True, act_recip=True)
        if stage <= 8:
            return nc
        phase_C(nc, P, IN, SC, 1, gateR[1], out)
    return nc


def phase_A(nc, P, IN, SC, layer, affA, affB, ident_bf, ones_f):
    NW = 3808 if layer == 0 else 3616
    w_in_d = IN["l%d_w_in" % layer]
    with ExitStack() as es:
        A = Alloc(nc, es)
        w_in = A.sb([128, 8, NW], BF16, "w_in")
        if layer == 0:
            w_uq = A.sb([128, 3, 1536], BF16, "w_uq")
            w_ukv = A.sb([128, 2, 1024], BF16, "w_ukv")
        with ExitStack() as es2:
            A2 = Alloc(nc, es2)
            stg = [A2.sb([128, 8, 512], F32, "stg%d" % i) for i in range(2)]
            i = 0
            for c0 in range(0, NW, 512):
                cw = min(512, NW - c0)
                s = stg[i % 2]
                P.dma("sp" if i % 2 == 0 else "pool", s[:, :, 0:cw],
                      w_in_d[:, c0:c0 + cw].rearrange("(k p) n -> p k n", p=128), writes=[s])
                P.op("dve" if i % 2 == 0 else "act", "tensor_copy" if i % 2 == 0 else "copy", [s], [w_in],
                     out=w_in[:, :, c0:c0 + cw], in_=s[:, :, 0:cw])
                i += 1
            if layer == 0:
                s = stg[i % 2]
                for kk in range(3):
                    s = stg[i % 2]
                    P.dma("sp", s[:, 0:3, :], IN["l0_w_uq"][kk * 128:(kk + 1) * 128, :].rearrange("p (a n) -> p a n", a=3),
                          writes=[s])
                    P.op("dve", "tensor_copy", [s], [w_uq], out=w_uq[:, kk, :].rearrange("p (a n) -> p a n", a=3),
                         in_=s[:, 0:3, :])
                    i += 1
                s = stg[i % 2]
                for kk in range(2):
                    P.dma("sp", s[:, 2 * kk:2 * kk + 2, :],
                          IN["l0_w_ukv"][kk * 128:(kk + 1) * 128, :].rearrange("p (a n) -> p a n", a=2), writes=[s])
                P.op("dve", "tensor_copy", [s], [w_ukv], out=w_ukv[:].rearrange("p k (a n) -> p (k a) n", a=2),
                     in_=s[:, 0:4, :])
                i += 1
            P.barrier()
        if layer == 0:
            qnT = A.sb([128, 3], F32, "qnT")
            kvnT = A.sb([128, 2], F32, "kvnT")
            P.dma("sp", qnT[:], IN["l0_qnT"][:, :], writes=[qnT])
            P.dma("sp", kvnT[:], IN["l0_kvnT"][:, :], writes=[kvnT])
            cqT = A.sb([128, 3, 512], F32, "cqT")
            ckvT = A.sb([128, 2, 512], F32, "ckvT")
            sq = A.sb([128, 3, 512], F32, "sq")
            rstd = A.sb([128, 512], F32, "rstd")
            cqn = A.sb([128, 3, 512], BF16, "cqn")
            ckvn = A.sb([128, 2, 512], BF16, "ckvn")
            rC = A.sb([128, 512], F32, "rC")
            rS = A.sb([128, 512], F32, "rS")
            rt1 = A.sb([128, 512], F32, "rt1")
            rt2 = A.sb([128, 512], F32, "rt2")
            qo = [A.sb([128, 512], BF16, "qo%d" % i) for i in range(2)]
            kro = A.sb([32, 512], BF16, "kro")
        else:
            gaT = [A.sb([16, 512], F32, "gaT%d" % d) for d in range(2)]
            wg = A.sb([16, 2, 256], F32, "wg")
            P.dma("sp", wg[:], IN["l1_w_gate"].rearrange("d r k -> r d k"), writes=[wg])
            bgT = A.sb([128, 2, 2], F32, "bgT")
            nbg = A.sb([128, 2, 2], F32, "nbg")
            P.dma("sp", bgT[:], IN["l1_bgT"][:, :, :], writes=[bgT])
            P.op("dve", "tensor_scalar", [bgT], [nbg], out=nbg[:], in0=bgT[:], scalar1=-1.0, scalar2=None, op0=ALU.mult)
            one1 = A.sb([128, 1], F32, "one1a")
            P.op("pool", "memset", [], [one1], one1[:], 1.0)
            lge = A.sb([128, 512], F32, "lge")
            lgo = [A.sb([128, 512], F32, "lgo%d" % i) for i in range(2)]
        hb = [A.sb([128, 1024], F32, "hb%d" % i) for i in range(3)]
        junk = A.sb([128, 1024], F32, "junk")
        st = [A.sb([128, 4], F32, "st%d" % i) for i in range(2)]
        xn2 = [[A.sb([128, 1024], BF16, "xn%d_%d" % (s_, i)) for i in range(4)] for s_ in range(2)]
        epsT = A.sb([128, 1], F32, "epsT")
        P.op("pool", "memset", [], [epsT], epsT[:], EPS)
        uT = [A.sb([128, 8, 512], BF16, "uT%d" % i) for i in range(2)]
        fo_bf = [A.sb([128, 512], BF16, "fobf%d" % i) for i in range(4)]
        fo_f = [A.sb([128, 512], F32, "fof%d" % i) for i in range(3)]
        tp = [A.ps([128, 512], BF16, "tp%d" % i) for i in range(2)]
        acc = [A.ps([128, 512], F32, "acc%d" % i) for i in range(5)]
        cnt = {"acc": 0, "fobf": 0, "fof": 0, "ev": 0, "q": 0, "hb": 0, "xn": 0, "tp": 0}

        def nxt(name, lst):
            r = lst[cnt[name] % len(lst)]
            cnt[name] += 1
            return r

        def evac_engine():
            cnt["ev"] += 1
            return "dve" if cnt["ev"] % 2 == 0 else "act"

        def copy_op(eng, src_t, src_ap, dst_t, dst_ap):
            if eng == "act":
                P.op("act", "copy", [src_t], [dst_t], out=dst_ap, in_=src_ap)
            else:
                P.op(eng, "tensor_copy", [src_t], [dst_t], out=dst_ap, in_=src_ap)

        def stq():
            cnt["q"] += 1
            return "pool" if cnt["q"] % 2 == 0 else "sp"

        def norm_part(gi):
            t0, n = GROUPS[gi]
            ntl = n // 128
            sta = st[gi % 2]
            xn = xn2[gi % 2]
            for ti in range(ntl):
                h = nxt("hb", hb)
                tok = t0 + ti * 128
                if layer == 0:
                    src = IN["ctx"][tok:tok + 128, :] if gi == 0 else IN["x"][tok - TC:tok - TC + 128, :]
                else:
                    src = SC["H1"][tok:tok + 128, :]
                P.dma("sp", h[:], src, writes=[h])
                P.op("act", "activation", [h], [junk, sta], out=junk[:], in_=h[:], func=AF.Square,
                     accum_out=sta[:, ti:ti + 1])
                P.op("act", "activation", [sta, epsT], [sta], out=sta[:, ti:ti + 1], in_=sta[:, ti:ti + 1], func=AF.Sqrt,
                     scale=1.0 / D, bias=epsT[:, 0:1])
                P.op("dve", "reciprocal", [sta], [sta], out=sta[:, ti:ti + 1], in_=sta[:, ti:ti + 1])
                x_ = xn[ti]
                P.op("dve", "tensor_scalar", [h, sta], [x_], out=x_[:], in0=h[:], scalar1=sta[:, ti:ti + 1],
                     scalar2=None, op0=ALU.mult)

        def transpose_part(gi):
            t0, n = GROUPS[gi]
            ntl = n // 128
            s = 1 if gi == 0 else 0
            u = uT[gi % 2]
            xn = xn2[gi % 2]
            for j in range(8):
                tpp = nxt("tp", tp)
                for ti in range(ntl):
                    P.op("pe", "transpose", [xn[ti], ident_bf], [tpp], tpp[:, ti * 128:(ti + 1) * 128],
                         xn[ti][:, j * 128:(j + 1) * 128], ident_bf[:])
                P.op("dve", "tensor_scalar", [tpp, affA, affB], [u], out=u[:, j, 0:n],
                     in0=tpp[:, 0:n], scalar1=affA[:, j, s:s + 1], scalar2=affB[:, j, s:s + 1], op0=ALU.mult,
                     op1=ALU.add)


        def proj_part(gi):
            t0, n = GROUPS[gi]
            ntl = n // 128
            u = uT[gi % 2]

            def fm_proj(c0, ncol):
                ps = nxt("acc", acc)
                for k in range(8):
                    P.op("pe", "matmul", [w_in, u], [ps], ps[0:ncol, 0:n], w_in[:, k, c0:c0 + ncol], u[:, k, 0:n],
                         start=(k == 0), stop=(k == 7))
                return ps

            def store_fm(ps, ncol, dst, dt, func=None, eng=None):
                o = nxt("fobf", fo_bf) if dt == BF16 else nxt("fof", fo_f)
                if func is not None:
                    P.op("act", "activation", [ps], [o], out=o[0:ncol, 0:n], in_=ps[0:ncol, 0:n], func=func)
                else:
                    copy_op(eng or evac_engine(), ps, ps[0:ncol, 0:n], o, o[0:ncol, 0:n])
                P.dma(stq(), dst, o[0:ncol, 0:n], reads=[o])

            tsl = slice(t0, t0 + n)
            if layer == 0:
                for j in range(3):
                    ps = fm_proj(j * 128, 128)
                    copy_op(evac_engine(), ps, ps[:, 0:n], cqT, cqT[:, j, 0:n])
                for j in range(2):
                    ps = fm_proj(384 + j * 128, 128)
                    copy_op(evac_engine(), ps, ps[:, 0:n], ckvT, ckvT[:, j, 0:n])
                for (src_t, nk, nrm, dst_t, dim) in ((cqT, 3, qnT, cqn, 384.0), (ckvT, 2, kvnT, ckvn, 256.0)):
                    P.op("act", "activation", [src_t], [sq], out=sq[:, 0:nk, 0:n], in_=src_t[:, 0:nk, 0:n], func=AF.Square)
                    ps = nxt("acc", acc)
                    for k in range(nk):
                        P.op("pe", "matmul", [ones_f, sq], [ps], ps[:, 0:n], ones_f[:], sq[:, k, 0:n], start=(k == 0),
                             stop=(k == nk - 1))
                    P.op("act", "activation", [ps, epsT], [rstd], out=rstd[:, 0:n], in_=ps[:, 0:n], func=AF.Sqrt,
                         scale=1.0 / dim, bias=epsT[:, 0:1])
                    P.op("dve", "reciprocal", [rstd], [rstd], out=rstd[:, 0:n], in_=rstd[:, 0:n])
                    for k in range(nk):
                        P.op("dve", "scalar_tensor_tensor", [src_t, nrm, rstd], [dst_t], out=dst_t[:, k, 0:n],
                             in0=src_t[:, k, 0:n], scalar=nrm[:, k:k + 1], in1=rstd[:, 0:n], op0=ALU.mult, op1=ALU.mult)
                rot = gi > 0
                if rot:
                    P.dma("sp", rC[:, 0:n], IN["ropeC"][:, t0 - TC:t0 - TC + n], writes=[rC])
                    P.dma("sp", rS[:, 0:n], IN["ropeS"][:, t0 - TC:t0 - TC + n], writes=[rS])
                for hh in range(8):
                    ps = nxt("acc", acc)
                    for k in range(3):
                        P.op("pe", "matmul", [w_uq, cqn], [ps], ps[0:96, 0:n], w_uq[:, k, hh * 192:hh * 192 + 96],
                             cqn[:, k, 0:n], start=(k == 0), stop=(k == 2))
                    o = nxt("fobf", fo_bf)
                    if rot:
                        ps2 = nxt("acc", acc)
                        for k in range(3):
                            P.op("pe", "matmul", [w_uq, cqn], [ps2], ps2[0:96, 0:n],
                                 w_uq[:, k, hh * 192 + 96:hh * 192 + 192], cqn[:, k, 0:n], start=(k == 0), stop=(k == 2))
                        copy_op("act", ps, ps[0:64, 0:n], o, o[0:64, 0:n])
                        P.op("dve", "tensor_tensor", [ps, rC], [rt1], out=rt1[64:96, 0:n], in0=ps[64:96, 0:n],
                             in1=rC[64:96, 0:n], op=ALU.mult)
                        P.op("dve", "tensor_tensor", [ps2, rS], [rt2], out=rt2[64:96, 0:n], in0=ps2[64:96, 0:n],
                             in1=rS[64:96, 0:n], op=ALU.mult)
                        P.op("pool", "tensor_tensor", [rt1, rt2], [o], out=o[64:96, 0:n], in0=rt1[64:96, 0:n],
                             in1=rt2[64:96, 0:n], op=ALU.add)
                    else:
                        copy_op(evac_engine(), ps, ps[0:96, 0:n], o, o[0:96, 0:n])
                    P.dma(stq(), SC["QT"][hh, :, tsl], o[0:96, 0:n], reads=[o])
                for c in range(4):
                    ps = nxt("acc", acc)
                    for k in range(2):
                        P.op("pe", "matmul", [w_ukv, ckvn], [ps], ps[:, 0:n], w_ukv[:, k, c * 128:(c + 1) * 128],
                             ckvn[:, k, 0:n], start=(k == 0), stop=(k == 1))
                    o = nxt("fobf", fo_bf)
                    copy_op(evac_engine(), ps, ps[:, 0:n], o, o[:, 0:n])
                    for hh in range(2):
                        P.dma(stq(), SC["KT"][c * 2 + hh, 0:64, tsl], o[hh * 64:(hh + 1) * 64, 0:n], reads=[o])
                for ti in range(ntl):
                    ps = nxt("acc", acc)
                    for k in range(2):
                        P.op("pe", "matmul", [w_ukv, ckvn], [ps], ps[:, :], ckvn[:, k, ti * 128:(ti + 1) * 128],
                             w_ukv[:, k, 512:1024], start=(k == 0), stop=(k == 1))
                    o = nxt("fobf", fo_bf)
                    copy_op(evac_engine(), ps, ps[:, :], o, o[:, :])
                    P.dma(stq(), SC["V"][t0 + ti * 128:t0 + (ti + 1) * 128, :], o[:, :], reads=[o])
                ps = fm_proj(640, 32)
                if rot:
                    ps2 = fm_proj(3760, 32)
                    P.op("dve", "tensor_tensor", [ps, rC], [rt1], out=rt1[0:32, 0:n], in0=ps[0:32, 0:n], in1=rC[0:32, 0:n],
                         op=ALU.mult)
                    P.op("dve", "tensor_tensor", [ps2, rS], [rt2], out=rt2[0:32, 0:n], in0=ps2[0:32, 0:n],
                         in1=rS[0:32, 0:n], op=ALU.mult)
                    P.op("pool", "tensor_tensor", [rt1, rt2], [kro], out=kro[0:32, 0:n], in0=rt1[0:32, 0:n],
                         in1=rt2[0:32, 0:n], op=ALU.add)
                else:
                    copy_op("dve", ps, ps[0:32, 0:n], kro, kro[0:32, 0:n])
                for hh in range(8):
                    P.dma(stq(), SC["KT"][hh, 64:96, tsl], kro[0:32, 0:n], reads=[kro])
                yield
                for c in range(8):
                    ps = fm_proj(672 + c * 128, 128)
                    store_fm(ps, 128, SC["MQK"][c * 128:(c + 1) * 128, tsl], F32)
                ps = fm_proj(3792, 8)
                store_fm(ps, 8, SC["GI"][:, tsl], F32)
                ps = fm_proj(3800, 8)
                store_fm(ps, 8, SC["GF"][:, tsl], F32)
                for c in range(8):
                    ps = fm_proj(2736 + c * 128, 128)
                    store_fm(ps, 128, SC["SZT"][c * 128:(c + 1) * 128, tsl], BF16, func=AF.Silu)
                tm_specs = [(1696, SC["MV"], None), (2208, SC["MO"], AF.Sigmoid)]
            else:
                for c in range(4):
                    ps = fm_proj(c * 128, 128)
                    store_fm(ps, 128, SC["MQK"][c * 128:(c + 1) * 128, tsl], F32)
                yield
                for d in range(2):
                    ps = fm_proj(1024 + 16 * d, 16)
                    copy_op("dve", ps, ps[0:16, 0:n], gaT[d], gaT[d][0:16, 0:n])
                for d in range(2):
                    for c2 in range(2):
                        ps = nxt("acc", acc)
                        P.op("pe", "matmul", [wg, gaT[d]], [ps], ps[:, 0:n], wg[0:16, d, c2 * 128:(c2 + 1) * 128],
                             gaT[d][0:16, 0:n], start=True, stop=True)
                        P.op("act", "activation", [ps, nbg], [lge], out=lge[:, 0:n], in_=ps[:, 0:n], func=AF.Exp, scale=-1.0,
                             bias=nbg[:, d, c2:c2 + 1])
                        P.op("act", "activation", [lge, one1], [lge], out=lge[:, 0:n], in_=lge[:, 0:n], func=AF.Ln,
                             bias=one1[:, 0:1])
                        o = lgo[(d * 2 + c2) % 2]
                        P.op("dve", "tensor_scalar", [lge], [o], out=o[:, 0:n], in0=lge[:, 0:n], scalar1=-1.0 / 16.0,
                             scalar2=None, op0=ALU.mult)
                        P.dma(stq(), SC["LG"][d, c2 * 128:(c2 + 1) * 128, tsl], o[:, 0:n], reads=[o])
                for c in range(4):
                    ps = fm_proj(1056 + c * 128, 128)
                    store_fm(ps, 128, SC["NQ"][c * 128:(c + 1) * 128, tsl], BF16)
                for c in range(4):
                    ps = fm_proj(1568 + c * 128, 128)
                    store_fm(ps, 128, SC["NK"][c * 128:(c + 1) * 128, tsl], BF16)
                for c in range(8):
                    ps = fm_proj(2592 + c * 128, 128)
                    store_fm(ps, 128, SC["SZT"][c * 128:(c + 1) * 128, tsl], BF16, func=AF.Silu)
                tm_specs = [(512, SC["MV"], None), (2080, SC["V"], None)]
            for (c0, dst, func) in tm_specs:
                for ti in range(ntl):
                    ps = nxt("acc", acc)
                    for k in range(8):
                        P.op("pe", "matmul", [w_in, u], [ps], ps[:, :], u[:, k, ti * 128:(ti + 1) * 128],
                             w_in[:, k, c0:c0 + 512], start=(k == 0), stop=(k == 7))
                    o = nxt("fobf", fo_bf)
                    if func is not None:
                        P.op("act", "activation", [ps], [o], out=o[:, :], in_=ps[:, :], func=func)
                    else:
                        copy_op(evac_engine(), ps, ps[:, :], o, o[:, :])
                    P.dma(stq(), dst[t0 + ti * 128:t0 + (ti + 1) * 128, :], o[:, :], reads=[o])

        norm_part(0)
        transpose_part(0)
        for gi in range(len(GROUPS)):
            if gi + 1 < len(GROUPS):
                norm_part(gi + 1)
            gen = proj_part(gi)
            next(gen)
            if gi + 1 < len(GROUPS):
                transpose_part(gi + 1)
            for _ in gen:
                pass
        P.barrier()


def attention(nc, P, SC, ones_f, heads, dq, scale, load_q, load_k, Vd, cat_row0, groups, bias_fn=None, ident_bf=None,
              early_release=False, act_recip=False):
    LOOK = 4
    NS = 5
    EPI_DELAY = 8
    with ExitStack() as es:
        A = Alloc(nc, es)
        V = A.sb([128, NT, heads, 65], BF16, "Vall")
        P.op("pool", "memset", [], [V], V[:, :, :, 64:65], 1.0)
        for half in range(2):
            tl = slice(half * 17, (half + 1) * 17)
            for hh in range(heads):
                P.dma("sp" if hh % 2 == 0 else "pool", V[:, tl, hh, 0:64],
                      Vd[half * 17 * 128:(half + 1) * 17 * 128, hh * 64:(hh + 1) * 64].rearrange("(t p) d -> p t d", p=128),
                      writes=[V])
        kT = [A.sb([128, T], BF16, "kT%d" % i) for i in range(2)]
        qT = [A.sb([128, T], BF16, "qT%d" % i) for i in range(2)]
        pt = [A.sb([128, 512], BF16, "pt%d" % i) for i in range(NS)]
        sb_t = [A.sb([128, 512], F32, "sbt%d" % i) for i in range(3)] if bias_fn is not None else None
        rden = [A.sb([128, 512], F32, "rden%d" % i) for i in range(2)]
        ocp = [A.sb([128, 512], F32, "ocp%d" % i) for i in range(3)] if early_release else None
        rsc = A.sb([128, 512], F32, "rsc")
        bcs = [A.sb([128, 512], F32, "bcs%d" % i) for i in range(2)]
        szt = [A.sb([64, 512], BF16, "szt%d" % i) for i in range(3)]
        tmp = [A.sb([64, 512], F32, "atmp%d" % i) for i in range(2)]
        ao = [A.sb([64, 512], BF16, "ao%d" % i) for i in range(2)]
        Sps = [A.ps([128, 512], F32, "Sps%d" % i) for i in range(NS)]
        Ops = [A.ps([128, 512], F32, "Ops%d" % i) for i in range(2)]
        Bps = A.ps([128, 512], F32, "Bps")
        if dq < 128:
            for t_ in kT + qT:
                P.op("pool", "memset", [], [t_], t_[64:128, :], 0.0)
        P.dma("sp", kT[0][0:dq, :], load_k(0), writes=[kT[0]])
        P.dma("pool", qT[0][0:dq, :], load_q(0), writes=[qT[0]])
        gcount = 0
        it = 0
        rd_dep = [DramDep() for _ in range(16)]
        pend = []
        for h in range(heads):
            k_ = kT[h % 2]
            q_ = qT[h % 2]
            if h + 1 < heads:
                P.dma("sp", kT[(h + 1) % 2][0:dq, :], load_k(h + 1), writes=[kT[(h + 1) % 2]])
                P.dma("pool", qT[(h + 1) % 2][0:dq, :], load_q(h + 1), writes=[qT[(h + 1) % 2]])
            bias_loader = None
            if bias_fn is not None:
                if h == 0:
                    bias_cur, ld0 = bias_fn(0, A)
                    for _ in ld0:
                        pass
                bias_tiles = bias_cur
                if h + 1 < heads:
                    bias_cur, bias_loader = bias_fn(h + 1, A if h == 0 else None)
            else:
                bias_tiles = None
            r0 = cat_row0 + h * 64
            items = []
            for (q0, n, tiles, gkey) in groups:
                gid = gcount
                gcount += 1
                for j, kt in enumerate(tiles):
                    if isinstance(kt, tuple):
                        kt, c0, c1 = kt
                    else:
                        c0, c1 = 0, n
                    items.append((gid, q0, n, gkey, j, kt, len(tiles), c0, c1))

            def flush(cond):
                for e_ in pend[:]:
                    if cond(e_[1][0]):
                        emit_epi(*e_[1])
                        pend.remove(e_)

            def emit_S(item, slot):
                gid, q0, n, gkey, j, kt, nt_, c0, c1 = item
                S = Sps[slot % NS]
                p_ = pt[slot % NS]
                if j == 0:
                    flush(lambda g2: g2 % 3 == gid % 3)
                    sz = szt[gid % 3]
                    P.dma("sp", sz[:, 0:n], SC["SZT"][r0:r0 + 64, q0:q0 + n], writes=[sz])
                P.op("pe", "matmul", [k_, q_], [S], S[:, c0:c1], k_[:, kt * 128:(kt + 1) * 128], q_[:, q0 + c0:q0 + c1],
                     start=True, stop=True)
                bt = bias_tiles.get((gkey, kt)) if bias_tiles is not None else None
                if bt is not None:
                    sb = sb_t[slot % 3]
                    P.op("dve", "scalar_tensor_tensor", [S, bt], [sb], out=sb[:, c0:c1], in0=S[:, c0:c1], scalar=scale,
                         in1=bt[:, c0:c1], op0=ALU.mult, op1=ALU.add)
                    P.op("act", "activation", [sb], [p_], out=p_[:, c0:c1], in_=sb[:, c0:c1], func=AF.Exp)
                else:
                    P.op("act", "activation", [S], [p_], out=p_[:, c0:c1], in_=S[:, c0:c1], func=AF.Exp, scale=scale)

            def emit_PV(item, slot):
                gid, q0, n, gkey, j, kt, nt_, c0, c1 = item
                O = Ops[gid % 2]
                p_ = pt[slot % NS]
                assert j > 0 or (c0 == 0 and c1 == n)
                if j == 0:
                    flush(lambda g2: g2 % 2 == gid % 2)
                P.op("pe", "matmul", [V, p_], [O], O[0:65, c0:c1], V[:, kt, h, :], p_[:, c0:c1], start=(j == 0),
                     stop=(j == nt_ - 1))
                if j == nt_ - 1:
                    rd = rden[gid % 2]
                    if early_release:
                        oc = ocp[gid % 3]
                        P.op("act", "copy", [O], [oc], out=oc[0:65, 0:n], in_=O[0:65, 0:n])
                        P.op("act", "activation", [oc], [rsc], out=rsc[64:65, 0:n], in_=oc[64:65, 0:n], func=AF.Ln)
                        P.op("act", "activation", [rsc], [rd], out=rd[64:65, 0:n], in_=rsc[64:65, 0:n], func=AF.Exp, scale=-1.0)
                    elif act_recip:
                        P.op("act", "activation", [O], [rsc], out=rsc[64:65, 0:n], in_=O[64:65, 0:n], func=AF.Ln)
                        P.op("act", "activation", [rsc], [rd], out=rd[64:65, 0:n], in_=rsc[64:65, 0:n], func=AF.Exp, scale=-1.0)
                    else:
                        P.op("dve", "reciprocal", [O], [rd], out=rd[64:65, 0:n], in_=O[64:65, 0:n])
                    pend.append([EPI_DELAY, (gid, q0, n, r0, h)])

            def emit_epi(gid, q0, n, r0, h):
                O = Ops[gid % 2]
                rd = rden[gid % 2]
                bc_ = bcs[gid % 2]
                tm_ = tmp[gid % 2]
                a_ = ao[gid % 2]
                sz = szt[gid % 3]
                P.op("pe", "matmul", [ones_f, rd], [Bps], Bps[0:64, 0:n], ones_f[64:65, 0:64], rd[64:65, 0:n],
                     start=True, stop=True)
                if early_release:
                    oc = ocp[gid % 3]
                    P.op("dve", "tensor_tensor", [oc, Bps], [tm_], out=tm_[:, 0:n], in0=oc[0:64, 0:n], in1=Bps[0:64, 0:n],
                         op=ALU.mult)
                else:
                    P.op("act", "copy", [Bps], [bc_], out=bc_[0:64, 0:n], in_=Bps[0:64, 0:n])
                    P.op("dve", "tensor_tensor", [O, bc_], [tm_], out=tm_[:, 0:n], in0=O[0:64, 0:n], in1=bc_[0:64, 0:n],
                         op=ALU.mult)
                P.op("pool", "tensor_tensor", [tm_, sz], [a_], out=a_[:, 0:n], in0=tm_[:, 0:n], in1=sz[:, 0:n], op=ALU.mult)
                P.dma("pool", SC["CATT"][r0:r0 + 64, q0:q0 + n], a_[:, 0:n], reads=[a_])

            nI = len(items)
            for idx in range(nI + LOOK):
                if idx < nI:
                    emit_S(items[idx], it + idx)
                for e_ in pend[:]:
                    e_[0] -= 1
                    if e_[0] <= 0:
                        emit_epi(*e_[1])
                        pend.remove(e_)
                if idx - LOOK >= 0:
                    emit_PV(items[idx - LOOK], it + idx - LOOK)
                if bias_loader is not None and idx % 3 == 2:
                    next(bias_loader, None)
            if bias_loader is not None:
                for _ in bias_loader:
                    pass
            it += nI
        for e_ in pend:
            emit_epi(*e_[1])
        P.barrier()


def mlstm_phase(nc, P, IN, SC, ident_bf, ident_f, ones_f):
    NB = NT
    NCH = T // 64
    with ExitStack() as es:
        A = Alloc(nc, es)
        esT = A.sb([128, NB, 64], F32, "esT")
        fT = A.sb([128, NB, 64], F32, "fT")
        decbc = A.sb([128, 8, NCH], F32, "decbc")
        mask = [A.sb([128, 64], F32, "mask%d" % d) for d in range(2)]
        for d in range(2):
            P.op("pool", "memset", [], [mask[d]], mask[d][:], 1.0)
            for half in range(2):
                pr = slice(half * 64, half * 64 + 64)
                P.op("pool", "affine_select", [mask[d]], [mask[d]], out=mask[d][pr, :], in_=mask[d][pr, :],
                     pattern=[[1 if d == 0 else -1, 64]], compare_op=ALU.is_ge, fill=0.0, base=0,
                     channel_multiplier=-1 if d == 0 else 1)
        with ExitStack() as es2:
            A2 = Alloc(nc, es2)
            X1 = A2.sb([64, T], F32, "X1")
            X2 = A2.sb([64, T], F32, "X2")
            X3 = A2.sb([64, T], F32, "X3")
            X4 = A2.sb([64, T], F32, "X4")
            gb = A2.sb([64, 2], F32, "gb")
            nbf = A2.sb([64, 1], F32, "nbf")
            one1 = A2.sb([64, 1], F32, "one1")
            dec = A2.sb([64, NCH], F32, "dec")
            aprev = A2.sb([64, NCH], F32, "aprev")
            sel = A2.sb([64, 128], F32, "sel")
            pst = [A2.ps([128, 8, 64], F32, "pst%d" % i) for i in range(2)]
            psd = A2.ps([128, NCH], F32, "psd")
            P.op("pool", "memset", [], [X1], X1[:], 0.0)
            P.op("pool", "memset", [], [X3], X3[:], 0.0)
            P.op("pool", "memset", [], [one1], one1[:], 1.0)
            P.dma("sp", gb[:], IN["l0_gb2"][:, :], writes=[gb])
            for d in range(2):
                P.dma("sp", X1[d * 32:d * 32 + 4, :], SC["GF"][d * 4:d * 4 + 4, :], writes=[X1])
                P.dma("pool", X3[d * 32:d * 32 + 4, :], SC["GI"][d * 4:d * 4 + 4, :], writes=[X3])
            P.op("dve", "tensor_scalar", [gb], [nbf], out=nbf[:], in0=gb[:, 1:2], scalar1=-1.0, scalar2=None, op0=ALU.mult)
            P.op("act", "activation", [X1, nbf], [X1], out=X1[:], in_=X1[:], func=AF.Exp, scale=-1.0, bias=nbf[:, 0:1])
            P.op("act", "activation", [X1, one1], [X1], out=X1[:], in_=X1[:], func=AF.Ln, bias=one1[:, 0:1])

            def seg_views(tile_, prng, d):
                if d == 0:
                    return [tile_[prng, 0:T]]
                return [tile_[prng, 0:TC][:, ::-1], tile_[prng, TC:T][:, ::-1]]

            def scan(dst, src, op0, d):
                prng = slice(d * 32, d * 32 + 32)
                dv = seg_views(dst, prng, d)
                sv = seg_views(src, prng, d)
                for i in range(len(dv)):
                    init = 0.0 if i == 0 else dst[prng, 0:1]
                    P.op("dve", "tensor_tensor_scan", [src, dst], [dst], out=dv[i], data0=sv[i], data1=sv[i],
                         initial=init, op0=op0, op1=ALU.bypass)

            for d in range(2):
                scan(X2, X1, ALU.add, d)
            P.op("dve", "scalar_tensor_tensor", [X3, gb, X2], [X3], out=X3[:], in0=X3[:], scalar=gb[:, 0:1], in1=X2[:],
                 op0=ALU.add, op1=ALU.add)
            for d in range(2):
                scan(X1, X3, ALU.max, d)
            for d in range(2):
                prng = slice(d * 32, d * 32 + 32)
                jj = 63 if d == 0 else 0
                P.op("dve", "tensor_copy", [X1], [X4], out=X4[prng, :].rearrange("p (c j) -> p c j", j=64),
                     in_=X1[prng, :].rearrange("p (c j) -> p c j", j=64)[:, :, jj:jj + 1].to_broadcast([32, NCH, 64]))
            aend = X4[:, :].rearrange("p (c j) -> p c j", j=64)[:, :, 0]
            P.op("pool", "memset", [], [aprev], aprev[:], 0.0)
            P.op("dve", "tensor_copy", [X4], [aprev], out=aprev[0:32, 1:NCH], in_=aend[0:32, 0:NCH - 1])
            P.op("dve", "tensor_copy", [X4], [aprev], out=aprev[32:64, 0:3], in_=aend[32:64, 1:4])
            P.op("dve", "tensor_copy", [X4], [aprev], out=aprev[32:64, 4:NCH - 1], in_=aend[32:64, 5:NCH])
            P.op("dve", "tensor_copy", [X4], [aprev], out=aprev[32:64, NCH - 1:NCH], in_=aend[32:64, 0:1])
            P.op("dve", "tensor_tensor", [aprev, X4], [dec], out=dec[:], in0=aprev[:], in1=aend, op=ALU.subtract)
            P.op("act", "activation", [dec], [dec], out=dec[:], in_=dec[:], func=AF.Exp)
            P.op("dve", "tensor_tensor", [X3, X4], [X3], out=X3[:], in0=X3[:], in1=X4[:], op=ALU.subtract)
            P.op("act", "activation", [X3], [X3], out=X3[:], in_=X3[:], func=AF.Exp)
            P.op("dve", "tensor_tensor", [X2, X4], [X2], out=X2[:], in0=X2[:], in1=X4[:], op=ALU.subtract)
            P.op("act", "activation", [X2], [X2], out=X2[:], in_=X2[:], func=AF.Exp)
            for (srcX, dstT) in ((X3, esT), (X2, fT)):
                for b0 in range(0, NB, 8):
                    nb = min(8, NB - b0)
                    ps = pst[(b0 // 8) % 2]
                    for bb in range(nb):
                        P.op("pe", "transpose", [srcX, ident_f], [ps], ps[:, bb, :], srcX[:, (b0 + bb) * 128:(b0 + bb + 1) * 128],
                             ident_f[0:64, 0:64])
                    P.op("act", "copy", [ps], [dstT], out=dstT[:, b0:b0 + nb, :], in_=ps[:, 0:nb, :])
            for idx in range(8):
                r = (idx // 4) * 32 + idx % 4
                P.op("dve", "tensor_copy", [ident_f], [sel], out=sel[:], in_=ident_f[0:64, r:r + 1].to_broadcast([64, 128]))
                P.op("pe", "matmul", [sel, dec], [psd], psd[:, :], sel[:, :], dec[:, :], start=True, stop=True)
                P.op("act", "copy", [psd], [decbc], out=decbc[:, idx, :], in_=psd[:, :])
            P.barrier()
        P.op("dve", "tensor_scalar", [esT], [esT], out=esT[:], in0=esT[:], scalar1=128.0 ** -0.5, scalar2=None, op0=ALU.mult)
        xraw = A.sb([128, T], F32, "xraw")
        cvw = A.sb([128, 8, 4], F32, "cvw")
        P.dma("sp", cvw[:], IN["l0_convT"][:, :, :], writes=[cvw])
        dg = [A.sb([128, 3, 128], F32, "dg%d" % i) for i in range(2)]
        qT = A.sb([128, T], BF16, "mqT")
        qd = [A.sb([128, T], BF16, "mqd%d" % d) for d in range(2)]
        kT = A.sb([128, T], BF16, "mkT")
        kTok = A.sb([128, NB, 128], BF16, "kTok")
        vtok = A.sb([128, NB, 128], BF16, "vtok")
        vpp = [A.sb([128, NB, 129], BF16, "vpp%d" % d) for d in range(2)]
        SmT = [A.sb([128, NB, 64], BF16, "SmT%d" % d) for d in range(2)]
        hbuf = [A.sb([128, NB, 129], F32, "hbuf%d" % d) for d in range(2)]
        Cst = [[A.sb([128, 129], F32, "C%d_%d" % (d, i)) for i in range(2)] for d in range(2)]
        Cb = [[A.sb([128, 129], BF16, "Cb%d_%d" % (d, i)) for i in range(2)] for d in range(2)]
        dn = [A.sb([128, NB], F32, "dn%d" % d) for d in range(2)]
        pcv = [A.ps([128, 512], F32, "pcv%d" % i) for i in range(2)]
        pU = [A.ps([128, 129], F32, "pU%d" % i) for i in range(2)]
        pN = [[A.ps([128, 129], F32, "pN%d_%d" % (d, i)) for i in range(2)] for d in range(2)]
        order = [list(range(NCH)), [3, 2, 1, 0] + list(range(NCH - 1, 3, -1))]
        pieces = [(0, TC)] + [(TC + 512 * i, TC + 512 * (i + 1)) for i in range(8)]
        pc = 0
        for h in range(4):
            for which in range(2):
                ch = which * 4 + h
                dg_ = dg[which]
                P.dma("sp" if which == 0 else "pool", xraw[:], SC["MQK"][ch * 128:(ch + 1) * 128, :], writes=[xraw])
                for j in range(3):
                    P.op("dve", "tensor_scalar", [ident_f, cvw], [dg_], out=dg_[:, j, :], in0=ident_f[:], scalar1=cvw[:, ch, j:j + 1],
                         scalar2=None, op0=ALU.mult)
                dst = qT if which == 0 else kT
                for (a, b) in pieces:
                    s0, s1 = (0, TC) if a < TC else (TC, T)
                    ps = pcv[pc % 2]
                    pc += 1
                    P.op("pe", "matmul", [dg_, xraw], [ps], ps[:, 0:b - a], dg_[:, 1, :], xraw[:, a:b], start=True, stop=False)
                    lo = max(a, s0 + 1)
                    P.op("pe", "matmul", [dg_, xraw], [ps], ps[:, lo - a:b - a], dg_[:, 0, :], xraw[:, lo - 1:b - 1], start=False,
                         stop=False)
                    hi = min(b, s1 - 1)
                    P.op("pe", "matmul", [dg_, xraw], [ps], ps[:, 0:hi - a], dg_[:, 2, :], xraw[:, a + 1:hi + 1], start=False,
                         stop=True)
                    P.op("act", "activation", [ps, cvw], [dst], out=dst[:, a:b], in_=ps[:, 0:b - a], func=AF.Silu,
                         bias=cvw[:, ch, 3:4])
            for b0 in range(0, NB, 4):
                nb = min(4, NB - b0)
                ps = pcv[pc % 2]
                pc += 1
                psb = ps[:, 0:256].bitcast(BF16)
                for bb in range(nb):
                    P.op("pe", "transpose", [kT, ident_bf], [ps], psb[:, bb * 128:(bb + 1) * 128],
                         kT[:, (b0 + bb) * 128:(b0 + bb + 1) * 128], ident_bf[:])
                P.op("act", "copy", [ps], [kTok], out=kTok[:, b0:b0 + nb, :],
                     in_=psb[:, 0:nb * 128].rearrange("p (b j) -> p b j", j=128))
            P.dma("sp", vtok[:], SC["MV"][:, h * 128:(h + 1) * 128].rearrange("(b p) j -> p b j", p=128), writes=[vtok])
            for d in range(2):
                col = d * 32 + h
                idx = d * 4 + h
                P.op("pool" if d == 0 else "dve", "tensor_tensor", [vtok, esT], [vpp[d]], out=vpp[d][:, :, 0:128], in0=vtok[:],
                     in1=esT[:, :, col:col + 1].to_broadcast([128, NB, 128]), op=ALU.mult)
                P.op("dve", "tensor_copy", [esT], [vpp[d]], out=vpp[d][:, :, 128:129], in_=esT[:, :, col:col + 1])
                P.op("pool" if d == 1 else "dve", "tensor_tensor", [qT, decbc], [qd[d]],
                     out=qd[d][:, :].rearrange("p (c j) -> p c j", j=64), in0=qT[:, :].rearrange("p (c j) -> p c j", j=64),
                     in1=decbc[:, idx, :].unsqueeze(2).to_broadcast([128, NCH, 64]), op=ALU.mult)
            for b0 in range(0, NB, 4):
                nb = min(4, NB - b0)
                ps = pcv[pc % 2]
                pc += 1
                psv = ps[:, 0:256].rearrange("p (b j) -> p b j", j=64)
                for bb in range(nb):
                    for half in range(2):
                        c = (b0 + bb) * 2 + half
                        pr = slice(half * 64, half * 64 + 64)
                        P.op("pe", "matmul", [kT, qT], [ps], psv[pr, bb, :], kT[:, c * 64:(c + 1) * 64], qT[:, c * 64:(c + 1) * 64],
                             start=True, stop=True)
                for d in range(2):
                    P.op("dve", "tensor_tensor", [ps, mask[d]], [SmT[d]], out=SmT[d][:, b0:b0 + nb, :], in0=psv[:, 0:nb, :],
                         in1=mask[d][:].unsqueeze(1).to_broadcast([128, nb, 64]), op=ALU.mult)
            for d in range(2):
                P.op("pool", "memset", [], [Cst[d][0]], Cst[d][0][:], 0.0)
                P.op("pool", "memset", [], [Cb[d][0]], Cb[d][0][:], 0.0)
            for i in range(NCH):
                for d in range(2):
                    c = order[d][i]
                    b, half = c // 2, c % 2
                    pr = slice(half * 64, half * 64 + 64)
                    idx = d * 4 + h
                    Cold, Cnew = Cst[d][i % 2], Cst[d][(i + 1) % 2]
                    cbo, cbn = Cb[d][i % 2], Cb[d][(i + 1) % 2]
                    U = pU[d]
                    N = pN[d][i % 2]
                    P.op("pe", "matmul", [kTok, vpp[d]], [U], U[:, :], kTok[pr, b, :], vpp[d][pr, b, :], start=True, stop=True)
                    P.op("pe", "matmul", [SmT[d], vpp[d]], [N], N[pr, :], SmT[d][pr, b, :], vpp[d][pr, b, :], start=True,
                         stop=False)
                    P.op("pe", "matmul", [qd[d], cbo], [N], N[pr, :], qd[d][:, c * 64:(c + 1) * 64], cbo[:], start=False, stop=True)
                    P.op("dve", "scalar_tensor_tensor", [Cold, decbc, U], [cbn], out=cbn[:], in0=Cold[:],
                         scalar=decbc[:, idx, c:c + 1], in1=U[:, :], op0=ALU.mult, op1=ALU.add)
                    P.op("dve", "scalar_tensor_tensor", [Cold, decbc, U], [Cnew], out=Cnew[:], in0=Cold[:],
                         scalar=decbc[:, idx, c:c + 1], in1=U[:, :], op0=ALU.mult, op1=ALU.add)
                    P.op("act", "copy", [N], [hbuf[d]], out=hbuf[d][pr, b, :], in_=N[pr, :])
            for d in range(2):
                col = d * 32 + h
                P.op("act", "activation", [hbuf[d]], [dn[d]], out=dn[d][:, :].unsqueeze(2), in_=hbuf[d][:, :, 128:129], func=AF.Abs)
                P.op("dve", "tensor_tensor", [dn[d], fT], [dn[d]], out=dn[d][:, :].unsqueeze(2), in0=dn[d][:, :].unsqueeze(2),
                     in1=fT[:, :, col:col + 1], op=ALU.max)
                P.op("dve", "reciprocal", [dn[d]], [dn[d]], out=dn[d][:], in_=dn[d][:])
                P.op("dve" if d == 0 else "pool", "tensor_tensor", [hbuf[d], dn[d]], [hbuf[d]], out=hbuf[d][:, :, 0:128],
                     in0=hbuf[d][:, :, 0:128], in1=dn[d][:, :].unsqueeze(2).to_broadcast([128, NB, 128]), op=ALU.mult)
                P.dma("sp" if d == 0 else "pool", SC["HM"][d, :, h * 128:(h + 1) * 128].rearrange("(b p) j -> p b j", p=128),
                      hbuf[d][:, :, 0:128], reads=[hbuf[d]])
        P.barrier()


def combine_phase(nc, P, IN, SC, ident_bf, src0, src1, mul, norm_name, cat_row0, groups):
    with ExitStack() as es:
        A = Alloc(nc, es)
        nrow = A.sb([128, 512], F32, "nrow")
        P.dma("sp", nrow[:], IN[norm_name][0:1, :].to_broadcast([128, 512]), writes=[nrow])
        epsT = A.sb([128, 1], F32, "epsTc")
        P.op("pool", "memset", [], [epsT], epsT[:], EPS)
        a_ = [A.sb([128, 512], F32, "cA%d" % i) for i in range(4)]
        b_ = [A.sb([128, 512], F32, "cB%d" % i) for i in range(4)]
        m_ = [A.sb([128, 512], BF16, "cM%d" % i) for i in range(4)]
        junk = A.sb([128, 128], F32, "cjunk")
        ss = [A.sb([128, 4], F32, "css%d" % i) for i in range(4)]
        hn = [A.sb([128, 512], F32, "chn%d" % i) for i in range(4)]
        hb = [A.sb([128, 512], BF16, "chb%d" % i) for i in range(4)]
        sz = [A.sb([128, 512], BF16, "csz%d" % i) for i in range(2)]
        oo = [A.sb([128, 512], BF16, "coo%d" % i) for i in range(2)]
        tp = [A.ps([128, 512], BF16, "ctp%d" % i) for i in range(8)]
        tiles = []
        for gi, (t0, n) in enumerate(groups):
            for ti in range(n // 128):
                tiles.append((gi, t0, n, ti))

        def stage1(it):
            gi, t0, n, ti = tiles[it]
            tok = t0 + ti * 128
            a, b, m = a_[it % 4], b_[it % 4], m_[it % 4]
            P.dma("sp", a[:], src0[tok:tok + 128, :], writes=[a])
            P.dma("pool", b[:], src1[tok:tok + 128, :], writes=[b])
            if mul is not None:
                P.dma("sp", m[:], mul[tok:tok + 128, :], writes=[m])
            P.op("dve", "tensor_tensor", [a, b], [a], out=a[:], in0=a[:], in1=b[:], op=ALU.add)
            if mul is not None:
                P.op("pool", "tensor_tensor", [a, m], [a], out=a[:], in0=a[:], in1=m[:], op=ALU.mult)

        def stage2(it):
            gi, t0, n, ti = tiles[it]
            a, s_, hb_ = a_[it % 4], ss[it % 4], hb[it % 4]
            for hh in range(4):
                P.op("act", "activation", [a], [junk, s_], out=junk[:], in_=a[:, hh * 128:(hh + 1) * 128], func=AF.Square,
                     accum_out=s_[:, hh:hh + 1])
            P.op("act", "activation", [s_, epsT], [s_], out=s_[:], in_=s_[:], func=AF.Sqrt, scale=1.0 / 128, bias=epsT[:, 0:1])
            P.op("dve", "reciprocal", [s_], [s_], out=s_[:], in_=s_[:])
            for hh in range(4):
                sl = slice(hh * 128, (hh + 1) * 128)
                P.op("dve", "scalar_tensor_tensor", [a, s_, nrow], [hb_], out=hb_[:, sl], in0=a[:, sl], scalar=s_[:, hh:hh + 1],
                     in1=nrow[:, sl], op0=ALU.mult, op1=ALU.mult)
            for j in range(4):
                tpj = tp[(gi % 2) * 4 + j]
                P.op("pe", "transpose", [hb_, ident_bf], [tpj], tpj[:, ti * 128:(ti + 1) * 128],
                     hb_[:, j * 128:(j + 1) * 128], ident_bf[:])
            if ti == n // 128 - 1:
                for j in range(4):
                    tpj = tp[(gi % 2) * 4 + j]
                    r0 = cat_row0 + j * 128
                    z_, o_ = sz[j % 2], oo[j % 2]
                    P.dma("sp", z_[:, 0:n], SC["SZT"][r0:r0 + 128, t0:t0 + n], writes=[z_])
                    P.op("dve", "tensor_tensor", [tpj, z_], [o_], out=o_[:, 0:n], in0=tpj[:, 0:n], in1=z_[:, 0:n], op=ALU.mult)
                    P.dma("pool", SC["CATT"][r0:r0 + 128, t0:t0 + n], o_[:, 0:n], reads=[o_])

        stage1(0)
        for it in range(len(tiles)):
            if it + 1 < len(tiles):
                stage1(it + 1)
            stage2(it)
        P.barrier()


def phase_C(nc, P, IN, SC, layer, gateR, out_ap):
    with ExitStack() as es:
        A = Alloc(nc, es)
        w = A.sb([128, 8, 1024], BF16, "w_out")
        with ExitStack() as es2:
            A2 = Alloc(nc, es2)
            stg = [A2.sb([128, 8, 512], F32, "wstg%d" % i) for i in range(2)]
            for i in range(2):
                P.dma("sp" if i == 0 else "pool", stg[i][:],
                      IN["l%d_w_out" % layer][:, i * 512:(i + 1) * 512].rearrange("(k p) n -> p k n", p=128), writes=[stg[i]])
                P.op("dve" if i == 0 else "act", "tensor_copy" if i == 0 else "copy", [stg[i]], [w], out=w[:, :, i * 512:(i + 1) * 512],
                     in_=stg[i][:])
            P.barrier()
        cat = [A.sb([128, 8, 512], BF16, "catT%d" % i) for i in range(3)]
        hold = [A.sb([128, 1024], F32, "hold%d" % i) for i in range(4)]
        tmp = [A.sb([128, 1024], F32, "ctmp%d" % i) for i in range(4)]
        hnew = [A.sb([128, 1024], F32, "hnew%d" % i) for i in range(4)]
        ps = [A.ps([128, 512], F32, "yps%d" % i) for i in range(4)]
        if layer == 1:
            frow = A.sb([128, 1024], F32, "frow")
            P.dma("sp", frow[:], IN["final_norm"][0:1, :].to_broadcast([128, 1024]), writes=[frow])
            epsT = A.sb([128, 1], F32, "epsTf")
            P.op("pool", "memset", [], [epsT], epsT[:], EPS)
            junk = A.sb([128, 1024], F32, "fjunk")
            st = [A.sb([128, 1], F32, "fst%d" % i) for i in range(4)]
            ob = [A.sb([128, 1024], F32, "fob%d" % i) for i in range(4)]
        it = 0
        groups = GROUPS if layer == 0 else GROUPS[1:]
        def load_cat(gi):
            t0, n = groups[gi]
            c_ = cat[gi % 3]
            for k2 in range(2):
                P.dma("sp", c_[:, k2 * 4:(k2 + 1) * 4, 0:n],
                      SC["CATT"][k2 * 512:(k2 + 1) * 512, t0:t0 + n].rearrange("(k p) t -> p k t", p=128), writes=[c_])

        load_cat(0)
        for gi, (t0, n) in enumerate(groups):
            c_ = cat[gi % 3]
            if gi + 1 < len(groups):
                load_cat(gi + 1)
            g_ = gateR[1] if (layer == 0 and t0 == 0) else gateR[0]
            for ti in range(n // 128):
                tok = t0 + ti * 128
                ho, tm, hn_ = hold[it % 4], tmp[it % 4], hnew[it % 4]
                if layer == 0:
                    srcp = IN["ctx"][tok:tok + 128, :] if t0 == 0 else IN["x"][tok - TC:tok - TC + 128, :]
                else:
                    srcp = SC["H1"][tok:tok + 128, :]
                P.dma("sp", ho[:], srcp, writes=[ho])
                for half in range(2):
                    p_ = ps[(it * 2 + half) % 4]
                    for k in range(8):
                        P.op("pe", "matmul", [c_, w], [p_], p_[:, :], c_[:, k, ti * 128:(ti + 1) * 128],
                             w[:, k, half * 512:(half + 1) * 512], start=(k == 0), stop=(k == 7))
                    P.op("dve", "tensor_tensor", [p_, g_], [tm], out=tm[:, half * 512:(half + 1) * 512], in0=p_[:, :],
                         in1=g_[:, half * 512:(half + 1) * 512], op=ALU.mult)
                P.op("pool", "tensor_tensor", [tm, ho], [hn_], out=hn_[:], in0=tm[:], in1=ho[:], op=ALU.add)
                if layer == 0:
                    P.dma("pool", SC["H1"][tok:tok + 128, :], hn_[:], reads=[hn_])
                else:
                    s_, o_ = st[it % 4], ob[it % 4]
                    P.op("act", "activation", [hn_], [junk, s_], out=junk[:], in_=hn_[:], func=AF.Square, accum_out=s_[:, 0:1])
                    P.op("act", "activation", [s_, epsT], [s_], out=s_[:], in_=s_[:], func=AF.Sqrt, scale=1.0 / D, bias=epsT[:, 0:1])
                    P.op("dve", "reciprocal", [s_], [s_], out=s_[:], in_=s_[:])
                    P.op("act", "activation", [hn_, s_], [o_], out=o_[:], in_=hn_[:], func=AF.Copy, scale=s_[:, 0:1])
                    P.op("dve", "tensor_tensor", [o_, frow], [o_], out=o_[:], in0=o_[:], in1=frow[:], op=ALU.mult)
                    P.dma("pool", out_ap[tok - TC:tok - TC + 128, :], o_[:], reads=[o_])
                it += 1
        P.barrier()


def gla_phase(nc, P, IN, SC, ident_bf):
    NB = NT
    NCH = T // 64
    with ExitStack() as es:
        A = Alloc(nc, es)
        mask = [A.sb([128, 64], F32, "gmask%d" % d) for d in range(2)]
        for d in range(2):
            P.op("pool", "memset", [], [mask[d]], mask[d][:], 1.0)
            for half in range(2):
                pr = slice(half * 64, half * 64 + 64)
                P.op("pool", "affine_select", [mask[d]], [mask[d]], out=mask[d][pr, :], in_=mask[d][pr, :],
                     pattern=[[1 if d == 0 else -1, 64]], compare_op=ALU.is_ge, fill=0.0, base=0,
                     channel_multiplier=-1 if d == 0 else 1)
        rm = [A.sb([64, T], F32, "rm%d" % d) for d in range(2)]
        for d in range(2):
            P.op("pool", "memset", [], [rm[d]], rm[d][:], 1.0)
            j0 = 0 if d == 0 else 63
            P.op("pool", "memset", [rm[d]], [rm[d]], rm[d][:, :].rearrange("p (c j) -> p c j", j=64)[:, :, j0:j0 + 1], 0.0)
        qf = A.sb([64, T], F32, "gqf")
        kf = A.sb([64, T], F32, "gkf")
        lg = A.sb([64, T], F32, "glg")
        bc = A.sb([64, T], F32, "gbc")
        tmp = A.sb([64, T], F32, "gtmp")
        qg = [A.sb([64, T], BF16, "qg%d" % d) for d in range(2)]
        kg = [A.sb([64, T], BF16, "kg%d" % d) for d in range(2)]
        kbf = A.sb([64, T], BF16, "kbf")
        eb = [A.sb([64, NCH], F32, "eb%d" % d) for d in range(2)]
        kbTok = [A.sb([128, NB, 64], BF16, "kbTok%d" % d) for d in range(2)]
        SmT = [A.sb([128, NB, 64], BF16, "gSmT%d" % d) for d in range(2)]
        vtok = A.sb([128, NB, 128], BF16, "gvtok")
        obuf = [[A.sb([128, 128], F32, "gob%d_%d" % (d, i)) for i in range(3)] for d in range(2)]
        Sst = [[A.sb([64, 128], F32, "S%d_%d" % (d, i)) for i in range(2)] for d in range(2)]
        Sb = [[A.sb([64, 128], BF16, "Sb%d_%d" % (d, i)) for i in range(2)] for d in range(2)]
        ptr = [A.ps([128, 8, 64], BF16, "gptr%d" % i) for i in range(2)]
        pS = [A.ps([128, 8, 64], F32, "gpS%d" % i) for i in range(2)]
        pU = [A.ps([128, 128], F32, "gpU%d" % i) for i in range(2)]
        pN = [A.ps([128, 128], F32, "gpN%d" % i) for i in range(2)]
        order = [list(range(NCH)), [3, 2, 1, 0] + list(range(NCH - 1, 3, -1))]
        for h in range(4):
            P.dma("sp", qf[:], SC["MQK"][h * 64:(h + 1) * 64, :], writes=[qf])
            P.dma("pool", kf[:], SC["MQK"][256 + h * 64:256 + (h + 1) * 64, :], writes=[kf])
            P.dma("sp", vtok[:], SC["MV"][:, h * 128:(h + 1) * 128].rearrange("(b p) j -> p b j", p=128), writes=[vtok])
            for d in range(2):
                P.dma("pool", lg[:], SC["LG"][d, h * 64:(h + 1) * 64, :], writes=[lg])
                if d == 0:
                    P.op("dve", "tensor_tensor_scan", [rm[d], lg], [bc], out=bc[:, :], data0=rm[d][:, :], data1=lg[:, :],
                         initial=0.0, op0=ALU.mult, op1=ALU.add)
                else:
                    P.op("dve", "tensor_tensor_scan", [rm[d], lg], [bc], out=bc[:, ::-1], data0=rm[d][:, ::-1],
                         data1=lg[:, ::-1], initial=0.0, op0=ALU.mult, op1=ALU.add)
                jl = 63 if d == 0 else 0
                bl = bc[:, :].rearrange("p (c j) -> p c j", j=64)[:, :, jl:jl + 1]
                P.op("act", "activation", [bc], [eb[d]], out=eb[d][:, :].unsqueeze(2), in_=bl, func=AF.Exp)
                P.op("act", "activation", [bc], [tmp], out=tmp[:], in_=bc[:], func=AF.Exp)
                P.op("dve", "scalar_tensor_tensor", [qf, tmp], [qg[d]], out=qg[d][:], in0=qf[:], scalar=64.0 ** -0.5, in1=tmp[:],
                     op0=ALU.mult, op1=ALU.mult)
                P.op("act", "activation", [bc], [tmp], out=tmp[:], in_=bc[:], func=AF.Exp, scale=-1.0)
                P.op("dve", "tensor_tensor", [kf, tmp], [kg[d]], out=kg[d][:], in0=kf[:], in1=tmp[:], op=ALU.mult)
                P.op("dve", "tensor_tensor", [bc], [tmp], out=tmp[:, :].rearrange("p (c j) -> p c j", j=64),
                     in0=bl.to_broadcast([64, NCH, 64]), in1=bc[:, :].rearrange("p (c j) -> p c j", j=64), op=ALU.subtract)
                P.op("act", "activation", [tmp], [tmp], out=tmp[:], in_=tmp[:], func=AF.Exp)
                P.op("dve", "tensor_tensor", [kf, tmp], [kbf], out=kbf[:], in0=kf[:], in1=tmp[:], op=ALU.mult)
                for b0 in range(0, NB, 8):
                    nb = min(8, NB - b0)
                    ps = ptr[(b0 // 8) % 2]
                    for bb in range(nb):
                        P.op("pe", "transpose", [kbf, ident_bf], [ps], ps[:, bb, :], kbf[:, (b0 + bb) * 128:(b0 + bb + 1) * 128],
                             ident_bf[0:64, 0:64])
                    P.op("act", "copy", [ps], [kbTok[d]], out=kbTok[d][:, b0:b0 + nb, :], in_=ps[:, 0:nb, :])
                for b0 in range(0, NB, 8):
                    nb = min(8, NB - b0)
                    ps = pS[(b0 // 8) % 2]
                    for bb in range(nb):
                        for half in range(2):
                            c = (b0 + bb) * 2 + half
                            pr = slice(half * 64, half * 64 + 64)
                            P.op("pe", "matmul", [kg[d], qg[d]], [ps], ps[pr, bb, :], kg[d][:, c * 64:(c + 1) * 64],
                                 qg[d][:, c * 64:(c + 1) * 64], start=True, stop=True)
                    P.op("dve", "tensor_tensor", [ps, mask[d]], [SmT[d]], out=SmT[d][:, b0:b0 + nb, :], in0=ps[:, 0:nb, :],
                         in1=mask[d][:].unsqueeze(1).to_broadcast([128, nb, 64]), op=ALU.mult)
            for d in range(2):
                P.op("pool", "memset", [], [Sst[d][0]], Sst[d][0][:], 0.0)
                P.op("pool", "memset", [], [Sb[d][0]], Sb[d][0][:], 0.0)
            for i in range(NCH):
                for d in range(2):
                    c = order[d][i]
                    b, half = c // 2, c % 2
                    pr = slice(half * 64, half * 64 + 64)
                    Sold, Snew = Sst[d][i % 2], Sst[d][(i + 1) % 2]
                    sbo, sbn = Sb[d][i % 2], Sb[d][(i + 1) % 2]
                    U, N = pU[d], pN[d]
                    ob = obuf[d][(i // 2) % 3]
                    P.op("pe", "matmul", [SmT[d], vtok], [N], N[pr, :], SmT[d][pr, b, :], vtok[pr, b, :], start=True, stop=False)
                    P.op("pe", "matmul", [qg[d], sbo], [N], N[pr, :], qg[d][:, c * 64:(c + 1) * 64], sbo[:, :], start=False,
                         stop=True)
                    P.op("pe", "matmul", [kbTok[d], vtok], [U], U[0:64, :], kbTok[d][pr, b, :], vtok[pr, b, :], start=True,
                         stop=True)
                    P.op("dve", "scalar_tensor_tensor", [Sold, eb[d], U], [sbn], out=sbn[:], in0=Sold[:],
                         scalar=eb[d][:, c:c + 1], in1=U[0:64, :], op0=ALU.mult, op1=ALU.add)
                    P.op("dve", "scalar_tensor_tensor", [Sold, eb[d], U], [Snew], out=Snew[:], in0=Sold[:],
                         scalar=eb[d][:, c:c + 1], in1=U[0:64, :], op0=ALU.mult, op1=ALU.add)
                    P.op("act", "copy", [N], [ob], out=ob[pr, :], in_=N[pr, :])
                    if i % 2 == 1:
                        P.dma("sp" if d == 0 else "pool", SC["HM"][d, b * 128:(b + 1) * 128, h * 128:(h + 1) * 128], ob[:],
                              reads=[ob])
        P.barrier()


def _na_rows_ok(qr, kr):
    lo = min(max(qr - 4, 0), 56)
    return lo <= kr < lo + 8


def _na_cfg(g):
    if g == 0:
        return "first", 0, 6
    if g == 7:
        return "last", 26, 6
    return "mid", 4 * g - 2, 8


def _na_range(g, ktl):
    js = [j for j in range(8) for i in range(2) if _na_rows_ok(8 * g + j, 2 * ktl + i)]
    return min(js), max(js)


def na_bias_fn(nc, P, IN, state):
    def fn(h, A):
        if A is not None:
            state.setdefault("sets", {})
            for key, ntile in (("first", 6), ("mid", 8), ("last", 6)):
                state["sets"][(key, h % 2)] = [A.sb([128, 512], F32, "nab_%s%d_%d" % (key, i, h % 2)) for i in range(ntile)]
        out = {}

        def loader():
            for key, g, t_lo in (("first", 0, 0), ("mid", 1, 2), ("last", 7, 26)):
                tiles = state["sets"][(key, h % 2)]
                for r, bt in enumerate(tiles):
                    ktl = t_lo + r
                    u0, u1 = _na_range(g, ktl)
                    P.op("pool", "memset", [], [bt], bt[:, u0 * 64:(u1 + 1) * 64], MASKV)
                    for i in range(2):
                        kr = 2 * ktl + i
                        js = [j for j in range(8) if _na_rows_ok(8 * g + j, kr)]
                        if not js:
                            continue
                        j0, j1 = js[0], js[-1]
                        assert js == list(range(j0, j1 + 1))
                        m0 = 7 - (kr - 8 * g - j0)
                        nj = j1 - j0 + 1
                        P.dma("sp" if i == 0 else "pool",
                              bt[i * 64:(i + 1) * 64, j0 * 64:(j1 + 1) * 64].rearrange("p (m q) -> p m q", q=64),
                              IN["na_bias"][h, m0:m0 + nj, :, :].rearrange("m k q -> k m q"), writes=[bt])
                    yield

        for g in range(8):
            key, t_lo, nt_ = _na_cfg(g)
            for r in range(nt_):
                out[(g, 2 + t_lo + r)] = state["sets"][(key, h % 2)][r]
        return out, loader()

    return fn


def _shapes(d):
    return {k: (v.shape, "bf16" if v.dtype == ml_dtypes.bfloat16 else "f32") for k, v in d.items()}


def run(inputs, stage=99, debug=(), cores=8, skip=()):
    inputs = {k: np.asarray(v) for k, v in inputs.items()}
    sh, per = prep_inputs(inputs)
    nc = build(_shapes(sh), _shapes(per[0]), stage=stage, debug=debug, skip=skip)
    in_maps = [dict(sh, **per[b]) for b in range(cores)]
    res = run_bass_kernel_spmd(nc, in_maps, core_ids=list(range(cores)))
    return res


def kernel(**inputs):
    res = run(inputs)
    return np.stack([np.asarray(r["out"], dtype=np.float32) for r in res.results], axis=0)
```

```python
import numpy as np
from contextlib import ExitStack
import ml_dtypes
import concourse.bass as bass
import concourse.mybir as mybir
from concourse.bass_utils import run_bass_kernel_spmd

F32 = mybir.dt.float32
BF16 = mybir.dt.bfloat16
AF = mybir.ActivationFunctionType
ALU = mybir.AluOpType
AX = mybir.AxisListType

D = 1024
TC = 256
TL = 4096
T = TC + TL
NT = T // 128
EPS = 1e-6
MASKV = -30000.0

GROUPS = [(0, 256)] + [(256 + 512 * i, 512) for i in range(8)]


class Dep:
    __slots__ = ("w", "r")

    def __init__(self):
        self.w = None
        self.r = {}


class Tile:
    def __init__(self, t):
        self.t = t
        self.d = Dep()

    def __getitem__(self, k):
        return self.t[k]


class DramDep:
    def __init__(self):
        self.d = Dep()


class Prog:
    def __init__(self, nc, es):
        self.nc = nc
        self.eng = {"pe": nc.tensor, "act": nc.scalar, "dve": nc.vector, "pool": nc.gpsimd, "sp": nc.sync}
        self.R = 12
        self.keys = [("pe", "c"), ("act", "c"), ("dve", "c"), ("pool", "c")]
        for q in ("sp", "pool"):
            self.keys += [(q, "d%d" % i) for i in range(self.R)]
        self.ndma = {"sp": 0, "pool": 0}
        self.sem = {k: es.enter_context(nc.semaphore("s_%s_%s" % k)) for k in self.keys}
        self.cnt = {k: 0 for k in self.keys}
        self.waited = {e: {} for e in self.eng}
        self.n = 0

    def _emit(self, eng, kind, fn, reads, writes):
        if kind == "d":
            kind = "d%d" % (self.ndma[eng] % self.R)
            self.ndma[eng] += 1
        key = (eng, kind)
        deps = {}
        if kind != "c" and self.cnt[key] > 0:
            deps[key] = self.cnt[key]

        def add(tok):
            if tok is None:
                return
            k, v = tok
            if deps.get(k, 0) < v:
                deps[k] = v

        for b in reads:
            add(b.d.w)
        for b in writes:
            add(b.d.w)
            for k, v in b.d.r.items():
                add((k, v))
        e = self.eng[eng]
        wd = self.waited[eng]
        for k, v in deps.items():
            if k == ("pe", "c") and eng == "pe":
                continue
            if wd.get(k, 0) >= v:
                continue
            e.wait_ge(self.sem[k], v)
            wd[k] = v
        inc = 16 if kind != "c" else 1
        self.cnt[key] += inc
        fn(e).then_inc(self.sem[key], inc)
        v = self.cnt[key]
        for b in reads:
            if b.d.r.get(key, 0) < v:
                b.d.r[key] = v
        for b in writes:
            b.d.w = (key, v)
            b.d.r = {}
        self.n += 1

    def op(self, eng, name, reads, writes, *a, **kw):
        self._emit(eng, "c", lambda e: getattr(e, name)(*a, **kw), reads, writes)

    def dma(self, q, out, in_, reads=(), writes=(), **kw):
        self._emit(q, "d", lambda e: e.dma_start(out=out, in_=in_, **kw), reads, writes)

    def barrier(self):
        for en, e in self.eng.items():
            wd = self.waited[en]
            for k in self.keys:
                v = self.cnt[k]
                if v > 0 and wd.get(k, 0) < v:
                    e.wait_ge(self.sem[k], v)
                    wd[k] = v


class Alloc:
    def __init__(self, nc, es):
        self.nc = nc
        self.es = es
        _CTR.setdefault(id(nc), 0)

    def _nm(self, name):
        _CTR[id(self.nc)] = _CTR.get(id(self.nc), 0) + 1
        return "%s_%d" % (name, _CTR[id(self.nc)])

    def sb(self, shape, dt, name=None):
        return Tile(self.es.enter_context(self.nc.sbuf_tensor(self._nm(name or "sb"), list(shape), dt)))

    def ps(self, shape, dt, name=None):
        return Tile(self.es.enter_context(self.nc.psum_tensor(self._nm(name or "ps"), list(shape), dt)))


_CTR = {}


def _fm(v, nchunk):
    return np.ascontiguousarray(v.reshape(nchunk, 128).T)


def _rope_perm():
    perm = np.zeros(32, np.int64)
    for i in range(32):
        r = i % 16
        perm[i] = i + 8 if r < 8 else i - 8
    return perm


def _rope_tables():
    t = np.arange(TL)
    inv = (1.0 / (10000.0 ** (np.arange(8, dtype=np.float32) / 8))).astype(np.float32)
    pos = [(t // 64).astype(np.float32), (t % 64).astype(np.float32)]
    C = np.zeros((32, TL), np.float32)
    S = np.zeros((32, TL), np.float32)
    for i in range(32):
        a = i // 16
        r = i % 16
        p = r % 8
        ang = (pos[a] * inv[p]).astype(np.float32)
        C[i] = np.cos(ang)
        S[i] = -np.sin(ang) if r < 8 else np.sin(ang)
    Cf = np.zeros((128, TL), np.float32)
    Sf = np.zeros((128, TL), np.float32)
    Cf[0:32] = C
    Cf[64:96] = C
    Sf[0:32] = S
    Sf[64:96] = S
    return Cf, Sf


def prep_inputs(inp):
    sh = {}
    sh["ident_bf"] = np.eye(128, dtype=np.float32).astype(ml_dtypes.bfloat16)
    sh["ident_f"] = np.eye(128, dtype=np.float32)
    perm = _rope_perm()
    w_in = inp["l0_w_in"]
    gi_cols = [2720 + d * 8 + h for d in range(2) for h in range(4)]
    gf_cols = [2720 + d * 8 + 4 + h for d in range(2) for h in range(4)]
    sh["l0_w_in"] = np.ascontiguousarray(
        np.concatenate([w_in, w_in[:, 640:672][:, perm], w_in[:, gi_cols], w_in[:, gf_cols]], axis=1))
    w_uq = inp["l0_mla_w_uq"].reshape(384, 8, 96)
    ext = np.concatenate([w_uq, w_uq[:, :, 0:64], w_uq[:, :, 64:96][:, :, perm]], axis=2)
    sh["l0_w_uq"] = np.ascontiguousarray(ext.reshape(384, 8 * 192))
    w_ukv = inp["l0_mla_w_ukv"].reshape(256, 8, 128)
    sh["l0_w_ukv"] = np.ascontiguousarray(
        np.concatenate([w_ukv[:, :, 0:64].reshape(256, 512), w_ukv[:, :, 64:128].reshape(256, 512)], axis=1))
    sh["l0_qnT"] = _fm(inp["l0_mla_q_norm"], 3)
    sh["l0_kvnT"] = _fm(inp["l0_mla_kv_norm"], 2)
    Cf, Sf = _rope_tables()
    sh["ropeC"] = Cf
    sh["ropeS"] = Sf
    cw = inp["l0_mlstm_conv_w"]
    sh["l0_convT"] = np.ascontiguousarray(
        np.concatenate([cw.reshape(3, 8, 128).transpose(2, 1, 0), inp["l0_mlstm_conv_b"].reshape(8, 128).T[:, :, None]],
                       axis=2))
    gb = np.zeros((16, 1), np.float32)
    for d in range(2):
        for h in range(4):
            gb[d * 8 + h, 0] = inp["l0_mlstm_b_i"][d, h]
            gb[d * 8 + 4 + h, 0] = inp["l0_mlstm_b_f"][d, h]
    sh["l0_gbias"] = gb
    gb2 = np.zeros((64, 2), np.float32)
    for d in range(2):
        for h in range(4):
            gb2[d * 32 + h, 0] = inp["l0_mlstm_b_i"][d, h]
            gb2[d * 32 + h, 1] = inp["l0_mlstm_b_f"][d, h]
    sh["l0_gb2"] = gb2
    sh["l0_hnorm"] = np.ascontiguousarray(inp["l0_mlstm_norm"].reshape(1, 512))
    sh["l0_w_out"] = inp["l0_w_out"]
    sh["l1_w_in"] = inp["l1_w_in"]
    sh["l1_w_gate"] = np.ascontiguousarray(inp["l1_gla_w_gate"])
    sh["l1_bgT"] = np.ascontiguousarray(inp["l1_gla_b_gate"].reshape(2, 2, 128).transpose(2, 0, 1))
    sh["l1_gnorm"] = np.ascontiguousarray(inp["l1_gla_norm"].reshape(1, 512))
    sh["l1_w_out"] = inp["l1_w_out"]
    sh["final_norm"] = np.ascontiguousarray(inp["final_norm"].reshape(1, 1024))
    rpb = inp["l1_na_rpb"]
    kc = np.arange(64)[:, None]
    qc = np.arange(64)[None, :]
    wc0 = np.clip(qc - 8, 0, 48)
    okc = (kc >= wc0) & (kc < wc0 + 16)
    dcol = np.clip(kc - qc + 15, 0, 30)
    Tb = np.full((8, 15, 64, 64), MASKV, np.float32)
    for m in range(15):
        dr = 7 - m
        blk = rpb[:, dr + 7][:, dcol]
        Tb[:, m] = np.where(okc[None], blk, np.float32(MASKV))
    sh["na_bias"] = Tb
    mods = [(inp["l0_norm"], inp["l0_w_mod"], inp["l0_b_mod"]), (inp["l1_norm"], inp["l1_w_mod"], inp["l1_b_mod"])]
    for l, (g_, wm_, bm_) in enumerate(mods):
        sh["l%d_w_mod" % l] = wm_
        sh["l%d_bmodT" % l] = _fm(bm_, 24)
        sh["l%d_bmod_gate" % l] = np.ascontiguousarray(bm_[2048:3072].reshape(1, 1024))
        sh["l%d_gT" % l] = _fm(g_, 8)
    per = []
    for b in range(8):
        d = {}
        d["x"] = inp["x"][b]
        d["ctx"] = inp["ctx"][b]
        cv = np.stack([inp["c"][b], inp["c_ctx"]], axis=1)
        d["cvec"] = np.ascontiguousarray(cv.reshape(8, 128, 2).transpose(1, 0, 2))
        per.append(d)
    return sh, per


def build(sh_shapes, per_shapes, stage=99, debug=(), skip=()):
    nc = bass.Bass("TRN2", target_bir_lowering=False)
    IN = {}
    for k, (shape, dt) in list(sh_shapes.items()) + list(per_shapes.items()):
        IN[k] = nc.dram_tensor(k, list(shape), BF16 if dt == "bf16" else F32, kind="ExternalInput").ap()
    out = nc.dram_tensor("out", [TL, D], F32, kind="ExternalOutput").ap()

    def scratch(name, shape, dt):
        kind = "ExternalOutput" if name in debug else "Internal"
        return nc.dram_tensor(name, list(shape), dt, kind=kind).ap()

    SC = {}
    SC["H1"] = scratch("H1", [T, D], F32)
    SC["SZT"] = scratch("SZT", [1024, T], BF16)
    SC["CATT"] = scratch("CATT", [1024, T], BF16)
    SC["QT"] = scratch("QT", [8, 96, T], BF16)
    SC["KT"] = scratch("KT", [8, 96, T], BF16)
    SC["V"] = scratch("V", [T, 512], BF16)
    SC["MQK"] = scratch("MQK", [1024, T], F32)
    SC["GI"] = scratch("GI", [8, T], F32)
    SC["GF"] = scratch("GF", [8, T], F32)
    SC["MV"] = scratch("MV", [T, 512], BF16)
    SC["MO"] = scratch("MO", [T, 512], BF16)
    SC["HM"] = scratch("HM", [2, T, 512], F32)
    SC["RD"] = scratch("RD", [16, 512], F32)
    SC["LG"] = scratch("LG", [2, 256, T], F32)
    SC["NQ"] = scratch("NQ", [512, T], BF16)
    SC["NK"] = scratch("NK", [512, T], BF16)

    with ExitStack() as es0:
        P = Prog(nc, es0)
        A0 = Alloc(nc, es0)
        ident_bf = A0.sb([128, 128], BF16, "identbf")
        ident_f = A0.sb([128, 128], F32, "identf")
        ones_f = A0.sb([128, 128], F32, "onesf")
        P.dma("sp", ident_bf[:], IN["ident_bf"][:, :], writes=[ident_bf])
        P.dma("sp", ident_f[:], IN["ident_f"][:, :], writes=[ident_f])
        P.op("pool", "memset", [], [ones_f], ones_f[:], 1.0)
        affA = [A0.sb([128, 8, 2], F32, "affA%d" % l) for l in range(2)]
        affB = [A0.sb([128, 8, 2], F32, "affB%d" % l) for l in range(2)]
        gateR = [[A0.sb([128, 1024], F32, "gateR%d_%d" % (l, s)) for s in range(2 if l == 0 else 1)] for l in range(2)]

        esA0 = es0.enter_context(ExitStack())
        Aw0 = Alloc(nc, esA0)
        w0 = Aw0.sb([128, 8, 3808], BF16, "w_in0")
        w_uq0 = Aw0.sb([128, 3, 1536], BF16, "w_uq0")
        w_ukv0 = Aw0.sb([128, 2, 1024], BF16, "w_ukv0")
        stgA = [Aw0.sb([128, 1024], F32, "stgA%d" % i) for i in range(2)]

        def w0_loader():
            i = 0
            for c0 in range(0, 3808, 128):
                cw = min(128, 3808 - c0)
                s = stgA[i % 2]
                sv = s[:, :].rearrange("p (k n) -> p k n", k=8)
                P.dma("sp" if i % 2 == 0 else "pool", sv[:, :, 0:cw],
                      IN["l0_w_in"][:, c0:c0 + cw].rearrange("(k p) n -> p k n", p=128), writes=[s])
                P.op("dve" if i % 2 == 0 else "act", "tensor_copy" if i % 2 == 0 else "copy", [s], [w0],
                     out=w0[:, :, c0:c0 + cw], in_=sv[:, :, 0:cw])
                i += 1
                yield
            for kk in range(3):
                for hf in range(2):
                    s = stgA[i % 2]
                    P.dma("sp" if i % 2 == 0 else "pool", s[:, 0:768], IN["l0_w_uq"][kk * 128:(kk + 1) * 128, hf * 768:(hf + 1) * 768],
                          writes=[s])
                    P.op("dve" if i % 2 == 0 else "act", "tensor_copy" if i % 2 == 0 else "copy", [s], [w_uq0],
                         out=w_uq0[:, kk, hf * 768:(hf + 1) * 768], in_=s[:, 0:768])
                    i += 1
                    yield
            for kk in range(2):
                s = stgA[i % 2]
                P.dma("sp" if i % 2 == 0 else "pool", s[:, :], IN["l0_w_ukv"][kk * 128:(kk + 1) * 128, :], writes=[s])
                P.op("dve" if i % 2 == 0 else "act", "tensor_copy" if i % 2 == 0 else "copy", [s], [w_ukv0],
                     out=w_ukv0[:, kk, :], in_=s[:, :])
                i += 1
                yield

        wgen = w0_loader()
        with ExitStack() as es:
            A = Alloc(nc, es)
            cv = A.sb([128, 8, 2], F32, "cv")
            sc = A.sb([128, 8, 2], F32, "sc")
            screp = [A.sb([128, 8, 128], F32, "screp%d" % s) for s in range(2)]
            P.dma("sp", cv[:], IN["cvec"][:, :, :], writes=[cv])
            P.op("act", "activation", [cv], [sc], out=sc[:], in_=cv[:], func=AF.Silu)
            for s in range(2):
                for k in range(8):
                    P.op("dve", "tensor_copy", [sc], [screp[s]], out=screp[s][:, k, :],
                         in_=sc[:, k, s:s + 1].to_broadcast([128, 128]))
            wpan = [A.sb([128, 8, 384], F32, "wpan%d" % i) for i in range(2)]
            wgate = [A.sb([128, 512], F32, "wgate%d" % i) for i in range(3)]
            pm = A.ps([128, 24, 2], F32, "pm")
            pg = [A.ps([128, 512], F32, "pg%d" % i) for i in range(2)]
            bmT = A.sb([128, 24], F32, "bmT")
            gT = A.sb([128, 8], F32, "gT")
            modT = A.sb([128, 24, 2], F32, "modT")
            bgrow = A.sb([128, 1024], F32, "bgrow")
            for l in range(2):
                wm = IN["l%d_w_mod" % l]
                P.dma("sp", bmT[:], IN["l%d_bmodT" % l][:, :], writes=[bmT])
                P.dma("sp", gT[:], IN["l%d_gT" % l][:, :], writes=[gT])
                P.dma("sp", bgrow[:], IN["l%d_bmod_gate" % l][0:1, :].to_broadcast([128, 1024]), writes=[bgrow])
                for pn in range(8):
                    wp = wpan[pn % 2]
                    P.dma("sp" if pn % 2 == 0 else "pool", wp[:],
                          wm[:, pn * 384:(pn + 1) * 384].rearrange("(k p) n -> p k n", p=128), writes=[wp])
                    for j in range(3):
                        n = pn * 3 + j
                        for k in range(8):
                            P.op("pe", "matmul", [wp, sc], [pm], pm[:, n, :], wp[:, k, j * 128:(j + 1) * 128],
                                 sc[:, k, :], start=(k == 0), stop=(k == 7))
                    for _ in range(3):
                        next(wgen, None)
                P.op("dve", "tensor_tensor", [pm, bmT], [modT], out=modT[:], in0=pm[:],
                     in1=bmT[:].unsqueeze(2).to_broadcast([128, 24, 2]), op=ALU.add)
                P.op("dve", "tensor_scalar", [modT], [affA[l]], out=affA[l][:], in0=modT[:, 8:16, :], scalar1=1.0,
                     scalar2=None, op0=ALU.add)
                P.op("dve", "tensor_tensor", [affA[l], gT], [affA[l]], out=affA[l][:], in0=affA[l][:],
                     in1=gT[:].unsqueeze(2).to_broadcast([128, 8, 2]), op=ALU.mult)
                P.op("dve", "tensor_copy", [modT], [affB[l]], out=affB[l][:], in_=modT[:, 0:8, :])
                for s in range(len(gateR[l])):
                    for hf in range(2):
                        ps = pg[hf]
                        for k in range(8):
                            wg = wgate[(hf * 8 + k) % 3]
                            P.dma("sp" if k % 2 == 0 else "pool", wg[:],
                                  wm[k * 128:(k + 1) * 128, 2048 + hf * 512:2048 + (hf + 1) * 512], writes=[wg])
                            P.op("pe", "matmul", [wg, screp[s]], [ps], ps[:], screp[s][:, k, :], wg[:],
                                 start=(k == 0), stop=(k == 7))
                        P.op("dve", "tensor_tensor", [ps, bgrow], [gateR[l][s]],
                             out=gateR[l][s][:, hf * 512:(hf + 1) * 512], in0=ps[:],
                             in1=bgrow[:, hf * 512:(hf + 1) * 512], op=ALU.add)
            for _ in wgen:
                pass
            P.barrier()
        if stage <= 0:
            dbg = nc.dram_tensor("dbg_mod", [128, 2, 2, 8, 2], F32, kind="ExternalOutput").ap()
            dbg2 = nc.dram_tensor("dbg_gate", [128, 1024], F32, kind="ExternalOutput").ap()
            for l in range(2):
                P.dma("sp", dbg[:, l, 0], affA[l][:], reads=[affA[l]])
                P.dma("sp", dbg[:, l, 1], affB[l][:], reads=[affB[l]])
            P.dma("sp", dbg2[:, :], gateR[0][1][:], reads=[gateR[0][1]])
            P.barrier()
            return nc

        phase_A(nc, P, IN, SC, 0, affA[0], affB[0], ident_bf, ones_f, w_pre=(w0, w_uq0, w_ukv0))
        esA0.close()
        if stage <= 1:
            return nc
        if 2 not in skip:
            mla_groups = [(0, 256, [0, 1], 0)] + [(256 + 512 * g, 512, list(range(NT)), 0) for g in range(8)]
            attention(nc, P, SC, ones_f, 8, 96, 96.0 ** -0.5, lambda h: SC["QT"][h, :, :], lambda h: SC["KT"][h, :, :],
                      SC["V"], 0, mla_groups)
        if stage <= 2:
            return nc
        if 3 not in skip:
            mlstm_phase(nc, P, IN, SC, ident_bf, ident_f, ones_f)
        if stage <= 3:
            return nc
        combine_phase(nc, P, IN, SC, ident_bf, SC["HM"][0], SC["HM"][1], SC["MO"], "l0_hnorm", 512, GROUPS)
        if stage <= 4:
            return nc
        phase_C(nc, P, IN, SC, 0, gateR[0], out)
        if stage <= 5:
            return nc
        phase_A(nc, P, IN, SC, 1, affA[1], affB[1], ident_bf, ones_f)
        if stage <= 6:
            return nc
        if 7 not in skip:
            gla_phase(nc, P, IN, SC, ident_bf)
            combine_phase(nc, P, IN, SC, ident_bf, SC["HM"][0], SC["HM"][1], None, "l1_gnorm", 0, GROUPS[1:])
        if stage <= 7:
            return nc
        if 8 not in skip:
            na_groups = []
            for g in range(8):
                key, t_lo, nt_ = _na_cfg(g)
                loc = []
                for r in range(nt_):
                    u0, u1 = _na_range(g, t_lo + r)
                    loc.append((2 + t_lo + r, u0 * 64, (u1 + 1) * 64))
                na_groups.append((256 + 512 * g, 512, [0, 1] + loc, g))
            attention(nc, P, SC, ones_f, 8, 64, 64.0 ** -0.5, lambda h: SC["NQ"][h * 64:(h + 1) * 64, :],
                      lambda h: SC["NK"][h * 64:(h + 1) * 64, :], SC["V"], 512, na_groups, bias_fn=na_bias_fn(nc, P, IN, {}),
                      ident_bf=ident_bf, early_release=True, act_recip=True)
        if stage <= 8:
            return nc
        phase_C(nc, P, IN, SC, 1, gateR[1], out)
    return nc


def phase_A(nc, P, IN, SC, layer, affA, affB, ident_bf, ones_f, w_pre=None):
    NW = 3808 if layer == 0 else 3616
    w_in_d = IN["l%d_w_in" % layer]
    with ExitStack() as es:
        A = Alloc(nc, es)
        if w_pre is not None:
            w_in, w_uq, w_ukv = w_pre
        else:
            w_in = A.sb([128, 8, NW], BF16, "w_in")
            if layer == 0:
                w_uq = A.sb([128, 3, 1536], BF16, "w_uq")
                w_ukv = A.sb([128, 2, 1024], BF16, "w_ukv")
        with ExitStack() as es2:
            A2 = Alloc(nc, es2)
            stg = [A2.sb([128, 8, 512], F32, "stg%d" % i) for i in range(2)] if w_pre is None else None
            i = 0
            for c0 in (range(0, NW, 512) if w_pre is None else ()):
                cw = min(512, NW - c0)
                s = stg[i % 2]
                P.dma("sp" if i % 2 == 0 else "pool", s[:, :, 0:cw],
                      w_in_d[:, c0:c0 + cw].rearrange("(k p) n -> p k n", p=128), writes=[s])
                P.op("dve" if i % 2 == 0 else "act", "tensor_copy" if i % 2 == 0 else "copy", [s], [w_in],
                     out=w_in[:, :, c0:c0 + cw], in_=s[:, :, 0:cw])
                i += 1
            if layer == 0 and w_pre is None:
                s = stg[i % 2]
                for kk in range(3):
                    s = stg[i % 2]
                    P.dma("sp", s[:, 0:3, :], IN["l0_w_uq"][kk * 128:(kk + 1) * 128, :].rearrange("p (a n) -> p a n", a=3),
                          writes=[s])
                    P.op("dve", "tensor_copy", [s], [w_uq], out=w_uq[:, kk, :].rearrange("p (a n) -> p a n", a=3),
                         in_=s[:, 0:3, :])
                    i += 1
                s = stg[i % 2]
                for kk in range(2):
                    P.dma("sp", s[:, 2 * kk:2 * kk + 2, :],
                          IN["l0_w_ukv"][kk * 128:(kk + 1) * 128, :].rearrange("p (a n) -> p a n", a=2), writes=[s])
                P.op("dve", "tensor_copy", [s], [w_ukv], out=w_ukv[:].rearrange("p k (a n) -> p (k a) n", a=2),
                     in_=s[:, 0:4, :])
                i += 1
            P.barrier()
        if layer == 0:
            qnT = A.sb([128, 3], F32, "qnT")
            kvnT = A.sb([128, 2], F32, "kvnT")
            P.dma("sp", qnT[:], IN["l0_qnT"][:, :], writes=[qnT])
            P.dma("sp", kvnT[:], IN["l0_kvnT"][:, :], writes=[kvnT])
            cqT = A.sb([128, 3, 512], F32, "cqT")
            ckvT = A.sb([128, 2, 512], F32, "ckvT")
            sq = A.sb([128, 3, 512], F32, "sq")
            rstd = A.sb([128, 512], F32, "rstd")
            cqn = A.sb([128, 3, 512], BF16, "cqn")
            ckvn = A.sb([128, 2, 512], BF16, "ckvn")
            rC = A.sb([128, 512], F32, "rC")
            rS = A.sb([128, 512], F32, "rS")
            rt1 = A.sb([128, 512], F32, "rt1")
            rt2 = A.sb([128, 512], F32, "rt2")
            qo = [A.sb([128, 512], BF16, "qo%d" % i) for i in range(2)]
            kro = A.sb([32, 512], BF16, "kro")
        else:
            gaT = [A.sb([16, 512], F32, "gaT%d" % d) for d in range(2)]
            wg = A.sb([16, 2, 256], F32, "wg")
            P.dma("sp", wg[:], IN["l1_w_gate"].rearrange("d r k -> r d k"), writes=[wg])
            bgT = A.sb([128, 2, 2], F32, "bgT")
            nbg = A.sb([128, 2, 2], F32, "nbg")
            P.dma("sp", bgT[:], IN["l1_bgT"][:, :, :], writes=[bgT])
            P.op("dve", "tensor_scalar", [bgT], [nbg], out=nbg[:], in0=bgT[:], scalar1=-1.0, scalar2=None, op0=ALU.mult)
            one1 = A.sb([128, 1], F32, "one1a")
            P.op("pool", "memset", [], [one1], one1[:], 1.0)
            lge = A.sb([128, 512], F32, "lge")
            lgo = [A.sb([128, 512], F32, "lgo%d" % i) for i in range(2)]
        hb = [A.sb([128, 1024], F32, "hb%d" % i) for i in range(3)]
        junk = A.sb([128, 1024], F32, "junk")
        st = [A.sb([128, 4], F32, "st%d" % i) for i in range(2)]
        xn2 = [[A.sb([128, 1024], BF16, "xn%d_%d" % (s_, i)) for i in range(4)] for s_ in range(2)]
        epsT = A.sb([128, 1], F32, "epsT")
        P.op("pool", "memset", [], [epsT], epsT[:], EPS)
        uT = [A.sb([128, 8, 512], BF16, "uT%d" % i) for i in range(2)]
        fo_bf = [A.sb([128, 512], BF16, "fobf%d" % i) for i in range(4)]
        fo_f = [A.sb([128, 512], F32, "fof%d" % i) for i in range(3)]
        tp = [A.ps([128, 512], BF16, "tp%d" % i) for i in range(2)]
        acc = [A.ps([128, 512], F32, "acc%d" % i) for i in range(5)]
        cnt = {"acc": 0, "fobf": 0, "fof": 0, "ev": 0, "q": 0, "hb": 0, "xn": 0, "tp": 0}

        def nxt(name, lst):
            r = lst[cnt[name] % len(lst)]
            cnt[name] += 1
            return r

        def evac_engine():
            cnt["ev"] += 1
            return "dve" if cnt["ev"] % 2 == 0 else "act"

        def copy_op(eng, src_t, src_ap, dst_t, dst_ap):
            if eng == "act":
                P.op("act", "copy", [src_t], [dst_t], out=dst_ap, in_=src_ap)
            else:
                P.op(eng, "tensor_copy", [src_t], [dst_t], out=dst_ap, in_=src_ap)

        def stq():
            cnt["q"] += 1
            return "pool" if cnt["q"] % 2 == 0 else "sp"

        def norm_part(gi):
            t0, n = GROUPS[gi]
            ntl = n // 128
            sta = st[gi % 2]
            xn = xn2[gi % 2]
            for ti in range(ntl):
                h = nxt("hb", hb)
                tok = t0 + ti * 128
                if layer == 0:
                    src = IN["ctx"][tok:tok + 128, :] if gi == 0 else IN["x"][tok - TC:tok - TC + 128, :]
                else:
                    src = SC["H1"][tok:tok + 128, :]
                P.dma("sp", h[:], src, writes=[h])
                P.op("act", "activation", [h], [junk, sta], out=junk[:], in_=h[:], func=AF.Square,
                     accum_out=sta[:, ti:ti + 1])
                P.op("act", "activation", [sta, epsT], [sta], out=sta[:, ti:ti + 1], in_=sta[:, ti:ti + 1], func=AF.Sqrt,
                     scale=1.0 / D, bias=epsT[:, 0:1])
                P.op("dve", "reciprocal", [sta], [sta], out=sta[:, ti:ti + 1], in_=sta[:, ti:ti + 1])
                x_ = xn[ti]
                P.op("dve", "tensor_scalar", [h, sta], [x_], out=x_[:], in0=h[:], scalar1=sta[:, ti:ti + 1],
                     scalar2=None, op0=ALU.mult)

        def transpose_part(gi):
            t0, n = GROUPS[gi]
            ntl = n // 128
            s = 1 if gi == 0 else 0
            u = uT[gi % 2]
            xn = xn2[gi % 2]
            for j in range(8):
                tpp = nxt("tp", tp)
                for ti in range(ntl):
                    P.op("pe", "transpose", [xn[ti], ident_bf], [tpp], tpp[:, ti * 128:(ti + 1) * 128],
                         xn[ti][:, j * 128:(j + 1) * 128], ident_bf[:])
                P.op("dve", "tensor_scalar", [tpp, affA, affB], [u], out=u[:, j, 0:n],
                     in0=tpp[:, 0:n], scalar1=affA[:, j, s:s + 1], scalar2=affB[:, j, s:s + 1], op0=ALU.mult,
                     op1=ALU.add)


        def proj_part(gi):
            t0, n = GROUPS[gi]
            ntl = n // 128
            u = uT[gi % 2]

            def fm_proj(c0, ncol):
                ps = nxt("acc", acc)
                for k in range(8):
                    P.op("pe", "matmul", [w_in, u], [ps], ps[0:ncol, 0:n], w_in[:, k, c0:c0 + ncol], u[:, k, 0:n],
                         start=(k == 0), stop=(k == 7))
                return ps

            def store_fm(ps, ncol, dst, dt, func=None, eng=None):
                o = nxt("fobf", fo_bf) if dt == BF16 else nxt("fof", fo_f)
                if func is not None:
                    P.op("act", "activation", [ps], [o], out=o[0:ncol, 0:n], in_=ps[0:ncol, 0:n], func=func)
                else:
                    copy_op(eng or evac_engine(), ps, ps[0:ncol, 0:n], o, o[0:ncol, 0:n])
                P.dma(stq(), dst, o[0:ncol, 0:n], reads=[o])

            tsl = slice(t0, t0 + n)
            if layer == 0:
                for j in range(3):
                    ps = fm_proj(j * 128, 128)
                    copy_op(evac_engine(), ps, ps[:, 0:n], cqT, cqT[:, j, 0:n])
                for j in range(2):
                    ps = fm_proj(384 + j * 128, 128)
                    copy_op(evac_engine(), ps, ps[:, 0:n], ckvT, ckvT[:, j, 0:n])
                for (src_t, nk, nrm, dst_t, dim) in ((cqT, 3, qnT, cqn, 384.0), (ckvT, 2, kvnT, ckvn, 256.0)):
                    P.op("act", "activation", [src_t], [sq], out=sq[:, 0:nk, 0:n], in_=src_t[:, 0:nk, 0:n], func=AF.Square)
                    ps = nxt("acc", acc)
                    for k in range(nk):
                        P.op("pe", "matmul", [ones_f, sq], [ps], ps[:, 0:n], ones_f[:], sq[:, k, 0:n], start=(k == 0),
                             stop=(k == nk - 1))
                    P.op("act", "activation", [ps, epsT], [rstd], out=rstd[:, 0:n], in_=ps[:, 0:n], func=AF.Sqrt,
                         scale=1.0 / dim, bias=epsT[:, 0:1])
                    P.op("dve", "reciprocal", [rstd], [rstd], out=rstd[:, 0:n], in_=rstd[:, 0:n])
                    for k in range(nk):
                        P.op("dve", "scalar_tensor_tensor", [src_t, nrm, rstd], [dst_t], out=dst_t[:, k, 0:n],
                             in0=src_t[:, k, 0:n], scalar=nrm[:, k:k + 1], in1=rstd[:, 0:n], op0=ALU.mult, op1=ALU.mult)
                rot = gi > 0
                if rot:
                    P.dma("sp", rC[:, 0:n], IN["ropeC"][:, t0 - TC:t0 - TC + n], writes=[rC])
                    P.dma("sp", rS[:, 0:n], IN["ropeS"][:, t0 - TC:t0 - TC + n], writes=[rS])
                for hh in range(8):
                    ps = nxt("acc", acc)
                    for k in range(3):
                        P.op("pe", "matmul", [w_uq, cqn], [ps], ps[0:96, 0:n], w_uq[:, k, hh * 192:hh * 192 + 96],
                             cqn[:, k, 0:n], start=(k == 0), stop=(k == 2))
                    o = nxt("fobf", fo_bf)
                    if rot:
                        ps2 = nxt("acc", acc)
                        for k in range(3):
                            P.op("pe", "matmul", [w_uq, cqn], [ps2], ps2[0:96, 0:n],
                                 w_uq[:, k, hh * 192 + 96:hh * 192 + 192], cqn[:, k, 0:n], start=(k == 0), stop=(k == 2))
                        copy_op("act", ps, ps[0:64, 0:n], o, o[0:64, 0:n])
                        P.op("dve", "tensor_tensor", [ps, rC], [rt1], out=rt1[64:96, 0:n], in0=ps[64:96, 0:n],
                             in1=rC[64:96, 0:n], op=ALU.mult)
                        P.op("dve", "tensor_tensor", [ps2, rS], [rt2], out=rt2[64:96, 0:n], in0=ps2[64:96, 0:n],
                             in1=rS[64:96, 0:n], op=ALU.mult)
                        P.op("pool", "tensor_tensor", [rt1, rt2], [o], out=o[64:96, 0:n], in0=rt1[64:96, 0:n],
                             in1=rt2[64:96, 0:n], op=ALU.add)
                    else:
                        copy_op(evac_engine(), ps, ps[0:96, 0:n], o, o[0:96, 0:n])
                    P.dma(stq(), SC["QT"][hh, :, tsl], o[0:96, 0:n], reads=[o])
                for c in range(4):
                    ps = nxt("acc", acc)
                    for k in range(2):
                        P.op("pe", "matmul", [w_ukv, ckvn], [ps], ps[:, 0:n], w_ukv[:, k, c * 128:(c + 1) * 128],
                             ckvn[:, k, 0:n], start=(k == 0), stop=(k == 1))
                    o = nxt("fobf", fo_bf)
                    copy_op(evac_engine(), ps, ps[:, 0:n], o, o[:, 0:n])
                    for hh in range(2):
                        P.dma(stq(), SC["KT"][c * 2 + hh, 0:64, tsl], o[hh * 64:(hh + 1) * 64, 0:n], reads=[o])
                for ti in range(ntl):
                    ps = nxt("acc", acc)
                    for k in range(2):
                        P.op("pe", "matmul", [w_ukv, ckvn], [ps], ps[:, :], ckvn[:, k, ti * 128:(ti + 1) * 128],
                             w_ukv[:, k, 512:1024], start=(k == 0), stop=(k == 1))
                    o = nxt("fobf", fo_bf)
                    copy_op(evac_engine(), ps, ps[:, :], o, o[:, :])
                    P.dma(stq(), SC["V"][t0 + ti * 128:t0 + (ti + 1) * 128, :], o[:, :], reads=[o])
                ps = fm_proj(640, 32)
                if rot:
                    ps2 = fm_proj(3760, 32)
                    P.op("dve", "tensor_tensor", [ps, rC], [rt1], out=rt1[0:32, 0:n], in0=ps[0:32, 0:n], in1=rC[0:32, 0:n],
                         op=ALU.mult)
                    P.op("dve", "tensor_tensor", [ps2, rS], [rt2], out=rt2[0:32, 0:n], in0=ps2[0:32, 0:n],
                         in1=rS[0:32, 0:n], op=ALU.mult)
                    P.op("pool", "tensor_tensor", [rt1, rt2], [kro], out=kro[0:32, 0:n], in0=rt1[0:32, 0:n],
                         in1=rt2[0:32, 0:n], op=ALU.add)
                else:
                    copy_op("dve", ps, ps[0:32, 0:n], kro, kro[0:32, 0:n])
                for hh in range(8):
                    P.dma(stq(), SC["KT"][hh, 64:96, tsl], kro[0:32, 0:n], reads=[kro])
                yield
                for c in range(8):
                    ps = fm_proj(672 + c * 128, 128)
                    store_fm(ps, 128, SC["MQK"][c * 128:(c + 1) * 128, tsl], F32)
                ps = fm_proj(3792, 8)
                store_fm(ps, 8, SC["GI"][:, tsl], F32)
                ps = fm_proj(3800, 8)
                store_fm(ps, 8, SC["GF"][:, tsl], F32)
                for c in range(8):
                    ps = fm_proj(2736 + c * 128, 128)
                    store_fm(ps, 128, SC["SZT"][c * 128:(c + 1) * 128, tsl], BF16, func=AF.Silu)
                tm_specs = [(1696, SC["MV"], None), (2208, SC["MO"], AF.Sigmoid)]
            else:
                for c in range(4):
                    ps = fm_proj(c * 128, 128)
                    store_fm(ps, 128, SC["MQK"][c * 128:(c + 1) * 128, tsl], F32)
                yield
                for d in range(2):
                    ps = fm_proj(1024 + 16 * d, 16)
                    copy_op("dve", ps, ps[0:16, 0:n], gaT[d], gaT[d][0:16, 0:n])
                for d in range(2):
                    for c2 in range(2):
                        ps = nxt("acc", acc)
                        P.op("pe", "matmul", [wg, gaT[d]], [ps], ps[:, 0:n], wg[0:16, d, c2 * 128:(c2 + 1) * 128],
                             gaT[d][0:16, 0:n], start=True, stop=True)
                        P.op("act", "activation", [ps, nbg], [lge], out=lge[:, 0:n], in_=ps[:, 0:n], func=AF.Exp, scale=-1.0,
                             bias=nbg[:, d, c2:c2 + 1])
                        P.op("act", "activation", [lge, one1], [lge], out=lge[:, 0:n], in_=lge[:, 0:n], func=AF.Ln,
                             bias=one1[:, 0:1])
                        o = lgo[(d * 2 + c2) % 2]
                        P.op("dve", "tensor_scalar", [lge], [o], out=o[:, 0:n], in0=lge[:, 0:n], scalar1=-1.0 / 16.0,
                             scalar2=None, op0=ALU.mult)
                        P.dma(stq(), SC["LG"][d, c2 * 128:(c2 + 1) * 128, tsl], o[:, 0:n], reads=[o])
                for c in range(4):
                    ps = fm_proj(1056 + c * 128, 128)
                    store_fm(ps, 128, SC["NQ"][c * 128:(c + 1) * 128, tsl], BF16)
                for c in range(4):
                    ps = fm_proj(1568 + c * 128, 128)
                    store_fm(ps, 128, SC["NK"][c * 128:(c + 1) * 128, tsl], BF16)
                for c in range(8):
                    ps = fm_proj(2592 + c * 128, 128)
                    store_fm(ps, 128, SC["SZT"][c * 128:(c + 1) * 128, tsl], BF16, func=AF.Silu)
                tm_specs = [(512, SC["MV"], None), (2080, SC["V"], None)]
            for (c0, dst, func) in tm_specs:
                for ti in range(ntl):
                    ps = nxt("acc", acc)
                    for k in range(8):
                        P.op("pe", "matmul", [w_in, u], [ps], ps[:, :], u[:, k, ti * 128:(ti + 1) * 128],
                             w_in[:, k, c0:c0 + 512], start=(k == 0), stop=(k == 7))
                    o = nxt("fobf", fo_bf)
                    if func is not None:
                        P.op("act", "activation", [ps], [o], out=o[:, :], in_=ps[:, :], func=func)
                    else:
                        copy_op(evac_engine(), ps, ps[:, :], o, o[:, :])
                    P.dma(stq(), dst[t0 + ti * 128:t0 + (ti + 1) * 128, :], o[:, :], reads=[o])

        norm_part(0)
        transpose_part(0)
        for gi in range(len(GROUPS)):
            if gi + 1 < len(GROUPS):
                norm_part(gi + 1)
            gen = proj_part(gi)
            next(gen)
            if gi + 1 < len(GROUPS):
                transpose_part(gi + 1)
            for _ in gen:
                pass
        P.barrier()


def attention(nc, P, SC, ones_f, heads, dq, scale, load_q, load_k, Vd, cat_row0, groups, bias_fn=None, ident_bf=None,
              early_release=False, act_recip=False):
    LOOK = 4
    NS = 5
    EPI_DELAY = 8
    with ExitStack() as es:
        A = Alloc(nc, es)
        V = A.sb([128, NT, heads, 65], BF16, "Vall")
        P.op("pool", "memset", [], [V], V[:, :, :, 64:65], 1.0)
        for half in range(2):
            tl = slice(half * 17, (half + 1) * 17)
            for hh in range(heads):
                P.dma("sp" if hh % 2 == 0 else "pool", V[:, tl, hh, 0:64],
                      Vd[half * 17 * 128:(half + 1) * 17 * 128, hh * 64:(hh + 1) * 64].rearrange("(t p) d -> p t d", p=128),
                      writes=[V])
        kT = [A.sb([128, T], BF16, "kT%d" % i) for i in range(2)]
        qT = [A.sb([128, T], BF16, "qT%d" % i) for i in range(2)]
        pt = [A.sb([128, 512], BF16, "pt%d" % i) for i in range(NS)]
        sb_t = [A.sb([128, 512], F32, "sbt%d" % i) for i in range(3)] if bias_fn is not None else None
        rden = [A.sb([128, 512], F32, "rden%d" % i) for i in range(2)]
        ocp = [A.sb([128, 512], F32, "ocp%d" % i) for i in range(3)] if early_release else None
        rsc = A.sb([128, 512], F32, "rsc")
        bcs = [A.sb([128, 512], F32, "bcs%d" % i) for i in range(2)]
        szt = [A.sb([64, 512], BF16, "szt%d" % i) for i in range(3)]
        tmp = [A.sb([64, 512], F32, "atmp%d" % i) for i in range(2)]
        ao = [A.sb([64, 512], BF16, "ao%d" % i) for i in range(2)]
        Sps = [A.ps([128, 512], F32, "Sps%d" % i) for i in range(NS)]
        Ops = [A.ps([128, 512], F32, "Ops%d" % i) for i in range(2)]
        Bps = A.ps([128, 512], F32, "Bps")
        if dq < 128:
            for t_ in kT + qT:
                P.op("pool", "memset", [], [t_], t_[64:128, :], 0.0)
        P.dma("sp", kT[0][0:dq, :], load_k(0), writes=[kT[0]])
        P.dma("pool", qT[0][0:dq, :], load_q(0), writes=[qT[0]])
        gcount = 0
        it = 0
        rd_dep = [DramDep() for _ in range(16)]
        pend = []
        for h in range(heads):
            k_ = kT[h % 2]
            q_ = qT[h % 2]
            if h + 1 < heads:
                P.dma("sp", kT[(h + 1) % 2][0:dq, :], load_k(h + 1), writes=[kT[(h + 1) % 2]])
                P.dma("pool", qT[(h + 1) % 2][0:dq, :], load_q(h + 1), writes=[qT[(h + 1) % 2]])
            bias_loader = None
            if bias_fn is not None:
                if h == 0:
                    bias_cur, ld0 = bias_fn(0, A)
                    for _ in ld0:
                        pass
                bias_tiles = bias_cur
                if h + 1 < heads:
                    bias_cur, bias_loader = bias_fn(h + 1, A if h == 0 else None)
            else:
                bias_tiles = None
            r0 = cat_row0 + h * 64
            items = []
            for (q0, n, tiles, gkey) in groups:
                gid = gcount
                gcount += 1
                for j, kt in enumerate(tiles):
                    if isinstance(kt, tuple):
                        kt, c0, c1 = kt
                    else:
                        c0, c1 = 0, n
                    items.append((gid, q0, n, gkey, j, kt, len(tiles), c0, c1))

            def flush(cond):
                for e_ in pend[:]:
                    if cond(e_[1][0]):
                        emit_epi(*e_[1])
                        pend.remove(e_)

            def emit_S(item, slot):
                gid, q0, n, gkey, j, kt, nt_, c0, c1 = item
                S = Sps[slot % NS]
                p_ = pt[slot % NS]
                if j == 0:
                    flush(lambda g2: g2 % 3 == gid % 3)
                    sz = szt[gid % 3]
                    P.dma("sp", sz[:, 0:n], SC["SZT"][r0:r0 + 64, q0:q0 + n], writes=[sz])
                P.op("pe", "matmul", [k_, q_], [S], S[:, c0:c1], k_[:, kt * 128:(kt + 1) * 128], q_[:, q0 + c0:q0 + c1],
                     start=True, stop=True)
                bt = bias_tiles.get((gkey, kt)) if bias_tiles is not None else None
                if bt is not None:
                    sb = sb_t[slot % 3]
                    P.op("dve", "scalar_tensor_tensor", [S, bt], [sb], out=sb[:, c0:c1], in0=S[:, c0:c1], scalar=scale,
                         in1=bt[:, c0:c1], op0=ALU.mult, op1=ALU.add)
                    P.op("act", "activation", [sb], [p_], out=p_[:, c0:c1], in_=sb[:, c0:c1], func=AF.Exp)
                else:
                    P.op("act", "activation", [S], [p_], out=p_[:, c0:c1], in_=S[:, c0:c1], func=AF.Exp, scale=scale)

            def emit_PV(item, slot):
                gid, q0, n, gkey, j, kt, nt_, c0, c1 = item
                O = Ops[gid % 2]
                p_ = pt[slot % NS]
                assert j > 0 or (c0 == 0 and c1 == n)
                if j == 0:
                    flush(lambda g2: g2 % 2 == gid % 2)
                P.op("pe", "matmul", [V, p_], [O], O[0:65, c0:c1], V[:, kt, h, :], p_[:, c0:c1], start=(j == 0),
                     stop=(j == nt_ - 1))
                if j == nt_ - 1:
                    rd = rden[gid % 2]
                    if early_release:
                        oc = ocp[gid % 3]
                        P.op("act", "copy", [O], [oc], out=oc[0:65, 0:n], in_=O[0:65, 0:n])
                        P.op("act", "activation", [oc], [rsc], out=rsc[64:65, 0:n], in_=oc[64:65, 0:n], func=AF.Ln)
                        P.op("act", "activation", [rsc], [rd], out=rd[64:65, 0:n], in_=rsc[64:65, 0:n], func=AF.Exp, scale=-1.0)
                    elif act_recip:
                        P.op("act", "activation", [O], [rsc], out=rsc[64:65, 0:n], in_=O[64:65, 0:n], func=AF.Ln)
                        P.op("act", "activation", [rsc], [rd], out=rd[64:65, 0:n], in_=rsc[64:65, 0:n], func=AF.Exp, scale=-1.0)
                    else:
                        P.op("dve", "reciprocal", [O], [rd], out=rd[64:65, 0:n], in_=O[64:65, 0:n])
                    pend.append([EPI_DELAY, (gid, q0, n, r0, h)])

            def emit_epi(gid, q0, n, r0, h):
                O = Ops[gid % 2]
                rd = rden[gid % 2]
                bc_ = bcs[gid % 2]
                tm_ = tmp[gid % 2]
                a_ = ao[gid % 2]
                sz = szt[gid % 3]
                P.op("pe", "matmul", [ones_f, rd], [Bps], Bps[0:64, 0:n], ones_f[64:65, 0:64], rd[64:65, 0:n],
                     start=True, stop=True)
                if early_release:
                    oc = ocp[gid % 3]
                    P.op("dve", "tensor_tensor", [oc, Bps], [tm_], out=tm_[:, 0:n], in0=oc[0:64, 0:n], in1=Bps[0:64, 0:n],
                         op=ALU.mult)
                else:
                    P.op("act", "copy", [Bps], [bc_], out=bc_[0:64, 0:n], in_=Bps[0:64, 0:n])
                    P.op("dve", "tensor_tensor", [O, bc_], [tm_], out=tm_[:, 0:n], in0=O[0:64, 0:n], in1=bc_[0:64, 0:n],
                         op=ALU.mult)
                P.op("pool", "tensor_tensor", [tm_, sz], [a_], out=a_[:, 0:n], in0=tm_[:, 0:n], in1=sz[:, 0:n], op=ALU.mult)
                P.dma("pool", SC["CATT"][r0:r0 + 64, q0:q0 + n], a_[:, 0:n], reads=[a_])

            nI = len(items)
            for idx in range(nI + LOOK):
                if idx < nI:
                    emit_S(items[idx], it + idx)
                for e_ in pend[:]:
                    e_[0] -= 1
                    if e_[0] <= 0:
                        emit_epi(*e_[1])
                        pend.remove(e_)
                if idx - LOOK >= 0:
                    emit_PV(items[idx - LOOK], it + idx - LOOK)
                if bias_loader is not None and idx % 3 == 2:
                    next(bias_loader, None)
            if bias_loader is not None:
                for _ in bias_loader:
                    pass
            it += nI
        for e_ in pend:
            emit_epi(*e_[1])
        P.barrier()


def mlstm_phase(nc, P, IN, SC, ident_bf, ident_f, ones_f):
    NB = NT
    NCH = T // 64
    with ExitStack() as es:
        A = Alloc(nc, es)
        esT = A.sb([128, NB, 64], F32, "esT")
        fT = A.sb([128, NB, 64], F32, "fT")
        decbc = A.sb([128, 8, NCH], F32, "decbc")
        mask = [A.sb([128, 64], F32, "mask%d" % d) for d in range(2)]
        for d in range(2):
            P.op("pool", "memset", [], [mask[d]], mask[d][:], 1.0)
            for half in range(2):
                pr = slice(half * 64, half * 64 + 64)
                P.op("pool", "affine_select", [mask[d]], [mask[d]], out=mask[d][pr, :], in_=mask[d][pr, :],
                     pattern=[[1 if d == 0 else -1, 64]], compare_op=ALU.is_ge, fill=0.0, base=0,
                     channel_multiplier=-1 if d == 0 else 1)
        with ExitStack() as es2:
            A2 = Alloc(nc, es2)
            X1 = A2.sb([64, T], F32, "X1")
            X2 = A2.sb([64, T], F32, "X2")
            X3 = A2.sb([64, T], F32, "X3")
            X4 = A2.sb([64, T], F32, "X4")
            gb = A2.sb([64, 2], F32, "gb")
            nbf = A2.sb([64, 1], F32, "nbf")
            one1 = A2.sb([64, 1], F32, "one1")
            dec = A2.sb([64, NCH], F32, "dec")
            aprev = A2.sb([64, NCH], F32, "aprev")
            sel = A2.sb([64, 128], F32, "sel")
            pst = [A2.ps([128, 8, 64], F32, "pst%d" % i) for i in range(2)]
            psd = A2.ps([128, NCH], F32, "psd")
            P.op("pool", "memset", [], [X1], X1[:], 0.0)
            P.op("pool", "memset", [], [X3], X3[:], 0.0)
            P.op("pool", "memset", [], [one1], one1[:], 1.0)
            P.dma("sp", gb[:], IN["l0_gb2"][:, :], writes=[gb])
            for d in range(2):
                P.dma("sp", X1[d * 32:d * 32 + 4, :], SC["GF"][d * 4:d * 4 + 4, :], writes=[X1])
                P.dma("pool", X3[d * 32:d * 32 + 4, :], SC["GI"][d * 4:d * 4 + 4, :], writes=[X3])
            P.op("dve", "tensor_scalar", [gb], [nbf], out=nbf[:], in0=gb[:, 1:2], scalar1=-1.0, scalar2=None, op0=ALU.mult)
            P.op("act", "activation", [X1, nbf], [X1], out=X1[:], in_=X1[:], func=AF.Exp, scale=-1.0, bias=nbf[:, 0:1])
            P.op("act", "activation", [X1, one1], [X1], out=X1[:], in_=X1[:], func=AF.Ln, bias=one1[:, 0:1])

            def seg_views(tile_, prng, d):
                if d == 0:
                    return [tile_[prng, 0:T]]
                return [tile_[prng, 0:TC][:, ::-1], tile_[prng, TC:T][:, ::-1]]

            def scan(dst, src, op0, d):
                prng = slice(d * 32, d * 32 + 32)
                dv = seg_views(dst, prng, d)
                sv = seg_views(src, prng, d)
                for i in range(len(dv)):
                    init = 0.0 if i == 0 else dst[prng, 0:1]
                    P.op("dve", "tensor_tensor_scan", [src, dst], [dst], out=dv[i], data0=sv[i], data1=sv[i],
                         initial=init, op0=op0, op1=ALU.bypass)

            for d in range(2):
                scan(X2, X1, ALU.add, d)
            P.op("dve", "scalar_tensor_tensor", [X3, gb, X2], [X3], out=X3[:], in0=X3[:], scalar=gb[:, 0:1], in1=X2[:],
                 op0=ALU.add, op1=ALU.add)
            for d in range(2):
                scan(X1, X3, ALU.max, d)
            for d in range(2):
                prng = slice(d * 32, d * 32 + 32)
                jj = 63 if d == 0 else 0
                P.op("dve", "tensor_copy", [X1], [X4], out=X4[prng, :].rearrange("p (c j) -> p c j", j=64),
                     in_=X1[prng, :].rearrange("p (c j) -> p c j", j=64)[:, :, jj:jj + 1].to_broadcast([32, NCH, 64]))
            aend = X4[:, :].rearrange("p (c j) -> p c j", j=64)[:, :, 0]
            P.op("pool", "memset", [], [aprev], aprev[:], 0.0)
            P.op("dve", "tensor_copy", [X4], [aprev], out=aprev[0:32, 1:NCH], in_=aend[0:32, 0:NCH - 1])
            P.op("dve", "tensor_copy", [X4], [aprev], out=aprev[32:64, 0:3], in_=aend[32:64, 1:4])
            P.op("dve", "tensor_copy", [X4], [aprev], out=aprev[32:64, 4:NCH - 1], in_=aend[32:64, 5:NCH])
            P.op("dve", "tensor_copy", [X4], [aprev], out=aprev[32:64, NCH - 1:NCH], in_=aend[32:64, 0:1])
            P.op("dve", "tensor_tensor", [aprev, X4], [dec], out=dec[:], in0=aprev[:], in1=aend, op=ALU.subtract)
            P.op("act", "activation", [dec], [dec], out=dec[:], in_=dec[:], func=AF.Exp)
            P.op("dve", "tensor_tensor", [X3, X4], [X3], out=X3[:], in0=X3[:], in1=X4[:], op=ALU.subtract)
            P.op("act", "activation", [X3], [X3], out=X3[:], in_=X3[:], func=AF.Exp)
            P.op("dve", "tensor_tensor", [X2, X4], [X2], out=X2[:], in0=X2[:], in1=X4[:], op=ALU.subtract)
            P.op("act", "activation", [X2], [X2], out=X2[:], in_=X2[:], func=AF.Exp)
            for (srcX, dstT) in ((X3, esT), (X2, fT)):
                for b0 in range(0, NB, 8):
                    nb = min(8, NB - b0)
                    ps = pst[(b0 // 8) % 2]
                    for bb in range(nb):
                        P.op("pe", "transpose", [srcX, ident_f], [ps], ps[:, bb, :], srcX[:, (b0 + bb) * 128:(b0 + bb + 1) * 128],
                             ident_f[0:64, 0:64])
                    P.op("act", "copy", [ps], [dstT], out=dstT[:, b0:b0 + nb, :], in_=ps[:, 0:nb, :])
            for idx in range(8):
                r = (idx // 4) * 32 + idx % 4
                P.op("dve", "tensor_copy", [ident_f], [sel], out=sel[:], in_=ident_f[0:64, r:r + 1].to_broadcast([64, 128]))
                P.op("pe", "matmul", [sel, dec], [psd], psd[:, :], sel[:, :], dec[:, :], start=True, stop=True)
                P.op("act", "copy", [psd], [decbc], out=decbc[:, idx, :], in_=psd[:, :])
            P.barrier()
        P.op("dve", "tensor_scalar", [esT], [esT], out=esT[:], in0=esT[:], scalar1=128.0 ** -0.5, scalar2=None, op0=ALU.mult)
        xraw = A.sb([128, T], F32, "xraw")
        cvw = A.sb([128, 8, 4], F32, "cvw")
        P.dma("sp", cvw[:], IN["l0_convT"][:, :, :], writes=[cvw])
        dg = [A.sb([128, 3, 128], F32, "dg%d" % i) for i in range(2)]
        qT = A.sb([128, T], BF16, "mqT")
        qd = [A.sb([128, T], BF16, "mqd%d" % d) for d in range(2)]
        kT = A.sb([128, T], BF16, "mkT")
        kTok = A.sb([128, NB, 128], BF16, "kTok")
        vtok = A.sb([128, NB, 128], BF16, "vtok")
        vpp = [A.sb([128, NB, 129], BF16, "vpp%d" % d) for d in range(2)]
        SmT = [A.sb([128, NB, 64], BF16, "SmT%d" % d) for d in range(2)]
        hbuf = [A.sb([128, NB, 129], F32, "hbuf%d" % d) for d in range(2)]
        Cst = [[A.sb([128, 129], F32, "C%d_%d" % (d, i)) for i in range(2)] for d in range(2)]
        Cb = [[A.sb([128, 129], BF16, "Cb%d_%d" % (d, i)) for i in range(2)] for d in range(2)]
        dn = [A.sb([128, NB], F32, "dn%d" % d) for d in range(2)]
        pcv = [A.ps([128, 512], F32, "pcv%d" % i) for i in range(2)]
        pU = [A.ps([128, 129], F32, "pU%d" % i) for i in range(2)]
        pN = [[A.ps([128, 129], F32, "pN%d_%d" % (d, i)) for i in range(2)] for d in range(2)]
        order = [list(range(NCH)), [3, 2, 1, 0] + list(range(NCH - 1, 3, -1))]
        pieces = [(0, TC)] + [(TC + 512 * i, TC + 512 * (i + 1)) for i in range(8)]
        pc = 0
        for h in range(4):
            for which in range(2):
                ch = which * 4 + h
                dg_ = dg[which]
                P.dma("sp" if which == 0 else "pool", xraw[:], SC["MQK"][ch * 128:(ch + 1) * 128, :], writes=[xraw])
                for j in range(3):
                    P.op("dve", "tensor_scalar", [ident_f, cvw], [dg_], out=dg_[:, j, :], in0=ident_f[:], scalar1=cvw[:, ch, j:j + 1],
                         scalar2=None, op0=ALU.mult)
                dst = qT if which == 0 else kT
                for (a, b) in pieces:
                    s0, s1 = (0, TC) if a < TC else (TC, T)
                    ps = pcv[pc % 2]
                    pc += 1
                    P.op("pe", "matmul", [dg_, xraw], [ps], ps[:, 0:b - a], dg_[:, 1, :], xraw[:, a:b], start=True, stop=False)
                    lo = max(a, s0 + 1)
                    P.op("pe", "matmul", [dg_, xraw], [ps], ps[:, lo - a:b - a], dg_[:, 0, :], xraw[:, lo - 1:b - 1], start=False,
                         stop=False)
                    hi = min(b, s1 - 1)
                    P.op("pe", "matmul", [dg_, xraw], [ps], ps[:, 0:hi - a], dg_[:, 2, :], xraw[:, a + 1:hi + 1], start=False,
                         stop=True)
                    P.op("act", "activation", [ps, cvw], [dst], out=dst[:, a:b], in_=ps[:, 0:b - a], func=AF.Silu,
                         bias=cvw[:, ch, 3:4])
            for b0 in range(0, NB, 4):
                nb = min(4, NB - b0)
                ps = pcv[pc % 2]
                pc += 1
                psb = ps[:, 0:256].bitcast(BF16)
                for bb in range(nb):
                    P.op("pe", "transpose", [kT, ident_bf], [ps], psb[:, bb * 128:(bb + 1) * 128],
                         kT[:, (b0 + bb) * 128:(b0 + bb + 1) * 128], ident_bf[:])
                P.op("act", "copy", [ps], [kTok], out=kTok[:, b0:b0 + nb, :],
                     in_=psb[:, 0:nb * 128].rearrange("p (b j) -> p b j", j=128))
            P.dma("sp", vtok[:], SC["MV"][:, h * 128:(h + 1) * 128].rearrange("(b p) j -> p b j", p=128), writes=[vtok])
            for d in range(2):
                col = d * 32 + h
                idx = d * 4 + h
                P.op("pool" if d == 0 else "dve", "tensor_tensor", [vtok, esT], [vpp[d]], out=vpp[d][:, :, 0:128], in0=vtok[:],
                     in1=esT[:, :, col:col + 1].to_broadcast([128, NB, 128]), op=ALU.mult)
                P.op("dve", "tensor_copy", [esT], [vpp[d]], out=vpp[d][:, :, 128:129], in_=esT[:, :, col:col + 1])
                P.op("pool" if d == 1 else "dve", "tensor_tensor", [qT, decbc], [qd[d]],
                     out=qd[d][:, :].rearrange("p (c j) -> p c j", j=64), in0=qT[:, :].rearrange("p (c j) -> p c j", j=64),
                     in1=decbc[:, idx, :].unsqueeze(2).to_broadcast([128, NCH, 64]), op=ALU.mult)
            for b0 in range(0, NB, 4):
                nb = min(4, NB - b0)
                ps = pcv[pc % 2]
                pc += 1
                psv = ps[:, 0:256].rearrange("p (b j) -> p b j", j=64)
                for bb in range(nb):
                    for half in range(2):
                        c = (b0 + bb) * 2 + half
                        pr = slice(half * 64, half * 64 + 64)
                        P.op("pe", "matmul", [kT, qT], [ps], psv[pr, bb, :], kT[:, c * 64:(c + 1) * 64], qT[:, c * 64:(c + 1) * 64],
                             start=True, stop=True)
                for d in range(2):
                    P.op("dve", "tensor_tensor", [ps, mask[d]], [SmT[d]], out=SmT[d][:, b0:b0 + nb, :], in0=psv[:, 0:nb, :],
                         in1=mask[d][:].unsqueeze(1).to_broadcast([128, nb, 64]), op=ALU.mult)
            for d in range(2):
                P.op("pool", "memset", [], [Cst[d][0]], Cst[d][0][:], 0.0)
                P.op("pool", "memset", [], [Cb[d][0]], Cb[d][0][:], 0.0)
            for i in range(NCH):
                for d in range(2):
                    c = order[d][i]
                    b, half = c // 2, c % 2
                    pr = slice(half * 64, half * 64 + 64)
                    idx = d * 4 + h
                    Cold, Cnew = Cst[d][i % 2], Cst[d][(i + 1) % 2]
                    cbo, cbn = Cb[d][i % 2], Cb[d][(i + 1) % 2]
                    U = pU[d]
                    N = pN[d][i % 2]
                    P.op("pe", "matmul", [kTok, vpp[d]], [U], U[:, :], kTok[pr, b, :], vpp[d][pr, b, :], start=True, stop=True)
                    P.op("pe", "matmul", [SmT[d], vpp[d]], [N], N[pr, :], SmT[d][pr, b, :], vpp[d][pr, b, :], start=True,
                         stop=False)
                    P.op("pe", "matmul", [qd[d], cbo], [N], N[pr, :], qd[d][:, c * 64:(c + 1) * 64], cbo[:], start=False, stop=True)
                    P.op("dve", "scalar_tensor_tensor", [Cold, decbc, U], [cbn], out=cbn[:], in0=Cold[:],
                         scalar=decbc[:, idx, c:c + 1], in1=U[:, :], op0=ALU.mult, op1=ALU.add)
                    P.op("dve", "scalar_tensor_tensor", [Cold, decbc, U], [Cnew], out=Cnew[:], in0=Cold[:],
                         scalar=decbc[:, idx, c:c + 1], in1=U[:, :], op0=ALU.mult, op1=ALU.add)
                    P.op("act", "copy", [N], [hbuf[d]], out=hbuf[d][pr, b, :], in_=N[pr, :])
            for d in range(2):
                col = d * 32 + h
                P.op("act", "activation", [hbuf[d]], [dn[d]], out=dn[d][:, :].unsqueeze(2), in_=hbuf[d][:, :, 128:129], func=AF.Abs)
                P.op("dve", "tensor_tensor", [dn[d], fT], [dn[d]], out=dn[d][:, :].unsqueeze(2), in0=dn[d][:, :].unsqueeze(2),
                     in1=fT[:, :, col:col + 1], op=ALU.max)
                P.op("dve", "reciprocal", [dn[d]], [dn[d]], out=dn[d][:], in_=dn[d][:])
                P.op("dve" if d == 0 else "pool", "tensor_tensor", [hbuf[d], dn[d]], [hbuf[d]], out=hbuf[d][:, :, 0:128],
                     in0=hbuf[d][:, :, 0:128], in1=dn[d][:, :].unsqueeze(2).to_broadcast([128, NB, 128]), op=ALU.mult)
                P.dma("sp" if d == 0 else "pool", SC["HM"][d, :, h * 128:(h + 1) * 128].rearrange("(b p) j -> p b j", p=128),
                      hbuf[d][:, :, 0:128], reads=[hbuf[d]])
        P.barrier()


def combine_phase(nc, P, IN, SC, ident_bf, src0, src1, mul, norm_name, cat_row0, groups):
    with ExitStack() as es:
        A = Alloc(nc, es)
        nrow = A.sb([128, 512], F32, "nrow")
        P.dma("sp", nrow[:], IN[norm_name][0:1, :].to_broadcast([128, 512]), writes=[nrow])
        epsT = A.sb([128, 1], F32, "epsTc")
        P.op("pool", "memset", [], [epsT], epsT[:], EPS)
        a_ = [A.sb([128, 512], F32, "cA%d" % i) for i in range(4)]
        b_ = [A.sb([128, 512], F32, "cB%d" % i) for i in range(4)]
        m_ = [A.sb([128, 512], BF16, "cM%d" % i) for i in range(4)]
        junk = A.sb([128, 128], F32, "cjunk")
        ss = [A.sb([128, 4], F32, "css%d" % i) for i in range(4)]
        hn = [A.sb([128, 512], F32, "chn%d" % i) for i in range(4)]
        hb = [A.sb([128, 512], BF16, "chb%d" % i) for i in range(4)]
        sz = [A.sb([128, 512], BF16, "csz%d" % i) for i in range(2)]
        oo = [A.sb([128, 512], BF16, "coo%d" % i) for i in range(2)]
        tp = [A.ps([128, 512], BF16, "ctp%d" % i) for i in range(8)]
        tiles = []
        for gi, (t0, n) in enumerate(groups):
            for ti in range(n // 128):
                tiles.append((gi, t0, n, ti))

        def stage1(it):
            gi, t0, n, ti = tiles[it]
            tok = t0 + ti * 128
            a, b, m = a_[it % 4], b_[it % 4], m_[it % 4]
            P.dma("sp", a[:], src0[tok:tok + 128, :], writes=[a])
            P.dma("pool", b[:], src1[tok:tok + 128, :], writes=[b])
            if mul is not None:
                P.dma("sp", m[:], mul[tok:tok + 128, :], writes=[m])
            P.op("dve", "tensor_tensor", [a, b], [a], out=a[:], in0=a[:], in1=b[:], op=ALU.add)
            if mul is not None:
                P.op("pool", "tensor_tensor", [a, m], [a], out=a[:], in0=a[:], in1=m[:], op=ALU.mult)

        def stage2(it):
            gi, t0, n, ti = tiles[it]
            a, s_, hb_ = a_[it % 4], ss[it % 4], hb[it % 4]
            for hh in range(4):
                P.op("act", "activation", [a], [junk, s_], out=junk[:], in_=a[:, hh * 128:(hh + 1) * 128], func=AF.Square,
                     accum_out=s_[:, hh:hh + 1])
            P.op("act", "activation", [s_, epsT], [s_], out=s_[:], in_=s_[:], func=AF.Sqrt, scale=1.0 / 128, bias=epsT[:, 0:1])
            P.op("dve", "reciprocal", [s_], [s_], out=s_[:], in_=s_[:])
            for hh in range(4):
                sl = slice(hh * 128, (hh + 1) * 128)
                P.op("dve", "scalar_tensor_tensor", [a, s_, nrow], [hb_], out=hb_[:, sl], in0=a[:, sl], scalar=s_[:, hh:hh + 1],
                     in1=nrow[:, sl], op0=ALU.mult, op1=ALU.mult)
            for j in range(4):
                tpj = tp[(gi % 2) * 4 + j]
                P.op("pe", "transpose", [hb_, ident_bf], [tpj], tpj[:, ti * 128:(ti + 1) * 128],
                     hb_[:, j * 128:(j + 1) * 128], ident_bf[:])
            if ti == n // 128 - 1:
                for j in range(4):
                    tpj = tp[(gi % 2) * 4 + j]
                    r0 = cat_row0 + j * 128
                    z_, o_ = sz[j % 2], oo[j % 2]
                    P.dma("sp", z_[:, 0:n], SC["SZT"][r0:r0 + 128, t0:t0 + n], writes=[z_])
                    P.op("dve", "tensor_tensor", [tpj, z_], [o_], out=o_[:, 0:n], in0=tpj[:, 0:n], in1=z_[:, 0:n], op=ALU.mult)
                    P.dma("pool", SC["CATT"][r0:r0 + 128, t0:t0 + n], o_[:, 0:n], reads=[o_])

        stage1(0)
        for it in range(len(tiles)):
            if it + 1 < len(tiles):
                stage1(it + 1)
            stage2(it)
        P.barrier()


def phase_C(nc, P, IN, SC, layer, gateR, out_ap):
    with ExitStack() as es:
        A = Alloc(nc, es)
        w = A.sb([128, 8, 1024], BF16, "w_out")
        with ExitStack() as es2:
            A2 = Alloc(nc, es2)
            stg = [A2.sb([128, 8, 512], F32, "wstg%d" % i) for i in range(2)]
            for i in range(2):
                P.dma("sp" if i == 0 else "pool", stg[i][:],
                      IN["l%d_w_out" % layer][:, i * 512:(i + 1) * 512].rearrange("(k p) n -> p k n", p=128), writes=[stg[i]])
                P.op("dve" if i == 0 else "act", "tensor_copy" if i == 0 else "copy", [stg[i]], [w], out=w[:, :, i * 512:(i + 1) * 512],
                     in_=stg[i][:])
            P.barrier()
        cat = [A.sb([128, 8, 512], BF16, "catT%d" % i) for i in range(3)]
        hold = [A.sb([128, 1024], F32, "hold%d" % i) for i in range(4)]
        tmp = [A.sb([128, 1024], F32, "ctmp%d" % i) for i in range(4)]
        hnew = [A.sb([128, 1024], F32, "hnew%d" % i) for i in range(4)]
        ps = [A.ps([128, 512], F32, "yps%d" % i) for i in range(4)]
        if layer == 1:
            frow = A.sb([128, 1024], F32, "frow")
            P.dma("sp", frow[:], IN["final_norm"][0:1, :].to_broadcast([128, 1024]), writes=[frow])
            epsT = A.sb([128, 1], F32, "epsTf")
            P.op("pool", "memset", [], [epsT], epsT[:], EPS)
            junk = A.sb([128, 1024], F32, "fjunk")
            st = [A.sb([128, 1], F32, "fst%d" % i) for i in range(4)]
            ob = [A.sb([128, 1024], F32, "fob%d" % i) for i in range(4)]
        it = 0
        groups = GROUPS if layer == 0 else GROUPS[1:]
        def load_cat(gi):
            t0, n = groups[gi]
            c_ = cat[gi % 3]
            for k2 in range(2):
                P.dma("sp", c_[:, k2 * 4:(k2 + 1) * 4, 0:n],
                      SC["CATT"][k2 * 512:(k2 + 1) * 512, t0:t0 + n].rearrange("(k p) t -> p k t", p=128), writes=[c_])

        load_cat(0)
        for gi, (t0, n) in enumerate(groups):
            c_ = cat[gi % 3]
            if gi + 1 < len(groups):
                load_cat(gi + 1)
            g_ = gateR[1] if (layer == 0 and t0 == 0) else gateR[0]
            for ti in range(n // 128):
                tok = t0 + ti * 128
                ho, tm, hn_ = hold[it % 4], tmp[it % 4], hnew[it % 4]
                if layer == 0:
                    srcp = IN["ctx"][tok:tok + 128, :] if t0 == 0 else IN["x"][tok - TC:tok - TC + 128, :]
                else:
                    srcp = SC["H1"][tok:tok + 128, :]
                P.dma("sp", ho[:], srcp, writes=[ho])
                for half in range(2):
                    p_ = ps[(it * 2 + half) % 4]
                    for k in range(8):
                        P.op("pe", "matmul", [c_, w], [p_], p_[:, :], c_[:, k, ti * 128:(ti + 1) * 128],
                             w[:, k, half * 512:(half + 1) * 512], start=(k == 0), stop=(k == 7))
                    P.op("dve", "tensor_tensor", [p_, g_], [tm], out=tm[:, half * 512:(half + 1) * 512], in0=p_[:, :],
                         in1=g_[:, half * 512:(half + 1) * 512], op=ALU.mult)
                P.op("pool", "tensor_tensor", [tm, ho], [hn_], out=hn_[:], in0=tm[:], in1=ho[:], op=ALU.add)
                if layer == 0:
                    P.dma("pool", SC["H1"][tok:tok + 128, :], hn_[:], reads=[hn_])
                else:
                    s_, o_ = st[it % 4], ob[it % 4]
                    P.op("act", "activation", [hn_], [junk, s_], out=junk[:], in_=hn_[:], func=AF.Square, accum_out=s_[:, 0:1])
                    P.op("act", "activation", [s_, epsT], [s_], out=s_[:], in_=s_[:], func=AF.Sqrt, scale=1.0 / D, bias=epsT[:, 0:1])
                    P.op("dve", "reciprocal", [s_], [s_], out=s_[:], in_=s_[:])
                    P.op("act", "activation", [hn_, s_], [o_], out=o_[:], in_=hn_[:], func=AF.Copy, scale=s_[:, 0:1])
                    P.op("dve", "tensor_tensor", [o_, frow], [o_], out=o_[:], in0=o_[:], in1=frow[:], op=ALU.mult)
                    P.dma("pool", out_ap[tok - TC:tok - TC + 128, :], o_[:], reads=[o_])
                it += 1
        P.barrier()


def gla_phase(nc, P, IN, SC, ident_bf):
    NB = NT
    NCH = T // 64
    with ExitStack() as es:
        A = Alloc(nc, es)
        mask = [A.sb([128, 64], F32, "gmask%d" % d) for d in range(2)]
        for d in range(2):
            P.op("pool", "memset", [], [mask[d]], mask[d][:], 1.0)
            for half in range(2):
                pr = slice(half * 64, half * 64 + 64)
                P.op("pool", "affine_select", [mask[d]], [mask[d]], out=mask[d][pr, :], in_=mask[d][pr, :],
                     pattern=[[1 if d == 0 else -1, 64]], compare_op=ALU.is_ge, fill=0.0, base=0,
                     channel_multiplier=-1 if d == 0 else 1)
        rm = [A.sb([64, T], F32, "rm%d" % d) for d in range(2)]
        for d in range(2):
            P.op("pool", "memset", [], [rm[d]], rm[d][:], 1.0)
            j0 = 0 if d == 0 else 63
            P.op("pool", "memset", [rm[d]], [rm[d]], rm[d][:, :].rearrange("p (c j) -> p c j", j=64)[:, :, j0:j0 + 1], 0.0)
        qf = A.sb([64, T], F32, "gqf")
        kf = A.sb([64, T], F32, "gkf")
        lg = A.sb([64, T], F32, "glg")
        bc = A.sb([64, T], F32, "gbc")
        tmp = A.sb([64, T], F32, "gtmp")
        qg = [A.sb([64, T], BF16, "qg%d" % d) for d in range(2)]
        kg = [A.sb([64, T], BF16, "kg%d" % d) for d in range(2)]
        kbf = A.sb([64, T], BF16, "kbf")
        eb = [A.sb([64, NCH], F32, "eb%d" % d) for d in range(2)]
        kbTok = [A.sb([128, NB, 64], BF16, "kbTok%d" % d) for d in range(2)]
        SmT = [A.sb([128, NB, 64], BF16, "gSmT%d" % d) for d in range(2)]
        vtok = A.sb([128, NB, 128], BF16, "gvtok")
        obuf = [[A.sb([128, 128], F32, "gob%d_%d" % (d, i)) for i in range(3)] for d in range(2)]
        Sst = [[A.sb([64, 128], F32, "S%d_%d" % (d, i)) for i in range(2)] for d in range(2)]
        Sb = [[A.sb([64, 128], BF16, "Sb%d_%d" % (d, i)) for i in range(2)] for d in range(2)]
        ptr = [A.ps([128, 8, 64], BF16, "gptr%d" % i) for i in range(2)]
        pS = [A.ps([128, 8, 64], F32, "gpS%d" % i) for i in range(2)]
        pU = [A.ps([128, 128], F32, "gpU%d" % i) for i in range(2)]
        pN = [A.ps([128, 128], F32, "gpN%d" % i) for i in range(2)]
        order = [list(range(NCH)), [3, 2, 1, 0] + list(range(NCH - 1, 3, -1))]
        for h in range(4):
            P.dma("sp", qf[:], SC["MQK"][h * 64:(h + 1) * 64, :], writes=[qf])
            P.dma("pool", kf[:], SC["MQK"][256 + h * 64:256 + (h + 1) * 64, :], writes=[kf])
            P.dma("sp", vtok[:], SC["MV"][:, h * 128:(h + 1) * 128].rearrange("(b p) j -> p b j", p=128), writes=[vtok])
            for d in range(2):
                P.dma("pool", lg[:], SC["LG"][d, h * 64:(h + 1) * 64, :], writes=[lg])
                if d == 0:
                    P.op("dve", "tensor_tensor_scan", [rm[d], lg], [bc], out=bc[:, :], data0=rm[d][:, :], data1=lg[:, :],
                         initial=0.0, op0=ALU.mult, op1=ALU.add)
                else:
                    P.op("dve", "tensor_tensor_scan", [rm[d], lg], [bc], out=bc[:, ::-1], data0=rm[d][:, ::-1],
                         data1=lg[:, ::-1], initial=0.0, op0=ALU.mult, op1=ALU.add)
                jl = 63 if d == 0 else 0
                bl = bc[:, :].rearrange("p (c j) -> p c j", j=64)[:, :, jl:jl + 1]
                P.op("act", "activation", [bc], [eb[d]], out=eb[d][:, :].unsqueeze(2), in_=bl, func=AF.Exp)
                P.op("act", "activation", [bc], [tmp], out=tmp[:], in_=bc[:], func=AF.Exp)
                P.op("dve", "scalar_tensor_tensor", [qf, tmp], [qg[d]], out=qg[d][:], in0=qf[:], scalar=64.0 ** -0.5, in1=tmp[:],
                     op0=ALU.mult, op1=ALU.mult)
                P.op("act", "activation", [bc], [tmp], out=tmp[:], in_=bc[:], func=AF.Exp, scale=-1.0)
                P.op("dve", "tensor_tensor", [kf, tmp], [kg[d]], out=kg[d][:], in0=kf[:], in1=tmp[:], op=ALU.mult)
                P.op("dve", "tensor_tensor", [bc], [tmp], out=tmp[:, :].rearrange("p (c j) -> p c j", j=64),
                     in0=bl.to_broadcast([64, NCH, 64]), in1=bc[:, :].rearrange("p (c j) -> p c j", j=64), op=ALU.subtract)
                P.op("act", "activation", [tmp], [tmp], out=tmp[:], in_=tmp[:], func=AF.Exp)
                P.op("dve", "tensor_tensor", [kf, tmp], [kbf], out=kbf[:], in0=kf[:], in1=tmp[:], op=ALU.mult)
                for b0 in range(0, NB, 8):
                    nb = min(8, NB - b0)
                    ps = ptr[(b0 // 8) % 2]
                    for bb in range(nb):
                        P.op("pe", "transpose", [kbf, ident_bf], [ps], ps[:, bb, :], kbf[:, (b0 + bb) * 128:(b0 + bb + 1) * 128],
                             ident_bf[0:64, 0:64])
                    P.op("act", "copy", [ps], [kbTok[d]], out=kbTok[d][:, b0:b0 + nb, :], in_=ps[:, 0:nb, :])
                for b0 in range(0, NB, 8):
                    nb = min(8, NB - b0)
                    ps = pS[(b0 // 8) % 2]
                    for bb in range(nb):
                        for half in range(2):
                            c = (b0 + bb) * 2 + half
                            pr = slice(half * 64, half * 64 + 64)
                            P.op("pe", "matmul", [kg[d], qg[d]], [ps], ps[pr, bb, :], kg[d][:, c * 64:(c + 1) * 64],
                                 qg[d][:, c * 64:(c + 1) * 64], start=True, stop=True)
                    P.op("dve", "tensor_tensor", [ps, mask[d]], [SmT[d]], out=SmT[d][:, b0:b0 + nb, :], in0=ps[:, 0:nb, :],
                         in1=mask[d][:].unsqueeze(1).to_broadcast([128, nb, 64]), op=ALU.mult)
            for d in range(2):
                P.op("pool", "memset", [], [Sst[d][0]], Sst[d][0][:], 0.0)
                P.op("pool", "memset", [], [Sb[d][0]], Sb[d][0][:], 0.0)
            for i in range(NCH):
                for d in range(2):
                    c = order[d][i]
                    b, half = c // 2, c % 2
                    pr = slice(half * 64, half * 64 + 64)
                    Sold, Snew = Sst[d][i % 2], Sst[d][(i + 1) % 2]
                    sbo, sbn = Sb[d][i % 2], Sb[d][(i + 1) % 2]
                    U, N = pU[d], pN[d]
                    ob = obuf[d][(i // 2) % 3]
                    P.op("pe", "matmul", [SmT[d], vtok], [N], N[pr, :], SmT[d][pr, b, :], vtok[pr, b, :], start=True, stop=False)
                    P.op("pe", "matmul", [qg[d], sbo], [N], N[pr, :], qg[d][:, c * 64:(c + 1) * 64], sbo[:, :], start=False,
                         stop=True)
                    P.op("pe", "matmul", [kbTok[d], vtok], [U], U[0:64, :], kbTok[d][pr, b, :], vtok[pr, b, :], start=True,
                         stop=True)
                    P.op("dve", "scalar_tensor_tensor", [Sold, eb[d], U], [sbn], out=sbn[:], in0=Sold[:],
                         scalar=eb[d][:, c:c + 1], in1=U[0:64, :], op0=ALU.mult, op1=ALU.add)
                    P.op("dve", "scalar_tensor_tensor", [Sold, eb[d], U], [Snew], out=Snew[:], in0=Sold[:],
                         scalar=eb[d][:, c:c + 1], in1=U[0:64, :], op0=ALU.mult, op1=ALU.add)
                    P.op("act", "copy", [N], [ob], out=ob[pr, :], in_=N[pr, :])
                    if i % 2 == 1:
                        P.dma("sp" if d == 0 else "pool", SC["HM"][d, b * 128:(b + 1) * 128, h * 128:(h + 1) * 128], ob[:],
                              reads=[ob])
        P.barrier()


def _na_rows_ok(qr, kr):
    lo = min(max(qr - 4, 0), 56)
    return lo <= kr < lo + 8


def _na_cfg(g):
    if g == 0:
        return "first", 0, 6
    if g == 7:
        return "last", 26, 6
    return "mid", 4 * g - 2, 8


def _na_range(g, ktl):
    js = [j for j in range(8) for i in range(2) if _na_rows_ok(8 * g + j, 2 * ktl + i)]
    return min(js), max(js)


def na_bias_fn(nc, P, IN, state):
    def fn(h, A):
        if A is not None:
            state.setdefault("sets", {})
            for key, ntile in (("first", 6), ("mid", 8), ("last", 6)):
                state["sets"][(key, h % 2)] = [A.sb([128, 512], F32, "nab_%s%d_%d" % (key, i, h % 2)) for i in range(ntile)]
        out = {}

        def loader():
            for key, g, t_lo in (("first", 0, 0), ("mid", 1, 2), ("last", 7, 26)):
                tiles = state["sets"][(key, h % 2)]
                for r, bt in enumerate(tiles):
                    ktl = t_lo + r
                    u0, u1 = _na_range(g, ktl)
                    P.op("pool", "memset", [], [bt], bt[:, u0 * 64:(u1 + 1) * 64], MASKV)
                    for i in range(2):
                        kr = 2 * ktl + i
                        js = [j for j in range(8) if _na_rows_ok(8 * g + j, kr)]
                        if not js:
                            continue
                        j0, j1 = js[0], js[-1]
                        assert js == list(range(j0, j1 + 1))
                        m0 = 7 - (kr - 8 * g - j0)
                        nj = j1 - j0 + 1
                        P.dma("sp" if i == 0 else "pool",
                              bt[i * 64:(i + 1) * 64, j0 * 64:(j1 + 1) * 64].rearrange("p (m q) -> p m q", q=64),
                              IN["na_bias"][h, m0:m0 + nj, :, :].rearrange("m k q -> k m q"), writes=[bt])
                    yield

        for g in range(8):
            key, t_lo, nt_ = _na_cfg(g)
            for r in range(nt_):
                out[(g, 2 + t_lo + r)] = state["sets"][(key, h % 2)][r]
        return out, loader()

    return fn


def _shapes(d):
    return {k: (v.shape, "bf16" if v.dtype == ml_dtypes.bfloat16 else "f32") for k, v in d.items()}


def run(inputs, stage=99, debug=(), cores=8, skip=()):
    inputs = {k: np.asarray(v) for k, v in inputs.items()}
    sh, per = prep_inputs(inputs)
    nc = build(_shapes(sh), _shapes(per[0]), stage=stage, debug=debug, skip=skip)
    in_maps = [dict(sh, **per[b]) for b in range(cores)]
    res = run_bass_kernel_spmd(nc, in_maps, core_ids=list(range(cores)))
    return res


def kernel(**inputs):
    res = run(inputs)
    return np.stack([np.asarray(r["out"], dtype=np.float32) for r in res.results], axis=0)
```

```python
import numpy as np
from contextlib import ExitStack
import ml_dtypes
import concourse.bass as bass
import concourse.mybir as mybir
from concourse.bass_utils import run_bass_kernel_spmd

F32 = mybir.dt.float32
BF16 = mybir.dt.bfloat16
AF = mybir.ActivationFunctionType
ALU = mybir.AluOpType
AX = mybir.AxisListType

D = 1024
TC = 256
TL = 4096
T = TC + TL
NT = T // 128
EPS = 1e-6
MASKV = -30000.0

GROUPS = [(0, 256)] + [(256 + 512 * i, 512) for i in range(8)]


class Dep:
    __slots__ = ("w", "r")

    def __init__(self):
        self.w = None
        self.r = {}


class Tile:
    def __init__(self, t):
        self.t = t
        self.d = Dep()

    def __getitem__(self, k):
        return self.t[k]


class DramDep:
    def __init__(self):
        self.d = Dep()


class Prog:
    def __init__(self, nc, es):
        self.nc = nc
        self.eng = {"pe": nc.tensor, "act": nc.scalar, "dve": nc.vector, "pool": nc.gpsimd, "sp": nc.sync}
        self.R = 12
        self.keys = [("pe", "c"), ("act", "c"), ("dve", "c"), ("pool", "c")]
        for q in ("sp", "pool"):
            self.keys += [(q, "d%d" % i) for i in range(self.R)]
        self.ndma = {"sp": 0, "pool": 0}
        self.sem = {k: es.enter_context(nc.semaphore("s_%s_%s" % k)) for k in self.keys}
        self.cnt = {k: 0 for k in self.keys}
        self.waited = {e: {} for e in self.eng}
        self.n = 0

    def _emit(self, eng, kind, fn, reads, writes):
        if kind == "d":
            kind = "d%d" % (self.ndma[eng] % self.R)
            self.ndma[eng] += 1
        key = (eng, kind)
        deps = {}
        if kind != "c" and self.cnt[key] > 0:
            deps[key] = self.cnt[key]

        def add(tok):
            if tok is None:
                return
            k, v = tok
            if deps.get(k, 0) < v:
                deps[k] = v

        for b in reads:
            add(b.d.w)
        for b in writes:
            add(b.d.w)
            for k, v in b.d.r.items():
                add((k, v))
        e = self.eng[eng]
        wd = self.waited[eng]
        for k, v in deps.items():
            if k == ("pe", "c") and eng == "pe":
                continue
            if wd.get(k, 0) >= v:
                continue
            e.wait_ge(self.sem[k], v)
            wd[k] = v
        inc = 16 if kind != "c" else 1
        self.cnt[key] += inc
        fn(e).then_inc(self.sem[key], inc)
        v = self.cnt[key]
        for b in reads:
            if b.d.r.get(key, 0) < v:
                b.d.r[key] = v
        for b in writes:
            b.d.w = (key, v)
            b.d.r = {}
        self.n += 1

    def op(self, eng, name, reads, writes, *a, **kw):
        self._emit(eng, "c", lambda e: getattr(e, name)(*a, **kw), reads, writes)

    def dma(self, q, out, in_, reads=(), writes=(), **kw):
        self._emit(q, "d", lambda e: e.dma_start(out=out, in_=in_, **kw), reads, writes)

    def barrier(self):
        for en, e in self.eng.items():
            wd = self.waited[en]
            for k in self.keys:
                v = self.cnt[k]
                if v > 0 and wd.get(k, 0) < v:
                    e.wait_ge(self.sem[k], v)
                    wd[k] = v


class Alloc:
    def __init__(self, nc, es):
        self.nc = nc
        self.es = es
        _CTR.setdefault(id(nc), 0)

    def _nm(self, name):
        _CTR[id(self.nc)] = _CTR.get(id(self.nc), 0) + 1
        return "%s_%d" % (name, _CTR[id(self.nc)])

    def sb(self, shape, dt, name=None):
        return Tile(self.es.enter_context(self.nc.sbuf_tensor(self._nm(name or "sb"), list(shape), dt)))

    def ps(self, shape, dt, name=None):
        return Tile(self.es.enter_context(self.nc.psum_tensor(self._nm(name or "ps"), list(shape), dt)))


_CTR = {}


def _fm(v, nchunk):
    return np.ascontiguousarray(v.reshape(nchunk, 128).T)


def _rope_perm():
    perm = np.zeros(32, np.int64)
    for i in range(32):
        r = i % 16
        perm[i] = i + 8 if r < 8 else i - 8
    return perm


def _rope_tables():
    t = np.arange(TL)
    inv = (1.0 / (10000.0 ** (np.arange(8, dtype=np.float32) / 8))).astype(np.float32)
    pos = [(t // 64).astype(np.float32), (t % 64).astype(np.float32)]
    C = np.zeros((32, TL), np.float32)
    S = np.zeros((32, TL), np.float32)
    for i in range(32):
        a = i // 16
        r = i % 16
        p = r % 8
        ang = (pos[a] * inv[p]).astype(np.float32)
        C[i] = np.cos(ang)
        S[i] = -np.sin(ang) if r < 8 else np.sin(ang)
    Cf = np.zeros((128, TL), np.float32)
    Sf = np.zeros((128, TL), np.float32)
    Cf[0:32] = C
    Cf[64:96] = C
    Sf[0:32] = S
    Sf[64:96] = S
    return Cf, Sf


def prep_inputs(inp):
    sh = {}
    sh["ident_bf"] = np.eye(128, dtype=np.float32).astype(ml_dtypes.bfloat16)
    sh["ident_f"] = np.eye(128, dtype=np.float32)
    perm = _rope_perm()
    w_in = inp["l0_w_in"]
    gi_cols = [2720 + d * 8 + h for d in range(2) for h in range(4)]
    gf_cols = [2720 + d * 8 + 4 + h for d in range(2) for h in range(4)]
    sh["l0_w_in"] = np.ascontiguousarray(
        np.concatenate([w_in, w_in[:, 640:672][:, perm], w_in[:, gi_cols], w_in[:, gf_cols]], axis=1))
    w_uq = inp["l0_mla_w_uq"].reshape(384, 8, 96)
    ext = np.concatenate([w_uq, w_uq[:, :, 0:64], w_uq[:, :, 64:96][:, :, perm]], axis=2)
    sh["l0_w_uq"] = np.ascontiguousarray(ext.reshape(384, 8 * 192))
    w_ukv = inp["l0_mla_w_ukv"].reshape(256, 8, 128)
    sh["l0_w_ukv"] = np.ascontiguousarray(
        np.concatenate([w_ukv[:, :, 0:64].reshape(256, 512), w_ukv[:, :, 64:128].reshape(256, 512)], axis=1))
    sh["l0_qnT"] = _fm(inp["l0_mla_q_norm"], 3)
    sh["l0_kvnT"] = _fm(inp["l0_mla_kv_norm"], 2)
    Cf, Sf = _rope_tables()
    sh["ropeC"] = Cf
    sh["ropeS"] = Sf
    cw = inp["l0_mlstm_conv_w"]
    sh["l0_convT"] = np.ascontiguousarray(
        np.concatenate([cw.reshape(3, 8, 128).transpose(2, 1, 0), inp["l0_mlstm_conv_b"].reshape(8, 128).T[:, :, None]],
                       axis=2))
    gb = np.zeros((16, 1), np.float32)
    for d in range(2):
        for h in range(4):
            gb[d * 8 + h, 0] = inp["l0_mlstm_b_i"][d, h]
            gb[d * 8 + 4 + h, 0] = inp["l0_mlstm_b_f"][d, h]
    sh["l0_gbias"] = gb
    gb2 = np.zeros((64, 2), np.float32)
    for d in range(2):
        for h in range(4):
            gb2[d * 32 + h, 0] = inp["l0_mlstm_b_i"][d, h]
            gb2[d * 32 + h, 1] = inp["l0_mlstm_b_f"][d, h]
    sh["l0_gb2"] = gb2
    sh["l0_hnorm"] = np.ascontiguousarray(inp["l0_mlstm_norm"].reshape(1, 512))
    sh["l0_w_out"] = inp["l0_w_out"]
    sh["l1_w_in"] = inp["l1_w_in"]
    sh["l1_w_gate"] = np.ascontiguousarray(inp["l1_gla_w_gate"])
    sh["l1_bgT"] = np.ascontiguousarray(inp["l1_gla_b_gate"].reshape(2, 2, 128).transpose(2, 0, 1))
    sh["l1_gnorm"] = np.ascontiguousarray(inp["l1_gla_norm"].reshape(1, 512))
    sh["l1_w_out"] = inp["l1_w_out"]
    sh["final_norm"] = np.ascontiguousarray(inp["final_norm"].reshape(1, 1024))
    rpb = inp["l1_na_rpb"]
    kc = np.arange(64)[:, None]
    qc = np.arange(64)[None, :]
    wc0 = np.clip(qc - 8, 0, 48)
    okc = (kc >= wc0) & (kc < wc0 + 16)
    dcol = np.clip(kc - qc + 15, 0, 30)
    Tb = np.full((8, 15, 64, 64), MASKV, np.float32)
    for m in range(15):
        dr = 7 - m
        blk = rpb[:, dr + 7][:, dcol]
        Tb[:, m] = np.where(okc[None], blk, np.float32(MASKV))
    sh["na_bias"] = Tb
    mods = [(inp["l0_norm"], inp["l0_w_mod"], inp["l0_b_mod"]), (inp["l1_norm"], inp["l1_w_mod"], inp["l1_b_mod"])]
    for l, (g_, wm_, bm_) in enumerate(mods):
        sh["l%d_w_mod" % l] = wm_
        sh["l%d_bmodT" % l] = _fm(bm_, 24)
        sh["l%d_bmod_gate" % l] = np.ascontiguousarray(bm_[2048:3072].reshape(1, 1024))
        sh["l%d_gT" % l] = _fm(g_, 8)
    per = []
    for b in range(8):
        d = {}
        d["x"] = inp["x"][b]
        d["ctx"] = inp["ctx"][b]
        cv = np.stack([inp["c"][b], inp["c_ctx"]], axis=1)
        d["cvec"] = np.ascontiguousarray(cv.reshape(8, 128, 2).transpose(1, 0, 2))
        per.append(d)
    return sh, per


def build(sh_shapes, per_shapes, stage=99, debug=(), skip=()):
    nc = bass.Bass("TRN2", target_bir_lowering=False)
    IN = {}
    for k, (shape, dt) in list(sh_shapes.items()) + list(per_shapes.items()):
        IN[k] = nc.dram_tensor(k, list(shape), BF16 if dt == "bf16" else F32, kind="ExternalInput").ap()
    out = nc.dram_tensor("out", [TL, D], F32, kind="ExternalOutput").ap()

    def scratch(name, shape, dt):
        kind = "ExternalOutput" if name in debug else "Internal"
        return nc.dram_tensor(name, list(shape), dt, kind=kind).ap()

    SC = {}
    SC["H1"] = scratch("H1", [T, D], F32)
    SC["SZT"] = scratch("SZT", [1024, T], BF16)
    SC["CATT"] = scratch("CATT", [1024, T], BF16)
    SC["QT"] = scratch("QT", [8, 96, T], BF16)
    SC["KT"] = scratch("KT", [8, 96, T], BF16)
    SC["V"] = scratch("V", [T, 512], BF16)
    SC["MQK"] = scratch("MQK", [1024, T], F32)
    SC["MQKB"] = scratch("MQKB", [1024, T], BF16)
    SC["GI"] = scratch("GI", [8, T], F32)
    SC["GF"] = scratch("GF", [8, T], F32)
    SC["MV"] = scratch("MV", [T, 512], BF16)
    SC["MO"] = scratch("MO", [T, 512], BF16)
    SC["HM"] = scratch("HM", [2, T, 512], F32)
    SC["RD"] = scratch("RD", [16, 512], F32)
    SC["LG"] = scratch("LG", [2, 256, T], F32)
    SC["NQ"] = scratch("NQ", [512, T], BF16)
    SC["NK"] = scratch("NK", [512, T], BF16)

    with ExitStack() as es0:
        P = Prog(nc, es0)
        A0 = Alloc(nc, es0)
        ident_bf = A0.sb([128, 128], BF16, "identbf")
        ident_f = A0.sb([128, 128], F32, "identf")
        ones_f = A0.sb([128, 128], F32, "onesf")
        P.dma("sp", ident_bf[:], IN["ident_bf"][:, :], writes=[ident_bf])
        P.dma("sp", ident_f[:], IN["ident_f"][:, :], writes=[ident_f])
        P.op("pool", "memset", [], [ones_f], ones_f[:], 1.0)
        affA = [A0.sb([128, 8, 2], F32, "affA%d" % l) for l in range(2)]
        affB = [A0.sb([128, 8, 2], F32, "affB%d" % l) for l in range(2)]
        gateR = [[A0.sb([128, 1024], F32, "gateR%d_%d" % (l, s)) for s in range(2 if l == 0 else 1)] for l in range(2)]

        esA0 = es0.enter_context(ExitStack())
        Aw0 = Alloc(nc, esA0)
        w0 = Aw0.sb([128, 8, 3808], BF16, "w_in0")
        w_uq0 = Aw0.sb([128, 3, 1536], BF16, "w_uq0")
        w_ukv0 = Aw0.sb([128, 2, 1024], BF16, "w_ukv0")
        stgA = [Aw0.sb([128, 1024], F32, "stgA%d" % i) for i in range(2)]

        def w0_loader():
            i = 0
            for c0 in range(0, 3808, 128):
                cw = min(128, 3808 - c0)
                s = stgA[i % 2]
                sv = s[:, :].rearrange("p (k n) -> p k n", k=8)
                P.dma("sp" if i % 2 == 0 else "pool", sv[:, :, 0:cw],
                      IN["l0_w_in"][:, c0:c0 + cw].rearrange("(k p) n -> p k n", p=128), writes=[s])
                P.op("dve" if i % 2 == 0 else "act", "tensor_copy" if i % 2 == 0 else "copy", [s], [w0],
                     out=w0[:, :, c0:c0 + cw], in_=sv[:, :, 0:cw])
                i += 1
                yield
            for kk in range(3):
                for hf in range(2):
                    s = stgA[i % 2]
                    P.dma("sp" if i % 2 == 0 else "pool", s[:, 0:768], IN["l0_w_uq"][kk * 128:(kk + 1) * 128, hf * 768:(hf + 1) * 768],
                          writes=[s])
                    P.op("dve" if i % 2 == 0 else "act", "tensor_copy" if i % 2 == 0 else "copy", [s], [w_uq0],
                         out=w_uq0[:, kk, hf * 768:(hf + 1) * 768], in_=s[:, 0:768])
                    i += 1
                    yield
            for kk in range(2):
                s = stgA[i % 2]
                P.dma("sp" if i % 2 == 0 else "pool", s[:, :], IN["l0_w_ukv"][kk * 128:(kk + 1) * 128, :], writes=[s])
                P.op("dve" if i % 2 == 0 else "act", "tensor_copy" if i % 2 == 0 else "copy", [s], [w_ukv0],
                     out=w_ukv0[:, kk, :], in_=s[:, :])
                i += 1
                yield

        wgen = w0_loader()
        with ExitStack() as es:
            A = Alloc(nc, es)
            cv = A.sb([128, 8, 2], F32, "cv")
            sc = A.sb([128, 8, 2], F32, "sc")
            screp = [A.sb([128, 8, 128], F32, "screp%d" % s) for s in range(2)]
            P.dma("sp", cv[:], IN["cvec"][:, :, :], writes=[cv])
            P.op("act", "activation", [cv], [sc], out=sc[:], in_=cv[:], func=AF.Silu)
            for s in range(2):
                for k in range(8):
                    P.op("dve", "tensor_copy", [sc], [screp[s]], out=screp[s][:, k, :],
                         in_=sc[:, k, s:s + 1].to_broadcast([128, 128]))
            wpan = [A.sb([128, 8, 384], F32, "wpan%d" % i) for i in range(2)]
            wgate = [A.sb([128, 512], F32, "wgate%d" % i) for i in range(3)]
            pm = A.ps([128, 24, 2], F32, "pm")
            pg = [A.ps([128, 512], F32, "pg%d" % i) for i in range(2)]
            bmT = A.sb([128, 24], F32, "bmT")
            gT = A.sb([128, 8], F32, "gT")
            modT = A.sb([128, 24, 2], F32, "modT")
            bgrow = A.sb([128, 1024], F32, "bgrow")
            for l in range(2):
                wm = IN["l%d_w_mod" % l]
                P.dma("sp", bmT[:], IN["l%d_bmodT" % l][:, :], writes=[bmT])
                P.dma("sp", gT[:], IN["l%d_gT" % l][:, :], writes=[gT])
                P.dma("sp", bgrow[:], IN["l%d_bmod_gate" % l][0:1, :].to_broadcast([128, 1024]), writes=[bgrow])
                for pn in range(8):
                    wp = wpan[pn % 2]
                    P.dma("sp" if pn % 2 == 0 else "pool", wp[:],
                          wm[:, pn * 384:(pn + 1) * 384].rearrange("(k p) n -> p k n", p=128), writes=[wp])
                    for j in range(3):
                        n = pn * 3 + j
                        for k in range(8):
                            P.op("pe", "matmul", [wp, sc], [pm], pm[:, n, :], wp[:, k, j * 128:(j + 1) * 128],
                                 sc[:, k, :], start=(k == 0), stop=(k == 7))
                    for _ in range(3):
                        next(wgen, None)
                P.op("dve", "tensor_tensor", [pm, bmT], [modT], out=modT[:], in0=pm[:],
                     in1=bmT[:].unsqueeze(2).to_broadcast([128, 24, 2]), op=ALU.add)
                P.op("dve", "tensor_scalar", [modT], [affA[l]], out=affA[l][:], in0=modT[:, 8:16, :], scalar1=1.0,
                     scalar2=None, op0=ALU.add)
                P.op("dve", "tensor_tensor", [affA[l], gT], [affA[l]], out=affA[l][:], in0=affA[l][:],
                     in1=gT[:].unsqueeze(2).to_broadcast([128, 8, 2]), op=ALU.mult)
                P.op("dve", "tensor_copy", [modT], [affB[l]], out=affB[l][:], in_=modT[:, 0:8, :])
                for s in range(len(gateR[l])):
                    for hf in range(2):
                        ps = pg[hf]
                        for k in range(8):
                            wg = wgate[(hf * 8 + k) % 3]
                            P.dma("sp" if k % 2 == 0 else "pool", wg[:],
                                  wm[k * 128:(k + 1) * 128, 2048 + hf * 512:2048 + (hf + 1) * 512], writes=[wg])
                            P.op("pe", "matmul", [wg, screp[s]], [ps], ps[:], screp[s][:, k, :], wg[:],
                                 start=(k == 0), stop=(k == 7))
                        P.op("dve", "tensor_tensor", [ps, bgrow], [gateR[l][s]],
                             out=gateR[l][s][:, hf * 512:(hf + 1) * 512], in0=ps[:],
                             in1=bgrow[:, hf * 512:(hf + 1) * 512], op=ALU.add)
            for _ in wgen:
                pass
            P.barrier()
        if stage <= 0:
            dbg = nc.dram_tensor("dbg_mod", [128, 2, 2, 8, 2], F32, kind="ExternalOutput").ap()
            dbg2 = nc.dram_tensor("dbg_gate", [128, 1024], F32, kind="ExternalOutput").ap()
            for l in range(2):
                P.dma("sp", dbg[:, l, 0], affA[l][:], reads=[affA[l]])
                P.dma("sp", dbg[:, l, 1], affB[l][:], reads=[affB[l]])
            P.dma("sp", dbg2[:, :], gateR[0][1][:], reads=[gateR[0][1]])
            P.barrier()
            return nc

        phase_A(nc, P, IN, SC, 0, affA[0], affB[0], ident_bf, ones_f, w_pre=(w0, w_uq0, w_ukv0))
        esA0.close()
        if stage <= 1:
            return nc
        if 2 not in skip:
            mla_groups = [(0, 256, [0, 1], 0)] + [(256 + 512 * g, 512, list(range(NT)), 0) for g in range(8)]
            attention(nc, P, SC, ones_f, 8, 96, 96.0 ** -0.5, lambda h: SC["QT"][h, :, :], lambda h: SC["KT"][h, :, :],
                      SC["V"], 0, mla_groups)
        if stage <= 2:
            return nc
        if 3 not in skip:
            mlstm_phase(nc, P, IN, SC, ident_bf, ident_f, ones_f)
        if stage <= 3:
            return nc
        combine_phase(nc, P, IN, SC, ident_bf, SC["HM"][0], SC["HM"][1], SC["MO"], "l0_hnorm", 512, GROUPS)
        if stage <= 4:
            return nc
        phase_C(nc, P, IN, SC, 0, gateR[0], out)
        if stage <= 5:
            return nc
        phase_A(nc, P, IN, SC, 1, affA[1], affB[1], ident_bf, ones_f)
        if stage <= 6:
            return nc
        if 7 not in skip:
            gla_phase(nc, P, IN, SC, ident_bf)
            combine_phase(nc, P, IN, SC, ident_bf, SC["HM"][0], SC["HM"][1], None, "l1_gnorm", 0, GROUPS[1:])
        if stage <= 7:
            return nc
        if 8 not in skip:
            na_groups = []
            for g in range(8):
                key, t_lo, nt_ = _na_cfg(g)
                loc = []
                for r in range(nt_):
                    u0, u1 = _na_range(g, t_lo + r)
                    loc.append((2 + t_lo + r, u0 * 64, (u1 + 1) * 64))
                na_groups.append((256 + 512 * g, 512, [0, 1] + loc, g))
            attention(nc, P, SC, ones_f, 8, 64, 64.0 ** -0.5, lambda h: SC["NQ"][h * 64:(h + 1) * 64, :],
                      lambda h: SC["NK"][h * 64:(h + 1) * 64, :], SC["V"], 512, na_groups, bias_fn=na_bias_fn(nc, P, IN, {}),
                      ident_bf=ident_bf, early_release=True, act_recip=True)
        if stage <= 8:
            return nc
        phase_C(nc, P, IN, SC, 1, gateR[1], out)
    return nc


def phase_A(nc, P, IN, SC, layer, affA, affB, ident_bf, ones_f, w_pre=None):
    NW = 3808 if layer == 0 else 3616
    w_in_d = IN["l%d_w_in" % layer]
    with ExitStack() as es:
        A = Alloc(nc, es)
        if w_pre is not None:
            w_in, w_uq, w_ukv = w_pre
        else:
            w_in = A.sb([128, 8, NW], BF16, "w_in")
            if layer == 0:
                w_uq = A.sb([128, 3, 1536], BF16, "w_uq")
                w_ukv = A.sb([128, 2, 1024], BF16, "w_ukv")
        with ExitStack() as es2:
            A2 = Alloc(nc, es2)
            stg = [A2.sb([128, 8, 512], F32, "stg%d" % i) for i in range(2)] if w_pre is None else None
            i = 0
            for c0 in (range(0, NW, 512) if w_pre is None else ()):
                cw = min(512, NW - c0)
                s = stg[i % 2]
                P.dma("sp" if i % 2 == 0 else "pool", s[:, :, 0:cw],
                      w_in_d[:, c0:c0 + cw].rearrange("(k p) n -> p k n", p=128), writes=[s])
                P.op("dve" if i % 2 == 0 else "act", "tensor_copy" if i % 2 == 0 else "copy", [s], [w_in],
                     out=w_in[:, :, c0:c0 + cw], in_=s[:, :, 0:cw])
                i += 1
            if layer == 0 and w_pre is None:
                s = stg[i % 2]
                for kk in range(3):
                    s = stg[i % 2]
                    P.dma("sp", s[:, 0:3, :], IN["l0_w_uq"][kk * 128:(kk + 1) * 128, :].rearrange("p (a n) -> p a n", a=3),
                          writes=[s])
                    P.op("dve", "tensor_copy", [s], [w_uq], out=w_uq[:, kk, :].rearrange("p (a n) -> p a n", a=3),
                         in_=s[:, 0:3, :])
                    i += 1
                s = stg[i % 2]
                for kk in range(2):
                    P.dma("sp", s[:, 2 * kk:2 * kk + 2, :],
                          IN["l0_w_ukv"][kk * 128:(kk + 1) * 128, :].rearrange("p (a n) -> p a n", a=2), writes=[s])
                P.op("dve", "tensor_copy", [s], [w_ukv], out=w_ukv[:].rearrange("p k (a n) -> p (k a) n", a=2),
                     in_=s[:, 0:4, :])
                i += 1
            P.barrier()
        if layer == 0:
            qnT = A.sb([128, 3], F32, "qnT")
            kvnT = A.sb([128, 2], F32, "kvnT")
            P.dma("sp", qnT[:], IN["l0_qnT"][:, :], writes=[qnT])
            P.dma("sp", kvnT[:], IN["l0_kvnT"][:, :], writes=[kvnT])
            cqT = A.sb([128, 3, 512], F32, "cqT")
            ckvT = A.sb([128, 2, 512], F32, "ckvT")
            sq = A.sb([128, 3, 512], F32, "sq")
            rstd = A.sb([128, 512], F32, "rstd")
            cqn = A.sb([128, 3, 512], BF16, "cqn")
            ckvn = A.sb([128, 2, 512], BF16, "ckvn")
            rC = A.sb([128, 512], F32, "rC")
            rS = A.sb([128, 512], F32, "rS")
            rt1 = A.sb([128, 512], F32, "rt1")
            rt2 = A.sb([128, 512], F32, "rt2")
            qo = [A.sb([128, 512], BF16, "qo%d" % i) for i in range(2)]
            kro = A.sb([32, 512], BF16, "kro")
        else:
            gaT = [A.sb([16, 512], F32, "gaT%d" % d) for d in range(2)]
            wg = A.sb([16, 2, 256], F32, "wg")
            P.dma("sp", wg[:], IN["l1_w_gate"].rearrange("d r k -> r d k"), writes=[wg])
            bgT = A.sb([128, 2, 2], F32, "bgT")
            nbg = A.sb([128, 2, 2], F32, "nbg")
            P.dma("sp", bgT[:], IN["l1_bgT"][:, :, :], writes=[bgT])
            P.op("dve", "tensor_scalar", [bgT], [nbg], out=nbg[:], in0=bgT[:], scalar1=-1.0, scalar2=None, op0=ALU.mult)
            one1 = A.sb([128, 1], F32, "one1a")
            P.op("pool", "memset", [], [one1], one1[:], 1.0)
            lge = A.sb([128, 512], F32, "lge")
            lgo = [A.sb([128, 512], F32, "lgo%d" % i) for i in range(2)]
        hb = [A.sb([128, 1024], F32, "hb%d" % i) for i in range(3)]
        junk = A.sb([128, 1024], F32, "junk")
        st = [A.sb([128, 4], F32, "st%d" % i) for i in range(2)]
        xn2 = [[A.sb([128, 1024], BF16, "xn%d_%d" % (s_, i)) for i in range(4)] for s_ in range(2)]
        epsT = A.sb([128, 1], F32, "epsT")
        P.op("pool", "memset", [], [epsT], epsT[:], EPS)
        uT = [A.sb([128, 8, 512], BF16, "uT%d" % i) for i in range(2)]
        fo_bf = [A.sb([128, 512], BF16, "fobf%d" % i) for i in range(4)]
        fo_f = [A.sb([128, 512], F32, "fof%d" % i) for i in range(3)]
        tp = [A.ps([128, 512], BF16, "tp%d" % i) for i in range(2)]
        acc = [A.ps([128, 512], F32, "acc%d" % i) for i in range(5)]
        cnt = {"acc": 0, "fobf": 0, "fof": 0, "ev": 0, "q": 0, "hb": 0, "xn": 0, "tp": 0}

        def nxt(name, lst):
            r = lst[cnt[name] % len(lst)]
            cnt[name] += 1
            return r

        def evac_engine():
            cnt["ev"] += 1
            return "dve" if cnt["ev"] % 2 == 0 else "act"

        def copy_op(eng, src_t, src_ap, dst_t, dst_ap):
            if eng == "act":
                P.op("act", "copy", [src_t], [dst_t], out=dst_ap, in_=src_ap)
            else:
                P.op(eng, "tensor_copy", [src_t], [dst_t], out=dst_ap, in_=src_ap)

        def stq():
            cnt["q"] += 1
            return "pool" if cnt["q"] % 2 == 0 else "sp"

        def norm_part(gi):
            t0, n = GROUPS[gi]
            ntl = n // 128
            sta = st[gi % 2]
            xn = xn2[gi % 2]
            for ti in range(ntl):
                h = nxt("hb", hb)
                tok = t0 + ti * 128
                if layer == 0:
                    src = IN["ctx"][tok:tok + 128, :] if gi == 0 else IN["x"][tok - TC:tok - TC + 128, :]
                else:
                    src = SC["H1"][tok:tok + 128, :]
                P.dma("sp", h[:], src, writes=[h])
                P.op("act", "activation", [h], [junk, sta], out=junk[:], in_=h[:], func=AF.Square,
                     accum_out=sta[:, ti:ti + 1])
                P.op("act", "activation", [sta, epsT], [sta], out=sta[:, ti:ti + 1], in_=sta[:, ti:ti + 1], func=AF.Sqrt,
                     scale=1.0 / D, bias=epsT[:, 0:1])
                P.op("dve", "reciprocal", [sta], [sta], out=sta[:, ti:ti + 1], in_=sta[:, ti:ti + 1])
                x_ = xn[ti]
                P.op("dve", "tensor_scalar", [h, sta], [x_], out=x_[:], in0=h[:], scalar1=sta[:, ti:ti + 1],
                     scalar2=None, op0=ALU.mult)

        def transpose_part(gi):
            t0, n = GROUPS[gi]
            ntl = n // 128
            s = 1 if gi == 0 else 0
            u = uT[gi % 2]
            xn = xn2[gi % 2]
            for j in range(8):
                tpp = nxt("tp", tp)
                for ti in range(ntl):
                    P.op("pe", "transpose", [xn[ti], ident_bf], [tpp], tpp[:, ti * 128:(ti + 1) * 128],
                         xn[ti][:, j * 128:(j + 1) * 128], ident_bf[:])
                P.op("dve", "tensor_scalar", [tpp, affA, affB], [u], out=u[:, j, 0:n],
                     in0=tpp[:, 0:n], scalar1=affA[:, j, s:s + 1], scalar2=affB[:, j, s:s + 1], op0=ALU.mult,
                     op1=ALU.add)


        def proj_part(gi):
            t0, n = GROUPS[gi]
            ntl = n // 128
            u = uT[gi % 2]

            def fm_proj(c0, ncol):
                ps = nxt("acc", acc)
                for k in range(8):
                    P.op("pe", "matmul", [w_in, u], [ps], ps[0:ncol, 0:n], w_in[:, k, c0:c0 + ncol], u[:, k, 0:n],
                         start=(k == 0), stop=(k == 7))
                return ps

            def store_fm(ps, ncol, dst, dt, func=None, eng=None):
                o = nxt("fobf", fo_bf) if dt == BF16 else nxt("fof", fo_f)
                if func is not None:
                    P.op("act", "activation", [ps], [o], out=o[0:ncol, 0:n], in_=ps[0:ncol, 0:n], func=func)
                else:
                    copy_op(eng or evac_engine(), ps, ps[0:ncol, 0:n], o, o[0:ncol, 0:n])
                P.dma(stq(), dst, o[0:ncol, 0:n], reads=[o])

            tsl = slice(t0, t0 + n)
            if layer == 0:
                for j in range(3):
                    ps = fm_proj(j * 128, 128)
                    copy_op(evac_engine(), ps, ps[:, 0:n], cqT, cqT[:, j, 0:n])
                for j in range(2):
                    ps = fm_proj(384 + j * 128, 128)
                    copy_op(evac_engine(), ps, ps[:, 0:n], ckvT, ckvT[:, j, 0:n])
                for (src_t, nk, nrm, dst_t, dim) in ((cqT, 3, qnT, cqn, 384.0), (ckvT, 2, kvnT, ckvn, 256.0)):
                    P.op("act", "activation", [src_t], [sq], out=sq[:, 0:nk, 0:n], in_=src_t[:, 0:nk, 0:n], func=AF.Square)
                    ps = nxt("acc", acc)
                    for k in range(nk):
                        P.op("pe", "matmul", [ones_f, sq], [ps], ps[:, 0:n], ones_f[:], sq[:, k, 0:n], start=(k == 0),
                             stop=(k == nk - 1))
                    P.op("act", "activation", [ps, epsT], [rstd], out=rstd[:, 0:n], in_=ps[:, 0:n], func=AF.Sqrt,
                         scale=1.0 / dim, bias=epsT[:, 0:1])
                    P.op("dve", "reciprocal", [rstd], [rstd], out=rstd[:, 0:n], in_=rstd[:, 0:n])
                    for k in range(nk):
                        P.op("dve", "scalar_tensor_tensor", [src_t, nrm, rstd], [dst_t], out=dst_t[:, k, 0:n],
                             in0=src_t[:, k, 0:n], scalar=nrm[:, k:k + 1], in1=rstd[:, 0:n], op0=ALU.mult, op1=ALU.mult)
                rot = gi > 0
                if rot:
                    P.dma("sp", rC[:, 0:n], IN["ropeC"][:, t0 - TC:t0 - TC + n], writes=[rC])
                    P.dma("sp", rS[:, 0:n], IN["ropeS"][:, t0 - TC:t0 - TC + n], writes=[rS])
                for hh in range(8):
                    ps = nxt("acc", acc)
                    for k in range(3):
                        P.op("pe", "matmul", [w_uq, cqn], [ps], ps[0:96, 0:n], w_uq[:, k, hh * 192:hh * 192 + 96],
                             cqn[:, k, 0:n], start=(k == 0), stop=(k == 2))
                    o = nxt("fobf", fo_bf)
                    if rot:
                        ps2 = nxt("acc", acc)
                        for k in range(3):
                            P.op("pe", "matmul", [w_uq, cqn], [ps2], ps2[0:96, 0:n],
                                 w_uq[:, k, hh * 192 + 96:hh * 192 + 192], cqn[:, k, 0:n], start=(k == 0), stop=(k == 2))
                        copy_op("act", ps, ps[0:64, 0:n], o, o[0:64, 0:n])
                        P.op("dve", "tensor_tensor", [ps, rC], [rt1], out=rt1[64:96, 0:n], in0=ps[64:96, 0:n],
                             in1=rC[64:96, 0:n], op=ALU.mult)
                        P.op("dve", "tensor_tensor", [ps2, rS], [rt2], out=rt2[64:96, 0:n], in0=ps2[64:96, 0:n],
                             in1=rS[64:96, 0:n], op=ALU.mult)
                        P.op("pool", "tensor_tensor", [rt1, rt2], [o], out=o[64:96, 0:n], in0=rt1[64:96, 0:n],
                             in1=rt2[64:96, 0:n], op=ALU.add)
                    else:
                        copy_op(evac_engine(), ps, ps[0:96, 0:n], o, o[0:96, 0:n])
                    P.dma(stq(), SC["QT"][hh, :, tsl], o[0:96, 0:n], reads=[o])
                for c in range(4):
                    ps = nxt("acc", acc)
                    for k in range(2):
                        P.op("pe", "matmul", [w_ukv, ckvn], [ps], ps[:, 0:n], w_ukv[:, k, c * 128:(c + 1) * 128],
                             ckvn[:, k, 0:n], start=(k == 0), stop=(k == 1))
                    o = nxt("fobf", fo_bf)
                    copy_op(evac_engine(), ps, ps[:, 0:n], o, o[:, 0:n])
                    for hh in range(2):
                        P.dma(stq(), SC["KT"][c * 2 + hh, 0:64, tsl], o[hh * 64:(hh + 1) * 64, 0:n], reads=[o])
                for ti in range(ntl):
                    ps = nxt("acc", acc)
                    for k in range(2):
                        P.op("pe", "matmul", [w_ukv, ckvn], [ps], ps[:, :], ckvn[:, k, ti * 128:(ti + 1) * 128],
                             w_ukv[:, k, 512:1024], start=(k == 0), stop=(k == 1))
                    o = nxt("fobf", fo_bf)
                    copy_op(evac_engine(), ps, ps[:, :], o, o[:, :])
                    P.dma(stq(), SC["V"][t0 + ti * 128:t0 + (ti + 1) * 128, :], o[:, :], reads=[o])
                ps = fm_proj(640, 32)
                if rot:
                    ps2 = fm_proj(3760, 32)
                    P.op("dve", "tensor_tensor", [ps, rC], [rt1], out=rt1[0:32, 0:n], in0=ps[0:32, 0:n], in1=rC[0:32, 0:n],
                         op=ALU.mult)
                    P.op("dve", "tensor_tensor", [ps2, rS], [rt2], out=rt2[0:32, 0:n], in0=ps2[0:32, 0:n],
                         in1=rS[0:32, 0:n], op=ALU.mult)
                    P.op("pool", "tensor_tensor", [rt1, rt2], [kro], out=kro[0:32, 0:n], in0=rt1[0:32, 0:n],
                         in1=rt2[0:32, 0:n], op=ALU.add)
                else:
                    copy_op("dve", ps, ps[0:32, 0:n], kro, kro[0:32, 0:n])
                for hh in range(8):
                    P.dma(stq(), SC["KT"][hh, 64:96, tsl], kro[0:32, 0:n], reads=[kro])
                yield
                for c in range(8):
                    ps = fm_proj(672 + c * 128, 128)
                    store_fm(ps, 128, SC["MQKB"][c * 128:(c + 1) * 128, tsl], BF16)
                ps = fm_proj(3792, 8)
                store_fm(ps, 8, SC["GI"][:, tsl], F32)
                ps = fm_proj(3800, 8)
                store_fm(ps, 8, SC["GF"][:, tsl], F32)
                for c in range(8):
                    ps = fm_proj(2736 + c * 128, 128)
                    store_fm(ps, 128, SC["SZT"][c * 128:(c + 1) * 128, tsl], BF16, func=AF.Silu)
                tm_specs = [(1696, SC["MV"], None), (2208, SC["MO"], AF.Sigmoid)]
            else:
                for c in range(4):
                    ps = fm_proj(c * 128, 128)
                    store_fm(ps, 128, SC["MQK"][c * 128:(c + 1) * 128, tsl], F32)
                yield
                for d in range(2):
                    ps = fm_proj(1024 + 16 * d, 16)
                    copy_op("dve", ps, ps[0:16, 0:n], gaT[d], gaT[d][0:16, 0:n])
                for d in range(2):
                    for c2 in range(2):
                        ps = nxt("acc", acc)
                        P.op("pe", "matmul", [wg, gaT[d]], [ps], ps[:, 0:n], wg[0:16, d, c2 * 128:(c2 + 1) * 128],
                             gaT[d][0:16, 0:n], start=True, stop=True)
                        P.op("act", "activation", [ps, nbg], [lge], out=lge[:, 0:n], in_=ps[:, 0:n], func=AF.Exp, scale=-1.0,
                             bias=nbg[:, d, c2:c2 + 1])
                        P.op("act", "activation", [lge, one1], [lge], out=lge[:, 0:n], in_=lge[:, 0:n], func=AF.Ln,
                             bias=one1[:, 0:1])
                        o = lgo[(d * 2 + c2) % 2]
                        P.op("dve", "tensor_scalar", [lge], [o], out=o[:, 0:n], in0=lge[:, 0:n], scalar1=-1.0 / 16.0,
                             scalar2=None, op0=ALU.mult)
                        P.dma(stq(), SC["LG"][d, c2 * 128:(c2 + 1) * 128, tsl], o[:, 0:n], reads=[o])
                for c in range(4):
                    ps = fm_proj(1056 + c * 128, 128)
                    store_fm(ps, 128, SC["NQ"][c * 128:(c + 1) * 128, tsl], BF16)
                for c in range(4):
                    ps = fm_proj(1568 + c * 128, 128)
                    store_fm(ps, 128, SC["NK"][c * 128:(c + 1) * 128, tsl], BF16)
                for c in range(8):
                    ps = fm_proj(2592 + c * 128, 128)
                    store_fm(ps, 128, SC["SZT"][c * 128:(c + 1) * 128, tsl], BF16, func=AF.Silu)
                tm_specs = [(512, SC["MV"], None), (2080, SC["V"], None)]
            for (c0, dst, func) in tm_specs:
                for ti in range(ntl):
                    ps = nxt("acc", acc)
                    for k in range(8):
                        P.op("pe", "matmul", [w_in, u], [ps], ps[:, :], u[:, k, ti * 128:(ti + 1) * 128],
                             w_in[:, k, c0:c0 + 512], start=(k == 0), stop=(k == 7))
                    o = nxt("fobf", fo_bf)
                    if func is not None:
                        P.op("act", "activation", [ps], [o], out=o[:, :], in_=ps[:, :], func=func)
                    else:
                        copy_op(evac_engine(), ps, ps[:, :], o, o[:, :])
                    P.dma(stq(), dst[t0 + ti * 128:t0 + (ti + 1) * 128, :], o[:, :], reads=[o])

        norm_part(0)
        transpose_part(0)
        for gi in range(len(GROUPS)):
            if gi + 1 < len(GROUPS):
                norm_part(gi + 1)
            gen = proj_part(gi)
            next(gen)
            if gi + 1 < len(GROUPS):
                transpose_part(gi + 1)
            for _ in gen:
                pass
        P.barrier()


def attention(nc, P, SC, ones_f, heads, dq, scale, load_q, load_k, Vd, cat_row0, groups, bias_fn=None, ident_bf=None,
              early_release=False, act_recip=False):
    LOOK = 4
    NS = 5
    EPI_DELAY = 8
    with ExitStack() as es:
        A = Alloc(nc, es)
        V = A.sb([128, NT, heads, 65], BF16, "Vall")
        P.op("pool", "memset", [], [V], V[:, :, :, 64:65], 1.0)
        for half in range(2):
            tl = slice(half * 17, (half + 1) * 17)
            for hh in range(heads):
                P.dma("sp" if hh % 2 == 0 else "pool", V[:, tl, hh, 0:64],
                      Vd[half * 17 * 128:(half + 1) * 17 * 128, hh * 64:(hh + 1) * 64].rearrange("(t p) d -> p t d", p=128),
                      writes=[V])
        kT = [A.sb([128, T], BF16, "kT%d" % i) for i in range(2)]
        qT = [A.sb([128, T], BF16, "qT%d" % i) for i in range(2)]
        pt = [A.sb([128, 512], BF16, "pt%d" % i) for i in range(NS)]
        sb_t = [A.sb([128, 512], F32, "sbt%d" % i) for i in range(3)] if bias_fn is not None else None
        rden = [A.sb([128, 512], F32, "rden%d" % i) for i in range(2)]
        ocp = [A.sb([128, 512], F32, "ocp%d" % i) for i in range(3)] if early_release else None
        rsc = A.sb([128, 512], F32, "rsc")
        bcs = [A.sb([128, 512], F32, "bcs%d" % i) for i in range(2)]
        szt = [A.sb([64, 512], BF16, "szt%d" % i) for i in range(3)]
        tmp = [A.sb([64, 512], F32, "atmp%d" % i) for i in range(2)]
        ao = [A.sb([64, 512], BF16, "ao%d" % i) for i in range(2)]
        Sps = [A.ps([128, 512], F32, "Sps%d" % i) for i in range(NS)]
        Ops = [A.ps([128, 512], F32, "Ops%d" % i) for i in range(2)]
        Bps = A.ps([128, 512], F32, "Bps")
        if dq < 128:
            for t_ in kT + qT:
                P.op("pool", "memset", [], [t_], t_[64:128, :], 0.0)
        P.dma("sp", kT[0][0:dq, :], load_k(0), writes=[kT[0]])
        P.dma("pool", qT[0][0:dq, :], load_q(0), writes=[qT[0]])
        gcount = 0
        it = 0
        rd_dep = [DramDep() for _ in range(16)]
        pend = []
        for h in range(heads):
            k_ = kT[h % 2]
            q_ = qT[h % 2]
            if h + 1 < heads:
                P.dma("sp", kT[(h + 1) % 2][0:dq, :], load_k(h + 1), writes=[kT[(h + 1) % 2]])
                P.dma("pool", qT[(h + 1) % 2][0:dq, :], load_q(h + 1), writes=[qT[(h + 1) % 2]])
            bias_loader = None
            if bias_fn is not None:
                if h == 0:
                    bias_cur, ld0 = bias_fn(0, A)
                    for _ in ld0:
                        pass
                bias_tiles = bias_cur
                if h + 1 < heads:
                    bias_cur, bias_loader = bias_fn(h + 1, A if h == 0 else None)
            else:
                bias_tiles = None
            r0 = cat_row0 + h * 64
            items = []
            for (q0, n, tiles, gkey) in groups:
                gid = gcount
                gcount += 1
                for j, kt in enumerate(tiles):
                    if isinstance(kt, tuple):
                        kt, c0, c1 = kt
                    else:
                        c0, c1 = 0, n
                    items.append((gid, q0, n, gkey, j, kt, len(tiles), c0, c1))

            def flush(cond):
                for e_ in pend[:]:
                    if cond(e_[1][0]):
                        emit_epi(*e_[1])
                        pend.remove(e_)

            def emit_S(item, slot):
                gid, q0, n, gkey, j, kt, nt_, c0, c1 = item
                S = Sps[slot % NS]
                p_ = pt[slot % NS]
                if j == 0:
                    flush(lambda g2: g2 % 3 == gid % 3)
                    sz = szt[gid % 3]
                    P.dma("sp", sz[:, 0:n], SC["SZT"][r0:r0 + 64, q0:q0 + n], writes=[sz])
                P.op("pe", "matmul", [k_, q_], [S], S[:, c0:c1], k_[:, kt * 128:(kt + 1) * 128], q_[:, q0 + c0:q0 + c1],
                     start=True, stop=True)
                bt = bias_tiles.get((gkey, kt)) if bias_tiles is not None else None
                if bt is not None:
                    sb = sb_t[slot % 3]
                    P.op("dve", "scalar_tensor_tensor", [S, bt], [sb], out=sb[:, c0:c1], in0=S[:, c0:c1], scalar=scale,
                         in1=bt[:, c0:c1], op0=ALU.mult, op1=ALU.add)
                    P.op("act", "activation", [sb], [p_], out=p_[:, c0:c1], in_=sb[:, c0:c1], func=AF.Exp)
                else:
                    P.op("act", "activation", [S], [p_], out=p_[:, c0:c1], in_=S[:, c0:c1], func=AF.Exp, scale=scale)

            def emit_PV(item, slot):
                gid, q0, n, gkey, j, kt, nt_, c0, c1 = item
                O = Ops[gid % 2]
                p_ = pt[slot % NS]
                assert j > 0 or (c0 == 0 and c1 == n)
                if j == 0:
                    flush(lambda g2: g2 % 2 == gid % 2)
                P.op("pe", "matmul", [V, p_], [O], O[0:65, c0:c1], V[:, kt, h, :], p_[:, c0:c1], start=(j == 0),
                     stop=(j == nt_ - 1))
                if j == nt_ - 1:
                    rd = rden[gid % 2]
                    if early_release:
                        oc = ocp[gid % 3]
                        P.op("act", "copy", [O], [oc], out=oc[0:65, 0:n], in_=O[0:65, 0:n])
                        P.op("act", "activation", [oc], [rsc], out=rsc[64:65, 0:n], in_=oc[64:65, 0:n], func=AF.Ln)
                        P.op("act", "activation", [rsc], [rd], out=rd[64:65, 0:n], in_=rsc[64:65, 0:n], func=AF.Exp, scale=-1.0)
                    elif act_recip:
                        P.op("act", "activation", [O], [rsc], out=rsc[64:65, 0:n], in_=O[64:65, 0:n], func=AF.Ln)
                        P.op("act", "activation", [rsc], [rd], out=rd[64:65, 0:n], in_=rsc[64:65, 0:n], func=AF.Exp, scale=-1.0)
                    else:
                        P.op("dve", "reciprocal", [O], [rd], out=rd[64:65, 0:n], in_=O[64:65, 0:n])
                    pend.append([EPI_DELAY, (gid, q0, n, r0, h)])

            def emit_epi(gid, q0, n, r0, h):
                O = Ops[gid % 2]
                rd = rden[gid % 2]
                bc_ = bcs[gid % 2]
                tm_ = tmp[gid % 2]
                a_ = ao[gid % 2]
                sz = szt[gid % 3]
                P.op("pe", "matmul", [ones_f, rd], [Bps], Bps[0:64, 0:n], ones_f[64:65, 0:64], rd[64:65, 0:n],
                     start=True, stop=True)
                if early_release:
                    oc = ocp[gid % 3]
                    P.op("dve", "tensor_tensor", [oc, Bps], [tm_], out=tm_[:, 0:n], in0=oc[0:64, 0:n], in1=Bps[0:64, 0:n],
                         op=ALU.mult)
                else:
                    P.op("act", "copy", [Bps], [bc_], out=bc_[0:64, 0:n], in_=Bps[0:64, 0:n])
                    P.op("dve", "tensor_tensor", [O, bc_], [tm_], out=tm_[:, 0:n], in0=O[0:64, 0:n], in1=bc_[0:64, 0:n],
                         op=ALU.mult)
                P.op("pool", "tensor_tensor", [tm_, sz], [a_], out=a_[:, 0:n], in0=tm_[:, 0:n], in1=sz[:, 0:n], op=ALU.mult)
                P.dma("pool", SC["CATT"][r0:r0 + 64, q0:q0 + n], a_[:, 0:n], reads=[a_])

            nI = len(items)
            for idx in range(nI + LOOK):
                if idx < nI:
                    emit_S(items[idx], it + idx)
                for e_ in pend[:]:
                    e_[0] -= 1
                    if e_[0] <= 0:
                        emit_epi(*e_[1])
                        pend.remove(e_)
                if idx - LOOK >= 0:
                    emit_PV(items[idx - LOOK], it + idx - LOOK)
                if bias_loader is not None and idx % 3 == 2:
                    next(bias_loader, None)
            if bias_loader is not None:
                for _ in bias_loader:
                    pass
            it += nI
        for e_ in pend:
            emit_epi(*e_[1])
        P.barrier()


def mlstm_phase(nc, P, IN, SC, ident_bf, ident_f, ones_f):
    NB = NT
    NCH = T // 64
    with ExitStack() as es:
        A = Alloc(nc, es)
        esT = A.sb([128, NB, 64], F32, "esT")
        fT = A.sb([128, NB, 64], F32, "fT")
        decbc = A.sb([128, 8, NCH], F32, "decbc")
        mask = [A.sb([128, 64], F32, "mask%d" % d) for d in range(2)]
        for d in range(2):
            P.op("pool", "memset", [], [mask[d]], mask[d][:], 1.0)
            for half in range(2):
                pr = slice(half * 64, half * 64 + 64)
                P.op("pool", "affine_select", [mask[d]], [mask[d]], out=mask[d][pr, :], in_=mask[d][pr, :],
                     pattern=[[1 if d == 0 else -1, 64]], compare_op=ALU.is_ge, fill=0.0, base=0,
                     channel_multiplier=-1 if d == 0 else 1)
        with ExitStack() as es2:
            A2 = Alloc(nc, es2)
            X1 = A2.sb([64, T], F32, "X1")
            X2 = A2.sb([64, T], F32, "X2")
            X3 = A2.sb([64, T], F32, "X3")
            X4 = A2.sb([64, T], F32, "X4")
            gb = A2.sb([64, 2], F32, "gb")
            nbf = A2.sb([64, 1], F32, "nbf")
            one1 = A2.sb([64, 1], F32, "one1")
            dec = A2.sb([64, NCH], F32, "dec")
            aprev = A2.sb([64, NCH], F32, "aprev")
            sel = A2.sb([64, 128], F32, "sel")
            pst = [A2.ps([128, 8, 64], F32, "pst%d" % i) for i in range(2)]
            psd = A2.ps([128, NCH], F32, "psd")
            P.op("pool", "memset", [], [X1], X1[:], 0.0)
            P.op("pool", "memset", [], [X3], X3[:], 0.0)
            P.op("pool", "memset", [], [one1], one1[:], 1.0)
            P.dma("sp", gb[:], IN["l0_gb2"][:, :], writes=[gb])
            for d in range(2):
                P.dma("sp", X1[d * 32:d * 32 + 4, :], SC["GF"][d * 4:d * 4 + 4, :], writes=[X1])
                P.dma("pool", X3[d * 32:d * 32 + 4, :], SC["GI"][d * 4:d * 4 + 4, :], writes=[X3])
            P.op("dve", "tensor_scalar", [gb], [nbf], out=nbf[:], in0=gb[:, 1:2], scalar1=-1.0, scalar2=None, op0=ALU.mult)
            P.op("act", "activation", [X1, nbf], [X1], out=X1[:], in_=X1[:], func=AF.Exp, scale=-1.0, bias=nbf[:, 0:1])
            P.op("act", "activation", [X1, one1], [X1], out=X1[:], in_=X1[:], func=AF.Ln, bias=one1[:, 0:1])

            def seg_views(tile_, prng, d):
                if d == 0:
                    return [tile_[prng, 0:T]]
                return [tile_[prng, 0:TC][:, ::-1], tile_[prng, TC:T][:, ::-1]]

            def scan(dst, src, op0, d):
                prng = slice(d * 32, d * 32 + 32)
                dv = seg_views(dst, prng, d)
                sv = seg_views(src, prng, d)
                for i in range(len(dv)):
                    init = 0.0 if i == 0 else dst[prng, 0:1]
                    P.op("dve", "tensor_tensor_scan", [src, dst], [dst], out=dv[i], data0=sv[i], data1=sv[i],
                         initial=init, op0=op0, op1=ALU.bypass)

            for d in range(2):
                scan(X2, X1, ALU.add, d)
            P.op("dve", "scalar_tensor_tensor", [X3, gb, X2], [X3], out=X3[:], in0=X3[:], scalar=gb[:, 0:1], in1=X2[:],
                 op0=ALU.add, op1=ALU.add)
            for d in range(2):
                scan(X1, X3, ALU.max, d)
            for d in range(2):
                prng = slice(d * 32, d * 32 + 32)
                jj = 63 if d == 0 else 0
                P.op("dve", "tensor_copy", [X1], [X4], out=X4[prng, :].rearrange("p (c j) -> p c j", j=64),
                     in_=X1[prng, :].rearrange("p (c j) -> p c j", j=64)[:, :, jj:jj + 1].to_broadcast([32, NCH, 64]))
            aend = X4[:, :].rearrange("p (c j) -> p c j", j=64)[:, :, 0]
            P.op("pool", "memset", [], [aprev], aprev[:], 0.0)
            P.op("dve", "tensor_copy", [X4], [aprev], out=aprev[0:32, 1:NCH], in_=aend[0:32, 0:NCH - 1])
            P.op("dve", "tensor_copy", [X4], [aprev], out=aprev[32:64, 0:3], in_=aend[32:64, 1:4])
            P.op("dve", "tensor_copy", [X4], [aprev], out=aprev[32:64, 4:NCH - 1], in_=aend[32:64, 5:NCH])
            P.op("dve", "tensor_copy", [X4], [aprev], out=aprev[32:64, NCH - 1:NCH], in_=aend[32:64, 0:1])
            P.op("dve", "tensor_tensor", [aprev, X4], [dec], out=dec[:], in0=aprev[:], in1=aend, op=ALU.subtract)
            P.op("act", "activation", [dec], [dec], out=dec[:], in_=dec[:], func=AF.Exp)
            P.op("dve", "tensor_tensor", [X3, X4], [X3], out=X3[:], in0=X3[:], in1=X4[:], op=ALU.subtract)
            P.op("act", "activation", [X3], [X3], out=X3[:], in_=X3[:], func=AF.Exp)
            P.op("dve", "tensor_tensor", [X2, X4], [X2], out=X2[:], in0=X2[:], in1=X4[:], op=ALU.subtract)
            P.op("act", "activation", [X2], [X2], out=X2[:], in_=X2[:], func=AF.Exp)
            for (srcX, dstT) in ((X3, esT), (X2, fT)):
                for b0 in range(0, NB, 8):
                    nb = min(8, NB - b0)
                    ps = pst[(b0 // 8) % 2]
                    for bb in range(nb):
                        P.op("pe", "transpose", [srcX, ident_f], [ps], ps[:, bb, :], srcX[:, (b0 + bb) * 128:(b0 + bb + 1) * 128],
                             ident_f[0:64, 0:64])
                    P.op("act", "copy", [ps], [dstT], out=dstT[:, b0:b0 + nb, :], in_=ps[:, 0:nb, :])
            for idx in range(8):
                r = (idx // 4) * 32 + idx % 4
                P.op("dve", "tensor_copy", [ident_f], [sel], out=sel[:], in_=ident_f[0:64, r:r + 1].to_broadcast([64, 128]))
                P.op("pe", "matmul", [sel, dec], [psd], psd[:, :], sel[:, :], dec[:, :], start=True, stop=True)
                P.op("act", "copy", [psd], [decbc], out=decbc[:, idx, :], in_=psd[:, :])
            P.barrier()
        P.op("dve", "tensor_scalar", [esT], [esT], out=esT[:], in0=esT[:], scalar1=128.0 ** -0.5, scalar2=None, op0=ALU.mult)
        xraw = A.sb([128, T], BF16, "xraw")
        cvw = A.sb([128, 8, 4], F32, "cvw")
        P.dma("sp", cvw[:], IN["l0_convT"][:, :, :], writes=[cvw])
        dg = [A.sb([128, 3, 128], BF16, "dg%d" % i) for i in range(2)]
        qT = A.sb([128, T], BF16, "mqT")
        qd = [A.sb([128, T], BF16, "mqd%d" % d) for d in range(2)]
        kT = A.sb([128, T], BF16, "mkT")
        kTok = A.sb([128, NB, 128], BF16, "kTok")
        vtok = A.sb([128, NB, 128], BF16, "vtok")
        vpp = [A.sb([128, NB, 129], BF16, "vpp%d" % d) for d in range(2)]
        SmT = [A.sb([128, NB, 64], BF16, "SmT%d" % d) for d in range(2)]
        hbuf = [A.sb([128, NB, 129], F32, "hbuf%d" % d) for d in range(2)]
        Cst = [[A.sb([128, 129], F32, "C%d_%d" % (d, i)) for i in range(2)] for d in range(2)]
        Cb = [[A.sb([128, 129], BF16, "Cb%d_%d" % (d, i)) for i in range(2)] for d in range(2)]
        dn = [A.sb([128, NB], F32, "dn%d" % d) for d in range(2)]
        pcv = [A.ps([128, 512], F32, "pcv%d" % i) for i in range(2)]
        pU = [A.ps([128, 129], F32, "pU%d" % i) for i in range(2)]
        pN = [[A.ps([128, 129], F32, "pN%d_%d" % (d, i)) for i in range(2)] for d in range(2)]
        order = [list(range(NCH)), [3, 2, 1, 0] + list(range(NCH - 1, 3, -1))]
        pieces = [(0, TC)] + [(TC + 512 * i, TC + 512 * (i + 1)) for i in range(8)]
        pc = 0
        for h in range(4):
            for which in range(2):
                ch = which * 4 + h
                dg_ = dg[which]
                P.dma("sp" if which == 0 else "pool", xraw[:], SC["MQKB"][ch * 128:(ch + 1) * 128, :], writes=[xraw])
                for j in range(3):
                    P.op("dve", "tensor_scalar", [ident_f, cvw], [dg_], out=dg_[:, j, :], in0=ident_f[:], scalar1=cvw[:, ch, j:j + 1],
                         scalar2=None, op0=ALU.mult)
                dst = qT if which == 0 else kT
                for (a, b) in pieces:
                    s0, s1 = (0, TC) if a < TC else (TC, T)
                    ps = pcv[pc % 2]
                    pc += 1
                    P.op("pe", "matmul", [dg_, xraw], [ps], ps[:, 0:b - a], dg_[:, 1, :], xraw[:, a:b], start=True, stop=False)
                    lo = max(a, s0 + 1)
                    P.op("pe", "matmul", [dg_, xraw], [ps], ps[:, lo - a:b - a], dg_[:, 0, :], xraw[:, lo - 1:b - 1], start=False,
                         stop=False)
                    hi = min(b, s1 - 1)
                    P.op("pe", "matmul", [dg_, xraw], [ps], ps[:, 0:hi - a], dg_[:, 2, :], xraw[:, a + 1:hi + 1], start=False,
                         stop=True)
                    P.op("act", "activation", [ps, cvw], [dst], out=dst[:, a:b], in_=ps[:, 0:b - a], func=AF.Silu,
                         bias=cvw[:, ch, 3:4])
            for b0 in range(0, NB, 4):
                nb = min(4, NB - b0)
                ps = pcv[pc % 2]
                pc += 1
                psb = ps[:, 0:256].bitcast(BF16)
                for bb in range(nb):
                    P.op("pe", "transpose", [kT, ident_bf], [ps], psb[:, bb * 128:(bb + 1) * 128],
                         kT[:, (b0 + bb) * 128:(b0 + bb + 1) * 128], ident_bf[:])
                P.op("act", "copy", [ps], [kTok], out=kTok[:, b0:b0 + nb, :],
                     in_=psb[:, 0:nb * 128].rearrange("p (b j) -> p b j", j=128))
            P.dma("sp", vtok[:], SC["MV"][:, h * 128:(h + 1) * 128].rearrange("(b p) j -> p b j", p=128), writes=[vtok])
            for d in range(2):
                col = d * 32 + h
                idx = d * 4 + h
                P.op("pool" if d == 0 else "dve", "tensor_tensor", [vtok, esT], [vpp[d]], out=vpp[d][:, :, 0:128], in0=vtok[:],
                     in1=esT[:, :, col:col + 1].to_broadcast([128, NB, 128]), op=ALU.mult)
                P.op("dve", "tensor_copy", [esT], [vpp[d]], out=vpp[d][:, :, 128:129], in_=esT[:, :, col:col + 1])
                P.op("pool" if d == 1 else "dve", "tensor_tensor", [qT, decbc], [qd[d]],
                     out=qd[d][:, :].rearrange("p (c j) -> p c j", j=64), in0=qT[:, :].rearrange("p (c j) -> p c j", j=64),
                     in1=decbc[:, idx, :].unsqueeze(2).to_broadcast([128, NCH, 64]), op=ALU.mult)
            for b0 in range(0, NB, 4):
                nb = min(4, NB - b0)
                ps = pcv[pc % 2]
                pc += 1
                psv = ps[:, 0:256].rearrange("p (b j) -> p b j", j=64)
                for bb in range(nb):
                    for half in range(2):
                        c = (b0 + bb) * 2 + half
                        pr = slice(half * 64, half * 64 + 64)
                        P.op("pe", "matmul", [kT, qT], [ps], psv[pr, bb, :], kT[:, c * 64:(c + 1) * 64], qT[:, c * 64:(c + 1) * 64],
                             start=True, stop=True)
                for d in range(2):
                    P.op("dve", "tensor_tensor", [ps, mask[d]], [SmT[d]], out=SmT[d][:, b0:b0 + nb, :], in0=psv[:, 0:nb, :],
                         in1=mask[d][:].unsqueeze(1).to_broadcast([128, nb, 64]), op=ALU.mult)
            for d in range(2):
                P.op("pool", "memset", [], [Cst[d][0]], Cst[d][0][:], 0.0)
                P.op("pool", "memset", [], [Cb[d][0]], Cb[d][0][:], 0.0)
            for i in range(NCH):
                for d in range(2):
                    c = order[d][i]
                    b, half = c // 2, c % 2
                    pr = slice(half * 64, half * 64 + 64)
                    idx = d * 4 + h
                    Cold, Cnew = Cst[d][i % 2], Cst[d][(i + 1) % 2]
                    cbo, cbn = Cb[d][i % 2], Cb[d][(i + 1) % 2]
                    U = pU[d]
                    N = pN[d][i % 2]
                    P.op("pe", "matmul", [kTok, vpp[d]], [U], U[:, :], kTok[pr, b, :], vpp[d][pr, b, :], start=True, stop=True)
                    P.op("pe", "matmul", [SmT[d], vpp[d]], [N], N[pr, :], SmT[d][pr, b, :], vpp[d][pr, b, :], start=True,
                         stop=False)
                    P.op("pe", "matmul", [qd[d], cbo], [N], N[pr, :], qd[d][:, c * 64:(c + 1) * 64], cbo[:], start=False, stop=True)
                    P.op("dve", "scalar_tensor_tensor", [Cold, decbc, U], [cbn], out=cbn[:], in0=Cold[:],
                         scalar=decbc[:, idx, c:c + 1], in1=U[:, :], op0=ALU.mult, op1=ALU.add)
                    P.op("dve", "scalar_tensor_tensor", [Cold, decbc, U], [Cnew], out=Cnew[:], in0=Cold[:],
                         scalar=decbc[:, idx, c:c + 1], in1=U[:, :], op0=ALU.mult, op1=ALU.add)
                    P.op("act", "copy", [N], [hbuf[d]], out=hbuf[d][pr, b, :], in_=N[pr, :])
            for d in range(2):
                col = d * 32 + h
                P.op("act", "activation", [hbuf[d]], [dn[d]], out=dn[d][:, :].unsqueeze(2), in_=hbuf[d][:, :, 128:129], func=AF.Abs)
                P.op("dve", "tensor_tensor", [dn[d], fT], [dn[d]], out=dn[d][:, :].unsqueeze(2), in0=dn[d][:, :].unsqueeze(2),
                     in1=fT[:, :, col:col + 1], op=ALU.max)
                P.op("dve", "reciprocal", [dn[d]], [dn[d]], out=dn[d][:], in_=dn[d][:])
                P.op("dve" if d == 0 else "pool", "tensor_tensor", [hbuf[d], dn[d]], [hbuf[d]], out=hbuf[d][:, :, 0:128],
                     in0=hbuf[d][:, :, 0:128], in1=dn[d][:, :].unsqueeze(2).to_broadcast([128, NB, 128]), op=ALU.mult)
                P.dma("sp" if d == 0 else "pool", SC["HM"][d, :, h * 128:(h + 1) * 128].rearrange("(b p) j -> p b j", p=128),
                      hbuf[d][:, :, 0:128], reads=[hbuf[d]])
        P.barrier()


def combine_phase(nc, P, IN, SC, ident_bf, src0, src1, mul, norm_name, cat_row0, groups):
    with ExitStack() as es:
        A = Alloc(nc, es)
        nrow = A.sb([128, 512], F32, "nrow")
        P.dma("sp", nrow[:], IN[norm_name][0:1, :].to_broadcast([128, 512]), writes=[nrow])
        epsT = A.sb([128, 1], F32, "epsTc")
        P.op("pool", "memset", [], [epsT], epsT[:], EPS)
        a_ = [A.sb([128, 512], F32, "cA%d" % i) for i in range(4)]
        b_ = [A.sb([128, 512], F32, "cB%d" % i) for i in range(4)]
        m_ = [A.sb([128, 512], BF16, "cM%d" % i) for i in range(4)]
        junk = A.sb([128, 128], F32, "cjunk")
        ss = [A.sb([128, 4], F32, "css%d" % i) for i in range(4)]
        hn = [A.sb([128, 512], F32, "chn%d" % i) for i in range(4)]
        hb = [A.sb([128, 512], BF16, "chb%d" % i) for i in range(4)]
        sz = [A.sb([128, 512], BF16, "csz%d" % i) for i in range(2)]
        oo = [A.sb([128, 512], BF16, "coo%d" % i) for i in range(2)]
        tp = [A.ps([128, 512], BF16, "ctp%d" % i) for i in range(8)]
        tiles = []
        for gi, (t0, n) in enumerate(groups):
            for ti in range(n // 128):
                tiles.append((gi, t0, n, ti))

        def stage1(it):
            gi, t0, n, ti = tiles[it]
            tok = t0 + ti * 128
            a, b, m = a_[it % 4], b_[it % 4], m_[it % 4]
            P.dma("sp", a[:], src0[tok:tok + 128, :], writes=[a])
            P.dma("pool", b[:], src1[tok:tok + 128, :], writes=[b])
            if mul is not None:
                P.dma("sp", m[:], mul[tok:tok + 128, :], writes=[m])
            P.op("dve", "tensor_tensor", [a, b], [a], out=a[:], in0=a[:], in1=b[:], op=ALU.add)
            if mul is not None:
                P.op("pool", "tensor_tensor", [a, m], [a], out=a[:], in0=a[:], in1=m[:], op=ALU.mult)

        def stage2(it):
            gi, t0, n, ti = tiles[it]
            a, s_, hb_ = a_[it % 4], ss[it % 4], hb[it % 4]
            for hh in range(4):
                P.op("act", "activation", [a], [junk, s_], out=junk[:], in_=a[:, hh * 128:(hh + 1) * 128], func=AF.Square,
                     accum_out=s_[:, hh:hh + 1])
            P.op("act", "activation", [s_, epsT], [s_], out=s_[:], in_=s_[:], func=AF.Sqrt, scale=1.0 / 128, bias=epsT[:, 0:1])
            P.op("dve", "reciprocal", [s_], [s_], out=s_[:], in_=s_[:])
            for hh in range(4):
                sl = slice(hh * 128, (hh + 1) * 128)
                P.op("dve", "scalar_tensor_tensor", [a, s_, nrow], [hb_], out=hb_[:, sl], in0=a[:, sl], scalar=s_[:, hh:hh + 1],
                     in1=nrow[:, sl], op0=ALU.mult, op1=ALU.mult)
            for j in range(4):
                tpj = tp[(gi % 2) * 4 + j]
                P.op("pe", "transpose", [hb_, ident_bf], [tpj], tpj[:, ti * 128:(ti + 1) * 128],
                     hb_[:, j * 128:(j + 1) * 128], ident_bf[:])
            if ti == n // 128 - 1:
                for j in range(4):
                    tpj = tp[(gi % 2) * 4 + j]
                    r0 = cat_row0 + j * 128
                    z_, o_ = sz[j % 2], oo[j % 2]
                    P.dma("sp", z_[:, 0:n], SC["SZT"][r0:r0 + 128, t0:t0 + n], writes=[z_])
                    P.op("dve", "tensor_tensor", [tpj, z_], [o_], out=o_[:, 0:n], in0=tpj[:, 0:n], in1=z_[:, 0:n], op=ALU.mult)
                    P.dma("pool", SC["CATT"][r0:r0 + 128, t0:t0 + n], o_[:, 0:n], reads=[o_])

        stage1(0)
        for it in range(len(tiles)):
            if it + 1 < len(tiles):
                stage1(it + 1)
            stage2(it)
        P.barrier()


def phase_C(nc, P, IN, SC, layer, gateR, out_ap):
    with ExitStack() as es:
        A = Alloc(nc, es)
        w = A.sb([128, 8, 1024], BF16, "w_out")
        with ExitStack() as es2:
            A2 = Alloc(nc, es2)
            stg = [A2.sb([128, 8, 512], F32, "wstg%d" % i) for i in range(2)]
            for i in range(2):
                P.dma("sp" if i == 0 else "pool", stg[i][:],
                      IN["l%d_w_out" % layer][:, i * 512:(i + 1) * 512].rearrange("(k p) n -> p k n", p=128), writes=[stg[i]])
                P.op("dve" if i == 0 else "act", "tensor_copy" if i == 0 else "copy", [stg[i]], [w], out=w[:, :, i * 512:(i + 1) * 512],
                     in_=stg[i][:])
            P.barrier()
        cat = [A.sb([128, 8, 512], BF16, "catT%d" % i) for i in range(3)]
        hold = [A.sb([128, 1024], F32, "hold%d" % i) for i in range(4)]
        tmp = [A.sb([128, 1024], F32, "ctmp%d" % i) for i in range(4)]
        hnew = [A.sb([128, 1024], F32, "hnew%d" % i) for i in range(4)]
        ps = [A.ps([128, 512], F32, "yps%d" % i) for i in range(4)]
        if layer == 1:
            frow = A.sb([128, 1024], F32, "frow")
            P.dma("sp", frow[:], IN["final_norm"][0:1, :].to_broadcast([128, 1024]), writes=[frow])
            epsT = A.sb([128, 1], F32, "epsTf")
            P.op("pool", "memset", [], [epsT], epsT[:], EPS)
            junk = A.sb([128, 1024], F32, "fjunk")
            st = [A.sb([128, 1], F32, "fst%d" % i) for i in range(4)]
            ob = [A.sb([128, 1024], F32, "fob%d" % i) for i in range(4)]
        it = 0
        groups = GROUPS if layer == 0 else GROUPS[1:]
        def load_cat(gi):
            t0, n = groups[gi]
            c_ = cat[gi % 3]
            for k2 in range(2):
                P.dma("sp", c_[:, k2 * 4:(k2 + 1) * 4, 0:n],
                      SC["CATT"][k2 * 512:(k2 + 1) * 512, t0:t0 + n].rearrange("(k p) t -> p k t", p=128), writes=[c_])

        load_cat(0)
        for gi, (t0, n) in enumerate(groups):
            c_ = cat[gi % 3]
            if gi + 1 < len(groups):
                load_cat(gi + 1)
            g_ = gateR[1] if (layer == 0 and t0 == 0) else gateR[0]
            for ti in range(n // 128):
                tok = t0 + ti * 128
                ho, tm, hn_ = hold[it % 4], tmp[it % 4], hnew[it % 4]
                if layer == 0:
                    srcp = IN["ctx"][tok:tok + 128, :] if t0 == 0 else IN["x"][tok - TC:tok - TC + 128, :]
                else:
                    srcp = SC["H1"][tok:tok + 128, :]
                P.dma("sp", ho[:], srcp, writes=[ho])
                for half in range(2):
                    p_ = ps[(it * 2 + half) % 4]
                    for k in range(8):
                        P.op("pe", "matmul", [c_, w], [p_], p_[:, :], c_[:, k, ti * 128:(ti + 1) * 128],
                             w[:, k, half * 512:(half + 1) * 512], start=(k == 0), stop=(k == 7))
                    P.op("dve", "tensor_tensor", [p_, g_], [tm], out=tm[:, half * 512:(half + 1) * 512], in0=p_[:, :],
                         in1=g_[:, half * 512:(half + 1) * 512], op=ALU.mult)
                P.op("pool", "tensor_tensor", [tm, ho], [hn_], out=hn_[:], in0=tm[:], in1=ho[:], op=ALU.add)
                if layer == 0:
                    P.dma("pool", SC["H1"][tok:tok + 128, :], hn_[:], reads=[hn_])
                else:
                    s_, o_ = st[it % 4], ob[it % 4]
                    P.op("act", "activation", [hn_], [junk, s_], out=junk[:], in_=hn_[:], func=AF.Square, accum_out=s_[:, 0:1])
                    P.op("act", "activation", [s_, epsT], [s_], out=s_[:], in_=s_[:], func=AF.Sqrt, scale=1.0 / D, bias=epsT[:, 0:1])
                    P.op("dve", "reciprocal", [s_], [s_], out=s_[:], in_=s_[:])
                    P.op("act", "activation", [hn_, s_], [o_], out=o_[:], in_=hn_[:], func=AF.Copy, scale=s_[:, 0:1])
                    P.op("dve", "tensor_tensor", [o_, frow], [o_], out=o_[:], in0=o_[:], in1=frow[:], op=ALU.mult)
                    P.dma("pool", out_ap[tok - TC:tok - TC + 128, :], o_[:], reads=[o_])
                it += 1
        P.barrier()


def gla_phase(nc, P, IN, SC, ident_bf):
    NB = NT
    NCH = T // 64
    with ExitStack() as es:
        A = Alloc(nc, es)
        mask = [A.sb([128, 64], F32, "gmask%d" % d) for d in range(2)]
        for d in range(2):
            P.op("pool", "memset", [], [mask[d]], mask[d][:], 1.0)
            for half in range(2):
                pr = slice(half * 64, half * 64 + 64)
                P.op("pool", "affine_select", [mask[d]], [mask[d]], out=mask[d][pr, :], in_=mask[d][pr, :],
                     pattern=[[1 if d == 0 else -1, 64]], compare_op=ALU.is_ge, fill=0.0, base=0,
                     channel_multiplier=-1 if d == 0 else 1)
        rm = [A.sb([64, T], F32, "rm%d" % d) for d in range(2)]
        for d in range(2):
            P.op("pool", "memset", [], [rm[d]], rm[d][:], 1.0)
            j0 = 0 if d == 0 else 63
            P.op("pool", "memset", [rm[d]], [rm[d]], rm[d][:, :].rearrange("p (c j) -> p c j", j=64)[:, :, j0:j0 + 1], 0.0)
        qf = A.sb([64, T], F32, "gqf")
        kf = A.sb([64, T], F32, "gkf")
        lg = A.sb([64, T], F32, "glg")
        bc = A.sb([64, T], F32, "gbc")
        tmp = A.sb([64, T], F32, "gtmp")
        qg = [A.sb([64, T], BF16, "qg%d" % d) for d in range(2)]
        kg = [A.sb([64, T], BF16, "kg%d" % d) for d in range(2)]
        kbf = A.sb([64, T], BF16, "kbf")
        eb = [A.sb([64, NCH], F32, "eb%d" % d) for d in range(2)]
        kbTok = [A.sb([128, NB, 64], BF16, "kbTok%d" % d) for d in range(2)]
        SmT = [A.sb([128, NB, 64], BF16, "gSmT%d" % d) for d in range(2)]
        vtok = A.sb([128, NB, 128], BF16, "gvtok")
        obuf = [[A.sb([128, 128], F32, "gob%d_%d" % (d, i)) for i in range(3)] for d in range(2)]
        Sst = [[A.sb([64, 128], F32, "S%d_%d" % (d, i)) for i in range(2)] for d in range(2)]
        Sb = [[A.sb([64, 128], BF16, "Sb%d_%d" % (d, i)) for i in range(2)] for d in range(2)]
        ptr = [A.ps([128, 8, 64], BF16, "gptr%d" % i) for i in range(2)]
        pS = [A.ps([128, 8, 64], F32, "gpS%d" % i) for i in range(2)]
        pU = [A.ps([128, 128], F32, "gpU%d" % i) for i in range(2)]
        pN = [A.ps([128, 128], F32, "gpN%d" % i) for i in range(2)]
        order = [list(range(NCH)), [3, 2, 1, 0] + list(range(NCH - 1, 3, -1))]
        for h in range(4):
            P.dma("sp", qf[:], SC["MQK"][h * 64:(h + 1) * 64, :], writes=[qf])
            P.dma("pool", kf[:], SC["MQK"][256 + h * 64:256 + (h + 1) * 64, :], writes=[kf])
            P.dma("sp", vtok[:], SC["MV"][:, h * 128:(h + 1) * 128].rearrange("(b p) j -> p b j", p=128), writes=[vtok])
            for d in range(2):
                P.dma("pool", lg[:], SC["LG"][d, h * 64:(h + 1) * 64, :], writes=[lg])
                if d == 0:
                    P.op("dve", "tensor_tensor_scan", [rm[d], lg], [bc], out=bc[:, :], data0=rm[d][:, :], data1=lg[:, :],
                         initial=0.0, op0=ALU.mult, op1=ALU.add)
                else:
                    P.op("dve", "tensor_tensor_scan", [rm[d], lg], [bc], out=bc[:, ::-1], data0=rm[d][:, ::-1],
                         data1=lg[:, ::-1], initial=0.0, op0=ALU.mult, op1=ALU.add)
                jl = 63 if d == 0 else 0
                bl = bc[:, :].rearrange("p (c j) -> p c j", j=64)[:, :, jl:jl + 1]
                P.op("act", "activation", [bc], [eb[d]], out=eb[d][:, :].unsqueeze(2), in_=bl, func=AF.Exp)
                P.op("act", "activation", [bc], [tmp], out=tmp[:], in_=bc[:], func=AF.Exp)
                P.op("dve", "scalar_tensor_tensor", [qf, tmp], [qg[d]], out=qg[d][:], in0=qf[:], scalar=64.0 ** -0.5, in1=tmp[:],
                     op0=ALU.mult, op1=ALU.mult)
                P.op("act", "activation", [bc], [tmp], out=tmp[:], in_=bc[:], func=AF.Exp, scale=-1.0)
                P.op("dve", "tensor_tensor", [kf, tmp], [kg[d]], out=kg[d][:], in0=kf[:], in1=tmp[:], op=ALU.mult)
                P.op("dve", "tensor_tensor", [bc], [tmp], out=tmp[:, :].rearrange("p (c j) -> p c j", j=64),
                     in0=bl.to_broadcast([64, NCH, 64]), in1=bc[:, :].rearrange("p (c j) -> p c j", j=64), op=ALU.subtract)
                P.op("act", "activation", [tmp], [tmp], out=tmp[:], in_=tmp[:], func=AF.Exp)
                P.op("dve", "tensor_tensor", [kf, tmp], [kbf], out=kbf[:], in0=kf[:], in1=tmp[:], op=ALU.mult)
                for b0 in range(0, NB, 8):
                    nb = min(8, NB - b0)
                    ps = ptr[(b0 // 8) % 2]
                    for bb in range(nb):
                        P.op("pe", "transpose", [kbf, ident_bf], [ps], ps[:, bb, :], kbf[:, (b0 + bb) * 128:(b0 + bb + 1) * 128],
                             ident_bf[0:64, 0:64])
                    P.op("act", "copy", [ps], [kbTok[d]], out=kbTok[d][:, b0:b0 + nb, :], in_=ps[:, 0:nb, :])
                for b0 in range(0, NB, 8):
                    nb = min(8, NB - b0)
                    ps = pS[(b0 // 8) % 2]
                    for bb in range(nb):
                        for half in range(2):
                            c = (b0 + bb) * 2 + half
                            pr = slice(half * 64, half * 64 + 64)
                            P.op("pe", "matmul", [kg[d], qg[d]], [ps], ps[pr, bb, :], kg[d][:, c * 64:(c + 1) * 64],
                                 qg[d][:, c * 64:(c + 1) * 64], start=True, stop=True)
                    P.op("dve", "tensor_tensor", [ps, mask[d]], [SmT[d]], out=SmT[d][:, b0:b0 + nb, :], in0=ps[:, 0:nb, :],
                         in1=mask[d][:].unsqueeze(1).to_broadcast([128, nb, 64]), op=ALU.mult)
            for d in range(2):
                P.op("pool", "memset", [], [Sst[d][0]], Sst[d][0][:], 0.0)
                P.op("pool", "memset", [], [Sb[d][0]], Sb[d][0][:], 0.0)
            for i in range(NCH):
                for d in range(2):
                    c = order[d][i]
                    b, half = c // 2, c % 2
                    pr = slice(half * 64, half * 64 + 64)
                    Sold, Snew = Sst[d][i % 2], Sst[d][(i + 1) % 2]
                    sbo, sbn = Sb[d][i % 2], Sb[d][(i + 1) % 2]
                    U, N = pU[d], pN[d]
                    ob = obuf[d][(i // 2) % 3]
                    P.op("pe", "matmul", [SmT[d], vtok], [N], N[pr, :], SmT[d][pr, b, :], vtok[pr, b, :], start=True, stop=False)
                    P.op("pe", "matmul", [qg[d], sbo], [N], N[pr, :], qg[d][:, c * 64:(c + 1) * 64], sbo[:, :], start=False,
                         stop=True)
                    P.op("pe", "matmul", [kbTok[d], vtok], [U], U[0:64, :], kbTok[d][pr, b, :], vtok[pr, b, :], start=True,
                         stop=True)
                    P.op("dve", "scalar_tensor_tensor", [Sold, eb[d], U], [sbn], out=sbn[:], in0=Sold[:],
                         scalar=eb[d][:, c:c + 1], in1=U[0:64, :], op0=ALU.mult, op1=ALU.add)
                    P.op("dve", "scalar_tensor_tensor", [Sold, eb[d], U], [Snew], out=Snew[:], in0=Sold[:],
                         scalar=eb[d][:, c:c + 1], in1=U[0:64, :], op0=ALU.mult, op1=ALU.add)
                    P.op("act", "copy", [N], [ob], out=ob[pr, :], in_=N[pr, :])
                    if i % 2 == 1:
                        P.dma("sp" if d == 0 else "pool", SC["HM"][d, b * 128:(b + 1) * 128, h * 128:(h + 1) * 128], ob[:],
                              reads=[ob])
        P.barrier()


def _na_rows_ok(qr, kr):
    lo = min(max(qr - 4, 0), 56)
    return lo <= kr < lo + 8


def _na_cfg(g):
    if g == 0:
        return "first", 0, 6
    if g == 7:
        return "last", 26, 6
    return "mid", 4 * g - 2, 8


def _na_range(g, ktl):
    js = [j for j in range(8) for i in range(2) if _na_rows_ok(8 * g + j, 2 * ktl + i)]
    return min(js), max(js)


def na_bias_fn(nc, P, IN, state):
    def fn(h, A):
        if A is not None:
            state.setdefault("sets", {})
            for key, ntile in (("first", 6), ("mid", 8), ("last", 6)):
                state["sets"][(key, h % 2)] = [A.sb([128, 512], F32, "nab_%s%d_%d" % (key, i, h % 2)) for i in range(ntile)]
        out = {}

        def loader():
            for key, g, t_lo in (("first", 0, 0), ("mid", 1, 2), ("last", 7, 26)):
                tiles = state["sets"][(key, h % 2)]
                for r, bt in enumerate(tiles):
                    ktl = t_lo + r
                    u0, u1 = _na_range(g, ktl)
                    P.op("pool", "memset", [], [bt], bt[:, u0 * 64:(u1 + 1) * 64], MASKV)
                    for i in range(2):
                        kr = 2 * ktl + i
                        js = [j for j in range(8) if _na_rows_ok(8 * g + j, kr)]
                        if not js:
                            continue
                        j0, j1 = js[0], js[-1]
                        assert js == list(range(j0, j1 + 1))
                        m0 = 7 - (kr - 8 * g - j0)
                        nj = j1 - j0 + 1
                        P.dma("sp" if i == 0 else "pool",
                              bt[i * 64:(i + 1) * 64, j0 * 64:(j1 + 1) * 64].rearrange("p (m q) -> p m q", q=64),
                              IN["na_bias"][h, m0:m0 + nj, :, :].rearrange("m k q -> k m q"), writes=[bt])
                    yield

        for g in range(8):
            key, t_lo, nt_ = _na_cfg(g)
            for r in range(nt_):
                out[(g, 2 + t_lo + r)] = state["sets"][(key, h % 2)][r]
        return out, loader()

    return fn


def _shapes(d):
    return {k: (v.shape, "bf16" if v.dtype == ml_dtypes.bfloat16 else "f32") for k, v in d.items()}


def run(inputs, stage=99, debug=(), cores=8, skip=()):
    inputs = {k: np.asarray(v) for k, v in inputs.items()}
    sh, per = prep_inputs(inputs)
    nc = build(_shapes(sh), _shapes(per[0]), stage=stage, debug=debug, skip=skip)
    in_maps = [dict(sh, **per[b]) for b in range(cores)]
    res = run_bass_kernel_spmd(nc, in_maps, core_ids=list(range(cores)))
    return res


def kernel(**inputs):
    res = run(inputs)
    return np.stack([np.asarray(r["out"], dtype=np.float32) for r in res.results], axis=0)
```

```python
import numpy as np
from contextlib import ExitStack
import ml_dtypes
import concourse.bass as bass
import concourse.mybir as mybir
from concourse.bass_utils import run_bass_kernel_spmd

F32 = mybir.dt.float32
BF16 = mybir.dt.bfloat16
AF = mybir.ActivationFunctionType
ALU = mybir.AluOpType
AX = mybir.AxisListType

D = 1024
TC = 256
TL = 4096
T = TC + TL
NT = T // 128
EPS = 1e-6
MASKV = -30000.0

GROUPS = [(0, 256)] + [(256 + 512 * i, 512) for i in range(8)]


class Dep:
    __slots__ = ("w", "r")

    def __init__(self):
        self.w = None
        self.r = {}


class Tile:
    def __init__(self, t):
        self.t = t
        self.d = Dep()

    def __getitem__(self, k):
        return self.t[k]


class DramDep:
    def __init__(self):
        self.d = Dep()


class Prog:
    def __init__(self, nc, es):
        self.nc = nc
        self.eng = {"pe": nc.tensor, "act": nc.scalar, "dve": nc.vector, "pool": nc.gpsimd, "sp": nc.sync}
        self.R = 12
        self.keys = [("pe", "c"), ("act", "c"), ("dve", "c"), ("pool", "c")]
        for q in ("sp", "pool"):
            self.keys += [(q, "d%d" % i) for i in range(self.R)]
        self.ndma = {"sp": 0, "pool": 0}
        self.sem = {k: es.enter_context(nc.semaphore("s_%s_%s" % k)) for k in self.keys}
        self.cnt = {k: 0 for k in self.keys}
        self.waited = {e: {} for e in self.eng}
        self.n = 0

    def _emit(self, eng, kind, fn, reads, writes):
        if kind == "d":
            kind = "d%d" % (self.ndma[eng] % self.R)
            self.ndma[eng] += 1
        key = (eng, kind)
        deps = {}
        if kind != "c" and self.cnt[key] > 0:
            deps[key] = self.cnt[key]

        def add(tok):
            if tok is None:
                return
            k, v = tok
            if deps.get(k, 0) < v:
                deps[k] = v

        for b in reads:
            add(b.d.w)
        for b in writes:
            add(b.d.w)
            for k, v in b.d.r.items():
                add((k, v))
        e = self.eng[eng]
        wd = self.waited[eng]
        for k, v in deps.items():
            if k == ("pe", "c") and eng == "pe":
                continue
            if wd.get(k, 0) >= v:
                continue
            e.wait_ge(self.sem[k], v)
            wd[k] = v
        inc = 16 if kind != "c" else 1
        self.cnt[key] += inc
        fn(e).then_inc(self.sem[key], inc)
        v = self.cnt[key]
        for b in reads:
            if b.d.r.get(key, 0) < v:
                b.d.r[key] = v
        for b in writes:
            b.d.w = (key, v)
            b.d.r = {}
        self.n += 1

    def op(self, eng, name, reads, writes, *a, **kw):
        self._emit(eng, "c", lambda e: getattr(e, name)(*a, **kw), reads, writes)

    def dma(self, q, out, in_, reads=(), writes=(), **kw):
        self._emit(q, "d", lambda e: e.dma_start(out=out, in_=in_, **kw), reads, writes)

    def barrier(self):
        for en, e in self.eng.items():
            wd = self.waited[en]
            for k in self.keys:
                v = self.cnt[k]
                if v > 0 and wd.get(k, 0) < v:
                    e.wait_ge(self.sem[k], v)
                    wd[k] = v


class Alloc:
    def __init__(self, nc, es):
        self.nc = nc
        self.es = es
        _CTR.setdefault(id(nc), 0)

    def _nm(self, name):
        _CTR[id(self.nc)] = _CTR.get(id(self.nc), 0) + 1
        return "%s_%d" % (name, _CTR[id(self.nc)])

    def sb(self, shape, dt, name=None):
        return Tile(self.es.enter_context(self.nc.sbuf_tensor(self._nm(name or "sb"), list(shape), dt)))

    def ps(self, shape, dt, name=None):
        return Tile(self.es.enter_context(self.nc.psum_tensor(self._nm(name or "ps"), list(shape), dt)))


_CTR = {}


def _fm(v, nchunk):
    return np.ascontiguousarray(v.reshape(nchunk, 128).T)


def _rope_perm():
    perm = np.zeros(32, np.int64)
    for i in range(32):
        r = i % 16
        perm[i] = i + 8 if r < 8 else i - 8
    return perm


def _rope_tables():
    t = np.arange(TL)
    inv = (1.0 / (10000.0 ** (np.arange(8, dtype=np.float32) / 8))).astype(np.float32)
    pos = [(t // 64).astype(np.float32), (t % 64).astype(np.float32)]
    C = np.zeros((32, TL), np.float32)
    S = np.zeros((32, TL), np.float32)
    for i in range(32):
        a = i // 16
        r = i % 16
        p = r % 8
        ang = (pos[a] * inv[p]).astype(np.float32)
        C[i] = np.cos(ang)
        S[i] = -np.sin(ang) if r < 8 else np.sin(ang)
    Cf = np.zeros((128, TL), np.float32)
    Sf = np.zeros((128, TL), np.float32)
    Cf[0:32] = C
    Cf[64:96] = C
    Sf[0:32] = S
    Sf[64:96] = S
    return Cf, Sf


def prep_inputs(inp):
    sh = {}
    sh["ident_bf"] = np.eye(128, dtype=np.float32).astype(ml_dtypes.bfloat16)
    sh["ident_f"] = np.eye(128, dtype=np.float32)
    perm = _rope_perm()
    w_in = inp["l0_w_in"]
    gi_cols = [2720 + d * 8 + h for d in range(2) for h in range(4)]
    gf_cols = [2720 + d * 8 + 4 + h for d in range(2) for h in range(4)]
    sh["l0_w_in"] = np.ascontiguousarray(
        np.concatenate([w_in, w_in[:, 640:672][:, perm], w_in[:, gi_cols], w_in[:, gf_cols]], axis=1))
    w_uq = inp["l0_mla_w_uq"].reshape(384, 8, 96)
    ext = np.concatenate([w_uq, w_uq[:, :, 0:64], w_uq[:, :, 64:96][:, :, perm]], axis=2)
    sh["l0_w_uq"] = np.ascontiguousarray(ext.reshape(384, 8 * 192))
    w_ukv = inp["l0_mla_w_ukv"].reshape(256, 8, 128)
    sh["l0_w_ukv"] = np.ascontiguousarray(
        np.concatenate([w_ukv[:, :, 0:64].reshape(256, 512), w_ukv[:, :, 64:128].reshape(256, 512)], axis=1))
    sh["l0_qnT"] = _fm(inp["l0_mla_q_norm"], 3)
    sh["l0_kvnT"] = _fm(inp["l0_mla_kv_norm"], 2)
    Cf, Sf = _rope_tables()
    sh["ropeC"] = Cf
    sh["ropeS"] = Sf
    cw = inp["l0_mlstm_conv_w"]
    sh["l0_convT"] = np.ascontiguousarray(
        np.concatenate([cw.reshape(3, 8, 128).transpose(2, 1, 0), inp["l0_mlstm_conv_b"].reshape(8, 128).T[:, :, None]],
                       axis=2))
    gb = np.zeros((16, 1), np.float32)
    for d in range(2):
        for h in range(4):
            gb[d * 8 + h, 0] = inp["l0_mlstm_b_i"][d, h]
            gb[d * 8 + 4 + h, 0] = inp["l0_mlstm_b_f"][d, h]
    sh["l0_gbias"] = gb
    gb2 = np.zeros((64, 2), np.float32)
    for d in range(2):
        for h in range(4):
            gb2[d * 32 + h, 0] = inp["l0_mlstm_b_i"][d, h]
            gb2[d * 32 + h, 1] = inp["l0_mlstm_b_f"][d, h]
    sh["l0_gb2"] = gb2
    sh["l0_hnorm"] = np.ascontiguousarray(inp["l0_mlstm_norm"].reshape(1, 512))
    sh["l0_w_out"] = inp["l0_w_out"]
    sh["l1_w_in"] = inp["l1_w_in"]
    sh["l1_w_gate"] = np.ascontiguousarray(inp["l1_gla_w_gate"])
    sh["l1_bgT"] = np.ascontiguousarray(inp["l1_gla_b_gate"].reshape(2, 2, 128).transpose(2, 0, 1))
    sh["l1_gnorm"] = np.ascontiguousarray(inp["l1_gla_norm"].reshape(1, 512))
    sh["l1_w_out"] = inp["l1_w_out"]
    sh["final_norm"] = np.ascontiguousarray(inp["final_norm"].reshape(1, 1024))
    rpb = inp["l1_na_rpb"]
    kc = np.arange(64)[:, None]
    qc = np.arange(64)[None, :]
    wc0 = np.clip(qc - 8, 0, 48)
    okc = (kc >= wc0) & (kc < wc0 + 16)
    dcol = np.clip(kc - qc + 15, 0, 30)
    Tb = np.full((8, 15, 64, 64), MASKV, np.float32)
    for m in range(15):
        dr = 7 - m
        blk = rpb[:, dr + 7][:, dcol]
        Tb[:, m] = np.where(okc[None], blk, np.float32(MASKV))
    sh["na_bias"] = Tb
    mods = [(inp["l0_norm"], inp["l0_w_mod"], inp["l0_b_mod"]), (inp["l1_norm"], inp["l1_w_mod"], inp["l1_b_mod"])]
    for l, (g_, wm_, bm_) in enumerate(mods):
        sh["l%d_w_mod" % l] = wm_
        sh["l%d_bmodT" % l] = _fm(bm_, 24)
        sh["l%d_bmod_gate" % l] = np.ascontiguousarray(bm_[2048:3072].reshape(1, 1024))
        sh["l%d_gT" % l] = _fm(g_, 8)
    per = []
    for b in range(8):
        d = {}
        d["x"] = inp["x"][b]
        d["ctx"] = inp["ctx"][b]
        cv = np.stack([inp["c"][b], inp["c_ctx"]], axis=1)
        d["cvec"] = np.ascontiguousarray(cv.reshape(8, 128, 2).transpose(1, 0, 2))
        per.append(d)
    return sh, per


def build(sh_shapes, per_shapes, stage=99, debug=(), skip=()):
    nc = bass.Bass("TRN2", target_bir_lowering=False)
    IN = {}
    for k, (shape, dt) in list(sh_shapes.items()) + list(per_shapes.items()):
        IN[k] = nc.dram_tensor(k, list(shape), BF16 if dt == "bf16" else F32, kind="ExternalInput").ap()
    out = nc.dram_tensor("out", [TL, D], F32, kind="ExternalOutput").ap()

    def scratch(name, shape, dt):
        kind = "ExternalOutput" if name in debug else "Internal"
        return nc.dram_tensor(name, list(shape), dt, kind=kind).ap()

    SC = {}
    SC["H1"] = scratch("H1", [T, D], F32)
    SC["SZT"] = scratch("SZT", [1024, T], BF16)
    SC["CATT"] = scratch("CATT", [1024, T], BF16)
    SC["QT"] = scratch("QT", [8, 96, T], BF16)
    SC["KT"] = scratch("KT", [8, 96, T], BF16)
    SC["V"] = scratch("V", [T, 512], BF16)
    SC["MQK"] = scratch("MQK", [1024, T], F32)
    SC["MQKB"] = scratch("MQKB", [1024, T], BF16)
    SC["GI"] = scratch("GI", [8, T], F32)
    SC["GF"] = scratch("GF", [8, T], F32)
    SC["MV"] = scratch("MV", [T, 512], BF16)
    SC["MO"] = scratch("MO", [T, 512], BF16)
    SC["HM"] = scratch("HM", [2, T, 512], F32)
    SC["RD"] = scratch("RD", [16, 512], F32)
    SC["LG"] = scratch("LG", [2, 256, T], F32)
    SC["NQ"] = scratch("NQ", [512, T], BF16)
    SC["NK"] = scratch("NK", [512, T], BF16)

    with ExitStack() as es0:
        P = Prog(nc, es0)
        A0 = Alloc(nc, es0)
        ident_bf = A0.sb([128, 128], BF16, "identbf")
        ident_f = A0.sb([128, 128], F32, "identf")
        ones_f = A0.sb([128, 128], F32, "onesf")
        P.dma("sp", ident_bf[:], IN["ident_bf"][:, :], writes=[ident_bf])
        P.dma("sp", ident_f[:], IN["ident_f"][:, :], writes=[ident_f])
        P.op("pool", "memset", [], [ones_f], ones_f[:], 1.0)
        affA = [A0.sb([128, 8, 2], F32, "affA%d" % l) for l in range(2)]
        affB = [A0.sb([128, 8, 2], F32, "affB%d" % l) for l in range(2)]
        gateR = [[A0.sb([128, 1024], F32, "gateR%d_%d" % (l, s)) for s in range(2 if l == 0 else 1)] for l in range(2)]

        esA0 = es0.enter_context(ExitStack())
        Aw0 = Alloc(nc, esA0)
        w0 = Aw0.sb([128, 8, 3808], BF16, "w_in0")
        w_uq0 = Aw0.sb([128, 3, 1536], BF16, "w_uq0")
        w_ukv0 = Aw0.sb([128, 2, 1024], BF16, "w_ukv0")
        stgA = [Aw0.sb([128, 1024], F32, "stgA%d" % i) for i in range(2)]

        def w0_loader():
            i = 0
            for c0 in range(0, 3808, 128):
                cw = min(128, 3808 - c0)
                s = stgA[i % 2]
                sv = s[:, :].rearrange("p (k n) -> p k n", k=8)
                P.dma("sp" if i % 2 == 0 else "pool", sv[:, :, 0:cw],
                      IN["l0_w_in"][:, c0:c0 + cw].rearrange("(k p) n -> p k n", p=128), writes=[s])
                P.op("dve" if i % 2 == 0 else "act", "tensor_copy" if i % 2 == 0 else "copy", [s], [w0],
                     out=w0[:, :, c0:c0 + cw], in_=sv[:, :, 0:cw])
                i += 1
                yield
            for kk in range(3):
                for hf in range(2):
                    s = stgA[i % 2]
                    P.dma("sp" if i % 2 == 0 else "pool", s[:, 0:768], IN["l0_w_uq"][kk * 128:(kk + 1) * 128, hf * 768:(hf + 1) * 768],
                          writes=[s])
                    P.op("dve" if i % 2 == 0 else "act", "tensor_copy" if i % 2 == 0 else "copy", [s], [w_uq0],
                         out=w_uq0[:, kk, hf * 768:(hf + 1) * 768], in_=s[:, 0:768])
                    i += 1
                    yield
            for kk in range(2):
                s = stgA[i % 2]
                P.dma("sp" if i % 2 == 0 else "pool", s[:, :], IN["l0_w_ukv"][kk * 128:(kk + 1) * 128, :], writes=[s])
                P.op("dve" if i % 2 == 0 else "act", "tensor_copy" if i % 2 == 0 else "copy", [s], [w_ukv0],
                     out=w_ukv0[:, kk, :], in_=s[:, :])
                i += 1
                yield

        wgen = w0_loader()
        with ExitStack() as es:
            A = Alloc(nc, es)
            cv = A.sb([128, 8, 2], F32, "cv")
            sc = A.sb([128, 8, 2], F32, "sc")
            screp = [A.sb([128, 8, 128], F32, "screp%d" % s) for s in range(2)]
            P.dma("sp", cv[:], IN["cvec"][:, :, :], writes=[cv])
            P.op("act", "activation", [cv], [sc], out=sc[:], in_=cv[:], func=AF.Silu)
            for s in range(2):
                for k in range(8):
                    P.op("dve", "tensor_copy", [sc], [screp[s]], out=screp[s][:, k, :],
                         in_=sc[:, k, s:s + 1].to_broadcast([128, 128]))
            wpan = [A.sb([128, 8, 384], F32, "wpan%d" % i) for i in range(2)]
            wgate = [A.sb([128, 512], F32, "wgate%d" % i) for i in range(3)]
            pm = A.ps([128, 24, 2], F32, "pm")
            pg = [A.ps([128, 512], F32, "pg%d" % i) for i in range(2)]
            bmT = A.sb([128, 24], F32, "bmT")
            gT = A.sb([128, 8], F32, "gT")
            modT = A.sb([128, 24, 2], F32, "modT")
            bgrow = A.sb([128, 1024], F32, "bgrow")
            for l in range(2):
                wm = IN["l%d_w_mod" % l]
                P.dma("sp", bmT[:], IN["l%d_bmodT" % l][:, :], writes=[bmT])
                P.dma("sp", gT[:], IN["l%d_gT" % l][:, :], writes=[gT])
                P.dma("sp", bgrow[:], IN["l%d_bmod_gate" % l][0:1, :].to_broadcast([128, 1024]), writes=[bgrow])
                for pn in range(8):
                    wp = wpan[pn % 2]
                    P.dma("sp" if pn % 2 == 0 else "pool", wp[:],
                          wm[:, pn * 384:(pn + 1) * 384].rearrange("(k p) n -> p k n", p=128), writes=[wp])
                    for j in range(3):
                        n = pn * 3 + j
                        for k in range(8):
                            P.op("pe", "matmul", [wp, sc], [pm], pm[:, n, :], wp[:, k, j * 128:(j + 1) * 128],
                                 sc[:, k, :], start=(k == 0), stop=(k == 7))
                    for _ in range(3):
                        next(wgen, None)
                P.op("dve", "tensor_tensor", [pm, bmT], [modT], out=modT[:], in0=pm[:],
                     in1=bmT[:].unsqueeze(2).to_broadcast([128, 24, 2]), op=ALU.add)
                P.op("dve", "tensor_scalar", [modT], [affA[l]], out=affA[l][:], in0=modT[:, 8:16, :], scalar1=1.0,
                     scalar2=None, op0=ALU.add)
                P.op("dve", "tensor_tensor", [affA[l], gT], [affA[l]], out=affA[l][:], in0=affA[l][:],
                     in1=gT[:].unsqueeze(2).to_broadcast([128, 8, 2]), op=ALU.mult)
                P.op("dve", "tensor_copy", [modT], [affB[l]], out=affB[l][:], in_=modT[:, 0:8, :])
                for s in range(len(gateR[l])):
                    for hf in range(2):
                        ps = pg[hf]
                        for k in range(8):
                            wg = wgate[(hf * 8 + k) % 3]
                            P.dma("sp" if k % 2 == 0 else "pool", wg[:],
                                  wm[k * 128:(k + 1) * 128, 2048 + hf * 512:2048 + (hf + 1) * 512], writes=[wg])
                            P.op("pe", "matmul", [wg, screp[s]], [ps], ps[:], screp[s][:, k, :], wg[:],
                                 start=(k == 0), stop=(k == 7))
                        P.op("dve", "tensor_tensor", [ps, bgrow], [gateR[l][s]],
                             out=gateR[l][s][:, hf * 512:(hf + 1) * 512], in0=ps[:],
                             in1=bgrow[:, hf * 512:(hf + 1) * 512], op=ALU.add)
            for _ in wgen:
                pass
            P.barrier()
        if stage <= 0:
            dbg = nc.dram_tensor("dbg_mod", [128, 2, 2, 8, 2], F32, kind="ExternalOutput").ap()
            dbg2 = nc.dram_tensor("dbg_gate", [128, 1024], F32, kind="ExternalOutput").ap()
            for l in range(2):
                P.dma("sp", dbg[:, l, 0], affA[l][:], reads=[affA[l]])
                P.dma("sp", dbg[:, l, 1], affB[l][:], reads=[affB[l]])
            P.dma("sp", dbg2[:, :], gateR[0][1][:], reads=[gateR[0][1]])
            P.barrier()
            return nc

        phase_A(nc, P, IN, SC, 0, affA[0], affB[0], ident_bf, ones_f, w_pre=(w0, w_uq0, w_ukv0))
        esA0.close()
        if stage <= 1:
            return nc
        if 2 not in skip:
            mla_groups = [(0, 256, [0, 1], 0)] + [(256 + 512 * g, 512, list(range(NT)), 0) for g in range(8)]
            attention(nc, P, SC, ones_f, 8, 96, 96.0 ** -0.5, lambda h: SC["QT"][h, :, :], lambda h: SC["KT"][h, :, :],
                      SC["V"], 0, mla_groups)
        if stage <= 2:
            return nc
        if 3 not in skip:
            mlstm_phase(nc, P, IN, SC, ident_bf, ident_f, ones_f)
        if stage <= 3:
            return nc
        combine_phase(nc, P, IN, SC, ident_bf, SC["HM"][0], SC["HM"][1], SC["MO"], "l0_hnorm", 512, GROUPS)
        if stage <= 4:
            return nc
        phase_C(nc, P, IN, SC, 0, gateR[0], out)
        if stage <= 5:
            return nc
        phase_A(nc, P, IN, SC, 1, affA[1], affB[1], ident_bf, ones_f)
        if stage <= 6:
            return nc
        if 7 not in skip:
            gla_phase(nc, P, IN, SC, ident_bf)
            combine_phase(nc, P, IN, SC, ident_bf, SC["HM"][0], SC["HM"][1], None, "l1_gnorm", 0, GROUPS[1:])
        if stage <= 7:
            return nc
        if 8 not in skip:
            na_groups = []
            for g in range(8):
                key, t_lo, nt_ = _na_cfg(g)
                loc = []
                for r in range(nt_):
                    u0, u1 = _na_range(g, t_lo + r)
                    loc.append((2 + t_lo + r, u0 * 64, (u1 + 1) * 64))
                na_groups.append((256 + 512 * g, 512, [0, 1] + loc, g))
            attention(nc, P, SC, ones_f, 8, 64, 64.0 ** -0.5, lambda h: SC["NQ"][h * 64:(h + 1) * 64, :],
                      lambda h: SC["NK"][h * 64:(h + 1) * 64, :], SC["V"], 512, na_groups, bias_fn=na_bias_fn(nc, P, IN, {}),
                      ident_bf=ident_bf, early_release=True, act_recip=True)
        if stage <= 8:
            return nc
        phase_C(nc, P, IN, SC, 1, gateR[1], out)
    return nc


def phase_A(nc, P, IN, SC, layer, affA, affB, ident_bf, ones_f, w_pre=None):
    NW = 3808 if layer == 0 else 3616
    w_in_d = IN["l%d_w_in" % layer]
    with ExitStack() as es:
        A = Alloc(nc, es)
        if w_pre is not None:
            w_in, w_uq, w_ukv = w_pre
        else:
            w_in = A.sb([128, 8, NW], BF16, "w_in")
            if layer == 0:
                w_uq = A.sb([128, 3, 1536], BF16, "w_uq")
                w_ukv = A.sb([128, 2, 1024], BF16, "w_ukv")
        with ExitStack() as es2:
            A2 = Alloc(nc, es2)
            stg = [A2.sb([128, 8, 512], F32, "stg%d" % i) for i in range(2)] if w_pre is None else None
            i = 0
            for c0 in (range(0, NW, 512) if w_pre is None else ()):
                cw = min(512, NW - c0)
                s = stg[i % 2]
                P.dma("sp" if i % 2 == 0 else "pool", s[:, :, 0:cw],
                      w_in_d[:, c0:c0 + cw].rearrange("(k p) n -> p k n", p=128), writes=[s])
                P.op("dve" if i % 2 == 0 else "act", "tensor_copy" if i % 2 == 0 else "copy", [s], [w_in],
                     out=w_in[:, :, c0:c0 + cw], in_=s[:, :, 0:cw])
                i += 1
            if layer == 0 and w_pre is None:
                s = stg[i % 2]
                for kk in range(3):
                    s = stg[i % 2]
                    P.dma("sp", s[:, 0:3, :], IN["l0_w_uq"][kk * 128:(kk + 1) * 128, :].rearrange("p (a n) -> p a n", a=3),
                          writes=[s])
                    P.op("dve", "tensor_copy", [s], [w_uq], out=w_uq[:, kk, :].rearrange("p (a n) -> p a n", a=3),
                         in_=s[:, 0:3, :])
                    i += 1
                s = stg[i % 2]
                for kk in range(2):
                    P.dma("sp", s[:, 2 * kk:2 * kk + 2, :],
                          IN["l0_w_ukv"][kk * 128:(kk + 1) * 128, :].rearrange("p (a n) -> p a n", a=2), writes=[s])
                P.op("dve", "tensor_copy", [s], [w_ukv], out=w_ukv[:].rearrange("p k (a n) -> p (k a) n", a=2),
                     in_=s[:, 0:4, :])
                i += 1
            P.barrier()
        if layer == 0:
            qnT = A.sb([128, 3], F32, "qnT")
            kvnT = A.sb([128, 2], F32, "kvnT")
            P.dma("sp", qnT[:], IN["l0_qnT"][:, :], writes=[qnT])
            P.dma("sp", kvnT[:], IN["l0_kvnT"][:, :], writes=[kvnT])
            cqT = A.sb([128, 3, 512], F32, "cqT")
            ckvT = A.sb([128, 2, 512], F32, "ckvT")
            sq = A.sb([128, 3, 512], F32, "sq")
            rstd = A.sb([128, 512], F32, "rstd")
            cqn = A.sb([128, 3, 512], BF16, "cqn")
            ckvn = A.sb([128, 2, 512], BF16, "ckvn")
            rC = A.sb([128, 512], F32, "rC")
            rS = A.sb([128, 512], F32, "rS")
            rt1 = A.sb([128, 512], F32, "rt1")
            rt2 = A.sb([128, 512], F32, "rt2")
            qo = [A.sb([128, 512], BF16, "qo%d" % i) for i in range(2)]
            kro = A.sb([32, 512], BF16, "kro")
        else:
            gaT = [A.sb([16, 512], F32, "gaT%d" % d) for d in range(2)]
            wg = A.sb([16, 2, 256], F32, "wg")
            P.dma("sp", wg[:], IN["l1_w_gate"].rearrange("d r k -> r d k"), writes=[wg])
            bgT = A.sb([128, 2, 2], F32, "bgT")
            nbg = A.sb([128, 2, 2], F32, "nbg")
            P.dma("sp", bgT[:], IN["l1_bgT"][:, :, :], writes=[bgT])
            P.op("dve", "tensor_scalar", [bgT], [nbg], out=nbg[:], in0=bgT[:], scalar1=-1.0, scalar2=None, op0=ALU.mult)
            one1 = A.sb([128, 1], F32, "one1a")
            P.op("pool", "memset", [], [one1], one1[:], 1.0)
            lge = A.sb([128, 512], F32, "lge")
            lgo = [A.sb([128, 512], F32, "lgo%d" % i) for i in range(2)]
        hb = [A.sb([128, 1024], F32, "hb%d" % i) for i in range(3)]
        junk = A.sb([128, 1024], F32, "junk")
        st = [A.sb([128, 4], F32, "st%d" % i) for i in range(2)]
        xn2 = [[A.sb([128, 1024], BF16, "xn%d_%d" % (s_, i)) for i in range(4)] for s_ in range(2)]
        epsT = A.sb([128, 1], F32, "epsT")
        P.op("pool", "memset", [], [epsT], epsT[:], EPS)
        uT = [A.sb([128, 8, 512], BF16, "uT%d" % i) for i in range(2)]
        fo_bf = [A.sb([128, 512], BF16, "fobf%d" % i) for i in range(4)]
        fo_f = [A.sb([128, 512], F32, "fof%d" % i) for i in range(3)]
        tp = [A.ps([128, 512], BF16, "tp%d" % i) for i in range(2)]
        acc = [A.ps([128, 512], F32, "acc%d" % i) for i in range(5)]
        cnt = {"acc": 0, "fobf": 0, "fof": 0, "ev": 0, "q": 0, "hb": 0, "xn": 0, "tp": 0}

        def nxt(name, lst):
            r = lst[cnt[name] % len(lst)]
            cnt[name] += 1
            return r

        def evac_engine():
            cnt["ev"] += 1
            return "dve" if cnt["ev"] % 2 == 0 else "act"

        def copy_op(eng, src_t, src_ap, dst_t, dst_ap):
            if eng == "act":
                P.op("act", "copy", [src_t], [dst_t], out=dst_ap, in_=src_ap)
            else:
                P.op(eng, "tensor_copy", [src_t], [dst_t], out=dst_ap, in_=src_ap)

        def stq():
            cnt["q"] += 1
            return "pool" if cnt["q"] % 2 == 0 else "sp"

        def norm_part(gi):
            t0, n = GROUPS[gi]
            ntl = n // 128
            sta = st[gi % 2]
            xn = xn2[gi % 2]
            for ti in range(ntl):
                h = nxt("hb", hb)
                tok = t0 + ti * 128
                if layer == 0:
                    src = IN["ctx"][tok:tok + 128, :] if gi == 0 else IN["x"][tok - TC:tok - TC + 128, :]
                else:
                    src = SC["H1"][tok:tok + 128, :]
                P.dma("sp", h[:], src, writes=[h])
                P.op("act", "activation", [h], [junk, sta], out=junk[:], in_=h[:], func=AF.Square,
                     accum_out=sta[:, ti:ti + 1])
                P.op("act", "activation", [sta, epsT], [sta], out=sta[:, ti:ti + 1], in_=sta[:, ti:ti + 1], func=AF.Sqrt,
                     scale=1.0 / D, bias=epsT[:, 0:1])
                P.op("dve", "reciprocal", [sta], [sta], out=sta[:, ti:ti + 1], in_=sta[:, ti:ti + 1])
                x_ = xn[ti]
                P.op("dve", "tensor_scalar", [h, sta], [x_], out=x_[:], in0=h[:], scalar1=sta[:, ti:ti + 1],
                     scalar2=None, op0=ALU.mult)

        def transpose_part(gi):
            t0, n = GROUPS[gi]
            ntl = n // 128
            s = 1 if gi == 0 else 0
            u = uT[gi % 2]
            xn = xn2[gi % 2]
            for j in range(8):
                tpp = nxt("tp", tp)
                for ti in range(ntl):
                    P.op("pe", "transpose", [xn[ti], ident_bf], [tpp], tpp[:, ti * 128:(ti + 1) * 128],
                         xn[ti][:, j * 128:(j + 1) * 128], ident_bf[:])
                P.op("dve", "tensor_scalar", [tpp, affA, affB], [u], out=u[:, j, 0:n],
                     in0=tpp[:, 0:n], scalar1=affA[:, j, s:s + 1], scalar2=affB[:, j, s:s + 1], op0=ALU.mult,
                     op1=ALU.add)


        def proj_part(gi):
            t0, n = GROUPS[gi]
            ntl = n // 128
            u = uT[gi % 2]

            def fm_proj(c0, ncol):
                ps = nxt("acc", acc)
                for k in range(8):
                    P.op("pe", "matmul", [w_in, u], [ps], ps[0:ncol, 0:n], w_in[:, k, c0:c0 + ncol], u[:, k, 0:n],
                         start=(k == 0), stop=(k == 7))
                return ps

            def store_fm(ps, ncol, dst, dt, func=None, eng=None):
                o = nxt("fobf", fo_bf) if dt == BF16 else nxt("fof", fo_f)
                if func is not None:
                    P.op("act", "activation", [ps], [o], out=o[0:ncol, 0:n], in_=ps[0:ncol, 0:n], func=func)
                else:
                    copy_op(eng or evac_engine(), ps, ps[0:ncol, 0:n], o, o[0:ncol, 0:n])
                P.dma(stq(), dst, o[0:ncol, 0:n], reads=[o])

            tsl = slice(t0, t0 + n)
            if layer == 0:
                for j in range(3):
                    ps = fm_proj(j * 128, 128)
                    copy_op(evac_engine(), ps, ps[:, 0:n], cqT, cqT[:, j, 0:n])
                for j in range(2):
                    ps = fm_proj(384 + j * 128, 128)
                    copy_op(evac_engine(), ps, ps[:, 0:n], ckvT, ckvT[:, j, 0:n])
                for (src_t, nk, nrm, dst_t, dim) in ((cqT, 3, qnT, cqn, 384.0), (ckvT, 2, kvnT, ckvn, 256.0)):
                    P.op("act", "activation", [src_t], [sq], out=sq[:, 0:nk, 0:n], in_=src_t[:, 0:nk, 0:n], func=AF.Square)
                    ps = nxt("acc", acc)
                    for k in range(nk):
                        P.op("pe", "matmul", [ones_f, sq], [ps], ps[:, 0:n], ones_f[:], sq[:, k, 0:n], start=(k == 0),
                             stop=(k == nk - 1))
                    P.op("act", "activation", [ps, epsT], [rstd], out=rstd[:, 0:n], in_=ps[:, 0:n], func=AF.Sqrt,
                         scale=1.0 / dim, bias=epsT[:, 0:1])
                    P.op("dve", "reciprocal", [rstd], [rstd], out=rstd[:, 0:n], in_=rstd[:, 0:n])
                    for k in range(nk):
                        P.op("dve", "scalar_tensor_tensor", [src_t, nrm, rstd], [dst_t], out=dst_t[:, k, 0:n],
                             in0=src_t[:, k, 0:n], scalar=nrm[:, k:k + 1], in1=rstd[:, 0:n], op0=ALU.mult, op1=ALU.mult)
                rot = gi > 0
                if rot:
                    P.dma("sp", rC[:, 0:n], IN["ropeC"][:, t0 - TC:t0 - TC + n], writes=[rC])
                    P.dma("sp", rS[:, 0:n], IN["ropeS"][:, t0 - TC:t0 - TC + n], writes=[rS])
                for hh in range(8):
                    ps = nxt("acc", acc)
                    for k in range(3):
                        P.op("pe", "matmul", [w_uq, cqn], [ps], ps[0:96, 0:n], w_uq[:, k, hh * 192:hh * 192 + 96],
                             cqn[:, k, 0:n], start=(k == 0), stop=(k == 2))
                    o = nxt("fobf", fo_bf)
                    if rot:
                        ps2 = nxt("acc", acc)
                        for k in range(3):
                            P.op("pe", "matmul", [w_uq, cqn], [ps2], ps2[0:96, 0:n],
                                 w_uq[:, k, hh * 192 + 96:hh * 192 + 192], cqn[:, k, 0:n], start=(k == 0), stop=(k == 2))
                        copy_op("act", ps, ps[0:64, 0:n], o, o[0:64, 0:n])
                        P.op("dve", "tensor_tensor", [ps, rC], [rt1], out=rt1[64:96, 0:n], in0=ps[64:96, 0:n],
                             in1=rC[64:96, 0:n], op=ALU.mult)
                        P.op("dve", "tensor_tensor", [ps2, rS], [rt2], out=rt2[64:96, 0:n], in0=ps2[64:96, 0:n],
                             in1=rS[64:96, 0:n], op=ALU.mult)
                        P.op("pool", "tensor_tensor", [rt1, rt2], [o], out=o[64:96, 0:n], in0=rt1[64:96, 0:n],
                             in1=rt2[64:96, 0:n], op=ALU.add)
                    else:
                        copy_op(evac_engine(), ps, ps[0:96, 0:n], o, o[0:96, 0:n])
                    P.dma(stq(), SC["QT"][hh, :, tsl], o[0:96, 0:n], reads=[o])
                for c in range(4):
                    ps = nxt("acc", acc)
                    for k in range(2):
                        P.op("pe", "matmul", [w_ukv, ckvn], [ps], ps[:, 0:n], w_ukv[:, k, c * 128:(c + 1) * 128],
                             ckvn[:, k, 0:n], start=(k == 0), stop=(k == 1))
                    o = nxt("fobf", fo_bf)
                    copy_op(evac_engine(), ps, ps[:, 0:n], o, o[:, 0:n])
                    for hh in range(2):
                        P.dma(stq(), SC["KT"][c * 2 + hh, 0:64, tsl], o[hh * 64:(hh + 1) * 64, 0:n], reads=[o])
                for ti in range(ntl):
                    ps = nxt("acc", acc)
                    for k in range(2):
                        P.op("pe", "matmul", [w_ukv, ckvn], [ps], ps[:, :], ckvn[:, k, ti * 128:(ti + 1) * 128],
                             w_ukv[:, k, 512:1024], start=(k == 0), stop=(k == 1))
                    o = nxt("fobf", fo_bf)
                    copy_op(evac_engine(), ps, ps[:, :], o, o[:, :])
                    P.dma(stq(), SC["V"][t0 + ti * 128:t0 + (ti + 1) * 128, :], o[:, :], reads=[o])
                ps = fm_proj(640, 32)
                if rot:
                    ps2 = fm_proj(3760, 32)
                    P.op("dve", "tensor_tensor", [ps, rC], [rt1], out=rt1[0:32, 0:n], in0=ps[0:32, 0:n], in1=rC[0:32, 0:n],
                         op=ALU.mult)
                    P.op("dve", "tensor_tensor", [ps2, rS], [rt2], out=rt2[0:32, 0:n], in0=ps2[0:32, 0:n],
                         in1=rS[0:32, 0:n], op=ALU.mult)
                    P.op("pool", "tensor_tensor", [rt1, rt2], [kro], out=kro[0:32, 0:n], in0=rt1[0:32, 0:n],
                         in1=rt2[0:32, 0:n], op=ALU.add)
                else:
                    copy_op("dve", ps, ps[0:32, 0:n], kro, kro[0:32, 0:n])
                for hh in range(8):
                    P.dma(stq(), SC["KT"][hh, 64:96, tsl], kro[0:32, 0:n], reads=[kro])
                yield
                for c in range(8):
                    ps = fm_proj(672 + c * 128, 128)
                    store_fm(ps, 128, SC["MQKB"][c * 128:(c + 1) * 128, tsl], BF16)
                ps = fm_proj(3792, 8)
                store_fm(ps, 8, SC["GI"][:, tsl], F32)
                ps = fm_proj(3800, 8)
                store_fm(ps, 8, SC["GF"][:, tsl], F32)
                for c in range(8):
                    ps = fm_proj(2736 + c * 128, 128)
                    store_fm(ps, 128, SC["SZT"][c * 128:(c + 1) * 128, tsl], BF16, func=AF.Silu)
                tm_specs = [(1696, SC["MV"], None), (2208, SC["MO"], AF.Sigmoid)]
            else:
                for c in range(4):
                    ps = fm_proj(c * 128, 128)
                    store_fm(ps, 128, SC["MQK"][c * 128:(c + 1) * 128, tsl], F32)
                yield
                for d in range(2):
                    ps = fm_proj(1024 + 16 * d, 16)
                    copy_op("dve", ps, ps[0:16, 0:n], gaT[d], gaT[d][0:16, 0:n])
                for d in range(2):
                    for c2 in range(2):
                        ps = nxt("acc", acc)
                        P.op("pe", "matmul", [wg, gaT[d]], [ps], ps[:, 0:n], wg[0:16, d, c2 * 128:(c2 + 1) * 128],
                             gaT[d][0:16, 0:n], start=True, stop=True)
                        P.op("act", "activation", [ps, nbg], [lge], out=lge[:, 0:n], in_=ps[:, 0:n], func=AF.Exp, scale=-1.0,
                             bias=nbg[:, d, c2:c2 + 1])
                        P.op("act", "activation", [lge, one1], [lge], out=lge[:, 0:n], in_=lge[:, 0:n], func=AF.Ln,
                             bias=one1[:, 0:1])
                        o = lgo[(d * 2 + c2) % 2]
                        P.op("dve", "tensor_scalar", [lge], [o], out=o[:, 0:n], in0=lge[:, 0:n], scalar1=-1.0 / 16.0,
                             scalar2=None, op0=ALU.mult)
                        P.dma(stq(), SC["LG"][d, c2 * 128:(c2 + 1) * 128, tsl], o[:, 0:n], reads=[o])
                for c in range(4):
                    ps = fm_proj(1056 + c * 128, 128)
                    store_fm(ps, 128, SC["NQ"][c * 128:(c + 1) * 128, tsl], BF16)
                for c in range(4):
                    ps = fm_proj(1568 + c * 128, 128)
                    store_fm(ps, 128, SC["NK"][c * 128:(c + 1) * 128, tsl], BF16)
                for c in range(8):
                    ps = fm_proj(2592 + c * 128, 128)
                    store_fm(ps, 128, SC["SZT"][c * 128:(c + 1) * 128, tsl], BF16, func=AF.Silu)
                tm_specs = [(512, SC["MV"], None), (2080, SC["V"], None)]
            for (c0, dst, func) in tm_specs:
                for ti in range(ntl):
                    ps = nxt("acc", acc)
                    for k in range(8):
                        P.op("pe", "matmul", [w_in, u], [ps], ps[:, :], u[:, k, ti * 128:(ti + 1) * 128],
                             w_in[:, k, c0:c0 + 512], start=(k == 0), stop=(k == 7))
                    o = nxt("fobf", fo_bf)
                    if func is not None:
                        P.op("act", "activation", [ps], [o], out=o[:, :], in_=ps[:, :], func=func)
                    else:
                        copy_op(evac_engine(), ps, ps[:, :], o, o[:, :])
                    P.dma(stq(), dst[t0 + ti * 128:t0 + (ti + 1) * 128, :], o[:, :], reads=[o])

        norm_part(0)
        transpose_part(0)
        for gi in range(len(GROUPS)):
            if gi + 1 < len(GROUPS):
                norm_part(gi + 1)
            gen = proj_part(gi)
            next(gen)
            if gi + 1 < len(GROUPS):
                transpose_part(gi + 1)
            for _ in gen:
                pass
        P.barrier()


def attention(nc, P, SC, ones_f, heads, dq, scale, load_q, load_k, Vd, cat_row0, groups, bias_fn=None, ident_bf=None,
              early_release=False, act_recip=False):
    LOOK = 4
    NS = 5
    EPI_DELAY = 8
    with ExitStack() as es:
        A = Alloc(nc, es)
        V = A.sb([128, NT, heads, 65], BF16, "Vall")
        P.op("pool", "memset", [], [V], V[:, :, :, 64:65], 1.0)
        for half in range(2):
            tl = slice(half * 17, (half + 1) * 17)
            for hh in range(heads):
                P.dma("sp" if hh % 2 == 0 else "pool", V[:, tl, hh, 0:64],
                      Vd[half * 17 * 128:(half + 1) * 17 * 128, hh * 64:(hh + 1) * 64].rearrange("(t p) d -> p t d", p=128),
                      writes=[V])
        kT = [A.sb([128, T], BF16, "kT%d" % i) for i in range(2)]
        qT = [A.sb([128, T], BF16, "qT%d" % i) for i in range(2)]
        pt = [A.sb([128, 512], BF16, "pt%d" % i) for i in range(NS)]
        sb_t = [A.sb([128, 512], F32, "sbt%d" % i) for i in range(3)] if bias_fn is not None else None
        rden = [A.sb([128, 512], F32, "rden%d" % i) for i in range(2)]
        ocp = [A.sb([128, 512], F32, "ocp%d" % i) for i in range(3)] if early_release else None
        rsc = A.sb([128, 512], F32, "rsc")
        bcs = [A.sb([128, 512], F32, "bcs%d" % i) for i in range(2)]
        szt = [A.sb([64, 512], BF16, "szt%d" % i) for i in range(3)]
        tmp = [A.sb([64, 512], F32, "atmp%d" % i) for i in range(2)]
        ao = [A.sb([64, 512], BF16, "ao%d" % i) for i in range(2)]
        Sps = [A.ps([128, 512], F32, "Sps%d" % i) for i in range(NS)]
        Ops = [A.ps([128, 512], F32, "Ops%d" % i) for i in range(2)]
        Bps = A.ps([128, 512], F32, "Bps")
        if dq < 128:
            for t_ in kT + qT:
                P.op("pool", "memset", [], [t_], t_[64:128, :], 0.0)
        P.dma("sp", kT[0][0:dq, :], load_k(0), writes=[kT[0]])
        P.dma("pool", qT[0][0:dq, :], load_q(0), writes=[qT[0]])
        gcount = 0
        it = 0
        rd_dep = [DramDep() for _ in range(16)]
        pend = []
        for h in range(heads):
            k_ = kT[h % 2]
            q_ = qT[h % 2]
            if h + 1 < heads:
                P.dma("sp", kT[(h + 1) % 2][0:dq, :], load_k(h + 1), writes=[kT[(h + 1) % 2]])
                P.dma("pool", qT[(h + 1) % 2][0:dq, :], load_q(h + 1), writes=[qT[(h + 1) % 2]])
            bias_loader = None
            if bias_fn is not None:
                if h == 0:
                    bias_cur, ld0 = bias_fn(0, A)
                    for _ in ld0:
                        pass
                bias_tiles = bias_cur
                if h + 1 < heads:
                    bias_cur, bias_loader = bias_fn(h + 1, A if h == 0 else None)
            else:
                bias_tiles = None
            r0 = cat_row0 + h * 64
            items = []
            for (q0, n, tiles, gkey) in groups:
                gid = gcount
                gcount += 1
                for j, kt in enumerate(tiles):
                    if isinstance(kt, tuple):
                        kt, c0, c1 = kt
                    else:
                        c0, c1 = 0, n
                    items.append((gid, q0, n, gkey, j, kt, len(tiles), c0, c1))

            def flush(cond):
                for e_ in pend[:]:
                    if cond(e_[1][0]):
                        emit_epi(*e_[1])
                        pend.remove(e_)

            def emit_S(item, slot):
                gid, q0, n, gkey, j, kt, nt_, c0, c1 = item
                S = Sps[slot % NS]
                p_ = pt[slot % NS]
                if j == 0:
                    flush(lambda g2: g2 % 3 == gid % 3)
                    sz = szt[gid % 3]
                    P.dma("sp", sz[:, 0:n], SC["SZT"][r0:r0 + 64, q0:q0 + n], writes=[sz])
                P.op("pe", "matmul", [k_, q_], [S], S[:, c0:c1], k_[:, kt * 128:(kt + 1) * 128], q_[:, q0 + c0:q0 + c1],
                     start=True, stop=True)
                bt = bias_tiles.get((gkey, kt)) if bias_tiles is not None else None
                if bt is not None:
                    sb = sb_t[slot % 3]
                    P.op("dve", "scalar_tensor_tensor", [S, bt], [sb], out=sb[:, c0:c1], in0=S[:, c0:c1], scalar=scale,
                         in1=bt[:, c0:c1], op0=ALU.mult, op1=ALU.add)
                    P.op("act", "activation", [sb], [p_], out=p_[:, c0:c1], in_=sb[:, c0:c1], func=AF.Exp)
                else:
                    P.op("act", "activation", [S], [p_], out=p_[:, c0:c1], in_=S[:, c0:c1], func=AF.Exp, scale=scale)

            def emit_PV(item, slot):
                gid, q0, n, gkey, j, kt, nt_, c0, c1 = item
                O = Ops[gid % 2]
                p_ = pt[slot % NS]
                assert j > 0 or (c0 == 0 and c1 == n)
                if j == 0:
                    flush(lambda g2: g2 % 2 == gid % 2)
                P.op("pe", "matmul", [V, p_], [O], O[0:65, c0:c1], V[:, kt, h, :], p_[:, c0:c1], start=(j == 0),
                     stop=(j == nt_ - 1))
                if j == nt_ - 1:
                    rd = rden[gid % 2]
                    if early_release:
                        oc = ocp[gid % 3]
                        P.op("act", "copy", [O], [oc], out=oc[0:65, 0:n], in_=O[0:65, 0:n])
                        P.op("act", "activation", [oc], [rsc], out=rsc[64:65, 0:n], in_=oc[64:65, 0:n], func=AF.Ln)
                        P.op("act", "activation", [rsc], [rd], out=rd[64:65, 0:n], in_=rsc[64:65, 0:n], func=AF.Exp, scale=-1.0)
                    elif act_recip:
                        P.op("act", "activation", [O], [rsc], out=rsc[64:65, 0:n], in_=O[64:65, 0:n], func=AF.Ln)
                        P.op("act", "activation", [rsc], [rd], out=rd[64:65, 0:n], in_=rsc[64:65, 0:n], func=AF.Exp, scale=-1.0)
                    else:
                        P.op("dve", "reciprocal", [O], [rd], out=rd[64:65, 0:n], in_=O[64:65, 0:n])
                    pend.append([EPI_DELAY, (gid, q0, n, r0, h)])

            def emit_epi(gid, q0, n, r0, h):
                O = Ops[gid % 2]
                rd = rden[gid % 2]
                bc_ = bcs[gid % 2]
                tm_ = tmp[gid % 2]
                a_ = ao[gid % 2]
                sz = szt[gid % 3]
                P.op("pe", "matmul", [ones_f, rd], [Bps], Bps[0:64, 0:n], ones_f[64:65, 0:64], rd[64:65, 0:n],
                     start=True, stop=True)
                if early_release:
                    oc = ocp[gid % 3]
                    P.op("dve", "tensor_tensor", [oc, Bps], [tm_], out=tm_[:, 0:n], in0=oc[0:64, 0:n], in1=Bps[0:64, 0:n],
                         op=ALU.mult)
                else:
                    P.op("act", "copy", [Bps], [bc_], out=bc_[0:64, 0:n], in_=Bps[0:64, 0:n])
                    P.op("dve", "tensor_tensor", [O, bc_], [tm_], out=tm_[:, 0:n], in0=O[0:64, 0:n], in1=bc_[0:64, 0:n],
                         op=ALU.mult)
                P.op("pool", "tensor_tensor", [tm_, sz], [a_], out=a_[:, 0:n], in0=tm_[:, 0:n], in1=sz[:, 0:n], op=ALU.mult)
                P.dma("pool", SC["CATT"][r0:r0 + 64, q0:q0 + n], a_[:, 0:n], reads=[a_])

            nI = len(items)
            for idx in range(nI + LOOK):
                if idx < nI:
                    emit_S(items[idx], it + idx)
                for e_ in pend[:]:
                    e_[0] -= 1
                    if e_[0] <= 0:
                        emit_epi(*e_[1])
                        pend.remove(e_)
                if idx - LOOK >= 0:
                    emit_PV(items[idx - LOOK], it + idx - LOOK)
                if bias_loader is not None and idx % 3 == 2:
                    next(bias_loader, None)
            if bias_loader is not None:
                for _ in bias_loader:
                    pass
            it += nI
        for e_ in pend:
            emit_epi(*e_[1])
        P.barrier()


def mlstm_phase(nc, P, IN, SC, ident_bf, ident_f, ones_f):
    NB = NT
    NCH = T // 64
    with ExitStack() as es:
        A = Alloc(nc, es)
        esT = A.sb([128, NB, 64], F32, "esT")
        fT = A.sb([128, NB, 64], F32, "fT")
        decbc = A.sb([128, 8, NCH], F32, "decbc")
        mask = [A.sb([128, 64], F32, "mask%d" % d) for d in range(2)]
        for d in range(2):
            P.op("pool", "memset", [], [mask[d]], mask[d][:], 1.0)
            for half in range(2):
                pr = slice(half * 64, half * 64 + 64)
                P.op("pool", "affine_select", [mask[d]], [mask[d]], out=mask[d][pr, :], in_=mask[d][pr, :],
                     pattern=[[1 if d == 0 else -1, 64]], compare_op=ALU.is_ge, fill=0.0, base=0,
                     channel_multiplier=-1 if d == 0 else 1)
        with ExitStack() as es2:
            A2 = Alloc(nc, es2)
            X1 = A2.sb([64, T], F32, "X1")
            X2 = A2.sb([64, T], F32, "X2")
            X3 = A2.sb([64, T], F32, "X3")
            X4 = A2.sb([64, T], F32, "X4")
            gb = A2.sb([64, 2], F32, "gb")
            nbf = A2.sb([64, 1], F32, "nbf")
            one1 = A2.sb([64, 1], F32, "one1")
            dec = A2.sb([64, NCH], F32, "dec")
            aprev = A2.sb([64, NCH], F32, "aprev")
            sel = A2.sb([64, 128], F32, "sel")
            pst = [A2.ps([128, 8, 64], F32, "pst%d" % i) for i in range(2)]
            psd = A2.ps([128, NCH], F32, "psd")
            P.op("pool", "memset", [], [X1], X1[:], 0.0)
            P.op("pool", "memset", [], [X3], X3[:], 0.0)
            P.op("pool", "memset", [], [one1], one1[:], 1.0)
            P.dma("sp", gb[:], IN["l0_gb2"][:, :], writes=[gb])
            for d in range(2):
                P.dma("sp", X1[d * 32:d * 32 + 4, :], SC["GF"][d * 4:d * 4 + 4, :], writes=[X1])
                P.dma("pool", X3[d * 32:d * 32 + 4, :], SC["GI"][d * 4:d * 4 + 4, :], writes=[X3])
            P.op("dve", "tensor_scalar", [gb], [nbf], out=nbf[:], in0=gb[:, 1:2], scalar1=-1.0, scalar2=None, op0=ALU.mult)
            P.op("act", "activation", [X1, nbf], [X1], out=X1[:], in_=X1[:], func=AF.Exp, scale=-1.0, bias=nbf[:, 0:1])
            P.op("act", "activation", [X1, one1], [X1], out=X1[:], in_=X1[:], func=AF.Ln, bias=one1[:, 0:1])

            def seg_views(tile_, prng, d):
                if d == 0:
                    return [tile_[prng, 0:T]]
                return [tile_[prng, 0:TC][:, ::-1], tile_[prng, TC:T][:, ::-1]]

            def scan(dst, src, op0, d):
                prng = slice(d * 32, d * 32 + 32)
                dv = seg_views(dst, prng, d)
                sv = seg_views(src, prng, d)
                for i in range(len(dv)):
                    init = 0.0 if i == 0 else dst[prng, 0:1]
                    P.op("dve", "tensor_tensor_scan", [src, dst], [dst], out=dv[i], data0=sv[i], data1=sv[i],
                         initial=init, op0=op0, op1=ALU.bypass)

            for d in range(2):
                scan(X2, X1, ALU.add, d)
            P.op("dve", "scalar_tensor_tensor", [X3, gb, X2], [X3], out=X3[:], in0=X3[:], scalar=gb[:, 0:1], in1=X2[:],
                 op0=ALU.add, op1=ALU.add)
            for d in range(2):
                scan(X1, X3, ALU.max, d)
            for d in range(2):
                prng = slice(d * 32, d * 32 + 32)
                jj = 63 if d == 0 else 0
                P.op("dve", "tensor_copy", [X1], [X4], out=X4[prng, :].rearrange("p (c j) -> p c j", j=64),
                     in_=X1[prng, :].rearrange("p (c j) -> p c j", j=64)[:, :, jj:jj + 1].to_broadcast([32, NCH, 64]))
            aend = X4[:, :].rearrange("p (c j) -> p c j", j=64)[:, :, 0]
            P.op("pool", "memset", [], [aprev], aprev[:], 0.0)
            P.op("dve", "tensor_copy", [X4], [aprev], out=aprev[0:32, 1:NCH], in_=aend[0:32, 0:NCH - 1])
            P.op("dve", "tensor_copy", [X4], [aprev], out=aprev[32:64, 0:3], in_=aend[32:64, 1:4])
            P.op("dve", "tensor_copy", [X4], [aprev], out=aprev[32:64, 4:NCH - 1], in_=aend[32:64, 5:NCH])
            P.op("dve", "tensor_copy", [X4], [aprev], out=aprev[32:64, NCH - 1:NCH], in_=aend[32:64, 0:1])
            P.op("dve", "tensor_tensor", [aprev, X4], [dec], out=dec[:], in0=aprev[:], in1=aend, op=ALU.subtract)
            P.op("act", "activation", [dec], [dec], out=dec[:], in_=dec[:], func=AF.Exp)
            P.op("dve", "tensor_tensor", [X3, X4], [X3], out=X3[:], in0=X3[:], in1=X4[:], op=ALU.subtract)
            P.op("act", "activation", [X3], [X3], out=X3[:], in_=X3[:], func=AF.Exp)
            P.op("dve", "tensor_tensor", [X2, X4], [X2], out=X2[:], in0=X2[:], in1=X4[:], op=ALU.subtract)
            P.op("act", "activation", [X2], [X2], out=X2[:], in_=X2[:], func=AF.Exp)
            for (srcX, dstT) in ((X3, esT), (X2, fT)):
                for b0 in range(0, NB, 8):
                    nb = min(8, NB - b0)
                    ps = pst[(b0 // 8) % 2]
                    for bb in range(nb):
                        P.op("pe", "transpose", [srcX, ident_f], [ps], ps[:, bb, :], srcX[:, (b0 + bb) * 128:(b0 + bb + 1) * 128],
                             ident_f[0:64, 0:64])
                    P.op("act", "copy", [ps], [dstT], out=dstT[:, b0:b0 + nb, :], in_=ps[:, 0:nb, :])
            for idx in range(8):
                r = (idx // 4) * 32 + idx % 4
                P.op("dve", "tensor_copy", [ident_f], [sel], out=sel[:], in_=ident_f[0:64, r:r + 1].to_broadcast([64, 128]))
                P.op("pe", "matmul", [sel, dec], [psd], psd[:, :], sel[:, :], dec[:, :], start=True, stop=True)
                P.op("act", "copy", [psd], [decbc], out=decbc[:, idx, :], in_=psd[:, :])
            P.barrier()
        P.op("dve", "tensor_scalar", [esT], [esT], out=esT[:], in0=esT[:], scalar1=128.0 ** -0.5, scalar2=None, op0=ALU.mult)
        xraw = A.sb([128, T], BF16, "xraw")
        cvw = A.sb([128, 8, 4], F32, "cvw")
        P.dma("sp", cvw[:], IN["l0_convT"][:, :, :], writes=[cvw])
        dg = [A.sb([128, 3, 128], BF16, "dg%d" % i) for i in range(2)]
        qT = A.sb([128, T], BF16, "mqT")
        qd = [A.sb([128, T], BF16, "mqd%d" % d) for d in range(2)]
        kT = A.sb([128, T], BF16, "mkT")
        kTok = A.sb([128, NB, 128], BF16, "kTok")
        vtok = A.sb([128, NB, 128], BF16, "vtok")
        vpp = [A.sb([128, NB, 129], BF16, "vpp%d" % d) for d in range(2)]
        SmT = [A.sb([128, NB, 64], BF16, "SmT%d" % d) for d in range(2)]
        hbuf = [A.sb([128, NB, 129], F32, "hbuf%d" % d) for d in range(2)]
        Cst = [[A.sb([128, 129], F32, "C%d_%d" % (d, i)) for i in range(2)] for d in range(2)]
        Cb = [[A.sb([128, 129], BF16, "Cb%d_%d" % (d, i)) for i in range(2)] for d in range(2)]
        dn = [A.sb([128, NB], F32, "dn%d" % d) for d in range(2)]
        pcv = [A.ps([128, 512], F32, "pcv%d" % i) for i in range(2)]
        pU = [A.ps([128, 129], F32, "pU%d" % i) for i in range(2)]
        pN = [[A.ps([128, 129], F32, "pN%d_%d" % (d, i)) for i in range(2)] for d in range(2)]
        order = [list(range(NCH)), [3, 2, 1, 0] + list(range(NCH - 1, 3, -1))]
        pieces = [(0, TC)] + [(TC + 512 * i, TC + 512 * (i + 1)) for i in range(8)]
        pc = 0
        for h in range(4):
            for which in range(2):
                ch = which * 4 + h
                dg_ = dg[which]
                P.dma("sp" if which == 0 else "pool", xraw[:], SC["MQKB"][ch * 128:(ch + 1) * 128, :], writes=[xraw])
                for j in range(3):
                    P.op("dve", "tensor_scalar", [ident_f, cvw], [dg_], out=dg_[:, j, :], in0=ident_f[:], scalar1=cvw[:, ch, j:j + 1],
                         scalar2=None, op0=ALU.mult)
                dst = qT if which == 0 else kT
                for (a, b) in pieces:
                    s0, s1 = (0, TC) if a < TC else (TC, T)
                    ps = pcv[pc % 2]
                    pc += 1
                    P.op("pe", "matmul", [dg_, xraw], [ps], ps[:, 0:b - a], dg_[:, 1, :], xraw[:, a:b], start=True, stop=False)
                    lo = max(a, s0 + 1)
                    P.op("pe", "matmul", [dg_, xraw], [ps], ps[:, lo - a:b - a], dg_[:, 0, :], xraw[:, lo - 1:b - 1], start=False,
                         stop=False)
                    hi = min(b, s1 - 1)
                    P.op("pe", "matmul", [dg_, xraw], [ps], ps[:, 0:hi - a], dg_[:, 2, :], xraw[:, a + 1:hi + 1], start=False,
                         stop=True)
                    P.op("act", "activation", [ps, cvw], [dst], out=dst[:, a:b], in_=ps[:, 0:b - a], func=AF.Silu,
                         bias=cvw[:, ch, 3:4])
            for b0 in range(0, NB, 4):
                nb = min(4, NB - b0)
                ps = pcv[pc % 2]
                pc += 1
                psb = ps[:, 0:256].bitcast(BF16)
                for bb in range(nb):
                    P.op("pe", "transpose", [kT, ident_bf], [ps], psb[:, bb * 128:(bb + 1) * 128],
                         kT[:, (b0 + bb) * 128:(b0 + bb + 1) * 128], ident_bf[:])
                P.op("act", "copy", [ps], [kTok], out=kTok[:, b0:b0 + nb, :],
                     in_=psb[:, 0:nb * 128].rearrange("p (b j) -> p b j", j=128))
            P.dma("sp", vtok[:], SC["MV"][:, h * 128:(h + 1) * 128].rearrange("(b p) j -> p b j", p=128), writes=[vtok])
            for d in range(2):
                col = d * 32 + h
                idx = d * 4 + h
                P.op("pool" if d == 0 else "dve", "tensor_tensor", [vtok, esT], [vpp[d]], out=vpp[d][:, :, 0:128], in0=vtok[:],
                     in1=esT[:, :, col:col + 1].to_broadcast([128, NB, 128]), op=ALU.mult)
                P.op("dve", "tensor_copy", [esT], [vpp[d]], out=vpp[d][:, :, 128:129], in_=esT[:, :, col:col + 1])
                P.op("pool" if d == 1 else "dve", "tensor_tensor", [qT, decbc], [qd[d]],
                     out=qd[d][:, :].rearrange("p (c j) -> p c j", j=64), in0=qT[:, :].rearrange("p (c j) -> p c j", j=64),
                     in1=decbc[:, idx, :].unsqueeze(2).to_broadcast([128, NCH, 64]), op=ALU.mult)
            for b0 in range(0, NB, 4):
                nb = min(4, NB - b0)
                ps = pcv[pc % 2]
                pc += 1
                psv = ps[:, 0:256].rearrange("p (b j) -> p b j", j=64)
                for bb in range(nb):
                    for half in range(2):
                        c = (b0 + bb) * 2 + half
                        pr = slice(half * 64, half * 64 + 64)
                        P.op("pe", "matmul", [kT, qT], [ps], psv[pr, bb, :], kT[:, c * 64:(c + 1) * 64], qT[:, c * 64:(c + 1) * 64],
                             start=True, stop=True)
                for d in range(2):
                    P.op("dve", "tensor_tensor", [ps, mask[d]], [SmT[d]], out=SmT[d][:, b0:b0 + nb, :], in0=psv[:, 0:nb, :],
                         in1=mask[d][:].unsqueeze(1).to_broadcast([128, nb, 64]), op=ALU.mult)
            for d in range(2):
                P.op("pool", "memset", [], [Cst[d][0]], Cst[d][0][:], 0.0)
                P.op("pool", "memset", [], [Cb[d][0]], Cb[d][0][:], 0.0)
            for i in range(NCH):
                for d in range(2):
                    c = order[d][i]
                    b, half = c // 2, c % 2
                    pr = slice(half * 64, half * 64 + 64)
                    idx = d * 4 + h
                    Cold, Cnew = Cst[d][i % 2], Cst[d][(i + 1) % 2]
                    cbo, cbn = Cb[d][i % 2], Cb[d][(i + 1) % 2]
                    U = pU[d]
                    N = pN[d][i % 2]
                    P.op("pe", "matmul", [kTok, vpp[d]], [U], U[:, :], kTok[pr, b, :], vpp[d][pr, b, :], start=True, stop=True)
                    P.op("pe", "matmul", [SmT[d], vpp[d]], [N], N[pr, :], SmT[d][pr, b, :], vpp[d][pr, b, :], start=True,
                         stop=False)
                    P.op("pe", "matmul", [qd[d], cbo], [N], N[pr, :], qd[d][:, c * 64:(c + 1) * 64], cbo[:], start=False, stop=True)
                    P.op("dve", "scalar_tensor_tensor", [Cold, decbc, U], [cbn], out=cbn[:], in0=Cold[:],
                         scalar=decbc[:, idx, c:c + 1], in1=U[:, :], op0=ALU.mult, op1=ALU.add)
                    P.op("dve", "scalar_tensor_tensor", [Cold, decbc, U], [Cnew], out=Cnew[:], in0=Cold[:],
                         scalar=decbc[:, idx, c:c + 1], in1=U[:, :], op0=ALU.mult, op1=ALU.add)
                    P.op("act", "copy", [N], [hbuf[d]], out=hbuf[d][pr, b, :], in_=N[pr, :])
            for d in range(2):
                col = d * 32 + h
                P.op("act", "activation", [hbuf[d]], [dn[d]], out=dn[d][:, :].unsqueeze(2), in_=hbuf[d][:, :, 128:129], func=AF.Abs)
                P.op("dve", "tensor_tensor", [dn[d], fT], [dn[d]], out=dn[d][:, :].unsqueeze(2), in0=dn[d][:, :].unsqueeze(2),
                     in1=fT[:, :, col:col + 1], op=ALU.max)
                P.op("dve", "reciprocal", [dn[d]], [dn[d]], out=dn[d][:], in_=dn[d][:])
                P.op("dve" if d == 0 else "pool", "tensor_tensor", [hbuf[d], dn[d]], [hbuf[d]], out=hbuf[d][:, :, 0:128],
                     in0=hbuf[d][:, :, 0:128], in1=dn[d][:, :].unsqueeze(2).to_broadcast([128, NB, 128]), op=ALU.mult)
                P.dma("sp" if d == 0 else "pool", SC["HM"][d, :, h * 128:(h + 1) * 128].rearrange("(b p) j -> p b j", p=128),
                      hbuf[d][:, :, 0:128], reads=[hbuf[d]])
        P.barrier()


def combine_phase(nc, P, IN, SC, ident_bf, src0, src1, mul, norm_name, cat_row0, groups):
    with ExitStack() as es:
        A = Alloc(nc, es)
        nrow = A.sb([128, 512], F32, "nrow")
        P.dma("sp", nrow[:], IN[norm_name][0:1, :].to_broadcast([128, 512]), writes=[nrow])
        epsT = A.sb([128, 1], F32, "epsTc")
        P.op("pool", "memset", [], [epsT], epsT[:], EPS)
        a_ = [A.sb([128, 512], F32, "cA%d" % i) for i in range(4)]
        b_ = [A.sb([128, 512], F32, "cB%d" % i) for i in range(4)]
        m_ = [A.sb([128, 512], BF16, "cM%d" % i) for i in range(4)]
        junk = A.sb([128, 128], F32, "cjunk")
        ss = [A.sb([128, 4], F32, "css%d" % i) for i in range(4)]
        hn = [A.sb([128, 512], F32, "chn%d" % i) for i in range(4)]
        hb = [A.sb([128, 512], BF16, "chb%d" % i) for i in range(4)]
        sz = [A.sb([128, 512], BF16, "csz%d" % i) for i in range(2)]
        oo = [A.sb([128, 512], BF16, "coo%d" % i) for i in range(2)]
        tp = [A.ps([128, 512], BF16, "ctp%d" % i) for i in range(8)]
        tiles = []
        for gi, (t0, n) in enumerate(groups):
            for ti in range(n // 128):
                tiles.append((gi, t0, n, ti))

        def stage1(it):
            gi, t0, n, ti = tiles[it]
            tok = t0 + ti * 128
            a, b, m = a_[it % 4], b_[it % 4], m_[it % 4]
            P.dma("sp", a[:], src0[tok:tok + 128, :], writes=[a])
            P.dma("pool", b[:], src1[tok:tok + 128, :], writes=[b])
            if mul is not None:
                P.dma("sp", m[:], mul[tok:tok + 128, :], writes=[m])
            P.op("dve", "tensor_tensor", [a, b], [a], out=a[:], in0=a[:], in1=b[:], op=ALU.add)
            if mul is not None:
                P.op("pool", "tensor_tensor", [a, m], [a], out=a[:], in0=a[:], in1=m[:], op=ALU.mult)

        def stage2(it):
            gi, t0, n, ti = tiles[it]
            a, s_, hb_ = a_[it % 4], ss[it % 4], hb[it % 4]
            for hh in range(4):
                P.op("act", "activation", [a], [junk, s_], out=junk[:], in_=a[:, hh * 128:(hh + 1) * 128], func=AF.Square,
                     accum_out=s_[:, hh:hh + 1])
            P.op("act", "activation", [s_, epsT], [s_], out=s_[:], in_=s_[:], func=AF.Sqrt, scale=1.0 / 128, bias=epsT[:, 0:1])
            P.op("dve", "reciprocal", [s_], [s_], out=s_[:], in_=s_[:])
            for hh in range(4):
                sl = slice(hh * 128, (hh + 1) * 128)
                P.op("dve", "scalar_tensor_tensor", [a, s_, nrow], [hb_], out=hb_[:, sl], in0=a[:, sl], scalar=s_[:, hh:hh + 1],
                     in1=nrow[:, sl], op0=ALU.mult, op1=ALU.mult)
            for j in range(4):
                tpj = tp[(gi % 2) * 4 + j]
                P.op("pe", "transpose", [hb_, ident_bf], [tpj], tpj[:, ti * 128:(ti + 1) * 128],
                     hb_[:, j * 128:(j + 1) * 128], ident_bf[:])
            if ti == n // 128 - 1:
                for j in range(4):
                    tpj = tp[(gi % 2) * 4 + j]
                    r0 = cat_row0 + j * 128
                    z_, o_ = sz[j % 2], oo[j % 2]
                    P.dma("sp", z_[:, 0:n], SC["SZT"][r0:r0 + 128, t0:t0 + n], writes=[z_])
                    P.op("dve", "tensor_tensor", [tpj, z_], [o_], out=o_[:, 0:n], in0=tpj[:, 0:n], in1=z_[:, 0:n], op=ALU.mult)
                    P.dma("pool", SC["CATT"][r0:r0 + 128, t0:t0 + n], o_[:, 0:n], reads=[o_])

        stage1(0)
        for it in range(len(tiles)):
            if it + 1 < len(tiles):
                stage1(it + 1)
            stage2(it)
        P.barrier()


def phase_C(nc, P, IN, SC, layer, gateR, out_ap):
    with ExitStack() as es:
        A = Alloc(nc, es)
        w = A.sb([128, 8, 1024], BF16, "w_out")
        with ExitStack() as es2:
            A2 = Alloc(nc, es2)
            stg = [A2.sb([128, 8, 512], F32, "wstg%d" % i) for i in range(2)]
            for i in range(2):
                P.dma("sp" if i == 0 else "pool", stg[i][:],
                      IN["l%d_w_out" % layer][:, i * 512:(i + 1) * 512].rearrange("(k p) n -> p k n", p=128), writes=[stg[i]])
                P.op("dve" if i == 0 else "act", "tensor_copy" if i == 0 else "copy", [stg[i]], [w], out=w[:, :, i * 512:(i + 1) * 512],
                     in_=stg[i][:])
            P.barrier()
        cat = [A.sb([128, 8, 512], BF16, "catT%d" % i) for i in range(3)]
        hold = [A.sb([128, 1024], F32, "hold%d" % i) for i in range(4)]
        tmp = [A.sb([128, 1024], F32, "ctmp%d" % i) for i in range(4)]
        hnew = [A.sb([128, 1024], F32, "hnew%d" % i) for i in range(4)]
        ps = [A.ps([128, 512], F32, "yps%d" % i) for i in range(4)]
        if layer == 1:
            frow = A.sb([128, 1024], F32, "frow")
            P.dma("sp", frow[:], IN["final_norm"][0:1, :].to_broadcast([128, 1024]), writes=[frow])
            epsT = A.sb([128, 1], F32, "epsTf")
            P.op("pool", "memset", [], [epsT], epsT[:], EPS)
            junk = A.sb([128, 1024], F32, "fjunk")
            st = [A.sb([128, 1], F32, "fst%d" % i) for i in range(4)]
            ob = [A.sb([128, 1024], F32, "fob%d" % i) for i in range(4)]
        it = 0
        groups = GROUPS if layer == 0 else GROUPS[1:]
        def load_cat(gi):
            t0, n = groups[gi]
            c_ = cat[gi % 3]
            for k2 in range(2):
                P.dma("sp", c_[:, k2 * 4:(k2 + 1) * 4, 0:n],
                      SC["CATT"][k2 * 512:(k2 + 1) * 512, t0:t0 + n].rearrange("(k p) t -> p k t", p=128), writes=[c_])

        load_cat(0)
        for gi, (t0, n) in enumerate(groups):
            c_ = cat[gi % 3]
            if gi + 1 < len(groups):
                load_cat(gi + 1)
            g_ = gateR[1] if (layer == 0 and t0 == 0) else gateR[0]
            for ti in range(n // 128):
                tok = t0 + ti * 128
                ho, tm, hn_ = hold[it % 4], tmp[it % 4], hnew[it % 4]
                if layer == 0:
                    srcp = IN["ctx"][tok:tok + 128, :] if t0 == 0 else IN["x"][tok - TC:tok - TC + 128, :]
                else:
                    srcp = SC["H1"][tok:tok + 128, :]
                P.dma("sp", ho[:], srcp, writes=[ho])
                for half in range(2):
                    p_ = ps[(it * 2 + half) % 4]
                    for k in range(8):
                        P.op("pe", "matmul", [c_, w], [p_], p_[:, :], c_[:, k, ti * 128:(ti + 1) * 128],
                             w[:, k, half * 512:(half + 1) * 512], start=(k == 0), stop=(k == 7))
                    P.op("dve", "tensor_tensor", [p_, g_], [tm], out=tm[:, half * 512:(half + 1) * 512], in0=p_[:, :],
                         in1=g_[:, half * 512:(half + 1) * 512], op=ALU.mult)
                P.op("pool", "tensor_tensor", [tm, ho], [hn_], out=hn_[:], in0=tm[:], in1=ho[:], op=ALU.add)
                if layer == 0:
                    P.dma("pool", SC["H1"][tok:tok + 128, :], hn_[:], reads=[hn_])
                else:
                    s_, o_ = st[it % 4], ob[it % 4]
                    P.op("act", "activation", [hn_], [junk, s_], out=junk[:], in_=hn_[:], func=AF.Square, accum_out=s_[:, 0:1])
                    P.op("act", "activation", [s_, epsT], [s_], out=s_[:], in_=s_[:], func=AF.Sqrt, scale=1.0 / D, bias=epsT[:, 0:1])
                    P.op("dve", "reciprocal", [s_], [s_], out=s_[:], in_=s_[:])
                    P.op("act", "activation", [hn_, s_], [o_], out=o_[:], in_=hn_[:], func=AF.Copy, scale=s_[:, 0:1])
                    P.op("dve", "tensor_tensor", [o_, frow], [o_], out=o_[:], in0=o_[:], in1=frow[:], op=ALU.mult)
                    P.dma("pool", out_ap[tok - TC:tok - TC + 128, :], o_[:], reads=[o_])
                it += 1
        P.barrier()


def gla_phase(nc, P, IN, SC, ident_bf):
    NB = NT
    NCH = T // 64
    with ExitStack() as es:
        A = Alloc(nc, es)
        mask = [A.sb([128, 64], F32, "gmask%d" % d) for d in range(2)]
        for d in range(2):
            P.op("pool", "memset", [], [mask[d]], mask[d][:], 1.0)
            for half in range(2):
                pr = slice(half * 64, half * 64 + 64)
                P.op("pool", "affine_select", [mask[d]], [mask[d]], out=mask[d][pr, :], in_=mask[d][pr, :],
                     pattern=[[1 if d == 0 else -1, 64]], compare_op=ALU.is_ge, fill=0.0, base=0,
                     channel_multiplier=-1 if d == 0 else 1)
        rm = [A.sb([128, T], BF16, "rm%d" % d) for d in range(2)]
        for d in range(2):
            P.op("pool", "memset", [], [rm[d]], rm[d][:], 1.0)
            j0 = 0 if d == 0 else 63
            P.op("pool", "memset", [rm[d]], [rm[d]], rm[d][:, :].rearrange("p (c j) -> p c j", j=64)[:, :, j0:j0 + 1], 0.0)
        qf = A.sb([128, T], F32, "gqf")
        kf = A.sb([128, T], F32, "gkf")
        lg = A.sb([128, T], F32, "glg")
        bc = A.sb([128, T], F32, "gbc")
        tmp = A.sb([128, T], F32, "gtmp")
        qg = [A.sb([128, T], BF16, "qg%d" % d) for d in range(2)]
        kg = [A.sb([128, T], BF16, "kg%d" % d) for d in range(2)]
        kbf = A.sb([128, T], BF16, "kbf")
        eb = [A.sb([128, NCH], F32, "eb%d" % d) for d in range(2)]
        kbTok = [A.sb([128, NB, 128], BF16, "kbTok%d" % d) for d in range(2)]
        SmT = [A.sb([128, NB, 64], BF16, "gSmT%d" % d) for d in range(2)]
        vtok = A.sb([128, NB, 128], BF16, "gvtok")
        obuf = [[A.sb([128, 128], F32, "gob%d_%d" % (d, i)) for i in range(3)] for d in range(2)]
        Sst = [[A.sb([128, 128], F32, "S%d_%d" % (d, i)) for i in range(2)] for d in range(2)]
        Sb = [[A.sb([128, 128], BF16, "Sb%d_%d" % (d, i)) for i in range(2)] for d in range(2)]
        ptr = [A.ps([128, 512], BF16, "gptr%d" % i) for i in range(2)]
        pS = [A.ps([128, 8, 64], F32, "gpS%d" % i) for i in range(2)]
        pU = [A.ps([128, 128], F32, "gpU%d" % i) for i in range(2)]
        pN = [A.ps([128, 128], F32, "gpN%d" % i) for i in range(2)]
        order = [list(range(NCH)), [3, 2, 1, 0] + list(range(NCH - 1, 3, -1))]
        for hp_ in range(2):
            P.dma("sp", qf[:], SC["MQK"][hp_ * 128:(hp_ + 1) * 128, :], writes=[qf])
            P.dma("pool", kf[:], SC["MQK"][256 + hp_ * 128:256 + (hp_ + 1) * 128, :], writes=[kf])
            for d in range(2):
                P.dma("pool", lg[:], SC["LG"][d, hp_ * 128:(hp_ + 1) * 128, :], writes=[lg])
                if d == 0:
                    P.op("dve", "tensor_tensor_scan", [rm[d], lg], [bc], out=bc[:, :], data0=rm[d][:, :], data1=lg[:, :],
                         initial=0.0, op0=ALU.mult, op1=ALU.add)
                else:
                    P.op("dve", "tensor_tensor_scan", [rm[d], lg], [bc], out=bc[:, ::-1], data0=rm[d][:, ::-1],
                         data1=lg[:, ::-1], initial=0.0, op0=ALU.mult, op1=ALU.add)
                jl = 63 if d == 0 else 0
                bl = bc[:, :].rearrange("p (c j) -> p c j", j=64)[:, :, jl:jl + 1]
                P.op("act", "activation", [bc], [eb[d]], out=eb[d][:, :].unsqueeze(2), in_=bl, func=AF.Exp)
                P.op("act", "activation", [bc], [tmp], out=tmp[:], in_=bc[:], func=AF.Exp)
                P.op("dve", "scalar_tensor_tensor", [qf, tmp], [qg[d]], out=qg[d][:], in0=qf[:], scalar=64.0 ** -0.5, in1=tmp[:],
                     op0=ALU.mult, op1=ALU.mult)
                P.op("act", "activation", [bc], [tmp], out=tmp[:], in_=bc[:], func=AF.Exp, scale=-1.0)
                P.op("dve", "tensor_tensor", [kf, tmp], [kg[d]], out=kg[d][:], in0=kf[:], in1=tmp[:], op=ALU.mult)
                P.op("dve", "tensor_tensor", [bc], [tmp], out=tmp[:, :].rearrange("p (c j) -> p c j", j=64),
                     in0=bl.to_broadcast([128, NCH, 64]), in1=bc[:, :].rearrange("p (c j) -> p c j", j=64), op=ALU.subtract)
                P.op("act", "activation", [tmp], [tmp], out=tmp[:], in_=tmp[:], func=AF.Exp)
                P.op("dve", "tensor_tensor", [kf, tmp], [kbf], out=kbf[:], in0=kf[:], in1=tmp[:], op=ALU.mult)
                for b0 in range(0, NB, 4):
                    nb = min(4, NB - b0)
                    ps = ptr[(b0 // 4) % 2]
                    for bb in range(nb):
                        P.op("pe", "transpose", [kbf, ident_bf], [ps], ps[:, bb * 128:(bb + 1) * 128],
                             kbf[:, (b0 + bb) * 128:(b0 + bb + 1) * 128], ident_bf[:])
                    P.op("act", "copy", [ps], [kbTok[d]], out=kbTok[d][:, b0:b0 + nb, :],
                         in_=ps[:, 0:nb * 128].rearrange("p (b j) -> p b j", j=128))
            for hh in range(2):
                h = hp_ * 2 + hh
                ps_ = slice(hh * 64, hh * 64 + 64)
                P.dma("sp", vtok[:], SC["MV"][:, h * 128:(h + 1) * 128].rearrange("(b p) j -> p b j", p=128), writes=[vtok])
                for d in range(2):
                    for b0 in range(0, NB, 8):
                        nb = min(8, NB - b0)
                        ps = pS[(b0 // 8) % 2]
                        for bb in range(nb):
                            for half in range(2):
                                c = (b0 + bb) * 2 + half
                                pr = slice(half * 64, half * 64 + 64)
                                P.op("pe", "matmul", [kg[d], qg[d]], [ps], ps[pr, bb, :], kg[d][ps_, c * 64:(c + 1) * 64],
                                     qg[d][ps_, c * 64:(c + 1) * 64], start=True, stop=True)
                        P.op("dve", "tensor_tensor", [ps, mask[d]], [SmT[d]], out=SmT[d][:, b0:b0 + nb, :], in0=ps[:, 0:nb, :],
                             in1=mask[d][:].unsqueeze(1).to_broadcast([128, nb, 64]), op=ALU.mult)
                for d in range(2):
                    P.op("pool", "memset", [], [Sst[d][0]], Sst[d][0][ps_, :], 0.0)
                    P.op("pool", "memset", [], [Sb[d][0]], Sb[d][0][ps_, :], 0.0)
                for i in range(NCH):
                    for d in range(2):
                        c = order[d][i]
                        b, half = c // 2, c % 2
                        pr = slice(half * 64, half * 64 + 64)
                        Sold, Snew = Sst[d][i % 2], Sst[d][(i + 1) % 2]
                        sbo, sbn = Sb[d][i % 2], Sb[d][(i + 1) % 2]
                        U, N = pU[d], pN[d]
                        ob = obuf[d][(i // 2) % 3]
                        P.op("pe", "matmul", [SmT[d], vtok], [N], N[pr, :], SmT[d][pr, b, :], vtok[pr, b, :], start=True,
                             stop=False)
                        P.op("pe", "matmul", [qg[d], sbo], [N], N[pr, :], qg[d][ps_, c * 64:(c + 1) * 64], sbo[ps_, :], start=False,
                             stop=True)
                        P.op("pe", "matmul", [kbTok[d], vtok], [U], U[ps_, :], kbTok[d][pr, b, hh * 64:(hh + 1) * 64], vtok[pr, b, :],
                             start=True, stop=True)
                        P.op("dve", "scalar_tensor_tensor", [Sold, eb[d], U], [sbn], out=sbn[ps_, :], in0=Sold[ps_, :],
                             scalar=eb[d][ps_, c:c + 1], in1=U[ps_, :], op0=ALU.mult, op1=ALU.add)
                        P.op("dve", "scalar_tensor_tensor", [Sold, eb[d], U], [Snew], out=Snew[ps_, :], in0=Sold[ps_, :],
                             scalar=eb[d][ps_, c:c + 1], in1=U[ps_, :], op0=ALU.mult, op1=ALU.add)
                        P.op("act", "copy", [N], [ob], out=ob[pr, :], in_=N[pr, :])
                        if i % 2 == 1:
                            P.dma("sp" if d == 0 else "pool", SC["HM"][d, b * 128:(b + 1) * 128, h * 128:(h + 1) * 128], ob[:],
                                  reads=[ob])
        P.barrier()


def _na_rows_ok(qr, kr):
    lo = min(max(qr - 4, 0), 56)
    return lo <= kr < lo + 8


def _na_cfg(g):
    if g == 0:
        return "first", 0, 6
    if g == 7:
        return "last", 26, 6
    return "mid", 4 * g - 2, 8


def _na_range(g, ktl):
    js = [j for j in range(8) for i in range(2) if _na_rows_ok(8 * g + j, 2 * ktl + i)]
    return min(js), max(js)


def na_bias_fn(nc, P, IN, state):
    def fn(h, A):
        if A is not None:
            state.setdefault("sets", {})
            for key, ntile in (("first", 6), ("mid", 8), ("last", 6)):
                state["sets"][(key, h % 2)] = [A.sb([128, 512], F32, "nab_%s%d_%d" % (key, i, h % 2)) for i in range(ntile)]
        out = {}

        def loader():
            for key, g, t_lo in (("first", 0, 0), ("mid", 1, 2), ("last", 7, 26)):
                tiles = state["sets"][(key, h % 2)]
                for r, bt in enumerate(tiles):
                    ktl = t_lo + r
                    u0, u1 = _na_range(g, ktl)
                    P.op("pool", "memset", [], [bt], bt[:, u0 * 64:(u1 + 1) * 64], MASKV)
                    for i in range(2):
                        kr = 2 * ktl + i
                        js = [j for j in range(8) if _na_rows_ok(8 * g + j, kr)]
                        if not js:
                            continue
                        j0, j1 = js[0], js[-1]
                        assert js == list(range(j0, j1 + 1))
                        m0 = 7 - (kr - 8 * g - j0)
                        nj = j1 - j0 + 1
                        P.dma("sp" if i == 0 else "pool",
                              bt[i * 64:(i + 1) * 64, j0 * 64:(j1 + 1) * 64].rearrange("p (m q) -> p m q", q=64),
                              IN["na_bias"][h, m0:m0 + nj, :, :].rearrange("m k q -> k m q"), writes=[bt])
                    yield

        for g in range(8):
            key, t_lo, nt_ = _na_cfg(g)
            for r in range(nt_):
                out[(g, 2 + t_lo + r)] = state["sets"][(key, h % 2)][r]
        return out, loader()

    return fn


def _shapes(d):
    return {k: (v.shape, "bf16" if v.dtype == ml_dtypes.bfloat16 else "f32") for k, v in d.items()}


def run(inputs, stage=99, debug=(), cores=8, skip=()):
    inputs = {k: np.asarray(v) for k, v in inputs.items()}
    sh, per = prep_inputs(inputs)
    nc = build(_shapes(sh), _shapes(per[0]), stage=stage, debug=debug, skip=skip)
    in_maps = [dict(sh, **per[b]) for b in range(cores)]
    res = run_bass_kernel_spmd(nc, in_maps, core_ids=list(range(cores)))
    return res


def kernel(**inputs):
    res = run(inputs)
    return np.stack([np.asarray(r["out"], dtype=np.float32) for r in res.results], axis=0)
```

```python
import numpy as np
from contextlib import ExitStack
import ml_dtypes
import concourse.bass as bass
import concourse.mybir as mybir
from concourse.bass_utils import run_bass_kernel_spmd

F32 = mybir.dt.float32
BF16 = mybir.dt.bfloat16
AF = mybir.ActivationFunctionType
ALU = mybir.AluOpType
AX = mybir.AxisListType

D = 1024
TC = 256
TL = 4096
T = TC + TL
NT = T // 128
EPS = 1e-6
MASKV = -30000.0

GROUPS = [(0, 256)] + [(256 + 512 * i, 512) for i in range(8)]


class Dep:
    __slots__ = ("w", "r")

    def __init__(self):
        self.w = None
        self.r = {}


class Tile:
    def __init__(self, t):
        self.t = t
        self.d = Dep()

    def __getitem__(self, k):
        return self.t[k]


class DramDep:
    def __init__(self):
        self.d = Dep()


class Prog:
    def __init__(self, nc, es):
        self.nc = nc
        self.eng = {"pe": nc.tensor, "act": nc.scalar, "dve": nc.vector, "pool": nc.gpsimd, "sp": nc.sync}
        self.R = 12
        self.keys = [("pe", "c"), ("act", "c"), ("dve", "c"), ("pool", "c")]
        for q in ("sp", "pool"):
            self.keys += [(q, "d%d" % i) for i in range(self.R)]
        self.ndma = {"sp": 0, "pool": 0}
        self.sem = {k: es.enter_context(nc.semaphore("s_%s_%s" % k)) for k in self.keys}
        self.cnt = {k: 0 for k in self.keys}
        self.waited = {e: {} for e in self.eng}
        self.n = 0

    def _emit(self, eng, kind, fn, reads, writes):
        if kind == "d":
            kind = "d%d" % (self.ndma[eng] % self.R)
            self.ndma[eng] += 1
        key = (eng, kind)
        deps = {}
        if kind != "c" and self.cnt[key] > 0:
            deps[key] = self.cnt[key]

        def add(tok):
            if tok is None:
                return
            k, v = tok
            if deps.get(k, 0) < v:
                deps[k] = v

        for b in reads:
            add(b.d.w)
        for b in writes:
            add(b.d.w)
            for k, v in b.d.r.items():
                add((k, v))
        e = self.eng[eng]
        wd = self.waited[eng]
        for k, v in deps.items():
            if k == ("pe", "c") and eng == "pe":
                continue
            if wd.get(k, 0) >= v:
                continue
            e.wait_ge(self.sem[k], v)
            wd[k] = v
        inc = 16 if kind != "c" else 1
        self.cnt[key] += inc
        fn(e).then_inc(self.sem[key], inc)
        v = self.cnt[key]
        for b in reads:
            if b.d.r.get(key, 0) < v:
                b.d.r[key] = v
        for b in writes:
            b.d.w = (key, v)
            b.d.r = {}
        self.n += 1

    def op(self, eng, name, reads, writes, *a, **kw):
        self._emit(eng, "c", lambda e: getattr(e, name)(*a, **kw), reads, writes)

    def dma(self, q, out, in_, reads=(), writes=(), **kw):
        self._emit(q, "d", lambda e: e.dma_start(out=out, in_=in_, **kw), reads, writes)

    def barrier(self):
        for en, e in self.eng.items():
            wd = self.waited[en]
            for k in self.keys:
                v = self.cnt[k]
                if v > 0 and wd.get(k, 0) < v:
                    e.wait_ge(self.sem[k], v)
                    wd[k] = v


class Alloc:
    def __init__(self, nc, es):
        self.nc = nc
        self.es = es
        _CTR.setdefault(id(nc), 0)

    def _nm(self, name):
        _CTR[id(self.nc)] = _CTR.get(id(self.nc), 0) + 1
        return "%s_%d" % (name, _CTR[id(self.nc)])

    def sb(self, shape, dt, name=None):
        return Tile(self.es.enter_context(self.nc.sbuf_tensor(self._nm(name or "sb"), list(shape), dt)))

    def ps(self, shape, dt, name=None):
        return Tile(self.es.enter_context(self.nc.psum_tensor(self._nm(name or "ps"), list(shape), dt)))


_CTR = {}


def _fm(v, nchunk):
    return np.ascontiguousarray(v.reshape(nchunk, 128).T)


def _rope_perm():
    perm = np.zeros(32, np.int64)
    for i in range(32):
        r = i % 16
        perm[i] = i + 8 if r < 8 else i - 8
    return perm


def _rope_tables():
    t = np.arange(TL)
    inv = (1.0 / (10000.0 ** (np.arange(8, dtype=np.float32) / 8))).astype(np.float32)
    pos = [(t // 64).astype(np.float32), (t % 64).astype(np.float32)]
    C = np.zeros((32, TL), np.float32)
    S = np.zeros((32, TL), np.float32)
    for i in range(32):
        a = i // 16
        r = i % 16
        p = r % 8
        ang = (pos[a] * inv[p]).astype(np.float32)
        C[i] = np.cos(ang)
        S[i] = -np.sin(ang) if r < 8 else np.sin(ang)
    Cf = np.zeros((128, TL), np.float32)
    Sf = np.zeros((128, TL), np.float32)
    Cf[0:32] = C
    Cf[64:96] = C
    Sf[0:32] = S
    Sf[64:96] = S
    return Cf, Sf


def prep_inputs(inp):
    sh = {}
    sh["ident_bf"] = np.eye(128, dtype=np.float32).astype(ml_dtypes.bfloat16)
    sh["ident_f"] = np.eye(128, dtype=np.float32)
    perm = _rope_perm()
    w_in = inp["l0_w_in"]
    gi_cols = [2720 + d * 8 + h for d in range(2) for h in range(4)]
    gf_cols = [2720 + d * 8 + 4 + h for d in range(2) for h in range(4)]
    sh["l0_w_in"] = np.ascontiguousarray(
        np.concatenate([w_in, w_in[:, 640:672][:, perm], w_in[:, gi_cols], w_in[:, gf_cols]], axis=1))
    w_uq = inp["l0_mla_w_uq"].reshape(384, 8, 96)
    ext = np.concatenate([w_uq, w_uq[:, :, 0:64], w_uq[:, :, 64:96][:, :, perm]], axis=2)
    sh["l0_w_uq"] = np.ascontiguousarray(ext.reshape(384, 8 * 192))
    w_ukv = inp["l0_mla_w_ukv"].reshape(256, 8, 128)
    sh["l0_w_ukv"] = np.ascontiguousarray(
        np.concatenate([w_ukv[:, :, 0:64].reshape(256, 512), w_ukv[:, :, 64:128].reshape(256, 512)], axis=1))
    sh["l0_qnT"] = _fm(inp["l0_mla_q_norm"], 3)
    sh["l0_kvnT"] = _fm(inp["l0_mla_kv_norm"], 2)
    Cf, Sf = _rope_tables()
    sh["ropeC"] = Cf
    sh["ropeS"] = Sf
    cw = inp["l0_mlstm_conv_w"]
    sh["l0_convT"] = np.ascontiguousarray(
        np.concatenate([cw.reshape(3, 8, 128).transpose(2, 1, 0), inp["l0_mlstm_conv_b"].reshape(8, 128).T[:, :, None]],
                       axis=2))
    gb = np.zeros((16, 1), np.float32)
    for d in range(2):
        for h in range(4):
            gb[d * 8 + h, 0] = inp["l0_mlstm_b_i"][d, h]
            gb[d * 8 + 4 + h, 0] = inp["l0_mlstm_b_f"][d, h]
    sh["l0_gbias"] = gb
    gb2 = np.zeros((64, 2), np.float32)
    for d in range(2):
        for h in range(4):
            gb2[d * 32 + h, 0] = inp["l0_mlstm_b_i"][d, h]
            gb2[d * 32 + h, 1] = inp["l0_mlstm_b_f"][d, h]
    sh["l0_gb2"] = gb2
    sh["l0_hnorm"] = np.ascontiguousarray(inp["l0_mlstm_norm"].reshape(1, 512))
    sh["l0_w_out"] = inp["l0_w_out"]
    sh["l1_w_in"] = inp["l1_w_in"]
    sh["l1_w_gate"] = np.ascontiguousarray(inp["l1_gla_w_gate"])
    sh["l1_bgT"] = np.ascontiguousarray(inp["l1_gla_b_gate"].reshape(2, 2, 128).transpose(2, 0, 1))
    sh["l1_gnorm"] = np.ascontiguousarray(inp["l1_gla_norm"].reshape(1, 512))
    sh["l1_w_out"] = inp["l1_w_out"]
    sh["final_norm"] = np.ascontiguousarray(inp["final_norm"].reshape(1, 1024))
    rpb = inp["l1_na_rpb"]
    kc = np.arange(64)[:, None]
    qc = np.arange(64)[None, :]
    wc0 = np.clip(qc - 8, 0, 48)
    okc = (kc >= wc0) & (kc < wc0 + 16)
    dcol = np.clip(kc - qc + 15, 0, 30)
    Tb = np.full((8, 15, 64, 64), MASKV, np.float32)
    for m in range(15):
        dr = 7 - m
        blk = rpb[:, dr + 7][:, dcol]
        Tb[:, m] = np.where(okc[None], blk, np.float32(MASKV))
    sh["na_bias"] = Tb
    mods = [(inp["l0_norm"], inp["l0_w_mod"], inp["l0_b_mod"]), (inp["l1_norm"], inp["l1_w_mod"], inp["l1_b_mod"])]
    for l, (g_, wm_, bm_) in enumerate(mods):
        sh["l%d_w_mod" % l] = wm_
        sh["l%d_bmodT" % l] = _fm(bm_, 24)
        sh["l%d_bmod_gate" % l] = np.ascontiguousarray(bm_[2048:3072].reshape(1, 1024))
        sh["l%d_gT" % l] = _fm(g_, 8)
    per = []
    for b in range(8):
        d = {}
        d["x"] = inp["x"][b]
        d["ctx"] = inp["ctx"][b]
        cv = np.stack([inp["c"][b], inp["c_ctx"]], axis=1)
        d["cvec"] = np.ascontiguousarray(cv.reshape(8, 128, 2).transpose(1, 0, 2))
        per.append(d)
    return sh, per


def build(sh_shapes, per_shapes, stage=99, debug=(), skip=()):
    nc = bass.Bass("TRN2", target_bir_lowering=False)
    IN = {}
    for k, (shape, dt) in list(sh_shapes.items()) + list(per_shapes.items()):
        IN[k] = nc.dram_tensor(k, list(shape), BF16 if dt == "bf16" else F32, kind="ExternalInput").ap()
    out = nc.dram_tensor("out", [TL, D], F32, kind="ExternalOutput").ap()

    def scratch(name, shape, dt):
        kind = "ExternalOutput" if name in debug else "Internal"
        return nc.dram_tensor(name, list(shape), dt, kind=kind).ap()

    SC = {}
    SC["H1"] = scratch("H1", [T, D], F32)
    SC["SZT"] = scratch("SZT", [1024, T], BF16)
    SC["CATT"] = scratch("CATT", [1024, T], BF16)
    SC["QT"] = scratch("QT", [8, 96, T], BF16)
    SC["KT"] = scratch("KT", [8, 96, T], BF16)
    SC["V"] = scratch("V", [T, 512], BF16)
    SC["MQK"] = scratch("MQK", [1024, T], F32)
    SC["MQKB"] = scratch("MQKB", [1024, T], BF16)
    SC["GI"] = scratch("GI", [8, T], F32)
    SC["GF"] = scratch("GF", [8, T], F32)
    SC["MV"] = scratch("MV", [T, 512], BF16)
    SC["MO"] = scratch("MO", [T, 512], BF16)
    SC["HM"] = scratch("HM", [2, T, 512], F32)
    SC["RD"] = scratch("RD", [16, 512], F32)
    SC["LG"] = scratch("LG", [2, 256, T], F32)
    SC["NQ"] = scratch("NQ", [512, T], BF16)
    SC["NK"] = scratch("NK", [512, T], BF16)

    with ExitStack() as es0:
        P = Prog(nc, es0)
        A0 = Alloc(nc, es0)
        ident_bf = A0.sb([128, 128], BF16, "identbf")
        ident_f = A0.sb([128, 128], F32, "identf")
        ones_f = A0.sb([128, 128], F32, "onesf")
        P.dma("sp", ident_bf[:], IN["ident_bf"][:, :], writes=[ident_bf])
        P.dma("sp", ident_f[:], IN["ident_f"][:, :], writes=[ident_f])
        P.op("pool", "memset", [], [ones_f], ones_f[:], 1.0)
        affA = [A0.sb([128, 8, 2], F32, "affA%d" % l) for l in range(2)]
        affB = [A0.sb([128, 8, 2], F32, "affB%d" % l) for l in range(2)]
        gateR = [[A0.sb([128, 1024], F32, "gateR%d_%d" % (l, s)) for s in range(2 if l == 0 else 1)] for l in range(2)]

        esA0 = es0.enter_context(ExitStack())
        Aw0 = Alloc(nc, esA0)
        w0 = Aw0.sb([128, 8, 3808], BF16, "w_in0")
        w_uq0 = Aw0.sb([128, 3, 1536], BF16, "w_uq0")
        w_ukv0 = Aw0.sb([128, 2, 1024], BF16, "w_ukv0")
        stgA = [Aw0.sb([128, 1024], F32, "stgA%d" % i) for i in range(2)]

        def w0_loader():
            i = 0
            for c0 in range(0, 3808, 128):
                cw = min(128, 3808 - c0)
                s = stgA[i % 2]
                sv = s[:, :].rearrange("p (k n) -> p k n", k=8)
                P.dma("sp" if i % 2 == 0 else "pool", sv[:, :, 0:cw],
                      IN["l0_w_in"][:, c0:c0 + cw].rearrange("(k p) n -> p k n", p=128), writes=[s])
                P.op("dve" if i % 2 == 0 else "act", "tensor_copy" if i % 2 == 0 else "copy", [s], [w0],
                     out=w0[:, :, c0:c0 + cw], in_=sv[:, :, 0:cw])
                i += 1
                yield
            for kk in range(3):
                for hf in range(2):
                    s = stgA[i % 2]
                    P.dma("sp" if i % 2 == 0 else "pool", s[:, 0:768], IN["l0_w_uq"][kk * 128:(kk + 1) * 128, hf * 768:(hf + 1) * 768],
                          writes=[s])
                    P.op("dve" if i % 2 == 0 else "act", "tensor_copy" if i % 2 == 0 else "copy", [s], [w_uq0],
                         out=w_uq0[:, kk, hf * 768:(hf + 1) * 768], in_=s[:, 0:768])
                    i += 1
                    yield
            for kk in range(2):
                s = stgA[i % 2]
                P.dma("sp" if i % 2 == 0 else "pool", s[:, :], IN["l0_w_ukv"][kk * 128:(kk + 1) * 128, :], writes=[s])
                P.op("dve" if i % 2 == 0 else "act", "tensor_copy" if i % 2 == 0 else "copy", [s], [w_ukv0],
                     out=w_ukv0[:, kk, :], in_=s[:, :])
                i += 1
                yield

        wgen = w0_loader()
        with ExitStack() as es:
            A = Alloc(nc, es)
            cv = A.sb([128, 8, 2], F32, "cv")
            sc = A.sb([128, 8, 2], F32, "sc")
            screp = [A.sb([128, 8, 128], F32, "screp%d" % s) for s in range(2)]
            P.dma("sp", cv[:], IN["cvec"][:, :, :], writes=[cv])
            P.op("act", "activation", [cv], [sc], out=sc[:], in_=cv[:], func=AF.Silu)
            for s in range(2):
                for k in range(8):
                    P.op("dve", "tensor_copy", [sc], [screp[s]], out=screp[s][:, k, :],
                         in_=sc[:, k, s:s + 1].to_broadcast([128, 128]))
            wpan = [A.sb([128, 8, 384], F32, "wpan%d" % i) for i in range(2)]
            wgate = [A.sb([128, 512], F32, "wgate%d" % i) for i in range(3)]
            pm = A.ps([128, 24, 2], F32, "pm")
            pg = [A.ps([128, 512], F32, "pg%d" % i) for i in range(2)]
            bmT = A.sb([128, 24], F32, "bmT")
            gT = A.sb([128, 8], F32, "gT")
            modT = A.sb([128, 24, 2], F32, "modT")
            bgrow = A.sb([128, 1024], F32, "bgrow")
            for l in range(2):
                wm = IN["l%d_w_mod" % l]
                P.dma("sp", bmT[:], IN["l%d_bmodT" % l][:, :], writes=[bmT])
                P.dma("sp", gT[:], IN["l%d_gT" % l][:, :], writes=[gT])
                P.dma("sp", bgrow[:], IN["l%d_bmod_gate" % l][0:1, :].to_broadcast([128, 1024]), writes=[bgrow])
                for pn in range(8):
                    wp = wpan[pn % 2]
                    P.dma("sp" if pn % 2 == 0 else "pool", wp[:],
                          wm[:, pn * 384:(pn + 1) * 384].rearrange("(k p) n -> p k n", p=128), writes=[wp])
                    for j in range(3):
                        n = pn * 3 + j
                        for k in range(8):
                            P.op("pe", "matmul", [wp, sc], [pm], pm[:, n, :], wp[:, k, j * 128:(j + 1) * 128],
                                 sc[:, k, :], start=(k == 0), stop=(k == 7))
                    for _ in range(3):
                        next(wgen, None)
                P.op("dve", "tensor_tensor", [pm, bmT], [modT], out=modT[:], in0=pm[:],
                     in1=bmT[:].unsqueeze(2).to_broadcast([128, 24, 2]), op=ALU.add)
                P.op("dve", "tensor_scalar", [modT], [affA[l]], out=affA[l][:], in0=modT[:, 8:16, :], scalar1=1.0,
                     scalar2=None, op0=ALU.add)
                P.op("dve", "tensor_tensor", [affA[l], gT], [affA[l]], out=affA[l][:], in0=affA[l][:],
                     in1=gT[:].unsqueeze(2).to_broadcast([128, 8, 2]), op=ALU.mult)
                P.op("dve", "tensor_copy", [modT], [affB[l]], out=affB[l][:], in_=modT[:, 0:8, :])
                for s in range(len(gateR[l])):
                    for hf in range(2):
                        ps = pg[hf]
                        for k in range(8):
                            wg = wgate[(hf * 8 + k) % 3]
                            P.dma("sp" if k % 2 == 0 else "pool", wg[:],
                                  wm[k * 128:(k + 1) * 128, 2048 + hf * 512:2048 + (hf + 1) * 512], writes=[wg])
                            P.op("pe", "matmul", [wg, screp[s]], [ps], ps[:], screp[s][:, k, :], wg[:],
                                 start=(k == 0), stop=(k == 7))
                        P.op("dve", "tensor_tensor", [ps, bgrow], [gateR[l][s]],
                             out=gateR[l][s][:, hf * 512:(hf + 1) * 512], in0=ps[:],
                             in1=bgrow[:, hf * 512:(hf + 1) * 512], op=ALU.add)
            for _ in wgen:
                pass
            P.barrier()
        if stage <= 0:
            dbg = nc.dram_tensor("dbg_mod", [128, 2, 2, 8, 2], F32, kind="ExternalOutput").ap()
            dbg2 = nc.dram_tensor("dbg_gate", [128, 1024], F32, kind="ExternalOutput").ap()
            for l in range(2):
                P.dma("sp", dbg[:, l, 0], affA[l][:], reads=[affA[l]])
                P.dma("sp", dbg[:, l, 1], affB[l][:], reads=[affB[l]])
            P.dma("sp", dbg2[:, :], gateR[0][1][:], reads=[gateR[0][1]])
            P.barrier()
            return nc

        phase_A(nc, P, IN, SC, 0, affA[0], affB[0], ident_bf, ones_f, w_pre=(w0, w_uq0, w_ukv0))
        esA0.close()
        if stage <= 1:
            return nc
        if 2 not in skip:
            mla_groups = [(0, 256, [0, 1], 0)] + [(256 + 512 * g, 512, list(range(NT)), 0) for g in range(8)]
            attention(nc, P, SC, ones_f, 8, 96, 96.0 ** -0.5, lambda h: SC["QT"][h, :, :], lambda h: SC["KT"][h, :, :],
                      SC["V"], 0, mla_groups)
        if stage <= 2:
            return nc
        if 3 not in skip:
            mlstm_phase(nc, P, IN, SC, ident_bf, ident_f, ones_f)
        if stage <= 3:
            return nc
        combine_phase(nc, P, IN, SC, ident_bf, SC["HM"][0], SC["HM"][1], SC["MO"], "l0_hnorm", 512, GROUPS)
        if stage <= 4:
            return nc
        phase_C(nc, P, IN, SC, 0, gateR[0], out)
        if stage <= 5:
            return nc
        phase_A(nc, P, IN, SC, 1, affA[1], affB[1], ident_bf, ones_f)
        if stage <= 6:
            return nc
        if 7 not in skip:
            gla_phase(nc, P, IN, SC, ident_bf)
            combine_phase(nc, P, IN, SC, ident_bf, SC["HM"][0], SC["HM"][1], None, "l1_gnorm", 0, GROUPS[1:])
        if stage <= 7:
            return nc
        if 8 not in skip:
            na_groups = []
            for g in range(8):
                key, t_lo, nt_ = _na_cfg(g)
                loc = []
                for r in range(nt_):
                    u0, u1 = _na_range(g, t_lo + r)
                    loc.append((2 + t_lo + r, u0 * 64, (u1 + 1) * 64))
                na_groups.append((256 + 512 * g, 512, [0, 1] + loc, g))
            attention(nc, P, SC, ones_f, 8, 64, 64.0 ** -0.5, lambda h: SC["NQ"][h * 64:(h + 1) * 64, :],
                      lambda h: SC["NK"][h * 64:(h + 1) * 64, :], SC["V"], 512, na_groups, bias_fn=na_bias_fn(nc, P, IN, {}),
                      ident_bf=ident_bf, early_release=True, act_recip=True)
        if stage <= 8:
            return nc
        phase_C(nc, P, IN, SC, 1, gateR[1], out)
    return nc


def phase_A(nc, P, IN, SC, layer, affA, affB, ident_bf, ones_f, w_pre=None):
    NW = 3808 if layer == 0 else 3616
    w_in_d = IN["l%d_w_in" % layer]
    with ExitStack() as es:
        A = Alloc(nc, es)
        if w_pre is not None:
            w_in, w_uq, w_ukv = w_pre
        else:
            w_in = A.sb([128, 8, NW], BF16, "w_in")
            if layer == 0:
                w_uq = A.sb([128, 3, 1536], BF16, "w_uq")
                w_ukv = A.sb([128, 2, 1024], BF16, "w_ukv")
        with ExitStack() as es2:
            A2 = Alloc(nc, es2)
            stg = [A2.sb([128, 8, 512], F32, "stg%d" % i) for i in range(2)] if w_pre is None else None
            i = 0
            for c0 in (range(0, NW, 512) if w_pre is None else ()):
                cw = min(512, NW - c0)
                s = stg[i % 2]
                P.dma("sp" if i % 2 == 0 else "pool", s[:, :, 0:cw],
                      w_in_d[:, c0:c0 + cw].rearrange("(k p) n -> p k n", p=128), writes=[s])
                P.op("dve" if i % 2 == 0 else "act", "tensor_copy" if i % 2 == 0 else "copy", [s], [w_in],
                     out=w_in[:, :, c0:c0 + cw], in_=s[:, :, 0:cw])
                i += 1
            if layer == 0 and w_pre is None:
                s = stg[i % 2]
                for kk in range(3):
                    s = stg[i % 2]
                    P.dma("sp", s[:, 0:3, :], IN["l0_w_uq"][kk * 128:(kk + 1) * 128, :].rearrange("p (a n) -> p a n", a=3),
                          writes=[s])
                    P.op("dve", "tensor_copy", [s], [w_uq], out=w_uq[:, kk, :].rearrange("p (a n) -> p a n", a=3),
                         in_=s[:, 0:3, :])
                    i += 1
                s = stg[i % 2]
                for kk in range(2):
                    P.dma("sp", s[:, 2 * kk:2 * kk + 2, :],
                          IN["l0_w_ukv"][kk * 128:(kk + 1) * 128, :].rearrange("p (a n) -> p a n", a=2), writes=[s])
                P.op("dve", "tensor_copy", [s], [w_ukv], out=w_ukv[:].rearrange("p k (a n) -> p (k a) n", a=2),
                     in_=s[:, 0:4, :])
                i += 1
            P.barrier()
        if layer == 0:
            qnT = A.sb([128, 3], F32, "qnT")
            kvnT = A.sb([128, 2], F32, "kvnT")
            P.dma("sp", qnT[:], IN["l0_qnT"][:, :], writes=[qnT])
            P.dma("sp", kvnT[:], IN["l0_kvnT"][:, :], writes=[kvnT])
            cqT = A.sb([128, 3, 512], F32, "cqT")
            ckvT = A.sb([128, 2, 512], F32, "ckvT")
            sq = A.sb([128, 3, 512], BF16, "sq")
            ones_b = A.sb([128, 128], BF16, "ones_b")
            P.op("pool", "memset", [], [ones_b], ones_b[:], 1.0)
            rstd = A.sb([128, 512], F32, "rstd")
            cqn = A.sb([128, 3, 512], BF16, "cqn")
            ckvn = A.sb([128, 2, 512], BF16, "ckvn")
            rC = A.sb([128, 512], F32, "rC")
            rS = A.sb([128, 512], F32, "rS")
            rt1 = A.sb([128, 512], F32, "rt1")
            rt2 = A.sb([128, 512], F32, "rt2")
            qo = [A.sb([128, 512], BF16, "qo%d" % i) for i in range(2)]
            kro = A.sb([32, 512], BF16, "kro")
        else:
            gaT = [A.sb([16, 512], F32, "gaT%d" % d) for d in range(2)]
            wg = A.sb([16, 2, 256], F32, "wg")
            P.dma("sp", wg[:], IN["l1_w_gate"].rearrange("d r k -> r d k"), writes=[wg])
            bgT = A.sb([128, 2, 2], F32, "bgT")
            nbg = A.sb([128, 2, 2], F32, "nbg")
            P.dma("sp", bgT[:], IN["l1_bgT"][:, :, :], writes=[bgT])
            P.op("dve", "tensor_scalar", [bgT], [nbg], out=nbg[:], in0=bgT[:], scalar1=-1.0, scalar2=None, op0=ALU.mult)
            one1 = A.sb([128, 1], F32, "one1a")
            P.op("pool", "memset", [], [one1], one1[:], 1.0)
            lge = A.sb([128, 512], F32, "lge")
            lgo = [A.sb([128, 512], F32, "lgo%d" % i) for i in range(2)]
        hb = [A.sb([128, 1024], F32, "hb%d" % i) for i in range(3)]
        junk = A.sb([128, 1024], F32, "junk")
        st = [A.sb([128, 4], F32, "st%d" % i) for i in range(2)]
        xn2 = [[A.sb([128, 1024], BF16, "xn%d_%d" % (s_, i)) for i in range(4)] for s_ in range(2)]
        epsT = A.sb([128, 1], F32, "epsT")
        P.op("pool", "memset", [], [epsT], epsT[:], EPS)
        uT = [A.sb([128, 8, 512], BF16, "uT%d" % i) for i in range(2)]
        fo_bf = [A.sb([128, 512], BF16, "fobf%d" % i) for i in range(4)]
        fo_f = [A.sb([128, 512], F32, "fof%d" % i) for i in range(3)]
        tp = [A.ps([128, 512], BF16, "tp%d" % i) for i in range(2)]
        acc = [A.ps([128, 512], F32, "acc%d" % i) for i in range(5)]
        cnt = {"acc": 0, "fobf": 0, "fof": 0, "ev": 0, "q": 0, "hb": 0, "xn": 0, "tp": 0}

        def nxt(name, lst):
            r = lst[cnt[name] % len(lst)]
            cnt[name] += 1
            return r

        def evac_engine():
            cnt["ev"] += 1
            return "dve" if cnt["ev"] % 2 == 0 else "act"

        def copy_op(eng, src_t, src_ap, dst_t, dst_ap):
            if eng == "act":
                P.op("act", "copy", [src_t], [dst_t], out=dst_ap, in_=src_ap)
            else:
                P.op(eng, "tensor_copy", [src_t], [dst_t], out=dst_ap, in_=src_ap)

        def stq():
            cnt["q"] += 1
            return "pool" if cnt["q"] % 2 == 0 else "sp"

        def norm_part(gi):
            t0, n = GROUPS[gi]
            ntl = n // 128
            sta = st[gi % 2]
            xn = xn2[gi % 2]
            for ti in range(ntl):
                h = nxt("hb", hb)
                tok = t0 + ti * 128
                if layer == 0:
                    src = IN["ctx"][tok:tok + 128, :] if gi == 0 else IN["x"][tok - TC:tok - TC + 128, :]
                else:
                    src = SC["H1"][tok:tok + 128, :]
                P.dma("sp", h[:], src, writes=[h])
                P.op("act", "activation", [h], [junk, sta], out=junk[:], in_=h[:], func=AF.Square,
                     accum_out=sta[:, ti:ti + 1])
                P.op("act", "activation", [sta, epsT], [sta], out=sta[:, ti:ti + 1], in_=sta[:, ti:ti + 1], func=AF.Sqrt,
                     scale=1.0 / D, bias=epsT[:, 0:1])
                P.op("dve", "reciprocal", [sta], [sta], out=sta[:, ti:ti + 1], in_=sta[:, ti:ti + 1])
                x_ = xn[ti]
                P.op("dve", "tensor_scalar", [h, sta], [x_], out=x_[:], in0=h[:], scalar1=sta[:, ti:ti + 1],
                     scalar2=None, op0=ALU.mult)

        def transpose_part(gi):
            t0, n = GROUPS[gi]
            ntl = n // 128
            s = 1 if gi == 0 else 0
            u = uT[gi % 2]
            xn = xn2[gi % 2]
            for j in range(8):
                tpp = nxt("tp", tp)
                for ti in range(ntl):
                    P.op("pe", "transpose", [xn[ti], ident_bf], [tpp], tpp[:, ti * 128:(ti + 1) * 128],
                         xn[ti][:, j * 128:(j + 1) * 128], ident_bf[:])
                P.op("dve", "tensor_scalar", [tpp, affA, affB], [u], out=u[:, j, 0:n],
                     in0=tpp[:, 0:n], scalar1=affA[:, j, s:s + 1], scalar2=affB[:, j, s:s + 1], op0=ALU.mult,
                     op1=ALU.add)


        def proj_part(gi):
            t0, n = GROUPS[gi]
            ntl = n // 128
            u = uT[gi % 2]

            def fm_proj(c0, ncol):
                ps = nxt("acc", acc)
                for k in range(8):
                    P.op("pe", "matmul", [w_in, u], [ps], ps[0:ncol, 0:n], w_in[:, k, c0:c0 + ncol], u[:, k, 0:n],
                         start=(k == 0), stop=(k == 7))
                return ps

            def store_fm(ps, ncol, dst, dt, func=None, eng=None):
                o = nxt("fobf", fo_bf) if dt == BF16 else nxt("fof", fo_f)
                if func is not None:
                    P.op("act", "activation", [ps], [o], out=o[0:ncol, 0:n], in_=ps[0:ncol, 0:n], func=func)
                else:
                    copy_op(eng or evac_engine(), ps, ps[0:ncol, 0:n], o, o[0:ncol, 0:n])
                P.dma(stq(), dst, o[0:ncol, 0:n], reads=[o])

            tsl = slice(t0, t0 + n)
            if layer == 0:
                for j in range(3):
                    ps = fm_proj(j * 128, 128)
                    copy_op(evac_engine(), ps, ps[:, 0:n], cqT, cqT[:, j, 0:n])
                for j in range(2):
                    ps = fm_proj(384 + j * 128, 128)
                    copy_op(evac_engine(), ps, ps[:, 0:n], ckvT, ckvT[:, j, 0:n])
                for (src_t, nk, nrm, dst_t, dim) in ((cqT, 3, qnT, cqn, 384.0), (ckvT, 2, kvnT, ckvn, 256.0)):
                    P.op("act", "activation", [src_t], [sq], out=sq[:, 0:nk, 0:n], in_=src_t[:, 0:nk, 0:n], func=AF.Square)
                    ps = nxt("acc", acc)
                    for k in range(nk):
                        P.op("pe", "matmul", [ones_b, sq], [ps], ps[:, 0:n], ones_b[:], sq[:, k, 0:n], start=(k == 0),
                             stop=(k == nk - 1))
                    P.op("act", "activation", [ps, epsT], [rstd], out=rstd[:, 0:n], in_=ps[:, 0:n], func=AF.Sqrt,
                         scale=1.0 / dim, bias=epsT[:, 0:1])
                    P.op("dve", "reciprocal", [rstd], [rstd], out=rstd[:, 0:n], in_=rstd[:, 0:n])
                    for k in range(nk):
                        P.op("dve", "scalar_tensor_tensor", [src_t, nrm, rstd], [dst_t], out=dst_t[:, k, 0:n],
                             in0=src_t[:, k, 0:n], scalar=nrm[:, k:k + 1], in1=rstd[:, 0:n], op0=ALU.mult, op1=ALU.mult)
                rot = gi > 0
                if rot:
                    P.dma("sp", rC[:, 0:n], IN["ropeC"][:, t0 - TC:t0 - TC + n], writes=[rC])
                    P.dma("sp", rS[:, 0:n], IN["ropeS"][:, t0 - TC:t0 - TC + n], writes=[rS])
                for hh in range(8):
                    ps = nxt("acc", acc)
                    for k in range(3):
                        P.op("pe", "matmul", [w_uq, cqn], [ps], ps[0:96, 0:n], w_uq[:, k, hh * 192:hh * 192 + 96],
                             cqn[:, k, 0:n], start=(k == 0), stop=(k == 2))
                    o = nxt("fobf", fo_bf)
                    if rot:
                        ps2 = nxt("acc", acc)
                        for k in range(3):
                            P.op("pe", "matmul", [w_uq, cqn], [ps2], ps2[0:96, 0:n],
                                 w_uq[:, k, hh * 192 + 96:hh * 192 + 192], cqn[:, k, 0:n], start=(k == 0), stop=(k == 2))
                        copy_op("act", ps, ps[0:64, 0:n], o, o[0:64, 0:n])
                        P.op("dve", "tensor_tensor", [ps, rC], [rt1], out=rt1[64:96, 0:n], in0=ps[64:96, 0:n],
                             in1=rC[64:96, 0:n], op=ALU.mult)
                        P.op("dve", "tensor_tensor", [ps2, rS], [rt2], out=rt2[64:96, 0:n], in0=ps2[64:96, 0:n],
                             in1=rS[64:96, 0:n], op=ALU.mult)
                        P.op("pool", "tensor_tensor", [rt1, rt2], [o], out=o[64:96, 0:n], in0=rt1[64:96, 0:n],
                             in1=rt2[64:96, 0:n], op=ALU.add)
                    else:
                        copy_op(evac_engine(), ps, ps[0:96, 0:n], o, o[0:96, 0:n])
                    P.dma(stq(), SC["QT"][hh, :, tsl], o[0:96, 0:n], reads=[o])
                for c in range(4):
                    ps = nxt("acc", acc)
                    for k in range(2):
                        P.op("pe", "matmul", [w_ukv, ckvn], [ps], ps[:, 0:n], w_ukv[:, k, c * 128:(c + 1) * 128],
                             ckvn[:, k, 0:n], start=(k == 0), stop=(k == 1))
                    o = nxt("fobf", fo_bf)
                    copy_op(evac_engine(), ps, ps[:, 0:n], o, o[:, 0:n])
                    for hh in range(2):
                        P.dma(stq(), SC["KT"][c * 2 + hh, 0:64, tsl], o[hh * 64:(hh + 1) * 64, 0:n], reads=[o])
                for ti in range(ntl):
                    ps = nxt("acc", acc)
                    for k in range(2):
                        P.op("pe", "matmul", [w_ukv, ckvn], [ps], ps[:, :], ckvn[:, k, ti * 128:(ti + 1) * 128],
                             w_ukv[:, k, 512:1024], start=(k == 0), stop=(k == 1))
                    o = nxt("fobf", fo_bf)
                    copy_op(evac_engine(), ps, ps[:, :], o, o[:, :])
                    P.dma(stq(), SC["V"][t0 + ti * 128:t0 + (ti + 1) * 128, :], o[:, :], reads=[o])
                ps = fm_proj(640, 32)
                if rot:
                    ps2 = fm_proj(3760, 32)
                    P.op("dve", "tensor_tensor", [ps, rC], [rt1], out=rt1[0:32, 0:n], in0=ps[0:32, 0:n], in1=rC[0:32, 0:n],
                         op=ALU.mult)
                    P.op("dve", "tensor_tensor", [ps2, rS], [rt2], out=rt2[0:32, 0:n], in0=ps2[0:32, 0:n],
                         in1=rS[0:32, 0:n], op=ALU.mult)
                    P.op("pool", "tensor_tensor", [rt1, rt2], [kro], out=kro[0:32, 0:n], in0=rt1[0:32, 0:n],
                         in1=rt2[0:32, 0:n], op=ALU.add)
                else:
                    copy_op("dve", ps, ps[0:32, 0:n], kro, kro[0:32, 0:n])
                for hh in range(8):
                    P.dma(stq(), SC["KT"][hh, 64:96, tsl], kro[0:32, 0:n], reads=[kro])
                yield
                for c in range(8):
                    ps = fm_proj(672 + c * 128, 128)
                    store_fm(ps, 128, SC["MQKB"][c * 128:(c + 1) * 128, tsl], BF16)
                ps = fm_proj(3792, 8)
                store_fm(ps, 8, SC["GI"][:, tsl], F32)
                ps = fm_proj(3800, 8)
                store_fm(ps, 8, SC["GF"][:, tsl], F32)
                for c in range(8):
                    ps = fm_proj(2736 + c * 128, 128)
                    store_fm(ps, 128, SC["SZT"][c * 128:(c + 1) * 128, tsl], BF16, func=AF.Silu)
                tm_specs = [(1696, SC["MV"], None), (2208, SC["MO"], AF.Sigmoid)]
            else:
                for c in range(4):
                    ps = fm_proj(c * 128, 128)
                    store_fm(ps, 128, SC["MQK"][c * 128:(c + 1) * 128, tsl], F32)
                yield
                for d in range(2):
                    ps = fm_proj(1024 + 16 * d, 16)
                    copy_op("dve", ps, ps[0:16, 0:n], gaT[d], gaT[d][0:16, 0:n])
                for d in range(2):
                    for c2 in range(2):
                        ps = nxt("acc", acc)
                        P.op("pe", "matmul", [wg, gaT[d]], [ps], ps[:, 0:n], wg[0:16, d, c2 * 128:(c2 + 1) * 128],
                             gaT[d][0:16, 0:n], start=True, stop=True)
                        P.op("act", "activation", [ps, nbg], [lge], out=lge[:, 0:n], in_=ps[:, 0:n], func=AF.Exp, scale=-1.0,
                             bias=nbg[:, d, c2:c2 + 1])
                        P.op("act", "activation", [lge, one1], [lge], out=lge[:, 0:n], in_=lge[:, 0:n], func=AF.Ln,
                             bias=one1[:, 0:1])
                        o = lgo[(d * 2 + c2) % 2]
                        P.op("dve", "tensor_scalar", [lge], [o], out=o[:, 0:n], in0=lge[:, 0:n], scalar1=-1.0 / 16.0,
                             scalar2=None, op0=ALU.mult)
                        P.dma(stq(), SC["LG"][d, c2 * 128:(c2 + 1) * 128, tsl], o[:, 0:n], reads=[o])
                for c in range(4):
                    ps = fm_proj(1056 + c * 128, 128)
                    store_fm(ps, 128, SC["NQ"][c * 128:(c + 1) * 128, tsl], BF16)
                for c in range(4):
                    ps = fm_proj(1568 + c * 128, 128)
                    store_fm(ps, 128, SC["NK"][c * 128:(c + 1) * 128, tsl], BF16)
                for c in range(8):
                    ps = fm_proj(2592 + c * 128, 128)
                    store_fm(ps, 128, SC["SZT"][c * 128:(c + 1) * 128, tsl], BF16, func=AF.Silu)
                tm_specs = [(512, SC["MV"], None), (2080, SC["V"], None)]
            for (c0, dst, func) in tm_specs:
                for ti in range(ntl):
                    ps = nxt("acc", acc)
                    for k in range(8):
                        P.op("pe", "matmul", [w_in, u], [ps], ps[:, :], u[:, k, ti * 128:(ti + 1) * 128],
                             w_in[:, k, c0:c0 + 512], start=(k == 0), stop=(k == 7))
                    o = nxt("fobf", fo_bf)
                    if func is not None:
                        P.op("act", "activation", [ps], [o], out=o[:, :], in_=ps[:, :], func=func)
                    else:
                        copy_op(evac_engine(), ps, ps[:, :], o, o[:, :])
                    P.dma(stq(), dst[t0 + ti * 128:t0 + (ti + 1) * 128, :], o[:, :], reads=[o])

        norm_part(0)
        transpose_part(0)
        for gi in range(len(GROUPS)):
            if gi + 1 < len(GROUPS):
                norm_part(gi + 1)
            gen = proj_part(gi)
            next(gen)
            if gi + 1 < len(GROUPS):
                transpose_part(gi + 1)
            for _ in gen:
                pass
        P.barrier()


def attention(nc, P, SC, ones_f, heads, dq, scale, load_q, load_k, Vd, cat_row0, groups, bias_fn=None, ident_bf=None,
              early_release=False, act_recip=False):
    LOOK = 4
    NS = 5
    EPI_DELAY = 8
    with ExitStack() as es:
        A = Alloc(nc, es)
        V = A.sb([128, NT, heads, 65], BF16, "Vall")
        P.op("pool", "memset", [], [V], V[:, :, :, 64:65], 1.0)
        for half in range(2):
            tl = slice(half * 17, (half + 1) * 17)
            for hh in range(heads):
                P.dma("sp" if hh % 2 == 0 else "pool", V[:, tl, hh, 0:64],
                      Vd[half * 17 * 128:(half + 1) * 17 * 128, hh * 64:(hh + 1) * 64].rearrange("(t p) d -> p t d", p=128),
                      writes=[V])
        kT = [A.sb([128, T], BF16, "kT%d" % i) for i in range(2)]
        qT = [A.sb([128, T], BF16, "qT%d" % i) for i in range(2)]
        pt = [A.sb([128, 512], BF16, "pt%d" % i) for i in range(NS)]
        sb_t = [A.sb([128, 512], F32, "sbt%d" % i) for i in range(3)] if bias_fn is not None else None
        rden = [A.sb([128, 512], F32, "rden%d" % i) for i in range(2)]
        ocp = [A.sb([128, 512], F32, "ocp%d" % i) for i in range(3)] if early_release else None
        rsc = A.sb([128, 512], F32, "rsc")
        bcs = [A.sb([128, 512], F32, "bcs%d" % i) for i in range(2)]
        szt = [A.sb([64, 512], BF16, "szt%d" % i) for i in range(3)]
        tmp = [A.sb([64, 512], F32, "atmp%d" % i) for i in range(2)]
        ao = [A.sb([64, 512], BF16, "ao%d" % i) for i in range(2)]
        Sps = [A.ps([128, 512], F32, "Sps%d" % i) for i in range(NS)]
        Ops = [A.ps([128, 512], F32, "Ops%d" % i) for i in range(2)]
        Bps = A.ps([128, 512], F32, "Bps")
        if dq < 128:
            for t_ in kT + qT:
                P.op("pool", "memset", [], [t_], t_[64:128, :], 0.0)
        P.dma("sp", kT[0][0:dq, :], load_k(0), writes=[kT[0]])
        P.dma("pool", qT[0][0:dq, :], load_q(0), writes=[qT[0]])
        gcount = 0
        it = 0
        rd_dep = [DramDep() for _ in range(16)]
        pend = []
        for h in range(heads):
            k_ = kT[h % 2]
            q_ = qT[h % 2]
            if h + 1 < heads:
                P.dma("sp", kT[(h + 1) % 2][0:dq, :], load_k(h + 1), writes=[kT[(h + 1) % 2]])
                P.dma("pool", qT[(h + 1) % 2][0:dq, :], load_q(h + 1), writes=[qT[(h + 1) % 2]])
            bias_loader = None
            if bias_fn is not None:
                if h == 0:
                    bias_cur, ld0 = bias_fn(0, A)
                    for _ in ld0:
                        pass
                bias_tiles = bias_cur
                if h + 1 < heads:
                    bias_cur, bias_loader = bias_fn(h + 1, A if h == 0 else None)
            else:
                bias_tiles = None
            r0 = cat_row0 + h * 64
            items = []
            for (q0, n, tiles, gkey) in groups:
                gid = gcount
                gcount += 1
                for j, kt in enumerate(tiles):
                    if isinstance(kt, tuple):
                        kt, c0, c1 = kt
                    else:
                        c0, c1 = 0, n
                    items.append((gid, q0, n, gkey, j, kt, len(tiles), c0, c1))

            def flush(cond):
                for e_ in pend[:]:
                    if cond(e_[1][0]):
                        emit_epi(*e_[1])
                        pend.remove(e_)

            def emit_S(item, slot):
                gid, q0, n, gkey, j, kt, nt_, c0, c1 = item
                S = Sps[slot % NS]
                p_ = pt[slot % NS]
                if j == 0:
                    flush(lambda g2: g2 % 3 == gid % 3)
                    sz = szt[gid % 3]
                    P.dma("sp", sz[:, 0:n], SC["SZT"][r0:r0 + 64, q0:q0 + n], writes=[sz])
                P.op("pe", "matmul", [k_, q_], [S], S[:, c0:c1], k_[:, kt * 128:(kt + 1) * 128], q_[:, q0 + c0:q0 + c1],
                     start=True, stop=True)
                bt = bias_tiles.get((gkey, kt)) if bias_tiles is not None else None
                if bt is not None:
                    sb = sb_t[slot % 3]
                    P.op("dve", "scalar_tensor_tensor", [S, bt], [sb], out=sb[:, c0:c1], in0=S[:, c0:c1], scalar=scale,
                         in1=bt[:, c0:c1], op0=ALU.mult, op1=ALU.add)
                    P.op("act", "activation", [sb], [p_], out=p_[:, c0:c1], in_=sb[:, c0:c1], func=AF.Exp)
                else:
                    P.op("act", "activation", [S], [p_], out=p_[:, c0:c1], in_=S[:, c0:c1], func=AF.Exp, scale=scale)

            def emit_PV(item, slot):
                gid, q0, n, gkey, j, kt, nt_, c0, c1 = item
                O = Ops[gid % 2]
                p_ = pt[slot % NS]
                assert j > 0 or (c0 == 0 and c1 == n)
                if j == 0:
                    flush(lambda g2: g2 % 2 == gid % 2)
                P.op("pe", "matmul", [V, p_], [O], O[0:65, c0:c1], V[:, kt, h, :], p_[:, c0:c1], start=(j == 0),
                     stop=(j == nt_ - 1))
                if j == nt_ - 1:
                    rd = rden[gid % 2]
                    if early_release:
                        oc = ocp[gid % 3]
                        P.op("act", "copy", [O], [oc], out=oc[0:65, 0:n], in_=O[0:65, 0:n])
                        P.op("act", "activation", [oc], [rsc], out=rsc[64:65, 0:n], in_=oc[64:65, 0:n], func=AF.Ln)
                        P.op("act", "activation", [rsc], [rd], out=rd[64:65, 0:n], in_=rsc[64:65, 0:n], func=AF.Exp, scale=-1.0)
                    elif act_recip:
                        P.op("act", "activation", [O], [rsc], out=rsc[64:65, 0:n], in_=O[64:65, 0:n], func=AF.Ln)
                        P.op("act", "activation", [rsc], [rd], out=rd[64:65, 0:n], in_=rsc[64:65, 0:n], func=AF.Exp, scale=-1.0)
                    else:
                        P.op("dve", "reciprocal", [O], [rd], out=rd[64:65, 0:n], in_=O[64:65, 0:n])
                    pend.append([EPI_DELAY, (gid, q0, n, r0, h)])

            def emit_epi(gid, q0, n, r0, h):
                O = Ops[gid % 2]
                rd = rden[gid % 2]
                bc_ = bcs[gid % 2]
                tm_ = tmp[gid % 2]
                a_ = ao[gid % 2]
                sz = szt[gid % 3]
                P.op("pe", "matmul", [ones_f, rd], [Bps], Bps[0:64, 0:n], ones_f[64:65, 0:64], rd[64:65, 0:n],
                     start=True, stop=True)
                if early_release:
                    oc = ocp[gid % 3]
                    P.op("dve", "tensor_tensor", [oc, Bps], [tm_], out=tm_[:, 0:n], in0=oc[0:64, 0:n], in1=Bps[0:64, 0:n],
                         op=ALU.mult)
                else:
                    P.op("dve", "tensor_copy", [Bps], [bc_], out=bc_[0:64, 0:n], in_=Bps[0:64, 0:n])
                    P.op("dve", "tensor_tensor", [O, bc_], [tm_], out=tm_[:, 0:n], in0=O[0:64, 0:n], in1=bc_[0:64, 0:n],
                         op=ALU.mult)
                P.op("pool", "tensor_tensor", [tm_, sz], [a_], out=a_[:, 0:n], in0=tm_[:, 0:n], in1=sz[:, 0:n], op=ALU.mult)
                P.dma("pool", SC["CATT"][r0:r0 + 64, q0:q0 + n], a_[:, 0:n], reads=[a_])

            nI = len(items)
            for idx in range(nI + LOOK):
                if idx < nI:
                    emit_S(items[idx], it + idx)
                for e_ in pend[:]:
                    e_[0] -= 1
                    if e_[0] <= 0:
                        emit_epi(*e_[1])
                        pend.remove(e_)
                if idx - LOOK >= 0:
                    emit_PV(items[idx - LOOK], it + idx - LOOK)
                if bias_loader is not None and idx % 3 == 2:
                    next(bias_loader, None)
            if bias_loader is not None:
                for _ in bias_loader:
                    pass
            it += nI
        for e_ in pend:
            emit_epi(*e_[1])
        P.barrier()


def mlstm_phase(nc, P, IN, SC, ident_bf, ident_f, ones_f):
    NB = NT
    NCH = T // 64
    with ExitStack() as es:
        A = Alloc(nc, es)
        esT = A.sb([128, NB, 64], F32, "esT")
        fT = A.sb([128, NB, 64], F32, "fT")
        decbc = A.sb([128, 8, NCH], F32, "decbc")
        mask = [A.sb([128, 64], F32, "mask%d" % d) for d in range(2)]
        for d in range(2):
            P.op("pool", "memset", [], [mask[d]], mask[d][:], 1.0)
            for half in range(2):
                pr = slice(half * 64, half * 64 + 64)
                P.op("pool", "affine_select", [mask[d]], [mask[d]], out=mask[d][pr, :], in_=mask[d][pr, :],
                     pattern=[[1 if d == 0 else -1, 64]], compare_op=ALU.is_ge, fill=0.0, base=0,
                     channel_multiplier=-1 if d == 0 else 1)
        with ExitStack() as es2:
            A2 = Alloc(nc, es2)
            X1 = A2.sb([64, T], F32, "X1")
            X2 = A2.sb([64, T], F32, "X2")
            X3 = A2.sb([64, T], F32, "X3")
            X4 = A2.sb([64, T], F32, "X4")
            gb = A2.sb([64, 2], F32, "gb")
            nbf = A2.sb([64, 1], F32, "nbf")
            one1 = A2.sb([64, 1], F32, "one1")
            dec = A2.sb([64, NCH], F32, "dec")
            aprev = A2.sb([64, NCH], F32, "aprev")
            sel = A2.sb([64, 128], F32, "sel")
            pst = [A2.ps([128, 8, 64], F32, "pst%d" % i) for i in range(2)]
            psd = A2.ps([128, NCH], F32, "psd")
            P.op("pool", "memset", [], [X1], X1[:], 0.0)
            P.op("pool", "memset", [], [X3], X3[:], 0.0)
            P.op("pool", "memset", [], [one1], one1[:], 1.0)
            P.dma("sp", gb[:], IN["l0_gb2"][:, :], writes=[gb])
            for d in range(2):
                P.dma("sp", X1[d * 32:d * 32 + 4, :], SC["GF"][d * 4:d * 4 + 4, :], writes=[X1])
                P.dma("pool", X3[d * 32:d * 32 + 4, :], SC["GI"][d * 4:d * 4 + 4, :], writes=[X3])
            P.op("dve", "tensor_scalar", [gb], [nbf], out=nbf[:], in0=gb[:, 1:2], scalar1=-1.0, scalar2=None, op0=ALU.mult)
            P.op("act", "activation", [X1, nbf], [X1], out=X1[:], in_=X1[:], func=AF.Exp, scale=-1.0, bias=nbf[:, 0:1])
            P.op("act", "activation", [X1, one1], [X1], out=X1[:], in_=X1[:], func=AF.Ln, bias=one1[:, 0:1])

            def seg_views(tile_, prng, d):
                if d == 0:
                    return [tile_[prng, 0:T]]
                return [tile_[prng, 0:TC][:, ::-1], tile_[prng, TC:T][:, ::-1]]

            def scan(dst, src, op0, d):
                prng = slice(d * 32, d * 32 + 32)
                dv = seg_views(dst, prng, d)
                sv = seg_views(src, prng, d)
                for i in range(len(dv)):
                    init = 0.0 if i == 0 else dst[prng, 0:1]
                    P.op("dve", "tensor_tensor_scan", [src, dst], [dst], out=dv[i], data0=sv[i], data1=sv[i],
                         initial=init, op0=op0, op1=ALU.bypass)

            for d in range(2):
                scan(X2, X1, ALU.add, d)
            P.op("dve", "scalar_tensor_tensor", [X3, gb, X2], [X3], out=X3[:], in0=X3[:], scalar=gb[:, 0:1], in1=X2[:],
                 op0=ALU.add, op1=ALU.add)
            for d in range(2):
                scan(X1, X3, ALU.max, d)
            for d in range(2):
                prng = slice(d * 32, d * 32 + 32)
                jj = 63 if d == 0 else 0
                P.op("dve", "tensor_copy", [X1], [X4], out=X4[prng, :].rearrange("p (c j) -> p c j", j=64),
                     in_=X1[prng, :].rearrange("p (c j) -> p c j", j=64)[:, :, jj:jj + 1].to_broadcast([32, NCH, 64]))
            aend = X4[:, :].rearrange("p (c j) -> p c j", j=64)[:, :, 0]
            P.op("pool", "memset", [], [aprev], aprev[:], 0.0)
            P.op("dve", "tensor_copy", [X4], [aprev], out=aprev[0:32, 1:NCH], in_=aend[0:32, 0:NCH - 1])
            P.op("dve", "tensor_copy", [X4], [aprev], out=aprev[32:64, 0:3], in_=aend[32:64, 1:4])
            P.op("dve", "tensor_copy", [X4], [aprev], out=aprev[32:64, 4:NCH - 1], in_=aend[32:64, 5:NCH])
            P.op("dve", "tensor_copy", [X4], [aprev], out=aprev[32:64, NCH - 1:NCH], in_=aend[32:64, 0:1])
            P.op("dve", "tensor_tensor", [aprev, X4], [dec], out=dec[:], in0=aprev[:], in1=aend, op=ALU.subtract)
            P.op("act", "activation", [dec], [dec], out=dec[:], in_=dec[:], func=AF.Exp)
            P.op("dve", "tensor_tensor", [X3, X4], [X3], out=X3[:], in0=X3[:], in1=X4[:], op=ALU.subtract)
            P.op("act", "activation", [X3], [X3], out=X3[:], in_=X3[:], func=AF.Exp)
            P.op("dve", "tensor_tensor", [X2, X4], [X2], out=X2[:], in0=X2[:], in1=X4[:], op=ALU.subtract)
            P.op("act", "activation", [X2], [X2], out=X2[:], in_=X2[:], func=AF.Exp)
            for (srcX, dstT) in ((X3, esT), (X2, fT)):
                for b0 in range(0, NB, 8):
                    nb = min(8, NB - b0)
                    ps = pst[(b0 // 8) % 2]
                    for bb in range(nb):
                        P.op("pe", "transpose", [srcX, ident_f], [ps], ps[:, bb, :], srcX[:, (b0 + bb) * 128:(b0 + bb + 1) * 128],
                             ident_f[0:64, 0:64])
                    P.op("act", "copy", [ps], [dstT], out=dstT[:, b0:b0 + nb, :], in_=ps[:, 0:nb, :])
            for idx in range(8):
                r = (idx // 4) * 32 + idx % 4
                P.op("dve", "tensor_copy", [ident_f], [sel], out=sel[:], in_=ident_f[0:64, r:r + 1].to_broadcast([64, 128]))
                P.op("pe", "matmul", [sel, dec], [psd], psd[:, :], sel[:, :], dec[:, :], start=True, stop=True)
                P.op("act", "copy", [psd], [decbc], out=decbc[:, idx, :], in_=psd[:, :])
            P.barrier()
        P.op("dve", "tensor_scalar", [esT], [esT], out=esT[:], in0=esT[:], scalar1=128.0 ** -0.5, scalar2=None, op0=ALU.mult)
        xraw = A.sb([128, T], BF16, "xraw")
        cvw = A.sb([128, 8, 4], F32, "cvw")
        P.dma("sp", cvw[:], IN["l0_convT"][:, :, :], writes=[cvw])
        dg = [A.sb([128, 3, 128], BF16, "dg%d" % i) for i in range(2)]
        qT = A.sb([128, T], BF16, "mqT")
        qd = [A.sb([128, T], BF16, "mqd%d" % d) for d in range(2)]
        kT = A.sb([128, T], BF16, "mkT")
        kTok = A.sb([128, NB, 128], BF16, "kTok")
        vtok = A.sb([128, NB, 128], BF16, "vtok")
        vpp = [A.sb([128, NB, 129], BF16, "vpp%d" % d) for d in range(2)]
        SmT = [A.sb([128, NB, 64], BF16, "SmT%d" % d) for d in range(2)]
        hbuf = [A.sb([128, NB, 129], F32, "hbuf%d" % d) for d in range(2)]
        Cst = [[A.sb([128, 129], F32, "C%d_%d" % (d, i)) for i in range(2)] for d in range(2)]
        Cb = [[A.sb([128, 129], BF16, "Cb%d_%d" % (d, i)) for i in range(2)] for d in range(2)]
        dn = [A.sb([128, NB], F32, "dn%d" % d) for d in range(2)]
        pcv = [A.ps([128, 512], F32, "pcv%d" % i) for i in range(2)]
        pU = [A.ps([128, 129], F32, "pU%d" % i) for i in range(2)]
        pN = [[A.ps([128, 129], F32, "pN%d_%d" % (d, i)) for i in range(2)] for d in range(2)]
        order = [list(range(NCH)), [3, 2, 1, 0] + list(range(NCH - 1, 3, -1))]
        pieces = [(0, TC)] + [(TC + 512 * i, TC + 512 * (i + 1)) for i in range(8)]
        pc = 0
        for h in range(4):
            for which in range(2):
                ch = which * 4 + h
                dg_ = dg[which]
                P.dma("sp" if which == 0 else "pool", xraw[:], SC["MQKB"][ch * 128:(ch + 1) * 128, :], writes=[xraw])
                for j in range(3):
                    P.op("dve", "tensor_scalar", [ident_f, cvw], [dg_], out=dg_[:, j, :], in0=ident_f[:], scalar1=cvw[:, ch, j:j + 1],
                         scalar2=None, op0=ALU.mult)
                dst = qT if which == 0 else kT
                for (a, b) in pieces:
                    s0, s1 = (0, TC) if a < TC else (TC, T)
                    ps = pcv[pc % 2]
                    pc += 1
                    P.op("pe", "matmul", [dg_, xraw], [ps], ps[:, 0:b - a], dg_[:, 1, :], xraw[:, a:b], start=True, stop=False)
                    lo = max(a, s0 + 1)
                    P.op("pe", "matmul", [dg_, xraw], [ps], ps[:, lo - a:b - a], dg_[:, 0, :], xraw[:, lo - 1:b - 1], start=False,
                         stop=False)
                    hi = min(b, s1 - 1)
                    P.op("pe", "matmul", [dg_, xraw], [ps], ps[:, 0:hi - a], dg_[:, 2, :], xraw[:, a + 1:hi + 1], start=False,
                         stop=True)
                    P.op("act", "activation", [ps, cvw], [dst], out=dst[:, a:b], in_=ps[:, 0:b - a], func=AF.Silu,
                         bias=cvw[:, ch, 3:4])
            for b0 in range(0, NB, 4):
                nb = min(4, NB - b0)
                ps = pcv[pc % 2]
                pc += 1
                psb = ps[:, 0:256].bitcast(BF16)
                for bb in range(nb):
                    P.op("pe", "transpose", [kT, ident_bf], [ps], psb[:, bb * 128:(bb + 1) * 128],
                         kT[:, (b0 + bb) * 128:(b0 + bb + 1) * 128], ident_bf[:])
                P.op("act", "copy", [ps], [kTok], out=kTok[:, b0:b0 + nb, :],
                     in_=psb[:, 0:nb * 128].rearrange("p (b j) -> p b j", j=128))
            P.dma("sp", vtok[:], SC["MV"][:, h * 128:(h + 1) * 128].rearrange("(b p) j -> p b j", p=128), writes=[vtok])
            for d in range(2):
                col = d * 32 + h
                idx = d * 4 + h
                P.op("pool" if d == 0 else "dve", "tensor_tensor", [vtok, esT], [vpp[d]], out=vpp[d][:, :, 0:128], in0=vtok[:],
                     in1=esT[:, :, col:col + 1].to_broadcast([128, NB, 128]), op=ALU.mult)
                P.op("dve", "tensor_copy", [esT], [vpp[d]], out=vpp[d][:, :, 128:129], in_=esT[:, :, col:col + 1])
                P.op("pool" if d == 1 else "dve", "tensor_tensor", [qT, decbc], [qd[d]],
                     out=qd[d][:, :].rearrange("p (c j) -> p c j", j=64), in0=qT[:, :].rearrange("p (c j) -> p c j", j=64),
                     in1=decbc[:, idx, :].unsqueeze(2).to_broadcast([128, NCH, 64]), op=ALU.mult)
            for b0 in range(0, NB, 4):
                nb = min(4, NB - b0)
                ps = pcv[pc % 2]
                pc += 1
                psv = ps[:, 0:256].rearrange("p (b j) -> p b j", j=64)
                for bb in range(nb):
                    for half in range(2):
                        c = (b0 + bb) * 2 + half
                        pr = slice(half * 64, half * 64 + 64)
                        P.op("pe", "matmul", [kT, qT], [ps], psv[pr, bb, :], kT[:, c * 64:(c + 1) * 64], qT[:, c * 64:(c + 1) * 64],
                             start=True, stop=True)
                for d in range(2):
                    P.op("dve", "tensor_tensor", [ps, mask[d]], [SmT[d]], out=SmT[d][:, b0:b0 + nb, :], in0=psv[:, 0:nb, :],
                         in1=mask[d][:].unsqueeze(1).to_broadcast([128, nb, 64]), op=ALU.mult)
            for d in range(2):
                P.op("pool", "memset", [], [Cst[d][0]], Cst[d][0][:], 0.0)
                P.op("pool", "memset", [], [Cb[d][0]], Cb[d][0][:], 0.0)
            for i in range(NCH):
                for d in range(2):
                    c = order[d][i]
                    b, half = c // 2, c % 2
                    pr = slice(half * 64, half * 64 + 64)
                    idx = d * 4 + h
                    Cold, Cnew = Cst[d][i % 2], Cst[d][(i + 1) % 2]
                    cbo, cbn = Cb[d][i % 2], Cb[d][(i + 1) % 2]
                    U = pU[d]
                    N = pN[d][i % 2]
                    P.op("pe", "matmul", [kTok, vpp[d]], [U], U[:, :], kTok[pr, b, :], vpp[d][pr, b, :], start=True, stop=True)
                    P.op("pe", "matmul", [SmT[d], vpp[d]], [N], N[pr, :], SmT[d][pr, b, :], vpp[d][pr, b, :], start=True,
                         stop=False)
                    P.op("pe", "matmul", [qd[d], cbo], [N], N[pr, :], qd[d][:, c * 64:(c + 1) * 64], cbo[:], start=False, stop=True)
                    P.op("dve", "scalar_tensor_tensor", [Cold, decbc, U], [cbn], out=cbn[:], in0=Cold[:],
                         scalar=decbc[:, idx, c:c + 1], in1=U[:, :], op0=ALU.mult, op1=ALU.add)
                    P.op("dve", "scalar_tensor_tensor", [Cold, decbc, U], [Cnew], out=Cnew[:], in0=Cold[:],
                         scalar=decbc[:, idx, c:c + 1], in1=U[:, :], op0=ALU.mult, op1=ALU.add)
                    P.op("act", "copy", [N], [hbuf[d]], out=hbuf[d][pr, b, :], in_=N[pr, :])
            for d in range(2):
                col = d * 32 + h
                P.op("act", "activation", [hbuf[d]], [dn[d]], out=dn[d][:, :].unsqueeze(2), in_=hbuf[d][:, :, 128:129], func=AF.Abs)
                P.op("dve", "tensor_tensor", [dn[d], fT], [dn[d]], out=dn[d][:, :].unsqueeze(2), in0=dn[d][:, :].unsqueeze(2),
                     in1=fT[:, :, col:col + 1], op=ALU.max)
                P.op("dve", "reciprocal", [dn[d]], [dn[d]], out=dn[d][:], in_=dn[d][:])
                P.op("dve" if d == 0 else "pool", "tensor_tensor", [hbuf[d], dn[d]], [hbuf[d]], out=hbuf[d][:, :, 0:128],
                     in0=hbuf[d][:, :, 0:128], in1=dn[d][:, :].unsqueeze(2).to_broadcast([128, NB, 128]), op=ALU.mult)
                P.dma("sp" if d == 0 else "pool", SC["HM"][d, :, h * 128:(h + 1) * 128].rearrange("(b p) j -> p b j", p=128),
                      hbuf[d][:, :, 0:128], reads=[hbuf[d]])
        P.barrier()


def combine_phase(nc, P, IN, SC, ident_bf, src0, src1, mul, norm_name, cat_row0, groups):
    with ExitStack() as es:
        A = Alloc(nc, es)
        nrow = A.sb([128, 512], F32, "nrow")
        P.dma("sp", nrow[:], IN[norm_name][0:1, :].to_broadcast([128, 512]), writes=[nrow])
        epsT = A.sb([128, 1], F32, "epsTc")
        P.op("pool", "memset", [], [epsT], epsT[:], EPS)
        a_ = [A.sb([128, 512], F32, "cA%d" % i) for i in range(4)]
        b_ = [A.sb([128, 512], F32, "cB%d" % i) for i in range(4)]
        m_ = [A.sb([128, 512], BF16, "cM%d" % i) for i in range(4)]
        junk = A.sb([128, 128], F32, "cjunk")
        ss = [A.sb([128, 4], F32, "css%d" % i) for i in range(4)]
        hn = [A.sb([128, 512], F32, "chn%d" % i) for i in range(4)]
        hb = [A.sb([128, 512], BF16, "chb%d" % i) for i in range(4)]
        sz = [A.sb([128, 512], BF16, "csz%d" % i) for i in range(2)]
        oo = [A.sb([128, 512], BF16, "coo%d" % i) for i in range(2)]
        tp = [A.ps([128, 512], BF16, "ctp%d" % i) for i in range(8)]
        tiles = []
        for gi, (t0, n) in enumerate(groups):
            for ti in range(n // 128):
                tiles.append((gi, t0, n, ti))

        def stage1(it):
            gi, t0, n, ti = tiles[it]
            tok = t0 + ti * 128
            a, b, m = a_[it % 4], b_[it % 4], m_[it % 4]
            P.dma("sp", a[:], src0[tok:tok + 128, :], writes=[a])
            P.dma("pool", b[:], src1[tok:tok + 128, :], writes=[b])
            if mul is not None:
                P.dma("sp", m[:], mul[tok:tok + 128, :], writes=[m])
            P.op("dve", "tensor_tensor", [a, b], [a], out=a[:], in0=a[:], in1=b[:], op=ALU.add)
            if mul is not None:
                P.op("pool", "tensor_tensor", [a, m], [a], out=a[:], in0=a[:], in1=m[:], op=ALU.mult)

        def stage2(it):
            gi, t0, n, ti = tiles[it]
            a, s_, hb_ = a_[it % 4], ss[it % 4], hb[it % 4]
            for hh in range(4):
                P.op("act", "activation", [a], [junk, s_], out=junk[:], in_=a[:, hh * 128:(hh + 1) * 128], func=AF.Square,
                     accum_out=s_[:, hh:hh + 1])
            P.op("act", "activation", [s_, epsT], [s_], out=s_[:], in_=s_[:], func=AF.Sqrt, scale=1.0 / 128, bias=epsT[:, 0:1])
            P.op("dve", "reciprocal", [s_], [s_], out=s_[:], in_=s_[:])
            for hh in range(4):
                sl = slice(hh * 128, (hh + 1) * 128)
                P.op("dve", "scalar_tensor_tensor", [a, s_, nrow], [hb_], out=hb_[:, sl], in0=a[:, sl], scalar=s_[:, hh:hh + 1],
                     in1=nrow[:, sl], op0=ALU.mult, op1=ALU.mult)
            for j in range(4):
                tpj = tp[(gi % 2) * 4 + j]
                P.op("pe", "transpose", [hb_, ident_bf], [tpj], tpj[:, ti * 128:(ti + 1) * 128],
                     hb_[:, j * 128:(j + 1) * 128], ident_bf[:])
            if ti == n // 128 - 1:
                for j in range(4):
                    tpj = tp[(gi % 2) * 4 + j]
                    r0 = cat_row0 + j * 128
                    z_, o_ = sz[j % 2], oo[j % 2]
                    P.dma("sp", z_[:, 0:n], SC["SZT"][r0:r0 + 128, t0:t0 + n], writes=[z_])
                    P.op("dve", "tensor_tensor", [tpj, z_], [o_], out=o_[:, 0:n], in0=tpj[:, 0:n], in1=z_[:, 0:n], op=ALU.mult)
                    P.dma("pool", SC["CATT"][r0:r0 + 128, t0:t0 + n], o_[:, 0:n], reads=[o_])

        stage1(0)
        for it in range(len(tiles)):
            if it + 1 < len(tiles):
                stage1(it + 1)
            stage2(it)
        P.barrier()


def phase_C(nc, P, IN, SC, layer, gateR, out_ap):
    with ExitStack() as es:
        A = Alloc(nc, es)
        w = A.sb([128, 8, 1024], BF16, "w_out")
        with ExitStack() as es2:
            A2 = Alloc(nc, es2)
            stg = [A2.sb([128, 8, 512], F32, "wstg%d" % i) for i in range(2)]
            for i in range(2):
                P.dma("sp" if i == 0 else "pool", stg[i][:],
                      IN["l%d_w_out" % layer][:, i * 512:(i + 1) * 512].rearrange("(k p) n -> p k n", p=128), writes=[stg[i]])
                P.op("dve" if i == 0 else "act", "tensor_copy" if i == 0 else "copy", [stg[i]], [w], out=w[:, :, i * 512:(i + 1) * 512],
                     in_=stg[i][:])
            P.barrier()
        cat = [A.sb([128, 8, 512], BF16, "catT%d" % i) for i in range(3)]
        hold = [A.sb([128, 1024], F32, "hold%d" % i) for i in range(4)]
        tmp = [A.sb([128, 1024], F32, "ctmp%d" % i) for i in range(4)]
        hnew = [A.sb([128, 1024], F32, "hnew%d" % i) for i in range(4)]
        ps = [A.ps([128, 512], F32, "yps%d" % i) for i in range(4)]
        if layer == 1:
            frow = A.sb([128, 1024], F32, "frow")
            P.dma("sp", frow[:], IN["final_norm"][0:1, :].to_broadcast([128, 1024]), writes=[frow])
            epsT = A.sb([128, 1], F32, "epsTf")
            P.op("pool", "memset", [], [epsT], epsT[:], EPS)
            junk = A.sb([128, 1024], F32, "fjunk")
            st = [A.sb([128, 1], F32, "fst%d" % i) for i in range(4)]
            ob = [A.sb([128, 1024], F32, "fob%d" % i) for i in range(4)]
        it = 0
        groups = GROUPS if layer == 0 else GROUPS[1:]
        def load_cat(gi):
            t0, n = groups[gi]
            c_ = cat[gi % 3]
            for k2 in range(2):
                P.dma("sp", c_[:, k2 * 4:(k2 + 1) * 4, 0:n],
                      SC["CATT"][k2 * 512:(k2 + 1) * 512, t0:t0 + n].rearrange("(k p) t -> p k t", p=128), writes=[c_])

        load_cat(0)
        for gi, (t0, n) in enumerate(groups):
            c_ = cat[gi % 3]
            if gi + 1 < len(groups):
                load_cat(gi + 1)
            g_ = gateR[1] if (layer == 0 and t0 == 0) else gateR[0]
            for ti in range(n // 128):
                tok = t0 + ti * 128
                ho, tm, hn_ = hold[it % 4], tmp[it % 4], hnew[it % 4]
                if layer == 0:
                    srcp = IN["ctx"][tok:tok + 128, :] if t0 == 0 else IN["x"][tok - TC:tok - TC + 128, :]
                else:
                    srcp = SC["H1"][tok:tok + 128, :]
                P.dma("sp", ho[:], srcp, writes=[ho])
                for half in range(2):
                    p_ = ps[(it * 2 + half) % 4]
                    for k in range(8):
                        P.op("pe", "matmul", [c_, w], [p_], p_[:, :], c_[:, k, ti * 128:(ti + 1) * 128],
                             w[:, k, half * 512:(half + 1) * 512], start=(k == 0), stop=(k == 7))
                    P.op("dve", "tensor_tensor", [p_, g_], [tm], out=tm[:, half * 512:(half + 1) * 512], in0=p_[:, :],
                         in1=g_[:, half * 512:(half + 1) * 512], op=ALU.mult)
                P.op("pool", "tensor_tensor", [tm, ho], [hn_], out=hn_[:], in0=tm[:], in1=ho[:], op=ALU.add)
                if layer == 0:
                    P.dma("pool", SC["H1"][tok:tok + 128, :], hn_[:], reads=[hn_])
                else:
                    s_, o_ = st[it % 4], ob[it % 4]
                    P.op("act", "activation", [hn_], [junk, s_], out=junk[:], in_=hn_[:], func=AF.Square, accum_out=s_[:, 0:1])
                    P.op("act", "activation", [s_, epsT], [s_], out=s_[:], in_=s_[:], func=AF.Sqrt, scale=1.0 / D, bias=epsT[:, 0:1])
                    P.op("dve", "reciprocal", [s_], [s_], out=s_[:], in_=s_[:])
                    P.op("act", "activation", [hn_, s_], [o_], out=o_[:], in_=hn_[:], func=AF.Copy, scale=s_[:, 0:1])
                    P.op("dve", "tensor_tensor", [o_, frow], [o_], out=o_[:], in0=o_[:], in1=frow[:], op=ALU.mult)
                    P.dma("pool", out_ap[tok - TC:tok - TC + 128, :], o_[:], reads=[o_])
                it += 1
        P.barrier()


def gla_phase(nc, P, IN, SC, ident_bf):
    NB = NT
    NCH = T // 64
    with ExitStack() as es:
        A = Alloc(nc, es)
        mask = [A.sb([128, 64], F32, "gmask%d" % d) for d in range(2)]
        for d in range(2):
            P.op("pool", "memset", [], [mask[d]], mask[d][:], 1.0)
            for half in range(2):
                pr = slice(half * 64, half * 64 + 64)
                P.op("pool", "affine_select", [mask[d]], [mask[d]], out=mask[d][pr, :], in_=mask[d][pr, :],
                     pattern=[[1 if d == 0 else -1, 64]], compare_op=ALU.is_ge, fill=0.0, base=0,
                     channel_multiplier=-1 if d == 0 else 1)
        rm = [A.sb([128, T], BF16, "rm%d" % d) for d in range(2)]
        for d in range(2):
            P.op("pool", "memset", [], [rm[d]], rm[d][:], 1.0)
            j0 = 0 if d == 0 else 63
            P.op("pool", "memset", [rm[d]], [rm[d]], rm[d][:, :].rearrange("p (c j) -> p c j", j=64)[:, :, j0:j0 + 1], 0.0)
        qf = A.sb([128, T], F32, "gqf")
        kf = A.sb([128, T], F32, "gkf")
        lg = A.sb([128, T], F32, "glg")
        bc = A.sb([128, T], F32, "gbc")
        tmp = A.sb([128, T], F32, "gtmp")
        qg = [A.sb([128, T], BF16, "qg%d" % d) for d in range(2)]
        kg = [A.sb([128, T], BF16, "kg%d" % d) for d in range(2)]
        kbf = A.sb([128, T], BF16, "kbf")
        eb = [A.sb([128, NCH], F32, "eb%d" % d) for d in range(2)]
        kbTok = [A.sb([128, NB, 128], BF16, "kbTok%d" % d) for d in range(2)]
        SmT = [A.sb([128, NB, 64], BF16, "gSmT%d" % d) for d in range(2)]
        vtok = A.sb([128, NB, 128], BF16, "gvtok")
        obuf = [[A.sb([128, 128], F32, "gob%d_%d" % (d, i)) for i in range(3)] for d in range(2)]
        Sst = [[A.sb([128, 128], F32, "S%d_%d" % (d, i)) for i in range(2)] for d in range(2)]
        Sb = [[A.sb([128, 128], BF16, "Sb%d_%d" % (d, i)) for i in range(2)] for d in range(2)]
        ptr = [A.ps([128, 512], BF16, "gptr%d" % i) for i in range(2)]
        pS = [A.ps([128, 8, 64], F32, "gpS%d" % i) for i in range(2)]
        pU = [A.ps([128, 128], F32, "gpU%d" % i) for i in range(2)]
        pN = [A.ps([128, 128], F32, "gpN%d" % i) for i in range(2)]
        order = [list(range(NCH)), [3, 2, 1, 0] + list(range(NCH - 1, 3, -1))]
        for hp_ in range(2):
            P.dma("sp", qf[:], SC["MQK"][hp_ * 128:(hp_ + 1) * 128, :], writes=[qf])
            P.dma("pool", kf[:], SC["MQK"][256 + hp_ * 128:256 + (hp_ + 1) * 128, :], writes=[kf])
            for d in range(2):
                P.dma("pool", lg[:], SC["LG"][d, hp_ * 128:(hp_ + 1) * 128, :], writes=[lg])
                if d == 0:
                    P.op("dve", "tensor_tensor_scan", [rm[d], lg], [bc], out=bc[:, :], data0=rm[d][:, :], data1=lg[:, :],
                         initial=0.0, op0=ALU.mult, op1=ALU.add)
                else:
                    P.op("dve", "tensor_tensor_scan", [rm[d], lg], [bc], out=bc[:, ::-1], data0=rm[d][:, ::-1],
                         data1=lg[:, ::-1], initial=0.0, op0=ALU.mult, op1=ALU.add)
                jl = 63 if d == 0 else 0
                bl = bc[:, :].rearrange("p (c j) -> p c j", j=64)[:, :, jl:jl + 1]
                P.op("act", "activation", [bc], [eb[d]], out=eb[d][:, :].unsqueeze(2), in_=bl, func=AF.Exp)
                P.op("act", "activation", [bc], [tmp], out=tmp[:], in_=bc[:], func=AF.Exp)
                P.op("dve", "scalar_tensor_tensor", [qf, tmp], [qg[d]], out=qg[d][:], in0=qf[:], scalar=64.0 ** -0.5, in1=tmp[:],
                     op0=ALU.mult, op1=ALU.mult)
                P.op("act", "activation", [bc], [tmp], out=tmp[:], in_=bc[:], func=AF.Exp, scale=-1.0)
                P.op("dve", "tensor_tensor", [kf, tmp], [kg[d]], out=kg[d][:], in0=kf[:], in1=tmp[:], op=ALU.mult)
                P.op("dve", "tensor_tensor", [bc], [tmp], out=tmp[:, :].rearrange("p (c j) -> p c j", j=64),
                     in0=bl.to_broadcast([128, NCH, 64]), in1=bc[:, :].rearrange("p (c j) -> p c j", j=64), op=ALU.subtract)
                P.op("act", "activation", [tmp], [tmp], out=tmp[:], in_=tmp[:], func=AF.Exp)
                P.op("dve", "tensor_tensor", [kf, tmp], [kbf], out=kbf[:], in0=kf[:], in1=tmp[:], op=ALU.mult)
                for b0 in range(0, NB, 4):
                    nb = min(4, NB - b0)
                    ps = ptr[(b0 // 4) % 2]
                    for bb in range(nb):
                        P.op("pe", "transpose", [kbf, ident_bf], [ps], ps[:, bb * 128:(bb + 1) * 128],
                             kbf[:, (b0 + bb) * 128:(b0 + bb + 1) * 128], ident_bf[:])
                    P.op("act", "copy", [ps], [kbTok[d]], out=kbTok[d][:, b0:b0 + nb, :],
                         in_=ps[:, 0:nb * 128].rearrange("p (b j) -> p b j", j=128))
            for hh in range(2):
                h = hp_ * 2 + hh
                ps_ = slice(hh * 64, hh * 64 + 64)
                P.dma("sp", vtok[:], SC["MV"][:, h * 128:(h + 1) * 128].rearrange("(b p) j -> p b j", p=128), writes=[vtok])
                for d in range(2):
                    for b0 in range(0, NB, 8):
                        nb = min(8, NB - b0)
                        ps = pS[(b0 // 8) % 2]
                        for bb in range(nb):
                            for half in range(2):
                                c = (b0 + bb) * 2 + half
                                pr = slice(half * 64, half * 64 + 64)
                                P.op("pe", "matmul", [kg[d], qg[d]], [ps], ps[pr, bb, :], kg[d][ps_, c * 64:(c + 1) * 64],
                                     qg[d][ps_, c * 64:(c + 1) * 64], start=True, stop=True)
                        P.op("dve", "tensor_tensor", [ps, mask[d]], [SmT[d]], out=SmT[d][:, b0:b0 + nb, :], in0=ps[:, 0:nb, :],
                             in1=mask[d][:].unsqueeze(1).to_broadcast([128, nb, 64]), op=ALU.mult)
                for d in range(2):
                    P.op("pool", "memset", [], [Sst[d][0]], Sst[d][0][ps_, :], 0.0)
                    P.op("pool", "memset", [], [Sb[d][0]], Sb[d][0][ps_, :], 0.0)
                for i in range(NCH):
                    for d in range(2):
                        c = order[d][i]
                        b, half = c // 2, c % 2
                        pr = slice(half * 64, half * 64 + 64)
                        Sold, Snew = Sst[d][i % 2], Sst[d][(i + 1) % 2]
                        sbo, sbn = Sb[d][i % 2], Sb[d][(i + 1) % 2]
                        U, N = pU[d], pN[d]
                        ob = obuf[d][(i // 2) % 3]
                        P.op("pe", "matmul", [SmT[d], vtok], [N], N[pr, :], SmT[d][pr, b, :], vtok[pr, b, :], start=True,
                             stop=False)
                        P.op("pe", "matmul", [qg[d], sbo], [N], N[pr, :], qg[d][ps_, c * 64:(c + 1) * 64], sbo[ps_, :], start=False,
                             stop=True)
                        P.op("pe", "matmul", [kbTok[d], vtok], [U], U[ps_, :], kbTok[d][pr, b, hh * 64:(hh + 1) * 64], vtok[pr, b, :],
                             start=True, stop=True)
                        P.op("dve", "scalar_tensor_tensor", [Sold, eb[d], U], [sbn], out=sbn[ps_, :], in0=Sold[ps_, :],
                             scalar=eb[d][ps_, c:c + 1], in1=U[ps_, :], op0=ALU.mult, op1=ALU.add)
                        P.op("dve", "scalar_tensor_tensor", [Sold, eb[d], U], [Snew], out=Snew[ps_, :], in0=Sold[ps_, :],
                             scalar=eb[d][ps_, c:c + 1], in1=U[ps_, :], op0=ALU.mult, op1=ALU.add)
                        P.op("act", "copy", [N], [ob], out=ob[pr, :], in_=N[pr, :])
                        if i % 2 == 1:
                            P.dma("sp" if d == 0 else "pool", SC["HM"][d, b * 128:(b + 1) * 128, h * 128:(h + 1) * 128], ob[:],
                                  reads=[ob])
        P.barrier()


def _na_rows_ok(qr, kr):
    lo = min(max(qr - 4, 0), 56)
    return lo <= kr < lo + 8


def _na_cfg(g):
    if g == 0:
        return "first", 0, 6
    if g == 7:
        return "last", 26, 6
    return "mid", 4 * g - 2, 8


def _na_range(g, ktl):
    js = [j for j in range(8) for i in range(2) if _na_rows_ok(8 * g + j, 2 * ktl + i)]
    return min(js), max(js)


def na_bias_fn(nc, P, IN, state):
    def fn(h, A):
        if A is not None:
            state.setdefault("sets", {})
            for key, ntile in (("first", 6), ("mid", 8), ("last", 6)):
                state["sets"][(key, h % 2)] = [A.sb([128, 512], F32, "nab_%s%d_%d" % (key, i, h % 2)) for i in range(ntile)]
        out = {}

        def loader():
            for key, g, t_lo in (("first", 0, 0), ("mid", 1, 2), ("last", 7, 26)):
                tiles = state["sets"][(key, h % 2)]
                for r, bt in enumerate(tiles):
                    ktl = t_lo + r
                    u0, u1 = _na_range(g, ktl)
                    P.op("pool", "memset", [], [bt], bt[:, u0 * 64:(u1 + 1) * 64], MASKV)
                    for i in range(2):
                        kr = 2 * ktl + i
                        js = [j for j in range(8) if _na_rows_ok(8 * g + j, kr)]
                        if not js:
                            continue
                        j0, j1 = js[0], js[-1]
                        assert js == list(range(j0, j1 + 1))
                        m0 = 7 - (kr - 8 * g - j0)
                        nj = j1 - j0 + 1
                        P.dma("sp" if i == 0 else "pool",
                              bt[i * 64:(i + 1) * 64, j0 * 64:(j1 + 1) * 64].rearrange("p (m q) -> p m q", q=64),
                              IN["na_bias"][h, m0:m0 + nj, :, :].rearrange("m k q -> k m q"), writes=[bt])
                    yield

        for g in range(8):
            key, t_lo, nt_ = _na_cfg(g)
            for r in range(nt_):
                out[(g, 2 + t_lo + r)] = state["sets"][(key, h % 2)][r]
        return out, loader()

    return fn


def _shapes(d):
    return {k: (v.shape, "bf16" if v.dtype == ml_dtypes.bfloat16 else "f32") for k, v in d.items()}


def run(inputs, stage=99, debug=(), cores=8, skip=()):
    inputs = {k: np.asarray(v) for k, v in inputs.items()}
    sh, per = prep_inputs(inputs)
    nc = build(_shapes(sh), _shapes(per[0]), stage=stage, debug=debug, skip=skip)
    in_maps = [dict(sh, **per[b]) for b in range(cores)]
    res = run_bass_kernel_spmd(nc, in_maps, core_ids=list(range(cores)))
    return res


def kernel(**inputs):
    res = run(inputs)
    return np.stack([np.asarray(r["out"], dtype=np.float32) for r in res.results], axis=0)
```

```python
import numpy as np
from contextlib import ExitStack
import ml_dtypes
import concourse.bass as bass
import concourse.mybir as mybir
from concourse.bass_utils import run_bass_kernel_spmd

F32 = mybir.dt.float32
BF16 = mybir.dt.bfloat16
AF = mybir.ActivationFunctionType
ALU = mybir.AluOpType
AX = mybir.AxisListType

D = 1024
TC = 256
TL = 4096
T = TC + TL
NT = T // 128
EPS = 1e-6
MASKV = -30000.0

GROUPS = [(0, 256)] + [(256 + 512 * i, 512) for i in range(8)]


class Dep:
    __slots__ = ("w", "r")

    def __init__(self):
        self.w = None
        self.r = {}


class Tile:
    def __init__(self, t):
        self.t = t
        self.d = Dep()

    def __getitem__(self, k):
        return self.t[k]


class DramDep:
    def __init__(self):
        self.d = Dep()


class Prog:
    def __init__(self, nc, es):
        self.nc = nc
        self.eng = {"pe": nc.tensor, "act": nc.scalar, "dve": nc.vector, "pool": nc.gpsimd, "sp": nc.sync}
        self.R = 12
        self.keys = [("pe", "c"), ("act", "c"), ("dve", "c"), ("pool", "c")]
        for q in ("sp", "pool"):
            self.keys += [(q, "d%d" % i) for i in range(self.R)]
        self.ndma = {"sp": 0, "pool": 0}
        self.sem = {k: es.enter_context(nc.semaphore("s_%s_%s" % k)) for k in self.keys}
        self.cnt = {k: 0 for k in self.keys}
        self.waited = {e: {} for e in self.eng}
        self.n = 0

    def _emit(self, eng, kind, fn, reads, writes):
        if kind == "d":
            kind = "d%d" % (self.ndma[eng] % self.R)
            self.ndma[eng] += 1
        key = (eng, kind)
        deps = {}
        if kind != "c" and self.cnt[key] > 0:
            deps[key] = self.cnt[key]

        def add(tok):
            if tok is None:
                return
            k, v = tok
            if deps.get(k, 0) < v:
                deps[k] = v

        for b in reads:
            add(b.d.w)
        for b in writes:
            add(b.d.w)
            for k, v in b.d.r.items():
                add((k, v))
        e = self.eng[eng]
        wd = self.waited[eng]
        for k, v in deps.items():
            if k == ("pe", "c") and eng == "pe":
                continue
            if wd.get(k, 0) >= v:
                continue
            e.wait_ge(self.sem[k], v)
            wd[k] = v
        inc = 16 if kind != "c" else 1
        self.cnt[key] += inc
        fn(e).then_inc(self.sem[key], inc)
        v = self.cnt[key]
        for b in reads:
            if b.d.r.get(key, 0) < v:
                b.d.r[key] = v
        for b in writes:
            b.d.w = (key, v)
            b.d.r = {}
        self.n += 1

    def op(self, eng, name, reads, writes, *a, **kw):
        self._emit(eng, "c", lambda e: getattr(e, name)(*a, **kw), reads, writes)

    def dma(self, q, out, in_, reads=(), writes=(), **kw):
        self._emit(q, "d", lambda e: e.dma_start(out=out, in_=in_, **kw), reads, writes)

    def barrier(self):
        for en, e in self.eng.items():
            wd = self.waited[en]
            for k in self.keys:
                v = self.cnt[k]
                if v > 0 and wd.get(k, 0) < v:
                    e.wait_ge(self.sem[k], v)
                    wd[k] = v


class Alloc:
    def __init__(self, nc, es):
        self.nc = nc
        self.es = es
        _CTR.setdefault(id(nc), 0)

    def _nm(self, name):
        _CTR[id(self.nc)] = _CTR.get(id(self.nc), 0) + 1
        return "%s_%d" % (name, _CTR[id(self.nc)])

    def sb(self, shape, dt, name=None):
        return Tile(self.es.enter_context(self.nc.sbuf_tensor(self._nm(name or "sb"), list(shape), dt)))

    def ps(self, shape, dt, name=None):
        return Tile(self.es.enter_context(self.nc.psum_tensor(self._nm(name or "ps"), list(shape), dt)))


_CTR = {}


def _fm(v, nchunk):
    return np.ascontiguousarray(v.reshape(nchunk, 128).T)


def _rope_perm():
    perm = np.zeros(32, np.int64)
    for i in range(32):
        r = i % 16
        perm[i] = i + 8 if r < 8 else i - 8
    return perm


def _rope_tables():
    t = np.arange(TL)
    inv = (1.0 / (10000.0 ** (np.arange(8, dtype=np.float32) / 8))).astype(np.float32)
    pos = [(t // 64).astype(np.float32), (t % 64).astype(np.float32)]
    C = np.zeros((32, TL), np.float32)
    S = np.zeros((32, TL), np.float32)
    for i in range(32):
        a = i // 16
        r = i % 16
        p = r % 8
        ang = (pos[a] * inv[p]).astype(np.float32)
        C[i] = np.cos(ang)
        S[i] = -np.sin(ang) if r < 8 else np.sin(ang)
    Cf = np.zeros((128, TL), np.float32)
    Sf = np.zeros((128, TL), np.float32)
    Cf[0:32] = C
    Cf[64:96] = C
    Sf[0:32] = S
    Sf[64:96] = S
    return Cf, Sf


def prep_inputs(inp):
    sh = {}
    sh["ident_bf"] = np.eye(128, dtype=np.float32).astype(ml_dtypes.bfloat16)
    sh["ident_f"] = np.eye(128, dtype=np.float32)
    perm = _rope_perm()
    w_in = inp["l0_w_in"]
    gi_cols = [2720 + d * 8 + h for d in range(2) for h in range(4)]
    gf_cols = [2720 + d * 8 + 4 + h for d in range(2) for h in range(4)]
    sh["l0_w_in"] = np.ascontiguousarray(
        np.concatenate([w_in, w_in[:, 640:672][:, perm], w_in[:, gi_cols], w_in[:, gf_cols]], axis=1))
    w_uq = inp["l0_mla_w_uq"].reshape(384, 8, 96)
    ext = np.concatenate([w_uq, w_uq[:, :, 0:64], w_uq[:, :, 64:96][:, :, perm]], axis=2)
    sh["l0_w_uq"] = np.ascontiguousarray(ext.reshape(384, 8 * 192))
    w_ukv = inp["l0_mla_w_ukv"].reshape(256, 8, 128)
    sh["l0_w_ukv"] = np.ascontiguousarray(
        np.concatenate([w_ukv[:, :, 0:64].reshape(256, 512), w_ukv[:, :, 64:128].reshape(256, 512)], axis=1))
    sh["l0_qnT"] = _fm(inp["l0_mla_q_norm"], 3)
    sh["l0_kvnT"] = _fm(inp["l0_mla_kv_norm"], 2)
    Cf, Sf = _rope_tables()
    sh["ropeC"] = Cf
    sh["ropeS"] = Sf
    cw = inp["l0_mlstm_conv_w"]
    sh["l0_convT"] = np.ascontiguousarray(
        np.concatenate([cw.reshape(3, 8, 128).transpose(2, 1, 0), inp["l0_mlstm_conv_b"].reshape(8, 128).T[:, :, None]],
                       axis=2))
    gb = np.zeros((16, 1), np.float32)
    for d in range(2):
        for h in range(4):
            gb[d * 8 + h, 0] = inp["l0_mlstm_b_i"][d, h]
            gb[d * 8 + 4 + h, 0] = inp["l0_mlstm_b_f"][d, h]
    sh["l0_gbias"] = gb
    gb2 = np.zeros((64, 2), np.float32)
    for d in range(2):
        for h in range(4):
            gb2[d * 32 + h, 0] = inp["l0_mlstm_b_i"][d, h]
            gb2[d * 32 + h, 1] = inp["l0_mlstm_b_f"][d, h]
    sh["l0_gb2"] = gb2
    sh["l0_hnorm"] = np.ascontiguousarray(inp["l0_mlstm_norm"].reshape(1, 512))
    sh["l0_w_out"] = inp["l0_w_out"]
    sh["l1_w_in"] = inp["l1_w_in"]
    sh["l1_w_gate"] = np.ascontiguousarray(inp["l1_gla_w_gate"])
    sh["l1_bgT"] = np.ascontiguousarray(inp["l1_gla_b_gate"].reshape(2, 2, 128).transpose(2, 0, 1))
    sh["l1_gnorm"] = np.ascontiguousarray(inp["l1_gla_norm"].reshape(1, 512))
    sh["l1_w_out"] = inp["l1_w_out"]
    sh["final_norm"] = np.ascontiguousarray(inp["final_norm"].reshape(1, 1024))
    rpb = inp["l1_na_rpb"]
    kc = np.arange(64)[:, None]
    qc = np.arange(64)[None, :]
    wc0 = np.clip(qc - 8, 0, 48)
    okc = (kc >= wc0) & (kc < wc0 + 16)
    dcol = np.clip(kc - qc + 15, 0, 30)
    Tb = np.full((8, 15, 64, 64), MASKV, np.float32)
    for m in range(15):
        dr = 7 - m
        blk = rpb[:, dr + 7][:, dcol]
        Tb[:, m] = np.where(okc[None], blk, np.float32(MASKV))
    sh["na_bias"] = Tb
    mods = [(inp["l0_norm"], inp["l0_w_mod"], inp["l0_b_mod"]), (inp["l1_norm"], inp["l1_w_mod"], inp["l1_b_mod"])]
    for l, (g_, wm_, bm_) in enumerate(mods):
        sh["l%d_w_mod" % l] = wm_
        sh["l%d_bmodT" % l] = _fm(bm_, 24)
        sh["l%d_bmod_gate" % l] = np.ascontiguousarray(bm_[2048:3072].reshape(1, 1024))
        sh["l%d_gT" % l] = _fm(g_, 8)
    per = []
    for b in range(8):
        d = {}
        d["x"] = inp["x"][b]
        d["ctx"] = inp["ctx"][b]
        cv = np.stack([inp["c"][b], inp["c_ctx"]], axis=1)
        d["cvec"] = np.ascontiguousarray(cv.reshape(8, 128, 2).transpose(1, 0, 2))
        per.append(d)
    return sh, per


def build(sh_shapes, per_shapes, stage=99, debug=(), skip=()):
    nc = bass.Bass("TRN2", target_bir_lowering=False)
    IN = {}
    for k, (shape, dt) in list(sh_shapes.items()) + list(per_shapes.items()):
        IN[k] = nc.dram_tensor(k, list(shape), BF16 if dt == "bf16" else F32, kind="ExternalInput").ap()
    out = nc.dram_tensor("out", [TL, D], F32, kind="ExternalOutput").ap()

    def scratch(name, shape, dt):
        kind = "ExternalOutput" if name in debug else "Internal"
        return nc.dram_tensor(name, list(shape), dt, kind=kind).ap()

    SC = {}
    SC["H1"] = scratch("H1", [T, D], F32)
    SC["SZT"] = scratch("SZT", [1024, T], BF16)
    SC["CATT"] = scratch("CATT", [1024, T], BF16)
    SC["QT"] = scratch("QT", [8, 96, T], BF16)
    SC["KT"] = scratch("KT", [8, 96, T], BF16)
    SC["V"] = scratch("V", [T, 512], BF16)
    SC["MQK"] = scratch("MQK", [1024, T], F32)
    SC["MQKB"] = scratch("MQKB", [1024, T], BF16)
    SC["GI"] = scratch("GI", [8, T], F32)
    SC["GF"] = scratch("GF", [8, T], F32)
    SC["MV"] = scratch("MV", [T, 512], BF16)
    SC["MO"] = scratch("MO", [T, 512], BF16)
    SC["HM"] = scratch("HM", [2, T, 512], F32)
    SC["RD"] = scratch("RD", [16, 512], F32)
    SC["LG"] = scratch("LG", [2, 256, T], F32)
    SC["NQ"] = scratch("NQ", [512, T], BF16)
    SC["NK"] = scratch("NK", [512, T], BF16)

    with ExitStack() as es0:
        P = Prog(nc, es0)
        A0 = Alloc(nc, es0)
        ident_bf = A0.sb([128, 128], BF16, "identbf")
        ident_f = A0.sb([128, 128], F32, "identf")
        ones_f = A0.sb([128, 128], F32, "onesf")
        P.dma("sp", ident_bf[:], IN["ident_bf"][:, :], writes=[ident_bf])
        P.dma("sp", ident_f[:], IN["ident_f"][:, :], writes=[ident_f])
        P.op("pool", "memset", [], [ones_f], ones_f[:], 1.0)
        affA = [A0.sb([128, 8, 2], F32, "affA%d" % l) for l in range(2)]
        affB = [A0.sb([128, 8, 2], F32, "affB%d" % l) for l in range(2)]
        gateR = [[A0.sb([128, 1024], F32, "gateR%d_%d" % (l, s)) for s in range(2 if l == 0 else 1)] for l in range(2)]

        esA0 = es0.enter_context(ExitStack())
        Aw0 = Alloc(nc, esA0)
        w0 = Aw0.sb([128, 8, 3808], BF16, "w_in0")
        w_uq0 = Aw0.sb([128, 3, 1536], BF16, "w_uq0")
        w_ukv0 = Aw0.sb([128, 2, 1024], BF16, "w_ukv0")
        stgA = [Aw0.sb([128, 1024], F32, "stgA%d" % i) for i in range(2)]

        def w0_loader():
            i = 0
            for c0 in range(0, 3808, 128):
                cw = min(128, 3808 - c0)
                s = stgA[i % 2]
                sv = s[:, :].rearrange("p (k n) -> p k n", k=8)
                P.dma("sp" if i % 2 == 0 else "pool", sv[:, :, 0:cw],
                      IN["l0_w_in"][:, c0:c0 + cw].rearrange("(k p) n -> p k n", p=128), writes=[s])
                P.op("dve" if i % 2 == 0 else "act", "tensor_copy" if i % 2 == 0 else "copy", [s], [w0],
                     out=w0[:, :, c0:c0 + cw], in_=sv[:, :, 0:cw])
                i += 1
                yield
            for kk in range(3):
                for hf in range(2):
                    s = stgA[i % 2]
                    P.dma("sp" if i % 2 == 0 else "pool", s[:, 0:768], IN["l0_w_uq"][kk * 128:(kk + 1) * 128, hf * 768:(hf + 1) * 768],
                          writes=[s])
                    P.op("dve" if i % 2 == 0 else "act", "tensor_copy" if i % 2 == 0 else "copy", [s], [w_uq0],
                         out=w_uq0[:, kk, hf * 768:(hf + 1) * 768], in_=s[:, 0:768])
                    i += 1
                    yield
            for kk in range(2):
                s = stgA[i % 2]
                P.dma("sp" if i % 2 == 0 else "pool", s[:, :], IN["l0_w_ukv"][kk * 128:(kk + 1) * 128, :], writes=[s])
                P.op("dve" if i % 2 == 0 else "act", "tensor_copy" if i % 2 == 0 else "copy", [s], [w_ukv0],
                     out=w_ukv0[:, kk, :], in_=s[:, :])
                i += 1
                yield

        wgen = w0_loader()
        with ExitStack() as es:
            A = Alloc(nc, es)
            cv = A.sb([128, 8, 2], F32, "cv")
            sc = A.sb([128, 8, 2], F32, "sc")
            screp = [A.sb([128, 8, 128], F32, "screp%d" % s) for s in range(2)]
            P.dma("sp", cv[:], IN["cvec"][:, :, :], writes=[cv])
            P.op("act", "activation", [cv], [sc], out=sc[:], in_=cv[:], func=AF.Silu)
            for s in range(2):
                for k in range(8):
                    P.op("dve", "tensor_copy", [sc], [screp[s]], out=screp[s][:, k, :],
                         in_=sc[:, k, s:s + 1].to_broadcast([128, 128]))
            wpan = [A.sb([128, 8, 384], F32, "wpan%d" % i) for i in range(2)]
            wgate = [A.sb([128, 512], F32, "wgate%d" % i) for i in range(3)]
            pm = A.ps([128, 24, 2], F32, "pm")
            pg = [A.ps([128, 512], F32, "pg%d" % i) for i in range(2)]
            bmT = A.sb([128, 24], F32, "bmT")
            gT = A.sb([128, 8], F32, "gT")
            modT = A.sb([128, 24, 2], F32, "modT")
            bgrow = A.sb([128, 1024], F32, "bgrow")
            for l in range(2):
                wm = IN["l%d_w_mod" % l]
                P.dma("sp", bmT[:], IN["l%d_bmodT" % l][:, :], writes=[bmT])
                P.dma("sp", gT[:], IN["l%d_gT" % l][:, :], writes=[gT])
                P.dma("sp", bgrow[:], IN["l%d_bmod_gate" % l][0:1, :].to_broadcast([128, 1024]), writes=[bgrow])
                for pn in range(8):
                    wp = wpan[pn % 2]
                    P.dma("sp" if pn % 2 == 0 else "pool", wp[:],
                          wm[:, pn * 384:(pn + 1) * 384].rearrange("(k p) n -> p k n", p=128), writes=[wp])
                    for j in range(3):
                        n = pn * 3 + j
                        for k in range(8):
                            P.op("pe", "matmul", [wp, sc], [pm], pm[:, n, :], wp[:, k, j * 128:(j + 1) * 128],
                                 sc[:, k, :], start=(k == 0), stop=(k == 7))
                    for _ in range(3):
                        next(wgen, None)
                P.op("dve", "tensor_tensor", [pm, bmT], [modT], out=modT[:], in0=pm[:],
                     in1=bmT[:].unsqueeze(2).to_broadcast([128, 24, 2]), op=ALU.add)
                P.op("dve", "tensor_scalar", [modT], [affA[l]], out=affA[l][:], in0=modT[:, 8:16, :], scalar1=1.0,
                     scalar2=None, op0=ALU.add)
                P.op("dve", "tensor_tensor", [affA[l], gT], [affA[l]], out=affA[l][:], in0=affA[l][:],
                     in1=gT[:].unsqueeze(2).to_broadcast([128, 8, 2]), op=ALU.mult)
                P.op("dve", "tensor_copy", [modT], [affB[l]], out=affB[l][:], in_=modT[:, 0:8, :])
                for s in range(len(gateR[l])):
                    for hf in range(2):
                        ps = pg[hf]
                        for k in range(8):
                            wg = wgate[(hf * 8 + k) % 3]
                            P.dma("sp" if k % 2 == 0 else "pool", wg[:],
                                  wm[k * 128:(k + 1) * 128, 2048 + hf * 512:2048 + (hf + 1) * 512], writes=[wg])
                            P.op("pe", "matmul", [wg, screp[s]], [ps], ps[:], screp[s][:, k, :], wg[:],
                                 start=(k == 0), stop=(k == 7))
                        P.op("dve", "tensor_tensor", [ps, bgrow], [gateR[l][s]],
                             out=gateR[l][s][:, hf * 512:(hf + 1) * 512], in0=ps[:],
                             in1=bgrow[:, hf * 512:(hf + 1) * 512], op=ALU.add)
            for _ in wgen:
                pass
            P.barrier()
        if stage <= 0:
            dbg = nc.dram_tensor("dbg_mod", [128, 2, 2, 8, 2], F32, kind="ExternalOutput").ap()
            dbg2 = nc.dram_tensor("dbg_gate", [128, 1024], F32, kind="ExternalOutput").ap()
            for l in range(2):
                P.dma("sp", dbg[:, l, 0], affA[l][:], reads=[affA[l]])
                P.dma("sp", dbg[:, l, 1], affB[l][:], reads=[affB[l]])
            P.dma("sp", dbg2[:, :], gateR[0][1][:], reads=[gateR[0][1]])
            P.barrier()
            return nc

        phase_A(nc, P, IN, SC, 0, affA[0], affB[0], ident_bf, ones_f, w_pre=(w0, w_uq0, w_ukv0))
        esA0.close()
        if stage <= 1:
            return nc
        if 2 not in skip:
            mla_groups = [(0, 256, [0, 1], 0)] + [(256 + 512 * g, 512, list(range(NT)), 0) for g in range(8)]
            attention(nc, P, SC, ones_f, 8, 96, 96.0 ** -0.5, lambda h: SC["QT"][h, :, :], lambda h: SC["KT"][h, :, :],
                      SC["V"], 0, mla_groups)
        if stage <= 2:
            return nc
        if 3 not in skip:
            mlstm_phase(nc, P, IN, SC, ident_bf, ident_f, ones_f)
        if stage <= 3:
            return nc
        combine_phase(nc, P, IN, SC, ident_bf, SC["HM"][0], SC["HM"][1], SC["MO"], "l0_hnorm", 512, GROUPS)
        if stage <= 4:
            return nc
        phase_C(nc, P, IN, SC, 0, gateR[0], out)
        if stage <= 5:
            return nc
        phase_A(nc, P, IN, SC, 1, affA[1], affB[1], ident_bf, ones_f)
        if stage <= 6:
            return nc
        if 7 not in skip:
            gla_phase(nc, P, IN, SC, ident_bf)
            combine_phase(nc, P, IN, SC, ident_bf, SC["HM"][0], SC["HM"][1], None, "l1_gnorm", 0, GROUPS[1:])
        if stage <= 7:
            return nc
        if 8 not in skip:
            na_groups = []
            for g in range(8):
                key, t_lo, nt_ = _na_cfg(g)
                loc = []
                for r in range(nt_):
                    u0, u1 = _na_range(g, t_lo + r)
                    loc.append((2 + t_lo + r, u0 * 64, (u1 + 1) * 64))
                na_groups.append((256 + 512 * g, 512, [0, 1] + loc, g))
            attention(nc, P, SC, ones_f, 8, 64, 64.0 ** -0.5, lambda h: SC["NQ"][h * 64:(h + 1) * 64, :],
                      lambda h: SC["NK"][h * 64:(h + 1) * 64, :], SC["V"], 512, na_groups, bias_fn=na_bias_fn(nc, P, IN, {}),
                      ident_bf=ident_bf, early_release=True, act_recip=True)
        if stage <= 8:
            return nc
        phase_C(nc, P, IN, SC, 1, gateR[1], out)
    return nc


def phase_A(nc, P, IN, SC, layer, affA, affB, ident_bf, ones_f, w_pre=None):
    NW = 3808 if layer == 0 else 3616
    w_in_d = IN["l%d_w_in" % layer]
    with ExitStack() as es:
        A = Alloc(nc, es)
        if w_pre is not None:
            w_in, w_uq, w_ukv = w_pre
        else:
            w_in = A.sb([128, 8, NW], BF16, "w_in")
            if layer == 0:
                w_uq = A.sb([128, 3, 1536], BF16, "w_uq")
                w_ukv = A.sb([128, 2, 1024], BF16, "w_ukv")
        with ExitStack() as es2:
            A2 = Alloc(nc, es2)
            stg = [A2.sb([128, 8, 512], F32, "stg%d" % i) for i in range(2)] if w_pre is None else None
            i = 0
            for c0 in (range(0, NW, 512) if w_pre is None else ()):
                cw = min(512, NW - c0)
                s = stg[i % 2]
                P.dma("sp" if i % 2 == 0 else "pool", s[:, :, 0:cw],
                      w_in_d[:, c0:c0 + cw].rearrange("(k p) n -> p k n", p=128), writes=[s])
                P.op("dve" if i % 2 == 0 else "act", "tensor_copy" if i % 2 == 0 else "copy", [s], [w_in],
                     out=w_in[:, :, c0:c0 + cw], in_=s[:, :, 0:cw])
                i += 1
            if layer == 0 and w_pre is None:
                s = stg[i % 2]
                for kk in range(3):
                    s = stg[i % 2]
                    P.dma("sp", s[:, 0:3, :], IN["l0_w_uq"][kk * 128:(kk + 1) * 128, :].rearrange("p (a n) -> p a n", a=3),
                          writes=[s])
                    P.op("dve", "tensor_copy", [s], [w_uq], out=w_uq[:, kk, :].rearrange("p (a n) -> p a n", a=3),
                         in_=s[:, 0:3, :])
                    i += 1
                s = stg[i % 2]
                for kk in range(2):
                    P.dma("sp", s[:, 2 * kk:2 * kk + 2, :],
                          IN["l0_w_ukv"][kk * 128:(kk + 1) * 128, :].rearrange("p (a n) -> p a n", a=2), writes=[s])
                P.op("dve", "tensor_copy", [s], [w_ukv], out=w_ukv[:].rearrange("p k (a n) -> p (k a) n", a=2),
                     in_=s[:, 0:4, :])
                i += 1
            P.barrier()
        if layer == 0:
            qnT = A.sb([128, 3], F32, "qnT")
            kvnT = A.sb([128, 2], F32, "kvnT")
            P.dma("sp", qnT[:], IN["l0_qnT"][:, :], writes=[qnT])
            P.dma("sp", kvnT[:], IN["l0_kvnT"][:, :], writes=[kvnT])
            cqT = A.sb([128, 3, 512], F32, "cqT")
            ckvT = A.sb([128, 2, 512], F32, "ckvT")
            sq = A.sb([128, 3, 512], BF16, "sq")
            ones_b = A.sb([128, 128], BF16, "ones_b")
            P.op("pool", "memset", [], [ones_b], ones_b[:], 1.0)
            rstd = A.sb([128, 512], F32, "rstd")
            cqn = A.sb([128, 3, 512], BF16, "cqn")
            ckvn = A.sb([128, 2, 512], BF16, "ckvn")
            rC = A.sb([128, 512], F32, "rC")
            rS = A.sb([128, 512], F32, "rS")
            rt1 = A.sb([128, 512], F32, "rt1")
            rt2 = A.sb([128, 512], F32, "rt2")
            qo = [A.sb([128, 512], BF16, "qo%d" % i) for i in range(2)]
            kro = A.sb([32, 512], BF16, "kro")
        else:
            gaT = [A.sb([16, 512], F32, "gaT%d" % d) for d in range(2)]
            wg = A.sb([16, 2, 256], F32, "wg")
            P.dma("sp", wg[:], IN["l1_w_gate"].rearrange("d r k -> r d k"), writes=[wg])
            bgT = A.sb([128, 2, 2], F32, "bgT")
            nbg = A.sb([128, 2, 2], F32, "nbg")
            P.dma("sp", bgT[:], IN["l1_bgT"][:, :, :], writes=[bgT])
            P.op("dve", "tensor_scalar", [bgT], [nbg], out=nbg[:], in0=bgT[:], scalar1=-1.0, scalar2=None, op0=ALU.mult)
            one1 = A.sb([128, 1], F32, "one1a")
            P.op("pool", "memset", [], [one1], one1[:], 1.0)
            lge = A.sb([128, 512], F32, "lge")
            lgo = [A.sb([128, 512], F32, "lgo%d" % i) for i in range(2)]
        hb = [A.sb([128, 1024], F32, "hb%d" % i) for i in range(3)]
        junk = A.sb([128, 1024], F32, "junk")
        st = [A.sb([128, 4], F32, "st%d" % i) for i in range(2)]
        xn2 = [[A.sb([128, 1024], BF16, "xn%d_%d" % (s_, i)) for i in range(4)] for s_ in range(2)]
        epsT = A.sb([128, 1], F32, "epsT")
        P.op("pool", "memset", [], [epsT], epsT[:], EPS)
        uT = [A.sb([128, 8, 512], BF16, "uT%d" % i) for i in range(2)]
        fo_bf = [A.sb([128, 512], BF16, "fobf%d" % i) for i in range(4)]
        fo_f = [A.sb([128, 512], F32, "fof%d" % i) for i in range(3)]
        tp = [A.ps([128, 512], BF16, "tp%d" % i) for i in range(2)]
        acc = [A.ps([128, 512], F32, "acc%d" % i) for i in range(5)]
        cnt = {"acc": 0, "fobf": 0, "fof": 0, "ev": 0, "q": 0, "hb": 0, "xn": 0, "tp": 0}

        def nxt(name, lst):
            r = lst[cnt[name] % len(lst)]
            cnt[name] += 1
            return r

        def evac_engine():
            cnt["ev"] += 1
            return "dve" if cnt["ev"] % 2 == 0 else "act"

        def copy_op(eng, src_t, src_ap, dst_t, dst_ap):
            if eng == "act":
                P.op("act", "copy", [src_t], [dst_t], out=dst_ap, in_=src_ap)
            else:
                P.op(eng, "tensor_copy", [src_t], [dst_t], out=dst_ap, in_=src_ap)

        def stq():
            cnt["q"] += 1
            return "pool" if cnt["q"] % 2 == 0 else "sp"

        def norm_part(gi):
            t0, n = GROUPS[gi]
            ntl = n // 128
            sta = st[gi % 2]
            xn = xn2[gi % 2]
            for ti in range(ntl):
                h = nxt("hb", hb)
                tok = t0 + ti * 128
                if layer == 0:
                    src = IN["ctx"][tok:tok + 128, :] if gi == 0 else IN["x"][tok - TC:tok - TC + 128, :]
                else:
                    src = SC["H1"][tok:tok + 128, :]
                P.dma("sp", h[:], src, writes=[h])
                P.op("act", "activation", [h], [junk, sta], out=junk[:], in_=h[:], func=AF.Square,
                     accum_out=sta[:, ti:ti + 1])
                P.op("act", "activation", [sta, epsT], [sta], out=sta[:, ti:ti + 1], in_=sta[:, ti:ti + 1], func=AF.Sqrt,
                     scale=1.0 / D, bias=epsT[:, 0:1])
                P.op("dve", "reciprocal", [sta], [sta], out=sta[:, ti:ti + 1], in_=sta[:, ti:ti + 1])
                x_ = xn[ti]
                P.op("dve", "tensor_scalar", [h, sta], [x_], out=x_[:], in0=h[:], scalar1=sta[:, ti:ti + 1],
                     scalar2=None, op0=ALU.mult)

        def transpose_part(gi):
            t0, n = GROUPS[gi]
            ntl = n // 128
            s = 1 if gi == 0 else 0
            u = uT[gi % 2]
            xn = xn2[gi % 2]
            for j in range(8):
                tpp = nxt("tp", tp)
                for ti in range(ntl):
                    P.op("pe", "transpose", [xn[ti], ident_bf], [tpp], tpp[:, ti * 128:(ti + 1) * 128],
                         xn[ti][:, j * 128:(j + 1) * 128], ident_bf[:])
                P.op("dve", "tensor_scalar", [tpp, affA, affB], [u], out=u[:, j, 0:n],
                     in0=tpp[:, 0:n], scalar1=affA[:, j, s:s + 1], scalar2=affB[:, j, s:s + 1], op0=ALU.mult,
                     op1=ALU.add)


        def proj_part(gi):
            t0, n = GROUPS[gi]
            ntl = n // 128
            u = uT[gi % 2]

            def fm_proj(c0, ncol):
                ps = nxt("acc", acc)
                for k in range(8):
                    P.op("pe", "matmul", [w_in, u], [ps], ps[0:ncol, 0:n], w_in[:, k, c0:c0 + ncol], u[:, k, 0:n],
                         start=(k == 0), stop=(k == 7))
                return ps

            def store_fm(ps, ncol, dst, dt, func=None, eng=None):
                o = nxt("fobf", fo_bf) if dt == BF16 else nxt("fof", fo_f)
                if func is not None:
                    P.op("act", "activation", [ps], [o], out=o[0:ncol, 0:n], in_=ps[0:ncol, 0:n], func=func)
                else:
                    copy_op(eng or evac_engine(), ps, ps[0:ncol, 0:n], o, o[0:ncol, 0:n])
                P.dma(stq(), dst, o[0:ncol, 0:n], reads=[o])

            tsl = slice(t0, t0 + n)
            if layer == 0:
                for j in range(3):
                    ps = fm_proj(j * 128, 128)
                    copy_op(evac_engine(), ps, ps[:, 0:n], cqT, cqT[:, j, 0:n])
                for j in range(2):
                    ps = fm_proj(384 + j * 128, 128)
                    copy_op(evac_engine(), ps, ps[:, 0:n], ckvT, ckvT[:, j, 0:n])
                for (src_t, nk, nrm, dst_t, dim) in ((cqT, 3, qnT, cqn, 384.0), (ckvT, 2, kvnT, ckvn, 256.0)):
                    P.op("act", "activation", [src_t], [sq], out=sq[:, 0:nk, 0:n], in_=src_t[:, 0:nk, 0:n], func=AF.Square)
                    ps = nxt("acc", acc)
                    for k in range(nk):
                        P.op("pe", "matmul", [ones_b, sq], [ps], ps[:, 0:n], ones_b[:], sq[:, k, 0:n], start=(k == 0),
                             stop=(k == nk - 1))
                    P.op("act", "activation", [ps, epsT], [rstd], out=rstd[:, 0:n], in_=ps[:, 0:n], func=AF.Sqrt,
                         scale=1.0 / dim, bias=epsT[:, 0:1])
                    P.op("dve", "reciprocal", [rstd], [rstd], out=rstd[:, 0:n], in_=rstd[:, 0:n])
                    for k in range(nk):
                        P.op("dve", "scalar_tensor_tensor", [src_t, nrm, rstd], [dst_t], out=dst_t[:, k, 0:n],
                             in0=src_t[:, k, 0:n], scalar=nrm[:, k:k + 1], in1=rstd[:, 0:n], op0=ALU.mult, op1=ALU.mult)
                rot = gi > 0
                if rot:
                    P.dma("sp", rC[:, 0:n], IN["ropeC"][:, t0 - TC:t0 - TC + n], writes=[rC])
                    P.dma("sp", rS[:, 0:n], IN["ropeS"][:, t0 - TC:t0 - TC + n], writes=[rS])
                for hh in range(8):
                    ps = nxt("acc", acc)
                    for k in range(3):
                        P.op("pe", "matmul", [w_uq, cqn], [ps], ps[0:96, 0:n], w_uq[:, k, hh * 192:hh * 192 + 96],
                             cqn[:, k, 0:n], start=(k == 0), stop=(k == 2))
                    o = nxt("fobf", fo_bf)
                    if rot:
                        ps2 = nxt("acc", acc)
                        for k in range(3):
                            P.op("pe", "matmul", [w_uq, cqn], [ps2], ps2[0:96, 0:n],
                                 w_uq[:, k, hh * 192 + 96:hh * 192 + 192], cqn[:, k, 0:n], start=(k == 0), stop=(k == 2))
                        copy_op("act", ps, ps[0:64, 0:n], o, o[0:64, 0:n])
                        P.op("dve", "tensor_tensor", [ps, rC], [rt1], out=rt1[64:96, 0:n], in0=ps[64:96, 0:n],
                             in1=rC[64:96, 0:n], op=ALU.mult)
                        P.op("dve", "tensor_tensor", [ps2, rS], [rt2], out=rt2[64:96, 0:n], in0=ps2[64:96, 0:n],
                             in1=rS[64:96, 0:n], op=ALU.mult)
                        P.op("pool", "tensor_tensor", [rt1, rt2], [o], out=o[64:96, 0:n], in0=rt1[64:96, 0:n],
                             in1=rt2[64:96, 0:n], op=ALU.add)
                    else:
                        copy_op(evac_engine(), ps, ps[0:96, 0:n], o, o[0:96, 0:n])
                    P.dma(stq(), SC["QT"][hh, :, tsl], o[0:96, 0:n], reads=[o])
                for c in range(4):
                    ps = nxt("acc", acc)
                    for k in range(2):
                        P.op("pe", "matmul", [w_ukv, ckvn], [ps], ps[:, 0:n], w_ukv[:, k, c * 128:(c + 1) * 128],
                             ckvn[:, k, 0:n], start=(k == 0), stop=(k == 1))
                    o = nxt("fobf", fo_bf)
                    copy_op(evac_engine(), ps, ps[:, 0:n], o, o[:, 0:n])
                    for hh in range(2):
                        P.dma(stq(), SC["KT"][c * 2 + hh, 0:64, tsl], o[hh * 64:(hh + 1) * 64, 0:n], reads=[o])
                for ti in range(ntl):
                    ps = nxt("acc", acc)
                    for k in range(2):
                        P.op("pe", "matmul", [w_ukv, ckvn], [ps], ps[:, :], ckvn[:, k, ti * 128:(ti + 1) * 128],
                             w_ukv[:, k, 512:1024], start=(k == 0), stop=(k == 1))
                    o = nxt("fobf", fo_bf)
                    copy_op(evac_engine(), ps, ps[:, :], o, o[:, :])
                    P.dma(stq(), SC["V"][t0 + ti * 128:t0 + (ti + 1) * 128, :], o[:, :], reads=[o])
                ps = fm_proj(640, 32)
                if rot:
                    ps2 = fm_proj(3760, 32)
                    P.op("dve", "tensor_tensor", [ps, rC], [rt1], out=rt1[0:32, 0:n], in0=ps[0:32, 0:n], in1=rC[0:32, 0:n],
                         op=ALU.mult)
                    P.op("dve", "tensor_tensor", [ps2, rS], [rt2], out=rt2[0:32, 0:n], in0=ps2[0:32, 0:n],
                         in1=rS[0:32, 0:n], op=ALU.mult)
                    P.op("pool", "tensor_tensor", [rt1, rt2], [kro], out=kro[0:32, 0:n], in0=rt1[0:32, 0:n],
                         in1=rt2[0:32, 0:n], op=ALU.add)
                else:
                    copy_op("dve", ps, ps[0:32, 0:n], kro, kro[0:32, 0:n])
                for hh in range(8):
                    P.dma(stq(), SC["KT"][hh, 64:96, tsl], kro[0:32, 0:n], reads=[kro])
                yield
                for c in range(8):
                    ps = fm_proj(672 + c * 128, 128)
                    store_fm(ps, 128, SC["MQKB"][c * 128:(c + 1) * 128, tsl], BF16)
                ps = fm_proj(3792, 8)
                store_fm(ps, 8, SC["GI"][:, tsl], F32)
                ps = fm_proj(3800, 8)
                store_fm(ps, 8, SC["GF"][:, tsl], F32)
                for c in range(8):
                    ps = fm_proj(2736 + c * 128, 128)
                    store_fm(ps, 128, SC["SZT"][c * 128:(c + 1) * 128, tsl], BF16, func=AF.Silu)
                tm_specs = [(1696, SC["MV"], None), (2208, SC["MO"], AF.Sigmoid)]
            else:
                for c in range(4):
                    ps = fm_proj(c * 128, 128)
                    store_fm(ps, 128, SC["MQK"][c * 128:(c + 1) * 128, tsl], F32)
                yield
                for d in range(2):
                    ps = fm_proj(1024 + 16 * d, 16)
                    copy_op("dve", ps, ps[0:16, 0:n], gaT[d], gaT[d][0:16, 0:n])
                for d in range(2):
                    for c2 in range(2):
                        ps = nxt("acc", acc)
                        P.op("pe", "matmul", [wg, gaT[d]], [ps], ps[:, 0:n], wg[0:16, d, c2 * 128:(c2 + 1) * 128],
                             gaT[d][0:16, 0:n], start=True, stop=True)
                        P.op("act", "activation", [ps, nbg], [lge], out=lge[:, 0:n], in_=ps[:, 0:n], func=AF.Exp, scale=-1.0,
                             bias=nbg[:, d, c2:c2 + 1])
                        P.op("act", "activation", [lge, one1], [lge], out=lge[:, 0:n], in_=lge[:, 0:n], func=AF.Ln,
                             bias=one1[:, 0:1])
                        o = lgo[(d * 2 + c2) % 2]
                        P.op("dve", "tensor_scalar", [lge], [o], out=o[:, 0:n], in0=lge[:, 0:n], scalar1=-1.0 / 16.0,
                             scalar2=None, op0=ALU.mult)
                        P.dma(stq(), SC["LG"][d, c2 * 128:(c2 + 1) * 128, tsl], o[:, 0:n], reads=[o])
                for c in range(4):
                    ps = fm_proj(1056 + c * 128, 128)
                    store_fm(ps, 128, SC["NQ"][c * 128:(c + 1) * 128, tsl], BF16)
                for c in range(4):
                    ps = fm_proj(1568 + c * 128, 128)
                    store_fm(ps, 128, SC["NK"][c * 128:(c + 1) * 128, tsl], BF16)
                for c in range(8):
                    ps = fm_proj(2592 + c * 128, 128)
                    store_fm(ps, 128, SC["SZT"][c * 128:(c + 1) * 128, tsl], BF16, func=AF.Silu)
                tm_specs = [(512, SC["MV"], None), (2080, SC["V"], None)]
            for (c0, dst, func) in tm_specs:
                for ti in range(ntl):
                    ps = nxt("acc", acc)
                    for k in range(8):
                        P.op("pe", "matmul", [w_in, u], [ps], ps[:, :], u[:, k, ti * 128:(ti + 1) * 128],
                             w_in[:, k, c0:c0 + 512], start=(k == 0), stop=(k == 7))
                    o = nxt("fobf", fo_bf)
                    if func is not None:
                        P.op("act", "activation", [ps], [o], out=o[:, :], in_=ps[:, :], func=func)
                    else:
                        copy_op(evac_engine(), ps, ps[:, :], o, o[:, :])
                    P.dma(stq(), dst[t0 + ti * 128:t0 + (ti + 1) * 128, :], o[:, :], reads=[o])

        norm_part(0)
        transpose_part(0)
        for gi in range(len(GROUPS)):
            if gi + 1 < len(GROUPS):
                norm_part(gi + 1)
            gen = proj_part(gi)
            next(gen)
            if gi + 1 < len(GROUPS):
                transpose_part(gi + 1)
            for _ in gen:
                pass
        P.barrier()


def attention(nc, P, SC, ones_f, heads, dq, scale, load_q, load_k, Vd, cat_row0, groups, bias_fn=None, ident_bf=None,
              early_release=False, act_recip=False):
    LOOK = 4
    NS = 5
    EPI_DELAY = 8
    with ExitStack() as es:
        A = Alloc(nc, es)
        V = A.sb([128, NT, heads, 65], BF16, "Vall")
        P.op("pool", "memset", [], [V], V[:, :, :, 64:65], 1.0)
        for half in range(2):
            tl = slice(half * 17, (half + 1) * 17)
            for hh in range(heads):
                P.dma("sp" if hh % 2 == 0 else "pool", V[:, tl, hh, 0:64],
                      Vd[half * 17 * 128:(half + 1) * 17 * 128, hh * 64:(hh + 1) * 64].rearrange("(t p) d -> p t d", p=128),
                      writes=[V])
        kT = [A.sb([128, T], BF16, "kT%d" % i) for i in range(2)]
        qT = [A.sb([128, T], BF16, "qT%d" % i) for i in range(2)]
        pt = [A.sb([128, 512], BF16, "pt%d" % i) for i in range(NS)]
        sb_t = [A.sb([128, 512], F32, "sbt%d" % i) for i in range(3)] if bias_fn is not None else None
        rden = [A.sb([128, 512], F32, "rden%d" % i) for i in range(2)]
        ocp = [A.sb([128, 512], F32, "ocp%d" % i) for i in range(3)] if early_release else None
        rsc = A.sb([128, 512], F32, "rsc")
        bcs = [A.sb([128, 512], F32, "bcs%d" % i) for i in range(2)]
        szt = [A.sb([64, 512], BF16, "szt%d" % i) for i in range(3)]
        tmp = [A.sb([64, 512], F32, "atmp%d" % i) for i in range(2)]
        ao = [A.sb([64, 512], BF16, "ao%d" % i) for i in range(2)]
        Sps = [A.ps([128, 512], F32, "Sps%d" % i) for i in range(NS)]
        Ops = [A.ps([128, 512], F32, "Ops%d" % i) for i in range(2)]
        Bps = A.ps([128, 512], F32, "Bps")
        if dq < 128:
            for t_ in kT + qT:
                P.op("pool", "memset", [], [t_], t_[64:128, :], 0.0)
        P.dma("sp", kT[0][0:dq, :], load_k(0), writes=[kT[0]])
        P.dma("pool", qT[0][0:dq, :], load_q(0), writes=[qT[0]])
        gcount = 0
        it = 0
        rd_dep = [DramDep() for _ in range(16)]
        pend = []
        for h in range(heads):
            k_ = kT[h % 2]
            q_ = qT[h % 2]
            if h + 1 < heads:
                P.dma("sp", kT[(h + 1) % 2][0:dq, :], load_k(h + 1), writes=[kT[(h + 1) % 2]])
                P.dma("pool", qT[(h + 1) % 2][0:dq, :], load_q(h + 1), writes=[qT[(h + 1) % 2]])
            bias_loader = None
            if bias_fn is not None:
                if h == 0:
                    bias_cur, ld0 = bias_fn(0, A)
                    for _ in ld0:
                        pass
                bias_tiles = bias_cur
                if h + 1 < heads:
                    bias_cur, bias_loader = bias_fn(h + 1, A if h == 0 else None)
            else:
                bias_tiles = None
            r0 = cat_row0 + h * 64
            items = []
            for (q0, n, tiles, gkey) in groups:
                gid = gcount
                gcount += 1
                for j, kt in enumerate(tiles):
                    if isinstance(kt, tuple):
                        kt, c0, c1 = kt
                    else:
                        c0, c1 = 0, n
                    items.append((gid, q0, n, gkey, j, kt, len(tiles), c0, c1))

            def flush(cond):
                for e_ in pend[:]:
                    if cond(e_[1][0]):
                        emit_epi(*e_[1])
                        pend.remove(e_)

            def emit_S(item, slot):
                gid, q0, n, gkey, j, kt, nt_, c0, c1 = item
                S = Sps[slot % NS]
                p_ = pt[slot % NS]
                if j == 0:
                    flush(lambda g2: g2 % 3 == gid % 3)
                    sz = szt[gid % 3]
                    P.dma("sp", sz[:, 0:n], SC["SZT"][r0:r0 + 64, q0:q0 + n], writes=[sz])
                P.op("pe", "matmul", [k_, q_], [S], S[:, c0:c1], k_[:, kt * 128:(kt + 1) * 128], q_[:, q0 + c0:q0 + c1],
                     start=True, stop=True)
                bt = bias_tiles.get((gkey, kt)) if bias_tiles is not None else None
                if bt is not None:
                    sb = sb_t[slot % 3]
                    P.op("dve", "scalar_tensor_tensor", [S, bt], [sb], out=sb[:, c0:c1], in0=S[:, c0:c1], scalar=scale,
                         in1=bt[:, c0:c1], op0=ALU.mult, op1=ALU.add)
                    P.op("act", "activation", [sb], [p_], out=p_[:, c0:c1], in_=sb[:, c0:c1], func=AF.Exp)
                else:
                    P.op("act", "activation", [S], [p_], out=p_[:, c0:c1], in_=S[:, c0:c1], func=AF.Exp, scale=scale)

            def emit_PV(item, slot):
                gid, q0, n, gkey, j, kt, nt_, c0, c1 = item
                O = Ops[gid % 2]
                p_ = pt[slot % NS]
                assert j > 0 or (c0 == 0 and c1 == n)
                if j == 0:
                    flush(lambda g2: g2 % 2 == gid % 2)
                P.op("pe", "matmul", [V, p_], [O], O[0:65, c0:c1], V[:, kt, h, :], p_[:, c0:c1], start=(j == 0),
                     stop=(j == nt_ - 1))
                if j == nt_ - 1:
                    rd = rden[gid % 2]
                    if early_release:
                        oc = ocp[gid % 3]
                        P.op("act", "copy", [O], [oc], out=oc[0:65, 0:n], in_=O[0:65, 0:n])
                        P.op("act", "activation", [oc], [rsc], out=rsc[64:65, 0:n], in_=oc[64:65, 0:n], func=AF.Ln)
                        P.op("act", "activation", [rsc], [rd], out=rd[64:65, 0:n], in_=rsc[64:65, 0:n], func=AF.Exp, scale=-1.0)
                    elif act_recip:
                        P.op("act", "activation", [O], [rsc], out=rsc[64:65, 0:n], in_=O[64:65, 0:n], func=AF.Ln)
                        P.op("act", "activation", [rsc], [rd], out=rd[64:65, 0:n], in_=rsc[64:65, 0:n], func=AF.Exp, scale=-1.0)
                    else:
                        P.op("dve", "reciprocal", [O], [rd], out=rd[64:65, 0:n], in_=O[64:65, 0:n])
                    pend.append([EPI_DELAY, (gid, q0, n, r0, h)])

            def emit_epi(gid, q0, n, r0, h):
                O = Ops[gid % 2]
                rd = rden[gid % 2]
                bc_ = bcs[gid % 2]
                tm_ = tmp[gid % 2]
                a_ = ao[gid % 2]
                sz = szt[gid % 3]
                P.op("pe", "matmul", [ones_f, rd], [Bps], Bps[0:64, 0:n], ones_f[64:65, 0:64], rd[64:65, 0:n],
                     start=True, stop=True)
                if early_release:
                    oc = ocp[gid % 3]
                    P.op("dve", "tensor_tensor", [oc, Bps], [tm_], out=tm_[:, 0:n], in0=oc[0:64, 0:n], in1=Bps[0:64, 0:n],
                         op=ALU.mult)
                else:
                    P.op("dve", "tensor_copy", [Bps], [bc_], out=bc_[0:64, 0:n], in_=Bps[0:64, 0:n])
                    P.op("dve", "tensor_tensor", [O, bc_], [tm_], out=tm_[:, 0:n], in0=O[0:64, 0:n], in1=bc_[0:64, 0:n],
                         op=ALU.mult)
                P.op("pool", "tensor_tensor", [tm_, sz], [a_], out=a_[:, 0:n], in0=tm_[:, 0:n], in1=sz[:, 0:n], op=ALU.mult)
                P.dma("pool", SC["CATT"][r0:r0 + 64, q0:q0 + n], a_[:, 0:n], reads=[a_])

            nI = len(items)
            for idx in range(nI + LOOK):
                if idx < nI:
                    emit_S(items[idx], it + idx)
                for e_ in pend[:]:
                    e_[0] -= 1
                    if e_[0] <= 0:
                        emit_epi(*e_[1])
                        pend.remove(e_)
                if idx - LOOK >= 0:
                    emit_PV(items[idx - LOOK], it + idx - LOOK)
                if bias_loader is not None and idx % 3 == 2:
                    next(bias_loader, None)
            if bias_loader is not None:
                for _ in bias_loader:
                    pass
            it += nI
        for e_ in pend:
            emit_epi(*e_[1])
        P.barrier()


def mlstm_phase(nc, P, IN, SC, ident_bf, ident_f, ones_f):
    NB = NT
    NCH = T // 64
    with ExitStack() as es:
        A = Alloc(nc, es)
        esT = A.sb([128, NB, 64], F32, "esT")
        fT = A.sb([128, NB, 64], F32, "fT")
        decbc = A.sb([128, 8, NCH], F32, "decbc")
        mask = [A.sb([128, 64], F32, "mask%d" % d) for d in range(2)]
        for d in range(2):
            P.op("pool", "memset", [], [mask[d]], mask[d][:], 1.0)
            for half in range(2):
                pr = slice(half * 64, half * 64 + 64)
                P.op("pool", "affine_select", [mask[d]], [mask[d]], out=mask[d][pr, :], in_=mask[d][pr, :],
                     pattern=[[1 if d == 0 else -1, 64]], compare_op=ALU.is_ge, fill=0.0, base=0,
                     channel_multiplier=-1 if d == 0 else 1)
        with ExitStack() as es2:
            A2 = Alloc(nc, es2)
            X1 = A2.sb([64, T], F32, "X1")
            X2 = A2.sb([64, T], F32, "X2")
            X3 = A2.sb([64, T], F32, "X3")
            X4 = A2.sb([64, T], F32, "X4")
            gb = A2.sb([64, 2], F32, "gb")
            nbf = A2.sb([64, 1], F32, "nbf")
            one1 = A2.sb([64, 1], F32, "one1")
            dec = A2.sb([64, NCH], F32, "dec")
            aprev = A2.sb([64, NCH], F32, "aprev")
            sel = A2.sb([64, 128], F32, "sel")
            pst = [A2.ps([128, 8, 64], F32, "pst%d" % i) for i in range(2)]
            psd = A2.ps([128, NCH], F32, "psd")
            P.op("pool", "memset", [], [X1], X1[:], 0.0)
            P.op("pool", "memset", [], [X3], X3[:], 0.0)
            P.op("pool", "memset", [], [one1], one1[:], 1.0)
            P.dma("sp", gb[:], IN["l0_gb2"][:, :], writes=[gb])
            for d in range(2):
                P.dma("sp", X1[d * 32:d * 32 + 4, :], SC["GF"][d * 4:d * 4 + 4, :], writes=[X1])
                P.dma("pool", X3[d * 32:d * 32 + 4, :], SC["GI"][d * 4:d * 4 + 4, :], writes=[X3])
            P.op("dve", "tensor_scalar", [gb], [nbf], out=nbf[:], in0=gb[:, 1:2], scalar1=-1.0, scalar2=None, op0=ALU.mult)
            P.op("act", "activation", [X1, nbf], [X1], out=X1[:], in_=X1[:], func=AF.Exp, scale=-1.0, bias=nbf[:, 0:1])
            P.op("act", "activation", [X1, one1], [X1], out=X1[:], in_=X1[:], func=AF.Ln, bias=one1[:, 0:1])

            def seg_views(tile_, prng, d):
                if d == 0:
                    return [tile_[prng, 0:T]]
                return [tile_[prng, 0:TC][:, ::-1], tile_[prng, TC:T][:, ::-1]]

            def scan(dst, src, op0, d):
                prng = slice(d * 32, d * 32 + 32)
                dv = seg_views(dst, prng, d)
                sv = seg_views(src, prng, d)
                for i in range(len(dv)):
                    init = 0.0 if i == 0 else dst[prng, 0:1]
                    P.op("dve", "tensor_tensor_scan", [src, dst], [dst], out=dv[i], data0=sv[i], data1=sv[i],
                         initial=init, op0=op0, op1=ALU.bypass)

            for d in range(2):
                scan(X2, X1, ALU.add, d)
            P.op("dve", "scalar_tensor_tensor", [X3, gb, X2], [X3], out=X3[:], in0=X3[:], scalar=gb[:, 0:1], in1=X2[:],
                 op0=ALU.add, op1=ALU.add)
            for d in range(2):
                scan(X1, X3, ALU.max, d)
            for d in range(2):
                prng = slice(d * 32, d * 32 + 32)
                jj = 63 if d == 0 else 0
                P.op("dve", "tensor_copy", [X1], [X4], out=X4[prng, :].rearrange("p (c j) -> p c j", j=64),
                     in_=X1[prng, :].rearrange("p (c j) -> p c j", j=64)[:, :, jj:jj + 1].to_broadcast([32, NCH, 64]))
            aend = X4[:, :].rearrange("p (c j) -> p c j", j=64)[:, :, 0]
            P.op("pool", "memset", [], [aprev], aprev[:], 0.0)
            P.op("dve", "tensor_copy", [X4], [aprev], out=aprev[0:32, 1:NCH], in_=aend[0:32, 0:NCH - 1])
            P.op("dve", "tensor_copy", [X4], [aprev], out=aprev[32:64, 0:3], in_=aend[32:64, 1:4])
            P.op("dve", "tensor_copy", [X4], [aprev], out=aprev[32:64, 4:NCH - 1], in_=aend[32:64, 5:NCH])
            P.op("dve", "tensor_copy", [X4], [aprev], out=aprev[32:64, NCH - 1:NCH], in_=aend[32:64, 0:1])
            P.op("dve", "tensor_tensor", [aprev, X4], [dec], out=dec[:], in0=aprev[:], in1=aend, op=ALU.subtract)
            P.op("act", "activation", [dec], [dec], out=dec[:], in_=dec[:], func=AF.Exp)
            P.op("dve", "tensor_tensor", [X3, X4], [X3], out=X3[:], in0=X3[:], in1=X4[:], op=ALU.subtract)
            P.op("act", "activation", [X3], [X3], out=X3[:], in_=X3[:], func=AF.Exp)
            P.op("dve", "tensor_tensor", [X2, X4], [X2], out=X2[:], in0=X2[:], in1=X4[:], op=ALU.subtract)
            P.op("act", "activation", [X2], [X2], out=X2[:], in_=X2[:], func=AF.Exp)
            for (srcX, dstT) in ((X3, esT), (X2, fT)):
                for b0 in range(0, NB, 8):
                    nb = min(8, NB - b0)
                    ps = pst[(b0 // 8) % 2]
                    for bb in range(nb):
                        P.op("pe", "transpose", [srcX, ident_f], [ps], ps[:, bb, :], srcX[:, (b0 + bb) * 128:(b0 + bb + 1) * 128],
                             ident_f[0:64, 0:64])
                    P.op("act", "copy", [ps], [dstT], out=dstT[:, b0:b0 + nb, :], in_=ps[:, 0:nb, :])
            for idx in range(8):
                r = (idx // 4) * 32 + idx % 4
                P.op("dve", "tensor_copy", [ident_f], [sel], out=sel[:], in_=ident_f[0:64, r:r + 1].to_broadcast([64, 128]))
                P.op("pe", "matmul", [sel, dec], [psd], psd[:, :], sel[:, :], dec[:, :], start=True, stop=True)
                P.op("act", "copy", [psd], [decbc], out=decbc[:, idx, :], in_=psd[:, :])
            P.barrier()
        P.op("dve", "tensor_scalar", [esT], [esT], out=esT[:], in0=esT[:], scalar1=128.0 ** -0.5, scalar2=None, op0=ALU.mult)
        xraw = A.sb([128, T], BF16, "xraw")
        cvw = A.sb([128, 8, 4], F32, "cvw")
        P.dma("sp", cvw[:], IN["l0_convT"][:, :, :], writes=[cvw])
        dg = [A.sb([128, 3, 128], BF16, "dg%d" % i) for i in range(2)]
        qT = A.sb([128, T], BF16, "mqT")
        qd = [A.sb([128, T], BF16, "mqd%d" % d) for d in range(2)]
        kT = A.sb([128, T], BF16, "mkT")
        kTok = A.sb([128, NB, 128], BF16, "kTok")
        vtok = A.sb([128, NB, 128], BF16, "vtok")
        vpp = [A.sb([128, NB, 129], BF16, "vpp%d" % d) for d in range(2)]
        SmT = [A.sb([128, NB, 64], BF16, "SmT%d" % d) for d in range(2)]
        hbuf = [A.sb([128, NB, 129], F32, "hbuf%d" % d) for d in range(2)]
        Cst = [[A.sb([128, 129], F32, "C%d_%d" % (d, i)) for i in range(2)] for d in range(2)]
        Cb = [[A.sb([128, 129], BF16, "Cb%d_%d" % (d, i)) for i in range(2)] for d in range(2)]
        dn = [A.sb([128, NB], F32, "dn%d" % d) for d in range(2)]
        pcv = [A.ps([128, 512], F32, "pcv%d" % i) for i in range(2)]
        pU = [A.ps([128, 129], F32, "pU%d" % i) for i in range(2)]
        pN = [[A.ps([128, 129], F32, "pN%d_%d" % (d, i)) for i in range(2)] for d in range(2)]
        order = [list(range(NCH)), [3, 2, 1, 0] + list(range(NCH - 1, 3, -1))]
        pieces = [(0, TC)] + [(TC + 512 * i, TC + 512 * (i + 1)) for i in range(8)]
        pc = 0
        for h in range(4):
            for which in range(2):
                ch = which * 4 + h
                dg_ = dg[which]
                P.dma("sp" if which == 0 else "pool", xraw[:], SC["MQKB"][ch * 128:(ch + 1) * 128, :], writes=[xraw])
                for j in range(3):
                    P.op("dve", "tensor_scalar", [ident_f, cvw], [dg_], out=dg_[:, j, :], in0=ident_f[:], scalar1=cvw[:, ch, j:j + 1],
                         scalar2=None, op0=ALU.mult)
                dst = qT if which == 0 else kT
                for (a, b) in pieces:
                    s0, s1 = (0, TC) if a < TC else (TC, T)
                    ps = pcv[pc % 2]
                    pc += 1
                    P.op("pe", "matmul", [dg_, xraw], [ps], ps[:, 0:b - a], dg_[:, 1, :], xraw[:, a:b], start=True, stop=False)
                    lo = max(a, s0 + 1)
                    P.op("pe", "matmul", [dg_, xraw], [ps], ps[:, lo - a:b - a], dg_[:, 0, :], xraw[:, lo - 1:b - 1], start=False,
                         stop=False)
                    hi = min(b, s1 - 1)
                    P.op("pe", "matmul", [dg_, xraw], [ps], ps[:, 0:hi - a], dg_[:, 2, :], xraw[:, a + 1:hi + 1], start=False,
                         stop=True)
                    P.op("act", "activation", [ps, cvw], [dst], out=dst[:, a:b], in_=ps[:, 0:b - a], func=AF.Silu,
                         bias=cvw[:, ch, 3:4])
            for b0 in range(0, NB, 4):
                nb = min(4, NB - b0)
                ps = pcv[pc % 2]
                pc += 1
                psb = ps[:, 0:256].bitcast(BF16)
                for bb in range(nb):
                    P.op("pe", "transpose", [kT, ident_bf], [ps], psb[:, bb * 128:(bb + 1) * 128],
                         kT[:, (b0 + bb) * 128:(b0 + bb + 1) * 128], ident_bf[:])
                P.op("act", "copy", [ps], [kTok], out=kTok[:, b0:b0 + nb, :],
                     in_=psb[:, 0:nb * 128].rearrange("p (b j) -> p b j", j=128))
            P.dma("sp", vtok[:], SC["MV"][:, h * 128:(h + 1) * 128].rearrange("(b p) j -> p b j", p=128), writes=[vtok])
            for d in range(2):
                col = d * 32 + h
                idx = d * 4 + h
                P.op("pool" if d == 0 else "dve", "tensor_tensor", [vtok, esT], [vpp[d]], out=vpp[d][:, :, 0:128], in0=vtok[:],
                     in1=esT[:, :, col:col + 1].to_broadcast([128, NB, 128]), op=ALU.mult)
                P.op("dve", "tensor_copy", [esT], [vpp[d]], out=vpp[d][:, :, 128:129], in_=esT[:, :, col:col + 1])
                P.op("pool" if d == 1 else "dve", "tensor_tensor", [qT, decbc], [qd[d]],
                     out=qd[d][:, :].rearrange("p (c j) -> p c j", j=64), in0=qT[:, :].rearrange("p (c j) -> p c j", j=64),
                     in1=decbc[:, idx, :].unsqueeze(2).to_broadcast([128, NCH, 64]), op=ALU.mult)
            for b0 in range(0, NB, 4):
                nb = min(4, NB - b0)
                ps = pcv[pc % 2]
                pc += 1
                psv = ps[:, 0:256].rearrange("p (b j) -> p b j", j=64)
                for bb in range(nb):
                    for half in range(2):
                        c = (b0 + bb) * 2 + half
                        pr = slice(half * 64, half * 64 + 64)
                        P.op("pe", "matmul", [kT, qT], [ps], psv[pr, bb, :], kT[:, c * 64:(c + 1) * 64], qT[:, c * 64:(c + 1) * 64],
                             start=True, stop=True)
                for d in range(2):
                    P.op("dve", "tensor_tensor", [ps, mask[d]], [SmT[d]], out=SmT[d][:, b0:b0 + nb, :], in0=psv[:, 0:nb, :],
                         in1=mask[d][:].unsqueeze(1).to_broadcast([128, nb, 64]), op=ALU.mult)
            for d in range(2):
                P.op("pool", "memset", [], [Cst[d][0]], Cst[d][0][:], 0.0)
                P.op("pool", "memset", [], [Cb[d][0]], Cb[d][0][:], 0.0)
            for i in range(NCH):
                for d in range(2):
                    c = order[d][i]
                    b, half = c // 2, c % 2
                    pr = slice(half * 64, half * 64 + 64)
                    idx = d * 4 + h
                    Cold, Cnew = Cst[d][i % 2], Cst[d][(i + 1) % 2]
                    cbo, cbn = Cb[d][i % 2], Cb[d][(i + 1) % 2]
                    U = pU[d]
                    N = pN[d][i % 2]
                    P.op("pe", "matmul", [kTok, vpp[d]], [U], U[:, :], kTok[pr, b, :], vpp[d][pr, b, :], start=True, stop=True)
                    P.op("pe", "matmul", [SmT[d], vpp[d]], [N], N[pr, :], SmT[d][pr, b, :], vpp[d][pr, b, :], start=True,
                         stop=False)
                    P.op("pe", "matmul", [qd[d], cbo], [N], N[pr, :], qd[d][:, c * 64:(c + 1) * 64], cbo[:], start=False, stop=True)
                    P.op("dve", "scalar_tensor_tensor", [Cold, decbc, U], [cbn], out=cbn[:], in0=Cold[:],
                         scalar=decbc[:, idx, c:c + 1], in1=U[:, :], op0=ALU.mult, op1=ALU.add)
                    P.op("dve", "scalar_tensor_tensor", [Cold, decbc, U], [Cnew], out=Cnew[:], in0=Cold[:],
                         scalar=decbc[:, idx, c:c + 1], in1=U[:, :], op0=ALU.mult, op1=ALU.add)
                    P.op("act", "copy", [N], [hbuf[d]], out=hbuf[d][pr, b, :], in_=N[pr, :])
            for d in range(2):
                col = d * 32 + h
                P.op("act", "activation", [hbuf[d]], [dn[d]], out=dn[d][:, :].unsqueeze(2), in_=hbuf[d][:, :, 128:129], func=AF.Abs)
                P.op("dve", "tensor_tensor", [dn[d], fT], [dn[d]], out=dn[d][:, :].unsqueeze(2), in0=dn[d][:, :].unsqueeze(2),
                     in1=fT[:, :, col:col + 1], op=ALU.max)
                P.op("dve", "reciprocal", [dn[d]], [dn[d]], out=dn[d][:], in_=dn[d][:])
                P.op("dve" if d == 0 else "pool", "tensor_tensor", [hbuf[d], dn[d]], [hbuf[d]], out=hbuf[d][:, :, 0:128],
                     in0=hbuf[d][:, :, 0:128], in1=dn[d][:, :].unsqueeze(2).to_broadcast([128, NB, 128]), op=ALU.mult)
                P.dma("sp" if d == 0 else "pool", SC["HM"][d, :, h * 128:(h + 1) * 128].rearrange("(b p) j -> p b j", p=128),
                      hbuf[d][:, :, 0:128], reads=[hbuf[d]])
        P.barrier()


def combine_phase(nc, P, IN, SC, ident_bf, src0, src1, mul, norm_name, cat_row0, groups):
    with ExitStack() as es:
        A = Alloc(nc, es)
        nrow = A.sb([128, 512], F32, "nrow")
        P.dma("sp", nrow[:], IN[norm_name][0:1, :].to_broadcast([128, 512]), writes=[nrow])
        epsT = A.sb([128, 1], F32, "epsTc")
        P.op("pool", "memset", [], [epsT], epsT[:], EPS)
        a_ = [A.sb([128, 512], F32, "cA%d" % i) for i in range(4)]
        b_ = [A.sb([128, 512], F32, "cB%d" % i) for i in range(4)]
        m_ = [A.sb([128, 512], BF16, "cM%d" % i) for i in range(4)]
        junk = A.sb([128, 128], F32, "cjunk")
        ss = [A.sb([128, 4], F32, "css%d" % i) for i in range(4)]
        hn = [A.sb([128, 512], F32, "chn%d" % i) for i in range(4)]
        hb = [A.sb([128, 512], BF16, "chb%d" % i) for i in range(4)]
        sz = [A.sb([128, 512], BF16, "csz%d" % i) for i in range(2)]
        oo = [A.sb([128, 512], BF16, "coo%d" % i) for i in range(2)]
        tp = [A.ps([128, 512], BF16, "ctp%d" % i) for i in range(8)]
        tiles = []
        for gi, (t0, n) in enumerate(groups):
            for ti in range(n // 128):
                tiles.append((gi, t0, n, ti))

        def stage1(it):
            gi, t0, n, ti = tiles[it]
            tok = t0 + ti * 128
            a, b, m = a_[it % 4], b_[it % 4], m_[it % 4]
            P.dma("sp", a[:], src0[tok:tok + 128, :], writes=[a])
            P.dma("pool", b[:], src1[tok:tok + 128, :], writes=[b])
            if mul is not None:
                P.dma("sp", m[:], mul[tok:tok + 128, :], writes=[m])
            P.op("dve", "tensor_tensor", [a, b], [a], out=a[:], in0=a[:], in1=b[:], op=ALU.add)
            if mul is not None:
                P.op("pool", "tensor_tensor", [a, m], [a], out=a[:], in0=a[:], in1=m[:], op=ALU.mult)

        def stage2(it):
            gi, t0, n, ti = tiles[it]
            a, s_, hb_ = a_[it % 4], ss[it % 4], hb[it % 4]
            for hh in range(4):
                P.op("act", "activation", [a], [junk, s_], out=junk[:], in_=a[:, hh * 128:(hh + 1) * 128], func=AF.Square,
                     accum_out=s_[:, hh:hh + 1])
            P.op("act", "activation", [s_, epsT], [s_], out=s_[:], in_=s_[:], func=AF.Sqrt, scale=1.0 / 128, bias=epsT[:, 0:1])
            P.op("dve", "reciprocal", [s_], [s_], out=s_[:], in_=s_[:])
            for hh in range(4):
                sl = slice(hh * 128, (hh + 1) * 128)
                P.op("dve", "scalar_tensor_tensor", [a, s_, nrow], [hb_], out=hb_[:, sl], in0=a[:, sl], scalar=s_[:, hh:hh + 1],
                     in1=nrow[:, sl], op0=ALU.mult, op1=ALU.mult)
            for j in range(4):
                tpj = tp[(gi % 2) * 4 + j]
                P.op("pe", "transpose", [hb_, ident_bf], [tpj], tpj[:, ti * 128:(ti + 1) * 128],
                     hb_[:, j * 128:(j + 1) * 128], ident_bf[:])
            if ti == n // 128 - 1:
                for j in range(4):
                    tpj = tp[(gi % 2) * 4 + j]
                    r0 = cat_row0 + j * 128
                    z_, o_ = sz[j % 2], oo[j % 2]
                    P.dma("sp", z_[:, 0:n], SC["SZT"][r0:r0 + 128, t0:t0 + n], writes=[z_])
                    P.op("dve", "tensor_tensor", [tpj, z_], [o_], out=o_[:, 0:n], in0=tpj[:, 0:n], in1=z_[:, 0:n], op=ALU.mult)
                    P.dma("pool", SC["CATT"][r0:r0 + 128, t0:t0 + n], o_[:, 0:n], reads=[o_])

        stage1(0)
        for it in range(len(tiles)):
            if it + 1 < len(tiles):
                stage1(it + 1)
            stage2(it)
        P.barrier()


def phase_C(nc, P, IN, SC, layer, gateR, out_ap):
    with ExitStack() as es:
        A = Alloc(nc, es)
        w = A.sb([128, 8, 1024], BF16, "w_out")
        with ExitStack() as es2:
            A2 = Alloc(nc, es2)
            stg = [A2.sb([128, 8, 512], F32, "wstg%d" % i) for i in range(2)]
            for i in range(2):
                P.dma("sp" if i == 0 else "pool", stg[i][:],
                      IN["l%d_w_out" % layer][:, i * 512:(i + 1) * 512].rearrange("(k p) n -> p k n", p=128), writes=[stg[i]])
                P.op("dve" if i == 0 else "act", "tensor_copy" if i == 0 else "copy", [stg[i]], [w], out=w[:, :, i * 512:(i + 1) * 512],
                     in_=stg[i][:])
            P.barrier()
        cat = [A.sb([128, 8, 512], BF16, "catT%d" % i) for i in range(3)]
        hold = [A.sb([128, 1024], F32, "hold%d" % i) for i in range(4)]
        tmp = [A.sb([128, 1024], F32, "ctmp%d" % i) for i in range(4)]
        hnew = [A.sb([128, 1024], F32, "hnew%d" % i) for i in range(4)]
        ps = [A.ps([128, 512], F32, "yps%d" % i) for i in range(4)]
        if layer == 1:
            frow = A.sb([128, 1024], F32, "frow")
            P.dma("sp", frow[:], IN["final_norm"][0:1, :].to_broadcast([128, 1024]), writes=[frow])
            epsT = A.sb([128, 1], F32, "epsTf")
            P.op("pool", "memset", [], [epsT], epsT[:], EPS)
            junk = A.sb([128, 1024], F32, "fjunk")
            st = [A.sb([128, 1], F32, "fst%d" % i) for i in range(4)]
            ob = [A.sb([128, 1024], F32, "fob%d" % i) for i in range(4)]
        it = 0
        deferred = []
        groups = GROUPS if layer == 0 else GROUPS[1:]
        def load_cat(gi):
            t0, n = groups[gi]
            c_ = cat[gi % 3]
            for k2 in range(2):
                P.dma("sp", c_[:, k2 * 4:(k2 + 1) * 4, 0:n],
                      SC["CATT"][k2 * 512:(k2 + 1) * 512, t0:t0 + n].rearrange("(k p) t -> p k t", p=128), writes=[c_])

        load_cat(0)
        for gi, (t0, n) in enumerate(groups):
            c_ = cat[gi % 3]
            if gi + 1 < len(groups):
                load_cat(gi + 1)
            g_ = gateR[1] if (layer == 0 and t0 == 0) else gateR[0]
            for ti in range(n // 128):
                tok = t0 + ti * 128
                ho, tm, hn_ = hold[it % 4], tmp[it % 4], hnew[it % 4]
                if layer == 0:
                    srcp = IN["ctx"][tok:tok + 128, :] if t0 == 0 else IN["x"][tok - TC:tok - TC + 128, :]
                else:
                    srcp = SC["H1"][tok:tok + 128, :]
                P.dma("sp", ho[:], srcp, writes=[ho])
                for half in range(2):
                    p_ = ps[(it * 2 + half) % 4]
                    for k in range(8):
                        P.op("pe", "matmul", [c_, w], [p_], p_[:, :], c_[:, k, ti * 128:(ti + 1) * 128],
                             w[:, k, half * 512:(half + 1) * 512], start=(k == 0), stop=(k == 7))
                    P.op("dve", "tensor_tensor", [p_, g_], [tm], out=tm[:, half * 512:(half + 1) * 512], in0=p_[:, :],
                         in1=g_[:, half * 512:(half + 1) * 512], op=ALU.mult)
                P.op("pool", "tensor_tensor", [tm, ho], [hn_], out=hn_[:], in0=tm[:], in1=ho[:], op=ALU.add)
                if layer == 0:
                    P.dma("pool", SC["H1"][tok:tok + 128, :], hn_[:], reads=[hn_])
                else:
                    if deferred:
                        deferred.pop()()

                    def fin(hn_=hn_, s_=st[it % 4], o_=ob[it % 4], tok=tok):
                        P.op("act", "activation", [hn_], [junk, s_], out=junk[:], in_=hn_[:], func=AF.Square, accum_out=s_[:, 0:1])
                        P.op("act", "activation", [s_, epsT], [s_], out=s_[:], in_=s_[:], func=AF.Sqrt, scale=1.0 / D,
                             bias=epsT[:, 0:1])
                        P.op("dve", "reciprocal", [s_], [s_], out=s_[:], in_=s_[:])
                        P.op("act", "activation", [hn_, s_], [o_], out=o_[:], in_=hn_[:], func=AF.Copy, scale=s_[:, 0:1])
                        P.op("dve", "tensor_tensor", [o_, frow], [o_], out=o_[:], in0=o_[:], in1=frow[:], op=ALU.mult)
                        P.dma("pool", out_ap[tok - TC:tok - TC + 128, :], o_[:], reads=[o_])

                    deferred.append(fin)
                it += 1
        while deferred:
            deferred.pop()()
        P.barrier()


def gla_phase(nc, P, IN, SC, ident_bf):
    NB = NT
    NCH = T // 64
    with ExitStack() as es:
        A = Alloc(nc, es)
        mask = [A.sb([128, 64], F32, "gmask%d" % d) for d in range(2)]
        for d in range(2):
            P.op("pool", "memset", [], [mask[d]], mask[d][:], 1.0)
            for half in range(2):
                pr = slice(half * 64, half * 64 + 64)
                P.op("pool", "affine_select", [mask[d]], [mask[d]], out=mask[d][pr, :], in_=mask[d][pr, :],
                     pattern=[[1 if d == 0 else -1, 64]], compare_op=ALU.is_ge, fill=0.0, base=0,
                     channel_multiplier=-1 if d == 0 else 1)
        rm = [A.sb([128, T], BF16, "rm%d" % d) for d in range(2)]
        for d in range(2):
            P.op("pool", "memset", [], [rm[d]], rm[d][:], 1.0)
            j0 = 0 if d == 0 else 63
            P.op("pool", "memset", [rm[d]], [rm[d]], rm[d][:, :].rearrange("p (c j) -> p c j", j=64)[:, :, j0:j0 + 1], 0.0)
        qf = A.sb([128, T], F32, "gqf")
        kf = A.sb([128, T], F32, "gkf")
        lg = A.sb([128, T], F32, "glg")
        bc = A.sb([128, T], F32, "gbc")
        tmp = A.sb([128, T], F32, "gtmp")
        qg = [A.sb([128, T], BF16, "qg%d" % d) for d in range(2)]
        kg = [A.sb([128, T], BF16, "kg%d" % d) for d in range(2)]
        kbf = A.sb([128, T], BF16, "kbf")
        eb = [A.sb([128, NCH], F32, "eb%d" % d) for d in range(2)]
        kbTok = [A.sb([128, NB, 128], BF16, "kbTok%d" % d) for d in range(2)]
        SmT = [A.sb([128, NB, 64], BF16, "gSmT%d" % d) for d in range(2)]
        vtok = A.sb([128, NB, 128], BF16, "gvtok")
        obuf = [[A.sb([128, 128], F32, "gob%d_%d" % (d, i)) for i in range(3)] for d in range(2)]
        Sst = [[A.sb([128, 128], F32, "S%d_%d" % (d, i)) for i in range(2)] for d in range(2)]
        Sb = [[A.sb([128, 128], BF16, "Sb%d_%d" % (d, i)) for i in range(2)] for d in range(2)]
        ptr = [A.ps([128, 512], BF16, "gptr%d" % i) for i in range(2)]
        pS = [A.ps([128, 8, 64], F32, "gpS%d" % i) for i in range(2)]
        pU = [A.ps([128, 128], F32, "gpU%d" % i) for i in range(2)]
        pN = [A.ps([128, 128], F32, "gpN%d" % i) for i in range(2)]
        order = [list(range(NCH)), [3, 2, 1, 0] + list(range(NCH - 1, 3, -1))]
        for hp_ in range(2):
            P.dma("sp", qf[:], SC["MQK"][hp_ * 128:(hp_ + 1) * 128, :], writes=[qf])
            P.dma("pool", kf[:], SC["MQK"][256 + hp_ * 128:256 + (hp_ + 1) * 128, :], writes=[kf])
            for d in range(2):
                P.dma("pool", lg[:], SC["LG"][d, hp_ * 128:(hp_ + 1) * 128, :], writes=[lg])
                if d == 0:
                    P.op("dve", "tensor_tensor_scan", [rm[d], lg], [bc], out=bc[:, :], data0=rm[d][:, :], data1=lg[:, :],
                         initial=0.0, op0=ALU.mult, op1=ALU.add)
                else:
                    P.op("dve", "tensor_tensor_scan", [rm[d], lg], [bc], out=bc[:, ::-1], data0=rm[d][:, ::-1],
                         data1=lg[:, ::-1], initial=0.0, op0=ALU.mult, op1=ALU.add)
                jl = 63 if d == 0 else 0
                bl = bc[:, :].rearrange("p (c j) -> p c j", j=64)[:, :, jl:jl + 1]
                P.op("act", "activation", [bc], [eb[d]], out=eb[d][:, :].unsqueeze(2), in_=bl, func=AF.Exp)
                P.op("act", "activation", [bc], [tmp], out=tmp[:], in_=bc[:], func=AF.Exp)
                P.op("dve", "scalar_tensor_tensor", [qf, tmp], [qg[d]], out=qg[d][:], in0=qf[:], scalar=64.0 ** -0.5, in1=tmp[:],
                     op0=ALU.mult, op1=ALU.mult)
                P.op("act", "activation", [bc], [tmp], out=tmp[:], in_=bc[:], func=AF.Exp, scale=-1.0)
                P.op("dve", "tensor_tensor", [kf, tmp], [kg[d]], out=kg[d][:], in0=kf[:], in1=tmp[:], op=ALU.mult)
                P.op("dve", "tensor_tensor", [bc], [tmp], out=tmp[:, :].rearrange("p (c j) -> p c j", j=64),
                     in0=bl.to_broadcast([128, NCH, 64]), in1=bc[:, :].rearrange("p (c j) -> p c j", j=64), op=ALU.subtract)
                P.op("act", "activation", [tmp], [tmp], out=tmp[:], in_=tmp[:], func=AF.Exp)
                P.op("dve", "tensor_tensor", [kf, tmp], [kbf], out=kbf[:], in0=kf[:], in1=tmp[:], op=ALU.mult)
                for b0 in range(0, NB, 4):
                    nb = min(4, NB - b0)
                    ps = ptr[(b0 // 4) % 2]
                    for bb in range(nb):
                        P.op("pe", "transpose", [kbf, ident_bf], [ps], ps[:, bb * 128:(bb + 1) * 128],
                             kbf[:, (b0 + bb) * 128:(b0 + bb + 1) * 128], ident_bf[:])
                    P.op("act", "copy", [ps], [kbTok[d]], out=kbTok[d][:, b0:b0 + nb, :],
                         in_=ps[:, 0:nb * 128].rearrange("p (b j) -> p b j", j=128))
            for hh in range(2):
                h = hp_ * 2 + hh
                ps_ = slice(hh * 64, hh * 64 + 64)
                P.dma("sp", vtok[:], SC["MV"][:, h * 128:(h + 1) * 128].rearrange("(b p) j -> p b j", p=128), writes=[vtok])
                for d in range(2):
                    for b0 in range(0, NB, 8):
                        nb = min(8, NB - b0)
                        ps = pS[(b0 // 8) % 2]
                        for bb in range(nb):
                            for half in range(2):
                                c = (b0 + bb) * 2 + half
                                pr = slice(half * 64, half * 64 + 64)
                                P.op("pe", "matmul", [kg[d], qg[d]], [ps], ps[pr, bb, :], kg[d][ps_, c * 64:(c + 1) * 64],
                                     qg[d][ps_, c * 64:(c + 1) * 64], start=True, stop=True)
                        P.op("dve", "tensor_tensor", [ps, mask[d]], [SmT[d]], out=SmT[d][:, b0:b0 + nb, :], in0=ps[:, 0:nb, :],
                             in1=mask[d][:].unsqueeze(1).to_broadcast([128, nb, 64]), op=ALU.mult)
                for d in range(2):
                    P.op("pool", "memset", [], [Sst[d][0]], Sst[d][0][ps_, :], 0.0)
                    P.op("pool", "memset", [], [Sb[d][0]], Sb[d][0][ps_, :], 0.0)
                for i in range(NCH):
                    for d in range(2):
                        c = order[d][i]
                        b, half = c // 2, c % 2
                        pr = slice(half * 64, half * 64 + 64)
                        Sold, Snew = Sst[d][i % 2], Sst[d][(i + 1) % 2]
                        sbo, sbn = Sb[d][i % 2], Sb[d][(i + 1) % 2]
                        U, N = pU[d], pN[d]
                        ob = obuf[d][(i // 2) % 3]
                        P.op("pe", "matmul", [SmT[d], vtok], [N], N[pr, :], SmT[d][pr, b, :], vtok[pr, b, :], start=True,
                             stop=False)
                        P.op("pe", "matmul", [qg[d], sbo], [N], N[pr, :], qg[d][ps_, c * 64:(c + 1) * 64], sbo[ps_, :], start=False,
                             stop=True)
                        P.op("pe", "matmul", [kbTok[d], vtok], [U], U[ps_, :], kbTok[d][pr, b, hh * 64:(hh + 1) * 64], vtok[pr, b, :],
                             start=True, stop=True)
                        P.op("dve", "scalar_tensor_tensor", [Sold, eb[d], U], [sbn], out=sbn[ps_, :], in0=Sold[ps_, :],
                             scalar=eb[d][ps_, c:c + 1], in1=U[ps_, :], op0=ALU.mult, op1=ALU.add)
                        P.op("dve", "scalar_tensor_tensor", [Sold, eb[d], U], [Snew], out=Snew[ps_, :], in0=Sold[ps_, :],
                             scalar=eb[d][ps_, c:c + 1], in1=U[ps_, :], op0=ALU.mult, op1=ALU.add)
                        P.op("act", "copy", [N], [ob], out=ob[pr, :], in_=N[pr, :])
                        if i % 2 == 1:
                            P.dma("sp" if d == 0 else "pool", SC["HM"][d, b * 128:(b + 1) * 128, h * 128:(h + 1) * 128], ob[:],
                                  reads=[ob])
        P.barrier()


def _na_rows_ok(qr, kr):
    lo = min(max(qr - 4, 0), 56)
    return lo <= kr < lo + 8


def _na_cfg(g):
    if g == 0:
        return "first", 0, 6
    if g == 7:
        return "last", 26, 6
    return "mid", 4 * g - 2, 8


def _na_range(g, ktl):
    js = [j for j in range(8) for i in range(2) if _na_rows_ok(8 * g + j, 2 * ktl + i)]
    return min(js), max(js)


def na_bias_fn(nc, P, IN, state):
    def fn(h, A):
        if A is not None:
            state.setdefault("sets", {})
            for key, ntile in (("first", 6), ("mid", 8), ("last", 6)):
                state["sets"][(key, h % 2)] = [A.sb([128, 512], F32, "nab_%s%d_%d" % (key, i, h % 2)) for i in range(ntile)]
        out = {}

        def loader():
            for key, g, t_lo in (("first", 0, 0), ("mid", 1, 2), ("last", 7, 26)):
                tiles = state["sets"][(key, h % 2)]
                for r, bt in enumerate(tiles):
                    ktl = t_lo + r
                    u0, u1 = _na_range(g, ktl)
                    P.op("pool", "memset", [], [bt], bt[:, u0 * 64:(u1 + 1) * 64], MASKV)
                    for i in range(2):
                        kr = 2 * ktl + i
                        js = [j for j in range(8) if _na_rows_ok(8 * g + j, kr)]
                        if not js:
                            continue
                        j0, j1 = js[0], js[-1]
                        assert js == list(range(j0, j1 + 1))
                        m0 = 7 - (kr - 8 * g - j0)
                        nj = j1 - j0 + 1
                        P.dma("sp" if i == 0 else "pool",
                              bt[i * 64:(i + 1) * 64, j0 * 64:(j1 + 1) * 64].rearrange("p (m q) -> p m q", q=64),
                              IN["na_bias"][h, m0:m0 + nj, :, :].rearrange("m k q -> k m q"), writes=[bt])
                    yield

        for g in range(8):
            key, t_lo, nt_ = _na_cfg(g)
            for r in range(nt_):
                out[(g, 2 + t_lo + r)] = state["sets"][(key, h % 2)][r]
        return out, loader()

    return fn


def _shapes(d):
    return {k: (v.shape, "bf16" if v.dtype == ml_dtypes.bfloat16 else "f32") for k, v in d.items()}


def run(inputs, stage=99, debug=(), cores=8, skip=()):
    inputs = {k: np.asarray(v) for k, v in inputs.items()}
    sh, per = prep_inputs(inputs)
    nc = build(_shapes(sh), _shapes(per[0]), stage=stage, debug=debug, skip=skip)
    in_maps = [dict(sh, **per[b]) for b in range(cores)]
    res = run_bass_kernel_spmd(nc, in_maps, core_ids=list(range(cores)))
    return res


def kernel(**inputs):
    res = run(inputs)
    return np.stack([np.asarray(r["out"], dtype=np.float32) for r in res.results], axis=0)
```

```python
import numpy as np
from contextlib import ExitStack
import ml_dtypes
import concourse.bass as bass
import concourse.mybir as mybir
from concourse.bass_utils import run_bass_kernel_spmd

F32 = mybir.dt.float32
BF16 = mybir.dt.bfloat16
AF = mybir.ActivationFunctionType
ALU = mybir.AluOpType
AX = mybir.AxisListType

D = 1024
TC = 256
TL = 4096
T = TC + TL
NT = T // 128
EPS = 1e-6
MASKV = -30000.0

GROUPS = [(0, 256)] + [(256 + 512 * i, 512) for i in range(8)]


class Dep:
    __slots__ = ("w", "r")

    def __init__(self):
        self.w = None
        self.r = {}


class Tile:
    def __init__(self, t):
        self.t = t
        self.d = Dep()

    def __getitem__(self, k):
        return self.t[k]


class DramDep:
    def __init__(self):
        self.d = Dep()


class Prog:
    def __init__(self, nc, es):
        self.nc = nc
        self.eng = {"pe": nc.tensor, "act": nc.scalar, "dve": nc.vector, "pool": nc.gpsimd, "sp": nc.sync}
        self.R = 12
        self.keys = [("pe", "c"), ("act", "c"), ("dve", "c"), ("pool", "c")]
        for q in ("sp", "pool"):
            self.keys += [(q, "d%d" % i) for i in range(self.R)]
        self.ndma = {"sp": 0, "pool": 0}
        self.sem = {k: es.enter_context(nc.semaphore("s_%s_%s" % k)) for k in self.keys}
        self.cnt = {k: 0 for k in self.keys}
        self.waited = {e: {} for e in self.eng}
        self.n = 0

    def _emit(self, eng, kind, fn, reads, writes):
        if kind == "d":
            kind = "d%d" % (self.ndma[eng] % self.R)
            self.ndma[eng] += 1
        key = (eng, kind)
        deps = {}
        if kind != "c" and self.cnt[key] > 0:
            deps[key] = self.cnt[key]

        def add(tok):
            if tok is None:
                return
            k, v = tok
            if deps.get(k, 0) < v:
                deps[k] = v

        for b in reads:
            add(b.d.w)
        for b in writes:
            add(b.d.w)
            for k, v in b.d.r.items():
                add((k, v))
        e = self.eng[eng]
        wd = self.waited[eng]
        for k, v in deps.items():
            if k == ("pe", "c") and eng == "pe":
                continue
            if wd.get(k, 0) >= v:
                continue
            e.wait_ge(self.sem[k], v)
            wd[k] = v
        inc = 16 if kind != "c" else 1
        self.cnt[key] += inc
        fn(e).then_inc(self.sem[key], inc)
        v = self.cnt[key]
        for b in reads:
            if b.d.r.get(key, 0) < v:
                b.d.r[key] = v
        for b in writes:
            b.d.w = (key, v)
            b.d.r = {}
        self.n += 1

    def op(self, eng, name, reads, writes, *a, **kw):
        self._emit(eng, "c", lambda e: getattr(e, name)(*a, **kw), reads, writes)

    def dma(self, q, out, in_, reads=(), writes=(), **kw):
        self._emit(q, "d", lambda e: e.dma_start(out=out, in_=in_, **kw), reads, writes)

    def barrier(self):
        for en, e in self.eng.items():
            wd = self.waited[en]
            for k in self.keys:
                v = self.cnt[k]
                if v > 0 and wd.get(k, 0) < v:
                    e.wait_ge(self.sem[k], v)
                    wd[k] = v


class Alloc:
    def __init__(self, nc, es):
        self.nc = nc
        self.es = es
        _CTR.setdefault(id(nc), 0)

    def _nm(self, name):
        _CTR[id(self.nc)] = _CTR.get(id(self.nc), 0) + 1
        return "%s_%d" % (name, _CTR[id(self.nc)])

    def sb(self, shape, dt, name=None):
        return Tile(self.es.enter_context(self.nc.sbuf_tensor(self._nm(name or "sb"), list(shape), dt)))

    def ps(self, shape, dt, name=None):
        return Tile(self.es.enter_context(self.nc.psum_tensor(self._nm(name or "ps"), list(shape), dt)))


_CTR = {}


def _fm(v, nchunk):
    return np.ascontiguousarray(v.reshape(nchunk, 128).T)


def _rope_perm():
    perm = np.zeros(32, np.int64)
    for i in range(32):
        r = i % 16
        perm[i] = i + 8 if r < 8 else i - 8
    return perm


def _rope_tables():
    t = np.arange(TL)
    inv = (1.0 / (10000.0 ** (np.arange(8, dtype=np.float32) / 8))).astype(np.float32)
    pos = [(t // 64).astype(np.float32), (t % 64).astype(np.float32)]
    C = np.zeros((32, TL), np.float32)
    S = np.zeros((32, TL), np.float32)
    for i in range(32):
        a = i // 16
        r = i % 16
        p = r % 8
        ang = (pos[a] * inv[p]).astype(np.float32)
        C[i] = np.cos(ang)
        S[i] = -np.sin(ang) if r < 8 else np.sin(ang)
    Cf = np.zeros((128, TL), np.float32)
    Sf = np.zeros((128, TL), np.float32)
    Cf[0:32] = C
    Cf[64:96] = C
    Sf[0:32] = S
    Sf[64:96] = S
    return Cf, Sf


def prep_inputs(inp):
    sh = {}
    sh["ident_bf"] = np.eye(128, dtype=np.float32).astype(ml_dtypes.bfloat16)
    sh["ident_f"] = np.eye(128, dtype=np.float32)
    perm = _rope_perm()
    w_in = inp["l0_w_in"]
    gi_cols = [2720 + d * 8 + h for d in range(2) for h in range(4)]
    gf_cols = [2720 + d * 8 + 4 + h for d in range(2) for h in range(4)]
    sh["l0_w_in"] = np.ascontiguousarray(
        np.concatenate([w_in, w_in[:, 640:672][:, perm], w_in[:, gi_cols], w_in[:, gf_cols]], axis=1))
    w_uq = inp["l0_mla_w_uq"].reshape(384, 8, 96)
    ext = np.concatenate([w_uq, w_uq[:, :, 0:64], w_uq[:, :, 64:96][:, :, perm]], axis=2)
    sh["l0_w_uq"] = np.ascontiguousarray(ext.reshape(384, 8 * 192))
    w_ukv = inp["l0_mla_w_ukv"].reshape(256, 8, 128)
    sh["l0_w_ukv"] = np.ascontiguousarray(
        np.concatenate([w_ukv[:, :, 0:64].reshape(256, 512), w_ukv[:, :, 64:128].reshape(256, 512)], axis=1))
    sh["l0_qnT"] = _fm(inp["l0_mla_q_norm"], 3)
    sh["l0_kvnT"] = _fm(inp["l0_mla_kv_norm"], 2)
    Cf, Sf = _rope_tables()
    sh["ropeC"] = Cf
    sh["ropeS"] = Sf
    cw = inp["l0_mlstm_conv_w"]
    sh["l0_convT"] = np.ascontiguousarray(
        np.concatenate([cw.reshape(3, 8, 128).transpose(2, 1, 0), inp["l0_mlstm_conv_b"].reshape(8, 128).T[:, :, None]],
                       axis=2))
    gb = np.zeros((16, 1), np.float32)
    for d in range(2):
        for h in range(4):
            gb[d * 8 + h, 0] = inp["l0_mlstm_b_i"][d, h]
            gb[d * 8 + 4 + h, 0] = inp["l0_mlstm_b_f"][d, h]
    sh["l0_gbias"] = gb
    gb2 = np.zeros((64, 2), np.float32)
    for d in range(2):
        for h in range(4):
            gb2[d * 32 + h, 0] = inp["l0_mlstm_b_i"][d, h]
            gb2[d * 32 + h, 1] = inp["l0_mlstm_b_f"][d, h]
    sh["l0_gb2"] = gb2
    sh["l0_hnorm"] = np.ascontiguousarray(inp["l0_mlstm_norm"].reshape(1, 512))
    sh["l0_w_out"] = inp["l0_w_out"]
    sh["l1_w_in"] = inp["l1_w_in"]
    sh["l1_w_gate"] = np.ascontiguousarray(inp["l1_gla_w_gate"])
    sh["l1_bgT"] = np.ascontiguousarray(inp["l1_gla_b_gate"].reshape(2, 2, 128).transpose(2, 0, 1))
    sh["l1_gnorm"] = np.ascontiguousarray(inp["l1_gla_norm"].reshape(1, 512))
    sh["l1_w_out"] = inp["l1_w_out"]
    sh["final_norm"] = np.ascontiguousarray(inp["final_norm"].reshape(1, 1024))
    rpb = inp["l1_na_rpb"]
    kc = np.arange(64)[:, None]
    qc = np.arange(64)[None, :]
    wc0 = np.clip(qc - 8, 0, 48)
    okc = (kc >= wc0) & (kc < wc0 + 16)
    dcol = np.clip(kc - qc + 15, 0, 30)
    Tb = np.full((8, 15, 64, 64), MASKV, np.float32)
    for m in range(15):
        dr = 7 - m
        blk = rpb[:, dr + 7][:, dcol]
        Tb[:, m] = np.where(okc[None], blk, np.float32(MASKV))
    sh["na_bias"] = Tb
    mods = [(inp["l0_norm"], inp["l0_w_mod"], inp["l0_b_mod"]), (inp["l1_norm"], inp["l1_w_mod"], inp["l1_b_mod"])]
    for l, (g_, wm_, bm_) in enumerate(mods):
        sh["l%d_w_mod" % l] = wm_
        sh["l%d_bmodT" % l] = _fm(bm_, 24)
        sh["l%d_bmod_gate" % l] = np.ascontiguousarray(bm_[2048:3072].reshape(1, 1024))
        sh["l%d_gT" % l] = _fm(g_, 8)
    per = []
    for b in range(8):
        d = {}
        d["x"] = inp["x"][b]
        d["ctx"] = inp["ctx"][b]
        cv = np.stack([inp["c"][b], inp["c_ctx"]], axis=1)
        d["cvec"] = np.ascontiguousarray(cv.reshape(8, 128, 2).transpose(1, 0, 2))
        per.append(d)
    return sh, per


def build(sh_shapes, per_shapes, stage=99, debug=(), skip=()):
    nc = bass.Bass("TRN2", target_bir_lowering=False)
    IN = {}
    for k, (shape, dt) in list(sh_shapes.items()) + list(per_shapes.items()):
        IN[k] = nc.dram_tensor(k, list(shape), BF16 if dt == "bf16" else F32, kind="ExternalInput").ap()
    out = nc.dram_tensor("out", [TL, D], F32, kind="ExternalOutput").ap()

    def scratch(name, shape, dt):
        kind = "ExternalOutput" if name in debug else "Internal"
        return nc.dram_tensor(name, list(shape), dt, kind=kind).ap()

    SC = {}
    SC["H1"] = scratch("H1", [T, D], F32)
    SC["SZT"] = scratch("SZT", [1024, T], BF16)
    SC["CATT"] = scratch("CATT", [1024, T], BF16)
    SC["QT"] = scratch("QT", [8, 96, T], BF16)
    SC["KT"] = scratch("KT", [8, 96, T], BF16)
    SC["V"] = scratch("V", [T, 512], BF16)
    SC["MQK"] = scratch("MQK", [1024, T], F32)
    SC["MQKB"] = scratch("MQKB", [1024, T], BF16)
    SC["GI"] = scratch("GI", [8, T], F32)
    SC["GF"] = scratch("GF", [8, T], F32)
    SC["MV"] = scratch("MV", [T, 512], BF16)
    SC["MO"] = scratch("MO", [T, 512], BF16)
    SC["HM"] = scratch("HM", [2, T, 512], F32)
    SC["RD"] = scratch("RD", [16, 512], F32)
    SC["LG"] = scratch("LG", [2, 256, T], F32)
    SC["NQ"] = scratch("NQ", [512, T], BF16)
    SC["NK"] = scratch("NK", [512, T], BF16)

    with ExitStack() as es0:
        P = Prog(nc, es0)
        A0 = Alloc(nc, es0)
        ident_bf = A0.sb([128, 128], BF16, "identbf")
        ident_f = A0.sb([128, 128], F32, "identf")
        ones_f = A0.sb([128, 128], F32, "onesf")
        P.dma("sp", ident_bf[:], IN["ident_bf"][:, :], writes=[ident_bf])
        P.dma("sp", ident_f[:], IN["ident_f"][:, :], writes=[ident_f])
        P.op("pool", "memset", [], [ones_f], ones_f[:], 1.0)
        affA = [A0.sb([128, 8, 2], F32, "affA%d" % l) for l in range(2)]
        affB = [A0.sb([128, 8, 2], F32, "affB%d" % l) for l in range(2)]
        gateR = [[A0.sb([128, 1024], F32, "gateR%d_%d" % (l, s)) for s in range(2 if l == 0 else 1)] for l in range(2)]

        esA0 = es0.enter_context(ExitStack())
        Aw0 = Alloc(nc, esA0)
        w0 = Aw0.sb([128, 8, 3808], BF16, "w_in0")
        w_uq0 = Aw0.sb([128, 3, 1536], BF16, "w_uq0")
        w_ukv0 = Aw0.sb([128, 2, 1024], BF16, "w_ukv0")
        stgA = [Aw0.sb([128, 1024], F32, "stgA%d" % i) for i in range(2)]

        def w0_loader():
            i = 0
            for c0 in range(0, 3808, 128):
                cw = min(128, 3808 - c0)
                s = stgA[i % 2]
                sv = s[:, :].rearrange("p (k n) -> p k n", k=8)
                P.dma("sp" if i % 2 == 0 else "pool", sv[:, :, 0:cw],
                      IN["l0_w_in"][:, c0:c0 + cw].rearrange("(k p) n -> p k n", p=128), writes=[s])
                P.op("dve" if i % 2 == 0 else "act", "tensor_copy" if i % 2 == 0 else "copy", [s], [w0],
                     out=w0[:, :, c0:c0 + cw], in_=sv[:, :, 0:cw])
                i += 1
                yield
            for kk in range(3):
                for hf in range(2):
                    s = stgA[i % 2]
                    P.dma("sp" if i % 2 == 0 else "pool", s[:, 0:768], IN["l0_w_uq"][kk * 128:(kk + 1) * 128, hf * 768:(hf + 1) * 768],
                          writes=[s])
                    P.op("dve" if i % 2 == 0 else "act", "tensor_copy" if i % 2 == 0 else "copy", [s], [w_uq0],
                         out=w_uq0[:, kk, hf * 768:(hf + 1) * 768], in_=s[:, 0:768])
                    i += 1
                    yield
            for kk in range(2):
                s = stgA[i % 2]
                P.dma("sp" if i % 2 == 0 else "pool", s[:, :], IN["l0_w_ukv"][kk * 128:(kk + 1) * 128, :], writes=[s])
                P.op("dve" if i % 2 == 0 else "act", "tensor_copy" if i % 2 == 0 else "copy", [s], [w_ukv0],
                     out=w_ukv0[:, kk, :], in_=s[:, :])
                i += 1
                yield

        wgen = w0_loader()
        with ExitStack() as es:
            A = Alloc(nc, es)
            cv = A.sb([128, 8, 2], F32, "cv")
            sc = A.sb([128, 8, 2], F32, "sc")
            screp = [A.sb([128, 8, 128], F32, "screp%d" % s) for s in range(2)]
            P.dma("sp", cv[:], IN["cvec"][:, :, :], writes=[cv])
            P.op("act", "activation", [cv], [sc], out=sc[:], in_=cv[:], func=AF.Silu)
            for s in range(2):
                for k in range(8):
                    P.op("dve", "tensor_copy", [sc], [screp[s]], out=screp[s][:, k, :],
                         in_=sc[:, k, s:s + 1].to_broadcast([128, 128]))
            wpan = [A.sb([128, 8, 384], F32, "wpan%d" % i) for i in range(2)]
            wgate = [A.sb([128, 512], F32, "wgate%d" % i) for i in range(3)]
            pm = A.ps([128, 24, 2], F32, "pm")
            pg = [A.ps([128, 512], F32, "pg%d" % i) for i in range(2)]
            bmT = A.sb([128, 24], F32, "bmT")
            gT = A.sb([128, 8], F32, "gT")
            modT = A.sb([128, 24, 2], F32, "modT")
            bgrow = A.sb([128, 1024], F32, "bgrow")
            for l in range(2):
                wm = IN["l%d_w_mod" % l]
                P.dma("sp", bmT[:], IN["l%d_bmodT" % l][:, :], writes=[bmT])
                P.dma("sp", gT[:], IN["l%d_gT" % l][:, :], writes=[gT])
                P.dma("sp", bgrow[:], IN["l%d_bmod_gate" % l][0:1, :].to_broadcast([128, 1024]), writes=[bgrow])
                for pn in range(8):
                    wp = wpan[pn % 2]
                    P.dma("sp" if pn % 2 == 0 else "pool", wp[:],
                          wm[:, pn * 384:(pn + 1) * 384].rearrange("(k p) n -> p k n", p=128), writes=[wp])
                    for j in range(3):
                        n = pn * 3 + j
                        for k in range(8):
                            P.op("pe", "matmul", [wp, sc], [pm], pm[:, n, :], wp[:, k, j * 128:(j + 1) * 128],
                                 sc[:, k, :], start=(k == 0), stop=(k == 7))
                    for _ in range(3):
                        next(wgen, None)
                P.op("dve", "tensor_tensor", [pm, bmT], [modT], out=modT[:], in0=pm[:],
                     in1=bmT[:].unsqueeze(2).to_broadcast([128, 24, 2]), op=ALU.add)
                P.op("dve", "tensor_scalar", [modT], [affA[l]], out=affA[l][:], in0=modT[:, 8:16, :], scalar1=1.0,
                     scalar2=None, op0=ALU.add)
                P.op("dve", "tensor_tensor", [affA[l], gT], [affA[l]], out=affA[l][:], in0=affA[l][:],
                     in1=gT[:].unsqueeze(2).to_broadcast([128, 8, 2]), op=ALU.mult)
                P.op("dve", "tensor_copy", [modT], [affB[l]], out=affB[l][:], in_=modT[:, 0:8, :])
                for s in range(len(gateR[l])):
                    for hf in range(2):
                        ps = pg[hf]
                        for k in range(8):
                            wg = wgate[(hf * 8 + k) % 3]
                            P.dma("sp" if k % 2 == 0 else "pool", wg[:],
                                  wm[k * 128:(k + 1) * 128, 2048 + hf * 512:2048 + (hf + 1) * 512], writes=[wg])
                            P.op("pe", "matmul", [wg, screp[s]], [ps], ps[:], screp[s][:, k, :], wg[:],
                                 start=(k == 0), stop=(k == 7))
                        P.op("dve", "tensor_tensor", [ps, bgrow], [gateR[l][s]],
                             out=gateR[l][s][:, hf * 512:(hf + 1) * 512], in0=ps[:],
                             in1=bgrow[:, hf * 512:(hf + 1) * 512], op=ALU.add)
            for _ in wgen:
                pass
            P.barrier()
        if stage <= 0:
            dbg = nc.dram_tensor("dbg_mod", [128, 2, 2, 8, 2], F32, kind="ExternalOutput").ap()
            dbg2 = nc.dram_tensor("dbg_gate", [128, 1024], F32, kind="ExternalOutput").ap()
            for l in range(2):
                P.dma("sp", dbg[:, l, 0], affA[l][:], reads=[affA[l]])
                P.dma("sp", dbg[:, l, 1], affB[l][:], reads=[affB[l]])
            P.dma("sp", dbg2[:, :], gateR[0][1][:], reads=[gateR[0][1]])
            P.barrier()
            return nc

        phase_A(nc, P, IN, SC, 0, affA[0], affB[0], ident_bf, ones_f, w_pre=(w0, w_uq0, w_ukv0))
        esA0.close()
        if stage <= 1:
            return nc
        if 2 not in skip:
            mla_groups = [(0, 256, [0, 1], 0)] + [(256 + 512 * g, 512, list(range(NT)), 0) for g in range(8)]
            attention(nc, P, SC, ones_f, 8, 96, 96.0 ** -0.5, lambda h: SC["QT"][h, :, :], lambda h: SC["KT"][h, :, :],
                      SC["V"], 0, mla_groups)
        if stage <= 2:
            return nc
        if 3 not in skip:
            mlstm_phase(nc, P, IN, SC, ident_bf, ident_f, ones_f)
        if stage <= 3:
            return nc
        combine_phase(nc, P, IN, SC, ident_bf, SC["HM"][0], SC["HM"][1], SC["MO"], "l0_hnorm", 512, GROUPS)
        if stage <= 4:
            return nc
        phase_C(nc, P, IN, SC, 0, gateR[0], out)
        if stage <= 5:
            return nc
        phase_A(nc, P, IN, SC, 1, affA[1], affB[1], ident_bf, ones_f)
        if stage <= 6:
            return nc
        if 7 not in skip:
            gla_phase(nc, P, IN, SC, ident_bf)
            combine_phase(nc, P, IN, SC, ident_bf, SC["HM"][0], SC["HM"][1], None, "l1_gnorm", 0, GROUPS[1:])
        if stage <= 7:
            return nc
        if 8 not in skip:
            na_groups = []
            for g in range(8):
                key, t_lo, nt_ = _na_cfg(g)
                loc = []
                for r in range(nt_):
                    u0, u1 = _na_range(g, t_lo + r)
                    loc.append((2 + t_lo + r, u0 * 64, (u1 + 1) * 64))
                na_groups.append((256 + 512 * g, 512, [0, 1] + loc, g))
            attention(nc, P, SC, ones_f, 8, 64, 64.0 ** -0.5, lambda h: SC["NQ"][h * 64:(h + 1) * 64, :],
                      lambda h: SC["NK"][h * 64:(h + 1) * 64, :], SC["V"], 512, na_groups, bias_fn=na_bias_fn(nc, P, IN, {}),
                      ident_bf=ident_bf, early_release=True, act_recip=True)
        if stage <= 8:
            return nc
        phase_C(nc, P, IN, SC, 1, gateR[1], out)
    return nc


def phase_A(nc, P, IN, SC, layer, affA, affB, ident_bf, ones_f, w_pre=None):
    NW = 3808 if layer == 0 else 3616
    w_in_d = IN["l%d_w_in" % layer]
    with ExitStack() as es:
        A = Alloc(nc, es)
        if w_pre is not None:
            w_in, w_uq, w_ukv = w_pre
        else:
            w_in = A.sb([128, 8, NW], BF16, "w_in")
            if layer == 0:
                w_uq = A.sb([128, 3, 1536], BF16, "w_uq")
                w_ukv = A.sb([128, 2, 1024], BF16, "w_ukv")
        with ExitStack() as es2:
            A2 = Alloc(nc, es2)
            stg = [A2.sb([128, 8, 512], F32, "stg%d" % i) for i in range(2)] if w_pre is None else None
            i = 0
            for c0 in (range(0, NW, 512) if w_pre is None else ()):
                cw = min(512, NW - c0)
                s = stg[i % 2]
                P.dma("sp" if i % 2 == 0 else "pool", s[:, :, 0:cw],
                      w_in_d[:, c0:c0 + cw].rearrange("(k p) n -> p k n", p=128), writes=[s])
                P.op("dve" if i % 2 == 0 else "act", "tensor_copy" if i % 2 == 0 else "copy", [s], [w_in],
                     out=w_in[:, :, c0:c0 + cw], in_=s[:, :, 0:cw])
                i += 1
            if layer == 0 and w_pre is None:
                s = stg[i % 2]
                for kk in range(3):
                    s = stg[i % 2]
                    P.dma("sp", s[:, 0:3, :], IN["l0_w_uq"][kk * 128:(kk + 1) * 128, :].rearrange("p (a n) -> p a n", a=3),
                          writes=[s])
                    P.op("dve", "tensor_copy", [s], [w_uq], out=w_uq[:, kk, :].rearrange("p (a n) -> p a n", a=3),
                         in_=s[:, 0:3, :])
                    i += 1
                s = stg[i % 2]
                for kk in range(2):
                    P.dma("sp", s[:, 2 * kk:2 * kk + 2, :],
                          IN["l0_w_ukv"][kk * 128:(kk + 1) * 128, :].rearrange("p (a n) -> p a n", a=2), writes=[s])
                P.op("dve", "tensor_copy", [s], [w_ukv], out=w_ukv[:].rearrange("p k (a n) -> p (k a) n", a=2),
                     in_=s[:, 0:4, :])
                i += 1
            P.barrier()
        if layer == 0:
            qnT = A.sb([128, 3], F32, "qnT")
            kvnT = A.sb([128, 2], F32, "kvnT")
            P.dma("sp", qnT[:], IN["l0_qnT"][:, :], writes=[qnT])
            P.dma("sp", kvnT[:], IN["l0_kvnT"][:, :], writes=[kvnT])
            cqT = A.sb([128, 3, 512], F32, "cqT")
            ckvT = A.sb([128, 2, 512], F32, "ckvT")
            sq = A.sb([128, 3, 512], BF16, "sq")
            ones_b = A.sb([128, 128], BF16, "ones_b")
            P.op("pool", "memset", [], [ones_b], ones_b[:], 1.0)
            rstd = A.sb([128, 512], F32, "rstd")
            cqn = A.sb([128, 3, 512], BF16, "cqn")
            ckvn = A.sb([128, 2, 512], BF16, "ckvn")
            rC = A.sb([128, 512], F32, "rC")
            rS = A.sb([128, 512], F32, "rS")
            rt1 = A.sb([128, 512], F32, "rt1")
            rt2 = A.sb([128, 512], F32, "rt2")
            qo = [A.sb([128, 512], BF16, "qo%d" % i) for i in range(2)]
            kro = A.sb([32, 512], BF16, "kro")
        else:
            gaT = [A.sb([16, 512], F32, "gaT%d" % d) for d in range(2)]
            wg = A.sb([16, 2, 256], F32, "wg")
            P.dma("sp", wg[:], IN["l1_w_gate"].rearrange("d r k -> r d k"), writes=[wg])
            bgT = A.sb([128, 2, 2], F32, "bgT")
            nbg = A.sb([128, 2, 2], F32, "nbg")
            P.dma("sp", bgT[:], IN["l1_bgT"][:, :, :], writes=[bgT])
            P.op("dve", "tensor_scalar", [bgT], [nbg], out=nbg[:], in0=bgT[:], scalar1=-1.0, scalar2=None, op0=ALU.mult)
            one1 = A.sb([128, 1], F32, "one1a")
            P.op("pool", "memset", [], [one1], one1[:], 1.0)
            lge = A.sb([128, 512], F32, "lge")
            lgo = [A.sb([128, 512], F32, "lgo%d" % i) for i in range(2)]
        hb = [A.sb([128, 1024], F32, "hb%d" % i) for i in range(3)]
        junk = A.sb([128, 1024], F32, "junk")
        st = [A.sb([128, 4], F32, "st%d" % i) for i in range(2)]
        xn2 = [[A.sb([128, 1024], BF16, "xn%d_%d" % (s_, i)) for i in range(4)] for s_ in range(2)]
        epsT = A.sb([128, 1], F32, "epsT")
        P.op("pool", "memset", [], [epsT], epsT[:], EPS)
        uT = [A.sb([128, 8, 512], BF16, "uT%d" % i) for i in range(2)]
        fo_bf = [A.sb([128, 512], BF16, "fobf%d" % i) for i in range(8)]
        fo_f = [A.sb([128, 512], F32, "fof%d" % i) for i in range(6)]
        tp = [A.ps([128, 512], BF16, "tp%d" % i) for i in range(2)]
        acc = [A.ps([128, 512], F32, "acc%d" % i) for i in range(6)]
        cnt = {"acc": 0, "fobf": 0, "fof": 0, "ev": 0, "q": 0, "hb": 0, "xn": 0, "tp": 0}

        def nxt(name, lst):
            r = lst[cnt[name] % len(lst)]
            cnt[name] += 1
            return r

        def evac_engine():
            cnt["ev"] += 1
            return "dve" if cnt["ev"] % 2 == 0 else "act"

        def copy_op(eng, src_t, src_ap, dst_t, dst_ap):
            if eng == "act":
                P.op("act", "copy", [src_t], [dst_t], out=dst_ap, in_=src_ap)
            else:
                P.op(eng, "tensor_copy", [src_t], [dst_t], out=dst_ap, in_=src_ap)

        def stq():
            cnt["q"] += 1
            return "pool" if cnt["q"] % 2 == 0 else "sp"

        def norm_part(gi):
            t0, n = GROUPS[gi]
            ntl = n // 128
            sta = st[gi % 2]
            xn = xn2[gi % 2]
            for ti in range(ntl):
                h = nxt("hb", hb)
                tok = t0 + ti * 128
                if layer == 0:
                    src = IN["ctx"][tok:tok + 128, :] if gi == 0 else IN["x"][tok - TC:tok - TC + 128, :]
                else:
                    src = SC["H1"][tok:tok + 128, :]
                P.dma("sp", h[:], src, writes=[h])
                P.op("act", "activation", [h], [junk, sta], out=junk[:], in_=h[:], func=AF.Square,
                     accum_out=sta[:, ti:ti + 1])
                P.op("act", "activation", [sta, epsT], [sta], out=sta[:, ti:ti + 1], in_=sta[:, ti:ti + 1], func=AF.Sqrt,
                     scale=1.0 / D, bias=epsT[:, 0:1])
                P.op("dve", "reciprocal", [sta], [sta], out=sta[:, ti:ti + 1], in_=sta[:, ti:ti + 1])
                x_ = xn[ti]
                P.op("dve", "tensor_scalar", [h, sta], [x_], out=x_[:], in0=h[:], scalar1=sta[:, ti:ti + 1],
                     scalar2=None, op0=ALU.mult)

        def transpose_part(gi):
            t0, n = GROUPS[gi]
            ntl = n // 128
            s = 1 if gi == 0 else 0
            u = uT[gi % 2]
            xn = xn2[gi % 2]
            for j in range(8):
                tpp = nxt("tp", tp)
                for ti in range(ntl):
                    P.op("pe", "transpose", [xn[ti], ident_bf], [tpp], tpp[:, ti * 128:(ti + 1) * 128],
                         xn[ti][:, j * 128:(j + 1) * 128], ident_bf[:])
                P.op("dve", "tensor_scalar", [tpp, affA, affB], [u], out=u[:, j, 0:n],
                     in0=tpp[:, 0:n], scalar1=affA[:, j, s:s + 1], scalar2=affB[:, j, s:s + 1], op0=ALU.mult,
                     op1=ALU.add)


        def proj_part(gi):
            t0, n = GROUPS[gi]
            ntl = n // 128
            u = uT[gi % 2]

            def fm_proj(c0, ncol):
                ps = nxt("acc", acc)
                for k in range(8):
                    P.op("pe", "matmul", [w_in, u], [ps], ps[0:ncol, 0:n], w_in[:, k, c0:c0 + ncol], u[:, k, 0:n],
                         start=(k == 0), stop=(k == 7))
                return ps

            def store_fm(ps, ncol, dst, dt, func=None, eng=None):
                o = nxt("fobf", fo_bf) if dt == BF16 else nxt("fof", fo_f)
                if func is not None:
                    P.op("act", "activation", [ps], [o], out=o[0:ncol, 0:n], in_=ps[0:ncol, 0:n], func=func)
                else:
                    copy_op(eng or evac_engine(), ps, ps[0:ncol, 0:n], o, o[0:ncol, 0:n])
                P.dma(stq(), dst, o[0:ncol, 0:n], reads=[o])

            tsl = slice(t0, t0 + n)
            if layer == 0:
                for j in range(3):
                    ps = fm_proj(j * 128, 128)
                    copy_op(evac_engine(), ps, ps[:, 0:n], cqT, cqT[:, j, 0:n])
                for j in range(2):
                    ps = fm_proj(384 + j * 128, 128)
                    copy_op(evac_engine(), ps, ps[:, 0:n], ckvT, ckvT[:, j, 0:n])
                for (src_t, nk, nrm, dst_t, dim) in ((cqT, 3, qnT, cqn, 384.0), (ckvT, 2, kvnT, ckvn, 256.0)):
                    P.op("act", "activation", [src_t], [sq], out=sq[:, 0:nk, 0:n], in_=src_t[:, 0:nk, 0:n], func=AF.Square)
                    ps = nxt("acc", acc)
                    for k in range(nk):
                        P.op("pe", "matmul", [ones_b, sq], [ps], ps[:, 0:n], ones_b[:], sq[:, k, 0:n], start=(k == 0),
                             stop=(k == nk - 1))
                    P.op("act", "activation", [ps, epsT], [rstd], out=rstd[:, 0:n], in_=ps[:, 0:n], func=AF.Sqrt,
                         scale=1.0 / dim, bias=epsT[:, 0:1])
                    P.op("dve", "reciprocal", [rstd], [rstd], out=rstd[:, 0:n], in_=rstd[:, 0:n])
                    for k in range(nk):
                        P.op("dve", "scalar_tensor_tensor", [src_t, nrm, rstd], [dst_t], out=dst_t[:, k, 0:n],
                             in0=src_t[:, k, 0:n], scalar=nrm[:, k:k + 1], in1=rstd[:, 0:n], op0=ALU.mult, op1=ALU.mult)
                rot = gi > 0
                if rot:
                    P.dma("sp", rC[:, 0:n], IN["ropeC"][:, t0 - TC:t0 - TC + n], writes=[rC])
                    P.dma("sp", rS[:, 0:n], IN["ropeS"][:, t0 - TC:t0 - TC + n], writes=[rS])
                for hh in range(8):
                    ps = nxt("acc", acc)
                    for k in range(3):
                        P.op("pe", "matmul", [w_uq, cqn], [ps], ps[0:96, 0:n], w_uq[:, k, hh * 192:hh * 192 + 96],
                             cqn[:, k, 0:n], start=(k == 0), stop=(k == 2))
                    o = nxt("fobf", fo_bf)
                    if rot:
                        ps2 = nxt("acc", acc)
                        for k in range(3):
                            P.op("pe", "matmul", [w_uq, cqn], [ps2], ps2[0:96, 0:n],
                                 w_uq[:, k, hh * 192 + 96:hh * 192 + 192], cqn[:, k, 0:n], start=(k == 0), stop=(k == 2))
                        copy_op("act", ps, ps[0:64, 0:n], o, o[0:64, 0:n])
                        P.op("dve", "tensor_tensor", [ps, rC], [rt1], out=rt1[64:96, 0:n], in0=ps[64:96, 0:n],
                             in1=rC[64:96, 0:n], op=ALU.mult)
                        P.op("dve", "tensor_tensor", [ps2, rS], [rt2], out=rt2[64:96, 0:n], in0=ps2[64:96, 0:n],
                             in1=rS[64:96, 0:n], op=ALU.mult)
                        P.op("pool", "tensor_tensor", [rt1, rt2], [o], out=o[64:96, 0:n], in0=rt1[64:96, 0:n],
                             in1=rt2[64:96, 0:n], op=ALU.add)
                    else:
                        copy_op(evac_engine(), ps, ps[0:96, 0:n], o, o[0:96, 0:n])
                    P.dma(stq(), SC["QT"][hh, :, tsl], o[0:96, 0:n], reads=[o])
                for c in range(4):
                    ps = nxt("acc", acc)
                    for k in range(2):
                        P.op("pe", "matmul", [w_ukv, ckvn], [ps], ps[:, 0:n], w_ukv[:, k, c * 128:(c + 1) * 128],
                             ckvn[:, k, 0:n], start=(k == 0), stop=(k == 1))
                    o = nxt("fobf", fo_bf)
                    copy_op(evac_engine(), ps, ps[:, 0:n], o, o[:, 0:n])
                    for hh in range(2):
                        P.dma(stq(), SC["KT"][c * 2 + hh, 0:64, tsl], o[hh * 64:(hh + 1) * 64, 0:n], reads=[o])
                for ti in range(ntl):
                    ps = nxt("acc", acc)
                    for k in range(2):
                        P.op("pe", "matmul", [w_ukv, ckvn], [ps], ps[:, :], ckvn[:, k, ti * 128:(ti + 1) * 128],
                             w_ukv[:, k, 512:1024], start=(k == 0), stop=(k == 1))
                    o = nxt("fobf", fo_bf)
                    copy_op(evac_engine(), ps, ps[:, :], o, o[:, :])
                    P.dma(stq(), SC["V"][t0 + ti * 128:t0 + (ti + 1) * 128, :], o[:, :], reads=[o])
                ps = fm_proj(640, 32)
                if rot:
                    ps2 = fm_proj(3760, 32)
                    P.op("dve", "tensor_tensor", [ps, rC], [rt1], out=rt1[0:32, 0:n], in0=ps[0:32, 0:n], in1=rC[0:32, 0:n],
                         op=ALU.mult)
                    P.op("dve", "tensor_tensor", [ps2, rS], [rt2], out=rt2[0:32, 0:n], in0=ps2[0:32, 0:n],
                         in1=rS[0:32, 0:n], op=ALU.mult)
                    P.op("pool", "tensor_tensor", [rt1, rt2], [kro], out=kro[0:32, 0:n], in0=rt1[0:32, 0:n],
                         in1=rt2[0:32, 0:n], op=ALU.add)
                else:
                    copy_op("dve", ps, ps[0:32, 0:n], kro, kro[0:32, 0:n])
                for hh in range(8):
                    P.dma(stq(), SC["KT"][hh, 64:96, tsl], kro[0:32, 0:n], reads=[kro])
                yield
                for c in range(8):
                    ps = fm_proj(672 + c * 128, 128)
                    store_fm(ps, 128, SC["MQKB"][c * 128:(c + 1) * 128, tsl], BF16)
                ps = fm_proj(3792, 8)
                store_fm(ps, 8, SC["GI"][:, tsl], F32)
                ps = fm_proj(3800, 8)
                store_fm(ps, 8, SC["GF"][:, tsl], F32)
                for c in range(8):
                    ps = fm_proj(2736 + c * 128, 128)
                    store_fm(ps, 128, SC["SZT"][c * 128:(c + 1) * 128, tsl], BF16, func=AF.Silu)
                tm_specs = [(1696, SC["MV"], None), (2208, SC["MO"], AF.Sigmoid)]
            else:
                for c in range(4):
                    ps = fm_proj(c * 128, 128)
                    store_fm(ps, 128, SC["MQK"][c * 128:(c + 1) * 128, tsl], F32)
                yield
                for d in range(2):
                    ps = fm_proj(1024 + 16 * d, 16)
                    copy_op("dve", ps, ps[0:16, 0:n], gaT[d], gaT[d][0:16, 0:n])
                for d in range(2):
                    for c2 in range(2):
                        ps = nxt("acc", acc)
                        P.op("pe", "matmul", [wg, gaT[d]], [ps], ps[:, 0:n], wg[0:16, d, c2 * 128:(c2 + 1) * 128],
                             gaT[d][0:16, 0:n], start=True, stop=True)
                        P.op("act", "activation", [ps, nbg], [lge], out=lge[:, 0:n], in_=ps[:, 0:n], func=AF.Exp, scale=-1.0,
                             bias=nbg[:, d, c2:c2 + 1])
                        P.op("act", "activation", [lge, one1], [lge], out=lge[:, 0:n], in_=lge[:, 0:n], func=AF.Ln,
                             bias=one1[:, 0:1])
                        o = lgo[(d * 2 + c2) % 2]
                        P.op("dve", "tensor_scalar", [lge], [o], out=o[:, 0:n], in0=lge[:, 0:n], scalar1=-1.0 / 16.0,
                             scalar2=None, op0=ALU.mult)
                        P.dma(stq(), SC["LG"][d, c2 * 128:(c2 + 1) * 128, tsl], o[:, 0:n], reads=[o])
                for c in range(4):
                    ps = fm_proj(1056 + c * 128, 128)
                    store_fm(ps, 128, SC["NQ"][c * 128:(c + 1) * 128, tsl], BF16)
                for c in range(4):
                    ps = fm_proj(1568 + c * 128, 128)
                    store_fm(ps, 128, SC["NK"][c * 128:(c + 1) * 128, tsl], BF16)
                for c in range(8):
                    ps = fm_proj(2592 + c * 128, 128)
                    store_fm(ps, 128, SC["SZT"][c * 128:(c + 1) * 128, tsl], BF16, func=AF.Silu)
                tm_specs = [(512, SC["MV"], None), (2080, SC["V"], None)]
            for (c0, dst, func) in tm_specs:
                for ti in range(ntl):
                    ps = nxt("acc", acc)
                    for k in range(8):
                        P.op("pe", "matmul", [w_in, u], [ps], ps[:, :], u[:, k, ti * 128:(ti + 1) * 128],
                             w_in[:, k, c0:c0 + 512], start=(k == 0), stop=(k == 7))
                    o = nxt("fobf", fo_bf)
                    if func is not None:
                        P.op("act", "activation", [ps], [o], out=o[:, :], in_=ps[:, :], func=func)
                    else:
                        copy_op(evac_engine(), ps, ps[:, :], o, o[:, :])
                    P.dma(stq(), dst[t0 + ti * 128:t0 + (ti + 1) * 128, :], o[:, :], reads=[o])

        norm_part(0)
        transpose_part(0)
        for gi in range(len(GROUPS)):
            if gi + 1 < len(GROUPS):
                norm_part(gi + 1)
            gen = proj_part(gi)
            next(gen)
            if gi + 1 < len(GROUPS):
                transpose_part(gi + 1)
            for _ in gen:
                pass
        P.barrier()


def attention(nc, P, SC, ones_f, heads, dq, scale, load_q, load_k, Vd, cat_row0, groups, bias_fn=None, ident_bf=None,
              early_release=False, act_recip=False):
    LOOK = 4
    NS = 5
    EPI_DELAY = 8
    with ExitStack() as es:
        A = Alloc(nc, es)
        V = A.sb([128, NT, heads, 65], BF16, "Vall")
        P.op("pool", "memset", [], [V], V[:, :, :, 64:65], 1.0)
        for half in range(2):
            tl = slice(half * 17, (half + 1) * 17)
            for hh in range(heads):
                P.dma("sp" if hh % 2 == 0 else "pool", V[:, tl, hh, 0:64],
                      Vd[half * 17 * 128:(half + 1) * 17 * 128, hh * 64:(hh + 1) * 64].rearrange("(t p) d -> p t d", p=128),
                      writes=[V])
        kT = [A.sb([128, T], BF16, "kT%d" % i) for i in range(2)]
        qT = [A.sb([128, T], BF16, "qT%d" % i) for i in range(2)]
        pt = [A.sb([128, 512], BF16, "pt%d" % i) for i in range(NS)]
        sb_t = [A.sb([128, 512], F32, "sbt%d" % i) for i in range(3)] if bias_fn is not None else None
        rden = [A.sb([128, 512], F32, "rden%d" % i) for i in range(2)]
        ocp = [A.sb([128, 512], F32, "ocp%d" % i) for i in range(3)] if early_release else None
        rsc = A.sb([128, 512], F32, "rsc")
        bcs = [A.sb([128, 512], F32, "bcs%d" % i) for i in range(2)]
        szt = [A.sb([64, 512], BF16, "szt%d" % i) for i in range(3)]
        tmp = [A.sb([64, 512], F32, "atmp%d" % i) for i in range(2)]
        ao = [A.sb([64, 512], BF16, "ao%d" % i) for i in range(2)]
        Sps = [A.ps([128, 512], F32, "Sps%d" % i) for i in range(NS)]
        Ops = [A.ps([128, 512], F32, "Ops%d" % i) for i in range(2)]
        Bps = A.ps([128, 512], F32, "Bps")
        if dq < 128:
            for t_ in kT + qT:
                P.op("pool", "memset", [], [t_], t_[64:128, :], 0.0)
        P.dma("sp", kT[0][0:dq, :], load_k(0), writes=[kT[0]])
        P.dma("pool", qT[0][0:dq, :], load_q(0), writes=[qT[0]])
        gcount = 0
        it = 0
        rd_dep = [DramDep() for _ in range(16)]
        pend = []
        for h in range(heads):
            k_ = kT[h % 2]
            q_ = qT[h % 2]
            if h + 1 < heads:
                P.dma("sp", kT[(h + 1) % 2][0:dq, :], load_k(h + 1), writes=[kT[(h + 1) % 2]])
                P.dma("pool", qT[(h + 1) % 2][0:dq, :], load_q(h + 1), writes=[qT[(h + 1) % 2]])
            bias_loader = None
            if bias_fn is not None:
                if h == 0:
                    bias_cur, ld0 = bias_fn(0, A)
                    for _ in ld0:
                        pass
                bias_tiles = bias_cur
                if h + 1 < heads:
                    bias_cur, bias_loader = bias_fn(h + 1, A if h == 0 else None)
            else:
                bias_tiles = None
            r0 = cat_row0 + h * 64
            items = []
            for (q0, n, tiles, gkey) in groups:
                gid = gcount
                gcount += 1
                for j, kt in enumerate(tiles):
                    if isinstance(kt, tuple):
                        kt, c0, c1 = kt
                    else:
                        c0, c1 = 0, n
                    items.append((gid, q0, n, gkey, j, kt, len(tiles), c0, c1))

            def flush(cond):
                for e_ in pend[:]:
                    if cond(e_[1][0]):
                        emit_epi(*e_[1])
                        pend.remove(e_)

            def emit_S(item, slot):
                gid, q0, n, gkey, j, kt, nt_, c0, c1 = item
                S = Sps[slot % NS]
                p_ = pt[slot % NS]
                if j == 0:
                    flush(lambda g2: g2 % 3 == gid % 3)
                    sz = szt[gid % 3]
                    P.dma("sp", sz[:, 0:n], SC["SZT"][r0:r0 + 64, q0:q0 + n], writes=[sz])
                P.op("pe", "matmul", [k_, q_], [S], S[:, c0:c1], k_[:, kt * 128:(kt + 1) * 128], q_[:, q0 + c0:q0 + c1],
                     start=True, stop=True)
                bt = bias_tiles.get((gkey, kt)) if bias_tiles is not None else None
                if bt is not None:
                    sb = sb_t[slot % 3]
                    P.op("dve", "scalar_tensor_tensor", [S, bt], [sb], out=sb[:, c0:c1], in0=S[:, c0:c1], scalar=scale,
                         in1=bt[:, c0:c1], op0=ALU.mult, op1=ALU.add)
                    P.op("act", "activation", [sb], [p_], out=p_[:, c0:c1], in_=sb[:, c0:c1], func=AF.Exp)
                else:
                    P.op("act", "activation", [S], [p_], out=p_[:, c0:c1], in_=S[:, c0:c1], func=AF.Exp, scale=scale)

            def emit_PV(item, slot):
                gid, q0, n, gkey, j, kt, nt_, c0, c1 = item
                O = Ops[gid % 2]
                p_ = pt[slot % NS]
                assert j > 0 or (c0 == 0 and c1 == n)
                if j == 0:
                    flush(lambda g2: g2 % 2 == gid % 2)
                P.op("pe", "matmul", [V, p_], [O], O[0:65, c0:c1], V[:, kt, h, :], p_[:, c0:c1], start=(j == 0),
                     stop=(j == nt_ - 1))
                if j == nt_ - 1:
                    rd = rden[gid % 2]
                    if early_release:
                        oc = ocp[gid % 3]
                        P.op("act", "copy", [O], [oc], out=oc[0:65, 0:n], in_=O[0:65, 0:n])
                        P.op("act", "activation", [oc], [rsc], out=rsc[64:65, 0:n], in_=oc[64:65, 0:n], func=AF.Ln)
                        P.op("act", "activation", [rsc], [rd], out=rd[64:65, 0:n], in_=rsc[64:65, 0:n], func=AF.Exp, scale=-1.0)
                    elif act_recip:
                        P.op("act", "activation", [O], [rsc], out=rsc[64:65, 0:n], in_=O[64:65, 0:n], func=AF.Ln)
                        P.op("act", "activation", [rsc], [rd], out=rd[64:65, 0:n], in_=rsc[64:65, 0:n], func=AF.Exp, scale=-1.0)
                    else:
                        P.op("dve", "reciprocal", [O], [rd], out=rd[64:65, 0:n], in_=O[64:65, 0:n])
                    pend.append([EPI_DELAY, (gid, q0, n, r0, h)])

            def emit_epi(gid, q0, n, r0, h):
                O = Ops[gid % 2]
                rd = rden[gid % 2]
                bc_ = bcs[gid % 2]
                tm_ = tmp[gid % 2]
                a_ = ao[gid % 2]
                sz = szt[gid % 3]
                P.op("pe", "matmul", [ones_f, rd], [Bps], Bps[0:64, 0:n], ones_f[64:65, 0:64], rd[64:65, 0:n],
                     start=True, stop=True)
                if early_release:
                    oc = ocp[gid % 3]
                    P.op("dve", "tensor_tensor", [oc, Bps], [tm_], out=tm_[:, 0:n], in0=oc[0:64, 0:n], in1=Bps[0:64, 0:n],
                         op=ALU.mult)
                else:
                    P.op("dve", "tensor_copy", [Bps], [bc_], out=bc_[0:64, 0:n], in_=Bps[0:64, 0:n])
                    P.op("dve", "tensor_tensor", [O, bc_], [tm_], out=tm_[:, 0:n], in0=O[0:64, 0:n], in1=bc_[0:64, 0:n],
                         op=ALU.mult)
                P.op("pool", "tensor_tensor", [tm_, sz], [a_], out=a_[:, 0:n], in0=tm_[:, 0:n], in1=sz[:, 0:n], op=ALU.mult)
                P.dma("pool", SC["CATT"][r0:r0 + 64, q0:q0 + n], a_[:, 0:n], reads=[a_])

            nI = len(items)
            for idx in range(nI + LOOK):
                if idx < nI:
                    emit_S(items[idx], it + idx)
                for e_ in pend[:]:
                    e_[0] -= 1
                    if e_[0] <= 0:
                        emit_epi(*e_[1])
                        pend.remove(e_)
                if idx - LOOK >= 0:
                    emit_PV(items[idx - LOOK], it + idx - LOOK)
                if bias_loader is not None and idx % 3 == 2:
                    next(bias_loader, None)
            if bias_loader is not None:
                for _ in bias_loader:
                    pass
            it += nI
        for e_ in pend:
            emit_epi(*e_[1])
        P.barrier()


def mlstm_phase(nc, P, IN, SC, ident_bf, ident_f, ones_f):
    NB = NT
    NCH = T // 64
    with ExitStack() as es:
        A = Alloc(nc, es)
        esT = A.sb([128, NB, 64], F32, "esT")
        fT = A.sb([128, NB, 64], F32, "fT")
        decbc = A.sb([128, 8, NCH], F32, "decbc")
        mask = [A.sb([128, 64], F32, "mask%d" % d) for d in range(2)]
        for d in range(2):
            P.op("pool", "memset", [], [mask[d]], mask[d][:], 1.0)
            for half in range(2):
                pr = slice(half * 64, half * 64 + 64)
                P.op("pool", "affine_select", [mask[d]], [mask[d]], out=mask[d][pr, :], in_=mask[d][pr, :],
                     pattern=[[1 if d == 0 else -1, 64]], compare_op=ALU.is_ge, fill=0.0, base=0,
                     channel_multiplier=-1 if d == 0 else 1)
        with ExitStack() as es2:
            A2 = Alloc(nc, es2)
            X1 = A2.sb([64, T], F32, "X1")
            X2 = A2.sb([64, T], F32, "X2")
            X3 = A2.sb([64, T], F32, "X3")
            X4 = A2.sb([64, T], F32, "X4")
            gb = A2.sb([64, 2], F32, "gb")
            nbf = A2.sb([64, 1], F32, "nbf")
            one1 = A2.sb([64, 1], F32, "one1")
            dec = A2.sb([64, NCH], F32, "dec")
            aprev = A2.sb([64, NCH], F32, "aprev")
            sel = A2.sb([64, 128], F32, "sel")
            pst = [A2.ps([128, 8, 64], F32, "pst%d" % i) for i in range(2)]
            psd = A2.ps([128, NCH], F32, "psd")
            P.op("pool", "memset", [], [X1], X1[:], 0.0)
            P.op("pool", "memset", [], [X3], X3[:], 0.0)
            P.op("pool", "memset", [], [one1], one1[:], 1.0)
            P.dma("sp", gb[:], IN["l0_gb2"][:, :], writes=[gb])
            for d in range(2):
                P.dma("sp", X1[d * 32:d * 32 + 4, :], SC["GF"][d * 4:d * 4 + 4, :], writes=[X1])
                P.dma("pool", X3[d * 32:d * 32 + 4, :], SC["GI"][d * 4:d * 4 + 4, :], writes=[X3])
            P.op("dve", "tensor_scalar", [gb], [nbf], out=nbf[:], in0=gb[:, 1:2], scalar1=-1.0, scalar2=None, op0=ALU.mult)
            P.op("act", "activation", [X1, nbf], [X1], out=X1[:], in_=X1[:], func=AF.Exp, scale=-1.0, bias=nbf[:, 0:1])
            P.op("act", "activation", [X1, one1], [X1], out=X1[:], in_=X1[:], func=AF.Ln, bias=one1[:, 0:1])

            def seg_views(tile_, prng, d):
                if d == 0:
                    return [tile_[prng, 0:T]]
                return [tile_[prng, 0:TC][:, ::-1], tile_[prng, TC:T][:, ::-1]]

            def scan(dst, src, op0, d):
                prng = slice(d * 32, d * 32 + 32)
                dv = seg_views(dst, prng, d)
                sv = seg_views(src, prng, d)
                for i in range(len(dv)):
                    init = 0.0 if i == 0 else dst[prng, 0:1]
                    P.op("dve", "tensor_tensor_scan", [src, dst], [dst], out=dv[i], data0=sv[i], data1=sv[i],
                         initial=init, op0=op0, op1=ALU.bypass)

            for d in range(2):
                scan(X2, X1, ALU.add, d)
            P.op("dve", "scalar_tensor_tensor", [X3, gb, X2], [X3], out=X3[:], in0=X3[:], scalar=gb[:, 0:1], in1=X2[:],
                 op0=ALU.add, op1=ALU.add)
            for d in range(2):
                scan(X1, X3, ALU.max, d)
            for d in range(2):
                prng = slice(d * 32, d * 32 + 32)
                jj = 63 if d == 0 else 0
                P.op("dve", "tensor_copy", [X1], [X4], out=X4[prng, :].rearrange("p (c j) -> p c j", j=64),
                     in_=X1[prng, :].rearrange("p (c j) -> p c j", j=64)[:, :, jj:jj + 1].to_broadcast([32, NCH, 64]))
            aend = X4[:, :].rearrange("p (c j) -> p c j", j=64)[:, :, 0]
            P.op("pool", "memset", [], [aprev], aprev[:], 0.0)
            P.op("dve", "tensor_copy", [X4], [aprev], out=aprev[0:32, 1:NCH], in_=aend[0:32, 0:NCH - 1])
            P.op("dve", "tensor_copy", [X4], [aprev], out=aprev[32:64, 0:3], in_=aend[32:64, 1:4])
            P.op("dve", "tensor_copy", [X4], [aprev], out=aprev[32:64, 4:NCH - 1], in_=aend[32:64, 5:NCH])
            P.op("dve", "tensor_copy", [X4], [aprev], out=aprev[32:64, NCH - 1:NCH], in_=aend[32:64, 0:1])
            P.op("dve", "tensor_tensor", [aprev, X4], [dec], out=dec[:], in0=aprev[:], in1=aend, op=ALU.subtract)
            P.op("act", "activation", [dec], [dec], out=dec[:], in_=dec[:], func=AF.Exp)
            P.op("dve", "tensor_tensor", [X3, X4], [X3], out=X3[:], in0=X3[:], in1=X4[:], op=ALU.subtract)
            P.op("act", "activation", [X3], [X3], out=X3[:], in_=X3[:], func=AF.Exp)
            P.op("dve", "tensor_tensor", [X2, X4], [X2], out=X2[:], in0=X2[:], in1=X4[:], op=ALU.subtract)
            P.op("act", "activation", [X2], [X2], out=X2[:], in_=X2[:], func=AF.Exp)
            for (srcX, dstT) in ((X3, esT), (X2, fT)):
                for b0 in range(0, NB, 8):
                    nb = min(8, NB - b0)
                    ps = pst[(b0 // 8) % 2]
                    for bb in range(nb):
                        P.op("pe", "transpose", [srcX, ident_f], [ps], ps[:, bb, :], srcX[:, (b0 + bb) * 128:(b0 + bb + 1) * 128],
                             ident_f[0:64, 0:64])
                    P.op("act", "copy", [ps], [dstT], out=dstT[:, b0:b0 + nb, :], in_=ps[:, 0:nb, :])
            for idx in range(8):
                r = (idx // 4) * 32 + idx % 4
                P.op("dve", "tensor_copy", [ident_f], [sel], out=sel[:], in_=ident_f[0:64, r:r + 1].to_broadcast([64, 128]))
                P.op("pe", "matmul", [sel, dec], [psd], psd[:, :], sel[:, :], dec[:, :], start=True, stop=True)
                P.op("act", "copy", [psd], [decbc], out=decbc[:, idx, :], in_=psd[:, :])
            P.barrier()
        P.op("dve", "tensor_scalar", [esT], [esT], out=esT[:], in0=esT[:], scalar1=128.0 ** -0.5, scalar2=None, op0=ALU.mult)
        xraw = A.sb([128, T], BF16, "xraw")
        cvw = A.sb([128, 8, 4], F32, "cvw")
        P.dma("sp", cvw[:], IN["l0_convT"][:, :, :], writes=[cvw])
        dg = [A.sb([128, 3, 128], BF16, "dg%d" % i) for i in range(2)]
        qT = A.sb([128, T], BF16, "mqT")
        qd = [A.sb([128, T], BF16, "mqd%d" % d) for d in range(2)]
        kT = A.sb([128, T], BF16, "mkT")
        kTok = A.sb([128, NB, 128], BF16, "kTok")
        vtok = A.sb([128, NB, 128], BF16, "vtok")
        vpp = [A.sb([128, NB, 129], BF16, "vpp%d" % d) for d in range(2)]
        SmT = [A.sb([128, NB, 64], BF16, "SmT%d" % d) for d in range(2)]
        hbuf = [A.sb([128, NB, 129], F32, "hbuf%d" % d) for d in range(2)]
        Cst = [[A.sb([128, 129], F32, "C%d_%d" % (d, i)) for i in range(2)] for d in range(2)]
        Cb = [[A.sb([128, 129], BF16, "Cb%d_%d" % (d, i)) for i in range(2)] for d in range(2)]
        dn = [A.sb([128, NB], F32, "dn%d" % d) for d in range(2)]
        pcv = [A.ps([128, 512], F32, "pcv%d" % i) for i in range(2)]
        pU = [A.ps([128, 129], F32, "pU%d" % i) for i in range(2)]
        pN = [[A.ps([128, 129], F32, "pN%d_%d" % (d, i)) for i in range(2)] for d in range(2)]
        order = [list(range(NCH)), [3, 2, 1, 0] + list(range(NCH - 1, 3, -1))]
        pieces = [(0, TC)] + [(TC + 512 * i, TC + 512 * (i + 1)) for i in range(8)]
        pc = 0
        for h in range(4):
            for which in range(2):
                ch = which * 4 + h
                dg_ = dg[which]
                P.dma("sp" if which == 0 else "pool", xraw[:], SC["MQKB"][ch * 128:(ch + 1) * 128, :], writes=[xraw])
                for j in range(3):
                    P.op("dve", "tensor_scalar", [ident_f, cvw], [dg_], out=dg_[:, j, :], in0=ident_f[:], scalar1=cvw[:, ch, j:j + 1],
                         scalar2=None, op0=ALU.mult)
                dst = qT if which == 0 else kT
                for (a, b) in pieces:
                    s0, s1 = (0, TC) if a < TC else (TC, T)
                    ps = pcv[pc % 2]
                    pc += 1
                    P.op("pe", "matmul", [dg_, xraw], [ps], ps[:, 0:b - a], dg_[:, 1, :], xraw[:, a:b], start=True, stop=False)
                    lo = max(a, s0 + 1)
                    P.op("pe", "matmul", [dg_, xraw], [ps], ps[:, lo - a:b - a], dg_[:, 0, :], xraw[:, lo - 1:b - 1], start=False,
                         stop=False)
                    hi = min(b, s1 - 1)
                    P.op("pe", "matmul", [dg_, xraw], [ps], ps[:, 0:hi - a], dg_[:, 2, :], xraw[:, a + 1:hi + 1], start=False,
                         stop=True)
                    P.op("act", "activation", [ps, cvw], [dst], out=dst[:, a:b], in_=ps[:, 0:b - a], func=AF.Silu,
                         bias=cvw[:, ch, 3:4])
            for b0 in range(0, NB, 4):
                nb = min(4, NB - b0)
                ps = pcv[pc % 2]
                pc += 1
                psb = ps[:, 0:256].bitcast(BF16)
                for bb in range(nb):
                    P.op("pe", "transpose", [kT, ident_bf], [ps], psb[:, bb * 128:(bb + 1) * 128],
                         kT[:, (b0 + bb) * 128:(b0 + bb + 1) * 128], ident_bf[:])
                P.op("act", "copy", [ps], [kTok], out=kTok[:, b0:b0 + nb, :],
                     in_=psb[:, 0:nb * 128].rearrange("p (b j) -> p b j", j=128))
            P.dma("sp", vtok[:], SC["MV"][:, h * 128:(h + 1) * 128].rearrange("(b p) j -> p b j", p=128), writes=[vtok])
            for d in range(2):
                col = d * 32 + h
                idx = d * 4 + h
                P.op("pool" if d == 0 else "dve", "tensor_tensor", [vtok, esT], [vpp[d]], out=vpp[d][:, :, 0:128], in0=vtok[:],
                     in1=esT[:, :, col:col + 1].to_broadcast([128, NB, 128]), op=ALU.mult)
                P.op("dve", "tensor_copy", [esT], [vpp[d]], out=vpp[d][:, :, 128:129], in_=esT[:, :, col:col + 1])
                P.op("pool" if d == 1 else "dve", "tensor_tensor", [qT, decbc], [qd[d]],
                     out=qd[d][:, :].rearrange("p (c j) -> p c j", j=64), in0=qT[:, :].rearrange("p (c j) -> p c j", j=64),
                     in1=decbc[:, idx, :].unsqueeze(2).to_broadcast([128, NCH, 64]), op=ALU.mult)
            for b0 in range(0, NB, 4):
                nb = min(4, NB - b0)
                ps = pcv[pc % 2]
                pc += 1
                psv = ps[:, 0:256].rearrange("p (b j) -> p b j", j=64)
                for bb in range(nb):
                    for half in range(2):
                        c = (b0 + bb) * 2 + half
                        pr = slice(half * 64, half * 64 + 64)
                        P.op("pe", "matmul", [kT, qT], [ps], psv[pr, bb, :], kT[:, c * 64:(c + 1) * 64], qT[:, c * 64:(c + 1) * 64],
                             start=True, stop=True)
                for d in range(2):
                    P.op("dve", "tensor_tensor", [ps, mask[d]], [SmT[d]], out=SmT[d][:, b0:b0 + nb, :], in0=psv[:, 0:nb, :],
                         in1=mask[d][:].unsqueeze(1).to_broadcast([128, nb, 64]), op=ALU.mult)
            for d in range(2):
                P.op("pool", "memset", [], [Cst[d][0]], Cst[d][0][:], 0.0)
                P.op("pool", "memset", [], [Cb[d][0]], Cb[d][0][:], 0.0)
            for i in range(NCH):
                for d in range(2):
                    c = order[d][i]
                    b, half = c // 2, c % 2
                    pr = slice(half * 64, half * 64 + 64)
                    idx = d * 4 + h
                    Cold, Cnew = Cst[d][i % 2], Cst[d][(i + 1) % 2]
                    cbo, cbn = Cb[d][i % 2], Cb[d][(i + 1) % 2]
                    U = pU[d]
                    N = pN[d][i % 2]
                    P.op("pe", "matmul", [kTok, vpp[d]], [U], U[:, :], kTok[pr, b, :], vpp[d][pr, b, :], start=True, stop=True)
                    P.op("pe", "matmul", [SmT[d], vpp[d]], [N], N[pr, :], SmT[d][pr, b, :], vpp[d][pr, b, :], start=True,
                         stop=False)
                    P.op("pe", "matmul", [qd[d], cbo], [N], N[pr, :], qd[d][:, c * 64:(c + 1) * 64], cbo[:], start=False, stop=True)
                    P.op("dve", "scalar_tensor_tensor", [Cold, decbc, U], [cbn], out=cbn[:], in0=Cold[:],
                         scalar=decbc[:, idx, c:c + 1], in1=U[:, :], op0=ALU.mult, op1=ALU.add)
                    P.op("dve", "scalar_tensor_tensor", [Cold, decbc, U], [Cnew], out=Cnew[:], in0=Cold[:],
                         scalar=decbc[:, idx, c:c + 1], in1=U[:, :], op0=ALU.mult, op1=ALU.add)
                    P.op("act", "copy", [N], [hbuf[d]], out=hbuf[d][pr, b, :], in_=N[pr, :])
            for d in range(2):
                col = d * 32 + h
                P.op("act", "activation", [hbuf[d]], [dn[d]], out=dn[d][:, :].unsqueeze(2), in_=hbuf[d][:, :, 128:129], func=AF.Abs)
                P.op("dve", "tensor_tensor", [dn[d], fT], [dn[d]], out=dn[d][:, :].unsqueeze(2), in0=dn[d][:, :].unsqueeze(2),
                     in1=fT[:, :, col:col + 1], op=ALU.max)
                P.op("dve", "reciprocal", [dn[d]], [dn[d]], out=dn[d][:], in_=dn[d][:])
                P.op("dve" if d == 0 else "pool", "tensor_tensor", [hbuf[d], dn[d]], [hbuf[d]], out=hbuf[d][:, :, 0:128],
                     in0=hbuf[d][:, :, 0:128], in1=dn[d][:, :].unsqueeze(2).to_broadcast([128, NB, 128]), op=ALU.mult)
                P.dma("sp" if d == 0 else "pool", SC["HM"][d, :, h * 128:(h + 1) * 128].rearrange("(b p) j -> p b j", p=128),
                      hbuf[d][:, :, 0:128], reads=[hbuf[d]])
        P.barrier()


def combine_phase(nc, P, IN, SC, ident_bf, src0, src1, mul, norm_name, cat_row0, groups):
    with ExitStack() as es:
        A = Alloc(nc, es)
        nrow = A.sb([128, 512], F32, "nrow")
        P.dma("sp", nrow[:], IN[norm_name][0:1, :].to_broadcast([128, 512]), writes=[nrow])
        epsT = A.sb([128, 1], F32, "epsTc")
        P.op("pool", "memset", [], [epsT], epsT[:], EPS)
        a_ = [A.sb([128, 512], F32, "cA%d" % i) for i in range(4)]
        b_ = [A.sb([128, 512], F32, "cB%d" % i) for i in range(4)]
        m_ = [A.sb([128, 512], BF16, "cM%d" % i) for i in range(4)]
        junk = A.sb([128, 128], F32, "cjunk")
        ss = [A.sb([128, 4], F32, "css%d" % i) for i in range(4)]
        hn = [A.sb([128, 512], F32, "chn%d" % i) for i in range(4)]
        hb = [A.sb([128, 512], BF16, "chb%d" % i) for i in range(4)]
        sz = [A.sb([128, 512], BF16, "csz%d" % i) for i in range(2)]
        oo = [A.sb([128, 512], BF16, "coo%d" % i) for i in range(2)]
        tp = [A.ps([128, 512], BF16, "ctp%d" % i) for i in range(8)]
        tiles = []
        for gi, (t0, n) in enumerate(groups):
            for ti in range(n // 128):
                tiles.append((gi, t0, n, ti))

        def stage1(it):
            gi, t0, n, ti = tiles[it]
            tok = t0 + ti * 128
            a, b, m = a_[it % 4], b_[it % 4], m_[it % 4]
            P.dma("sp", a[:], src0[tok:tok + 128, :], writes=[a])
            P.dma("pool", b[:], src1[tok:tok + 128, :], writes=[b])
            if mul is not None:
                P.dma("sp", m[:], mul[tok:tok + 128, :], writes=[m])
            P.op("dve", "tensor_tensor", [a, b], [a], out=a[:], in0=a[:], in1=b[:], op=ALU.add)
            if mul is not None:
                P.op("pool", "tensor_tensor", [a, m], [a], out=a[:], in0=a[:], in1=m[:], op=ALU.mult)

        def stage2(it):
            gi, t0, n, ti = tiles[it]
            a, s_, hb_ = a_[it % 4], ss[it % 4], hb[it % 4]
            for hh in range(4):
                P.op("act", "activation", [a], [junk, s_], out=junk[:], in_=a[:, hh * 128:(hh + 1) * 128], func=AF.Square,
                     accum_out=s_[:, hh:hh + 1])
            P.op("act", "activation", [s_, epsT], [s_], out=s_[:], in_=s_[:], func=AF.Sqrt, scale=1.0 / 128, bias=epsT[:, 0:1])
            P.op("dve", "reciprocal", [s_], [s_], out=s_[:], in_=s_[:])
            for hh in range(4):
                sl = slice(hh * 128, (hh + 1) * 128)
                P.op("dve", "scalar_tensor_tensor", [a, s_, nrow], [hb_], out=hb_[:, sl], in0=a[:, sl], scalar=s_[:, hh:hh + 1],
                     in1=nrow[:, sl], op0=ALU.mult, op1=ALU.mult)
            for j in range(4):
                tpj = tp[(gi % 2) * 4 + j]
                P.op("pe", "transpose", [hb_, ident_bf], [tpj], tpj[:, ti * 128:(ti + 1) * 128],
                     hb_[:, j * 128:(j + 1) * 128], ident_bf[:])
            if ti == n // 128 - 1:
                for j in range(4):
                    tpj = tp[(gi % 2) * 4 + j]
                    r0 = cat_row0 + j * 128
                    z_, o_ = sz[j % 2], oo[j % 2]
                    P.dma("sp", z_[:, 0:n], SC["SZT"][r0:r0 + 128, t0:t0 + n], writes=[z_])
                    P.op("dve", "tensor_tensor", [tpj, z_], [o_], out=o_[:, 0:n], in0=tpj[:, 0:n], in1=z_[:, 0:n], op=ALU.mult)
                    P.dma("pool", SC["CATT"][r0:r0 + 128, t0:t0 + n], o_[:, 0:n], reads=[o_])

        stage1(0)
        for it in range(len(tiles)):
            if it + 1 < len(tiles):
                stage1(it + 1)
            stage2(it)
        P.barrier()


def phase_C(nc, P, IN, SC, layer, gateR, out_ap):
    with ExitStack() as es:
        A = Alloc(nc, es)
        w = A.sb([128, 8, 1024], BF16, "w_out")
        with ExitStack() as es2:
            A2 = Alloc(nc, es2)
            stg = [A2.sb([128, 8, 512], F32, "wstg%d" % i) for i in range(2)]
            for i in range(2):
                P.dma("sp" if i == 0 else "pool", stg[i][:],
                      IN["l%d_w_out" % layer][:, i * 512:(i + 1) * 512].rearrange("(k p) n -> p k n", p=128), writes=[stg[i]])
                P.op("dve" if i == 0 else "act", "tensor_copy" if i == 0 else "copy", [stg[i]], [w], out=w[:, :, i * 512:(i + 1) * 512],
                     in_=stg[i][:])
            P.barrier()
        cat = [A.sb([128, 8, 512], BF16, "catT%d" % i) for i in range(3)]
        hold = [A.sb([128, 1024], F32, "hold%d" % i) for i in range(4)]
        tmp = [A.sb([128, 1024], F32, "ctmp%d" % i) for i in range(4)]
        hnew = [A.sb([128, 1024], F32, "hnew%d" % i) for i in range(4)]
        ps = [A.ps([128, 512], F32, "yps%d" % i) for i in range(4)]
        if layer == 1:
            frow = A.sb([128, 1024], F32, "frow")
            P.dma("sp", frow[:], IN["final_norm"][0:1, :].to_broadcast([128, 1024]), writes=[frow])
            epsT = A.sb([128, 1], F32, "epsTf")
            P.op("pool", "memset", [], [epsT], epsT[:], EPS)
            junk = A.sb([128, 1024], F32, "fjunk")
            st = [A.sb([128, 1], F32, "fst%d" % i) for i in range(4)]
            ob = [A.sb([128, 1024], F32, "fob%d" % i) for i in range(4)]
        it = 0
        deferred = []
        groups = GROUPS if layer == 0 else GROUPS[1:]
        def load_cat(gi):
            t0, n = groups[gi]
            c_ = cat[gi % 3]
            for k2 in range(2):
                P.dma("sp", c_[:, k2 * 4:(k2 + 1) * 4, 0:n],
                      SC["CATT"][k2 * 512:(k2 + 1) * 512, t0:t0 + n].rearrange("(k p) t -> p k t", p=128), writes=[c_])

        load_cat(0)
        for gi, (t0, n) in enumerate(groups):
            c_ = cat[gi % 3]
            if gi + 1 < len(groups):
                load_cat(gi + 1)
            g_ = gateR[1] if (layer == 0 and t0 == 0) else gateR[0]
            for ti in range(n // 128):
                tok = t0 + ti * 128
                ho, tm, hn_ = hold[it % 4], tmp[it % 4], hnew[it % 4]
                if layer == 0:
                    srcp = IN["ctx"][tok:tok + 128, :] if t0 == 0 else IN["x"][tok - TC:tok - TC + 128, :]
                else:
                    srcp = SC["H1"][tok:tok + 128, :]
                P.dma("sp", ho[:], srcp, writes=[ho])
                for half in range(2):
                    p_ = ps[(it * 2 + half) % 4]
                    for k in range(8):
                        P.op("pe", "matmul", [c_, w], [p_], p_[:, :], c_[:, k, ti * 128:(ti + 1) * 128],
                             w[:, k, half * 512:(half + 1) * 512], start=(k == 0), stop=(k == 7))
                    P.op("dve", "tensor_tensor", [p_, g_], [tm], out=tm[:, half * 512:(half + 1) * 512], in0=p_[:, :],
                         in1=g_[:, half * 512:(half + 1) * 512], op=ALU.mult)
                P.op("pool", "tensor_tensor", [tm, ho], [hn_], out=hn_[:], in0=tm[:], in1=ho[:], op=ALU.add)
                if layer == 0:
                    P.dma("pool", SC["H1"][tok:tok + 128, :], hn_[:], reads=[hn_])
                else:
                    if deferred:
                        deferred.pop()()

                    def fin(hn_=hn_, s_=st[it % 4], o_=ob[it % 4], tok=tok):
                        P.op("act", "activation", [hn_], [junk, s_], out=junk[:], in_=hn_[:], func=AF.Square, accum_out=s_[:, 0:1])
                        P.op("act", "activation", [s_, epsT], [s_], out=s_[:], in_=s_[:], func=AF.Sqrt, scale=1.0 / D,
                             bias=epsT[:, 0:1])
                        P.op("dve", "reciprocal", [s_], [s_], out=s_[:], in_=s_[:])
                        P.op("act", "activation", [hn_, s_], [o_], out=o_[:], in_=hn_[:], func=AF.Copy, scale=s_[:, 0:1])
                        P.op("dve", "tensor_tensor", [o_, frow], [o_], out=o_[:], in0=o_[:], in1=frow[:], op=ALU.mult)
                        P.dma("pool", out_ap[tok - TC:tok - TC + 128, :], o_[:], reads=[o_])

                    deferred.append(fin)
                it += 1
        while deferred:
            deferred.pop()()
        P.barrier()


def gla_phase(nc, P, IN, SC, ident_bf):
    NB = NT
    NCH = T // 64
    with ExitStack() as es:
        A = Alloc(nc, es)
        mask = [A.sb([128, 64], F32, "gmask%d" % d) for d in range(2)]
        for d in range(2):
            P.op("pool", "memset", [], [mask[d]], mask[d][:], 1.0)
            for half in range(2):
                pr = slice(half * 64, half * 64 + 64)
                P.op("pool", "affine_select", [mask[d]], [mask[d]], out=mask[d][pr, :], in_=mask[d][pr, :],
                     pattern=[[1 if d == 0 else -1, 64]], compare_op=ALU.is_ge, fill=0.0, base=0,
                     channel_multiplier=-1 if d == 0 else 1)
        rm = [A.sb([128, T], BF16, "rm%d" % d) for d in range(2)]
        for d in range(2):
            P.op("pool", "memset", [], [rm[d]], rm[d][:], 1.0)
            j0 = 0 if d == 0 else 63
            P.op("pool", "memset", [rm[d]], [rm[d]], rm[d][:, :].rearrange("p (c j) -> p c j", j=64)[:, :, j0:j0 + 1], 0.0)
        qf = A.sb([128, T], F32, "gqf")
        kf = A.sb([128, T], F32, "gkf")
        lg = A.sb([128, T], F32, "glg")
        bc = A.sb([128, T], F32, "gbc")
        tmp = A.sb([128, T], F32, "gtmp")
        qg = [A.sb([128, T], BF16, "qg%d" % d) for d in range(2)]
        kg = [A.sb([128, T], BF16, "kg%d" % d) for d in range(2)]
        kbf = A.sb([128, T], BF16, "kbf")
        eb = [A.sb([128, NCH], F32, "eb%d" % d) for d in range(2)]
        kbTok = [A.sb([128, NB, 128], BF16, "kbTok%d" % d) for d in range(2)]
        SmT = [A.sb([128, NB, 64], BF16, "gSmT%d" % d) for d in range(2)]
        vtok = A.sb([128, NB, 128], BF16, "gvtok")
        obuf = [[A.sb([128, 128], F32, "gob%d_%d" % (d, i)) for i in range(3)] for d in range(2)]
        Sst = [[A.sb([128, 128], F32, "S%d_%d" % (d, i)) for i in range(2)] for d in range(2)]
        Sb = [[A.sb([128, 128], BF16, "Sb%d_%d" % (d, i)) for i in range(2)] for d in range(2)]
        ptr = [A.ps([128, 512], BF16, "gptr%d" % i) for i in range(2)]
        pS = [A.ps([128, 8, 64], F32, "gpS%d" % i) for i in range(2)]
        pU = [A.ps([128, 128], F32, "gpU%d" % i) for i in range(2)]
        pN = [A.ps([128, 128], F32, "gpN%d" % i) for i in range(2)]
        order = [list(range(NCH)), [3, 2, 1, 0] + list(range(NCH - 1, 3, -1))]
        for hp_ in range(2):
            P.dma("sp", qf[:], SC["MQK"][hp_ * 128:(hp_ + 1) * 128, :], writes=[qf])
            P.dma("pool", kf[:], SC["MQK"][256 + hp_ * 128:256 + (hp_ + 1) * 128, :], writes=[kf])
            for d in range(2):
                P.dma("pool", lg[:], SC["LG"][d, hp_ * 128:(hp_ + 1) * 128, :], writes=[lg])
                if d == 0:
                    P.op("dve", "tensor_tensor_scan", [rm[d], lg], [bc], out=bc[:, :], data0=rm[d][:, :], data1=lg[:, :],
                         initial=0.0, op0=ALU.mult, op1=ALU.add)
                else:
                    P.op("dve", "tensor_tensor_scan", [rm[d], lg], [bc], out=bc[:, ::-1], data0=rm[d][:, ::-1],
                         data1=lg[:, ::-1], initial=0.0, op0=ALU.mult, op1=ALU.add)
                jl = 63 if d == 0 else 0
                bl = bc[:, :].rearrange("p (c j) -> p c j", j=64)[:, :, jl:jl + 1]
                P.op("act", "activation", [bc], [eb[d]], out=eb[d][:, :].unsqueeze(2), in_=bl, func=AF.Exp)
                P.op("act", "activation", [bc], [tmp], out=tmp[:], in_=bc[:], func=AF.Exp)
                P.op("dve", "scalar_tensor_tensor", [qf, tmp], [qg[d]], out=qg[d][:], in0=qf[:], scalar=64.0 ** -0.5, in1=tmp[:],
                     op0=ALU.mult, op1=ALU.mult)
                P.op("act", "activation", [bc], [tmp], out=tmp[:], in_=bc[:], func=AF.Exp, scale=-1.0)
                P.op("dve", "tensor_tensor", [kf, tmp], [kg[d]], out=kg[d][:], in0=kf[:], in1=tmp[:], op=ALU.mult)
                P.op("dve", "tensor_tensor", [bc], [tmp], out=tmp[:, :].rearrange("p (c j) -> p c j", j=64),
                     in0=bl.to_broadcast([128, NCH, 64]), in1=bc[:, :].rearrange("p (c j) -> p c j", j=64), op=ALU.subtract)
                P.op("act", "activation", [tmp], [tmp], out=tmp[:], in_=tmp[:], func=AF.Exp)
                P.op("dve", "tensor_tensor", [kf, tmp], [kbf], out=kbf[:], in0=kf[:], in1=tmp[:], op=ALU.mult)
                for b0 in range(0, NB, 4):
                    nb = min(4, NB - b0)
                    ps = ptr[(b0 // 4) % 2]
                    for bb in range(nb):
                        P.op("pe", "transpose", [kbf, ident_bf], [ps], ps[:, bb * 128:(bb + 1) * 128],
                             kbf[:, (b0 + bb) * 128:(b0 + bb + 1) * 128], ident_bf[:])
                    P.op("act", "copy", [ps], [kbTok[d]], out=kbTok[d][:, b0:b0 + nb, :],
                         in_=ps[:, 0:nb * 128].rearrange("p (b j) -> p b j", j=128))
            for hh in range(2):
                h = hp_ * 2 + hh
                ps_ = slice(hh * 64, hh * 64 + 64)
                P.dma("sp", vtok[:], SC["MV"][:, h * 128:(h + 1) * 128].rearrange("(b p) j -> p b j", p=128), writes=[vtok])
                for d in range(2):
                    for b0 in range(0, NB, 8):
                        nb = min(8, NB - b0)
                        ps = pS[(b0 // 8) % 2]
                        for bb in range(nb):
                            for half in range(2):
                                c = (b0 + bb) * 2 + half
                                pr = slice(half * 64, half * 64 + 64)
                                P.op("pe", "matmul", [kg[d], qg[d]], [ps], ps[pr, bb, :], kg[d][ps_, c * 64:(c + 1) * 64],
                                     qg[d][ps_, c * 64:(c + 1) * 64], start=True, stop=True)
                        P.op("dve", "tensor_tensor", [ps, mask[d]], [SmT[d]], out=SmT[d][:, b0:b0 + nb, :], in0=ps[:, 0:nb, :],
                             in1=mask[d][:].unsqueeze(1).to_broadcast([128, nb, 64]), op=ALU.mult)
                for d in range(2):
                    P.op("pool", "memset", [], [Sst[d][0]], Sst[d][0][ps_, :], 0.0)
                    P.op("pool", "memset", [], [Sb[d][0]], Sb[d][0][ps_, :], 0.0)
                for i in range(NCH):
                    for d in range(2):
                        c = order[d][i]
                        b, half = c // 2, c % 2
                        pr = slice(half * 64, half * 64 + 64)
                        Sold, Snew = Sst[d][i % 2], Sst[d][(i + 1) % 2]
                        sbo, sbn = Sb[d][i % 2], Sb[d][(i + 1) % 2]
                        U, N = pU[d], pN[d]
                        ob = obuf[d][(i // 2) % 3]
                        P.op("pe", "matmul", [SmT[d], vtok], [N], N[pr, :], SmT[d][pr, b, :], vtok[pr, b, :], start=True,
                             stop=False)
                        P.op("pe", "matmul", [qg[d], sbo], [N], N[pr, :], qg[d][ps_, c * 64:(c + 1) * 64], sbo[ps_, :], start=False,
                             stop=True)
                        P.op("pe", "matmul", [kbTok[d], vtok], [U], U[ps_, :], kbTok[d][pr, b, hh * 64:(hh + 1) * 64], vtok[pr, b, :],
                             start=True, stop=True)
                        P.op("dve", "scalar_tensor_tensor", [Sold, eb[d], U], [sbn], out=sbn[ps_, :], in0=Sold[ps_, :],
                             scalar=eb[d][ps_, c:c + 1], in1=U[ps_, :], op0=ALU.mult, op1=ALU.add)
                        P.op("dve", "scalar_tensor_tensor", [Sold, eb[d], U], [Snew], out=Snew[ps_, :], in0=Sold[ps_, :],
                             scalar=eb[d][ps_, c:c + 1], in1=U[ps_, :], op0=ALU.mult, op1=ALU.add)
                        P.op("act", "copy", [N], [ob], out=ob[pr, :], in_=N[pr, :])
                        if i % 2 == 1:
                            P.dma("sp" if d == 0 else "pool", SC["HM"][d, b * 128:(b + 1) * 128, h * 128:(h + 1) * 128], ob[:],
                                  reads=[ob])
        P.barrier()


def _na_rows_ok(qr, kr):
    lo = min(max(qr - 4, 0), 56)
    return lo <= kr < lo + 8


def _na_cfg(g):
    if g == 0:
        return "first", 0, 6
    if g == 7:
        return "last", 26, 6
    return "mid", 4 * g - 2, 8


def _na_range(g, ktl):
    js = [j for j in range(8) for i in range(2) if _na_rows_ok(8 * g + j, 2 * ktl + i)]
    return min(js), max(js)


def na_bias_fn(nc, P, IN, state):
    def fn(h, A):
        if A is not None:
            state.setdefault("sets", {})
            for key, ntile in (("first", 6), ("mid", 8), ("last", 6)):
                state["sets"][(key, h % 2)] = [A.sb([128, 512], F32, "nab_%s%d_%d" % (key, i, h % 2)) for i in range(ntile)]
        out = {}

        def loader():
            for key, g, t_lo in (("first", 0, 0), ("mid", 1, 2), ("last", 7, 26)):
                tiles = state["sets"][(key, h % 2)]
                for r, bt in enumerate(tiles):
                    ktl = t_lo + r
                    u0, u1 = _na_range(g, ktl)
                    P.op("pool", "memset", [], [bt], bt[:, u0 * 64:(u1 + 1) * 64], MASKV)
                    for i in range(2):
                        kr = 2 * ktl + i
                        js = [j for j in range(8) if _na_rows_ok(8 * g + j, kr)]
                        if not js:
                            continue
                        j0, j1 = js[0], js[-1]
                        assert js == list(range(j0, j1 + 1))
                        m0 = 7 - (kr - 8 * g - j0)
                        nj = j1 - j0 + 1
                        P.dma("sp" if i == 0 else "pool",
                              bt[i * 64:(i + 1) * 64, j0 * 64:(j1 + 1) * 64].rearrange("p (m q) -> p m q", q=64),
                              IN["na_bias"][h, m0:m0 + nj, :, :].rearrange("m k q -> k m q"), writes=[bt])
                    yield

        for g in range(8):
            key, t_lo, nt_ = _na_cfg(g)
            for r in range(nt_):
                out[(g, 2 + t_lo + r)] = state["sets"][(key, h % 2)][r]
        return out, loader()

    return fn


def _shapes(d):
    return {k: (v.shape, "bf16" if v.dtype == ml_dtypes.bfloat16 else "f32") for k, v in d.items()}


def run(inputs, stage=99, debug=(), cores=8, skip=()):
    inputs = {k: np.asarray(v) for k, v in inputs.items()}
    sh, per = prep_inputs(inputs)
    nc = build(_shapes(sh), _shapes(per[0]), stage=stage, debug=debug, skip=skip)
    in_maps = [dict(sh, **per[b]) for b in range(cores)]
    res = run_bass_kernel_spmd(nc, in_maps, core_ids=list(range(cores)))
    return res


def kernel(**inputs):
    res = run(inputs)
    return np.stack([np.asarray(r["out"], dtype=np.float32) for r in res.results], axis=0)
```
